# Optimizing a Trainium2 kernel written in Bass

```python
import jax, jax.numpy as jnp
from jax import lax
import numpy as np


D_MODEL = 2048
BATCH = 2
SEQ = 4096
DEPTH = 2

D_A = D_MODEL // 2
A_HEAD = 128
A_HEADS = D_A // A_HEAD
A_CHUNK = 128
D_B = D_MODEL // 2
B_GROUPS = 4
B_GROUP = D_B // B_GROUPS
B_WINDOWS = (2, 4, 8, 16)
D_C = D_MODEL
C_HEAD = 128
C_HEADS = D_C // C_HEAD
C_CHUNK = 64
D_FF = 5632
CONV_W = 3
N_EVEN = (DEPTH + 1) // 2
N_ODD = DEPTH // 2
ALPHA = (2 * DEPTH) ** 0.25
BETA = (8 * DEPTH) ** -0.25
LN_EPS = 1e-5

kernel_name = 'hybrid_gmlp_pool_hgrn2_convffn_deepnorm'


def layer_norm(x, g, b):
    xf = x.astype(jnp.float32)
    mu = jnp.mean(xf, axis=-1, keepdims=True)
    var = jnp.mean(jnp.square(xf - mu), axis=-1, keepdims=True)
    return ((xf - mu) * lax.rsqrt(var + LN_EPS) * g + b).astype(x.dtype)


def rms_norm(x, g):
    xf = x.astype(jnp.float32)
    return xf * lax.rsqrt(jnp.mean(jnp.square(xf), axis=-1, keepdims=True) + LN_EPS) * g


def shift_right(x, s):
    pad = [(0, 0)] * x.ndim
    pad[1] = (s, 0)
    return jnp.pad(x, pad)[:, :x.shape[1]]


def spatial_gating(za, ln_g, ln_b, w_s, b_s):
    bn, t, _ = za.shape
    u, v = jnp.split(za, 2, axis=-1)
    v = layer_norm(v, ln_g, ln_b)
    v = v.reshape(bn, t // A_CHUNK, A_CHUNK, A_HEADS, A_HEAD)
    w = jnp.tril(w_s)
    s = jnp.einsum('hts,bnshc->bnthc', w, v) + b_s.T[None, None, :, :, None]
    return u * s.reshape(bn, t, D_A)


def multiscale_pool(xb, w_pool, scale):
    bn, t, _ = xb.shape
    xg = xb.reshape(bn, t, B_GROUPS, B_GROUP).astype(jnp.float32)
    csum = jnp.cumsum(xg, axis=1)
    pos = jnp.arange(1, t + 1, dtype=jnp.float32)
    outs = []
    for gi, win in enumerate(B_WINDOWS):
        c = csum[:, :, gi]
        wsum = c - shift_right(c, win)
        cnt = jnp.minimum(pos, float(win))[None, :, None]
        outs.append(wsum / cnt - xg[:, :, gi])
    p = jnp.stack(outs, axis=2).astype(xb.dtype)
    y = jnp.einsum('btgc,gcd->btgd', p, w_pool)
    return y.reshape(bn, t, D_B) * scale


def hgrn2(q, f_logit, inp, lb):
    bn, t, _ = q.shape
    n = t // C_CHUNK
    f32 = jnp.float32

    def heads(a):
        return a.astype(f32).reshape(bn, n, C_CHUNK, C_HEADS, C_HEAD).transpose(0, 3, 1, 2, 4)

    f = lb + (1.0 - lb) * jax.nn.sigmoid(f_logit.astype(f32))
    qh = heads(jax.nn.silu(q.astype(f32)))
    kh = heads(1.0 - f)
    vh = heads(inp)
    bcum = jnp.cumsum(heads(jnp.log(f)), axis=3)
    blast = bcum[:, :, :, -1:, :]
    q_dec = qh * jnp.exp(bcum)
    k_dec = kh * jnp.exp(-bcum)
    k_end = kh * jnp.exp(blast - bcum)
    mask = jnp.tril(jnp.ones((C_CHUNK, C_CHUNK), dtype=bool))
    att = jnp.where(mask, jnp.einsum('bhntd,bhnsd->bhnts', q_dec, k_dec), 0.0)
    o_intra = jnp.einsum('bhnts,bhnse->bhnte', att, vh)
    upd = jnp.einsum('bhnsd,bhnse->bhnde', k_end, vh)
    dec = jnp.exp(blast[:, :, :, 0, :])

    def step(state, xs):
        d_n, u_n = xs
        return d_n[..., None] * state + u_n, state

    s0 = jnp.zeros((bn, C_HEADS, C_HEAD, C_HEAD), f32)
    _, s_prev = lax.scan(step, s0, (jnp.moveaxis(dec, 2, 0), jnp.moveaxis(upd, 2, 0)))
    s_prev = jnp.moveaxis(s_prev, 0, 2)
    o = o_intra + jnp.einsum('bhntd,bhnde->bhnte', q_dec, s_prev)
    return o.transpose(0, 2, 3, 1, 4).reshape(bn, t, C_HEADS, C_HEAD)


def conv_ffn(x, w_up, conv_w, conv_b, w_down):
    h = x @ w_up
    hc = conv_b + conv_w[CONV_W - 1] * h
    for j in range(CONV_W - 1):
        hc = hc + conv_w[j] * shift_right(h, CONV_W - 1 - j)
    a, v = jnp.split(hc, 2, axis=-1)
    return (jax.nn.silu(a) * v) @ w_down


def setup_inputs(seed: int = 0) -> dict:
    key = jax.random.key(seed)
    ks = jax.random.split(key, 24)
    f32 = jnp.float32

    def nrm(k, shape, scale):
        return jax.random.normal(k, shape, f32) * scale

    return {
        'x': nrm(ks[0], (BATCH, SEQ, D_MODEL), 1.0),
        'ev_w_in': nrm(ks[1], (N_EVEN, D_MODEL, 2 * D_A + D_B), D_MODEL ** -0.5),
        'ev_ln_v_g': 1.0 + nrm(ks[2], (N_EVEN, D_A), 0.02),
        'ev_ln_v_b': nrm(ks[3], (N_EVEN, D_A), 0.02),
        'ev_w_s': nrm(ks[4], (N_EVEN, A_HEADS, A_CHUNK, A_CHUNK), 0.5 * A_CHUNK ** -0.5),
        'ev_b_s': 1.0 + nrm(ks[5], (N_EVEN, A_HEADS, A_CHUNK), 0.02),
        'ev_w_pool': nrm(ks[6], (N_EVEN, B_GROUPS, B_GROUP, B_GROUP), B_GROUP ** -0.5),
        'ev_pool_scale': 1.0 + nrm(ks[7], (N_EVEN, D_B), 0.02),
        'ev_w_out': nrm(ks[8], (N_EVEN, D_A + D_B, D_MODEL), BETA * (D_A + D_B) ** -0.5),
        'od_w_in': nrm(ks[9], (N_ODD, D_MODEL, 4 * D_C), D_MODEL ** -0.5),
        'od_norm_g': 1.0 + nrm(ks[10], (N_ODD, D_C), 0.02),
        'od_w_out': nrm(ks[11], (N_ODD, D_C, D_MODEL), BETA * D_C ** -0.5),
        'lb_param': nrm(ks[12], (DEPTH, D_C), 0.1),
        'ffn_w_up': nrm(ks[13], (DEPTH, D_MODEL, 2 * D_FF), D_MODEL ** -0.5),
        'ffn_conv_w': nrm(ks[14], (DEPTH, CONV_W, 2 * D_FF), CONV_W ** -0.5),
        'ffn_conv_b': nrm(ks[15], (DEPTH, 2 * D_FF), 0.02),
        'ffn_w_down': nrm(ks[16], (DEPTH, D_FF, D_MODEL), BETA * D_FF ** -0.5),
        'ln1_g': 1.0 + nrm(ks[17], (DEPTH, D_MODEL), 0.02),
        'ln1_b': nrm(ks[18], (DEPTH, D_MODEL), 0.02),
        'ln2_g': 1.0 + nrm(ks[19], (DEPTH, D_MODEL), 0.02),
        'ln2_b': nrm(ks[20], (DEPTH, D_MODEL), 0.02),
    }


def reference(x, ev_w_in, ev_ln_v_g, ev_ln_v_b, ev_w_s, ev_b_s, ev_w_pool, ev_pool_scale,
              ev_w_out, od_w_in, od_norm_g, od_w_out, lb_param, ffn_w_up, ffn_conv_w,
              ffn_conv_b, ffn_w_down, ln1_g, ln1_b, ln2_g, ln2_b):
    bn, t, _ = x.shape
    lb_all = jnp.cumsum(jax.nn.softmax(lb_param.astype(jnp.float32), axis=0), axis=0)
    lb_all = lb_all - lb_all[0]
    for l in range(DEPTH):
        if l % 2 == 0:
            e = l // 2
            h = x @ ev_w_in[e]
            za = jax.nn.gelu(h[..., :2 * D_A])
            xb = h[..., 2 * D_A:]
            ya = spatial_gating(za, ev_ln_v_g[e], ev_ln_v_b[e], ev_w_s[e], ev_b_s[e])
            yb = multiscale_pool(xb, ev_w_pool[e], ev_pool_scale[e])
            mix = jnp.concatenate([ya, yb], axis=-1) @ ev_w_out[e]
        else:
            o = l // 2
            h = x @ od_w_in[o]
            q, f_logit, inp, g = jnp.split(h, 4, axis=-1)
            y = hgrn2(q, f_logit, inp, lb_all[l])
            y = rms_norm(y, od_norm_g[o].reshape(C_HEADS, C_HEAD)).reshape(bn, t, D_C)
            y = (y * jax.nn.sigmoid(g.astype(jnp.float32))).astype(x.dtype)
            mix = y @ od_w_out[o]
        x = layer_norm(ALPHA * x + mix, ln1_g[l], ln1_b[l])
        x = layer_norm(ALPHA * x + conv_ffn(x, ffn_w_up[l], ffn_conv_w[l], ffn_conv_b[l], ffn_w_down[l]),
                       ln2_g[l], ln2_b[l])
    return x
```

```python
import numpy as np
from contextlib import ExitStack
import concourse.bass as bass
import concourse.mybir as mybir
from concourse.bass_utils import run_bass_kernel_spmd

F32 = mybir.dt.float32
BF16 = mybir.dt.bfloat16
AF = mybir.ActivationFunctionType
ALU = mybir.AluOpType

D = 2048
NCH = 16
T = 1024
NCORES = 8
DFF = 5632
NFF = 44
ALPHA = 4.0 ** 0.25
LN_EPS = 1e-5
ENGS = ("pe", "act", "dve", "pool", "sp")
_DT_SIZE = {F32: 4, BF16: 2}


def _dsize(dt):
    return _DT_SIZE.get(dt, 4)


class _Op:
    __slots__ = ("eng", "idx", "fn", "deps", "chan", "chan_val", "signal", "val")


class Sched:
    def __init__(self):
        self.ops = {e: [] for e in ENGS}
        self.track = {}
        self.chan_cnt = []
        self.chan_total = []

    @staticmethod
    def _rng(ap):
        t = ap.tensor
        name = t.name
        sp = str(ap.space) if hasattr(ap, "space") else ""
        pat = ap.ap
        esz = _dsize(ap.dtype)
        if "DRAM" in sp.upper() or "Dram" in type(t).__name__ or "DRam" in type(t).__name__:
            ext = 1
            for (st, cnt) in pat:
                ext += abs(st) * (cnt - 1)
            return name, ap.offset * esz, (ap.offset + ext) * esz
        pstride = pat[0][0]
        lo = ap.offset % pstride if pstride > 0 else ap.offset
        ext = 1
        for (st, cnt) in pat[1:]:
            ext += abs(st) * (cnt - 1)
        return name, lo * esz, (lo + ext) * esz

    def _touch(self, name, lo, hi, op, is_write, deps):
        segs = self.track.setdefault(name, [])
        new = []
        covered = []
        for s in segs:
            slo, shi, w, rs = s
            if shi <= lo or slo >= hi:
                new.append(s)
                continue
            if slo < lo:
                new.append([slo, lo, w, list(rs)])
            if shi > hi:
                new.append([hi, shi, w, list(rs)])
            olo, ohi = max(slo, lo), min(shi, hi)
            if w is not None:
                deps.add(w)
            if is_write:
                for r in rs:
                    deps.add(r)
            else:
                covered.append([olo, ohi, w, rs + [op]])
        if is_write:
            new.append([lo, hi, op, []])
        else:
            covered.sort(key=lambda s: s[0])
            cur = lo
            for c in covered:
                if c[0] > cur:
                    new.append([cur, c[0], None, [op]])
                new.append(c)
                cur = c[1]
            if cur < hi:
                new.append([cur, hi, None, [op]])
        self.track[name] = new

    def add(self, eng, fn, reads=(), writes=(), chan=None):
        o = _Op()
        o.eng = eng
        o.fn = fn
        o.chan = chan
        o.signal = False
        o.val = None
        o.chan_val = None
        deps = set()
        for ap in reads:
            if ap is None or isinstance(ap, (int, float)):
                continue
            n, lo, hi = self._rng(ap)
            self._touch(n, lo, hi, o, False, deps)
        for ap in writes:
            n, lo, hi = self._rng(ap)
            if eng == "pe":
                lo = (lo // 2048) * 2048
                hi = ((hi + 2047) // 2048) * 2048
            self._touch(n, lo, hi, o, True, deps)
        deps.discard(o)
        o.deps = deps
        if chan is not None:
            self.chan_cnt[chan] += 1
            o.chan_val = 16 * self.chan_cnt[chan]
        o.idx = len(self.ops[eng])
        self.ops[eng].append(o)
        return o

    def new_chan(self, total=False):
        self.chan_cnt.append(0)
        self.chan_total.append(total)
        return len(self.chan_cnt) - 1

    def emit(self, nc, es, final_chans=()):
        for e in ENGS:
            for o in self.ops[e]:
                for d in o.deps:
                    if d.chan is None:
                        d.signal = True
        for e in ENGS:
            c = 0
            for o in self.ops[e]:
                if o.chan is None and o.signal:
                    c += 1
                    o.val = c
        esem = {e: es.enter_context(nc.semaphore("s_" + e)) for e in ENGS}
        csem = [es.enter_context(nc.semaphore("c_%d" % i)) for i in range(len(self.chan_cnt))]
        block = es.enter_context(nc.Block())
        nwaits = {e: 0 for e in ENGS}

        def run(engname, eobj):
            seen = {}
            for o in self.ops[engname]:
                need = {}
                for d in o.deps:
                    if d.chan is not None:
                        key = ("c", d.chan)
                        v = 16 * self.chan_cnt[d.chan] if self.chan_total[d.chan] else d.chan_val
                    else:
                        if d.eng == engname and engname == "pe":
                            continue
                        key = ("e", d.eng)
                        v = d.val
                    if v > need.get(key, 0):
                        need[key] = v
                for key, v in need.items():
                    if v <= seen.get(key, 0):
                        continue
                    seen[key] = v
                    sem = csem[key[1]] if key[0] == "c" else esem[key[1]]
                    eobj.wait_ge(sem, v)
                    nwaits[engname] += 1
                inst = o.fn(eobj)
                if o.chan is not None:
                    inst.then_inc(csem[o.chan], 16)
                elif o.signal:
                    assert inst is not None
                    inst.then_inc(esem[engname], 1)
            if engname == "sp":
                for ch in final_chans:
                    if self.chan_cnt[ch] > 0:
                        eobj.wait_ge(csem[ch], 16 * self.chan_cnt[ch])

        @block.tensor
        def _(e):
            run("pe", e)

        @block.scalar
        def _(e):
            run("act", e)

        @block.vector
        def _(e):
            run("dve", e)

        @block.gpsimd
        def _(e):
            run("pool", e)

        @block.sync
        def _(e):
            run("sp", e)

        self.nwaits = nwaits

    def mm(self, out, lhsT, rhs, start=True, stop=True):
        return self.add("pe", lambda e: e.matmul(out, lhsT=lhsT, rhs=rhs, start=start, stop=stop),
                        reads=[lhsT, rhs], writes=[out])

    def transpose(self, out, in_, ident):
        return self.add("pe", lambda e: e.transpose(out, in_, ident), reads=[in_, ident], writes=[out])

    def act(self, out, in_, func, bias=None, scale=None):
        kw = {}
        rd = [in_]
        if bias is not None:
            kw["bias"] = bias
            rd.append(bias)
        if scale is not None:
            kw["scale"] = scale
            rd.append(scale)
        return self.add("act", lambda e: e.activation(out=out, in_=in_, func=func, **kw), reads=rd, writes=[out])

    def tt(self, eng, out, in0, in1, op):
        return self.add(eng, lambda e: e.tensor_tensor(out=out, in0=in0, in1=in1, op=op),
                        reads=[in0, in1], writes=[out])

    def ts(self, eng, out, in0, s1, s2, op0, op1=None):
        if op1 is None:
            return self.add(eng, lambda e: e.tensor_scalar(out=out, in0=in0, scalar1=s1, scalar2=None, op0=op0),
                            reads=[in0, s1], writes=[out])
        return self.add(eng, lambda e: e.tensor_scalar(out=out, in0=in0, scalar1=s1, scalar2=s2, op0=op0, op1=op1),
                        reads=[in0, s1, s2], writes=[out])

    def stt(self, out, in0, scalar, in1, op0, op1):
        return self.add("dve", lambda e: e.scalar_tensor_tensor(out=out, in0=in0, scalar=scalar, in1=in1,
                                                                op0=op0, op1=op1),
                        reads=[in0, scalar, in1], writes=[out])

    def copy(self, eng, out, in_):
        if eng == "act":
            return self.add("act", lambda e: e.copy(out=out, in_=in_), reads=[in_], writes=[out])
        return self.add(eng, lambda e: e.tensor_copy(out=out, in_=in_), reads=[in_], writes=[out])

    def memset(self, eng, ap, val):
        return self.add(eng, lambda e: e.memset(ap, val), writes=[ap])

    def dma(self, eng, out, in_, chan):
        return self.add(eng, lambda e: e.dma_start(out=out, in_=in_), reads=[in_], writes=[out], chan=chan)


class WeightStream:
    def __init__(self, S, nc, es, nslots, free_elems, name="wslot"):
        self.S = S
        self.slots = [es.enter_context(nc.sbuf_tensor("%s%d" % (name, i), [128, free_elems], BF16))
                      for i in range(nslots)]
        self.chans = [S.new_chan() for _ in range(nslots)]
        self.uses = []
        self.loaded = 0
        self.released = -1
        self.n = nslots

    def plan(self, shape, src):
        self.uses.append((shape, src))
        return len(self.uses) - 1

    def view(self, k):
        shape, _ = self.uses[k]
        sl = self.slots[k % self.n]
        n = 1
        for s in shape:
            n *= s
        v = sl[:, 0:n]
        if len(shape) == 2:
            return v.rearrange("p (a b) -> p a b", a=shape[0], b=shape[1])
        return v

    def _load_upto(self, k):
        while self.loaded < len(self.uses) and self.loaded <= k:
            j = self.loaded
            _, src = self.uses[j]
            self.S.dma("pool", self.view(j), src, self.chans[j % self.n])
            self.loaded += 1

    def get(self, k):
        assert k <= self.released + self.n, (k, self.released)
        self._load_upto(k)
        return self.view(k)

    def release(self, k):
        self.released = max(self.released, k)
        self._load_upto(self.released + self.n)


FF_QUARTERS = (12, 10, 12, 10)


def plan_ffn_weights(ws, w_up, w_down):
    plan = []
    base = 0
    wu = w_up.rearrange("(kc p) n -> p kc n", p=128)
    for q, nq in enumerate(FF_QUARTERS):
        ups = []
        for j in range(0, nq, 2):
            ca = base + j
            ua = ws.plan((16, 256), wu[:, :, ca * 128:ca * 128 + 256])
            uv = ws.plan((16, 256), wu[:, :, (NFF + ca) * 128:(NFF + ca) * 128 + 256])
            ups.append((ca, ua, uv))
        downs = []
        wd = w_down[base * 128:(base + nq) * 128, :].rearrange("(j p) n -> p j n", p=128)
        for op_ in range(8):
            downs.append((op_, ws.plan((nq, 256), wd[:, :, op_ * 256:(op_ + 1) * 256])))
        plan.append((base, nq, ups, downs))
        base += nq
    return plan


def emit_ffn(S, ws, plan, xf, xb, cwb, gq, tmps, ps):
    G = (0, 1536)
    gi = 0
    for (base, nq, ups, downs) in plan:
        for (ca, ua, uv) in ups:
            wa = ws.get(ua)
            wv = ws.get(uv)
            for sub in range(2):
                c_a = ca + sub
                c_v = NFF + ca + sub
                j = c_a - base
                tm = {}
                for which, (wt, cc) in enumerate(((wa, c_a), (wv, c_v))):
                    g0 = G[which]
                    for kc in range(NCH):
                        lw = wt[:, kc, sub * 128:(sub + 1) * 128]
                        S.mm(ps[:, g0 + 510:g0 + 512], lw, xb[:, kc, 0:2], start=(kc == 0), stop=(kc == NCH - 1))
                        S.mm(ps[:, g0 + 512:g0 + 1024], lw, xb[:, kc, 2:514], start=(kc == 0), stop=(kc == NCH - 1))
                        S.mm(ps[:, g0 + 1024:g0 + 1536], lw, xb[:, kc, 514:1026], start=(kc == 0),
                             stop=(kc == NCH - 1))
                    tmp = tmps["a" if which == 0 else "v"][gi % 2]
                    tm[which] = tmp
                    S.act(tmp[:, :], ps[:, g0 + 512:g0 + 1536], AF.Identity, bias=cwb[:, cc, 3:4],
                          scale=cwb[:, cc, 2:3])
                    S.stt(tmp[:, :], ps[:, g0 + 511:g0 + 1535], cwb[:, cc, 1:2], tmp[:, :], ALU.mult, ALU.add)
                    S.stt(tmp[:, :], ps[:, g0 + 510:g0 + 1534], cwb[:, cc, 0:1], tmp[:, :], ALU.mult, ALU.add)
                sa = tmps["s"][gi % 2]
                S.act(sa[:, :], tm[0][:, :], AF.Silu)
                S.tt("dve", gq[:, j, :], sa[:, :], tm[1][:, :], ALU.mult)
                gi += 1
            ws.release(uv)
        for (op_, ud) in downs:
            wd = ws.get(ud)
            for sub in range(2):
                oc = op_ * 2 + sub
                for tt_ in range(2):
                    for j in range(nq):
                        S.mm(ps[:, 3072 + tt_ * 512:3072 + (tt_ + 1) * 512], wd[:, j, sub * 128:(sub + 1) * 128],
                             gq[:, j, tt_ * 512:(tt_ + 1) * 512], start=(j == 0), stop=(j == nq - 1))
                if base == 0:
                    S.stt(xf[:, oc, 2:1026], xf[:, oc, 2:1026], ALPHA, ps[:, 3072:4096], ALU.mult, ALU.add)
                else:
                    S.tt("dve", xf[:, oc, 2:1026], xf[:, oc, 2:1026], ps[:, 3072:4096], ALU.add)
            ws.release(ud)


def emit_ln(S, zf, c0, n, gb, ones, tmps, ps, outs):
    nt = (n + 511) // 512
    zb = tmps["zb"]
    zs = tmps["zs"]
    for c in range(NCH):
        b0 = zb[c % 2]
        s0 = zs[c % 2]
        S.act(b0[:, 0:n], zf[:, c, c0:c0 + n], AF.Identity)
        S.act(s0[:, 0:n], zf[:, c, c0:c0 + n], AF.Square)
        for t_ in range(nt):
            w = min(512, n - t_ * 512)
            S.mm(ps[:, t_ * 512:t_ * 512 + w], ones[:, :], b0[:, t_ * 512:t_ * 512 + w], start=(c == 0),
                 stop=(c == NCH - 1))
            S.mm(ps[:, 2048 + t_ * 512:2048 + t_ * 512 + w], ones[:, :], s0[:, t_ * 512:t_ * 512 + w],
                 start=(c == 0), stop=(c == NCH - 1))
    mean = tmps["mean"]
    rstd = tmps["rstd"]
    S.ts("dve", mean[:, 0:n], ps[:, 0:n], 1.0 / D, None, ALU.mult)
    S.tt("dve", rstd[:, 0:n], mean[:, 0:n], mean[:, 0:n], ALU.mult)
    S.stt(rstd[:, 0:n], ps[:, 2048:2048 + n], 1.0 / D, rstd[:, 0:n], ALU.mult, ALU.subtract)
    S.act(rstd[:, 0:n], rstd[:, 0:n], AF.Sqrt, bias=tmps["eps"][:, 0:1], scale=1.0)
    S.add("dve", lambda e: e.reciprocal(out=rstd[:, 0:n], in_=rstd[:, 0:n]), reads=[rstd[:, 0:n]],
          writes=[rstd[:, 0:n]])
    for c in range(NCH):
        zc = zf[:, c, c0:c0 + n]
        S.tt("dve", zc, zc, mean[:, 0:n], ALU.subtract)
        S.tt("dve", zc, zc, rstd[:, 0:n], ALU.mult)
        for i, dst in enumerate(outs):
            S.act(dst(c), zc, AF.Identity, bias=gb[:, c, 1:2], scale=gb[:, c, 0:1])


def build_ffn_launch():
    nc = bass.Bass("TRN2", target_bir_lowering=False)
    xT = nc.dram_tensor("xT", [D, T + 2], F32, kind="ExternalInput").ap()
    w_up = nc.dram_tensor("w_up", [D, 2 * DFF], F32, kind="ExternalInput").ap()
    w_down = nc.dram_tensor("w_down", [DFF, D], F32, kind="ExternalInput").ap()
    cwb_d = nc.dram_tensor("cwb", [128, 88 * 4], F32, kind="ExternalInput").ap()
    gb_d = nc.dram_tensor("ln2gb", [128, 32], F32, kind="ExternalInput").ap()
    yT = nc.dram_tensor("yT", [D, T], F32, kind="ExternalOutput").ap()
    S = Sched()
    with ExitStack() as es:
        xf = es.enter_context(nc.sbuf_tensor("xf", [128, NCH, T + 2], F32))
        xb = es.enter_context(nc.sbuf_tensor("xb", [128, NCH, T + 2], BF16))
        cwb = es.enter_context(nc.sbuf_tensor("cwb_s", [128, 88, 4], F32))
        gb = es.enter_context(nc.sbuf_tensor("gb_s", [128, NCH, 2], F32))
        gq = es.enter_context(nc.sbuf_tensor("gq", [128, 12, T], BF16))
        ones = es.enter_context(nc.sbuf_tensor("ones", [128, 128], BF16))
        eps = es.enter_context(nc.sbuf_tensor("eps", [128, 1], F32))
        tmps = {
            "a": [es.enter_context(nc.sbuf_tensor("ta%d" % i, [128, T], F32)) for i in range(2)],
            "v": [es.enter_context(nc.sbuf_tensor("tv%d" % i, [128, T], F32)) for i in range(2)],
            "s": [es.enter_context(nc.sbuf_tensor("tsl%d" % i, [128, T], F32)) for i in range(2)],
            "eps": eps,
        }
        tmps["zb"] = [es.enter_context(nc.sbuf_tensor("zb%d" % i, [128, T], BF16)) for i in range(2)]
        tmps["zs"] = [es.enter_context(nc.sbuf_tensor("zs%d" % i, [128, T], BF16)) for i in range(2)]
        tmps["mean"] = tmps["a"][0]
        tmps["rstd"] = tmps["a"][1]
        ps = es.enter_context(nc.psum_tensor("ps", [128, 4096], F32))
        ws = WeightStream(S, nc, es, 4, 16 * 256)
        plan = plan_ffn_weights(ws, w_up, w_down)

        ch_in = S.new_chan(total=True)
        ch_p = S.new_chan(total=True)
        ch_out = [S.new_chan() for _ in range(4)]
        S.dma("sp", cwb[:, :, :], cwb_d.rearrange("p (c j) -> p c j", j=4), ch_p)
        S.dma("sp", gb[:, :, :], gb_d.rearrange("p (c j) -> p c j", j=2), ch_p)
        S.memset("dve", ones[:, :], 1.0)
        S.memset("dve", eps[:, :], LN_EPS)
        for c in range(NCH):
            S.dma("sp", xf[:, c, :], xT[c * 128:(c + 1) * 128, :], ch_in)
        for c in range(NCH):
            S.act(xb[:, c, :], xf[:, c, :], AF.Identity)
        emit_ffn(S, ws, plan, xf, xb, cwb, gq, tmps, ps)
        emit_ln(S, xf, 2, T, gb, ones, tmps, ps, [lambda c: xf[:, c, 2:T + 2]])
        for c in range(NCH):
            S.dma("sp", yT[c * 128:(c + 1) * 128, :], xf[:, c, 2:T + 2], ch_out[c % 4])
        S.emit(nc, es, final_chans=ch_out)
    return nc, S


def _pm(v, nch):
    return np.ascontiguousarray(np.asarray(v, np.float32).reshape(nch, 128).T)


def prep_ffn_params(l, ffn_conv_w, ffn_conv_b, ln2_g, ln2_b):
    cw = np.asarray(ffn_conv_w[l], np.float32)
    cb = np.asarray(ffn_conv_b[l], np.float32)
    cwb = np.stack([_pm(cw[0], 88), _pm(cw[1], 88), _pm(cw[2], 88), _pm(cb, 88)], axis=-1)
    gb = np.stack([_pm(ln2_g[l], 16), _pm(ln2_b[l], 16)], axis=-1)
    return np.ascontiguousarray(cwb.reshape(128, 88 * 4)), np.ascontiguousarray(gb.reshape(128, 32))


def run_ffn_launch(x1, l, ffn_w_up, ffn_conv_w, ffn_conv_b, ffn_w_down, ln2_g, ln2_b):
    nc, S = build_ffn_launch()
    cwb, gb = prep_ffn_params(l, ffn_conv_w, ffn_conv_b, ln2_g, ln2_b)
    wu = np.ascontiguousarray(ffn_w_up[l], np.float32)
    wd = np.ascontiguousarray(ffn_w_down[l], np.float32)
    x1 = np.asarray(x1, np.float32)
    in_maps = []
    for c in range(NCORES):
        b, s = divmod(c, 4)
        t0 = s * T
        xt = np.zeros((D, T + 2), np.float32)
        xt[:, 2:] = x1[b, t0:t0 + T].T
        if s > 0:
            xt[:, 0:2] = x1[b, t0 - 2:t0].T
        in_maps.append({"xT": xt, "w_up": wu, "w_down": wd, "cwb": cwb, "ln2gb": gb})
    res = run_bass_kernel_spmd(nc, in_maps, core_ids=list(range(NCORES)))
    out = np.zeros((2, 4096, D), np.float32)
    for c in range(NCORES):
        b, s = divmod(c, 4)
        out[b, s * T:(s + 1) * T] = res.results[c]["yT"].T
    return out


TH = T + 128
B_WINDOWS = (2, 4, 8, 16)
GELU_C = 0.044715
GELU_S = 2.0 * 0.7978845608028654


def emit_gelu(S, dst, src_ps, t1, t2):
    S.act(t1, src_ps, AF.Square)
    S.ts("dve", t1, t1, GELU_C, 1.0, ALU.mult, ALU.add)
    S.tt("dve", t1, t1, src_ps, ALU.mult)
    S.act(t2, t1, AF.Sigmoid, scale=GELU_S)
    S.tt("dve", dst, t2, src_ps, ALU.mult)


def build_mix0_launch():
    nc = bass.Bass("TRN2", target_bir_lowering=False)
    x0T = nc.dram_tensor("x0T", [D, TH], F32, kind="ExternalInput").ap()
    w_in = nc.dram_tensor("w_in", [D, 3072], F32, kind="ExternalInput").ap()
    w_out = nc.dram_tensor("w_out", [D, D], F32, kind="ExternalInput").ap()
    wsT_d = nc.dram_tensor("wsT", [128, 8 * 128], F32, kind="ExternalInput").ap()
    mask_d = nc.dram_tensor("maskA", [128, 128], F32, kind="ExternalInput").ap()
    bsT_d = nc.dram_tensor("bsT", [128, 8 * 128], F32, kind="ExternalInput").ap()
    lnv_d = nc.dram_tensor("lnv", [128, 2 * 1024], F32, kind="ExternalInput").ap()
    wp_d = nc.dram_tensor("w_pool", [4 * 256, 256], F32, kind="ExternalInput").ap()
    psc_d = nc.dram_tensor("pscale", [128, 8], F32, kind="ExternalInput").ap()
    gb_d = nc.dram_tensor("ln1gb", [128, 32], F32, kind="ExternalInput").ap()
    rc_d = nc.dram_tensor("rcnt", [128, 64], F32, kind="ExternalInput").ap()
    flag_d = nc.dram_tensor("flag", [128, 1], F32, kind="ExternalInput").ap()
    x1T = nc.dram_tensor("x1T", [D, T + 2], F32, kind="ExternalOutput").ap()
    S = Sched()
    with ExitStack() as es:
        arena = es.enter_context(nc.sbuf_tensor("arena", [128, NCH * (T + 2)], F32))
        zf = arena[:, :].rearrange("p (c t) -> p c t", c=NCH, t=T + 2)
        x0b = arena[:, 0:NCH * TH // 2].bitcast(BF16).rearrange("p (c t) -> p c t", c=NCH, t=TH)
        u = es.enter_context(nc.sbuf_tensor("u", [128, 8, TH], BF16))
        vt = es.enter_context(nc.sbuf_tensor("vt", [128, 9, 1024], BF16))
        pp = es.enter_context(nc.sbuf_tensor("pp", [128, 8, TH], BF16))
        wsT = es.enter_context(nc.sbuf_tensor("wsT_s", [128, 8, 128], BF16))
        mask = es.enter_context(nc.sbuf_tensor("mask_s", [128, 128], F32))
        bsT = es.enter_context(nc.sbuf_tensor("bsT_s", [128, 8, 128], F32))
        lnv = es.enter_context(nc.sbuf_tensor("lnv_s", [128, 2, 1024], F32))
        wp = es.enter_context(nc.sbuf_tensor("wp_s", [128, 8, 256], BF16))
        psc = es.enter_context(nc.sbuf_tensor("psc_s", [128, 8], F32))
        gb = es.enter_context(nc.sbuf_tensor("gb_s", [128, NCH, 2], F32))
        rc = es.enter_context(nc.sbuf_tensor("rc_s", [128, 4, 16], F32))
        flag = es.enter_context(nc.sbuf_tensor("flag_s", [128, 1], F32))
        ones = es.enter_context(nc.sbuf_tensor("ones", [128, 128], BF16))
        eps = es.enter_context(nc.sbuf_tensor("eps", [128, 1], F32))
        xbf = es.enter_context(nc.sbuf_tensor("xbf", [128, 16 + TH], F32))
        g1 = [es.enter_context(nc.sbuf_tensor("g1_%d" % i, [128, TH], F32)) for i in range(2)]
        g2p = [es.enter_context(nc.sbuf_tensor("g2_%d" % i, [128, 16 + TH], F32)) for i in range(2)]
        g2 = [t[:, 16:16 + TH] for t in g2p]
        tA, tB = g2p
        wsTf = g1[1][:, 0:1024].rearrange("p (h t) -> p h t", h=8)
        st = es.enter_context(nc.sbuf_tensor("st", [128, 8], F32))
        small = es.enter_context(nc.sbuf_tensor("small", [128, 32], F32))
        tmps = {"eps": eps,
                "zb": [g2p[i][:, 16:16 + 513].bitcast(BF16) for i in range(2)],
                "zs": [xbf[:, 16:16 + 513].bitcast(BF16), xbf[:, 600:600 + 513].bitcast(BF16)],
                "mean": g1[0], "rstd": g1[1]}
        ps = es.enter_context(nc.psum_tensor("ps", [128, 4096], F32))
        ws = WeightStream(S, nc, es, 4, 16 * 256)
        wi = w_in.rearrange("(kc p) n -> p kc n", p=128)
        wo = w_out.rearrange("(kc p) n -> p kc n", p=128)
        u_xb = [ws.plan((16, 256), wi[:, :, 2048 + i * 256:2048 + (i + 1) * 256]) for i in range(4)]
        u_u = [ws.plan((16, 256), wi[:, :, i * 256:(i + 1) * 256]) for i in range(4)]
        u_v = [ws.plan((16, 256), wi[:, :, 1024 + i * 256:1024 + (i + 1) * 256]) for i in range(4)]
        u_o = [ws.plan((16, 256), wo[:, :, i * 256:(i + 1) * 256]) for i in range(8)]

        chp = S.new_chan(total=True)
        chx = S.new_chan(total=True)
        S.dma("sp", wsTf, wsT_d.rearrange("p (h t) -> p h t", h=8), chp)
        S.dma("sp", mask[:, :], mask_d, chp)
        S.dma("sp", bsT[:, :, :], bsT_d.rearrange("p (h t) -> p h t", h=8), chp)
        S.dma("sp", lnv[:, :, :], lnv_d.rearrange("p (a c) -> p a c", a=2), chp)
        S.dma("sp", psc[:, :], psc_d, chp)
        S.dma("sp", gb[:, :, :], gb_d.rearrange("p (c j) -> p c j", j=2), chp)
        S.dma("sp", rc[:, :, :], rc_d.rearrange("p (g j) -> p g j", g=4), chp)
        S.dma("sp", flag[:, :], flag_d, chp)
        S.dma("pool", wp[:, :, :], wp_d.rearrange("(a p) n -> p a n", p=128), chx)
        for c in range(NCH):
            S.dma("pool", x0b[:, c, :], x0T[c * 128:(c + 1) * 128, :], chx)
        ws.release(-1)
        S.memset("dve", ones[:, :], 1.0)
        S.memset("dve", eps[:, :], LN_EPS)
        S.memset("dve", xbf[:, 0:16], 0.0)
        S.memset("dve", tA[:, 0:16], 0.0)
        S.memset("dve", tB[:, 0:16], 0.0)
        for h in range(8):
            S.tt("dve", wsT[:, h, :], wsTf[:, h, :], mask[:, :], ALU.mult)

        GR = (0, 1536)
        TT3 = ((0, 512), (512, 512), (1024, 128))

        def proj_fm(wt, sub, g0):
            for kc in range(NCH):
                for (t0, w) in TT3:
                    S.mm(ps[:, g0 + t0:g0 + t0 + w], wt[:, kc, sub * 128:(sub + 1) * 128], x0b[:, kc, t0:t0 + w],
                         start=(kc == 0), stop=(kc == NCH - 1))

        gi = 0
        for i in range(4):
            wt = ws.get(u_xb[i])
            for sub in range(2):
                c = i * 2 + sub
                g = c // 2
                g0 = GR[gi % 2]
                gi += 1
                proj_fm(wt, sub, g0)
                S.act(xbf[:, 16:16 + TH], ps[:, g0:g0 + TH], AF.Identity)
                src = xbf
                dsts = [tA, tB]
                for k in range(g + 1):
                    sh = 1 << k
                    dst = dsts[k % 2]
                    S.tt("dve", dst[:, 16:16 + TH], src[:, 16:16 + TH], src[:, 16 - sh:16 + TH - sh], ALU.add)
                    src = dst
                win = B_WINDOWS[g]
                S.stt(pp[:, c, :], src[:, 16:16 + TH], 1.0 / win, xbf[:, 16:16 + TH], ALU.mult, ALU.subtract)
                S.tt("dve", small[:, 0:16], src[:, 16 + 128:16 + 144], rc[:, g, :], ALU.mult)
                S.tt("dve", pp[:, c, 128:144], small[:, 0:16], xbf[:, 16 + 128:16 + 144], ALU.subtract)
            ws.release(u_xb[i])
        for i in range(4):
            wt = ws.get(u_u[i])
            for sub in range(2):
                c = i * 2 + sub
                g0 = GR[gi % 2]
                proj_fm(wt, sub, g0)
                emit_gelu(S, u[:, c, :], ps[:, g0:g0 + TH], g1[gi % 2][:, :], g2[gi % 2])
                gi += 1
            ws.release(u_u[i])
        wv = [ws.get(k) for k in u_v]
        for tk in range(9):
            vr = g1[tk % 2]
            for cg in range(4):
                r0 = 3072 + ((tk * 4 + cg) % 2) * 512
                for kc in range(NCH):
                    S.mm(ps[:, r0:r0 + 256], x0b[:, kc, tk * 128:(tk + 1) * 128], wv[cg][:, kc, :],
                         start=(kc == 0), stop=(kc == NCH - 1))
                emit_gelu(S, vr[:, cg * 256:(cg + 1) * 256], ps[:, r0:r0 + 256],
                          g2[0][:, cg * 256:(cg + 1) * 256], g2[1][:, cg * 256:(cg + 1) * 256])
            sq = g2[0]
            S.add("dve", lambda e, vr=vr: e.reduce_sum(out=st[:, 0:1], in_=vr[:, 0:1024], axis=mybir.AxisListType.X),
                  reads=[vr[:, 0:1024]], writes=[st[:, 0:1]])
            S.act(sq[:, 0:1024], vr[:, 0:1024], AF.Square)
            S.add("dve", lambda e, sq=sq: e.reduce_sum(out=st[:, 1:2], in_=sq[:, 0:1024], axis=mybir.AxisListType.X),
                  reads=[sq[:, 0:1024]], writes=[st[:, 1:2]])
            S.ts("dve", st[:, 2:3], st[:, 0:1], 1.0 / 1024, None, ALU.mult)
            S.tt("dve", st[:, 3:4], st[:, 2:3], st[:, 2:3], ALU.mult)
            S.stt(st[:, 4:5], st[:, 1:2], 1.0 / 1024, st[:, 3:4], ALU.mult, ALU.subtract)
            S.act(st[:, 5:6], st[:, 4:5], AF.Sqrt, bias=eps[:, 0:1], scale=1.0)
            S.add("dve", lambda e: e.reciprocal(out=st[:, 6:7], in_=st[:, 5:6]), reads=[st[:, 5:6]],
                  writes=[st[:, 6:7]])
            S.ts("dve", vr[:, 0:1024], vr[:, 0:1024], st[:, 2:3], st[:, 6:7], ALU.subtract, ALU.mult)
            S.tt("dve", vr[:, 0:1024], vr[:, 0:1024], lnv[:, 0, :], ALU.mult)
            S.tt("dve", vt[:, tk, :], vr[:, 0:1024], lnv[:, 1, :], ALU.add)
        ws.release(u_v[3])
        for tk in range(9):
            for half in range(2):
                r0 = 2048 + half * 512
                for hh in range(4):
                    h = half * 4 + hh
                    S.mm(ps[:, r0 + hh * 128:r0 + (hh + 1) * 128], vt[:, tk, h * 128:(h + 1) * 128], wsT[:, h, :],
                         start=True, stop=True)
                tmp = g2[half][:, 0:512].rearrange("p (h t) -> p h t", h=4)
                S.tt("dve", tmp, ps[:, r0:r0 + 512].rearrange("p (h t) -> p h t", h=4),
                     bsT[:, half * 4:half * 4 + 4, :], ALU.add)
                uu = u[:, half * 4:half * 4 + 4, tk * 128:(tk + 1) * 128]
                S.tt("dve", uu, tmp, uu, ALU.mult)
        for g in range(4):
            for oc in range(2):
                g0 = GR[oc]
                for kc in range(2):
                    for (t0, w) in TT3:
                        S.mm(ps[:, g0 + t0:g0 + t0 + w], wp[:, g * 2 + kc, oc * 128:(oc + 1) * 128],
                             pp[:, g * 2 + kc, t0:t0 + w], start=(kc == 0), stop=(kc == 1))
            for oc in range(2):
                g0 = GR[oc]
                c = g * 2 + oc
                S.act(pp[:, c, :], ps[:, g0:g0 + TH], AF.Identity, scale=psc[:, c:c + 1])
        xs = [g1[0], g1[1]]
        chs = [S.new_chan(), S.new_chan()]
        for i in range(8):
            wt = ws.get(u_o[i])
            for sub in range(2):
                oc = i * 2 + sub
                g0 = GR[oc % 2]
                xst = xs[oc % 2]
                S.dma("sp", xst[:, 0:T + 2], x0T[oc * 128:(oc + 1) * 128, 126:TH], chs[oc % 2])
                for kc in range(NCH):
                    src = u[:, kc, :] if kc < 8 else pp[:, kc - 8, :]
                    lw = wt[:, kc, sub * 128:(sub + 1) * 128]
                    S.mm(ps[:, g0 + 510:g0 + 512], lw, src[:, 126:128], start=(kc == 0), stop=(kc == NCH - 1))
                    S.mm(ps[:, g0 + 512:g0 + 1024], lw, src[:, 128:640], start=(kc == 0), stop=(kc == NCH - 1))
                    S.mm(ps[:, g0 + 1024:g0 + 1536], lw, src[:, 640:1152], start=(kc == 0), stop=(kc == NCH - 1))
                S.stt(zf[:, oc, :], xst[:, 0:T + 2], ALPHA, ps[:, g0 + 510:g0 + 1536], ALU.mult, ALU.add)
            ws.release(u_o[i])
        emit_ln(S, zf, 0, T + 2, gb, ones, tmps, ps, [lambda c: zf[:, c, :]])
        S.ts("dve", zf[:, :, 0:2], zf[:, :, 0:2], flag[:, 0:1], None, ALU.mult)
        cho = [S.new_chan() for _ in range(4)]
        for c in range(NCH):
            S.dma("sp", x1T[c * 128:(c + 1) * 128, :], zf[:, c, :], cho[c % 4])
        S.emit(nc, es, final_chans=cho)
    return nc, S


def prep_mix0_inputs(x, ev_w_in, ev_ln_v_g, ev_ln_v_b, ev_w_s, ev_b_s, ev_w_pool, ev_pool_scale, ev_w_out,
                     ln1_g, ln1_b):
    x = np.asarray(x, np.float32)
    ws = np.asarray(ev_w_s[0], np.float32)
    wsT = np.ascontiguousarray(ws.transpose(2, 0, 1)).reshape(128, 8 * 128)
    tt_ = np.arange(128)
    maskA = (tt_[None, :] >= tt_[:, None]).astype(np.float32)
    bsT = np.ascontiguousarray(np.broadcast_to(np.asarray(ev_b_s[0], np.float32).reshape(1, 8 * 128), (128, 8 * 128)))
    lnv = np.ascontiguousarray(np.broadcast_to(
        np.concatenate([np.asarray(ev_ln_v_g[0], np.float32), np.asarray(ev_ln_v_b[0], np.float32)])[None, :],
        (128, 2048)))
    wp = np.ascontiguousarray(np.asarray(ev_w_pool[0], np.float32).reshape(4 * 256, 256))
    psc = _pm(ev_pool_scale[0], 8)
    gb = np.ascontiguousarray(np.stack([_pm(ln1_g[0], 16), _pm(ln1_b[0], 16)], axis=-1).reshape(128, 32))
    common = {"w_in": np.ascontiguousarray(ev_w_in[0], np.float32),
              "w_out": np.ascontiguousarray(ev_w_out[0], np.float32),
              "wsT": wsT, "maskA": maskA, "bsT": bsT, "lnv": lnv, "w_pool": wp, "pscale": psc, "ln1gb": gb}
    in_maps = []
    for c in range(NCORES):
        b, s = divmod(c, 4)
        t0 = s * T
        xt = np.zeros((D, TH), np.float32)
        xt[:, 128:] = x[b, t0:t0 + T].T
        if s > 0:
            xt[:, 0:128] = x[b, t0 - 128:t0].T
        rc = np.zeros((4, 16), np.float32)
        for g, win in enumerate(B_WINDOWS):
            pos = np.arange(t0 + 1, t0 + 17, dtype=np.float32)
            rc[g] = 1.0 / np.minimum(pos, float(win))
        rcb = np.ascontiguousarray(np.broadcast_to(rc.reshape(1, 64), (128, 64)))
        m = dict(common)
        m.update({"x0T": xt, "rcnt": rcb, "flag": np.full((128, 1), 1.0 if s > 0 else 0.0, np.float32)})
        in_maps.append(m)
    return in_maps


def run_mix0_launch(inputs):
    nc, S = build_mix0_launch()
    in_maps = prep_mix0_inputs(inputs["x"], inputs["ev_w_in"], inputs["ev_ln_v_g"], inputs["ev_ln_v_b"],
                               inputs["ev_w_s"], inputs["ev_b_s"], inputs["ev_w_pool"], inputs["ev_pool_scale"],
                               inputs["ev_w_out"], inputs["ln1_g"], inputs["ln1_b"])
    res = run_bass_kernel_spmd(nc, in_maps, core_ids=list(range(NCORES)))
    return [res.results[c]["x1T"] for c in range(NCORES)]


NH = 16
CH = 64
NCK = T // CH


def hgrn_consts(S, nc, es, lbp_d, mask2_d, ident_d, chp):
    c = {}
    lbp = es.enter_context(nc.sbuf_tensor("lbp_s", [128, 2, NH], F32))
    c["lb"] = es.enter_context(nc.sbuf_tensor("lb", [128, NH], F32))
    c["oml"] = es.enter_context(nc.sbuf_tensor("oml", [128, NH], F32))
    c["rm"] = es.enter_context(nc.sbuf_tensor("rm", [128, T], F32))
    c["mask2"] = es.enter_context(nc.sbuf_tensor("mask2_s", [128, 128], F32))
    identf = es.enter_context(nc.sbuf_tensor("identf", [128, 128], F32))
    c["ident"] = es.enter_context(nc.sbuf_tensor("ident_s", [128, 128], BF16))
    S.dma("sp", lbp[:, :, :], lbp_d.rearrange("p (l h) -> p l h", l=2), chp)
    S.dma("sp", c["mask2"][:, :], mask2_d, chp)
    S.dma("sp", identf[:, :], ident_d, chp)
    S.copy("dve", c["ident"][:, :], identf[:, :])
    S.tt("dve", c["lb"][:, :], lbp[:, 1, :], lbp[:, 0, :], ALU.subtract)
    S.act(c["lb"][:, :], c["lb"][:, :], AF.Sigmoid)
    S.ts("dve", c["oml"][:, :], c["lb"][:, :], -1.0, 1.0, ALU.mult, ALU.add)
    S.memset("dve", c["rm"][:, :], 1.0)
    S.memset("dve", c["rm"][:, :].rearrange("p (c t) -> p c t", t=CH)[:, :, 0:1], 0.0)
    c["pm"] = es.enter_context(nc.sbuf_tensor("pm", [128, 2], F32))
    S.memset("dve", c["pm"][:, :], 0.0)
    S.memset("dve", c["pm"][0:64, 0:1], 1.0)
    S.memset("dve", c["pm"][64:128, 1:2], 1.0)
    return c


def hgrn_gates(S, h, cst, f_ps, A, B, C, kend_bf, dec, kdec_bf=None, eC=None):
    S.act(A, f_ps, AF.Sigmoid)
    S.ts("dve", A, A, cst["oml"][:, h:h + 1], cst["lb"][:, h:h + 1], ALU.mult, ALU.add)
    S.act(B, A, AF.Ln)
    S.add("dve", lambda e: e.tensor_tensor_scan(out=C, data0=cst["rm"][:, :], data1=B, initial=0.0,
                                                op0=ALU.mult, op1=ALU.add),
          reads=[cst["rm"][:, :], B], writes=[C])
    S.ts("dve", A, A, -1.0, 1.0, ALU.mult, ALU.add)
    S.act(B, C, AF.Exp, scale=-1.0)
    S.tt("dve", B, A, B, ALU.mult)
    if kdec_bf is not None:
        S.act(kdec_bf, B, AF.Identity)
    C3 = C.rearrange("p (c t) -> p c t", t=CH)
    S.act(dec.rearrange("p (c o) -> p c o", o=1), C3[:, :, CH - 1:CH], AF.Exp)
    S.tt("dve", kend_bf.rearrange("p (c t) -> p c t", t=CH), B.rearrange("p (c t) -> p c t", t=CH),
         dec.rearrange("p (c o) -> p c o", o=1).to_broadcast([128, NCK, CH]), ALU.mult)
    if eC is not None:
        S.act(eC, C, AF.Exp)


def hgrn_transposes(S, cst, psb, src_bf, dst_tok, dst_tok1=None):
    for j in range(8):
        S.transpose(psb[:, 4096 + j * 128:4096 + (j + 1) * 128], src_bf[:, j * 128:(j + 1) * 128], cst["ident"][:, :])
    if dst_tok1 is None:
        S.act(dst_tok.rearrange("p j d -> p (j d)"), psb[:, 4096:5120], AF.Identity)
    else:
        S.act(dst_tok.rearrange("p j d -> p (j d)"), psb[:, 4096:5120], AF.Identity, scale=cst["pm"][:, 0:1])
        S.act(dst_tok1.rearrange("p j d -> p (j d)"), psb[:, 4096:5120], AF.Identity, scale=cst["pm"][:, 1:2])


def hgrn_state_scan(S, ps, kt, vtk, dec, Sf, Sb=None):
    cur = 0
    if Sb is not None:
        S.act(Sb[:, 0, :], Sf[0][:, :], AF.Identity)
    for g4 in range(4):
        for cc in range(4):
            c = g4 * 4 + cc
            j, par = divmod(c, 2)
            S.mm(ps[:, 2560 + cc * 128:2560 + (cc + 1) * 128], kt[par][:, j, :], vtk[:, j, :], start=True, stop=True)
        for cc in range(4):
            c = g4 * 4 + cc
            nxt = 1 - cur
            S.stt(Sf[nxt][:, :], Sf[cur][:, :], dec[:, c:c + 1], ps[:, 2560 + cc * 128:2560 + (cc + 1) * 128],
                  ALU.mult, ALU.add)
            cur = nxt
            if Sb is not None and c + 1 < NCK:
                S.act(Sb[:, c + 1, :], Sf[cur][:, :], AF.Identity)
    return cur


def build_hgrn_pre_launch(stage=9, nheads=NH):
    nc = bass.Bass("TRN2", target_bir_lowering=False)
    xT = nc.dram_tensor("xT", [D, T], F32, kind="ExternalInput").ap()
    w_in = nc.dram_tensor("w_in", [D, 4 * D], F32, kind="ExternalInput").ap()
    lbp_d = nc.dram_tensor("lbp", [128, 2 * NH], F32, kind="ExternalInput").ap()
    mask2_d = nc.dram_tensor("mask2", [128, 128], F32, kind="ExternalInput").ap()
    ident_d = nc.dram_tensor("ident", [128, 128], F32, kind="ExternalInput").ap()
    U_d = nc.dram_tensor("U", [NH, 128, 128], F32, kind="ExternalOutput").ap()
    D_d = nc.dram_tensor("Dd", [128, NH], F32, kind="ExternalOutput").ap()
    S = Sched()
    with ExitStack() as es:
        xb = es.enter_context(nc.sbuf_tensor("xb", [128, NCH, T], BF16))
        A = [es.enter_context(nc.sbuf_tensor("A%d" % i, [128, T], F32)) for i in range(2)]
        B = [es.enter_context(nc.sbuf_tensor("B%d" % i, [128, T], F32)) for i in range(2)]
        C = [es.enter_context(nc.sbuf_tensor("C%d" % i, [128, T], F32)) for i in range(2)]
        kend = [es.enter_context(nc.sbuf_tensor("kend%d" % i, [128, T], BF16)) for i in range(2)]
        ibf = [es.enter_context(nc.sbuf_tensor("ibf%d" % i, [128, T], BF16)) for i in range(2)]
        kt = [[es.enter_context(nc.sbuf_tensor("kt%d_%d" % (i, k), [128, 8, 128], BF16)) for k in range(2)]
              for i in range(2)]
        vtk = [es.enter_context(nc.sbuf_tensor("vtk%d" % i, [128, 8, 128], BF16)) for i in range(2)]
        dec = [es.enter_context(nc.sbuf_tensor("dec%d" % i, [128, NCK], F32)) for i in range(2)]
        Sf = [[es.enter_context(nc.sbuf_tensor("Sf%d_%d" % (i, k), [128, 128], F32)) for k in range(2)]
              for i in range(2)]
        Dd = es.enter_context(nc.sbuf_tensor("Dd_s", [128, NH], F32))
        ps = es.enter_context(nc.psum_tensor("ps", [128, 4096], F32))
        psb = ps[:, :].bitcast(BF16)
        chp = S.new_chan(total=True)
        chx = S.new_chan(total=True)
        cst = hgrn_consts(S, nc, es, lbp_d, mask2_d, ident_d, chp)
        ws = WeightStream(S, nc, es, 4, 16 * 256)
        wi = w_in.rearrange("(kc p) n -> p kc n", p=128)
        uses = [ws.plan((16, 256), wi[:, :, h * 512 + 256:h * 512 + 512]) for h in range(NH)]
        for c in range(NCH):
            S.dma("pool", xb[:, c, :], xT[c * 128:(c + 1) * 128, :], chx)
        ws.release(-1)
        cho = [S.new_chan() for _ in range(2)]
        for h in range(nheads):
            b = h % 2
            wt = ws.get(uses[h])
            for which in range(2):
                g0 = which * 1024
                for kc in range(NCH):
                    for t_ in range(2):
                        S.mm(ps[:, g0 + t_ * 512:g0 + (t_ + 1) * 512], wt[:, kc, which * 128:(which + 1) * 128],
                             xb[:, kc, t_ * 512:(t_ + 1) * 512], start=(kc == 0), stop=(kc == NCH - 1))
            ws.release(uses[h])
            hgrn_gates(S, h, cst, ps[:, 0:1024], A[b][:, :], B[b][:, :], C[b][:, :], kend[b][:, :], dec[b][:, :])
            S.act(ibf[b][:, :], ps[:, 1024:2048], AF.Identity)
            C3 = C[b][:, :].rearrange("p (c t) -> p c t", t=CH)
            S.add("dve", lambda e, C3=C3, h=h: e.reduce_sum(out=Dd[:, h:h + 1], in_=C3[:, :, CH - 1:CH],
                                                           axis=mybir.AxisListType.XY),
                  reads=[C[b][:, :]], writes=[Dd[:, h:h + 1]])
            S.act(Dd[:, h:h + 1], Dd[:, h:h + 1], AF.Exp)
            if stage >= 2:
                hgrn_transposes(S, cst, psb, kend[b][:, :], kt[b][0][:, :, :], kt[b][1][:, :, :])
                hgrn_transposes(S, cst, psb, ibf[b][:, :], vtk[b][:, :, :])
            S.memset("dve", Sf[b][0][:, :], 0.0)
            fin = 0
            if stage >= 3:
                fin = hgrn_state_scan(S, ps, kt[b], vtk[b], dec[b], Sf[b])
            S.dma("sp", U_d[h, :, :], Sf[b][fin][:, :], cho[b])
        chd = S.new_chan()
        S.dma("sp", D_d, Dd[:, :], chd)
        S.emit(nc, es, final_chans=cho + [chd])
    return nc, S


def regroup_w_in(od_w_in):
    w = np.asarray(od_w_in, np.float32).reshape(D, 4, NH, 128)
    w = w[:, [0, 3, 1, 2]]
    return np.ascontiguousarray(w.transpose(0, 2, 1, 3).reshape(D, 4 * D))


def hgrn_const_inputs(lb_param):
    lbp = np.ascontiguousarray(np.stack([_pm(lb_param[0], NH), _pm(lb_param[1], NH)], axis=1).reshape(128, 2 * NH))
    i = np.arange(128)
    mask2 = ((i[None, :] >= i[:, None]) & ((i[None, :] // CH) == (i[:, None] // CH))).astype(np.float32)
    return {"lbp": lbp, "mask2": mask2, "ident": np.eye(128, dtype=np.float32)}


def _carve(arena, off_bytes, n_elems, dt):
    assert off_bytes % 4 == 0
    nb = n_elems * _dsize(dt)
    assert nb % 4 == 0
    v = arena[:, off_bytes // 4:(off_bytes + nb) // 4]
    return v if dt == F32 else v.bitcast(dt)


def build_hgrn_main_launch():
    nc = bass.Bass("TRN2", target_bir_lowering=False)
    xT = nc.dram_tensor("xT", [D, T], F32, kind="ExternalInput").ap()
    w_in = nc.dram_tensor("w_in", [D, 4 * D], F32, kind="ExternalInput").ap()
    w_out = nc.dram_tensor("w_out", [D, D], F32, kind="ExternalInput").ap()
    lbp_d = nc.dram_tensor("lbp", [128, 2 * NH], F32, kind="ExternalInput").ap()
    mask2_d = nc.dram_tensor("mask2", [128, 128], F32, kind="ExternalInput").ap()
    ident_d = nc.dram_tensor("ident", [128, 128], F32, kind="ExternalInput").ap()
    up_d = nc.dram_tensor("Uprev", [3, NH, 128, 128], F32, kind="ExternalInput").ap()
    dp_d = nc.dram_tensor("Dprev", [128, 3 * NH], F32, kind="ExternalInput").ap()
    gn_d = nc.dram_tensor("gn", [128, NH], F32, kind="ExternalInput").ap()
    gb_d = nc.dram_tensor("ln1gb", [128, 32], F32, kind="ExternalInput").ap()
    x1T = nc.dram_tensor("x1T", [D, T], F32, kind="ExternalOutput").ap()
    S = Sched()
    with ExitStack() as es:
        arena = es.enter_context(nc.sbuf_tensor("arena", [128, NCH * T], F32))
        zf = arena[:, :].rearrange("p (c t) -> p c t", c=NCH, t=T)
        xb = _carve(arena, 0, NCH * T, BF16).rearrange("p (c t) -> p c t", c=NCH, t=T)
        off = NCH * T * 2
        A = _carve(arena, off, T, F32); off += 4 * T
        B = _carve(arena, off, T, F32); off += 4 * T
        C = _carve(arena, off, T, F32); off += 4 * T
        kend = _carve(arena, off, T, BF16); off += 2 * T
        kdec = _carve(arena, off, T, BF16); off += 2 * T
        qdec = _carve(arena, off, T, BF16); off += 2 * T
        ibf = _carve(arena, off, T, BF16); off += 2 * T
        sg = _carve(arena, off, T, BF16); off += 2 * T
        osq = _carve(arena, off, T, BF16); off += 2 * T
        attm = _carve(arena, off, T, BF16).rearrange("p (j t) -> p j t", j=8); off += 2 * T
        kt0 = _carve(arena, off, T, BF16).rearrange("p (j t) -> p j t", j=8); off += 2 * T
        kt1 = _carve(arena, off, T, BF16).rearrange("p (j t) -> p j t", j=8); off += 2 * T
        kt = (kt0, kt1)
        vtk = _carve(arena, off, T, BF16).rearrange("p (j t) -> p j t", j=8); off += 2 * T
        assert off <= NCH * T * 4
        y = es.enter_context(nc.sbuf_tensor("y", [128, NH, T], BF16))
        Sb = es.enter_context(nc.sbuf_tensor("Sb", [128, NCK, 128], BF16))
        Sf = [es.enter_context(nc.sbuf_tensor("Sf%d" % k, [128, 128], F32)) for k in range(2)]
        upst = es.enter_context(nc.sbuf_tensor("upst", [128, 3, 128], F32))
        dec = es.enter_context(nc.sbuf_tensor("dec", [128, NCK], F32))
        dp = es.enter_context(nc.sbuf_tensor("dp", [128, 3, NH], F32))
        gn = es.enter_context(nc.sbuf_tensor("gn_s", [128, NH], F32))
        gb = es.enter_context(nc.sbuf_tensor("gb_s", [128, NCH, 2], F32))
        ones = es.enter_context(nc.sbuf_tensor("ones", [128, 128], BF16))
        eps = es.enter_context(nc.sbuf_tensor("eps", [128, 1], F32))
        xs = [es.enter_context(nc.sbuf_tensor("xs%d" % i, [128, T], F32)) for i in range(2)]
        tmps = {"eps": eps,
                "zb": [es.enter_context(nc.sbuf_tensor("zb%d" % i, [128, T], BF16)) for i in range(2)],
                "zs": [es.enter_context(nc.sbuf_tensor("zs%d" % i, [128, T], BF16)) for i in range(2)],
                "mean": xs[0], "rstd": xs[1]}
        ps = es.enter_context(nc.psum_tensor("ps", [128, 4096], F32))
        psb = ps[:, :].bitcast(BF16)
        chp = S.new_chan(total=True)
        chx = S.new_chan(total=True)
        cst = hgrn_consts(S, nc, es, lbp_d, mask2_d, ident_d, chp)
        S.dma("sp", dp[:, :, :], dp_d.rearrange("p (j h) -> p j h", j=3), chp)
        S.dma("sp", gn[:, :], gn_d, chp)
        S.dma("sp", gb[:, :, :], gb_d.rearrange("p (c j) -> p c j", j=2), chp)
        S.memset("dve", ones[:, :], 1.0)
        S.memset("dve", eps[:, :], LN_EPS)
        ws = WeightStream(S, nc, es, 3, 16 * 512)
        wi = w_in.rearrange("(kc p) n -> p kc n", p=128)
        wo = w_out.rearrange("(kc p) n -> p kc n", p=128)
        uses = [ws.plan((16, 512), wi[:, :, h * 512:(h + 1) * 512]) for h in range(NH)]
        u_o = [ws.plan((16, 512), wo[:, :, i * 512:(i + 1) * 512]) for i in range(4)]
        for c in range(NCH):
            S.dma("pool", xb[:, c, :], xT[c * 128:(c + 1) * 128, :], chx)
        ws.release(-1)
        chu = S.new_chan()
        G0, G1 = 0, 1024
        for h in range(NH):
            wt = ws.get(uses[h])

            def proj(blk, g0):
                for kc in range(NCH):
                    for t_ in range(2):
                        S.mm(ps[:, g0 + t_ * 512:g0 + (t_ + 1) * 512], wt[:, kc, blk * 128:(blk + 1) * 128],
                             xb[:, kc, t_ * 512:(t_ + 1) * 512], start=(kc == 0), stop=(kc == NCH - 1))
            S.dma("sp", upst[:, :, :], up_d[:, h, :, :].rearrange("j d e -> d j e"), chu)
            S.memset("dve", Sf[0][:, :], 0.0)
            cur = 0
            for j in range(3):
                S.stt(Sf[1 - cur][:, :], Sf[cur][:, :], dp[:, j, h:h + 1], upst[:, j, :], ALU.mult, ALU.add)
                cur = 1 - cur
            Sfl = [Sf[cur], Sf[1 - cur]]
            proj(2, G0)
            proj(3, G1)
            hgrn_gates(S, h, cst, ps[:, G0:G0 + T], A, B, C, kend, dec[:, :], kdec_bf=kdec, eC=A)
            S.act(ibf, ps[:, G1:G1 + T], AF.Identity)
            proj(0, G0)
            proj(1, G1)
            ws.release(uses[h])
            S.act(B, ps[:, G0:G0 + T], AF.Silu)
            S.tt("dve", qdec, B, A, ALU.mult)
            S.act(sg, ps[:, G1:G1 + T], AF.Sigmoid)
            hgrn_transposes(S, cst, psb, kend, kt0, kt1)
            hgrn_transposes(S, cst, psb, ibf, vtk)
            hgrn_state_scan(S, ps, kt, vtk, dec, Sfl, Sb=Sb)
            for half in range(2):
                for jj in range(4):
                    j = half * 4 + jj
                    S.mm(ps[:, 2560 + jj * 128:2560 + (jj + 1) * 128], kdec[:, j * 128:(j + 1) * 128],
                         qdec[:, j * 128:(j + 1) * 128], start=True, stop=True)
                for jj in range(4):
                    j = half * 4 + jj
                    S.tt("dve", attm[:, j, :], ps[:, 2560 + jj * 128:2560 + (jj + 1) * 128], cst["mask2"][:, :],
                         ALU.mult)
            O0 = 3072
            for j in range(8):
                S.mm(ps[:, O0 + j * 128:O0 + (j + 1) * 128], vtk[:, j, :], attm[:, j, :], start=True, stop=False)
                S.mm(ps[:, O0 + j * 128:O0 + j * 128 + 64], Sb[:, 2 * j, :], qdec[:, j * 128:j * 128 + 64],
                     start=False, stop=False)
                S.mm(ps[:, O0 + j * 128 + 64:O0 + (j + 1) * 128], Sb[:, 2 * j + 1, :],
                     qdec[:, j * 128 + 64:(j + 1) * 128], start=False, stop=True)
            S.act(osq, ps[:, O0:O0 + T], AF.Square)
            for t_ in range(2):
                S.mm(ps[:, G0 + t_ * 512:G0 + (t_ + 1) * 512], ones[:, :], osq[:, t_ * 512:(t_ + 1) * 512],
                     start=True, stop=True)
            S.act(A, ps[:, G0:G0 + T], AF.Ln, bias=eps[:, 0:1], scale=1.0 / 128)
            S.act(A, A, AF.Exp, scale=-0.5)
            S.stt(C, ps[:, O0:O0 + T], gn[:, h:h + 1], A, ALU.mult, ALU.mult)
            S.tt("dve", y[:, h, :], C, sg, ALU.mult)
        chs = [S.new_chan(), S.new_chan()]
        for i in range(4):
            wt = ws.get(u_o[i])
            for sub in range(4):
                oc = i * 4 + sub
                g0 = (oc % 2) * 1024
                xst = xs[oc % 2]
                S.dma("sp", xst[:, :], xT[oc * 128:(oc + 1) * 128, :], chs[oc % 2])
                for kc in range(NCH):
                    for t_ in range(2):
                        S.mm(ps[:, g0 + t_ * 512:g0 + (t_ + 1) * 512], wt[:, kc, sub * 128:(sub + 1) * 128],
                             y[:, kc, t_ * 512:(t_ + 1) * 512], start=(kc == 0), stop=(kc == NCH - 1))
                S.stt(zf[:, oc, :], xst[:, :], ALPHA, ps[:, g0:g0 + T], ALU.mult, ALU.add)
            ws.release(u_o[i])
        emit_ln(S, zf, 0, T, gb, ones, tmps, ps, [lambda c: zf[:, c, :]])
        cho = [S.new_chan() for _ in range(4)]
        for c in range(NCH):
            S.dma("sp", x1T[c * 128:(c + 1) * 128, :], zf[:, c, :], cho[c % 4])
        S.emit(nc, es, final_chans=cho)
    return nc, S


def _run(nc, in_maps):
    return run_bass_kernel_spmd(nc, in_maps, core_ids=list(range(NCORES))).results


def kernel_unfused(x, ev_w_in, ev_ln_v_g, ev_ln_v_b, ev_w_s, ev_b_s, ev_w_pool, ev_pool_scale,
           ev_w_out, od_w_in, od_norm_g, od_w_out, lb_param, ffn_w_up, ffn_conv_w,
           ffn_conv_b, ffn_w_down, ln1_g, ln1_b, ln2_g, ln2_b):
    f32 = np.float32
    nc0, _ = build_mix0_launch()
    maps0 = prep_mix0_inputs(x, ev_w_in, ev_ln_v_g, ev_ln_v_b, ev_w_s, ev_b_s, ev_w_pool, ev_pool_scale,
                             ev_w_out, ln1_g, ln1_b)
    r0 = _run(nc0, maps0)
    x1T = [r0[c]["x1T"] for c in range(NCORES)]
    ncf, _ = build_ffn_launch()

    def ffn(l, xTs):
        cwb, gb = prep_ffn_params(l, ffn_conv_w, ffn_conv_b, ln2_g, ln2_b)
        wu = np.ascontiguousarray(ffn_w_up[l], f32)
        wd = np.ascontiguousarray(ffn_w_down[l], f32)
        maps = [{"xT": np.ascontiguousarray(xTs[c], f32), "w_up": wu, "w_down": wd, "cwb": cwb, "ln2gb": gb}
                for c in range(NCORES)]
        r = _run(ncf, maps)
        return [r[c]["yT"] for c in range(NCORES)]

    x2T = ffn(0, x1T)
    ncp, _ = build_hgrn_pre_launch()
    w_in_r = regroup_w_in(od_w_in[0])
    hc = hgrn_const_inputs(lb_param)
    mapsp = []
    for c in range(NCORES):
        m = {"xT": np.ascontiguousarray(x2T[c], f32), "w_in": w_in_r}
        m.update(hc)
        mapsp.append(m)
    rp = _run(ncp, mapsp)
    ncm, _ = build_hgrn_main_launch()
    gn = _pm(od_norm_g[0], NH)
    gb1 = np.ascontiguousarray(np.stack([_pm(ln1_g[1], 16), _pm(ln1_b[1], 16)], axis=-1).reshape(128, 32))
    w_out1 = np.ascontiguousarray(od_w_out[0], f32)
    mapsm = []
    for c in range(NCORES):
        b, s = divmod(c, 4)
        up = np.zeros((3, NH, 128, 128), f32)
        dp = np.zeros((128, 3, NH), f32)
        for j in range(s):
            pos = 3 - s + j
            up[pos] = rp[b * 4 + j]["U"]
            dp[:, pos, :] = rp[b * 4 + j]["Dd"]
        m = {"xT": np.ascontiguousarray(x2T[c], f32), "w_in": w_in_r, "w_out": w_out1, "Uprev": up,
             "Dprev": np.ascontiguousarray(dp.reshape(128, 3 * NH)), "gn": gn, "ln1gb": gb1}
        m.update(hc)
        mapsm.append(m)
    rm = _run(ncm, mapsm)
    x1bT = []
    for c in range(NCORES):
        b, s = divmod(c, 4)
        xt = np.zeros((D, T + 2), f32)
        xt[:, 2:] = rm[c]["x1T"]
        if s > 0:
            xt[:, 0:2] = rm[c - 1]["x1T"][:, T - 2:T]
        x1bT.append(xt)
    outT = ffn(1, x1bT)
    out = np.zeros((2, 4 * T, D), f32)
    for c in range(NCORES):
        b, s = divmod(c, 4)
        out[b, s * T:(s + 1) * T] = outT[c].T
    return out


R0 = 0
R0_SZ = NCH * (T + 2) * 4
R1 = R0 + R0_SZ
R1_SZ = NCH * (T + 2) * 2
R2 = R1 + R1_SZ
R2_SZ = 66560
AR_BYTES = R2 + R2_SZ
SEQ_GROUPS = [[0, 1, 2, 3], [4, 5, 6, 7]]


def build_fused(use_cc=True):
    nc = bass.Bass("TRN2", target_bir_lowering=False)

    def din(name, shape):
        return nc.dram_tensor(name, shape, F32, kind="ExternalInput").ap()
    x0T = din("x0T", [D, TH])
    ev_w_in = din("ev_w_in", [D, 3072])
    ev_w_out = din("ev_w_out", [D, D])
    wsT_d = din("wsT", [128, 8 * 128])
    mask_d = din("maskA", [128, 128])
    bsT_d = din("bsT", [128, 8 * 128])
    lnv_d = din("lnv", [128, 2 * 1024])
    wp_d = din("w_pool", [4 * 256, 256])
    psc_d = din("pscale", [128, 8])
    rc_d = din("rcnt", [128, 64])
    flag_d = din("flag", [128, 1])
    oh_d = din("oh", [128, 8])
    ln1gb_d = din("ln1gb", [128, 64])
    ln2gb_d = din("ln2gb", [128, 64])
    cwb_d = din("cwb", [128, 2 * 88 * 4])
    w_up = [din("w_up%d" % l, [D, 2 * DFF]) for l in range(2)]
    w_down = [din("w_down%d" % l, [DFF, D]) for l in range(2)]
    od_w_in = din("od_w_in", [D, 4 * D])
    od_w_out = din("od_w_out", [D, D])
    lbp_d = din("lbp", [128, 2 * NH])
    mask2_d = din("mask2", [128, 128])
    ident_d = din("ident", [128, 128])
    gn_d = din("gn", [128, NH])
    outT = nc.dram_tensor("outT", [D, T], F32, kind="ExternalOutput").ap()
    xsp = nc.dram_tensor("xsp", [D, T], F32).ap()
    ccin = [nc.dram_tensor("ccin%d" % g, [4 * 4 * 128, 129], F32) for g in range(4)]
    ccout = [nc.dram_tensor("ccout%d" % g, [4 * 4 * 128, 129], F32) for g in range(4)]
    cch_in = nc.dram_tensor("cch_in", [4 * 128, 32], F32)
    cch_out = nc.dram_tensor("cch_out", [4 * 128, 32], F32)

    S = Sched()
    with ExitStack() as es:
        AR = es.enter_context(nc.sbuf_tensor("AR", [128, AR_BYTES // 4], F32))

        def cv(off, shape, dt):
            n = 1
            for k in shape:
                n *= k
            v = _carve(AR, off, n, dt)
            if len(shape) == 2:
                v = v.rearrange("p (a b) -> p a b", a=shape[0], b=shape[1])
            return v

        def sb(name, shape, dt=F32):
            return es.enter_context(nc.sbuf_tensor(name, shape, dt))
        psc = sb("psc_s", [128, 8]); rc = sb("rc_s", [128, 4, 16]); flag = sb("flag_s", [128, 1])
        oh = sb("oh_s", [128, 8]); ln1gb = sb("ln1gb_s", [128, 2, NCH, 2]); ln2gb = sb("ln2gb_s", [128, 2, NCH, 2])
        cwb = sb("cwb_s", [128, 2, 88, 4]); ones = sb("ones", [128, 128], BF16); eps = sb("eps", [128, 1])
        st = sb("st", [128, 8]); small = sb("small", [128, 32]); gn = sb("gn_s", [128, NH])
        dec = sb("dec", [128, NCK]); Dd = sb("Dd_s", [128, NH]); tiny = sb("tiny", [128, 8])
        Sf = [sb("Sf%d" % k, [128, 128]) for k in range(2)]
        Pp = [sb("Pp%d" % k, [128, 128]) for k in range(2)]
        hstg = sb("hstg", [128, 4, 32]); hld = sb("hld", [128, 4, 32]); hsum = sb("hsum", [128, 32])
        ps = es.enter_context(nc.psum_tensor("ps", [128, 4096], F32))
        psb = ps[:, :].bitcast(BF16)
        ccsem = [es.enter_context(nc.semaphore("ccs%d" % i)) for i in range(5)]
        ws = WeightStream(S, nc, es, 4, 16 * 256)

        wi0 = ev_w_in.rearrange("(kc p) n -> p kc n", p=128)
        wo0 = ev_w_out.rearrange("(kc p) n -> p kc n", p=128)
        u_xb = [ws.plan((16, 256), wi0[:, :, 2048 + i * 256:2048 + (i + 1) * 256]) for i in range(4)]
        u_u = [ws.plan((16, 256), wi0[:, :, i * 256:(i + 1) * 256]) for i in range(4)]
        u_v = [ws.plan((16, 256), wi0[:, :, 1024 + i * 256:1024 + (i + 1) * 256]) for i in range(4)]
        u_o = [ws.plan((16, 256), wo0[:, :, i * 256:(i + 1) * 256]) for i in range(8)]
        plan0 = plan_ffn_weights(ws, w_up[0], w_down[0])
        wi1 = od_w_in.rearrange("(kc p) n -> p kc n", p=128)
        wo1 = od_w_out.rearrange("(kc p) n -> p kc n", p=128)
        u_pre = [ws.plan((16, 256), wi1[:, :, h * 512 + 256:h * 512 + 512]) for h in range(NH)]
        u_main = []
        for h in range(NH):
            fi = ws.plan((16, 256), wi1[:, :, h * 512 + 256:h * 512 + 512])
            qg = ws.plan((16, 256), wi1[:, :, h * 512:h * 512 + 256])
            u_main.append((fi, qg))
        u_o1 = [ws.plan((16, 256), wo1[:, :, i * 256:(i + 1) * 256]) for i in range(8)]
        plan1 = plan_ffn_weights(ws, w_up[1], w_down[1])

        x0b = cv(R0, (NCH, TH), BF16)
        zf = cv(R0, (NCH, T + 2), F32)
        xb = cv(R1, (NCH, T + 2), BF16)
        pp = cv(R1, (8, TH), BF16)
        g1 = [cv(R1 + 18432, (TH,), F32), cv(R1 + 23040, (TH,), F32)]
        xbf = cv(R1 + 27648, (16 + TH,), F32)
        u = cv(R2, (8, TH), BF16)
        vt = cv(R2 + 18432, (9, 1024), BF16)
        g2p = [cv(R2 + 36864, (16 + TH,), F32), cv(R2 + 41536, (16 + TH,), F32)]
        g2 = [t[:, 16:16 + TH] for t in g2p]
        tA, tB = g2p
        wsT = cv(R2 + 46208, (8, 128), BF16)
        mask = cv(R2 + 48256, (128,), F32)
        bsT = cv(R2 + 48768, (8, 128), F32)
        lnv = cv(R2 + 52864, (2, 1024), F32)
        wp = cv(R2 + 61056, (8, 256), BF16)
        wsTf = g1[1][:, 0:1024].rearrange("p (h t) -> p h t", h=8)

        chp = S.new_chan(total=True)
        chx = S.new_chan(total=True)
        S.dma("sp", wsTf, wsT_d.rearrange("p (h t) -> p h t", h=8), chp)
        S.dma("sp", mask, mask_d, chp)
        S.dma("sp", bsT, bsT_d.rearrange("p (h t) -> p h t", h=8), chp)
        S.dma("sp", lnv, lnv_d.rearrange("p (a c) -> p a c", a=2), chp)
        S.dma("sp", psc[:, :], psc_d, chp)
        S.dma("sp", rc[:, :, :], rc_d.rearrange("p (g j) -> p g j", g=4), chp)
        S.dma("sp", flag[:, :], flag_d, chp)
        S.dma("sp", oh[:, :], oh_d, chp)
        S.dma("sp", ln1gb[:, :, :, :], ln1gb_d.rearrange("p (l c j) -> p l c j", l=2, j=2), chp)
        S.dma("sp", ln2gb[:, :, :, :], ln2gb_d.rearrange("p (l c j) -> p l c j", l=2, j=2), chp)
        S.dma("sp", cwb[:, :, :, :], cwb_d.rearrange("p (l c j) -> p l c j", l=2, j=4), chp)
        S.dma("sp", gn[:, :], gn_d, chp)
        S.dma("pool", wp, wp_d.rearrange("(a p) n -> p a n", p=128), chx)
        for c in range(NCH):
            S.dma("pool", x0b[:, c, :], x0T[c * 128:(c + 1) * 128, :], chx)
        ws.release(-1)
        S.memset("dve", ones[:, :], 1.0)
        S.memset("dve", eps[:, :], LN_EPS)
        S.memset("dve", xbf[:, 0:16], 0.0)
        S.memset("dve", tA[:, 0:16], 0.0)
        S.memset("dve", tB[:, 0:16], 0.0)
        for h in range(8):
            S.tt("dve", wsT[:, h, :], wsTf[:, h, :], mask, ALU.mult)

        GR = (0, 1536)
        TT3 = ((0, 512), (512, 512), (1024, 128))

        def proj_fm(wt, sub, g0):
            for kc in range(NCH):
                for (t0, w) in TT3:
                    S.mm(ps[:, g0 + t0:g0 + t0 + w], wt[:, kc, sub * 128:(sub + 1) * 128], x0b[:, kc, t0:t0 + w],
                         start=(kc == 0), stop=(kc == NCH - 1))
        gi = 0
        for i in range(4):
            wt = ws.get(u_xb[i])
            for sub in range(2):
                c = i * 2 + sub
                g = c // 2
                g0 = GR[gi % 2]
                gi += 1
                proj_fm(wt, sub, g0)
                S.act(xbf[:, 16:16 + TH], ps[:, g0:g0 + TH], AF.Identity)
                src = xbf
                dsts = [tA, tB]
                for k in range(g + 1):
                    sh = 1 << k
                    dst = dsts[k % 2]
                    S.tt("dve", dst[:, 16:16 + TH], src[:, 16:16 + TH], src[:, 16 - sh:16 + TH - sh], ALU.add)
                    src = dst
                win = B_WINDOWS[g]
                S.stt(pp[:, c, :], src[:, 16:16 + TH], 1.0 / win, xbf[:, 16:16 + TH], ALU.mult, ALU.subtract)
                S.tt("dve", small[:, 0:16], src[:, 16 + 128:16 + 144], rc[:, g, :], ALU.mult)
                S.tt("dve", pp[:, c, 128:144], small[:, 0:16], xbf[:, 16 + 128:16 + 144], ALU.subtract)
            ws.release(u_xb[i])
        for i in range(4):
            wt = ws.get(u_u[i])
            for sub in range(2):
                c = i * 2 + sub
                g0 = GR[gi % 2]
                proj_fm(wt, sub, g0)
                emit_gelu(S, u[:, c, :], ps[:, g0:g0 + TH], g1[gi % 2], g2[gi % 2])
                gi += 1
            ws.release(u_u[i])
        wv = [ws.get(k) for k in u_v]
        for tk in range(9):
            vr = g1[tk % 2]
            for cg in range(4):
                r0 = 3072 + ((tk * 4 + cg) % 2) * 512
                for kc in range(NCH):
                    S.mm(ps[:, r0:r0 + 256], x0b[:, kc, tk * 128:(tk + 1) * 128], wv[cg][:, kc, :],
                         start=(kc == 0), stop=(kc == NCH - 1))
                emit_gelu(S, vr[:, cg * 256:(cg + 1) * 256], ps[:, r0:r0 + 256],
                          g2[0][:, cg * 256:(cg + 1) * 256], g2[1][:, cg * 256:(cg + 1) * 256])
            sq = g2[0]
            S.add("dve", lambda e, vr=vr: e.reduce_sum(out=st[:, 0:1], in_=vr[:, 0:1024], axis=mybir.AxisListType.X),
                  reads=[vr[:, 0:1024]], writes=[st[:, 0:1]])
            S.act(sq[:, 0:1024], vr[:, 0:1024], AF.Square)
            S.add("dve", lambda e, sq=sq: e.reduce_sum(out=st[:, 1:2], in_=sq[:, 0:1024], axis=mybir.AxisListType.X),
                  reads=[sq[:, 0:1024]], writes=[st[:, 1:2]])
            S.ts("dve", st[:, 2:3], st[:, 0:1], 1.0 / 1024, None, ALU.mult)
            S.tt("dve", st[:, 3:4], st[:, 2:3], st[:, 2:3], ALU.mult)
            S.stt(st[:, 4:5], st[:, 1:2], 1.0 / 1024, st[:, 3:4], ALU.mult, ALU.subtract)
            S.act(st[:, 5:6], st[:, 4:5], AF.Sqrt, bias=eps[:, 0:1], scale=1.0)
            S.add("dve", lambda e: e.reciprocal(out=st[:, 6:7], in_=st[:, 5:6]), reads=[st[:, 5:6]],
                  writes=[st[:, 6:7]])
            S.ts("dve", vr[:, 0:1024], vr[:, 0:1024], st[:, 2:3], st[:, 6:7], ALU.subtract, ALU.mult)
            S.tt("dve", vr[:, 0:1024], vr[:, 0:1024], lnv[:, 0, :], ALU.mult)
            S.tt("dve", vt[:, tk, :], vr[:, 0:1024], lnv[:, 1, :], ALU.add)
        ws.release(u_v[3])
        for tk in range(9):
            for half in range(2):
                r0 = 2048 + half * 512
                for hh in range(4):
                    h = half * 4 + hh
                    S.mm(ps[:, r0 + hh * 128:r0 + (hh + 1) * 128], vt[:, tk, h * 128:(h + 1) * 128], wsT[:, h, :],
                         start=True, stop=True)
                tmp = g2[half][:, 0:512].rearrange("p (h t) -> p h t", h=4)
                S.tt("dve", tmp, ps[:, r0:r0 + 512].rearrange("p (h t) -> p h t", h=4),
                     bsT[:, half * 4:half * 4 + 4, :], ALU.add)
                uu = u[:, half * 4:half * 4 + 4, tk * 128:(tk + 1) * 128]
                S.tt("dve", uu, tmp, uu, ALU.mult)
        for g in range(4):
            for oc in range(2):
                g0 = GR[oc]
                for kc in range(2):
                    for (t0, w) in TT3:
                        S.mm(ps[:, g0 + t0:g0 + t0 + w], wp[:, g * 2 + kc, oc * 128:(oc + 1) * 128],
                             pp[:, g * 2 + kc, t0:t0 + w], start=(kc == 0), stop=(kc == 1))
            for oc in range(2):
                g0 = GR[oc]
                c = g * 2 + oc
                S.act(pp[:, c, :], ps[:, g0:g0 + TH], AF.Identity, scale=psc[:, c:c + 1])
        xs = [g1[0], g1[1]]
        chs = [S.new_chan(), S.new_chan()]
        for i in range(8):
            wt = ws.get(u_o[i])
            for sub in range(2):
                oc = i * 2 + sub
                g0 = GR[oc % 2]
                xst = xs[oc % 2]
                S.dma("sp", xst[:, 0:T + 2], x0T[oc * 128:(oc + 1) * 128, 126:TH], chs[oc % 2])
                for kc in range(NCH):
                    src = u[:, kc, :] if kc < 8 else pp[:, kc - 8, :]
                    lw = wt[:, kc, sub * 128:(sub + 1) * 128]
                    S.mm(ps[:, g0 + 510:g0 + 512], lw, src[:, 126:128], start=(kc == 0), stop=(kc == NCH - 1))
                    S.mm(ps[:, g0 + 512:g0 + 1024], lw, src[:, 128:640], start=(kc == 0), stop=(kc == NCH - 1))
                    S.mm(ps[:, g0 + 1024:g0 + 1536], lw, src[:, 640:1152], start=(kc == 0), stop=(kc == NCH - 1))
                S.stt(zf[:, oc, :], xst[:, 0:T + 2], ALPHA, ps[:, g0 + 510:g0 + 1536], ALU.mult, ALU.add)
            ws.release(u_o[i])
        tm_ln1 = {"eps": eps, "mean": g2[0], "rstd": g2[1],
                  "zb": [cv(R2 + k * 2052, (T + 2,), BF16) for k in range(2)],
                  "zs": [cv(R2 + (2 + k) * 2052, (T + 2,), BF16) for k in range(2)]}
        emit_ln(S, zf, 0, T + 2, ln1gb[:, 0, :, :], ones, tm_ln1, ps, [lambda c: zf[:, c, :]])
        S.ts("dve", zf[:, :, 0:2], zf[:, :, 0:2], flag[:, 0:1], None, ALU.mult)
        for c in range(NCH):
            S.act(xb[:, c, :], zf[:, c, :], AF.Identity)

        gq = cv(R2, (12, T), BF16)
        ft = [cv(R2 + 24576 + k * 4096, (T,), F32) for k in range(6)]
        tm_ffn = {"a": ft[0:2], "v": ft[2:4], "s": ft[4:6], "eps": eps, "mean": ft[0], "rstd": ft[1],
                  "zb": [cv(R2 + 49152 + k * 2048, (T,), BF16) for k in range(2)],
                  "zs": [cv(R2 + 53248 + k * 2048, (T,), BF16) for k in range(2)]}
        xb2 = cv(R1, (NCH, T), BF16)
        emit_ffn(S, ws, plan0, zf, xb, cwb[:, 0, :, :], gq, tm_ffn, ps)
        emit_ln(S, zf, 2, T, ln2gb[:, 0, :, :], ones, tm_ffn, ps,
                [lambda c: xb2[:, c, :], lambda c: zf[:, c, 2:T + 2]])
        chsp = [S.new_chan() for _ in range(NCH)]
        for c in range(NCH):
            S.dma("sp", xsp[c * 128:(c + 1) * 128, :], zf[:, c, 2:T + 2], chsp[c])

        A = cv(R0, (T,), F32); B = cv(R0 + 4096, (T,), F32); C = cv(R0 + 8192, (T,), F32)
        o_ = R0 + 12288
        kend = cv(o_, (T,), BF16); kdec = cv(o_ + 2048, (T,), BF16); qdec = cv(o_ + 4096, (T,), BF16)
        ibf = cv(o_ + 6144, (T,), BF16); sg = cv(o_ + 8192, (T,), BF16); osq = cv(o_ + 10240, (T,), BF16)
        attm = cv(o_ + 12288, (8, 128), BF16)
        kt0 = cv(o_ + 14336, (8, 128), BF16); kt1 = cv(o_ + 16384, (8, 128), BF16); vtk = cv(o_ + 18432, (8, 128), BF16)
        y = cv(R2, (NH, T), BF16)
        Sb = cv(R2 + 32768, (NCK, 128), BF16)
        xs1 = [cv(R2 + 40960, (T,), F32), cv(R2 + 45056, (T,), F32)]
        tm_ln1b = {"eps": eps, "mean": xs1[0], "rstd": xs1[1],
                   "zb": [cv(R2 + 49152 + k * 2048, (T,), BF16) for k in range(2)],
                   "zs": [cv(R2 + 53248 + k * 2048, (T,), BF16) for k in range(2)]}
        upst = cv(R2 + 57344, (4, 129), F32)
        stg = cv(R2 + 59408, (4, 129), F32)
        chc = S.new_chan(total=True)
        cst = {}
        lbp = sb("lbp_s", [128, 2, NH]); cst["lb"] = sb("lb", [128, NH]); cst["oml"] = sb("oml", [128, NH])
        cst["mask2"] = sb("mask2_s", [128, 128]); identf = sb("identf", [128, 128]); cst["ident"] = sb("ident_s", [128, 128], BF16)
        cst["pm"] = sb("pm", [128, 2])
        cst["rm"] = cv(R2 + 36864, (T,), F32)
        S.dma("sp", lbp[:, :, :], lbp_d.rearrange("p (l h) -> p l h", l=2), chc)
        S.dma("sp", cst["mask2"][:, :], mask2_d, chc)
        S.dma("sp", identf[:, :], ident_d, chc)
        S.copy("dve", cst["ident"][:, :], identf[:, :])
        S.tt("dve", cst["lb"][:, :], lbp[:, 1, :], lbp[:, 0, :], ALU.subtract)
        S.act(cst["lb"][:, :], cst["lb"][:, :], AF.Sigmoid)
        S.ts("dve", cst["oml"][:, :], cst["lb"][:, :], -1.0, 1.0, ALU.mult, ALU.add)
        S.memset("dve", cst["rm"], 1.0)
        S.memset("dve", cst["rm"].rearrange("p (c t) -> p c t", t=CH)[:, :, 0:1], 0.0)
        S.memset("dve", cst["pm"][:, :], 0.0)
        S.memset("dve", cst["pm"][0:64, 0:1], 1.0)
        S.memset("dve", cst["pm"][64:128, 1:2], 1.0)
        oh3 = oh[:, 0:4].rearrange("p (j o) -> p j o", o=1)

        def proj1(wt, blk, g0):
            for kc in range(NCH):
                for t_ in range(2):
                    S.mm(ps[:, g0 + t_ * 512:g0 + (t_ + 1) * 512], wt[:, kc, blk * 128:(blk + 1) * 128],
                         xb2[:, kc, t_ * 512:(t_ + 1) * 512], start=(kc == 0), stop=(kc == NCH - 1))

        def cc_op(idx, src_t, dst_t):
            if use_cc:
                def fn(e):
                    e.collective_compute("AllReduce", ALU.add, replica_groups=SEQ_GROUPS,
                                         ins=[src_t.ap().opt()], outs=[dst_t.ap().opt()]).then_inc(ccsem[idx])
                    return None
                S.add("pool", fn, reads=[src_t.ap()], writes=[])

                def fn2(e):
                    e.wait_ge(ccsem[idx], 1)
                    return e.memset(tiny[:, idx:idx + 1], 0.0)
                return lambda: S.add("pool", fn2, reads=[], writes=[dst_t.ap(), tiny[:, idx:idx + 1]])
            else:
                chq = S.new_chan()
                S.dma("sp", dst_t.ap(), src_t.ap(), chq)
                return lambda: None
        G0, G1 = 0, 1024
        chst = S.new_chan()
        cc_done = []
        for h in range(NH):
            wt = ws.get(u_pre[h])
            proj1(wt, 0, G0)
            proj1(wt, 1, G1)
            ws.release(u_pre[h])
            hgrn_gates(S, h, cst, ps[:, G0:G0 + T], A, B, C, kend, dec[:, :])
            S.act(ibf, ps[:, G1:G1 + T], AF.Identity)
            C3 = C.rearrange("p (c t) -> p c t", t=CH)
            S.add("dve", lambda e, C3=C3, h=h: e.reduce_sum(out=Dd[:, h:h + 1], in_=C3[:, :, CH - 1:CH],
                                                           axis=mybir.AxisListType.XY),
                  reads=[C], writes=[Dd[:, h:h + 1]])
            S.act(Dd[:, h:h + 1], Dd[:, h:h + 1], AF.Exp)
            hgrn_transposes(S, cst, psb, kend, kt0, kt1)
            hgrn_transposes(S, cst, psb, ibf, vtk)
            S.memset("dve", Sf[0][:, :], 0.0)
            fin = hgrn_state_scan(S, ps, (kt0, kt1), vtk, dec, Sf)
            for j in range(4):
                S.ts("dve", stg[:, j, 0:128], Sf[fin][:, :], oh[:, j:j + 1], None, ALU.mult)
            S.ts("dve", stg[:, :, 128:129], oh3, Dd[:, h:h + 1], None, ALU.mult)
            g, hl = divmod(h, 4)
            S.dma("sp", ccin[g].ap().rearrange("(j l d) n -> d j l n", j=4, l=4)[:, :, hl, :], stg, chst)
            if hl == 3:
                cc_done.append(cc_op(g, ccin[g], ccout[g]))
        chu = S.new_chan()
        for h in range(NH):
            g, hl = divmod(h, 4)
            if hl == 0:
                cc_done[g]()
            fi, qg = u_main[h]
            S.dma("sp", upst, ccout[g].ap().rearrange("(j l d) n -> d j l n", j=4, l=4)[:, :, hl, :], chu)
            S.stt(Pp[0][:, :], upst[:, 0, 0:128], upst[:, 1, 128:129], upst[:, 1, 0:128], ALU.mult, ALU.add)
            S.stt(Pp[1][:, :], Pp[0][:, :], upst[:, 2, 128:129], upst[:, 2, 0:128], ALU.mult, ALU.add)
            S.ts("dve", Sf[0][:, :], upst[:, 0, 0:128], oh[:, 1:2], None, ALU.mult)
            S.stt(Sf[0][:, :], Pp[0][:, :], oh[:, 2:3], Sf[0][:, :], ALU.mult, ALU.add)
            S.stt(Sf[0][:, :], Pp[1][:, :], oh[:, 3:4], Sf[0][:, :], ALU.mult, ALU.add)
            wt = ws.get(fi)
            proj1(wt, 0, G0)
            proj1(wt, 1, G1)
            ws.release(fi)
            hgrn_gates(S, h, cst, ps[:, G0:G0 + T], A, B, C, kend, dec[:, :], kdec_bf=kdec, eC=A)
            S.act(ibf, ps[:, G1:G1 + T], AF.Identity)
            wt = ws.get(qg)
            proj1(wt, 0, G0)
            proj1(wt, 1, G1)
            ws.release(qg)
            S.act(B, ps[:, G0:G0 + T], AF.Silu)
            S.tt("dve", qdec, B, A, ALU.mult)
            S.act(sg, ps[:, G1:G1 + T], AF.Sigmoid)
            hgrn_transposes(S, cst, psb, kend, kt0, kt1)
            hgrn_transposes(S, cst, psb, ibf, vtk)
            hgrn_state_scan(S, ps, (kt0, kt1), vtk, dec, Sf, Sb=Sb)
            for half in range(2):
                for jj in range(4):
                    j = half * 4 + jj
                    S.mm(ps[:, 2560 + jj * 128:2560 + (jj + 1) * 128], kdec[:, j * 128:(j + 1) * 128],
                         qdec[:, j * 128:(j + 1) * 128], start=True, stop=True)
                for jj in range(4):
                    j = half * 4 + jj
                    S.tt("dve", attm[:, j, :], ps[:, 2560 + jj * 128:2560 + (jj + 1) * 128], cst["mask2"][:, :],
                         ALU.mult)
            O0 = 3072
            for j in range(8):
                S.mm(ps[:, O0 + j * 128:O0 + (j + 1) * 128], vtk[:, j, :], attm[:, j, :], start=True, stop=False)
                S.mm(ps[:, O0 + j * 128:O0 + j * 128 + 64], Sb[:, 2 * j, :], qdec[:, j * 128:j * 128 + 64],
                     start=False, stop=False)
                S.mm(ps[:, O0 + j * 128 + 64:O0 + (j + 1) * 128], Sb[:, 2 * j + 1, :],
                     qdec[:, j * 128 + 64:(j + 1) * 128], start=False, stop=True)
            S.act(osq, ps[:, O0:O0 + T], AF.Square)
            for t_ in range(2):
                S.mm(ps[:, G0 + t_ * 512:G0 + (t_ + 1) * 512], ones[:, :], osq[:, t_ * 512:(t_ + 1) * 512],
                     start=True, stop=True)
            S.act(A, ps[:, G0:G0 + T], AF.Ln, bias=eps[:, 0:1], scale=1.0 / 128)
            S.act(A, A, AF.Exp, scale=-0.5)
            S.stt(C, ps[:, O0:O0 + T], gn[:, h:h + 1], A, ALU.mult, ALU.mult)
            S.tt("dve", y[:, h, :], C, sg, ALU.mult)
        chs1 = [S.new_chan(), S.new_chan()]
        for i in range(8):
            wt = ws.get(u_o1[i])
            for sub in range(2):
                oc = i * 2 + sub
                g0 = (oc % 2) * 1024
                xst = xs1[oc % 2]
                S.dma("sp", xst, xsp[oc * 128:(oc + 1) * 128, :], chs1[oc % 2])
                for kc in range(NCH):
                    for t_ in range(2):
                        S.mm(ps[:, g0 + t_ * 512:g0 + (t_ + 1) * 512], wt[:, kc, sub * 128:(sub + 1) * 128],
                             y[:, kc, t_ * 512:(t_ + 1) * 512], start=(kc == 0), stop=(kc == NCH - 1))
                S.stt(zf[:, oc, 2:T + 2], xst, ALPHA, ps[:, g0:g0 + T], ALU.mult, ALU.add)
            ws.release(u_o1[i])
        emit_ln(S, zf, 2, T, ln1gb[:, 1, :, :], ones, tm_ln1b, ps,
                [lambda c: xb[:, c, 2:T + 2], lambda c: zf[:, c, 2:T + 2]])
        for j in range(4):
            S.ts("dve", hstg[:, j, :].rearrange("p (c t) -> p c t", t=2), zf[:, :, T:T + 2], oh[:, j:j + 1], None,
                 ALU.mult)
        chh = S.new_chan()
        S.dma("sp", cch_in.ap().rearrange("(j p) n -> p j n", p=128), hstg[:, :, :], chh)
        done_h = cc_op(4, cch_in, cch_out)
        done_h()
        chh2 = S.new_chan()
        S.dma("sp", hld[:, :, :], cch_out.ap().rearrange("(j p) n -> p j n", p=128), chh2)
        S.ts("dve", hsum[:, :], hld[:, 0, :], oh[:, 4:5], None, ALU.mult)
        for j in range(1, 4):
            S.stt(hsum[:, :], hld[:, j, :], oh[:, 4 + j:5 + j], hsum[:, :], ALU.mult, ALU.add)
        S.act(xb[:, :, 0:2], hsum[:, :].rearrange("p (c t) -> p c t", t=2), AF.Identity)
        emit_ffn(S, ws, plan1, zf, xb, cwb[:, 1, :, :], gq, tm_ffn, ps)
        emit_ln(S, zf, 2, T, ln2gb[:, 1, :, :], ones, tm_ffn, ps, [lambda c: zf[:, c, 2:T + 2]])
        cho = [S.new_chan() for _ in range(4)]
        for c in range(NCH):
            S.dma("sp", outT[c * 128:(c + 1) * 128, :], zf[:, c, 2:T + 2], cho[c % 4])
        S.emit(nc, es, final_chans=cho)
    return nc, S


def fused_inputs(inp):
    f32 = np.float32
    maps = prep_mix0_inputs(inp["x"], inp["ev_w_in"], inp["ev_ln_v_g"], inp["ev_ln_v_b"], inp["ev_w_s"],
                            inp["ev_b_s"], inp["ev_w_pool"], inp["ev_pool_scale"], inp["ev_w_out"],
                            inp["ln1_g"], inp["ln1_b"])
    ln1gb = np.stack([np.stack([_pm(inp["ln1_g"][l], 16), _pm(inp["ln1_b"][l], 16)], axis=-1) for l in range(2)], axis=1)
    ln2gb = np.stack([np.stack([_pm(inp["ln2_g"][l], 16), _pm(inp["ln2_b"][l], 16)], axis=-1) for l in range(2)], axis=1)
    cwbs = []
    for l in range(2):
        cw = np.asarray(inp["ffn_conv_w"][l], f32)
        cb = np.asarray(inp["ffn_conv_b"][l], f32)
        cwbs.append(np.stack([_pm(cw[0], 88), _pm(cw[1], 88), _pm(cw[2], 88), _pm(cb, 88)], axis=-1))
    cwb = np.stack(cwbs, axis=1)
    common = {
        "ln1gb": np.ascontiguousarray(ln1gb.reshape(128, 64)), "ln2gb": np.ascontiguousarray(ln2gb.reshape(128, 64)),
        "cwb": np.ascontiguousarray(cwb.reshape(128, 2 * 88 * 4)),
        "w_up0": np.ascontiguousarray(inp["ffn_w_up"][0], f32), "w_up1": np.ascontiguousarray(inp["ffn_w_up"][1], f32),
        "w_down0": np.ascontiguousarray(inp["ffn_w_down"][0], f32),
        "w_down1": np.ascontiguousarray(inp["ffn_w_down"][1], f32),
        "od_w_in": regroup_w_in(inp["od_w_in"][0]), "od_w_out": np.ascontiguousarray(inp["od_w_out"][0], f32),
        "gn": _pm(inp["od_norm_g"][0], NH)}
    common.update(hgrn_const_inputs(inp["lb_param"]))
    out = []
    for c in range(NCORES):
        b, s = divmod(c, 4)
        m0 = maps[c]
        m = dict(common)
        for k in ("x0T", "wsT", "maskA", "bsT", "lnv", "w_pool", "pscale", "rcnt", "flag"):
            m[k] = m0[k]
        m["ev_w_in"] = m0["w_in"]
        m["ev_w_out"] = m0["w_out"]
        oh = np.zeros((128, 8), f32)
        oh[:, s] = 1.0
        if s > 0:
            oh[:, 4 + s - 1] = 1.0
        m["oh"] = oh
        out.append(m)
    return out


def kernel(**inputs):
    nc, _ = build_fused(use_cc=True)
    maps = fused_inputs(inputs)
    res = run_bass_kernel_spmd(nc, maps, core_ids=list(range(NCORES))).results
    out = np.zeros((2, 4 * T, D), np.float32)
    for c in range(NCORES):
        b, s = divmod(c, 4)
        out[b, s * T:(s + 1) * T] = res[c]["outT"].T
    return out
```

```python
import numpy as np
from contextlib import ExitStack
import concourse.bass as bass
import concourse.mybir as mybir
from concourse.bass_utils import run_bass_kernel_spmd

F32 = mybir.dt.float32
BF16 = mybir.dt.bfloat16
AF = mybir.ActivationFunctionType
ALU = mybir.AluOpType

D = 2048
NCH = 16
T = 1024
NCORES = 8
DFF = 5632
NFF = 44
ALPHA = 4.0 ** 0.25
LN_EPS = 1e-5
ENGS = ("pe", "act", "dve", "pool", "sp")
_DT_SIZE = {F32: 4, BF16: 2}


def _dsize(dt):
    return _DT_SIZE.get(dt, 4)


class _Op:
    __slots__ = ("eng", "idx", "fn", "deps", "chan", "chan_val", "signal", "val")


class Sched:
    def __init__(self):
        self.ops = {e: [] for e in ENGS}
        self.track = {}
        self.chan_cnt = []
        self.chan_total = []

    @staticmethod
    def _rng(ap):
        t = ap.tensor
        name = t.name
        sp = str(ap.space) if hasattr(ap, "space") else ""
        pat = ap.ap
        esz = _dsize(ap.dtype)
        if "DRAM" in sp.upper() or "Dram" in type(t).__name__ or "DRam" in type(t).__name__:
            ext = 1
            for (st, cnt) in pat:
                ext += abs(st) * (cnt - 1)
            return name, ap.offset * esz, (ap.offset + ext) * esz
        pstride = pat[0][0]
        lo = ap.offset % pstride if pstride > 0 else ap.offset
        ext = 1
        for (st, cnt) in pat[1:]:
            ext += abs(st) * (cnt - 1)
        return name, lo * esz, (lo + ext) * esz

    def _touch(self, name, lo, hi, op, is_write, deps):
        segs = self.track.setdefault(name, [])
        new = []
        covered = []
        for s in segs:
            slo, shi, w, rs = s
            if shi <= lo or slo >= hi:
                new.append(s)
                continue
            if slo < lo:
                new.append([slo, lo, w, list(rs)])
            if shi > hi:
                new.append([hi, shi, w, list(rs)])
            olo, ohi = max(slo, lo), min(shi, hi)
            if w is not None:
                deps.add(w)
            if is_write:
                for r in rs:
                    deps.add(r)
            else:
                covered.append([olo, ohi, w, rs + [op]])
        if is_write:
            new.append([lo, hi, op, []])
        else:
            covered.sort(key=lambda s: s[0])
            cur = lo
            for c in covered:
                if c[0] > cur:
                    new.append([cur, c[0], None, [op]])
                new.append(c)
                cur = c[1]
            if cur < hi:
                new.append([cur, hi, None, [op]])
        self.track[name] = new

    def add(self, eng, fn, reads=(), writes=(), chan=None):
        o = _Op()
        o.eng = eng
        o.fn = fn
        o.chan = chan
        o.signal = False
        o.val = None
        o.chan_val = None
        deps = set()
        for ap in reads:
            if ap is None or isinstance(ap, (int, float)):
                continue
            n, lo, hi = self._rng(ap)
            self._touch(n, lo, hi, o, False, deps)
        for ap in writes:
            n, lo, hi = self._rng(ap)
            if eng == "pe":
                lo = (lo // 2048) * 2048
                hi = ((hi + 2047) // 2048) * 2048
            self._touch(n, lo, hi, o, True, deps)
        deps.discard(o)
        o.deps = deps
        if chan is not None:
            self.chan_cnt[chan] += 1
            o.chan_val = 16 * self.chan_cnt[chan]
        o.idx = len(self.ops[eng])
        self.ops[eng].append(o)
        return o

    def new_chan(self, total=False):
        self.chan_cnt.append(0)
        self.chan_total.append(total)
        return len(self.chan_cnt) - 1

    def emit(self, nc, es, final_chans=()):
        for e in ENGS:
            for o in self.ops[e]:
                for d in o.deps:
                    if d.chan is None:
                        d.signal = True
        for e in ENGS:
            c = 0
            for o in self.ops[e]:
                if o.chan is None and o.signal:
                    c += 1
                    o.val = c
        esem = {e: es.enter_context(nc.semaphore("s_" + e)) for e in ENGS}
        csem = [es.enter_context(nc.semaphore("c_%d" % i)) for i in range(len(self.chan_cnt))]
        block = es.enter_context(nc.Block())
        nwaits = {e: 0 for e in ENGS}

        def run(engname, eobj):
            seen = {}
            for o in self.ops[engname]:
                need = {}
                for d in o.deps:
                    if d.chan is not None:
                        key = ("c", d.chan)
                        v = 16 * self.chan_cnt[d.chan] if self.chan_total[d.chan] else d.chan_val
                    else:
                        if d.eng == engname and engname == "pe":
                            continue
                        key = ("e", d.eng)
                        v = d.val
                    if v > need.get(key, 0):
                        need[key] = v
                for key, v in need.items():
                    if v <= seen.get(key, 0):
                        continue
                    seen[key] = v
                    sem = csem[key[1]] if key[0] == "c" else esem[key[1]]
                    eobj.wait_ge(sem, v)
                    nwaits[engname] += 1
                inst = o.fn(eobj)
                if o.chan is not None:
                    inst.then_inc(csem[o.chan], 16)
                elif o.signal:
                    assert inst is not None
                    inst.then_inc(esem[engname], 1)
            if engname == "sp":
                for ch in final_chans:
                    if self.chan_cnt[ch] > 0:
                        eobj.wait_ge(csem[ch], 16 * self.chan_cnt[ch])

        @block.tensor
        def _(e):
            run("pe", e)

        @block.scalar
        def _(e):
            run("act", e)

        @block.vector
        def _(e):
            run("dve", e)

        @block.gpsimd
        def _(e):
            run("pool", e)

        @block.sync
        def _(e):
            run("sp", e)

        self.nwaits = nwaits

    def mm(self, out, lhsT, rhs, start=True, stop=True):
        return self.add("pe", lambda e: e.matmul(out, lhsT=lhsT, rhs=rhs, start=start, stop=stop),
                        reads=[lhsT, rhs], writes=[out])

    def transpose(self, out, in_, ident):
        return self.add("pe", lambda e: e.transpose(out, in_, ident), reads=[in_, ident], writes=[out])

    def act(self, out, in_, func, bias=None, scale=None):
        kw = {}
        rd = [in_]
        if bias is not None:
            kw["bias"] = bias
            rd.append(bias)
        if scale is not None:
            kw["scale"] = scale
            rd.append(scale)
        return self.add("act", lambda e: e.activation(out=out, in_=in_, func=func, **kw), reads=rd, writes=[out])

    def tt(self, eng, out, in0, in1, op):
        return self.add(eng, lambda e: e.tensor_tensor(out=out, in0=in0, in1=in1, op=op),
                        reads=[in0, in1], writes=[out])

    def ts(self, eng, out, in0, s1, s2, op0, op1=None):
        if op1 is None:
            return self.add(eng, lambda e: e.tensor_scalar(out=out, in0=in0, scalar1=s1, scalar2=None, op0=op0),
                            reads=[in0, s1], writes=[out])
        return self.add(eng, lambda e: e.tensor_scalar(out=out, in0=in0, scalar1=s1, scalar2=s2, op0=op0, op1=op1),
                        reads=[in0, s1, s2], writes=[out])

    def stt(self, out, in0, scalar, in1, op0, op1):
        return self.add("dve", lambda e: e.scalar_tensor_tensor(out=out, in0=in0, scalar=scalar, in1=in1,
                                                                op0=op0, op1=op1),
                        reads=[in0, scalar, in1], writes=[out])

    def copy(self, eng, out, in_):
        if eng == "act":
            return self.add("act", lambda e: e.copy(out=out, in_=in_), reads=[in_], writes=[out])
        return self.add(eng, lambda e: e.tensor_copy(out=out, in_=in_), reads=[in_], writes=[out])

    def memset(self, eng, ap, val):
        return self.add(eng, lambda e: e.memset(ap, val), writes=[ap])

    def dma(self, eng, out, in_, chan):
        return self.add(eng, lambda e: e.dma_start(out=out, in_=in_), reads=[in_], writes=[out], chan=chan)


class WeightStream:
    def __init__(self, S, nc, es, nslots, free_elems, name="wslot"):
        self.S = S
        self.slots = [es.enter_context(nc.sbuf_tensor("%s%d" % (name, i), [128, free_elems], BF16))
                      for i in range(nslots)]
        self.chans = [S.new_chan() for _ in range(nslots)]
        self.uses = []
        self.loaded = 0
        self.released = -1
        self.n = nslots

    def plan(self, shape, src):
        self.uses.append((shape, src))
        return len(self.uses) - 1

    def view(self, k):
        shape, _ = self.uses[k]
        sl = self.slots[k % self.n]
        n = 1
        for s in shape:
            n *= s
        v = sl[:, 0:n]
        if len(shape) == 2:
            return v.rearrange("p (a b) -> p a b", a=shape[0], b=shape[1])
        return v

    def _load_upto(self, k):
        while self.loaded < len(self.uses) and self.loaded <= k:
            j = self.loaded
            _, src = self.uses[j]
            self.S.dma("pool", self.view(j), src, self.chans[j % self.n])
            self.loaded += 1

    def get(self, k):
        assert k <= self.released + self.n, (k, self.released)
        self._load_upto(k)
        return self.view(k)

    def release(self, k):
        self.released = max(self.released, k)
        self._load_upto(self.released + self.n)


FF_QUARTERS = (12, 10, 12, 10)


def plan_ffn_weights(ws, w_up, w_down):
    plan = []
    base = 0
    wu = w_up.rearrange("(kc p) n -> p kc n", p=128)
    for q, nq in enumerate(FF_QUARTERS):
        ups = []
        for j in range(0, nq, 2):
            ca = base + j
            ua = ws.plan((16, 256), wu[:, :, ca * 128:ca * 128 + 256])
            uv = ws.plan((16, 256), wu[:, :, (NFF + ca) * 128:(NFF + ca) * 128 + 256])
            ups.append((ca, ua, uv))
        downs = []
        wd = w_down[base * 128:(base + nq) * 128, :].rearrange("(j p) n -> p j n", p=128)
        for op_ in range(8):
            downs.append((op_, ws.plan((nq, 256), wd[:, :, op_ * 256:(op_ + 1) * 256])))
        plan.append((base, nq, ups, downs))
        base += nq
    return plan


def emit_ffn(S, ws, plan, xf, xb, cwb, gq, tmps, ps):
    G = (0, 1536)
    gi = 0
    for (base, nq, ups, downs) in plan:
        for (ca, ua, uv) in ups:
            wa = ws.get(ua)
            wv = ws.get(uv)
            for sub in range(2):
                c_a = ca + sub
                c_v = NFF + ca + sub
                j = c_a - base
                tm = {}
                for which, (wt, cc) in enumerate(((wa, c_a), (wv, c_v))):
                    g0 = G[which]
                    for kc in range(NCH):
                        lw = wt[:, kc, sub * 128:(sub + 1) * 128]
                        S.mm(ps[:, g0 + 510:g0 + 512], lw, xb[:, kc, 0:2], start=(kc == 0), stop=(kc == NCH - 1))
                        S.mm(ps[:, g0 + 512:g0 + 1024], lw, xb[:, kc, 2:514], start=(kc == 0), stop=(kc == NCH - 1))
                        S.mm(ps[:, g0 + 1024:g0 + 1536], lw, xb[:, kc, 514:1026], start=(kc == 0),
                             stop=(kc == NCH - 1))
                    tmp = tmps["a" if which == 0 else "v"][gi % 2]
                    tm[which] = tmp
                    S.act(tmp[:, :], ps[:, g0 + 512:g0 + 1536], AF.Identity, bias=cwb[:, cc, 3:4],
                          scale=cwb[:, cc, 2:3])
                    S.stt(tmp[:, :], ps[:, g0 + 511:g0 + 1535], cwb[:, cc, 1:2], tmp[:, :], ALU.mult, ALU.add)
                    S.stt(tmp[:, :], ps[:, g0 + 510:g0 + 1534], cwb[:, cc, 0:1], tmp[:, :], ALU.mult, ALU.add)
                sa = tmps["s"][gi % 2]
                S.act(sa[:, :], tm[0][:, :], AF.Silu)
                S.tt("dve", gq[:, j, :], sa[:, :], tm[1][:, :], ALU.mult)
                gi += 1
            ws.release(uv)
        for (op_, ud) in downs:
            wd = ws.get(ud)
            for sub in range(2):
                oc = op_ * 2 + sub
                for tt_ in range(2):
                    for j in range(nq):
                        S.mm(ps[:, 3072 + tt_ * 512:3072 + (tt_ + 1) * 512], wd[:, j, sub * 128:(sub + 1) * 128],
                             gq[:, j, tt_ * 512:(tt_ + 1) * 512], start=(j == 0), stop=(j == nq - 1))
                if base == 0:
                    S.stt(xf[:, oc, 2:1026], xf[:, oc, 2:1026], ALPHA, ps[:, 3072:4096], ALU.mult, ALU.add)
                else:
                    S.tt("dve", xf[:, oc, 2:1026], xf[:, oc, 2:1026], ps[:, 3072:4096], ALU.add)
            ws.release(ud)


def emit_ln(S, zf, c0, n, gb, ones, tmps, ps, outs):
    nt = (n + 511) // 512
    zb = tmps["zb"]
    zs = tmps["zs"]
    for c in range(NCH):
        b0 = zb[c % 2]
        s0 = zs[c % 2]
        S.act(b0[:, 0:n], zf[:, c, c0:c0 + n], AF.Identity)
        S.act(s0[:, 0:n], zf[:, c, c0:c0 + n], AF.Square)
        for t_ in range(nt):
            w = min(512, n - t_ * 512)
            S.mm(ps[:, t_ * 512:t_ * 512 + w], ones[:, :], b0[:, t_ * 512:t_ * 512 + w], start=(c == 0),
                 stop=(c == NCH - 1))
            S.mm(ps[:, 2048 + t_ * 512:2048 + t_ * 512 + w], ones[:, :], s0[:, t_ * 512:t_ * 512 + w],
                 start=(c == 0), stop=(c == NCH - 1))
    mean = tmps["mean"]
    rstd = tmps["rstd"]
    S.ts("dve", mean[:, 0:n], ps[:, 0:n], 1.0 / D, None, ALU.mult)
    S.tt("dve", rstd[:, 0:n], mean[:, 0:n], mean[:, 0:n], ALU.mult)
    S.stt(rstd[:, 0:n], ps[:, 2048:2048 + n], 1.0 / D, rstd[:, 0:n], ALU.mult, ALU.subtract)
    S.act(rstd[:, 0:n], rstd[:, 0:n], AF.Sqrt, bias=tmps["eps"][:, 0:1], scale=1.0)
    S.add("dve", lambda e: e.reciprocal(out=rstd[:, 0:n], in_=rstd[:, 0:n]), reads=[rstd[:, 0:n]],
          writes=[rstd[:, 0:n]])
    for c in range(NCH):
        zc = zf[:, c, c0:c0 + n]
        S.tt("dve", zc, zc, mean[:, 0:n], ALU.subtract)
        S.tt("dve", zc, zc, rstd[:, 0:n], ALU.mult)
        for i, dst in enumerate(outs):
            S.act(dst(c), zc, AF.Identity, bias=gb[:, c, 1:2], scale=gb[:, c, 0:1])


def build_ffn_launch():
    nc = bass.Bass("TRN2", target_bir_lowering=False)
    xT = nc.dram_tensor("xT", [D, T + 2], F32, kind="ExternalInput").ap()
    w_up = nc.dram_tensor("w_up", [D, 2 * DFF], F32, kind="ExternalInput").ap()
    w_down = nc.dram_tensor("w_down", [DFF, D], F32, kind="ExternalInput").ap()
    cwb_d = nc.dram_tensor("cwb", [128, 88 * 4], F32, kind="ExternalInput").ap()
    gb_d = nc.dram_tensor("ln2gb", [128, 32], F32, kind="ExternalInput").ap()
    yT = nc.dram_tensor("yT", [D, T], F32, kind="ExternalOutput").ap()
    S = Sched()
    with ExitStack() as es:
        xf = es.enter_context(nc.sbuf_tensor("xf", [128, NCH, T + 2], F32))
        xb = es.enter_context(nc.sbuf_tensor("xb", [128, NCH, T + 2], BF16))
        cwb = es.enter_context(nc.sbuf_tensor("cwb_s", [128, 88, 4], F32))
        gb = es.enter_context(nc.sbuf_tensor("gb_s", [128, NCH, 2], F32))
        gq = es.enter_context(nc.sbuf_tensor("gq", [128, 12, T], BF16))
        ones = es.enter_context(nc.sbuf_tensor("ones", [128, 128], BF16))
        eps = es.enter_context(nc.sbuf_tensor("eps", [128, 1], F32))
        tmps = {
            "a": [es.enter_context(nc.sbuf_tensor("ta%d" % i, [128, T], F32)) for i in range(2)],
            "v": [es.enter_context(nc.sbuf_tensor("tv%d" % i, [128, T], F32)) for i in range(2)],
            "s": [es.enter_context(nc.sbuf_tensor("tsl%d" % i, [128, T], F32)) for i in range(2)],
            "eps": eps,
        }
        tmps["zb"] = [es.enter_context(nc.sbuf_tensor("zb%d" % i, [128, T], BF16)) for i in range(2)]
        tmps["zs"] = [es.enter_context(nc.sbuf_tensor("zs%d" % i, [128, T], BF16)) for i in range(2)]
        tmps["mean"] = tmps["a"][0]
        tmps["rstd"] = tmps["a"][1]
        ps = es.enter_context(nc.psum_tensor("ps", [128, 4096], F32))
        ws = WeightStream(S, nc, es, 4, 16 * 256)
        plan = plan_ffn_weights(ws, w_up, w_down)

        ch_in = S.new_chan(total=True)
        ch_p = S.new_chan(total=True)
        ch_out = [S.new_chan() for _ in range(4)]
        S.dma("sp", cwb[:, :, :], cwb_d.rearrange("p (c j) -> p c j", j=4), ch_p)
        S.dma("sp", gb[:, :, :], gb_d.rearrange("p (c j) -> p c j", j=2), ch_p)
        S.memset("dve", ones[:, :], 1.0)
        S.memset("dve", eps[:, :], LN_EPS)
        for c in range(NCH):
            S.dma("sp", xf[:, c, :], xT[c * 128:(c + 1) * 128, :], ch_in)
        for c in range(NCH):
            S.act(xb[:, c, :], xf[:, c, :], AF.Identity)
        emit_ffn(S, ws, plan, xf, xb, cwb, gq, tmps, ps)
        emit_ln(S, xf, 2, T, gb, ones, tmps, ps, [lambda c: xf[:, c, 2:T + 2]])
        for c in range(NCH):
            S.dma("sp", yT[c * 128:(c + 1) * 128, :], xf[:, c, 2:T + 2], ch_out[c % 4])
        S.emit(nc, es, final_chans=ch_out)
    return nc, S


def _pm(v, nch):
    return np.ascontiguousarray(np.asarray(v, np.float32).reshape(nch, 128).T)


def prep_ffn_params(l, ffn_conv_w, ffn_conv_b, ln2_g, ln2_b):
    cw = np.asarray(ffn_conv_w[l], np.float32)
    cb = np.asarray(ffn_conv_b[l], np.float32)
    cwb = np.stack([_pm(cw[0], 88), _pm(cw[1], 88), _pm(cw[2], 88), _pm(cb, 88)], axis=-1)
    gb = np.stack([_pm(ln2_g[l], 16), _pm(ln2_b[l], 16)], axis=-1)
    return np.ascontiguousarray(cwb.reshape(128, 88 * 4)), np.ascontiguousarray(gb.reshape(128, 32))


def run_ffn_launch(x1, l, ffn_w_up, ffn_conv_w, ffn_conv_b, ffn_w_down, ln2_g, ln2_b):
    nc, S = build_ffn_launch()
    cwb, gb = prep_ffn_params(l, ffn_conv_w, ffn_conv_b, ln2_g, ln2_b)
    wu = np.ascontiguousarray(ffn_w_up[l], np.float32)
    wd = np.ascontiguousarray(ffn_w_down[l], np.float32)
    x1 = np.asarray(x1, np.float32)
    in_maps = []
    for c in range(NCORES):
        b, s = divmod(c, 4)
        t0 = s * T
        xt = np.zeros((D, T + 2), np.float32)
        xt[:, 2:] = x1[b, t0:t0 + T].T
        if s > 0:
            xt[:, 0:2] = x1[b, t0 - 2:t0].T
        in_maps.append({"xT": xt, "w_up": wu, "w_down": wd, "cwb": cwb, "ln2gb": gb})
    res = run_bass_kernel_spmd(nc, in_maps, core_ids=list(range(NCORES)))
    out = np.zeros((2, 4096, D), np.float32)
    for c in range(NCORES):
        b, s = divmod(c, 4)
        out[b, s * T:(s + 1) * T] = res.results[c]["yT"].T
    return out


TH = T + 128
B_WINDOWS = (2, 4, 8, 16)
GELU_C = 0.044715
GELU_S = 2.0 * 0.7978845608028654


def emit_gelu(S, dst, src_ps, t1, t2):
    S.act(t1, src_ps, AF.Square)
    S.ts("dve", t1, t1, GELU_C, 1.0, ALU.mult, ALU.add)
    S.tt("dve", t1, t1, src_ps, ALU.mult)
    S.act(t2, t1, AF.Sigmoid, scale=GELU_S)
    S.tt("dve", dst, t2, src_ps, ALU.mult)


def build_mix0_launch():
    nc = bass.Bass("TRN2", target_bir_lowering=False)
    x0T = nc.dram_tensor("x0T", [D, TH], F32, kind="ExternalInput").ap()
    w_in = nc.dram_tensor("w_in", [D, 3072], F32, kind="ExternalInput").ap()
    w_out = nc.dram_tensor("w_out", [D, D], F32, kind="ExternalInput").ap()
    wsT_d = nc.dram_tensor("wsT", [128, 8 * 128], F32, kind="ExternalInput").ap()
    mask_d = nc.dram_tensor("maskA", [128, 128], F32, kind="ExternalInput").ap()
    bsT_d = nc.dram_tensor("bsT", [128, 8 * 128], F32, kind="ExternalInput").ap()
    lnv_d = nc.dram_tensor("lnv", [128, 2 * 1024], F32, kind="ExternalInput").ap()
    wp_d = nc.dram_tensor("w_pool", [4 * 256, 256], F32, kind="ExternalInput").ap()
    psc_d = nc.dram_tensor("pscale", [128, 8], F32, kind="ExternalInput").ap()
    gb_d = nc.dram_tensor("ln1gb", [128, 32], F32, kind="ExternalInput").ap()
    rc_d = nc.dram_tensor("rcnt", [128, 64], F32, kind="ExternalInput").ap()
    flag_d = nc.dram_tensor("flag", [128, 1], F32, kind="ExternalInput").ap()
    x1T = nc.dram_tensor("x1T", [D, T + 2], F32, kind="ExternalOutput").ap()
    S = Sched()
    with ExitStack() as es:
        arena = es.enter_context(nc.sbuf_tensor("arena", [128, NCH * (T + 2)], F32))
        zf = arena[:, :].rearrange("p (c t) -> p c t", c=NCH, t=T + 2)
        x0b = arena[:, 0:NCH * TH // 2].bitcast(BF16).rearrange("p (c t) -> p c t", c=NCH, t=TH)
        u = es.enter_context(nc.sbuf_tensor("u", [128, 8, TH], BF16))
        vt = es.enter_context(nc.sbuf_tensor("vt", [128, 9, 1024], BF16))
        pp = es.enter_context(nc.sbuf_tensor("pp", [128, 8, TH], BF16))
        wsT = es.enter_context(nc.sbuf_tensor("wsT_s", [128, 8, 128], BF16))
        mask = es.enter_context(nc.sbuf_tensor("mask_s", [128, 128], F32))
        bsT = es.enter_context(nc.sbuf_tensor("bsT_s", [128, 8, 128], F32))
        lnv = es.enter_context(nc.sbuf_tensor("lnv_s", [128, 2, 1024], F32))
        wp = es.enter_context(nc.sbuf_tensor("wp_s", [128, 8, 256], BF16))
        psc = es.enter_context(nc.sbuf_tensor("psc_s", [128, 8], F32))
        gb = es.enter_context(nc.sbuf_tensor("gb_s", [128, NCH, 2], F32))
        rc = es.enter_context(nc.sbuf_tensor("rc_s", [128, 4, 16], F32))
        flag = es.enter_context(nc.sbuf_tensor("flag_s", [128, 1], F32))
        ones = es.enter_context(nc.sbuf_tensor("ones", [128, 128], BF16))
        eps = es.enter_context(nc.sbuf_tensor("eps", [128, 1], F32))
        xbf = es.enter_context(nc.sbuf_tensor("xbf", [128, 16 + TH], F32))
        g1 = [es.enter_context(nc.sbuf_tensor("g1_%d" % i, [128, TH], F32)) for i in range(2)]
        g2p = [es.enter_context(nc.sbuf_tensor("g2_%d" % i, [128, 16 + TH], F32)) for i in range(2)]
        g2 = [t[:, 16:16 + TH] for t in g2p]
        tA, tB = g2p
        wsTf = g1[1][:, 0:1024].rearrange("p (h t) -> p h t", h=8)
        st = es.enter_context(nc.sbuf_tensor("st", [128, 8], F32))
        small = es.enter_context(nc.sbuf_tensor("small", [128, 32], F32))
        tmps = {"eps": eps,
                "zb": [g2p[i][:, 16:16 + 513].bitcast(BF16) for i in range(2)],
                "zs": [xbf[:, 16:16 + 513].bitcast(BF16), xbf[:, 600:600 + 513].bitcast(BF16)],
                "mean": g1[0], "rstd": g1[1]}
        ps = es.enter_context(nc.psum_tensor("ps", [128, 4096], F32))
        ws = WeightStream(S, nc, es, 4, 16 * 256)
        wi = w_in.rearrange("(kc p) n -> p kc n", p=128)
        wo = w_out.rearrange("(kc p) n -> p kc n", p=128)
        u_xb = [ws.plan((16, 256), wi[:, :, 2048 + i * 256:2048 + (i + 1) * 256]) for i in range(4)]
        u_u = [ws.plan((16, 256), wi[:, :, i * 256:(i + 1) * 256]) for i in range(4)]
        u_v = [ws.plan((16, 256), wi[:, :, 1024 + i * 256:1024 + (i + 1) * 256]) for i in range(4)]
        u_o = [ws.plan((16, 256), wo[:, :, i * 256:(i + 1) * 256]) for i in range(8)]

        chp = S.new_chan(total=True)
        chx = S.new_chan(total=True)
        S.dma("sp", wsTf, wsT_d.rearrange("p (h t) -> p h t", h=8), chp)
        S.dma("sp", mask[:, :], mask_d, chp)
        S.dma("sp", bsT[:, :, :], bsT_d.rearrange("p (h t) -> p h t", h=8), chp)
        S.dma("sp", lnv[:, :, :], lnv_d.rearrange("p (a c) -> p a c", a=2), chp)
        S.dma("sp", psc[:, :], psc_d, chp)
        S.dma("sp", gb[:, :, :], gb_d.rearrange("p (c j) -> p c j", j=2), chp)
        S.dma("sp", rc[:, :, :], rc_d.rearrange("p (g j) -> p g j", g=4), chp)
        S.dma("sp", flag[:, :], flag_d, chp)
        S.dma("pool", wp[:, :, :], wp_d.rearrange("(a p) n -> p a n", p=128), chx)
        for c in range(NCH):
            S.dma("pool", x0b[:, c, :], x0T[c * 128:(c + 1) * 128, :], chx)
        ws.release(-1)
        S.memset("dve", ones[:, :], 1.0)
        S.memset("dve", eps[:, :], LN_EPS)
        S.memset("dve", xbf[:, 0:16], 0.0)
        S.memset("dve", tA[:, 0:16], 0.0)
        S.memset("dve", tB[:, 0:16], 0.0)
        for h in range(8):
            S.tt("dve", wsT[:, h, :], wsTf[:, h, :], mask[:, :], ALU.mult)

        GR = (0, 1536)
        TT3 = ((0, 512), (512, 512), (1024, 128))

        def proj_fm(wt, sub, g0):
            for kc in range(NCH):
                for (t0, w) in TT3:
                    S.mm(ps[:, g0 + t0:g0 + t0 + w], wt[:, kc, sub * 128:(sub + 1) * 128], x0b[:, kc, t0:t0 + w],
                         start=(kc == 0), stop=(kc == NCH - 1))

        gi = 0
        for i in range(4):
            wt = ws.get(u_xb[i])
            for sub in range(2):
                c = i * 2 + sub
                g = c // 2
                g0 = GR[gi % 2]
                gi += 1
                proj_fm(wt, sub, g0)
                S.act(xbf[:, 16:16 + TH], ps[:, g0:g0 + TH], AF.Identity)
                src = xbf
                dsts = [tA, tB]
                for k in range(g + 1):
                    sh = 1 << k
                    dst = dsts[k % 2]
                    S.tt("dve", dst[:, 16:16 + TH], src[:, 16:16 + TH], src[:, 16 - sh:16 + TH - sh], ALU.add)
                    src = dst
                win = B_WINDOWS[g]
                S.stt(pp[:, c, :], src[:, 16:16 + TH], 1.0 / win, xbf[:, 16:16 + TH], ALU.mult, ALU.subtract)
                S.tt("dve", small[:, 0:16], src[:, 16 + 128:16 + 144], rc[:, g, :], ALU.mult)
                S.tt("dve", pp[:, c, 128:144], small[:, 0:16], xbf[:, 16 + 128:16 + 144], ALU.subtract)
            ws.release(u_xb[i])
        for i in range(4):
            wt = ws.get(u_u[i])
            for sub in range(2):
                c = i * 2 + sub
                g0 = GR[gi % 2]
                proj_fm(wt, sub, g0)
                emit_gelu(S, u[:, c, :], ps[:, g0:g0 + TH], g1[gi % 2][:, :], g2[gi % 2])
                gi += 1
            ws.release(u_u[i])
        wv = [ws.get(k) for k in u_v]
        for tk in range(9):
            vr = g1[tk % 2]
            for cg in range(4):
                r0 = 3072 + ((tk * 4 + cg) % 2) * 512
                for kc in range(NCH):
                    S.mm(ps[:, r0:r0 + 256], x0b[:, kc, tk * 128:(tk + 1) * 128], wv[cg][:, kc, :],
                         start=(kc == 0), stop=(kc == NCH - 1))
                emit_gelu(S, vr[:, cg * 256:(cg + 1) * 256], ps[:, r0:r0 + 256],
                          g2[0][:, cg * 256:(cg + 1) * 256], g2[1][:, cg * 256:(cg + 1) * 256])
            sq = g2[0]
            S.add("dve", lambda e, vr=vr: e.reduce_sum(out=st[:, 0:1], in_=vr[:, 0:1024], axis=mybir.AxisListType.X),
                  reads=[vr[:, 0:1024]], writes=[st[:, 0:1]])
            S.act(sq[:, 0:1024], vr[:, 0:1024], AF.Square)
            S.add("dve", lambda e, sq=sq: e.reduce_sum(out=st[:, 1:2], in_=sq[:, 0:1024], axis=mybir.AxisListType.X),
                  reads=[sq[:, 0:1024]], writes=[st[:, 1:2]])
            S.ts("dve", st[:, 2:3], st[:, 0:1], 1.0 / 1024, None, ALU.mult)
            S.tt("dve", st[:, 3:4], st[:, 2:3], st[:, 2:3], ALU.mult)
            S.stt(st[:, 4:5], st[:, 1:2], 1.0 / 1024, st[:, 3:4], ALU.mult, ALU.subtract)
            S.act(st[:, 5:6], st[:, 4:5], AF.Sqrt, bias=eps[:, 0:1], scale=1.0)
            S.add("dve", lambda e: e.reciprocal(out=st[:, 6:7], in_=st[:, 5:6]), reads=[st[:, 5:6]],
                  writes=[st[:, 6:7]])
            S.ts("dve", vr[:, 0:1024], vr[:, 0:1024], st[:, 2:3], st[:, 6:7], ALU.subtract, ALU.mult)
            S.tt("dve", vr[:, 0:1024], vr[:, 0:1024], lnv[:, 0, :], ALU.mult)
            S.tt("dve", vt[:, tk, :], vr[:, 0:1024], lnv[:, 1, :], ALU.add)
        ws.release(u_v[3])
        for tk in range(9):
            for half in range(2):
                r0 = 2048 + half * 512
                for hh in range(4):
                    h = half * 4 + hh
                    S.mm(ps[:, r0 + hh * 128:r0 + (hh + 1) * 128], vt[:, tk, h * 128:(h + 1) * 128], wsT[:, h, :],
                         start=True, stop=True)
                tmp = g2[half][:, 0:512].rearrange("p (h t) -> p h t", h=4)
                S.tt("dve", tmp, ps[:, r0:r0 + 512].rearrange("p (h t) -> p h t", h=4),
                     bsT[:, half * 4:half * 4 + 4, :], ALU.add)
                uu = u[:, half * 4:half * 4 + 4, tk * 128:(tk + 1) * 128]
                S.tt("dve", uu, tmp, uu, ALU.mult)
        for g in range(4):
            for oc in range(2):
                g0 = GR[oc]
                for kc in range(2):
                    for (t0, w) in TT3:
                        S.mm(ps[:, g0 + t0:g0 + t0 + w], wp[:, g * 2 + kc, oc * 128:(oc + 1) * 128],
                             pp[:, g * 2 + kc, t0:t0 + w], start=(kc == 0), stop=(kc == 1))
            for oc in range(2):
                g0 = GR[oc]
                c = g * 2 + oc
                S.act(pp[:, c, :], ps[:, g0:g0 + TH], AF.Identity, scale=psc[:, c:c + 1])
        xs = [g1[0], g1[1]]
        chs = [S.new_chan(), S.new_chan()]
        for i in range(8):
            wt = ws.get(u_o[i])
            for sub in range(2):
                oc = i * 2 + sub
                g0 = GR[oc % 2]
                xst = xs[oc % 2]
                S.dma("sp", xst[:, 0:T + 2], x0T[oc * 128:(oc + 1) * 128, 126:TH], chs[oc % 2])
                for kc in range(NCH):
                    src = u[:, kc, :] if kc < 8 else pp[:, kc - 8, :]
                    lw = wt[:, kc, sub * 128:(sub + 1) * 128]
                    S.mm(ps[:, g0 + 510:g0 + 512], lw, src[:, 126:128], start=(kc == 0), stop=(kc == NCH - 1))
                    S.mm(ps[:, g0 + 512:g0 + 1024], lw, src[:, 128:640], start=(kc == 0), stop=(kc == NCH - 1))
                    S.mm(ps[:, g0 + 1024:g0 + 1536], lw, src[:, 640:1152], start=(kc == 0), stop=(kc == NCH - 1))
                S.stt(zf[:, oc, :], xst[:, 0:T + 2], ALPHA, ps[:, g0 + 510:g0 + 1536], ALU.mult, ALU.add)
            ws.release(u_o[i])
        emit_ln(S, zf, 0, T + 2, gb, ones, tmps, ps, [lambda c: zf[:, c, :]])
        S.ts("dve", zf[:, :, 0:2], zf[:, :, 0:2], flag[:, 0:1], None, ALU.mult)
        cho = [S.new_chan() for _ in range(4)]
        for c in range(NCH):
            S.dma("sp", x1T[c * 128:(c + 1) * 128, :], zf[:, c, :], cho[c % 4])
        S.emit(nc, es, final_chans=cho)
    return nc, S


def prep_mix0_inputs(x, ev_w_in, ev_ln_v_g, ev_ln_v_b, ev_w_s, ev_b_s, ev_w_pool, ev_pool_scale, ev_w_out,
                     ln1_g, ln1_b):
    x = np.asarray(x, np.float32)
    ws = np.asarray(ev_w_s[0], np.float32)
    wsT = np.ascontiguousarray(ws.transpose(2, 0, 1)).reshape(128, 8 * 128)
    tt_ = np.arange(128)
    maskA = (tt_[None, :] >= tt_[:, None]).astype(np.float32)
    bsT = np.ascontiguousarray(np.broadcast_to(np.asarray(ev_b_s[0], np.float32).reshape(1, 8 * 128), (128, 8 * 128)))
    lnv = np.ascontiguousarray(np.broadcast_to(
        np.concatenate([np.asarray(ev_ln_v_g[0], np.float32), np.asarray(ev_ln_v_b[0], np.float32)])[None, :],
        (128, 2048)))
    wp = np.ascontiguousarray(np.asarray(ev_w_pool[0], np.float32).reshape(4 * 256, 256))
    psc = _pm(ev_pool_scale[0], 8)
    gb = np.ascontiguousarray(np.stack([_pm(ln1_g[0], 16), _pm(ln1_b[0], 16)], axis=-1).reshape(128, 32))
    common = {"w_in": np.ascontiguousarray(ev_w_in[0], np.float32),
              "w_out": np.ascontiguousarray(ev_w_out[0], np.float32),
              "wsT": wsT, "maskA": maskA, "bsT": bsT, "lnv": lnv, "w_pool": wp, "pscale": psc, "ln1gb": gb}
    in_maps = []
    for c in range(NCORES):
        b, s = divmod(c, 4)
        t0 = s * T
        xt = np.zeros((D, TH), np.float32)
        xt[:, 128:] = x[b, t0:t0 + T].T
        if s > 0:
            xt[:, 0:128] = x[b, t0 - 128:t0].T
        rc = np.zeros((4, 16), np.float32)
        for g, win in enumerate(B_WINDOWS):
            pos = np.arange(t0 + 1, t0 + 17, dtype=np.float32)
            rc[g] = 1.0 / np.minimum(pos, float(win))
        rcb = np.ascontiguousarray(np.broadcast_to(rc.reshape(1, 64), (128, 64)))
        m = dict(common)
        m.update({"x0T": xt, "rcnt": rcb, "flag": np.full((128, 1), 1.0 if s > 0 else 0.0, np.float32)})
        in_maps.append(m)
    return in_maps


def run_mix0_launch(inputs):
    nc, S = build_mix0_launch()
    in_maps = prep_mix0_inputs(inputs["x"], inputs["ev_w_in"], inputs["ev_ln_v_g"], inputs["ev_ln_v_b"],
                               inputs["ev_w_s"], inputs["ev_b_s"], inputs["ev_w_pool"], inputs["ev_pool_scale"],
                               inputs["ev_w_out"], inputs["ln1_g"], inputs["ln1_b"])
    res = run_bass_kernel_spmd(nc, in_maps, core_ids=list(range(NCORES)))
    return [res.results[c]["x1T"] for c in range(NCORES)]


NH = 16
CH = 64
NCK = T // CH


def hgrn_consts(S, nc, es, lbp_d, mask2_d, ident_d, chp):
    c = {}
    lbp = es.enter_context(nc.sbuf_tensor("lbp_s", [128, 2, NH], F32))
    c["lb"] = es.enter_context(nc.sbuf_tensor("lb", [128, NH], F32))
    c["oml"] = es.enter_context(nc.sbuf_tensor("oml", [128, NH], F32))
    c["rm"] = es.enter_context(nc.sbuf_tensor("rm", [128, T], F32))
    c["mask2"] = es.enter_context(nc.sbuf_tensor("mask2_s", [128, 128], F32))
    identf = es.enter_context(nc.sbuf_tensor("identf", [128, 128], F32))
    c["ident"] = es.enter_context(nc.sbuf_tensor("ident_s", [128, 128], BF16))
    S.dma("sp", lbp[:, :, :], lbp_d.rearrange("p (l h) -> p l h", l=2), chp)
    S.dma("sp", c["mask2"][:, :], mask2_d, chp)
    S.dma("sp", identf[:, :], ident_d, chp)
    S.copy("dve", c["ident"][:, :], identf[:, :])
    S.tt("dve", c["lb"][:, :], lbp[:, 1, :], lbp[:, 0, :], ALU.subtract)
    S.act(c["lb"][:, :], c["lb"][:, :], AF.Sigmoid)
    S.ts("dve", c["oml"][:, :], c["lb"][:, :], -1.0, 1.0, ALU.mult, ALU.add)
    S.memset("dve", c["rm"][:, :], 1.0)
    S.memset("dve", c["rm"][:, :].rearrange("p (c t) -> p c t", t=CH)[:, :, 0:1], 0.0)
    c["pm"] = es.enter_context(nc.sbuf_tensor("pm", [128, 2], F32))
    S.memset("dve", c["pm"][:, :], 0.0)
    S.memset("dve", c["pm"][0:64, 0:1], 1.0)
    S.memset("dve", c["pm"][64:128, 1:2], 1.0)
    return c


def hgrn_gates(S, h, cst, f_ps, A, B, C, kend_bf, dec, kdec_bf=None, eC=None):
    S.act(A, f_ps, AF.Sigmoid)
    S.ts("dve", A, A, cst["oml"][:, h:h + 1], cst["lb"][:, h:h + 1], ALU.mult, ALU.add)
    S.act(B, A, AF.Ln)
    S.add("dve", lambda e: e.tensor_tensor_scan(out=C, data0=cst["rm"][:, :], data1=B, initial=0.0,
                                                op0=ALU.mult, op1=ALU.add),
          reads=[cst["rm"][:, :], B], writes=[C])
    S.ts("dve", A, A, -1.0, 1.0, ALU.mult, ALU.add)
    S.act(B, C, AF.Exp, scale=-1.0)
    S.tt("dve", B, A, B, ALU.mult)
    if kdec_bf is not None:
        S.act(kdec_bf, B, AF.Identity)
    C3 = C.rearrange("p (c t) -> p c t", t=CH)
    S.act(dec.rearrange("p (c o) -> p c o", o=1), C3[:, :, CH - 1:CH], AF.Exp)
    S.tt("dve", kend_bf.rearrange("p (c t) -> p c t", t=CH), B.rearrange("p (c t) -> p c t", t=CH),
         dec.rearrange("p (c o) -> p c o", o=1).to_broadcast([128, NCK, CH]), ALU.mult)
    if eC is not None:
        S.act(eC, C, AF.Exp)


def hgrn_transposes(S, cst, psb, src_bf, dst_tok, dst_tok1=None):
    for j in range(8):
        S.transpose(psb[:, 4096 + j * 128:4096 + (j + 1) * 128], src_bf[:, j * 128:(j + 1) * 128], cst["ident"][:, :])
    if dst_tok1 is None:
        S.act(dst_tok.rearrange("p j d -> p (j d)"), psb[:, 4096:5120], AF.Identity)
    else:
        S.act(dst_tok.rearrange("p j d -> p (j d)"), psb[:, 4096:5120], AF.Identity, scale=cst["pm"][:, 0:1])
        S.act(dst_tok1.rearrange("p j d -> p (j d)"), psb[:, 4096:5120], AF.Identity, scale=cst["pm"][:, 1:2])


def hgrn_state_scan(S, ps, kt, vtk, dec, Sf, Sb=None):
    cur = 0
    if Sb is not None:
        S.act(Sb[:, 0, :], Sf[0][:, :], AF.Identity)
    for g4 in range(4):
        for cc in range(4):
            c = g4 * 4 + cc
            j, par = divmod(c, 2)
            S.mm(ps[:, 2560 + cc * 128:2560 + (cc + 1) * 128], kt[par][:, j, :], vtk[:, j, :], start=True, stop=True)
        for cc in range(4):
            c = g4 * 4 + cc
            nxt = 1 - cur
            S.stt(Sf[nxt][:, :], Sf[cur][:, :], dec[:, c:c + 1], ps[:, 2560 + cc * 128:2560 + (cc + 1) * 128],
                  ALU.mult, ALU.add)
            cur = nxt
            if Sb is not None and c + 1 < NCK:
                S.act(Sb[:, c + 1, :], Sf[cur][:, :], AF.Identity)
    return cur


def build_hgrn_pre_launch(stage=9, nheads=NH):
    nc = bass.Bass("TRN2", target_bir_lowering=False)
    xT = nc.dram_tensor("xT", [D, T], F32, kind="ExternalInput").ap()
    w_in = nc.dram_tensor("w_in", [D, 4 * D], F32, kind="ExternalInput").ap()
    lbp_d = nc.dram_tensor("lbp", [128, 2 * NH], F32, kind="ExternalInput").ap()
    mask2_d = nc.dram_tensor("mask2", [128, 128], F32, kind="ExternalInput").ap()
    ident_d = nc.dram_tensor("ident", [128, 128], F32, kind="ExternalInput").ap()
    U_d = nc.dram_tensor("U", [NH, 128, 128], F32, kind="ExternalOutput").ap()
    D_d = nc.dram_tensor("Dd", [128, NH], F32, kind="ExternalOutput").ap()
    S = Sched()
    with ExitStack() as es:
        xb = es.enter_context(nc.sbuf_tensor("xb", [128, NCH, T], BF16))
        A = [es.enter_context(nc.sbuf_tensor("A%d" % i, [128, T], F32)) for i in range(2)]
        B = [es.enter_context(nc.sbuf_tensor("B%d" % i, [128, T], F32)) for i in range(2)]
        C = [es.enter_context(nc.sbuf_tensor("C%d" % i, [128, T], F32)) for i in range(2)]
        kend = [es.enter_context(nc.sbuf_tensor("kend%d" % i, [128, T], BF16)) for i in range(2)]
        ibf = [es.enter_context(nc.sbuf_tensor("ibf%d" % i, [128, T], BF16)) for i in range(2)]
        kt = [[es.enter_context(nc.sbuf_tensor("kt%d_%d" % (i, k), [128, 8, 128], BF16)) for k in range(2)]
              for i in range(2)]
        vtk = [es.enter_context(nc.sbuf_tensor("vtk%d" % i, [128, 8, 128], BF16)) for i in range(2)]
        dec = [es.enter_context(nc.sbuf_tensor("dec%d" % i, [128, NCK], F32)) for i in range(2)]
        Sf = [[es.enter_context(nc.sbuf_tensor("Sf%d_%d" % (i, k), [128, 128], F32)) for k in range(2)]
              for i in range(2)]
        Dd = es.enter_context(nc.sbuf_tensor("Dd_s", [128, NH], F32))
        ps = es.enter_context(nc.psum_tensor("ps", [128, 4096], F32))
        psb = ps[:, :].bitcast(BF16)
        chp = S.new_chan(total=True)
        chx = S.new_chan(total=True)
        cst = hgrn_consts(S, nc, es, lbp_d, mask2_d, ident_d, chp)
        ws = WeightStream(S, nc, es, 4, 16 * 256)
        wi = w_in.rearrange("(kc p) n -> p kc n", p=128)
        uses = [ws.plan((16, 256), wi[:, :, h * 512 + 256:h * 512 + 512]) for h in range(NH)]
        for c in range(NCH):
            S.dma("pool", xb[:, c, :], xT[c * 128:(c + 1) * 128, :], chx)
        ws.release(-1)
        cho = [S.new_chan() for _ in range(2)]
        for h in range(nheads):
            b = h % 2
            wt = ws.get(uses[h])
            for which in range(2):
                g0 = which * 1024
                for kc in range(NCH):
                    for t_ in range(2):
                        S.mm(ps[:, g0 + t_ * 512:g0 + (t_ + 1) * 512], wt[:, kc, which * 128:(which + 1) * 128],
                             xb[:, kc, t_ * 512:(t_ + 1) * 512], start=(kc == 0), stop=(kc == NCH - 1))
            ws.release(uses[h])
            hgrn_gates(S, h, cst, ps[:, 0:1024], A[b][:, :], B[b][:, :], C[b][:, :], kend[b][:, :], dec[b][:, :])
            S.act(ibf[b][:, :], ps[:, 1024:2048], AF.Identity)
            C3 = C[b][:, :].rearrange("p (c t) -> p c t", t=CH)
            S.add("dve", lambda e, C3=C3, h=h: e.reduce_sum(out=Dd[:, h:h + 1], in_=C3[:, :, CH - 1:CH],
                                                           axis=mybir.AxisListType.XY),
                  reads=[C[b][:, :]], writes=[Dd[:, h:h + 1]])
            S.act(Dd[:, h:h + 1], Dd[:, h:h + 1], AF.Exp)
            if stage >= 2:
                hgrn_transposes(S, cst, psb, kend[b][:, :], kt[b][0][:, :, :], kt[b][1][:, :, :])
                hgrn_transposes(S, cst, psb, ibf[b][:, :], vtk[b][:, :, :])
            S.memset("dve", Sf[b][0][:, :], 0.0)
            fin = 0
            if stage >= 3:
                fin = hgrn_state_scan(S, ps, kt[b], vtk[b], dec[b], Sf[b])
            S.dma("sp", U_d[h, :, :], Sf[b][fin][:, :], cho[b])
        chd = S.new_chan()
        S.dma("sp", D_d, Dd[:, :], chd)
        S.emit(nc, es, final_chans=cho + [chd])
    return nc, S


def regroup_w_in(od_w_in):
    w = np.asarray(od_w_in, np.float32).reshape(D, 4, NH, 128)
    w = w[:, [0, 3, 1, 2]]
    return np.ascontiguousarray(w.transpose(0, 2, 1, 3).reshape(D, 4 * D))


def hgrn_const_inputs(lb_param):
    lbp = np.ascontiguousarray(np.stack([_pm(lb_param[0], NH), _pm(lb_param[1], NH)], axis=1).reshape(128, 2 * NH))
    i = np.arange(128)
    mask2 = ((i[None, :] >= i[:, None]) & ((i[None, :] // CH) == (i[:, None] // CH))).astype(np.float32)
    return {"lbp": lbp, "mask2": mask2, "ident": np.eye(128, dtype=np.float32)}


def _carve(arena, off_bytes, n_elems, dt):
    assert off_bytes % 4 == 0
    nb = n_elems * _dsize(dt)
    assert nb % 4 == 0
    v = arena[:, off_bytes // 4:(off_bytes + nb) // 4]
    return v if dt == F32 else v.bitcast(dt)


def build_hgrn_main_launch():
    nc = bass.Bass("TRN2", target_bir_lowering=False)
    xT = nc.dram_tensor("xT", [D, T], F32, kind="ExternalInput").ap()
    w_in = nc.dram_tensor("w_in", [D, 4 * D], F32, kind="ExternalInput").ap()
    w_out = nc.dram_tensor("w_out", [D, D], F32, kind="ExternalInput").ap()
    lbp_d = nc.dram_tensor("lbp", [128, 2 * NH], F32, kind="ExternalInput").ap()
    mask2_d = nc.dram_tensor("mask2", [128, 128], F32, kind="ExternalInput").ap()
    ident_d = nc.dram_tensor("ident", [128, 128], F32, kind="ExternalInput").ap()
    up_d = nc.dram_tensor("Uprev", [3, NH, 128, 128], F32, kind="ExternalInput").ap()
    dp_d = nc.dram_tensor("Dprev", [128, 3 * NH], F32, kind="ExternalInput").ap()
    gn_d = nc.dram_tensor("gn", [128, NH], F32, kind="ExternalInput").ap()
    gb_d = nc.dram_tensor("ln1gb", [128, 32], F32, kind="ExternalInput").ap()
    x1T = nc.dram_tensor("x1T", [D, T], F32, kind="ExternalOutput").ap()
    S = Sched()
    with ExitStack() as es:
        arena = es.enter_context(nc.sbuf_tensor("arena", [128, NCH * T], F32))
        zf = arena[:, :].rearrange("p (c t) -> p c t", c=NCH, t=T)
        xb = _carve(arena, 0, NCH * T, BF16).rearrange("p (c t) -> p c t", c=NCH, t=T)
        off = NCH * T * 2
        A = _carve(arena, off, T, F32); off += 4 * T
        B = _carve(arena, off, T, F32); off += 4 * T
        C = _carve(arena, off, T, F32); off += 4 * T
        kend = _carve(arena, off, T, BF16); off += 2 * T
        kdec = _carve(arena, off, T, BF16); off += 2 * T
        qdec = _carve(arena, off, T, BF16); off += 2 * T
        ibf = _carve(arena, off, T, BF16); off += 2 * T
        sg = _carve(arena, off, T, BF16); off += 2 * T
        osq = _carve(arena, off, T, BF16); off += 2 * T
        attm = _carve(arena, off, T, BF16).rearrange("p (j t) -> p j t", j=8); off += 2 * T
        kt0 = _carve(arena, off, T, BF16).rearrange("p (j t) -> p j t", j=8); off += 2 * T
        kt1 = _carve(arena, off, T, BF16).rearrange("p (j t) -> p j t", j=8); off += 2 * T
        kt = (kt0, kt1)
        vtk = _carve(arena, off, T, BF16).rearrange("p (j t) -> p j t", j=8); off += 2 * T
        assert off <= NCH * T * 4
        y = es.enter_context(nc.sbuf_tensor("y", [128, NH, T], BF16))
        Sb = es.enter_context(nc.sbuf_tensor("Sb", [128, NCK, 128], BF16))
        Sf = [es.enter_context(nc.sbuf_tensor("Sf%d" % k, [128, 128], F32)) for k in range(2)]
        upst = es.enter_context(nc.sbuf_tensor("upst", [128, 3, 128], F32))
        dec = es.enter_context(nc.sbuf_tensor("dec", [128, NCK], F32))
        dp = es.enter_context(nc.sbuf_tensor("dp", [128, 3, NH], F32))
        gn = es.enter_context(nc.sbuf_tensor("gn_s", [128, NH], F32))
        gb = es.enter_context(nc.sbuf_tensor("gb_s", [128, NCH, 2], F32))
        ones = es.enter_context(nc.sbuf_tensor("ones", [128, 128], BF16))
        eps = es.enter_context(nc.sbuf_tensor("eps", [128, 1], F32))
        xs = [es.enter_context(nc.sbuf_tensor("xs%d" % i, [128, T], F32)) for i in range(2)]
        tmps = {"eps": eps,
                "zb": [es.enter_context(nc.sbuf_tensor("zb%d" % i, [128, T], BF16)) for i in range(2)],
                "zs": [es.enter_context(nc.sbuf_tensor("zs%d" % i, [128, T], BF16)) for i in range(2)],
                "mean": xs[0], "rstd": xs[1]}
        ps = es.enter_context(nc.psum_tensor("ps", [128, 4096], F32))
        psb = ps[:, :].bitcast(BF16)
        chp = S.new_chan(total=True)
        chx = S.new_chan(total=True)
        cst = hgrn_consts(S, nc, es, lbp_d, mask2_d, ident_d, chp)
        S.dma("sp", dp[:, :, :], dp_d.rearrange("p (j h) -> p j h", j=3), chp)
        S.dma("sp", gn[:, :], gn_d, chp)
        S.dma("sp", gb[:, :, :], gb_d.rearrange("p (c j) -> p c j", j=2), chp)
        S.memset("dve", ones[:, :], 1.0)
        S.memset("dve", eps[:, :], LN_EPS)
        ws = WeightStream(S, nc, es, 3, 16 * 512)
        wi = w_in.rearrange("(kc p) n -> p kc n", p=128)
        wo = w_out.rearrange("(kc p) n -> p kc n", p=128)
        uses = [ws.plan((16, 512), wi[:, :, h * 512:(h + 1) * 512]) for h in range(NH)]
        u_o = [ws.plan((16, 512), wo[:, :, i * 512:(i + 1) * 512]) for i in range(4)]
        for c in range(NCH):
            S.dma("pool", xb[:, c, :], xT[c * 128:(c + 1) * 128, :], chx)
        ws.release(-1)
        chu = S.new_chan()
        G0, G1 = 0, 1024
        for h in range(NH):
            wt = ws.get(uses[h])

            def proj(blk, g0):
                for kc in range(NCH):
                    for t_ in range(2):
                        S.mm(ps[:, g0 + t_ * 512:g0 + (t_ + 1) * 512], wt[:, kc, blk * 128:(blk + 1) * 128],
                             xb[:, kc, t_ * 512:(t_ + 1) * 512], start=(kc == 0), stop=(kc == NCH - 1))
            S.dma("sp", upst[:, :, :], up_d[:, h, :, :].rearrange("j d e -> d j e"), chu)
            S.memset("dve", Sf[0][:, :], 0.0)
            cur = 0
            for j in range(3):
                S.stt(Sf[1 - cur][:, :], Sf[cur][:, :], dp[:, j, h:h + 1], upst[:, j, :], ALU.mult, ALU.add)
                cur = 1 - cur
            Sfl = [Sf[cur], Sf[1 - cur]]
            proj(2, G0)
            proj(3, G1)
            hgrn_gates(S, h, cst, ps[:, G0:G0 + T], A, B, C, kend, dec[:, :], kdec_bf=kdec, eC=A)
            S.act(ibf, ps[:, G1:G1 + T], AF.Identity)
            proj(0, G0)
            proj(1, G1)
            ws.release(uses[h])
            S.act(B, ps[:, G0:G0 + T], AF.Silu)
            S.tt("dve", qdec, B, A, ALU.mult)
            S.act(sg, ps[:, G1:G1 + T], AF.Sigmoid)
            hgrn_transposes(S, cst, psb, kend, kt0, kt1)
            hgrn_transposes(S, cst, psb, ibf, vtk)
            hgrn_state_scan(S, ps, kt, vtk, dec, Sfl, Sb=Sb)
            for half in range(2):
                for jj in range(4):
                    j = half * 4 + jj
                    S.mm(ps[:, 2560 + jj * 128:2560 + (jj + 1) * 128], kdec[:, j * 128:(j + 1) * 128],
                         qdec[:, j * 128:(j + 1) * 128], start=True, stop=True)
                for jj in range(4):
                    j = half * 4 + jj
                    S.tt("dve", attm[:, j, :], ps[:, 2560 + jj * 128:2560 + (jj + 1) * 128], cst["mask2"][:, :],
                         ALU.mult)
            O0 = 3072
            for j in range(8):
                S.mm(ps[:, O0 + j * 128:O0 + (j + 1) * 128], vtk[:, j, :], attm[:, j, :], start=True, stop=False)
                S.mm(ps[:, O0 + j * 128:O0 + j * 128 + 64], Sb[:, 2 * j, :], qdec[:, j * 128:j * 128 + 64],
                     start=False, stop=False)
                S.mm(ps[:, O0 + j * 128 + 64:O0 + (j + 1) * 128], Sb[:, 2 * j + 1, :],
                     qdec[:, j * 128 + 64:(j + 1) * 128], start=False, stop=True)
            S.act(osq, ps[:, O0:O0 + T], AF.Square)
            for t_ in range(2):
                S.mm(ps[:, G0 + t_ * 512:G0 + (t_ + 1) * 512], ones[:, :], osq[:, t_ * 512:(t_ + 1) * 512],
                     start=True, stop=True)
            S.act(A, ps[:, G0:G0 + T], AF.Ln, bias=eps[:, 0:1], scale=1.0 / 128)
            S.act(A, A, AF.Exp, scale=-0.5)
            S.stt(C, ps[:, O0:O0 + T], gn[:, h:h + 1], A, ALU.mult, ALU.mult)
            S.tt("dve", y[:, h, :], C, sg, ALU.mult)
        chs = [S.new_chan(), S.new_chan()]
        for i in range(4):
            wt = ws.get(u_o[i])
            for sub in range(4):
                oc = i * 4 + sub
                g0 = (oc % 2) * 1024
                xst = xs[oc % 2]
                S.dma("sp", xst[:, :], xT[oc * 128:(oc + 1) * 128, :], chs[oc % 2])
                for kc in range(NCH):
                    for t_ in range(2):
                        S.mm(ps[:, g0 + t_ * 512:g0 + (t_ + 1) * 512], wt[:, kc, sub * 128:(sub + 1) * 128],
                             y[:, kc, t_ * 512:(t_ + 1) * 512], start=(kc == 0), stop=(kc == NCH - 1))
                S.stt(zf[:, oc, :], xst[:, :], ALPHA, ps[:, g0:g0 + T], ALU.mult, ALU.add)
            ws.release(u_o[i])
        emit_ln(S, zf, 0, T, gb, ones, tmps, ps, [lambda c: zf[:, c, :]])
        cho = [S.new_chan() for _ in range(4)]
        for c in range(NCH):
            S.dma("sp", x1T[c * 128:(c + 1) * 128, :], zf[:, c, :], cho[c % 4])
        S.emit(nc, es, final_chans=cho)
    return nc, S


def _run(nc, in_maps):
    return run_bass_kernel_spmd(nc, in_maps, core_ids=list(range(NCORES))).results


def kernel_unfused(x, ev_w_in, ev_ln_v_g, ev_ln_v_b, ev_w_s, ev_b_s, ev_w_pool, ev_pool_scale,
           ev_w_out, od_w_in, od_norm_g, od_w_out, lb_param, ffn_w_up, ffn_conv_w,
           ffn_conv_b, ffn_w_down, ln1_g, ln1_b, ln2_g, ln2_b):
    f32 = np.float32
    nc0, _ = build_mix0_launch()
    maps0 = prep_mix0_inputs(x, ev_w_in, ev_ln_v_g, ev_ln_v_b, ev_w_s, ev_b_s, ev_w_pool, ev_pool_scale,
                             ev_w_out, ln1_g, ln1_b)
    r0 = _run(nc0, maps0)
    x1T = [r0[c]["x1T"] for c in range(NCORES)]
    ncf, _ = build_ffn_launch()

    def ffn(l, xTs):
        cwb, gb = prep_ffn_params(l, ffn_conv_w, ffn_conv_b, ln2_g, ln2_b)
        wu = np.ascontiguousarray(ffn_w_up[l], f32)
        wd = np.ascontiguousarray(ffn_w_down[l], f32)
        maps = [{"xT": np.ascontiguousarray(xTs[c], f32), "w_up": wu, "w_down": wd, "cwb": cwb, "ln2gb": gb}
                for c in range(NCORES)]
        r = _run(ncf, maps)
        return [r[c]["yT"] for c in range(NCORES)]

    x2T = ffn(0, x1T)
    ncp, _ = build_hgrn_pre_launch()
    w_in_r = regroup_w_in(od_w_in[0])
    hc = hgrn_const_inputs(lb_param)
    mapsp = []
    for c in range(NCORES):
        m = {"xT": np.ascontiguousarray(x2T[c], f32), "w_in": w_in_r}
        m.update(hc)
        mapsp.append(m)
    rp = _run(ncp, mapsp)
    ncm, _ = build_hgrn_main_launch()
    gn = _pm(od_norm_g[0], NH)
    gb1 = np.ascontiguousarray(np.stack([_pm(ln1_g[1], 16), _pm(ln1_b[1], 16)], axis=-1).reshape(128, 32))
    w_out1 = np.ascontiguousarray(od_w_out[0], f32)
    mapsm = []
    for c in range(NCORES):
        b, s = divmod(c, 4)
        up = np.zeros((3, NH, 128, 128), f32)
        dp = np.zeros((128, 3, NH), f32)
        for j in range(s):
            pos = 3 - s + j
            up[pos] = rp[b * 4 + j]["U"]
            dp[:, pos, :] = rp[b * 4 + j]["Dd"]
        m = {"xT": np.ascontiguousarray(x2T[c], f32), "w_in": w_in_r, "w_out": w_out1, "Uprev": up,
             "Dprev": np.ascontiguousarray(dp.reshape(128, 3 * NH)), "gn": gn, "ln1gb": gb1}
        m.update(hc)
        mapsm.append(m)
    rm = _run(ncm, mapsm)
    x1bT = []
    for c in range(NCORES):
        b, s = divmod(c, 4)
        xt = np.zeros((D, T + 2), f32)
        xt[:, 2:] = rm[c]["x1T"]
        if s > 0:
            xt[:, 0:2] = rm[c - 1]["x1T"][:, T - 2:T]
        x1bT.append(xt)
    outT = ffn(1, x1bT)
    out = np.zeros((2, 4 * T, D), f32)
    for c in range(NCORES):
        b, s = divmod(c, 4)
        out[b, s * T:(s + 1) * T] = outT[c].T
    return out


R0 = 0
R0_SZ = NCH * (T + 2) * 4
R1 = R0 + R0_SZ
R1_SZ = NCH * (T + 2) * 2
R2 = R1 + R1_SZ
R2_SZ = 66560
AR_BYTES = R2 + R2_SZ
SEQ_GROUPS = [[0, 1, 2, 3], [4, 5, 6, 7]]


def build_fused(use_cc=True):
    nc = bass.Bass("TRN2", target_bir_lowering=False)

    def din(name, shape):
        return nc.dram_tensor(name, shape, F32, kind="ExternalInput").ap()
    x0T = din("x0T", [D, TH])
    ev_w_in = din("ev_w_in", [D, 3072])
    ev_w_out = din("ev_w_out", [D, D])
    wsT_d = din("wsT", [128, 8 * 128])
    mask_d = din("maskA", [128, 128])
    bsT_d = din("bsT", [128, 8 * 128])
    lnv_d = din("lnv", [128, 2 * 1024])
    wp_d = din("w_pool", [4 * 256, 256])
    psc_d = din("pscale", [128, 8])
    rc_d = din("rcnt", [128, 64])
    flag_d = din("flag", [128, 1])
    oh_d = din("oh", [128, 8])
    ln1gb_d = din("ln1gb", [128, 64])
    ln2gb_d = din("ln2gb", [128, 64])
    cwb_d = din("cwb", [128, 2 * 88 * 4])
    w_up = [din("w_up%d" % l, [D, 2 * DFF]) for l in range(2)]
    w_down = [din("w_down%d" % l, [DFF, D]) for l in range(2)]
    od_w_in = din("od_w_in", [D, 4 * D])
    od_w_out = din("od_w_out", [D, D])
    lbp_d = din("lbp", [128, 2 * NH])
    mask2_d = din("mask2", [128, 128])
    ident_d = din("ident", [128, 128])
    gn_d = din("gn", [128, NH])
    outT = nc.dram_tensor("outT", [D, T], F32, kind="ExternalOutput").ap()
    xsp = nc.dram_tensor("xsp", [D, T], F32).ap()
    ccin = [nc.dram_tensor("ccin%d" % g, [4 * 4 * 128, 129], F32) for g in range(4)]
    ccout = [nc.dram_tensor("ccout%d" % g, [4 * 4 * 128, 129], F32) for g in range(4)]
    cch_in = nc.dram_tensor("cch_in", [4 * 128, 32], F32)
    cch_out = nc.dram_tensor("cch_out", [4 * 128, 32], F32)

    S = Sched()
    with ExitStack() as es:
        AR = es.enter_context(nc.sbuf_tensor("AR", [128, AR_BYTES // 4], F32))

        def cv(off, shape, dt):
            n = 1
            for k in shape:
                n *= k
            v = _carve(AR, off, n, dt)
            if len(shape) == 2:
                v = v.rearrange("p (a b) -> p a b", a=shape[0], b=shape[1])
            return v

        def sb(name, shape, dt=F32):
            return es.enter_context(nc.sbuf_tensor(name, shape, dt))
        psc = sb("psc_s", [128, 8]); rc = sb("rc_s", [128, 4, 16]); flag = sb("flag_s", [128, 1])
        oh = sb("oh_s", [128, 8]); ln1gb = sb("ln1gb_s", [128, 2, NCH, 2]); ln2gb = sb("ln2gb_s", [128, 2, NCH, 2])
        cwb = sb("cwb_s", [128, 2, 88, 4]); ones = sb("ones", [128, 128], BF16); eps = sb("eps", [128, 1])
        st = sb("st", [128, 8]); small = sb("small", [128, 32]); gn = sb("gn_s", [128, NH])
        Dd = sb("Dd_s", [128, NH]); tiny = sb("tiny", [128, 8])
        hstg = sb("hstg", [128, 4, 32]); hld = sb("hld", [128, 4, 32]); hsum = sb("hsum", [128, 32])
        ps = es.enter_context(nc.psum_tensor("ps", [128, 4096], F32))
        psb = ps[:, :].bitcast(BF16)
        ccsem = [es.enter_context(nc.semaphore("ccs%d" % i)) for i in range(5)]
        ws = WeightStream(S, nc, es, 4, 16 * 256)

        wi0 = ev_w_in.rearrange("(kc p) n -> p kc n", p=128)
        wo0 = ev_w_out.rearrange("(kc p) n -> p kc n", p=128)
        u_xb = [ws.plan((16, 256), wi0[:, :, 2048 + i * 256:2048 + (i + 1) * 256]) for i in range(4)]
        u_u = [ws.plan((16, 256), wi0[:, :, i * 256:(i + 1) * 256]) for i in range(4)]
        u_v = [ws.plan((16, 256), wi0[:, :, 1024 + i * 256:1024 + (i + 1) * 256]) for i in range(4)]
        u_o = [ws.plan((16, 256), wo0[:, :, i * 256:(i + 1) * 256]) for i in range(8)]
        plan0 = plan_ffn_weights(ws, w_up[0], w_down[0])
        wi1 = od_w_in.rearrange("(kc p) n -> p kc n", p=128)
        wo1 = od_w_out.rearrange("(kc p) n -> p kc n", p=128)
        u_pre = [ws.plan((16, 256), wi1[:, :, h * 512 + 256:h * 512 + 512]) for h in range(NH)]
        u_main = []
        for h in range(NH):
            fi = ws.plan((16, 256), wi1[:, :, h * 512 + 256:h * 512 + 512])
            qg = ws.plan((16, 256), wi1[:, :, h * 512:h * 512 + 256])
            u_main.append((fi, qg))
        u_o1 = [ws.plan((16, 256), wo1[:, :, i * 256:(i + 1) * 256]) for i in range(8)]
        plan1 = plan_ffn_weights(ws, w_up[1], w_down[1])

        x0b = cv(R0, (NCH, TH), BF16)
        zf = cv(R0, (NCH, T + 2), F32)
        xb = cv(R1, (NCH, T + 2), BF16)
        pp = cv(R1, (8, TH), BF16)
        g1 = [cv(R1 + 18432, (TH,), F32), cv(R1 + 23040, (TH,), F32)]
        xbf = cv(R1 + 27648, (16 + TH,), F32)
        u = cv(R2, (8, TH), BF16)
        vt = cv(R2 + 18432, (9, 1024), BF16)
        g2p = [cv(R2 + 36864, (16 + TH,), F32), cv(R2 + 41536, (16 + TH,), F32)]
        g2 = [t[:, 16:16 + TH] for t in g2p]
        tA, tB = g2p
        wsT = cv(R2 + 46208, (8, 128), BF16)
        mask = cv(R2 + 48256, (128,), F32)
        bsT = cv(R2 + 48768, (8, 128), F32)
        lnv = cv(R2 + 52864, (2, 1024), F32)
        wp = cv(R2 + 61056, (8, 256), BF16)
        wsTf = g1[1][:, 0:1024].rearrange("p (h t) -> p h t", h=8)

        chp = S.new_chan(total=True)
        chx = S.new_chan(total=True)
        S.dma("sp", wsTf, wsT_d.rearrange("p (h t) -> p h t", h=8), chp)
        S.dma("sp", mask, mask_d, chp)
        S.dma("sp", bsT, bsT_d.rearrange("p (h t) -> p h t", h=8), chp)
        S.dma("sp", lnv, lnv_d.rearrange("p (a c) -> p a c", a=2), chp)
        S.dma("sp", psc[:, :], psc_d, chp)
        S.dma("sp", rc[:, :, :], rc_d.rearrange("p (g j) -> p g j", g=4), chp)
        S.dma("sp", flag[:, :], flag_d, chp)
        S.dma("sp", oh[:, :], oh_d, chp)
        S.dma("sp", ln1gb[:, :, :, :], ln1gb_d.rearrange("p (l c j) -> p l c j", l=2, j=2), chp)
        S.dma("sp", ln2gb[:, :, :, :], ln2gb_d.rearrange("p (l c j) -> p l c j", l=2, j=2), chp)
        S.dma("sp", cwb[:, :, :, :], cwb_d.rearrange("p (l c j) -> p l c j", l=2, j=4), chp)
        S.dma("sp", gn[:, :], gn_d, chp)
        S.dma("pool", wp, wp_d.rearrange("(a p) n -> p a n", p=128), chx)
        for c in range(NCH):
            S.dma("pool", x0b[:, c, :], x0T[c * 128:(c + 1) * 128, :], chx)
        ws.release(-1)
        S.memset("dve", ones[:, :], 1.0)
        S.memset("dve", eps[:, :], LN_EPS)
        S.memset("dve", xbf[:, 0:16], 0.0)
        S.memset("dve", tA[:, 0:16], 0.0)
        S.memset("dve", tB[:, 0:16], 0.0)
        for h in range(8):
            S.tt("dve", wsT[:, h, :], wsTf[:, h, :], mask, ALU.mult)

        GR = (0, 1536)
        TT3 = ((0, 512), (512, 512), (1024, 128))

        def proj_fm(wt, sub, g0):
            for kc in range(NCH):
                for (t0, w) in TT3:
                    S.mm(ps[:, g0 + t0:g0 + t0 + w], wt[:, kc, sub * 128:(sub + 1) * 128], x0b[:, kc, t0:t0 + w],
                         start=(kc == 0), stop=(kc == NCH - 1))
        gi = 0
        for i in range(4):
            wt = ws.get(u_xb[i])
            for sub in range(2):
                c = i * 2 + sub
                g = c // 2
                g0 = GR[gi % 2]
                gi += 1
                proj_fm(wt, sub, g0)
                S.act(xbf[:, 16:16 + TH], ps[:, g0:g0 + TH], AF.Identity)
                src = xbf
                dsts = [tA, tB]
                for k in range(g + 1):
                    sh = 1 << k
                    dst = dsts[k % 2]
                    S.tt("dve", dst[:, 16:16 + TH], src[:, 16:16 + TH], src[:, 16 - sh:16 + TH - sh], ALU.add)
                    src = dst
                win = B_WINDOWS[g]
                S.stt(pp[:, c, :], src[:, 16:16 + TH], 1.0 / win, xbf[:, 16:16 + TH], ALU.mult, ALU.subtract)
                S.tt("dve", small[:, 0:16], src[:, 16 + 128:16 + 144], rc[:, g, :], ALU.mult)
                S.tt("dve", pp[:, c, 128:144], small[:, 0:16], xbf[:, 16 + 128:16 + 144], ALU.subtract)
            ws.release(u_xb[i])
        for i in range(4):
            wt = ws.get(u_u[i])
            for sub in range(2):
                c = i * 2 + sub
                g0 = GR[gi % 2]
                proj_fm(wt, sub, g0)
                emit_gelu(S, u[:, c, :], ps[:, g0:g0 + TH], g1[gi % 2], g2[gi % 2])
                gi += 1
            ws.release(u_u[i])
        wv = [ws.get(k) for k in u_v]
        for tk in range(9):
            vr = g1[tk % 2]
            for cg in range(4):
                r0 = 3072 + ((tk * 4 + cg) % 2) * 512
                for kc in range(NCH):
                    S.mm(ps[:, r0:r0 + 256], x0b[:, kc, tk * 128:(tk + 1) * 128], wv[cg][:, kc, :],
                         start=(kc == 0), stop=(kc == NCH - 1))
                emit_gelu(S, vr[:, cg * 256:(cg + 1) * 256], ps[:, r0:r0 + 256],
                          g2[0][:, cg * 256:(cg + 1) * 256], g2[1][:, cg * 256:(cg + 1) * 256])
            sq = g2[0]
            S.add("dve", lambda e, vr=vr: e.reduce_sum(out=st[:, 0:1], in_=vr[:, 0:1024], axis=mybir.AxisListType.X),
                  reads=[vr[:, 0:1024]], writes=[st[:, 0:1]])
            S.act(sq[:, 0:1024], vr[:, 0:1024], AF.Square)
            S.add("dve", lambda e, sq=sq: e.reduce_sum(out=st[:, 1:2], in_=sq[:, 0:1024], axis=mybir.AxisListType.X),
                  reads=[sq[:, 0:1024]], writes=[st[:, 1:2]])
            S.ts("dve", st[:, 2:3], st[:, 0:1], 1.0 / 1024, None, ALU.mult)
            S.tt("dve", st[:, 3:4], st[:, 2:3], st[:, 2:3], ALU.mult)
            S.stt(st[:, 4:5], st[:, 1:2], 1.0 / 1024, st[:, 3:4], ALU.mult, ALU.subtract)
            S.act(st[:, 5:6], st[:, 4:5], AF.Sqrt, bias=eps[:, 0:1], scale=1.0)
            S.add("dve", lambda e: e.reciprocal(out=st[:, 6:7], in_=st[:, 5:6]), reads=[st[:, 5:6]],
                  writes=[st[:, 6:7]])
            S.ts("dve", vr[:, 0:1024], vr[:, 0:1024], st[:, 2:3], st[:, 6:7], ALU.subtract, ALU.mult)
            S.tt("dve", vr[:, 0:1024], vr[:, 0:1024], lnv[:, 0, :], ALU.mult)
            S.tt("dve", vt[:, tk, :], vr[:, 0:1024], lnv[:, 1, :], ALU.add)
        ws.release(u_v[3])
        for tk in range(9):
            for half in range(2):
                r0 = 2048 + half * 512
                for hh in range(4):
                    h = half * 4 + hh
                    S.mm(ps[:, r0 + hh * 128:r0 + (hh + 1) * 128], vt[:, tk, h * 128:(h + 1) * 128], wsT[:, h, :],
                         start=True, stop=True)
                tmp = g2[half][:, 0:512].rearrange("p (h t) -> p h t", h=4)
                S.tt("dve", tmp, ps[:, r0:r0 + 512].rearrange("p (h t) -> p h t", h=4),
                     bsT[:, half * 4:half * 4 + 4, :], ALU.add)
                uu = u[:, half * 4:half * 4 + 4, tk * 128:(tk + 1) * 128]
                S.tt("dve", uu, tmp, uu, ALU.mult)
        for g in range(4):
            for oc in range(2):
                g0 = GR[oc]
                for kc in range(2):
                    for (t0, w) in TT3:
                        S.mm(ps[:, g0 + t0:g0 + t0 + w], wp[:, g * 2 + kc, oc * 128:(oc + 1) * 128],
                             pp[:, g * 2 + kc, t0:t0 + w], start=(kc == 0), stop=(kc == 1))
            for oc in range(2):
                g0 = GR[oc]
                c = g * 2 + oc
                S.act(pp[:, c, :], ps[:, g0:g0 + TH], AF.Identity, scale=psc[:, c:c + 1])
        xs = [g1[0], g1[1]]
        chs = [S.new_chan(), S.new_chan()]
        for i in range(8):
            wt = ws.get(u_o[i])
            for sub in range(2):
                oc = i * 2 + sub
                g0 = GR[oc % 2]
                xst = xs[oc % 2]
                S.dma("sp", xst[:, 0:T + 2], x0T[oc * 128:(oc + 1) * 128, 126:TH], chs[oc % 2])
                for kc in range(NCH):
                    src = u[:, kc, :] if kc < 8 else pp[:, kc - 8, :]
                    lw = wt[:, kc, sub * 128:(sub + 1) * 128]
                    S.mm(ps[:, g0 + 510:g0 + 512], lw, src[:, 126:128], start=(kc == 0), stop=(kc == NCH - 1))
                    S.mm(ps[:, g0 + 512:g0 + 1024], lw, src[:, 128:640], start=(kc == 0), stop=(kc == NCH - 1))
                    S.mm(ps[:, g0 + 1024:g0 + 1536], lw, src[:, 640:1152], start=(kc == 0), stop=(kc == NCH - 1))
                S.stt(zf[:, oc, :], xst[:, 0:T + 2], ALPHA, ps[:, g0 + 510:g0 + 1536], ALU.mult, ALU.add)
            ws.release(u_o[i])
        tm_ln1 = {"eps": eps, "mean": g2[0], "rstd": g2[1],
                  "zb": [cv(R2 + k * 2052, (T + 2,), BF16) for k in range(2)],
                  "zs": [cv(R2 + (2 + k) * 2052, (T + 2,), BF16) for k in range(2)]}
        emit_ln(S, zf, 0, T + 2, ln1gb[:, 0, :, :], ones, tm_ln1, ps, [lambda c: zf[:, c, :]])
        S.ts("dve", zf[:, :, 0:2], zf[:, :, 0:2], flag[:, 0:1], None, ALU.mult)
        for c in range(NCH):
            S.act(xb[:, c, :], zf[:, c, :], AF.Identity)

        gq = cv(R2, (12, T), BF16)
        ft = [cv(R2 + 24576 + k * 4096, (T,), F32) for k in range(6)]
        tm_ffn = {"a": ft[0:2], "v": ft[2:4], "s": ft[4:6], "eps": eps, "mean": ft[0], "rstd": ft[1],
                  "zb": [cv(R2 + 49152 + k * 2048, (T,), BF16) for k in range(2)],
                  "zs": [cv(R2 + 53248 + k * 2048, (T,), BF16) for k in range(2)]}
        xb2 = cv(R1, (NCH, T), BF16)
        emit_ffn(S, ws, plan0, zf, xb, cwb[:, 0, :, :], gq, tm_ffn, ps)
        emit_ln(S, zf, 2, T, ln2gb[:, 0, :, :], ones, tm_ffn, ps,
                [lambda c: xb2[:, c, :], lambda c: zf[:, c, 2:T + 2]])
        chsp = [S.new_chan() for _ in range(NCH)]
        for c in range(NCH):
            S.dma("sp", xsp[c * 128:(c + 1) * 128, :], zf[:, c, 2:T + 2], chsp[c])

        def mkset(k):
            o0 = R0 + k * 32768
            d_ = {"A": cv(o0, (T,), F32), "B": cv(o0 + 4096, (T,), F32), "C": cv(o0 + 8192, (T,), F32),
                  "kend": cv(o0 + 12288, (T,), BF16), "kdec": cv(o0 + 14336, (T,), BF16),
                  "qdec": cv(o0 + 16384, (T,), BF16), "ibf": cv(o0 + 18432, (T,), BF16),
                  "sg": cv(o0 + 20480, (T,), BF16), "osq": cv(o0 + 22528, (T,), BF16),
                  "attm": cv(o0 + 24576, (8, 128), BF16), "kt0": cv(o0 + 26624, (8, 128), BF16),
                  "kt1": cv(o0 + 28672, (8, 128), BF16), "vtk": cv(o0 + 30720, (8, 128), BF16),
                  "Sb": cv(R2 + 32768, (NCK, 128), BF16) if k == 0 else cv(R2 + 61472, (NCK, 128), BF16),
                  "dec": sb("dec%d" % k, [128, NCK]), "Sf": [sb("Sf%d_%d" % (k, i), [128, 128]) for i in range(2)],
                  "Pp": [sb("Pp%d_%d" % (k, i), [128, 128]) for i in range(2)],
                  "upst": cv(R2 + 59408, (4, 129), F32) if k == 0 else sb("upst1", [128, 4, 129]),
                  "stg": cv(R2 + 57344, (4, 129), F32),
                  "chu": S.new_chan(), "chst": S.new_chan()}
            return d_
        sets = [mkset(0), mkset(1)]
        y = cv(R2, (NH, T), BF16)
        xs1 = [cv(R2 + 40960, (T,), F32), cv(R2 + 45056, (T,), F32)]
        tm_ln1b = {"eps": eps, "mean": xs1[0], "rstd": xs1[1],
                   "zb": [cv(R2 + 49152 + k * 2048, (T,), BF16) for k in range(2)],
                   "zs": [cv(R2 + 53248 + k * 2048, (T,), BF16) for k in range(2)]}
        chc = S.new_chan(total=True)
        cst = {}
        lbp = sb("lbp_s", [128, 2, NH]); cst["lb"] = sb("lb", [128, NH]); cst["oml"] = sb("oml", [128, NH])
        cst["mask2"] = sb("mask2_s", [128, 128]); identf = sb("identf", [128, 128]); cst["ident"] = sb("ident_s", [128, 128], BF16)
        cst["pm"] = sb("pm", [128, 2])
        cst["rm"] = cv(R2 + 36864, (T,), F32)
        S.dma("sp", lbp[:, :, :], lbp_d.rearrange("p (l h) -> p l h", l=2), chc)
        S.dma("sp", cst["mask2"][:, :], mask2_d, chc)
        S.dma("sp", identf[:, :], ident_d, chc)
        S.copy("dve", cst["ident"][:, :], identf[:, :])
        S.tt("dve", cst["lb"][:, :], lbp[:, 1, :], lbp[:, 0, :], ALU.subtract)
        S.act(cst["lb"][:, :], cst["lb"][:, :], AF.Sigmoid)
        S.ts("dve", cst["oml"][:, :], cst["lb"][:, :], -1.0, 1.0, ALU.mult, ALU.add)
        S.memset("dve", cst["rm"], 1.0)
        S.memset("dve", cst["rm"].rearrange("p (c t) -> p c t", t=CH)[:, :, 0:1], 0.0)
        S.memset("dve", cst["pm"][:, :], 0.0)
        S.memset("dve", cst["pm"][0:64, 0:1], 1.0)
        S.memset("dve", cst["pm"][64:128, 1:2], 1.0)
        oh3 = oh[:, 0:4].rearrange("p (j o) -> p j o", o=1)
        G0, G1, PB5, O0 = 0, 1024, 2560, 3072

        def proj1(wt, blk, g0):
            for kc in range(NCH):
                for t_ in range(2):
                    S.mm(ps[:, g0 + t_ * 512:g0 + (t_ + 1) * 512], wt[:, kc, blk * 128:(blk + 1) * 128],
                         xb2[:, kc, t_ * 512:(t_ + 1) * 512], start=(kc == 0), stop=(kc == NCH - 1))

        def cc_op(idx, src_t, dst_t):
            if use_cc:
                def fn(e):
                    e.collective_compute("AllReduce", ALU.add, replica_groups=SEQ_GROUPS,
                                         ins=[src_t.ap().opt()], outs=[dst_t.ap().opt()]).then_inc(ccsem[idx])
                    return None
                S.add("pool", fn, reads=[src_t.ap()], writes=[])

                def fn2(e):
                    e.wait_ge(ccsem[idx], 1)
                    return e.memset(tiny[:, idx:idx + 1], 0.0)
                return lambda: S.add("pool", fn2, reads=[], writes=[dst_t.ap(), tiny[:, idx:idx + 1]])
            else:
                chq = S.new_chan()
                S.dma("sp", dst_t.ap(), src_t.ap(), chq)
                return lambda: None

        def scan_group(q, g4, st_):
            for cc in range(4):
                c = g4 * 4 + cc
                j, par = divmod(c, 2)
                S.mm(ps[:, PB5 + cc * 128:PB5 + (cc + 1) * 128], (q["kt0"], q["kt1"])[par][:, j, :], q["vtk"][:, j, :],
                     start=True, stop=True)
            for cc in range(4):
                c = g4 * 4 + cc
                cur = st_["cur"]
                S.stt(q["Sf"][1 - cur][:, :], q["Sf"][cur][:, :], q["dec"][:, c:c + 1],
                      ps[:, PB5 + cc * 128:PB5 + (cc + 1) * 128], ALU.mult, ALU.add)
                st_["cur"] = 1 - cur
                if st_["sb"] and c + 1 < NCK:
                    S.act(q["Sb"][:, c + 1, :], q["Sf"][1 - cur][:, :], AF.Identity)

        def interleave(bsteps, asteps, after):
            ai = 0
            for bi, bstep in enumerate(bsteps):
                bstep()
                while ai < len(asteps) and after[ai] == bi:
                    asteps[ai]()
                    ai += 1
            while ai < len(asteps):
                asteps[ai]()
                ai += 1

        cc_done = []

        def pre_A(h):
            q = sets[h % 2]

            def a1():
                q["wt"] = ws.get(u_pre[h])
                proj1(q["wt"], 0, G0)

            def a2():
                proj1(q["wt"], 1, G1)
                ws.release(u_pre[h])
                hgrn_gates(S, h, cst, ps[:, G0:G0 + T], q["A"], q["B"], q["C"], q["kend"], q["dec"][:, :])
                S.act(q["ibf"], ps[:, G1:G1 + T], AF.Identity)
                C3 = q["C"].rearrange("p (c t) -> p c t", t=CH)
                S.add("dve", lambda e, C3=C3, h=h: e.reduce_sum(out=Dd[:, h:h + 1], in_=C3[:, :, CH - 1:CH],
                                                               axis=mybir.AxisListType.XY),
                      reads=[q["C"]], writes=[Dd[:, h:h + 1]])
                S.act(Dd[:, h:h + 1], Dd[:, h:h + 1], AF.Exp)
            return [a1, a2]

        def pre_B(h):
            q = sets[h % 2]
            st_ = {"cur": 0, "sb": False}

            def b1():
                hgrn_transposes(S, cst, psb, q["kend"], q["kt0"], q["kt1"])
                hgrn_transposes(S, cst, psb, q["ibf"], q["vtk"])
                S.memset("dve", q["Sf"][0][:, :], 0.0)

            def bfin():
                fin = st_["cur"]
                for j in range(4):
                    S.ts("dve", q["stg"][:, j, 0:128], q["Sf"][fin][:, :], oh[:, j:j + 1], None, ALU.mult)
                S.ts("dve", q["stg"][:, :, 128:129], oh3, Dd[:, h:h + 1], None, ALU.mult)
                g, hl = divmod(h, 4)
                S.dma("sp", ccin[g].ap().rearrange("(j l d) n -> d j l n", j=4, l=4)[:, :, hl, :], q["stg"][:, :, :],
                      q["chst"])
                if hl == 3:
                    cc_done.append(cc_op(g, ccin[g], ccout[g]))
            return [b1] + [lambda g4=g4: scan_group(q, g4, st_) for g4 in range(4)] + [bfin]

        for stp in pre_A(0):
            stp()
        for h in range(NH):
            nxt = pre_A(h + 1) if h + 1 < NH else []
            interleave(pre_B(h), nxt, [0, 2])

        def main_A(h):
            q = sets[h % 2]
            g, hl = divmod(h, 4)
            fi, qg = u_main[h]

            def a1():
                if hl == 0:
                    cc_done[g]()
                up = q["upst"]
                S.dma("sp", up[:, :, :], ccout[g].ap().rearrange("(j l d) n -> d j l n", j=4, l=4)[:, :, hl, :], q["chu"])
                Pp_, Sf_ = q["Pp"], q["Sf"]
                S.stt(Pp_[0][:, :], up[:, 0, 0:128], up[:, 1, 128:129], up[:, 1, 0:128], ALU.mult, ALU.add)
                S.stt(Pp_[1][:, :], Pp_[0][:, :], up[:, 2, 128:129], up[:, 2, 0:128], ALU.mult, ALU.add)
                S.ts("dve", Sf_[0][:, :], up[:, 0, 0:128], oh[:, 1:2], None, ALU.mult)
                S.stt(Sf_[0][:, :], Pp_[0][:, :], oh[:, 2:3], Sf_[0][:, :], ALU.mult, ALU.add)
                S.stt(Sf_[0][:, :], Pp_[1][:, :], oh[:, 3:4], Sf_[0][:, :], ALU.mult, ALU.add)
                q["wt"] = ws.get(fi)
                proj1(q["wt"], 0, G0)

            def a2():
                proj1(q["wt"], 1, G1)
                ws.release(fi)
                hgrn_gates(S, h, cst, ps[:, G0:G0 + T], q["A"], q["B"], q["C"], q["kend"], q["dec"][:, :],
                           kdec_bf=q["kdec"], eC=q["A"])
                S.act(q["ibf"], ps[:, G1:G1 + T], AF.Identity)

            def a3():
                q["wt"] = ws.get(qg)
                proj1(q["wt"], 0, G0)

            def a4():
                proj1(q["wt"], 1, G1)
                ws.release(qg)
                S.act(q["B"], ps[:, G0:G0 + T], AF.Silu)
                S.tt("dve", q["qdec"], q["B"], q["A"], ALU.mult)
                S.act(q["sg"], ps[:, G1:G1 + T], AF.Sigmoid)
            return [a1, a2, a3, a4]

        def main_B(h):
            q = sets[h % 2]
            st_ = {"cur": 0, "sb": True}

            def b1():
                hgrn_transposes(S, cst, psb, q["kend"], q["kt0"], q["kt1"])
                hgrn_transposes(S, cst, psb, q["ibf"], q["vtk"])
                S.act(q["Sb"][:, 0, :], q["Sf"][0][:, :], AF.Identity)

            def batt(half):
                for jj in range(4):
                    j = half * 4 + jj
                    S.mm(ps[:, PB5 + jj * 128:PB5 + (jj + 1) * 128], q["kdec"][:, j * 128:(j + 1) * 128],
                         q["qdec"][:, j * 128:(j + 1) * 128], start=True, stop=True)
                for jj in range(4):
                    j = half * 4 + jj
                    S.tt("dve", q["attm"][:, j, :], ps[:, PB5 + jj * 128:PB5 + (jj + 1) * 128], cst["mask2"][:, :],
                         ALU.mult)

            def bo():
                for j in range(8):
                    S.mm(ps[:, O0 + j * 128:O0 + (j + 1) * 128], q["vtk"][:, j, :], q["attm"][:, j, :], start=True,
                         stop=False)
                    S.mm(ps[:, O0 + j * 128:O0 + j * 128 + 64], q["Sb"][:, 2 * j, :], q["qdec"][:, j * 128:j * 128 + 64],
                         start=False, stop=False)
                    S.mm(ps[:, O0 + j * 128 + 64:O0 + (j + 1) * 128], q["Sb"][:, 2 * j + 1, :],
                         q["qdec"][:, j * 128 + 64:(j + 1) * 128], start=False, stop=True)
                S.act(q["osq"], ps[:, O0:O0 + T], AF.Square)

            def bnorm():
                for t_ in range(2):
                    sl = slice(t_ * 512, (t_ + 1) * 512)
                    S.mm(ps[:, PB5:PB5 + 512], ones[:, :], q["osq"][:, sl], start=True, stop=True)
                    S.act(q["A"][:, sl], ps[:, PB5:PB5 + 512], AF.Ln, bias=eps[:, 0:1], scale=1.0 / 128)
                S.act(q["A"], q["A"], AF.Exp, scale=-0.5)
                S.stt(q["C"], ps[:, O0:O0 + T], gn[:, h:h + 1], q["A"], ALU.mult, ALU.mult)
                S.tt("dve", y[:, h, :], q["C"], q["sg"], ALU.mult)
            return ([b1] + [lambda g4=g4: scan_group(q, g4, st_) for g4 in range(4)]
                    + [lambda: batt(0), lambda: batt(1), bo, bnorm])

        for stp in main_A(0):
            stp()
        for h in range(NH):
            nxt = main_A(h + 1) if h + 1 < NH else []
            interleave(main_B(h), nxt, [0, 2, 4, 6])
        chs1 = [S.new_chan(), S.new_chan()]
        for i in range(8):
            wt = ws.get(u_o1[i])
            for sub in range(2):
                oc = i * 2 + sub
                g0 = (oc % 2) * 1024
                xst = xs1[oc % 2]
                S.dma("sp", xst, xsp[oc * 128:(oc + 1) * 128, :], chs1[oc % 2])
                for kc in range(NCH):
                    for t_ in range(2):
                        S.mm(ps[:, g0 + t_ * 512:g0 + (t_ + 1) * 512], wt[:, kc, sub * 128:(sub + 1) * 128],
                             y[:, kc, t_ * 512:(t_ + 1) * 512], start=(kc == 0), stop=(kc == NCH - 1))
                S.stt(zf[:, oc, 2:T + 2], xst, ALPHA, ps[:, g0:g0 + T], ALU.mult, ALU.add)
            ws.release(u_o1[i])
        emit_ln(S, zf, 2, T, ln1gb[:, 1, :, :], ones, tm_ln1b, ps,
                [lambda c: xb[:, c, 2:T + 2], lambda c: zf[:, c, 2:T + 2]])
        for j in range(4):
            S.ts("dve", hstg[:, j, :].rearrange("p (c t) -> p c t", t=2), zf[:, :, T:T + 2], oh[:, j:j + 1], None,
                 ALU.mult)
        chh = S.new_chan()
        S.dma("sp", cch_in.ap().rearrange("(j p) n -> p j n", p=128), hstg[:, :, :], chh)
        done_h = cc_op(4, cch_in, cch_out)
        done_h()
        chh2 = S.new_chan()
        S.dma("sp", hld[:, :, :], cch_out.ap().rearrange("(j p) n -> p j n", p=128), chh2)
        S.ts("dve", hsum[:, :], hld[:, 0, :], oh[:, 4:5], None, ALU.mult)
        for j in range(1, 4):
            S.stt(hsum[:, :], hld[:, j, :], oh[:, 4 + j:5 + j], hsum[:, :], ALU.mult, ALU.add)
        S.act(xb[:, :, 0:2], hsum[:, :].rearrange("p (c t) -> p c t", t=2), AF.Identity)
        emit_ffn(S, ws, plan1, zf, xb, cwb[:, 1, :, :], gq, tm_ffn, ps)
        emit_ln(S, zf, 2, T, ln2gb[:, 1, :, :], ones, tm_ffn, ps, [lambda c: zf[:, c, 2:T + 2]])
        cho = [S.new_chan() for _ in range(4)]
        for c in range(NCH):
            S.dma("sp", outT[c * 128:(c + 1) * 128, :], zf[:, c, 2:T + 2], cho[c % 4])
        S.emit(nc, es, final_chans=cho)
    return nc, S


def fused_inputs(inp):
    f32 = np.float32
    maps = prep_mix0_inputs(inp["x"], inp["ev_w_in"], inp["ev_ln_v_g"], inp["ev_ln_v_b"], inp["ev_w_s"],
                            inp["ev_b_s"], inp["ev_w_pool"], inp["ev_pool_scale"], inp["ev_w_out"],
                            inp["ln1_g"], inp["ln1_b"])
    ln1gb = np.stack([np.stack([_pm(inp["ln1_g"][l], 16), _pm(inp["ln1_b"][l], 16)], axis=-1) for l in range(2)], axis=1)
    ln2gb = np.stack([np.stack([_pm(inp["ln2_g"][l], 16), _pm(inp["ln2_b"][l], 16)], axis=-1) for l in range(2)], axis=1)
    cwbs = []
    for l in range(2):
        cw = np.asarray(inp["ffn_conv_w"][l], f32)
        cb = np.asarray(inp["ffn_conv_b"][l], f32)
        cwbs.append(np.stack([_pm(cw[0], 88), _pm(cw[1], 88), _pm(cw[2], 88), _pm(cb, 88)], axis=-1))
    cwb = np.stack(cwbs, axis=1)
    common = {
        "ln1gb": np.ascontiguousarray(ln1gb.reshape(128, 64)), "ln2gb": np.ascontiguousarray(ln2gb.reshape(128, 64)),
        "cwb": np.ascontiguousarray(cwb.reshape(128, 2 * 88 * 4)),
        "w_up0": np.ascontiguousarray(inp["ffn_w_up"][0], f32), "w_up1": np.ascontiguousarray(inp["ffn_w_up"][1], f32),
        "w_down0": np.ascontiguousarray(inp["ffn_w_down"][0], f32),
        "w_down1": np.ascontiguousarray(inp["ffn_w_down"][1], f32),
        "od_w_in": regroup_w_in(inp["od_w_in"][0]), "od_w_out": np.ascontiguousarray(inp["od_w_out"][0], f32),
        "gn": _pm(inp["od_norm_g"][0], NH)}
    common.update(hgrn_const_inputs(inp["lb_param"]))
    out = []
    for c in range(NCORES):
        b, s = divmod(c, 4)
        m0 = maps[c]
        m = dict(common)
        for k in ("x0T", "wsT", "maskA", "bsT", "lnv", "w_pool", "pscale", "rcnt", "flag"):
            m[k] = m0[k]
        m["ev_w_in"] = m0["w_in"]
        m["ev_w_out"] = m0["w_out"]
        oh = np.zeros((128, 8), f32)
        oh[:, s] = 1.0
        if s > 0:
            oh[:, 4 + s - 1] = 1.0
        m["oh"] = oh
        out.append(m)
    return out


def kernel(**inputs):
    nc, _ = build_fused(use_cc=True)
    maps = fused_inputs(inputs)
    res = run_bass_kernel_spmd(nc, maps, core_ids=list(range(NCORES))).results
    out = np.zeros((2, 4 * T, D), np.float32)
    for c in range(NCORES):
        b, s = divmod(c, 4)
        out[b, s * T:(s + 1) * T] = res[c]["outT"].T
    return out
```

```python
import numpy as np
from contextlib import ExitStack
import concourse.bass as bass
import concourse.mybir as mybir
from concourse.bass_utils import run_bass_kernel_spmd

F32 = mybir.dt.float32
BF16 = mybir.dt.bfloat16
AF = mybir.ActivationFunctionType
ALU = mybir.AluOpType

D = 2048
NCH = 16
T = 1024
NCORES = 8
DFF = 5632
NFF = 44
ALPHA = 4.0 ** 0.25
LN_EPS = 1e-5
ENGS = ("pe", "act", "dve", "pool", "sp")
_DT_SIZE = {F32: 4, BF16: 2}


def _dsize(dt):
    return _DT_SIZE.get(dt, 4)


class _Op:
    __slots__ = ("eng", "idx", "fn", "deps", "chan", "chan_val", "signal", "val")


class Sched:
    def __init__(self):
        self.ops = {e: [] for e in ENGS}
        self.track = {}
        self.chan_cnt = []
        self.chan_total = []

    @staticmethod
    def _rng(ap):
        t = ap.tensor
        name = t.name
        sp = str(ap.space) if hasattr(ap, "space") else ""
        pat = ap.ap
        esz = _dsize(ap.dtype)
        if "DRAM" in sp.upper() or "Dram" in type(t).__name__ or "DRam" in type(t).__name__:
            ext = 1
            for (st, cnt) in pat:
                ext += abs(st) * (cnt - 1)
            return name, ap.offset * esz, (ap.offset + ext) * esz
        pstride = pat[0][0]
        lo = ap.offset % pstride if pstride > 0 else ap.offset
        ext = 1
        for (st, cnt) in pat[1:]:
            ext += abs(st) * (cnt - 1)
        return name, lo * esz, (lo + ext) * esz

    def _touch(self, name, lo, hi, op, is_write, deps):
        segs = self.track.setdefault(name, [])
        new = []
        covered = []
        for s in segs:
            slo, shi, w, rs = s
            if shi <= lo or slo >= hi:
                new.append(s)
                continue
            if slo < lo:
                new.append([slo, lo, w, list(rs)])
            if shi > hi:
                new.append([hi, shi, w, list(rs)])
            olo, ohi = max(slo, lo), min(shi, hi)
            if w is not None:
                deps.add(w)
            if is_write:
                for r in rs:
                    deps.add(r)
            else:
                covered.append([olo, ohi, w, rs + [op]])
        if is_write:
            new.append([lo, hi, op, []])
        else:
            covered.sort(key=lambda s: s[0])
            cur = lo
            for c in covered:
                if c[0] > cur:
                    new.append([cur, c[0], None, [op]])
                new.append(c)
                cur = c[1]
            if cur < hi:
                new.append([cur, hi, None, [op]])
        self.track[name] = new

    def add(self, eng, fn, reads=(), writes=(), chan=None):
        o = _Op()
        o.eng = eng
        o.fn = fn
        o.chan = chan
        o.signal = False
        o.val = None
        o.chan_val = None
        deps = set()
        for ap in reads:
            if ap is None or isinstance(ap, (int, float)):
                continue
            n, lo, hi = self._rng(ap)
            self._touch(n, lo, hi, o, False, deps)
        for ap in writes:
            n, lo, hi = self._rng(ap)
            if eng == "pe":
                lo = (lo // 2048) * 2048
                hi = ((hi + 2047) // 2048) * 2048
            self._touch(n, lo, hi, o, True, deps)
        deps.discard(o)
        o.deps = deps
        if chan is not None:
            self.chan_cnt[chan] += 1
            o.chan_val = 16 * self.chan_cnt[chan]
        o.idx = len(self.ops[eng])
        self.ops[eng].append(o)
        return o

    def new_chan(self, total=False):
        self.chan_cnt.append(0)
        self.chan_total.append(total)
        return len(self.chan_cnt) - 1

    def emit(self, nc, es, final_chans=()):
        for e in ENGS:
            for o in self.ops[e]:
                for d in o.deps:
                    if d.chan is None:
                        d.signal = True
        for e in ENGS:
            c = 0
            for o in self.ops[e]:
                if o.chan is None and o.signal:
                    c += 1
                    o.val = c
        esem = {e: es.enter_context(nc.semaphore("s_" + e)) for e in ENGS}
        csem = [es.enter_context(nc.semaphore("c_%d" % i)) for i in range(len(self.chan_cnt))]
        block = es.enter_context(nc.Block())
        nwaits = {e: 0 for e in ENGS}

        def run(engname, eobj):
            seen = {}
            for o in self.ops[engname]:
                need = {}
                for d in o.deps:
                    if d.chan is not None:
                        key = ("c", d.chan)
                        v = 16 * self.chan_cnt[d.chan] if self.chan_total[d.chan] else d.chan_val
                    else:
                        if d.eng == engname and engname == "pe":
                            continue
                        key = ("e", d.eng)
                        v = d.val
                    if v > need.get(key, 0):
                        need[key] = v
                for key, v in need.items():
                    if v <= seen.get(key, 0):
                        continue
                    seen[key] = v
                    sem = csem[key[1]] if key[0] == "c" else esem[key[1]]
                    eobj.wait_ge(sem, v)
                    nwaits[engname] += 1
                inst = o.fn(eobj)
                if o.chan is not None:
                    inst.then_inc(csem[o.chan], 16)
                elif o.signal:
                    assert inst is not None
                    inst.then_inc(esem[engname], 1)
            if engname == "sp":
                for ch in final_chans:
                    if self.chan_cnt[ch] > 0:
                        eobj.wait_ge(csem[ch], 16 * self.chan_cnt[ch])

        @block.tensor
        def _(e):
            run("pe", e)

        @block.scalar
        def _(e):
            run("act", e)

        @block.vector
        def _(e):
            run("dve", e)

        @block.gpsimd
        def _(e):
            run("pool", e)

        @block.sync
        def _(e):
            run("sp", e)

        self.nwaits = nwaits

    def mm(self, out, lhsT, rhs, start=True, stop=True):
        return self.add("pe", lambda e: e.matmul(out, lhsT=lhsT, rhs=rhs, start=start, stop=stop),
                        reads=[lhsT, rhs], writes=[out])

    def transpose(self, out, in_, ident):
        return self.add("pe", lambda e: e.transpose(out, in_, ident), reads=[in_, ident], writes=[out])

    def act(self, out, in_, func, bias=None, scale=None):
        kw = {}
        rd = [in_]
        if bias is not None:
            kw["bias"] = bias
            rd.append(bias)
        if scale is not None:
            kw["scale"] = scale
            rd.append(scale)
        return self.add("act", lambda e: e.activation(out=out, in_=in_, func=func, **kw), reads=rd, writes=[out])

    def tt(self, eng, out, in0, in1, op):
        return self.add(eng, lambda e: e.tensor_tensor(out=out, in0=in0, in1=in1, op=op),
                        reads=[in0, in1], writes=[out])

    def ts(self, eng, out, in0, s1, s2, op0, op1=None):
        if op1 is None:
            return self.add(eng, lambda e: e.tensor_scalar(out=out, in0=in0, scalar1=s1, scalar2=None, op0=op0),
                            reads=[in0, s1], writes=[out])
        return self.add(eng, lambda e: e.tensor_scalar(out=out, in0=in0, scalar1=s1, scalar2=s2, op0=op0, op1=op1),
                        reads=[in0, s1, s2], writes=[out])

    def stt(self, out, in0, scalar, in1, op0, op1):
        return self.add("dve", lambda e: e.scalar_tensor_tensor(out=out, in0=in0, scalar=scalar, in1=in1,
                                                                op0=op0, op1=op1),
                        reads=[in0, scalar, in1], writes=[out])

    def copy(self, eng, out, in_):
        if eng == "act":
            return self.add("act", lambda e: e.copy(out=out, in_=in_), reads=[in_], writes=[out])
        return self.add(eng, lambda e: e.tensor_copy(out=out, in_=in_), reads=[in_], writes=[out])

    def memset(self, eng, ap, val):
        return self.add(eng, lambda e: e.memset(ap, val), writes=[ap])

    def dma(self, eng, out, in_, chan):
        return self.add(eng, lambda e: e.dma_start(out=out, in_=in_), reads=[in_], writes=[out], chan=chan)


class WeightStream:
    def __init__(self, S, nc, es, nslots, free_elems, name="wslot"):
        self.S = S
        self.slots = [es.enter_context(nc.sbuf_tensor("%s%d" % (name, i), [128, free_elems], BF16))
                      for i in range(nslots)]
        self.chans = [S.new_chan() for _ in range(nslots)]
        self.uses = []
        self.loaded = 0
        self.released = -1
        self.n = nslots

    def plan(self, shape, src):
        self.uses.append((shape, src))
        return len(self.uses) - 1

    def view(self, k):
        shape, _ = self.uses[k]
        sl = self.slots[k % self.n]
        n = 1
        for s in shape:
            n *= s
        v = sl[:, 0:n]
        if len(shape) == 2:
            return v.rearrange("p (a b) -> p a b", a=shape[0], b=shape[1])
        return v

    def _load_upto(self, k):
        while self.loaded < len(self.uses) and self.loaded <= k:
            j = self.loaded
            _, src = self.uses[j]
            self.S.dma("pool", self.view(j), src, self.chans[j % self.n])
            self.loaded += 1

    def get(self, k):
        assert k <= self.released + self.n, (k, self.released)
        self._load_upto(k)
        return self.view(k)

    def release(self, k):
        self.released = max(self.released, k)
        self._load_upto(self.released + self.n)


FF_QUARTERS = (12, 10, 12, 10)


def plan_ffn_weights(ws, w_up, w_down):
    plan = []
    base = 0
    wu = w_up.rearrange("(kc p) n -> p kc n", p=128)
    for q, nq in enumerate(FF_QUARTERS):
        ups = []
        for j in range(0, nq, 2):
            ca = base + j
            ua = ws.plan((16, 256), wu[:, :, ca * 128:ca * 128 + 256])
            uv = ws.plan((16, 256), wu[:, :, (NFF + ca) * 128:(NFF + ca) * 128 + 256])
            ups.append((ca, ua, uv))
        downs = []
        wd = w_down[base * 128:(base + nq) * 128, :].rearrange("(j p) n -> p j n", p=128)
        for op_ in range(8):
            downs.append((op_, ws.plan((nq, 256), wd[:, :, op_ * 256:(op_ + 1) * 256])))
        plan.append((base, nq, ups, downs))
        base += nq
    return plan


def emit_ffn(S, ws, plan, xf, xb, cwb, gq, tmps, ps):
    G = (0, 1536)
    gi = 0
    for (base, nq, ups, downs) in plan:
        for (ca, ua, uv) in ups:
            wa = ws.get(ua)
            wv = ws.get(uv)
            for sub in range(2):
                c_a = ca + sub
                c_v = NFF + ca + sub
                j = c_a - base
                tm = {}
                for which, (wt, cc) in enumerate(((wa, c_a), (wv, c_v))):
                    g0 = G[which]
                    for kc in range(NCH):
                        lw = wt[:, kc, sub * 128:(sub + 1) * 128]
                        S.mm(ps[:, g0 + 510:g0 + 512], lw, xb[:, kc, 0:2], start=(kc == 0), stop=(kc == NCH - 1))
                        S.mm(ps[:, g0 + 512:g0 + 1024], lw, xb[:, kc, 2:514], start=(kc == 0), stop=(kc == NCH - 1))
                        S.mm(ps[:, g0 + 1024:g0 + 1536], lw, xb[:, kc, 514:1026], start=(kc == 0),
                             stop=(kc == NCH - 1))
                    tmp = tmps["a" if which == 0 else "v"][gi % 2]
                    tm[which] = tmp
                    S.act(tmp[:, :], ps[:, g0 + 512:g0 + 1536], AF.Identity, bias=cwb[:, cc, 3:4],
                          scale=cwb[:, cc, 2:3])
                    S.stt(tmp[:, :], ps[:, g0 + 511:g0 + 1535], cwb[:, cc, 1:2], tmp[:, :], ALU.mult, ALU.add)
                    S.stt(tmp[:, :], ps[:, g0 + 510:g0 + 1534], cwb[:, cc, 0:1], tmp[:, :], ALU.mult, ALU.add)
                sa = tmps["s"][gi % 2]
                S.act(sa[:, :], tm[0][:, :], AF.Silu)
                S.tt("dve", gq[:, j, :], sa[:, :], tm[1][:, :], ALU.mult)
                gi += 1
            ws.release(uv)
        for (op_, ud) in downs:
            wd = ws.get(ud)
            for sub in range(2):
                oc = op_ * 2 + sub
                for tt_ in range(2):
                    for j in range(nq):
                        S.mm(ps[:, 3072 + tt_ * 512:3072 + (tt_ + 1) * 512], wd[:, j, sub * 128:(sub + 1) * 128],
                             gq[:, j, tt_ * 512:(tt_ + 1) * 512], start=(j == 0), stop=(j == nq - 1))
                if base == 0:
                    S.stt(xf[:, oc, 2:1026], xf[:, oc, 2:1026], ALPHA, ps[:, 3072:4096], ALU.mult, ALU.add)
                else:
                    S.tt("dve", xf[:, oc, 2:1026], xf[:, oc, 2:1026], ps[:, 3072:4096], ALU.add)
            ws.release(ud)


def emit_ln(S, zf, c0, n, gb, ones, tmps, ps, outs):
    nt = (n + 511) // 512
    zb = tmps["zb"]
    zs = tmps["zs"]
    for c in range(NCH):
        b0 = zb[c % 2]
        s0 = zs[c % 2]
        S.act(b0[:, 0:n], zf[:, c, c0:c0 + n], AF.Identity)
        S.act(s0[:, 0:n], zf[:, c, c0:c0 + n], AF.Square)
        for t_ in range(nt):
            w = min(512, n - t_ * 512)
            S.mm(ps[:, t_ * 512:t_ * 512 + w], ones[:, :], b0[:, t_ * 512:t_ * 512 + w], start=(c == 0),
                 stop=(c == NCH - 1))
            S.mm(ps[:, 2048 + t_ * 512:2048 + t_ * 512 + w], ones[:, :], s0[:, t_ * 512:t_ * 512 + w],
                 start=(c == 0), stop=(c == NCH - 1))
    mean = tmps["mean"]
    rstd = tmps["rstd"]
    S.ts("dve", mean[:, 0:n], ps[:, 0:n], 1.0 / D, None, ALU.mult)
    S.tt("dve", rstd[:, 0:n], mean[:, 0:n], mean[:, 0:n], ALU.mult)
    S.stt(rstd[:, 0:n], ps[:, 2048:2048 + n], 1.0 / D, rstd[:, 0:n], ALU.mult, ALU.subtract)
    S.act(rstd[:, 0:n], rstd[:, 0:n], AF.Sqrt, bias=tmps["eps"][:, 0:1], scale=1.0)
    S.add("dve", lambda e: e.reciprocal(out=rstd[:, 0:n], in_=rstd[:, 0:n]), reads=[rstd[:, 0:n]],
          writes=[rstd[:, 0:n]])
    for c in range(NCH):
        zc = zf[:, c, c0:c0 + n]
        S.tt("dve", zc, zc, mean[:, 0:n], ALU.subtract)
        S.tt("dve", zc, zc, rstd[:, 0:n], ALU.mult)
        for i, dst in enumerate(outs):
            S.act(dst(c), zc, AF.Identity, bias=gb[:, c, 1:2], scale=gb[:, c, 0:1])


def build_ffn_launch():
    nc = bass.Bass("TRN2", target_bir_lowering=False)
    xT = nc.dram_tensor("xT", [D, T + 2], F32, kind="ExternalInput").ap()
    w_up = nc.dram_tensor("w_up", [D, 2 * DFF], F32, kind="ExternalInput").ap()
    w_down = nc.dram_tensor("w_down", [DFF, D], F32, kind="ExternalInput").ap()
    cwb_d = nc.dram_tensor("cwb", [128, 88 * 4], F32, kind="ExternalInput").ap()
    gb_d = nc.dram_tensor("ln2gb", [128, 32], F32, kind="ExternalInput").ap()
    yT = nc.dram_tensor("yT", [D, T], F32, kind="ExternalOutput").ap()
    S = Sched()
    with ExitStack() as es:
        xf = es.enter_context(nc.sbuf_tensor("xf", [128, NCH, T + 2], F32))
        xb = es.enter_context(nc.sbuf_tensor("xb", [128, NCH, T + 2], BF16))
        cwb = es.enter_context(nc.sbuf_tensor("cwb_s", [128, 88, 4], F32))
        gb = es.enter_context(nc.sbuf_tensor("gb_s", [128, NCH, 2], F32))
        gq = es.enter_context(nc.sbuf_tensor("gq", [128, 12, T], BF16))
        ones = es.enter_context(nc.sbuf_tensor("ones", [128, 128], BF16))
        eps = es.enter_context(nc.sbuf_tensor("eps", [128, 1], F32))
        tmps = {
            "a": [es.enter_context(nc.sbuf_tensor("ta%d" % i, [128, T], F32)) for i in range(2)],
            "v": [es.enter_context(nc.sbuf_tensor("tv%d" % i, [128, T], F32)) for i in range(2)],
            "s": [es.enter_context(nc.sbuf_tensor("tsl%d" % i, [128, T], F32)) for i in range(2)],
            "eps": eps,
        }
        tmps["zb"] = [es.enter_context(nc.sbuf_tensor("zb%d" % i, [128, T], BF16)) for i in range(2)]
        tmps["zs"] = [es.enter_context(nc.sbuf_tensor("zs%d" % i, [128, T], BF16)) for i in range(2)]
        tmps["mean"] = tmps["a"][0]
        tmps["rstd"] = tmps["a"][1]
        ps = es.enter_context(nc.psum_tensor("ps", [128, 4096], F32))
        ws = WeightStream(S, nc, es, 4, 16 * 256)
        plan = plan_ffn_weights(ws, w_up, w_down)

        ch_in = S.new_chan(total=True)
        ch_p = S.new_chan(total=True)
        ch_out = [S.new_chan() for _ in range(4)]
        S.dma("sp", cwb[:, :, :], cwb_d.rearrange("p (c j) -> p c j", j=4), ch_p)
        S.dma("sp", gb[:, :, :], gb_d.rearrange("p (c j) -> p c j", j=2), ch_p)
        S.memset("dve", ones[:, :], 1.0)
        S.memset("dve", eps[:, :], LN_EPS)
        for c in range(NCH):
            S.dma("sp", xf[:, c, :], xT[c * 128:(c + 1) * 128, :], ch_in)
        for c in range(NCH):
            S.act(xb[:, c, :], xf[:, c, :], AF.Identity)
        emit_ffn(S, ws, plan, xf, xb, cwb, gq, tmps, ps)
        emit_ln(S, xf, 2, T, gb, ones, tmps, ps, [lambda c: xf[:, c, 2:T + 2]])
        for c in range(NCH):
            S.dma("sp", yT[c * 128:(c + 1) * 128, :], xf[:, c, 2:T + 2], ch_out[c % 4])
        S.emit(nc, es, final_chans=ch_out)
    return nc, S


def _pm(v, nch):
    return np.ascontiguousarray(np.asarray(v, np.float32).reshape(nch, 128).T)


def prep_ffn_params(l, ffn_conv_w, ffn_conv_b, ln2_g, ln2_b):
    cw = np.asarray(ffn_conv_w[l], np.float32)
    cb = np.asarray(ffn_conv_b[l], np.float32)
    cwb = np.stack([_pm(cw[0], 88), _pm(cw[1], 88), _pm(cw[2], 88), _pm(cb, 88)], axis=-1)
    gb = np.stack([_pm(ln2_g[l], 16), _pm(ln2_b[l], 16)], axis=-1)
    return np.ascontiguousarray(cwb.reshape(128, 88 * 4)), np.ascontiguousarray(gb.reshape(128, 32))


def run_ffn_launch(x1, l, ffn_w_up, ffn_conv_w, ffn_conv_b, ffn_w_down, ln2_g, ln2_b):
    nc, S = build_ffn_launch()
    cwb, gb = prep_ffn_params(l, ffn_conv_w, ffn_conv_b, ln2_g, ln2_b)
    wu = np.ascontiguousarray(ffn_w_up[l], np.float32)
    wd = np.ascontiguousarray(ffn_w_down[l], np.float32)
    x1 = np.asarray(x1, np.float32)
    in_maps = []
    for c in range(NCORES):
        b, s = divmod(c, 4)
        t0 = s * T
        xt = np.zeros((D, T + 2), np.float32)
        xt[:, 2:] = x1[b, t0:t0 + T].T
        if s > 0:
            xt[:, 0:2] = x1[b, t0 - 2:t0].T
        in_maps.append({"xT": xt, "w_up": wu, "w_down": wd, "cwb": cwb, "ln2gb": gb})
    res = run_bass_kernel_spmd(nc, in_maps, core_ids=list(range(NCORES)))
    out = np.zeros((2, 4096, D), np.float32)
    for c in range(NCORES):
        b, s = divmod(c, 4)
        out[b, s * T:(s + 1) * T] = res.results[c]["yT"].T
    return out


TH = T + 128
B_WINDOWS = (2, 4, 8, 16)
GELU_C = 0.044715
GELU_S = 2.0 * 0.7978845608028654


def emit_gelu(S, dst, src_ps, t1, t2):
    S.act(t1, src_ps, AF.Square)
    S.ts("dve", t1, t1, GELU_C, 1.0, ALU.mult, ALU.add)
    S.tt("dve", t1, t1, src_ps, ALU.mult)
    S.act(t2, t1, AF.Sigmoid, scale=GELU_S)
    S.tt("dve", dst, t2, src_ps, ALU.mult)


def build_mix0_launch():
    nc = bass.Bass("TRN2", target_bir_lowering=False)
    x0T = nc.dram_tensor("x0T", [D, TH], F32, kind="ExternalInput").ap()
    w_in = nc.dram_tensor("w_in", [D, 3072], F32, kind="ExternalInput").ap()
    w_out = nc.dram_tensor("w_out", [D, D], F32, kind="ExternalInput").ap()
    wsT_d = nc.dram_tensor("wsT", [128, 8 * 128], F32, kind="ExternalInput").ap()
    mask_d = nc.dram_tensor("maskA", [128, 128], F32, kind="ExternalInput").ap()
    bsT_d = nc.dram_tensor("bsT", [128, 8 * 128], F32, kind="ExternalInput").ap()
    lnv_d = nc.dram_tensor("lnv", [128, 2 * 1024], F32, kind="ExternalInput").ap()
    wp_d = nc.dram_tensor("w_pool", [4 * 256, 256], F32, kind="ExternalInput").ap()
    psc_d = nc.dram_tensor("pscale", [128, 8], F32, kind="ExternalInput").ap()
    gb_d = nc.dram_tensor("ln1gb", [128, 32], F32, kind="ExternalInput").ap()
    rc_d = nc.dram_tensor("rcnt", [128, 64], F32, kind="ExternalInput").ap()
    flag_d = nc.dram_tensor("flag", [128, 1], F32, kind="ExternalInput").ap()
    x1T = nc.dram_tensor("x1T", [D, T + 2], F32, kind="ExternalOutput").ap()
    S = Sched()
    with ExitStack() as es:
        arena = es.enter_context(nc.sbuf_tensor("arena", [128, NCH * (T + 2)], F32))
        zf = arena[:, :].rearrange("p (c t) -> p c t", c=NCH, t=T + 2)
        x0b = arena[:, 0:NCH * TH // 2].bitcast(BF16).rearrange("p (c t) -> p c t", c=NCH, t=TH)
        u = es.enter_context(nc.sbuf_tensor("u", [128, 8, TH], BF16))
        vt = es.enter_context(nc.sbuf_tensor("vt", [128, 9, 1024], BF16))
        pp = es.enter_context(nc.sbuf_tensor("pp", [128, 8, TH], BF16))
        wsT = es.enter_context(nc.sbuf_tensor("wsT_s", [128, 8, 128], BF16))
        mask = es.enter_context(nc.sbuf_tensor("mask_s", [128, 128], F32))
        bsT = es.enter_context(nc.sbuf_tensor("bsT_s", [128, 8, 128], F32))
        lnv = es.enter_context(nc.sbuf_tensor("lnv_s", [128, 2, 1024], F32))
        wp = es.enter_context(nc.sbuf_tensor("wp_s", [128, 8, 256], BF16))
        psc = es.enter_context(nc.sbuf_tensor("psc_s", [128, 8], F32))
        gb = es.enter_context(nc.sbuf_tensor("gb_s", [128, NCH, 2], F32))
        rc = es.enter_context(nc.sbuf_tensor("rc_s", [128, 4, 16], F32))
        flag = es.enter_context(nc.sbuf_tensor("flag_s", [128, 1], F32))
        ones = es.enter_context(nc.sbuf_tensor("ones", [128, 128], BF16))
        eps = es.enter_context(nc.sbuf_tensor("eps", [128, 1], F32))
        xbf = es.enter_context(nc.sbuf_tensor("xbf", [128, 16 + TH], F32))
        g1 = [es.enter_context(nc.sbuf_tensor("g1_%d" % i, [128, TH], F32)) for i in range(2)]
        g2p = [es.enter_context(nc.sbuf_tensor("g2_%d" % i, [128, 16 + TH], F32)) for i in range(2)]
        g2 = [t[:, 16:16 + TH] for t in g2p]
        tA, tB = g2p
        wsTf = g1[1][:, 0:1024].rearrange("p (h t) -> p h t", h=8)
        st = es.enter_context(nc.sbuf_tensor("st", [128, 8], F32))
        small = es.enter_context(nc.sbuf_tensor("small", [128, 32], F32))
        tmps = {"eps": eps,
                "zb": [g2p[i][:, 16:16 + 513].bitcast(BF16) for i in range(2)],
                "zs": [xbf[:, 16:16 + 513].bitcast(BF16), xbf[:, 600:600 + 513].bitcast(BF16)],
                "mean": g1[0], "rstd": g1[1]}
        ps = es.enter_context(nc.psum_tensor("ps", [128, 4096], F32))
        ws = WeightStream(S, nc, es, 4, 16 * 256)
        wi = w_in.rearrange("(kc p) n -> p kc n", p=128)
        wo = w_out.rearrange("(kc p) n -> p kc n", p=128)
        u_xb = [ws.plan((16, 256), wi[:, :, 2048 + i * 256:2048 + (i + 1) * 256]) for i in range(4)]
        u_u = [ws.plan((16, 256), wi[:, :, i * 256:(i + 1) * 256]) for i in range(4)]
        u_v = [ws.plan((16, 256), wi[:, :, 1024 + i * 256:1024 + (i + 1) * 256]) for i in range(4)]
        u_o = [ws.plan((16, 256), wo[:, :, i * 256:(i + 1) * 256]) for i in range(8)]

        chp = S.new_chan(total=True)
        chx = S.new_chan(total=True)
        S.dma("sp", wsTf, wsT_d.rearrange("p (h t) -> p h t", h=8), chp)
        S.dma("sp", mask[:, :], mask_d, chp)
        S.dma("sp", bsT[:, :, :], bsT_d.rearrange("p (h t) -> p h t", h=8), chp)
        S.dma("sp", lnv[:, :, :], lnv_d.rearrange("p (a c) -> p a c", a=2), chp)
        S.dma("sp", psc[:, :], psc_d, chp)
        S.dma("sp", gb[:, :, :], gb_d.rearrange("p (c j) -> p c j", j=2), chp)
        S.dma("sp", rc[:, :, :], rc_d.rearrange("p (g j) -> p g j", g=4), chp)
        S.dma("sp", flag[:, :], flag_d, chp)
        S.dma("pool", wp[:, :, :], wp_d.rearrange("(a p) n -> p a n", p=128), chx)
        for c in range(NCH):
            S.dma("pool", x0b[:, c, :], x0T[c * 128:(c + 1) * 128, :], chx)
        ws.release(-1)
        S.memset("dve", ones[:, :], 1.0)
        S.memset("dve", eps[:, :], LN_EPS)
        S.memset("dve", xbf[:, 0:16], 0.0)
        S.memset("dve", tA[:, 0:16], 0.0)
        S.memset("dve", tB[:, 0:16], 0.0)
        for h in range(8):
            S.tt("dve", wsT[:, h, :], wsTf[:, h, :], mask[:, :], ALU.mult)

        GR = (0, 1536)
        TT3 = ((0, 512), (512, 512), (1024, 128))

        def proj_fm(wt, sub, g0):
            for kc in range(NCH):
                for (t0, w) in TT3:
                    S.mm(ps[:, g0 + t0:g0 + t0 + w], wt[:, kc, sub * 128:(sub + 1) * 128], x0b[:, kc, t0:t0 + w],
                         start=(kc == 0), stop=(kc == NCH - 1))

        gi = 0
        for i in range(4):
            wt = ws.get(u_xb[i])
            for sub in range(2):
                c = i * 2 + sub
                g = c // 2
                g0 = GR[gi % 2]
                gi += 1
                proj_fm(wt, sub, g0)
                S.act(xbf[:, 16:16 + TH], ps[:, g0:g0 + TH], AF.Identity)
                src = xbf
                dsts = [tA, tB]
                for k in range(g + 1):
                    sh = 1 << k
                    dst = dsts[k % 2]
                    S.tt("dve", dst[:, 16:16 + TH], src[:, 16:16 + TH], src[:, 16 - sh:16 + TH - sh], ALU.add)
                    src = dst
                win = B_WINDOWS[g]
                S.stt(pp[:, c, :], src[:, 16:16 + TH], 1.0 / win, xbf[:, 16:16 + TH], ALU.mult, ALU.subtract)
                S.tt("dve", small[:, 0:16], src[:, 16 + 128:16 + 144], rc[:, g, :], ALU.mult)
                S.tt("dve", pp[:, c, 128:144], small[:, 0:16], xbf[:, 16 + 128:16 + 144], ALU.subtract)
            ws.release(u_xb[i])
        for i in range(4):
            wt = ws.get(u_u[i])
            for sub in range(2):
                c = i * 2 + sub
                g0 = GR[gi % 2]
                proj_fm(wt, sub, g0)
                emit_gelu(S, u[:, c, :], ps[:, g0:g0 + TH], g1[gi % 2][:, :], g2[gi % 2])
                gi += 1
            ws.release(u_u[i])
        wv = [ws.get(k) for k in u_v]
        for tk in range(9):
            vr = g1[tk % 2]
            for cg in range(4):
                r0 = 3072 + ((tk * 4 + cg) % 2) * 512
                for kc in range(NCH):
                    S.mm(ps[:, r0:r0 + 256], x0b[:, kc, tk * 128:(tk + 1) * 128], wv[cg][:, kc, :],
                         start=(kc == 0), stop=(kc == NCH - 1))
                emit_gelu(S, vr[:, cg * 256:(cg + 1) * 256], ps[:, r0:r0 + 256],
                          g2[0][:, cg * 256:(cg + 1) * 256], g2[1][:, cg * 256:(cg + 1) * 256])
            sq = g2[0]
            S.add("dve", lambda e, vr=vr: e.reduce_sum(out=st[:, 0:1], in_=vr[:, 0:1024], axis=mybir.AxisListType.X),
                  reads=[vr[:, 0:1024]], writes=[st[:, 0:1]])
            S.act(sq[:, 0:1024], vr[:, 0:1024], AF.Square)
            S.add("dve", lambda e, sq=sq: e.reduce_sum(out=st[:, 1:2], in_=sq[:, 0:1024], axis=mybir.AxisListType.X),
                  reads=[sq[:, 0:1024]], writes=[st[:, 1:2]])
            S.ts("dve", st[:, 2:3], st[:, 0:1], 1.0 / 1024, None, ALU.mult)
            S.tt("dve", st[:, 3:4], st[:, 2:3], st[:, 2:3], ALU.mult)
            S.stt(st[:, 4:5], st[:, 1:2], 1.0 / 1024, st[:, 3:4], ALU.mult, ALU.subtract)
            S.act(st[:, 5:6], st[:, 4:5], AF.Sqrt, bias=eps[:, 0:1], scale=1.0)
            S.add("dve", lambda e: e.reciprocal(out=st[:, 6:7], in_=st[:, 5:6]), reads=[st[:, 5:6]],
                  writes=[st[:, 6:7]])
            S.ts("dve", vr[:, 0:1024], vr[:, 0:1024], st[:, 2:3], st[:, 6:7], ALU.subtract, ALU.mult)
            S.tt("dve", vr[:, 0:1024], vr[:, 0:1024], lnv[:, 0, :], ALU.mult)
            S.tt("dve", vt[:, tk, :], vr[:, 0:1024], lnv[:, 1, :], ALU.add)
        ws.release(u_v[3])
        for tk in range(9):
            for half in range(2):
                r0 = 2048 + half * 512
                for hh in range(4):
                    h = half * 4 + hh
                    S.mm(ps[:, r0 + hh * 128:r0 + (hh + 1) * 128], vt[:, tk, h * 128:(h + 1) * 128], wsT[:, h, :],
                         start=True, stop=True)
                tmp = g2[half][:, 0:512].rearrange("p (h t) -> p h t", h=4)
                S.tt("dve", tmp, ps[:, r0:r0 + 512].rearrange("p (h t) -> p h t", h=4),
                     bsT[:, half * 4:half * 4 + 4, :], ALU.add)
                uu = u[:, half * 4:half * 4 + 4, tk * 128:(tk + 1) * 128]
                S.tt("dve", uu, tmp, uu, ALU.mult)
        for g in range(4):
            for oc in range(2):
                g0 = GR[oc]
                for kc in range(2):
                    for (t0, w) in TT3:
                        S.mm(ps[:, g0 + t0:g0 + t0 + w], wp[:, g * 2 + kc, oc * 128:(oc + 1) * 128],
                             pp[:, g * 2 + kc, t0:t0 + w], start=(kc == 0), stop=(kc == 1))
            for oc in range(2):
                g0 = GR[oc]
                c = g * 2 + oc
                S.act(pp[:, c, :], ps[:, g0:g0 + TH], AF.Identity, scale=psc[:, c:c + 1])
        xs = [g1[0], g1[1]]
        chs = [S.new_chan(), S.new_chan()]
        for i in range(8):
            wt = ws.get(u_o[i])
            for sub in range(2):
                oc = i * 2 + sub
                g0 = GR[oc % 2]
                xst = xs[oc % 2]
                S.dma("sp", xst[:, 0:T + 2], x0T[oc * 128:(oc + 1) * 128, 126:TH], chs[oc % 2])
                for kc in range(NCH):
                    src = u[:, kc, :] if kc < 8 else pp[:, kc - 8, :]
                    lw = wt[:, kc, sub * 128:(sub + 1) * 128]
                    S.mm(ps[:, g0 + 510:g0 + 512], lw, src[:, 126:128], start=(kc == 0), stop=(kc == NCH - 1))
                    S.mm(ps[:, g0 + 512:g0 + 1024], lw, src[:, 128:640], start=(kc == 0), stop=(kc == NCH - 1))
                    S.mm(ps[:, g0 + 1024:g0 + 1536], lw, src[:, 640:1152], start=(kc == 0), stop=(kc == NCH - 1))
                S.stt(zf[:, oc, :], xst[:, 0:T + 2], ALPHA, ps[:, g0 + 510:g0 + 1536], ALU.mult, ALU.add)
            ws.release(u_o[i])
        emit_ln(S, zf, 0, T + 2, gb, ones, tmps, ps, [lambda c: zf[:, c, :]])
        S.ts("dve", zf[:, :, 0:2], zf[:, :, 0:2], flag[:, 0:1], None, ALU.mult)
        cho = [S.new_chan() for _ in range(4)]
        for c in range(NCH):
            S.dma("sp", x1T[c * 128:(c + 1) * 128, :], zf[:, c, :], cho[c % 4])
        S.emit(nc, es, final_chans=cho)
    return nc, S


def prep_mix0_inputs(x, ev_w_in, ev_ln_v_g, ev_ln_v_b, ev_w_s, ev_b_s, ev_w_pool, ev_pool_scale, ev_w_out,
                     ln1_g, ln1_b):
    x = np.asarray(x, np.float32)
    ws = np.asarray(ev_w_s[0], np.float32)
    wsT = np.ascontiguousarray(ws.transpose(2, 0, 1)).reshape(128, 8 * 128)
    tt_ = np.arange(128)
    maskA = (tt_[None, :] >= tt_[:, None]).astype(np.float32)
    bsT = np.ascontiguousarray(np.broadcast_to(np.asarray(ev_b_s[0], np.float32).reshape(1, 8 * 128), (128, 8 * 128)))
    lnv = np.ascontiguousarray(np.broadcast_to(
        np.concatenate([np.asarray(ev_ln_v_g[0], np.float32), np.asarray(ev_ln_v_b[0], np.float32)])[None, :],
        (128, 2048)))
    wp = np.ascontiguousarray(np.asarray(ev_w_pool[0], np.float32).reshape(4 * 256, 256))
    psc = _pm(ev_pool_scale[0], 8)
    gb = np.ascontiguousarray(np.stack([_pm(ln1_g[0], 16), _pm(ln1_b[0], 16)], axis=-1).reshape(128, 32))
    common = {"w_in": np.ascontiguousarray(ev_w_in[0], np.float32),
              "w_out": np.ascontiguousarray(ev_w_out[0], np.float32),
              "wsT": wsT, "maskA": maskA, "bsT": bsT, "lnv": lnv, "w_pool": wp, "pscale": psc, "ln1gb": gb}
    in_maps = []
    for c in range(NCORES):
        b, s = divmod(c, 4)
        t0 = s * T
        xt = np.zeros((D, TH), np.float32)
        xt[:, 128:] = x[b, t0:t0 + T].T
        if s > 0:
            xt[:, 0:128] = x[b, t0 - 128:t0].T
        rc = np.zeros((4, 16), np.float32)
        for g, win in enumerate(B_WINDOWS):
            pos = np.arange(t0 + 1, t0 + 17, dtype=np.float32)
            rc[g] = 1.0 / np.minimum(pos, float(win))
        rcb = np.ascontiguousarray(np.broadcast_to(rc.reshape(1, 64), (128, 64)))
        m = dict(common)
        m.update({"x0T": xt, "rcnt": rcb, "flag": np.full((128, 1), 1.0 if s > 0 else 0.0, np.float32)})
        in_maps.append(m)
    return in_maps


def run_mix0_launch(inputs):
    nc, S = build_mix0_launch()
    in_maps = prep_mix0_inputs(inputs["x"], inputs["ev_w_in"], inputs["ev_ln_v_g"], inputs["ev_ln_v_b"],
                               inputs["ev_w_s"], inputs["ev_b_s"], inputs["ev_w_pool"], inputs["ev_pool_scale"],
                               inputs["ev_w_out"], inputs["ln1_g"], inputs["ln1_b"])
    res = run_bass_kernel_spmd(nc, in_maps, core_ids=list(range(NCORES)))
    return [res.results[c]["x1T"] for c in range(NCORES)]


NH = 16
CH = 64
NCK = T // CH


def hgrn_consts(S, nc, es, lbp_d, mask2_d, ident_d, chp):
    c = {}
    lbp = es.enter_context(nc.sbuf_tensor("lbp_s", [128, 2, NH], F32))
    c["lb"] = es.enter_context(nc.sbuf_tensor("lb", [128, NH], F32))
    c["oml"] = es.enter_context(nc.sbuf_tensor("oml", [128, NH], F32))
    c["rm"] = es.enter_context(nc.sbuf_tensor("rm", [128, T], F32))
    c["mask2"] = es.enter_context(nc.sbuf_tensor("mask2_s", [128, 128], F32))
    identf = es.enter_context(nc.sbuf_tensor("identf", [128, 128], F32))
    c["ident"] = es.enter_context(nc.sbuf_tensor("ident_s", [128, 128], BF16))
    S.dma("sp", lbp[:, :, :], lbp_d.rearrange("p (l h) -> p l h", l=2), chp)
    S.dma("sp", c["mask2"][:, :], mask2_d, chp)
    S.dma("sp", identf[:, :], ident_d, chp)
    S.copy("dve", c["ident"][:, :], identf[:, :])
    S.tt("dve", c["lb"][:, :], lbp[:, 1, :], lbp[:, 0, :], ALU.subtract)
    S.act(c["lb"][:, :], c["lb"][:, :], AF.Sigmoid)
    S.ts("dve", c["oml"][:, :], c["lb"][:, :], -1.0, 1.0, ALU.mult, ALU.add)
    S.memset("dve", c["rm"][:, :], 1.0)
    S.memset("dve", c["rm"][:, :].rearrange("p (c t) -> p c t", t=CH)[:, :, 0:1], 0.0)
    c["pm"] = es.enter_context(nc.sbuf_tensor("pm", [128, 2], F32))
    S.memset("dve", c["pm"][:, :], 0.0)
    S.memset("dve", c["pm"][0:64, 0:1], 1.0)
    S.memset("dve", c["pm"][64:128, 1:2], 1.0)
    return c


def hgrn_gates(S, h, cst, f_ps, A, B, C, kend_bf, dec, kdec_bf=None, eC=None):
    S.act(A, f_ps, AF.Sigmoid)
    S.ts("dve", A, A, cst["oml"][:, h:h + 1], cst["lb"][:, h:h + 1], ALU.mult, ALU.add)
    S.act(B, A, AF.Ln)
    S.add("dve", lambda e: e.tensor_tensor_scan(out=C, data0=cst["rm"][:, :], data1=B, initial=0.0,
                                                op0=ALU.mult, op1=ALU.add),
          reads=[cst["rm"][:, :], B], writes=[C])
    S.ts("dve", A, A, -1.0, 1.0, ALU.mult, ALU.add)
    S.act(B, C, AF.Exp, scale=-1.0)
    S.tt("dve", B, A, B, ALU.mult)
    if kdec_bf is not None:
        S.act(kdec_bf, B, AF.Identity)
    C3 = C.rearrange("p (c t) -> p c t", t=CH)
    S.act(dec.rearrange("p (c o) -> p c o", o=1), C3[:, :, CH - 1:CH], AF.Exp)
    S.tt("dve", kend_bf.rearrange("p (c t) -> p c t", t=CH), B.rearrange("p (c t) -> p c t", t=CH),
         dec.rearrange("p (c o) -> p c o", o=1).to_broadcast([128, NCK, CH]), ALU.mult)
    if eC is not None:
        S.act(eC, C, AF.Exp)


def hgrn_transposes(S, cst, psb, src_bf, dst_tok, dst_tok1=None):
    for j in range(8):
        S.transpose(psb[:, 4096 + j * 128:4096 + (j + 1) * 128], src_bf[:, j * 128:(j + 1) * 128], cst["ident"][:, :])
    if dst_tok1 is None:
        S.act(dst_tok.rearrange("p j d -> p (j d)"), psb[:, 4096:5120], AF.Identity)
    else:
        S.act(dst_tok.rearrange("p j d -> p (j d)"), psb[:, 4096:5120], AF.Identity, scale=cst["pm"][:, 0:1])
        S.act(dst_tok1.rearrange("p j d -> p (j d)"), psb[:, 4096:5120], AF.Identity, scale=cst["pm"][:, 1:2])


def hgrn_state_scan(S, ps, kt, vtk, dec, Sf, Sb=None):
    cur = 0
    if Sb is not None:
        S.act(Sb[:, 0, :], Sf[0][:, :], AF.Identity)
    for g4 in range(4):
        for cc in range(4):
            c = g4 * 4 + cc
            j, par = divmod(c, 2)
            S.mm(ps[:, 2560 + cc * 128:2560 + (cc + 1) * 128], kt[par][:, j, :], vtk[:, j, :], start=True, stop=True)
        for cc in range(4):
            c = g4 * 4 + cc
            nxt = 1 - cur
            S.stt(Sf[nxt][:, :], Sf[cur][:, :], dec[:, c:c + 1], ps[:, 2560 + cc * 128:2560 + (cc + 1) * 128],
                  ALU.mult, ALU.add)
            cur = nxt
            if Sb is not None and c + 1 < NCK:
                S.act(Sb[:, c + 1, :], Sf[cur][:, :], AF.Identity)
    return cur


def build_hgrn_pre_launch(stage=9, nheads=NH):
    nc = bass.Bass("TRN2", target_bir_lowering=False)
    xT = nc.dram_tensor("xT", [D, T], F32, kind="ExternalInput").ap()
    w_in = nc.dram_tensor("w_in", [D, 4 * D], F32, kind="ExternalInput").ap()
    lbp_d = nc.dram_tensor("lbp", [128, 2 * NH], F32, kind="ExternalInput").ap()
    mask2_d = nc.dram_tensor("mask2", [128, 128], F32, kind="ExternalInput").ap()
    ident_d = nc.dram_tensor("ident", [128, 128], F32, kind="ExternalInput").ap()
    U_d = nc.dram_tensor("U", [NH, 128, 128], F32, kind="ExternalOutput").ap()
    D_d = nc.dram_tensor("Dd", [128, NH], F32, kind="ExternalOutput").ap()
    S = Sched()
    with ExitStack() as es:
        xb = es.enter_context(nc.sbuf_tensor("xb", [128, NCH, T], BF16))
        A = [es.enter_context(nc.sbuf_tensor("A%d" % i, [128, T], F32)) for i in range(2)]
        B = [es.enter_context(nc.sbuf_tensor("B%d" % i, [128, T], F32)) for i in range(2)]
        C = [es.enter_context(nc.sbuf_tensor("C%d" % i, [128, T], F32)) for i in range(2)]
        kend = [es.enter_context(nc.sbuf_tensor("kend%d" % i, [128, T], BF16)) for i in range(2)]
        ibf = [es.enter_context(nc.sbuf_tensor("ibf%d" % i, [128, T], BF16)) for i in range(2)]
        kt = [[es.enter_context(nc.sbuf_tensor("kt%d_%d" % (i, k), [128, 8, 128], BF16)) for k in range(2)]
              for i in range(2)]
        vtk = [es.enter_context(nc.sbuf_tensor("vtk%d" % i, [128, 8, 128], BF16)) for i in range(2)]
        dec = [es.enter_context(nc.sbuf_tensor("dec%d" % i, [128, NCK], F32)) for i in range(2)]
        Sf = [[es.enter_context(nc.sbuf_tensor("Sf%d_%d" % (i, k), [128, 128], F32)) for k in range(2)]
              for i in range(2)]
        Dd = es.enter_context(nc.sbuf_tensor("Dd_s", [128, NH], F32))
        ps = es.enter_context(nc.psum_tensor("ps", [128, 4096], F32))
        psb = ps[:, :].bitcast(BF16)
        chp = S.new_chan(total=True)
        chx = S.new_chan(total=True)
        cst = hgrn_consts(S, nc, es, lbp_d, mask2_d, ident_d, chp)
        ws = WeightStream(S, nc, es, 4, 16 * 256)
        wi = w_in.rearrange("(kc p) n -> p kc n", p=128)
        uses = [ws.plan((16, 256), wi[:, :, h * 512 + 256:h * 512 + 512]) for h in range(NH)]
        for c in range(NCH):
            S.dma("pool", xb[:, c, :], xT[c * 128:(c + 1) * 128, :], chx)
        ws.release(-1)
        cho = [S.new_chan() for _ in range(2)]
        for h in range(nheads):
            b = h % 2
            wt = ws.get(uses[h])
            for which in range(2):
                g0 = which * 1024
                for kc in range(NCH):
                    for t_ in range(2):
                        S.mm(ps[:, g0 + t_ * 512:g0 + (t_ + 1) * 512], wt[:, kc, which * 128:(which + 1) * 128],
                             xb[:, kc, t_ * 512:(t_ + 1) * 512], start=(kc == 0), stop=(kc == NCH - 1))
            ws.release(uses[h])
            hgrn_gates(S, h, cst, ps[:, 0:1024], A[b][:, :], B[b][:, :], C[b][:, :], kend[b][:, :], dec[b][:, :])
            S.act(ibf[b][:, :], ps[:, 1024:2048], AF.Identity)
            C3 = C[b][:, :].rearrange("p (c t) -> p c t", t=CH)
            S.add("dve", lambda e, C3=C3, h=h: e.reduce_sum(out=Dd[:, h:h + 1], in_=C3[:, :, CH - 1:CH],
                                                           axis=mybir.AxisListType.XY),
                  reads=[C[b][:, :]], writes=[Dd[:, h:h + 1]])
            S.act(Dd[:, h:h + 1], Dd[:, h:h + 1], AF.Exp)
            if stage >= 2:
                hgrn_transposes(S, cst, psb, kend[b][:, :], kt[b][0][:, :, :], kt[b][1][:, :, :])
                hgrn_transposes(S, cst, psb, ibf[b][:, :], vtk[b][:, :, :])
            S.memset("dve", Sf[b][0][:, :], 0.0)
            fin = 0
            if stage >= 3:
                fin = hgrn_state_scan(S, ps, kt[b], vtk[b], dec[b], Sf[b])
            S.dma("sp", U_d[h, :, :], Sf[b][fin][:, :], cho[b])
        chd = S.new_chan()
        S.dma("sp", D_d, Dd[:, :], chd)
        S.emit(nc, es, final_chans=cho + [chd])
    return nc, S


def regroup_w_in(od_w_in):
    w = np.asarray(od_w_in, np.float32).reshape(D, 4, NH, 128)
    w = w[:, [0, 3, 1, 2]]
    return np.ascontiguousarray(w.transpose(0, 2, 1, 3).reshape(D, 4 * D))


def hgrn_const_inputs(lb_param):
    lbp = np.ascontiguousarray(np.stack([_pm(lb_param[0], NH), _pm(lb_param[1], NH)], axis=1).reshape(128, 2 * NH))
    i = np.arange(128)
    mask2 = ((i[None, :] >= i[:, None]) & ((i[None, :] // CH) == (i[:, None] // CH))).astype(np.float32)
    return {"lbp": lbp, "mask2": mask2, "ident": np.eye(128, dtype=np.float32)}


def _carve(arena, off_bytes, n_elems, dt):
    assert off_bytes % 4 == 0
    nb = n_elems * _dsize(dt)
    assert nb % 4 == 0
    v = arena[:, off_bytes // 4:(off_bytes + nb) // 4]
    return v if dt == F32 else v.bitcast(dt)


def build_hgrn_main_launch():
    nc = bass.Bass("TRN2", target_bir_lowering=False)
    xT = nc.dram_tensor("xT", [D, T], F32, kind="ExternalInput").ap()
    w_in = nc.dram_tensor("w_in", [D, 4 * D], F32, kind="ExternalInput").ap()
    w_out = nc.dram_tensor("w_out", [D, D], F32, kind="ExternalInput").ap()
    lbp_d = nc.dram_tensor("lbp", [128, 2 * NH], F32, kind="ExternalInput").ap()
    mask2_d = nc.dram_tensor("mask2", [128, 128], F32, kind="ExternalInput").ap()
    ident_d = nc.dram_tensor("ident", [128, 128], F32, kind="ExternalInput").ap()
    up_d = nc.dram_tensor("Uprev", [3, NH, 128, 128], F32, kind="ExternalInput").ap()
    dp_d = nc.dram_tensor("Dprev", [128, 3 * NH], F32, kind="ExternalInput").ap()
    gn_d = nc.dram_tensor("gn", [128, NH], F32, kind="ExternalInput").ap()
    gb_d = nc.dram_tensor("ln1gb", [128, 32], F32, kind="ExternalInput").ap()
    x1T = nc.dram_tensor("x1T", [D, T], F32, kind="ExternalOutput").ap()
    S = Sched()
    with ExitStack() as es:
        arena = es.enter_context(nc.sbuf_tensor("arena", [128, NCH * T], F32))
        zf = arena[:, :].rearrange("p (c t) -> p c t", c=NCH, t=T)
        xb = _carve(arena, 0, NCH * T, BF16).rearrange("p (c t) -> p c t", c=NCH, t=T)
        off = NCH * T * 2
        A = _carve(arena, off, T, F32); off += 4 * T
        B = _carve(arena, off, T, F32); off += 4 * T
        C = _carve(arena, off, T, F32); off += 4 * T
        kend = _carve(arena, off, T, BF16); off += 2 * T
        kdec = _carve(arena, off, T, BF16); off += 2 * T
        qdec = _carve(arena, off, T, BF16); off += 2 * T
        ibf = _carve(arena, off, T, BF16); off += 2 * T
        sg = _carve(arena, off, T, BF16); off += 2 * T
        osq = _carve(arena, off, T, BF16); off += 2 * T
        attm = _carve(arena, off, T, BF16).rearrange("p (j t) -> p j t", j=8); off += 2 * T
        kt0 = _carve(arena, off, T, BF16).rearrange("p (j t) -> p j t", j=8); off += 2 * T
        kt1 = _carve(arena, off, T, BF16).rearrange("p (j t) -> p j t", j=8); off += 2 * T
        kt = (kt0, kt1)
        vtk = _carve(arena, off, T, BF16).rearrange("p (j t) -> p j t", j=8); off += 2 * T
        assert off <= NCH * T * 4
        y = es.enter_context(nc.sbuf_tensor("y", [128, NH, T], BF16))
        Sb = es.enter_context(nc.sbuf_tensor("Sb", [128, NCK, 128], BF16))
        Sf = [es.enter_context(nc.sbuf_tensor("Sf%d" % k, [128, 128], F32)) for k in range(2)]
        upst = es.enter_context(nc.sbuf_tensor("upst", [128, 3, 128], F32))
        dec = es.enter_context(nc.sbuf_tensor("dec", [128, NCK], F32))
        dp = es.enter_context(nc.sbuf_tensor("dp", [128, 3, NH], F32))
        gn = es.enter_context(nc.sbuf_tensor("gn_s", [128, NH], F32))
        gb = es.enter_context(nc.sbuf_tensor("gb_s", [128, NCH, 2], F32))
        ones = es.enter_context(nc.sbuf_tensor("ones", [128, 128], BF16))
        eps = es.enter_context(nc.sbuf_tensor("eps", [128, 1], F32))
        xs = [es.enter_context(nc.sbuf_tensor("xs%d" % i, [128, T], F32)) for i in range(2)]
        tmps = {"eps": eps,
                "zb": [es.enter_context(nc.sbuf_tensor("zb%d" % i, [128, T], BF16)) for i in range(2)],
                "zs": [es.enter_context(nc.sbuf_tensor("zs%d" % i, [128, T], BF16)) for i in range(2)],
                "mean": xs[0], "rstd": xs[1]}
        ps = es.enter_context(nc.psum_tensor("ps", [128, 4096], F32))
        psb = ps[:, :].bitcast(BF16)
        chp = S.new_chan(total=True)
        chx = S.new_chan(total=True)
        cst = hgrn_consts(S, nc, es, lbp_d, mask2_d, ident_d, chp)
        S.dma("sp", dp[:, :, :], dp_d.rearrange("p (j h) -> p j h", j=3), chp)
        S.dma("sp", gn[:, :], gn_d, chp)
        S.dma("sp", gb[:, :, :], gb_d.rearrange("p (c j) -> p c j", j=2), chp)
        S.memset("dve", ones[:, :], 1.0)
        S.memset("dve", eps[:, :], LN_EPS)
        ws = WeightStream(S, nc, es, 3, 16 * 512)
        wi = w_in.rearrange("(kc p) n -> p kc n", p=128)
        wo = w_out.rearrange("(kc p) n -> p kc n", p=128)
        uses = [ws.plan((16, 512), wi[:, :, h * 512:(h + 1) * 512]) for h in range(NH)]
        u_o = [ws.plan((16, 512), wo[:, :, i * 512:(i + 1) * 512]) for i in range(4)]
        for c in range(NCH):
            S.dma("pool", xb[:, c, :], xT[c * 128:(c + 1) * 128, :], chx)
        ws.release(-1)
        chu = S.new_chan()
        G0, G1 = 0, 1024
        for h in range(NH):
            wt = ws.get(uses[h])

            def proj(blk, g0):
                for kc in range(NCH):
                    for t_ in range(2):
                        S.mm(ps[:, g0 + t_ * 512:g0 + (t_ + 1) * 512], wt[:, kc, blk * 128:(blk + 1) * 128],
                             xb[:, kc, t_ * 512:(t_ + 1) * 512], start=(kc == 0), stop=(kc == NCH - 1))
            S.dma("sp", upst[:, :, :], up_d[:, h, :, :].rearrange("j d e -> d j e"), chu)
            S.memset("dve", Sf[0][:, :], 0.0)
            cur = 0
            for j in range(3):
                S.stt(Sf[1 - cur][:, :], Sf[cur][:, :], dp[:, j, h:h + 1], upst[:, j, :], ALU.mult, ALU.add)
                cur = 1 - cur
            Sfl = [Sf[cur], Sf[1 - cur]]
            proj(2, G0)
            proj(3, G1)
            hgrn_gates(S, h, cst, ps[:, G0:G0 + T], A, B, C, kend, dec[:, :], kdec_bf=kdec, eC=A)
            S.act(ibf, ps[:, G1:G1 + T], AF.Identity)
            proj(0, G0)
            proj(1, G1)
            ws.release(uses[h])
            S.act(B, ps[:, G0:G0 + T], AF.Silu)
            S.tt("dve", qdec, B, A, ALU.mult)
            S.act(sg, ps[:, G1:G1 + T], AF.Sigmoid)
            hgrn_transposes(S, cst, psb, kend, kt0, kt1)
            hgrn_transposes(S, cst, psb, ibf, vtk)
            hgrn_state_scan(S, ps, kt, vtk, dec, Sfl, Sb=Sb)
            for half in range(2):
                for jj in range(4):
                    j = half * 4 + jj
                    S.mm(ps[:, 2560 + jj * 128:2560 + (jj + 1) * 128], kdec[:, j * 128:(j + 1) * 128],
                         qdec[:, j * 128:(j + 1) * 128], start=True, stop=True)
                for jj in range(4):
                    j = half * 4 + jj
                    S.tt("dve", attm[:, j, :], ps[:, 2560 + jj * 128:2560 + (jj + 1) * 128], cst["mask2"][:, :],
                         ALU.mult)
            O0 = 3072
            for j in range(8):
                S.mm(ps[:, O0 + j * 128:O0 + (j + 1) * 128], vtk[:, j, :], attm[:, j, :], start=True, stop=False)
                S.mm(ps[:, O0 + j * 128:O0 + j * 128 + 64], Sb[:, 2 * j, :], qdec[:, j * 128:j * 128 + 64],
                     start=False, stop=False)
                S.mm(ps[:, O0 + j * 128 + 64:O0 + (j + 1) * 128], Sb[:, 2 * j + 1, :],
                     qdec[:, j * 128 + 64:(j + 1) * 128], start=False, stop=True)
            S.act(osq, ps[:, O0:O0 + T], AF.Square)
            for t_ in range(2):
                S.mm(ps[:, G0 + t_ * 512:G0 + (t_ + 1) * 512], ones[:, :], osq[:, t_ * 512:(t_ + 1) * 512],
                     start=True, stop=True)
            S.act(A, ps[:, G0:G0 + T], AF.Ln, bias=eps[:, 0:1], scale=1.0 / 128)
            S.act(A, A, AF.Exp, scale=-0.5)
            S.stt(C, ps[:, O0:O0 + T], gn[:, h:h + 1], A, ALU.mult, ALU.mult)
            S.tt("dve", y[:, h, :], C, sg, ALU.mult)
        chs = [S.new_chan(), S.new_chan()]
        for i in range(4):
            wt = ws.get(u_o[i])
            for sub in range(4):
                oc = i * 4 + sub
                g0 = (oc % 2) * 1024
                xst = xs[oc % 2]
                S.dma("sp", xst[:, :], xT[oc * 128:(oc + 1) * 128, :], chs[oc % 2])
                for kc in range(NCH):
                    for t_ in range(2):
                        S.mm(ps[:, g0 + t_ * 512:g0 + (t_ + 1) * 512], wt[:, kc, sub * 128:(sub + 1) * 128],
                             y[:, kc, t_ * 512:(t_ + 1) * 512], start=(kc == 0), stop=(kc == NCH - 1))
                S.stt(zf[:, oc, :], xst[:, :], ALPHA, ps[:, g0:g0 + T], ALU.mult, ALU.add)
            ws.release(u_o[i])
        emit_ln(S, zf, 0, T, gb, ones, tmps, ps, [lambda c: zf[:, c, :]])
        cho = [S.new_chan() for _ in range(4)]
        for c in range(NCH):
            S.dma("sp", x1T[c * 128:(c + 1) * 128, :], zf[:, c, :], cho[c % 4])
        S.emit(nc, es, final_chans=cho)
    return nc, S


def _run(nc, in_maps):
    return run_bass_kernel_spmd(nc, in_maps, core_ids=list(range(NCORES))).results


def kernel_unfused(x, ev_w_in, ev_ln_v_g, ev_ln_v_b, ev_w_s, ev_b_s, ev_w_pool, ev_pool_scale,
           ev_w_out, od_w_in, od_norm_g, od_w_out, lb_param, ffn_w_up, ffn_conv_w,
           ffn_conv_b, ffn_w_down, ln1_g, ln1_b, ln2_g, ln2_b):
    f32 = np.float32
    nc0, _ = build_mix0_launch()
    maps0 = prep_mix0_inputs(x, ev_w_in, ev_ln_v_g, ev_ln_v_b, ev_w_s, ev_b_s, ev_w_pool, ev_pool_scale,
                             ev_w_out, ln1_g, ln1_b)
    r0 = _run(nc0, maps0)
    x1T = [r0[c]["x1T"] for c in range(NCORES)]
    ncf, _ = build_ffn_launch()

    def ffn(l, xTs):
        cwb, gb = prep_ffn_params(l, ffn_conv_w, ffn_conv_b, ln2_g, ln2_b)
        wu = np.ascontiguousarray(ffn_w_up[l], f32)
        wd = np.ascontiguousarray(ffn_w_down[l], f32)
        maps = [{"xT": np.ascontiguousarray(xTs[c], f32), "w_up": wu, "w_down": wd, "cwb": cwb, "ln2gb": gb}
                for c in range(NCORES)]
        r = _run(ncf, maps)
        return [r[c]["yT"] for c in range(NCORES)]

    x2T = ffn(0, x1T)
    ncp, _ = build_hgrn_pre_launch()
    w_in_r = regroup_w_in(od_w_in[0])
    hc = hgrn_const_inputs(lb_param)
    mapsp = []
    for c in range(NCORES):
        m = {"xT": np.ascontiguousarray(x2T[c], f32), "w_in": w_in_r}
        m.update(hc)
        mapsp.append(m)
    rp = _run(ncp, mapsp)
    ncm, _ = build_hgrn_main_launch()
    gn = _pm(od_norm_g[0], NH)
    gb1 = np.ascontiguousarray(np.stack([_pm(ln1_g[1], 16), _pm(ln1_b[1], 16)], axis=-1).reshape(128, 32))
    w_out1 = np.ascontiguousarray(od_w_out[0], f32)
    mapsm = []
    for c in range(NCORES):
        b, s = divmod(c, 4)
        up = np.zeros((3, NH, 128, 128), f32)
        dp = np.zeros((128, 3, NH), f32)
        for j in range(s):
            pos = 3 - s + j
            up[pos] = rp[b * 4 + j]["U"]
            dp[:, pos, :] = rp[b * 4 + j]["Dd"]
        m = {"xT": np.ascontiguousarray(x2T[c], f32), "w_in": w_in_r, "w_out": w_out1, "Uprev": up,
             "Dprev": np.ascontiguousarray(dp.reshape(128, 3 * NH)), "gn": gn, "ln1gb": gb1}
        m.update(hc)
        mapsm.append(m)
    rm = _run(ncm, mapsm)
    x1bT = []
    for c in range(NCORES):
        b, s = divmod(c, 4)
        xt = np.zeros((D, T + 2), f32)
        xt[:, 2:] = rm[c]["x1T"]
        if s > 0:
            xt[:, 0:2] = rm[c - 1]["x1T"][:, T - 2:T]
        x1bT.append(xt)
    outT = ffn(1, x1bT)
    out = np.zeros((2, 4 * T, D), f32)
    for c in range(NCORES):
        b, s = divmod(c, 4)
        out[b, s * T:(s + 1) * T] = outT[c].T
    return out


R0 = 0
R0_SZ = NCH * (T + 2) * 4
R1 = R0 + R0_SZ
R1_SZ = NCH * (T + 2) * 2
R2 = R1 + R1_SZ
R2_SZ = 66560
AR_BYTES = R2 + R2_SZ
SEQ_GROUPS = [[0, 1, 2, 3], [4, 5, 6, 7]]


def build_fused(use_cc=True):
    nc = bass.Bass("TRN2", target_bir_lowering=False)

    def din(name, shape):
        return nc.dram_tensor(name, shape, F32, kind="ExternalInput").ap()
    x0T = din("x0T", [D, TH])
    ev_w_in = din("ev_w_in", [D, 3072])
    ev_w_out = din("ev_w_out", [D, D])
    wsT_d = din("wsT", [128, 8 * 128])
    mask_d = din("maskA", [128, 128])
    bsT_d = din("bsT", [128, 8 * 128])
    lnv_d = din("lnv", [128, 2 * 1024])
    wp_d = din("w_pool", [4 * 256, 256])
    psc_d = din("pscale", [128, 8])
    rc_d = din("rcnt", [128, 64])
    flag_d = din("flag", [128, 1])
    oh_d = din("oh", [128, 8])
    ln1gb_d = din("ln1gb", [128, 64])
    ln2gb_d = din("ln2gb", [128, 64])
    cwb_d = din("cwb", [128, 2 * 88 * 4])
    w_up = [din("w_up%d" % l, [D, 2 * DFF]) for l in range(2)]
    w_down = [din("w_down%d" % l, [DFF, D]) for l in range(2)]
    od_w_in = din("od_w_in", [D, 4 * D])
    od_w_out = din("od_w_out", [D, D])
    lbp_d = din("lbp", [128, 2 * NH])
    mask2_d = din("mask2", [128, 128])
    ident_d = din("ident", [128, 128])
    gn_d = din("gn", [128, NH])
    outT = nc.dram_tensor("outT", [D, T], F32, kind="ExternalOutput").ap()
    xsp = nc.dram_tensor("xsp", [D, T], F32).ap()
    ccin = [nc.dram_tensor("ccin%d" % g, [4 * 4 * 128, 129], F32) for g in range(4)]
    ccout = [nc.dram_tensor("ccout%d" % g, [4 * 4 * 128, 129], F32) for g in range(4)]
    cch_in = nc.dram_tensor("cch_in", [4 * 128, 32], F32)
    cch_out = nc.dram_tensor("cch_out", [4 * 128, 32], F32)

    S = Sched()
    with ExitStack() as es:
        AR = es.enter_context(nc.sbuf_tensor("AR", [128, AR_BYTES // 4], F32))

        def cv(off, shape, dt):
            n = 1
            for k in shape:
                n *= k
            v = _carve(AR, off, n, dt)
            if len(shape) == 2:
                v = v.rearrange("p (a b) -> p a b", a=shape[0], b=shape[1])
            return v

        def sb(name, shape, dt=F32):
            return es.enter_context(nc.sbuf_tensor(name, shape, dt))
        psc = sb("psc_s", [128, 8]); rc = sb("rc_s", [128, 4, 16]); flag = sb("flag_s", [128, 1])
        oh = sb("oh_s", [128, 8]); ln1gb = sb("ln1gb_s", [128, 2, NCH, 2]); ln2gb = sb("ln2gb_s", [128, 2, NCH, 2])
        cwb = sb("cwb_s", [128, 2, 88, 4]); ones = sb("ones", [128, 128], BF16); eps = sb("eps", [128, 1])
        st = sb("st", [128, 8]); small = sb("small", [128, 32]); gn = sb("gn_s", [128, NH])
        Dd = sb("Dd_s", [128, NH]); tiny = sb("tiny", [128, 8])
        hstg = sb("hstg", [128, 4, 32]); hld = sb("hld", [128, 4, 32]); hsum = sb("hsum", [128, 32])
        ps = es.enter_context(nc.psum_tensor("ps", [128, 4096], F32))
        psb = ps[:, :].bitcast(BF16)
        ccsem = [es.enter_context(nc.semaphore("ccs%d" % i)) for i in range(5)]
        ws = WeightStream(S, nc, es, 4, 16 * 256)

        wi0 = ev_w_in.rearrange("(kc p) n -> p kc n", p=128)
        wo0 = ev_w_out.rearrange("(kc p) n -> p kc n", p=128)
        u_xb = [ws.plan((16, 256), wi0[:, :, 2048 + i * 256:2048 + (i + 1) * 256]) for i in range(4)]
        u_u = [ws.plan((16, 256), wi0[:, :, i * 256:(i + 1) * 256]) for i in range(4)]
        u_v = [ws.plan((16, 256), wi0[:, :, 1024 + i * 256:1024 + (i + 1) * 256]) for i in range(4)]
        u_o = [ws.plan((16, 256), wo0[:, :, i * 256:(i + 1) * 256]) for i in range(8)]
        plan0 = plan_ffn_weights(ws, w_up[0], w_down[0])
        wi1 = od_w_in.rearrange("(kc p) n -> p kc n", p=128)
        wo1 = od_w_out.rearrange("(kc p) n -> p kc n", p=128)
        u_pre = [ws.plan((16, 256), wi1[:, :, h * 512 + 256:h * 512 + 512]) for h in range(NH)]
        u_main = []
        for h in range(NH):
            fi = ws.plan((16, 256), wi1[:, :, h * 512 + 256:h * 512 + 512])
            qg = ws.plan((16, 256), wi1[:, :, h * 512:h * 512 + 256])
            u_main.append((fi, qg))
        u_o1 = [ws.plan((16, 256), wo1[:, :, i * 256:(i + 1) * 256]) for i in range(8)]
        plan1 = plan_ffn_weights(ws, w_up[1], w_down[1])

        x0b = cv(R0, (NCH, TH), BF16)
        zf = cv(R0, (NCH, T + 2), F32)
        xb = cv(R1, (NCH, T + 2), BF16)
        pp = cv(R1, (8, TH), BF16)
        g1 = [cv(R1 + 18432, (TH,), F32), cv(R1 + 23040, (TH,), F32)]
        xbf = cv(R1 + 27648, (16 + TH,), F32)
        u = cv(R2, (8, TH), BF16)
        vt = cv(R2 + 18432, (9, 1024), BF16)
        g2p = [cv(R2 + 36864, (16 + TH,), F32), cv(R2 + 41536, (16 + TH,), F32)]
        g2 = [t[:, 16:16 + TH] for t in g2p]
        tA, tB = g2p
        wsT = cv(R2 + 46208, (8, 128), BF16)
        mask = cv(R2 + 48256, (128,), F32)
        bsT = cv(R2 + 48768, (8, 128), F32)
        lnv = cv(R2 + 52864, (2, 1024), F32)
        wp = cv(R2 + 61056, (8, 256), BF16)
        wsTf = g1[1][:, 0:1024].rearrange("p (h t) -> p h t", h=8)

        chp = S.new_chan(total=True)
        chx = S.new_chan(total=True)
        S.dma("sp", wsTf, wsT_d.rearrange("p (h t) -> p h t", h=8), chp)
        S.dma("sp", mask, mask_d, chp)
        S.dma("sp", bsT, bsT_d.rearrange("p (h t) -> p h t", h=8), chp)
        S.dma("sp", lnv, lnv_d.rearrange("p (a c) -> p a c", a=2), chp)
        S.dma("sp", psc[:, :], psc_d, chp)
        S.dma("sp", rc[:, :, :], rc_d.rearrange("p (g j) -> p g j", g=4), chp)
        S.dma("sp", flag[:, :], flag_d, chp)
        S.dma("sp", oh[:, :], oh_d, chp)
        S.dma("sp", ln1gb[:, :, :, :], ln1gb_d.rearrange("p (l c j) -> p l c j", l=2, j=2), chp)
        S.dma("sp", ln2gb[:, :, :, :], ln2gb_d.rearrange("p (l c j) -> p l c j", l=2, j=2), chp)
        S.dma("sp", cwb[:, :, :, :], cwb_d.rearrange("p (l c j) -> p l c j", l=2, j=4), chp)
        S.dma("sp", gn[:, :], gn_d, chp)
        S.dma("pool", wp, wp_d.rearrange("(a p) n -> p a n", p=128), chx)
        for c in range(NCH):
            S.dma("pool", x0b[:, c, :], x0T[c * 128:(c + 1) * 128, :], chx)
        ws.release(-1)
        S.memset("dve", ones[:, :], 1.0)
        S.memset("dve", eps[:, :], LN_EPS)
        S.memset("dve", xbf[:, 0:16], 0.0)
        S.memset("dve", tA[:, 0:16], 0.0)
        S.memset("dve", tB[:, 0:16], 0.0)
        for h in range(8):
            S.tt("dve", wsT[:, h, :], wsTf[:, h, :], mask, ALU.mult)

        GR = (0, 1536)
        TT3 = ((0, 512), (512, 512), (1024, 128))

        def proj_fm(wt, sub, g0):
            for kc in range(NCH):
                for (t0, w) in TT3:
                    S.mm(ps[:, g0 + t0:g0 + t0 + w], wt[:, kc, sub * 128:(sub + 1) * 128], x0b[:, kc, t0:t0 + w],
                         start=(kc == 0), stop=(kc == NCH - 1))
        gi = 0
        for i in range(4):
            wt = ws.get(u_xb[i])
            for sub in range(2):
                c = i * 2 + sub
                g = c // 2
                g0 = GR[gi % 2]
                gi += 1
                proj_fm(wt, sub, g0)
                S.act(xbf[:, 16:16 + TH], ps[:, g0:g0 + TH], AF.Identity)
                src = xbf
                dsts = [tA, tB]
                for k in range(g + 1):
                    sh = 1 << k
                    dst = dsts[k % 2]
                    S.tt("dve", dst[:, 16:16 + TH], src[:, 16:16 + TH], src[:, 16 - sh:16 + TH - sh], ALU.add)
                    src = dst
                win = B_WINDOWS[g]
                S.stt(pp[:, c, :], src[:, 16:16 + TH], 1.0 / win, xbf[:, 16:16 + TH], ALU.mult, ALU.subtract)
                S.tt("dve", small[:, 0:16], src[:, 16 + 128:16 + 144], rc[:, g, :], ALU.mult)
                S.tt("dve", pp[:, c, 128:144], small[:, 0:16], xbf[:, 16 + 128:16 + 144], ALU.subtract)
            ws.release(u_xb[i])
        for i in range(4):
            wt = ws.get(u_u[i])
            for sub in range(2):
                c = i * 2 + sub
                g0 = GR[gi % 2]
                proj_fm(wt, sub, g0)
                emit_gelu(S, u[:, c, :], ps[:, g0:g0 + TH], g1[gi % 2], g2[gi % 2])
                gi += 1
            ws.release(u_u[i])
        wv = [ws.get(k) for k in u_v]
        for tk in range(9):
            vr = g1[tk % 2]
            for cg in range(4):
                r0 = 3072 + ((tk * 4 + cg) % 2) * 512
                for kc in range(NCH):
                    S.mm(ps[:, r0:r0 + 256], x0b[:, kc, tk * 128:(tk + 1) * 128], wv[cg][:, kc, :],
                         start=(kc == 0), stop=(kc == NCH - 1))
                emit_gelu(S, vr[:, cg * 256:(cg + 1) * 256], ps[:, r0:r0 + 256],
                          g2[0][:, cg * 256:(cg + 1) * 256], g2[1][:, cg * 256:(cg + 1) * 256])
            sq = g2[0]
            S.add("dve", lambda e, vr=vr: e.reduce_sum(out=st[:, 0:1], in_=vr[:, 0:1024], axis=mybir.AxisListType.X),
                  reads=[vr[:, 0:1024]], writes=[st[:, 0:1]])
            S.act(sq[:, 0:1024], vr[:, 0:1024], AF.Square)
            S.add("dve", lambda e, sq=sq: e.reduce_sum(out=st[:, 1:2], in_=sq[:, 0:1024], axis=mybir.AxisListType.X),
                  reads=[sq[:, 0:1024]], writes=[st[:, 1:2]])
            S.ts("dve", st[:, 2:3], st[:, 0:1], 1.0 / 1024, None, ALU.mult)
            S.tt("dve", st[:, 3:4], st[:, 2:3], st[:, 2:3], ALU.mult)
            S.stt(st[:, 4:5], st[:, 1:2], 1.0 / 1024, st[:, 3:4], ALU.mult, ALU.subtract)
            S.act(st[:, 5:6], st[:, 4:5], AF.Sqrt, bias=eps[:, 0:1], scale=1.0)
            S.add("dve", lambda e: e.reciprocal(out=st[:, 6:7], in_=st[:, 5:6]), reads=[st[:, 5:6]],
                  writes=[st[:, 6:7]])
            S.ts("dve", vr[:, 0:1024], vr[:, 0:1024], st[:, 2:3], st[:, 6:7], ALU.subtract, ALU.mult)
            S.tt("dve", vr[:, 0:1024], vr[:, 0:1024], lnv[:, 0, :], ALU.mult)
            S.tt("dve", vt[:, tk, :], vr[:, 0:1024], lnv[:, 1, :], ALU.add)
        ws.release(u_v[3])
        for tk in range(9):
            for half in range(2):
                r0 = 2048 + half * 512
                for hh in range(4):
                    h = half * 4 + hh
                    S.mm(ps[:, r0 + hh * 128:r0 + (hh + 1) * 128], vt[:, tk, h * 128:(h + 1) * 128], wsT[:, h, :],
                         start=True, stop=True)
                tmp = g2[half][:, 0:512].rearrange("p (h t) -> p h t", h=4)
                S.tt("dve", tmp, ps[:, r0:r0 + 512].rearrange("p (h t) -> p h t", h=4),
                     bsT[:, half * 4:half * 4 + 4, :], ALU.add)
                uu = u[:, half * 4:half * 4 + 4, tk * 128:(tk + 1) * 128]
                S.tt("dve", uu, tmp, uu, ALU.mult)
        for g in range(4):
            for oc in range(2):
                g0 = GR[oc]
                for kc in range(2):
                    for (t0, w) in TT3:
                        S.mm(ps[:, g0 + t0:g0 + t0 + w], wp[:, g * 2 + kc, oc * 128:(oc + 1) * 128],
                             pp[:, g * 2 + kc, t0:t0 + w], start=(kc == 0), stop=(kc == 1))
            for oc in range(2):
                g0 = GR[oc]
                c = g * 2 + oc
                S.act(pp[:, c, :], ps[:, g0:g0 + TH], AF.Identity, scale=psc[:, c:c + 1])
        xs = [g1[0], g1[1]]
        chs = [S.new_chan(), S.new_chan()]
        for i in range(8):
            wt = ws.get(u_o[i])
            for sub in range(2):
                oc = i * 2 + sub
                g0 = GR[oc % 2]
                xst = xs[oc % 2]
                S.dma("sp", xst[:, 0:T + 2], x0T[oc * 128:(oc + 1) * 128, 126:TH], chs[oc % 2])
                for kc in range(NCH):
                    src = u[:, kc, :] if kc < 8 else pp[:, kc - 8, :]
                    lw = wt[:, kc, sub * 128:(sub + 1) * 128]
                    S.mm(ps[:, g0 + 510:g0 + 512], lw, src[:, 126:128], start=(kc == 0), stop=(kc == NCH - 1))
                    S.mm(ps[:, g0 + 512:g0 + 1024], lw, src[:, 128:640], start=(kc == 0), stop=(kc == NCH - 1))
                    S.mm(ps[:, g0 + 1024:g0 + 1536], lw, src[:, 640:1152], start=(kc == 0), stop=(kc == NCH - 1))
                S.stt(zf[:, oc, :], xst[:, 0:T + 2], ALPHA, ps[:, g0 + 510:g0 + 1536], ALU.mult, ALU.add)
            ws.release(u_o[i])
        tm_ln1 = {"eps": eps, "mean": g2[0], "rstd": g2[1],
                  "zb": [cv(R2 + k * 2052, (T + 2,), BF16) for k in range(2)],
                  "zs": [cv(R2 + (2 + k) * 2052, (T + 2,), BF16) for k in range(2)]}
        emit_ln(S, zf, 0, T + 2, ln1gb[:, 0, :, :], ones, tm_ln1, ps, [lambda c: zf[:, c, :]])
        S.ts("dve", zf[:, :, 0:2], zf[:, :, 0:2], flag[:, 0:1], None, ALU.mult)
        for c in range(NCH):
            S.act(xb[:, c, :], zf[:, c, :], AF.Identity)

        gq = cv(R2, (12, T), BF16)
        ft = [cv(R2 + 24576 + k * 4096, (T,), F32) for k in range(6)]
        tm_ffn = {"a": ft[0:2], "v": ft[2:4], "s": ft[4:6], "eps": eps, "mean": ft[0], "rstd": ft[1],
                  "zb": [cv(R2 + 49152 + k * 2048, (T,), BF16) for k in range(2)],
                  "zs": [cv(R2 + 53248 + k * 2048, (T,), BF16) for k in range(2)]}
        xb2 = cv(R1, (NCH, T), BF16)
        emit_ffn(S, ws, plan0, zf, xb, cwb[:, 0, :, :], gq, tm_ffn, ps)
        emit_ln(S, zf, 2, T, ln2gb[:, 0, :, :], ones, tm_ffn, ps,
                [lambda c: xb2[:, c, :], lambda c: zf[:, c, 2:T + 2]])
        chsp = [S.new_chan() for _ in range(NCH)]
        for c in range(NCH):
            S.dma("sp", xsp[c * 128:(c + 1) * 128, :], zf[:, c, 2:T + 2], chsp[c])

        def mkset(k):
            o0 = R0 + k * 32768
            d_ = {"A": cv(o0, (T,), F32), "B": cv(o0 + 4096, (T,), F32), "C": cv(o0 + 8192, (T,), F32),
                  "kend": cv(o0 + 12288, (T,), BF16), "kdec": cv(o0 + 14336, (T,), BF16),
                  "qdec": cv(o0 + 16384, (T,), BF16), "ibf": cv(o0 + 18432, (T,), BF16),
                  "sg": cv(o0 + 20480, (T,), BF16), "osq": cv(o0 + 22528, (T,), BF16),
                  "attm": cv(o0 + 24576, (8, 128), BF16), "kt0": cv(o0 + 26624, (8, 128), BF16),
                  "kt1": cv(o0 + 28672, (8, 128), BF16), "vtk": cv(o0 + 30720, (8, 128), BF16),
                  "Sb": cv(R2 + 32768, (NCK, 128), BF16) if k == 0 else cv(R2 + 61472, (NCK, 128), BF16),
                  "dec": sb("dec%d" % k, [128, NCK]), "Sf": [sb("Sf%d_%d" % (k, i), [128, 128]) for i in range(2)],
                  "Pp": [sb("Pp%d_%d" % (k, i), [128, 128]) for i in range(2)],
                  "upst": cv(R2 + 59408, (4, 129), F32) if k == 0 else sb("upst1", [128, 4, 129]),
                  "stg": cv(R2 + 57344, (4, 129), F32),
                  "chu": S.new_chan(), "chst": S.new_chan()}
            return d_
        sets = [mkset(0), mkset(1)]
        y = cv(R2, (NH, T), BF16)
        xs1 = [cv(R2 + 40960, (T,), F32), cv(R2 + 45056, (T,), F32)]
        tm_ln1b = {"eps": eps, "mean": xs1[0], "rstd": xs1[1],
                   "zb": [cv(R2 + 49152 + k * 2048, (T,), BF16) for k in range(2)],
                   "zs": [cv(R2 + 53248 + k * 2048, (T,), BF16) for k in range(2)]}
        chc = S.new_chan(total=True)
        cst = {}
        lbp = sb("lbp_s", [128, 2, NH]); cst["lb"] = sb("lb", [128, NH]); cst["oml"] = sb("oml", [128, NH])
        cst["mask2"] = sb("mask2_s", [128, 128]); identf = sb("identf", [128, 128]); cst["ident"] = sb("ident_s", [128, 128], BF16)
        cst["pm"] = sb("pm", [128, 2])
        cst["rm"] = cv(R2 + 36864, (T,), F32)
        S.dma("sp", lbp[:, :, :], lbp_d.rearrange("p (l h) -> p l h", l=2), chc)
        S.dma("sp", cst["mask2"][:, :], mask2_d, chc)
        S.dma("sp", identf[:, :], ident_d, chc)
        S.copy("dve", cst["ident"][:, :], identf[:, :])
        S.tt("dve", cst["lb"][:, :], lbp[:, 1, :], lbp[:, 0, :], ALU.subtract)
        S.act(cst["lb"][:, :], cst["lb"][:, :], AF.Sigmoid)
        S.ts("dve", cst["oml"][:, :], cst["lb"][:, :], -1.0, 1.0, ALU.mult, ALU.add)
        S.memset("dve", cst["rm"], 1.0)
        S.memset("dve", cst["rm"].rearrange("p (c t) -> p c t", t=CH)[:, :, 0:1], 0.0)
        S.memset("dve", cst["pm"][:, :], 0.0)
        S.memset("dve", cst["pm"][0:64, 0:1], 1.0)
        S.memset("dve", cst["pm"][64:128, 1:2], 1.0)
        oh3 = oh[:, 0:4].rearrange("p (j o) -> p j o", o=1)
        G0, G1, PB5, O0 = 0, 1024, 2560, 3072

        def proj1(wt, blk, g0):
            for kc in range(NCH):
                for t_ in range(2):
                    S.mm(ps[:, g0 + t_ * 512:g0 + (t_ + 1) * 512], wt[:, kc, blk * 128:(blk + 1) * 128],
                         xb2[:, kc, t_ * 512:(t_ + 1) * 512], start=(kc == 0), stop=(kc == NCH - 1))

        def cc_op(idx, src_t, dst_t):
            if use_cc:
                def fn(e):
                    e.collective_compute("AllReduce", ALU.add, replica_groups=SEQ_GROUPS,
                                         ins=[src_t.ap().opt()], outs=[dst_t.ap().opt()]).then_inc(ccsem[idx])
                    return None
                S.add("pool", fn, reads=[src_t.ap()], writes=[])

                def fn2(e):
                    e.wait_ge(ccsem[idx], 1)
                    return e.memset(tiny[:, idx:idx + 1], 0.0)
                return lambda: S.add("pool", fn2, reads=[], writes=[dst_t.ap(), tiny[:, idx:idx + 1]])
            else:
                chq = S.new_chan()
                S.dma("sp", dst_t.ap(), src_t.ap(), chq)
                return lambda: None

        def scan_group(q, g4, st_):
            pb = (2560, 3072, 3584, 2560)[g4]
            for cc in range(4):
                c = g4 * 4 + cc
                j, par = divmod(c, 2)
                S.mm(ps[:, pb + cc * 128:pb + (cc + 1) * 128], (q["kt0"], q["kt1"])[par][:, j, :], q["vtk"][:, j, :],
                     start=True, stop=True)
            for cc in range(4):
                c = g4 * 4 + cc
                cur = st_["cur"]
                S.stt(q["Sf"][1 - cur][:, :], q["Sf"][cur][:, :], q["dec"][:, c:c + 1],
                      ps[:, pb + cc * 128:pb + (cc + 1) * 128], ALU.mult, ALU.add)
                st_["cur"] = 1 - cur
                if st_["sb"] and c + 1 < NCK:
                    S.act(q["Sb"][:, c + 1, :], q["Sf"][1 - cur][:, :], AF.Identity)

        def interleave(bsteps, asteps, after):
            ai = 0
            for bi, bstep in enumerate(bsteps):
                bstep()
                while ai < len(asteps) and after[ai] == bi:
                    asteps[ai]()
                    ai += 1
            while ai < len(asteps):
                asteps[ai]()
                ai += 1

        cc_done = []

        def pre_A(h):
            q = sets[h % 2]

            def a1():
                q["wt"] = ws.get(u_pre[h])
                proj1(q["wt"], 0, G0)

            def a2():
                proj1(q["wt"], 1, G1)
                ws.release(u_pre[h])
                hgrn_gates(S, h, cst, ps[:, G0:G0 + T], q["A"], q["B"], q["C"], q["kend"], q["dec"][:, :])
                S.act(q["ibf"], ps[:, G1:G1 + T], AF.Identity)
                C3 = q["C"].rearrange("p (c t) -> p c t", t=CH)
                S.add("dve", lambda e, C3=C3, h=h: e.reduce_sum(out=Dd[:, h:h + 1], in_=C3[:, :, CH - 1:CH],
                                                               axis=mybir.AxisListType.XY),
                      reads=[q["C"]], writes=[Dd[:, h:h + 1]])
                S.act(Dd[:, h:h + 1], Dd[:, h:h + 1], AF.Exp)
            return [a1, a2]

        def pre_B(h):
            q = sets[h % 2]
            st_ = {"cur": 0, "sb": False}

            def b1():
                hgrn_transposes(S, cst, psb, q["kend"], q["kt0"], q["kt1"])
                hgrn_transposes(S, cst, psb, q["ibf"], q["vtk"])
                S.memset("dve", q["Sf"][0][:, :], 0.0)

            def bfin():
                fin = st_["cur"]
                for j in range(4):
                    S.ts("dve", q["stg"][:, j, 0:128], q["Sf"][fin][:, :], oh[:, j:j + 1], None, ALU.mult)
                S.ts("dve", q["stg"][:, :, 128:129], oh3, Dd[:, h:h + 1], None, ALU.mult)
                g, hl = divmod(h, 4)
                S.dma("sp", ccin[g].ap().rearrange("(j l d) n -> d j l n", j=4, l=4)[:, :, hl, :], q["stg"][:, :, :],
                      q["chst"])
                if hl == 3:
                    cc_done.append(cc_op(g, ccin[g], ccout[g]))
            return [b1] + [lambda g4=g4: scan_group(q, g4, st_) for g4 in range(4)] + [bfin]

        for stp in pre_A(0):
            stp()
        for h in range(NH):
            nxt = pre_A(h + 1) if h + 1 < NH else []
            interleave(pre_B(h), nxt, [0, 2])

        def main_A(h):
            q = sets[h % 2]
            g, hl = divmod(h, 4)
            fi, qg = u_main[h]

            def a1():
                if hl == 0:
                    cc_done[g]()
                up = q["upst"]
                S.dma("sp", up[:, :, :], ccout[g].ap().rearrange("(j l d) n -> d j l n", j=4, l=4)[:, :, hl, :], q["chu"])
                Pp_, Sf_ = q["Pp"], q["Sf"]
                S.stt(Pp_[0][:, :], up[:, 0, 0:128], up[:, 1, 128:129], up[:, 1, 0:128], ALU.mult, ALU.add)
                S.stt(Pp_[1][:, :], Pp_[0][:, :], up[:, 2, 128:129], up[:, 2, 0:128], ALU.mult, ALU.add)
                S.ts("dve", Sf_[0][:, :], up[:, 0, 0:128], oh[:, 1:2], None, ALU.mult)
                S.stt(Sf_[0][:, :], Pp_[0][:, :], oh[:, 2:3], Sf_[0][:, :], ALU.mult, ALU.add)
                S.stt(Sf_[0][:, :], Pp_[1][:, :], oh[:, 3:4], Sf_[0][:, :], ALU.mult, ALU.add)
                q["wt"] = ws.get(fi)
                proj1(q["wt"], 0, G0)

            def a2():
                proj1(q["wt"], 1, G1)
                ws.release(fi)
                hgrn_gates(S, h, cst, ps[:, G0:G0 + T], q["A"], q["B"], q["C"], q["kend"], q["dec"][:, :],
                           kdec_bf=q["kdec"], eC=q["A"])
                S.act(q["ibf"], ps[:, G1:G1 + T], AF.Identity)

            def a3():
                q["wt"] = ws.get(qg)
                proj1(q["wt"], 0, G0)

            def a4():
                proj1(q["wt"], 1, G1)
                ws.release(qg)
                S.act(q["B"], ps[:, G0:G0 + T], AF.Silu)
                S.tt("dve", q["qdec"], q["B"], q["A"], ALU.mult)
                S.act(q["sg"], ps[:, G1:G1 + T], AF.Sigmoid)
            return [a1, a2, a3, a4]

        def main_B(h):
            q = sets[h % 2]
            st_ = {"cur": 0, "sb": True}

            def b1():
                hgrn_transposes(S, cst, psb, q["kend"], q["kt0"], q["kt1"])
                hgrn_transposes(S, cst, psb, q["ibf"], q["vtk"])
                S.act(q["Sb"][:, 0, :], q["Sf"][0][:, :], AF.Identity)

            def batt(half):
                pb = 3072 + half * 512
                for jj in range(4):
                    j = half * 4 + jj
                    S.mm(ps[:, pb + jj * 128:pb + (jj + 1) * 128], q["kdec"][:, j * 128:(j + 1) * 128],
                         q["qdec"][:, j * 128:(j + 1) * 128], start=True, stop=True)
                for jj in range(4):
                    j = half * 4 + jj
                    S.tt("dve", q["attm"][:, j, :], ps[:, pb + jj * 128:pb + (jj + 1) * 128], cst["mask2"][:, :],
                         ALU.mult)

            def bo():
                for j in range(8):
                    S.mm(ps[:, O0 + j * 128:O0 + (j + 1) * 128], q["vtk"][:, j, :], q["attm"][:, j, :], start=True,
                         stop=False)
                    S.mm(ps[:, O0 + j * 128:O0 + j * 128 + 64], q["Sb"][:, 2 * j, :], q["qdec"][:, j * 128:j * 128 + 64],
                         start=False, stop=False)
                    S.mm(ps[:, O0 + j * 128 + 64:O0 + (j + 1) * 128], q["Sb"][:, 2 * j + 1, :],
                         q["qdec"][:, j * 128 + 64:(j + 1) * 128], start=False, stop=True)
                S.act(q["osq"], ps[:, O0:O0 + T], AF.Square)

            def bnorm():
                for t_ in range(2):
                    sl = slice(t_ * 512, (t_ + 1) * 512)
                    S.mm(ps[:, PB5:PB5 + 512], ones[:, :], q["osq"][:, sl], start=True, stop=True)
                    S.act(q["A"][:, sl], ps[:, PB5:PB5 + 512], AF.Ln, bias=eps[:, 0:1], scale=1.0 / 128)
                S.act(q["A"], q["A"], AF.Exp, scale=-0.5)
                S.stt(q["C"], ps[:, O0:O0 + T], gn[:, h:h + 1], q["A"], ALU.mult, ALU.mult)
                S.tt("dve", y[:, h, :], q["C"], q["sg"], ALU.mult)
            return ([b1] + [lambda g4=g4: scan_group(q, g4, st_) for g4 in range(4)]
                    + [lambda: batt(0), lambda: batt(1), bo, bnorm])

        for stp in main_A(0):
            stp()
        for h in range(NH):
            nxt = main_A(h + 1) if h + 1 < NH else []
            interleave(main_B(h), nxt, [0, 2, 4, 6])
        chs1 = [S.new_chan(), S.new_chan()]
        for i in range(8):
            wt = ws.get(u_o1[i])
            for sub in range(2):
                oc = i * 2 + sub
                g0 = (oc % 2) * 1024
                xst = xs1[oc % 2]
                S.dma("sp", xst, xsp[oc * 128:(oc + 1) * 128, :], chs1[oc % 2])
                for kc in range(NCH):
                    for t_ in range(2):
                        S.mm(ps[:, g0 + t_ * 512:g0 + (t_ + 1) * 512], wt[:, kc, sub * 128:(sub + 1) * 128],
                             y[:, kc, t_ * 512:(t_ + 1) * 512], start=(kc == 0), stop=(kc == NCH - 1))
                S.stt(zf[:, oc, 2:T + 2], xst, ALPHA, ps[:, g0:g0 + T], ALU.mult, ALU.add)
            ws.release(u_o1[i])
        emit_ln(S, zf, 2, T, ln1gb[:, 1, :, :], ones, tm_ln1b, ps,
                [lambda c: xb[:, c, 2:T + 2], lambda c: zf[:, c, 2:T + 2]])
        for j in range(4):
            S.ts("dve", hstg[:, j, :].rearrange("p (c t) -> p c t", t=2), zf[:, :, T:T + 2], oh[:, j:j + 1], None,
                 ALU.mult)
        chh = S.new_chan()
        S.dma("sp", cch_in.ap().rearrange("(j p) n -> p j n", p=128), hstg[:, :, :], chh)
        done_h = cc_op(4, cch_in, cch_out)
        done_h()
        chh2 = S.new_chan()
        S.dma("sp", hld[:, :, :], cch_out.ap().rearrange("(j p) n -> p j n", p=128), chh2)
        S.ts("dve", hsum[:, :], hld[:, 0, :], oh[:, 4:5], None, ALU.mult)
        for j in range(1, 4):
            S.stt(hsum[:, :], hld[:, j, :], oh[:, 4 + j:5 + j], hsum[:, :], ALU.mult, ALU.add)
        S.act(xb[:, :, 0:2], hsum[:, :].rearrange("p (c t) -> p c t", t=2), AF.Identity)
        emit_ffn(S, ws, plan1, zf, xb, cwb[:, 1, :, :], gq, tm_ffn, ps)
        emit_ln(S, zf, 2, T, ln2gb[:, 1, :, :], ones, tm_ffn, ps, [lambda c: zf[:, c, 2:T + 2]])
        cho = [S.new_chan() for _ in range(4)]
        for c in range(NCH):
            S.dma("sp", outT[c * 128:(c + 1) * 128, :], zf[:, c, 2:T + 2], cho[c % 4])
        S.emit(nc, es, final_chans=cho)
    return nc, S


def fused_inputs(inp):
    f32 = np.float32
    maps = prep_mix0_inputs(inp["x"], inp["ev_w_in"], inp["ev_ln_v_g"], inp["ev_ln_v_b"], inp["ev_w_s"],
                            inp["ev_b_s"], inp["ev_w_pool"], inp["ev_pool_scale"], inp["ev_w_out"],
                            inp["ln1_g"], inp["ln1_b"])
    ln1gb = np.stack([np.stack([_pm(inp["ln1_g"][l], 16), _pm(inp["ln1_b"][l], 16)], axis=-1) for l in range(2)], axis=1)
    ln2gb = np.stack([np.stack([_pm(inp["ln2_g"][l], 16), _pm(inp["ln2_b"][l], 16)], axis=-1) for l in range(2)], axis=1)
    cwbs = []
    for l in range(2):
        cw = np.asarray(inp["ffn_conv_w"][l], f32)
        cb = np.asarray(inp["ffn_conv_b"][l], f32)
        cwbs.append(np.stack([_pm(cw[0], 88), _pm(cw[1], 88), _pm(cw[2], 88), _pm(cb, 88)], axis=-1))
    cwb = np.stack(cwbs, axis=1)
    common = {
        "ln1gb": np.ascontiguousarray(ln1gb.reshape(128, 64)), "ln2gb": np.ascontiguousarray(ln2gb.reshape(128, 64)),
        "cwb": np.ascontiguousarray(cwb.reshape(128, 2 * 88 * 4)),
        "w_up0": np.ascontiguousarray(inp["ffn_w_up"][0], f32), "w_up1": np.ascontiguousarray(inp["ffn_w_up"][1], f32),
        "w_down0": np.ascontiguousarray(inp["ffn_w_down"][0], f32),
        "w_down1": np.ascontiguousarray(inp["ffn_w_down"][1], f32),
        "od_w_in": regroup_w_in(inp["od_w_in"][0]), "od_w_out": np.ascontiguousarray(inp["od_w_out"][0], f32),
        "gn": _pm(inp["od_norm_g"][0], NH)}
    common.update(hgrn_const_inputs(inp["lb_param"]))
    out = []
    for c in range(NCORES):
        b, s = divmod(c, 4)
        m0 = maps[c]
        m = dict(common)
        for k in ("x0T", "wsT", "maskA", "bsT", "lnv", "w_pool", "pscale", "rcnt", "flag"):
            m[k] = m0[k]
        m["ev_w_in"] = m0["w_in"]
        m["ev_w_out"] = m0["w_out"]
        oh = np.zeros((128, 8), f32)
        oh[:, s] = 1.0
        if s > 0:
            oh[:, 4 + s - 1] = 1.0
        m["oh"] = oh
        out.append(m)
    return out


def kernel(**inputs):
    nc, _ = build_fused(use_cc=True)
    maps = fused_inputs(inputs)
    res = run_bass_kernel_spmd(nc, maps, core_ids=list(range(NCORES))).results
    out = np.zeros((2, 4 * T, D), np.float32)
    for c in range(NCORES):
        b, s = divmod(c, 4)
        out[b, s * T:(s + 1) * T] = res[c]["outT"].T
    return out
```

```python
import numpy as np
from contextlib import ExitStack
import concourse.bass as bass
import concourse.mybir as mybir
from concourse.bass_utils import run_bass_kernel_spmd

F32 = mybir.dt.float32
BF16 = mybir.dt.bfloat16
AF = mybir.ActivationFunctionType
ALU = mybir.AluOpType

D = 2048
NCH = 16
T = 1024
NCORES = 8
DFF = 5632
NFF = 44
ALPHA = 4.0 ** 0.25
LN_EPS = 1e-5
ENGS = ("pe", "act", "dve", "pool", "sp")
_DT_SIZE = {F32: 4, BF16: 2}


def _dsize(dt):
    return _DT_SIZE.get(dt, 4)


class _Op:
    __slots__ = ("eng", "idx", "fn", "deps", "chan", "chan_val", "signal", "val")


class Sched:
    def __init__(self):
        self.ops = {e: [] for e in ENGS}
        self.track = {}
        self.chan_cnt = []
        self.chan_total = []

    @staticmethod
    def _rng(ap):
        t = ap.tensor
        name = t.name
        sp = str(ap.space) if hasattr(ap, "space") else ""
        pat = ap.ap
        esz = _dsize(ap.dtype)
        if "DRAM" in sp.upper() or "Dram" in type(t).__name__ or "DRam" in type(t).__name__:
            ext = 1
            for (st, cnt) in pat:
                ext += abs(st) * (cnt - 1)
            return name, ap.offset * esz, (ap.offset + ext) * esz
        pstride = pat[0][0]
        lo = ap.offset % pstride if pstride > 0 else ap.offset
        ext = 1
        for (st, cnt) in pat[1:]:
            ext += abs(st) * (cnt - 1)
        return name, lo * esz, (lo + ext) * esz

    def _touch(self, name, lo, hi, op, is_write, deps):
        segs = self.track.setdefault(name, [])
        new = []
        covered = []
        for s in segs:
            slo, shi, w, rs = s
            if shi <= lo or slo >= hi:
                new.append(s)
                continue
            if slo < lo:
                new.append([slo, lo, w, list(rs)])
            if shi > hi:
                new.append([hi, shi, w, list(rs)])
            olo, ohi = max(slo, lo), min(shi, hi)
            if w is not None:
                deps.add(w)
            if is_write:
                for r in rs:
                    deps.add(r)
            else:
                covered.append([olo, ohi, w, rs + [op]])
        if is_write:
            new.append([lo, hi, op, []])
        else:
            covered.sort(key=lambda s: s[0])
            cur = lo
            for c in covered:
                if c[0] > cur:
                    new.append([cur, c[0], None, [op]])
                new.append(c)
                cur = c[1]
            if cur < hi:
                new.append([cur, hi, None, [op]])
        self.track[name] = new

    def add(self, eng, fn, reads=(), writes=(), chan=None):
        o = _Op()
        o.eng = eng
        o.fn = fn
        o.chan = chan
        o.signal = False
        o.val = None
        o.chan_val = None
        deps = set()
        for ap in reads:
            if ap is None or isinstance(ap, (int, float)):
                continue
            n, lo, hi = self._rng(ap)
            self._touch(n, lo, hi, o, False, deps)
        for ap in writes:
            n, lo, hi = self._rng(ap)
            if eng == "pe":
                lo = (lo // 2048) * 2048
                hi = ((hi + 2047) // 2048) * 2048
            self._touch(n, lo, hi, o, True, deps)
        deps.discard(o)
        o.deps = deps
        if chan is not None:
            self.chan_cnt[chan] += 1
            o.chan_val = 16 * self.chan_cnt[chan]
        o.idx = len(self.ops[eng])
        self.ops[eng].append(o)
        return o

    def new_chan(self, total=False):
        self.chan_cnt.append(0)
        self.chan_total.append(total)
        return len(self.chan_cnt) - 1

    def emit(self, nc, es, final_chans=()):
        for e in ENGS:
            for o in self.ops[e]:
                for d in o.deps:
                    if d.chan is None:
                        d.signal = True
        for e in ENGS:
            c = 0
            for o in self.ops[e]:
                if o.chan is None and o.signal:
                    c += 1
                    o.val = c
        esem = {e: es.enter_context(nc.semaphore("s_" + e)) for e in ENGS}
        csem = [es.enter_context(nc.semaphore("c_%d" % i)) for i in range(len(self.chan_cnt))]
        block = es.enter_context(nc.Block())
        nwaits = {e: 0 for e in ENGS}

        def run(engname, eobj):
            seen = {}
            for o in self.ops[engname]:
                need = {}
                for d in o.deps:
                    if d.chan is not None:
                        key = ("c", d.chan)
                        v = 16 * self.chan_cnt[d.chan] if self.chan_total[d.chan] else d.chan_val
                    else:
                        if d.eng == engname and engname == "pe":
                            continue
                        key = ("e", d.eng)
                        v = d.val
                    if v > need.get(key, 0):
                        need[key] = v
                for key, v in need.items():
                    if v <= seen.get(key, 0):
                        continue
                    seen[key] = v
                    sem = csem[key[1]] if key[0] == "c" else esem[key[1]]
                    eobj.wait_ge(sem, v)
                    nwaits[engname] += 1
                inst = o.fn(eobj)
                if o.chan is not None:
                    inst.then_inc(csem[o.chan], 16)
                elif o.signal:
                    assert inst is not None
                    inst.then_inc(esem[engname], 1)
            if engname == "sp":
                for ch in final_chans:
                    if self.chan_cnt[ch] > 0:
                        eobj.wait_ge(csem[ch], 16 * self.chan_cnt[ch])

        @block.tensor
        def _(e):
            run("pe", e)

        @block.scalar
        def _(e):
            run("act", e)

        @block.vector
        def _(e):
            run("dve", e)

        @block.gpsimd
        def _(e):
            run("pool", e)

        @block.sync
        def _(e):
            run("sp", e)

        self.nwaits = nwaits

    def mm(self, out, lhsT, rhs, start=True, stop=True):
        return self.add("pe", lambda e: e.matmul(out, lhsT=lhsT, rhs=rhs, start=start, stop=stop),
                        reads=[lhsT, rhs], writes=[out])

    def transpose(self, out, in_, ident):
        return self.add("pe", lambda e: e.transpose(out, in_, ident), reads=[in_, ident], writes=[out])

    def act(self, out, in_, func, bias=None, scale=None):
        kw = {}
        rd = [in_]
        if bias is not None:
            kw["bias"] = bias
            rd.append(bias)
        if scale is not None:
            kw["scale"] = scale
            rd.append(scale)
        return self.add("act", lambda e: e.activation(out=out, in_=in_, func=func, **kw), reads=rd, writes=[out])

    def tt(self, eng, out, in0, in1, op):
        return self.add(eng, lambda e: e.tensor_tensor(out=out, in0=in0, in1=in1, op=op),
                        reads=[in0, in1], writes=[out])

    def ts(self, eng, out, in0, s1, s2, op0, op1=None):
        if op1 is None:
            return self.add(eng, lambda e: e.tensor_scalar(out=out, in0=in0, scalar1=s1, scalar2=None, op0=op0),
                            reads=[in0, s1], writes=[out])
        return self.add(eng, lambda e: e.tensor_scalar(out=out, in0=in0, scalar1=s1, scalar2=s2, op0=op0, op1=op1),
                        reads=[in0, s1, s2], writes=[out])

    def stt(self, out, in0, scalar, in1, op0, op1):
        return self.add("dve", lambda e: e.scalar_tensor_tensor(out=out, in0=in0, scalar=scalar, in1=in1,
                                                                op0=op0, op1=op1),
                        reads=[in0, scalar, in1], writes=[out])

    def copy(self, eng, out, in_):
        if eng == "act":
            return self.add("act", lambda e: e.copy(out=out, in_=in_), reads=[in_], writes=[out])
        return self.add(eng, lambda e: e.tensor_copy(out=out, in_=in_), reads=[in_], writes=[out])

    def memset(self, eng, ap, val):
        return self.add(eng, lambda e: e.memset(ap, val), writes=[ap])

    def dma(self, eng, out, in_, chan):
        return self.add(eng, lambda e: e.dma_start(out=out, in_=in_), reads=[in_], writes=[out], chan=chan)


class WeightStream:
    def __init__(self, S, nc, es, nslots, free_elems, name="wslot"):
        self.S = S
        self.slots = [es.enter_context(nc.sbuf_tensor("%s%d" % (name, i), [128, free_elems], BF16))
                      for i in range(nslots)]
        self.chans = [S.new_chan() for _ in range(nslots)]
        self.uses = []
        self.loaded = 0
        self.released = -1
        self.n = nslots

    def plan(self, shape, src):
        self.uses.append((shape, src))
        return len(self.uses) - 1

    def view(self, k):
        shape, _ = self.uses[k]
        sl = self.slots[k % self.n]
        n = 1
        for s in shape:
            n *= s
        v = sl[:, 0:n]
        if len(shape) == 2:
            return v.rearrange("p (a b) -> p a b", a=shape[0], b=shape[1])
        return v

    def _load_upto(self, k):
        while self.loaded < len(self.uses) and self.loaded <= k:
            j = self.loaded
            _, src = self.uses[j]
            self.S.dma("pool", self.view(j), src, self.chans[j % self.n])
            self.loaded += 1

    def get(self, k):
        assert k <= self.released + self.n, (k, self.released)
        self._load_upto(k)
        return self.view(k)

    def release(self, k):
        self.released = max(self.released, k)
        self._load_upto(self.released + self.n)


FF_QUARTERS = (12, 10, 12, 10)


def plan_ffn_weights(ws, w_up, w_down):
    plan = []
    base = 0
    wu = w_up.rearrange("(kc p) n -> p kc n", p=128)
    for q, nq in enumerate(FF_QUARTERS):
        ups = []
        for j in range(0, nq, 2):
            ca = base + j
            ua = ws.plan((16, 256), wu[:, :, ca * 128:ca * 128 + 256])
            uv = ws.plan((16, 256), wu[:, :, (NFF + ca) * 128:(NFF + ca) * 128 + 256])
            ups.append((ca, ua, uv))
        downs = []
        wd = w_down[base * 128:(base + nq) * 128, :].rearrange("(j p) n -> p j n", p=128)
        for op_ in range(8):
            downs.append((op_, ws.plan((nq, 256), wd[:, :, op_ * 256:(op_ + 1) * 256])))
        plan.append((base, nq, ups, downs))
        base += nq
    return plan


def emit_ffn(S, ws, plan, xf, xb, cwb, gq, tmps, ps):
    G = (0, 1536)
    gi = 0
    for (base, nq, ups, downs) in plan:
        for (ca, ua, uv) in ups:
            wa = ws.get(ua)
            wv = ws.get(uv)
            for sub in range(2):
                c_a = ca + sub
                c_v = NFF + ca + sub
                j = c_a - base
                tm = {}
                for which, (wt, cc) in enumerate(((wa, c_a), (wv, c_v))):
                    g0 = G[which]
                    for kc in range(NCH):
                        lw = wt[:, kc, sub * 128:(sub + 1) * 128]
                        S.mm(ps[:, g0 + 510:g0 + 512], lw, xb[:, kc, 0:2], start=(kc == 0), stop=(kc == NCH - 1))
                        S.mm(ps[:, g0 + 512:g0 + 1024], lw, xb[:, kc, 2:514], start=(kc == 0), stop=(kc == NCH - 1))
                        S.mm(ps[:, g0 + 1024:g0 + 1536], lw, xb[:, kc, 514:1026], start=(kc == 0),
                             stop=(kc == NCH - 1))
                    tmp = tmps["a" if which == 0 else "v"][gi % 2]
                    tm[which] = tmp
                    S.act(tmp[:, :], ps[:, g0 + 512:g0 + 1536], AF.Identity, bias=cwb[:, cc, 3:4],
                          scale=cwb[:, cc, 2:3])
                    S.stt(tmp[:, :], ps[:, g0 + 511:g0 + 1535], cwb[:, cc, 1:2], tmp[:, :], ALU.mult, ALU.add)
                    S.stt(tmp[:, :], ps[:, g0 + 510:g0 + 1534], cwb[:, cc, 0:1], tmp[:, :], ALU.mult, ALU.add)
                sa = tmps["s"][gi % 2]
                S.act(sa[:, :], tm[0][:, :], AF.Silu)
                S.tt("dve", gq[:, j, :], sa[:, :], tm[1][:, :], ALU.mult)
                gi += 1
            ws.release(uv)
        for (op_, ud) in downs:
            wd = ws.get(ud)
            for sub in range(2):
                oc = op_ * 2 + sub
                for tt_ in range(2):
                    for j in range(nq):
                        S.mm(ps[:, 3072 + tt_ * 512:3072 + (tt_ + 1) * 512], wd[:, j, sub * 128:(sub + 1) * 128],
                             gq[:, j, tt_ * 512:(tt_ + 1) * 512], start=(j == 0), stop=(j == nq - 1))
                if base == 0:
                    S.stt(xf[:, oc, 2:1026], xf[:, oc, 2:1026], ALPHA, ps[:, 3072:4096], ALU.mult, ALU.add)
                else:
                    S.tt("dve", xf[:, oc, 2:1026], xf[:, oc, 2:1026], ps[:, 3072:4096], ALU.add)
            ws.release(ud)


def emit_ln(S, zf, c0, n, gb, ones, tmps, ps, outs):
    nt = (n + 511) // 512
    zb = tmps["zb"]
    zs = tmps["zs"]
    for c in range(NCH):
        b0 = zb[c % 2]
        s0 = zs[c % 2]
        S.act(b0[:, 0:n], zf[:, c, c0:c0 + n], AF.Identity)
        S.act(s0[:, 0:n], zf[:, c, c0:c0 + n], AF.Square)
        for t_ in range(nt):
            w = min(512, n - t_ * 512)
            S.mm(ps[:, t_ * 512:t_ * 512 + w], ones[:, :], b0[:, t_ * 512:t_ * 512 + w], start=(c == 0),
                 stop=(c == NCH - 1))
            S.mm(ps[:, 2048 + t_ * 512:2048 + t_ * 512 + w], ones[:, :], s0[:, t_ * 512:t_ * 512 + w],
                 start=(c == 0), stop=(c == NCH - 1))
    mean = tmps["mean"]
    rstd = tmps["rstd"]
    S.ts("dve", mean[:, 0:n], ps[:, 0:n], 1.0 / D, None, ALU.mult)
    S.tt("dve", rstd[:, 0:n], mean[:, 0:n], mean[:, 0:n], ALU.mult)
    S.stt(rstd[:, 0:n], ps[:, 2048:2048 + n], 1.0 / D, rstd[:, 0:n], ALU.mult, ALU.subtract)
    S.act(rstd[:, 0:n], rstd[:, 0:n], AF.Sqrt, bias=tmps["eps"][:, 0:1], scale=1.0)
    S.add("dve", lambda e: e.reciprocal(out=rstd[:, 0:n], in_=rstd[:, 0:n]), reads=[rstd[:, 0:n]],
          writes=[rstd[:, 0:n]])
    for c in range(NCH):
        zc = zf[:, c, c0:c0 + n]
        S.tt("dve", zc, zc, mean[:, 0:n], ALU.subtract)
        S.tt("dve", zc, zc, rstd[:, 0:n], ALU.mult)
        for i, dst in enumerate(outs):
            S.act(dst(c), zc, AF.Identity, bias=gb[:, c, 1:2], scale=gb[:, c, 0:1])


def build_ffn_launch():
    nc = bass.Bass("TRN2", target_bir_lowering=False)
    xT = nc.dram_tensor("xT", [D, T + 2], F32, kind="ExternalInput").ap()
    w_up = nc.dram_tensor("w_up", [D, 2 * DFF], F32, kind="ExternalInput").ap()
    w_down = nc.dram_tensor("w_down", [DFF, D], F32, kind="ExternalInput").ap()
    cwb_d = nc.dram_tensor("cwb", [128, 88 * 4], F32, kind="ExternalInput").ap()
    gb_d = nc.dram_tensor("ln2gb", [128, 32], F32, kind="ExternalInput").ap()
    yT = nc.dram_tensor("yT", [D, T], F32, kind="ExternalOutput").ap()
    S = Sched()
    with ExitStack() as es:
        xf = es.enter_context(nc.sbuf_tensor("xf", [128, NCH, T + 2], F32))
        xb = es.enter_context(nc.sbuf_tensor("xb", [128, NCH, T + 2], BF16))
        cwb = es.enter_context(nc.sbuf_tensor("cwb_s", [128, 88, 4], F32))
        gb = es.enter_context(nc.sbuf_tensor("gb_s", [128, NCH, 2], F32))
        gq = es.enter_context(nc.sbuf_tensor("gq", [128, 12, T], BF16))
        ones = es.enter_context(nc.sbuf_tensor("ones", [128, 128], BF16))
        eps = es.enter_context(nc.sbuf_tensor("eps", [128, 1], F32))
        tmps = {
            "a": [es.enter_context(nc.sbuf_tensor("ta%d" % i, [128, T], F32)) for i in range(2)],
            "v": [es.enter_context(nc.sbuf_tensor("tv%d" % i, [128, T], F32)) for i in range(2)],
            "s": [es.enter_context(nc.sbuf_tensor("tsl%d" % i, [128, T], F32)) for i in range(2)],
            "eps": eps,
        }
        tmps["zb"] = [es.enter_context(nc.sbuf_tensor("zb%d" % i, [128, T], BF16)) for i in range(2)]
        tmps["zs"] = [es.enter_context(nc.sbuf_tensor("zs%d" % i, [128, T], BF16)) for i in range(2)]
        tmps["mean"] = tmps["a"][0]
        tmps["rstd"] = tmps["a"][1]
        ps = es.enter_context(nc.psum_tensor("ps", [128, 4096], F32))
        ws = WeightStream(S, nc, es, 4, 16 * 256)
        plan = plan_ffn_weights(ws, w_up, w_down)

        ch_in = S.new_chan(total=True)
        ch_p = S.new_chan(total=True)
        ch_out = [S.new_chan() for _ in range(4)]
        S.dma("sp", cwb[:, :, :], cwb_d.rearrange("p (c j) -> p c j", j=4), ch_p)
        S.dma("sp", gb[:, :, :], gb_d.rearrange("p (c j) -> p c j", j=2), ch_p)
        S.memset("dve", ones[:, :], 1.0)
        S.memset("dve", eps[:, :], LN_EPS)
        for c in range(NCH):
            S.dma("sp", xf[:, c, :], xT[c * 128:(c + 1) * 128, :], ch_in)
        for c in range(NCH):
            S.act(xb[:, c, :], xf[:, c, :], AF.Identity)
        emit_ffn(S, ws, plan, xf, xb, cwb, gq, tmps, ps)
        emit_ln(S, xf, 2, T, gb, ones, tmps, ps, [lambda c: xf[:, c, 2:T + 2]])
        for c in range(NCH):
            S.dma("sp", yT[c * 128:(c + 1) * 128, :], xf[:, c, 2:T + 2], ch_out[c % 4])
        S.emit(nc, es, final_chans=ch_out)
    return nc, S


def _pm(v, nch):
    return np.ascontiguousarray(np.asarray(v, np.float32).reshape(nch, 128).T)


def prep_ffn_params(l, ffn_conv_w, ffn_conv_b, ln2_g, ln2_b):
    cw = np.asarray(ffn_conv_w[l], np.float32)
    cb = np.asarray(ffn_conv_b[l], np.float32)
    cwb = np.stack([_pm(cw[0], 88), _pm(cw[1], 88), _pm(cw[2], 88), _pm(cb, 88)], axis=-1)
    gb = np.stack([_pm(ln2_g[l], 16), _pm(ln2_b[l], 16)], axis=-1)
    return np.ascontiguousarray(cwb.reshape(128, 88 * 4)), np.ascontiguousarray(gb.reshape(128, 32))


def run_ffn_launch(x1, l, ffn_w_up, ffn_conv_w, ffn_conv_b, ffn_w_down, ln2_g, ln2_b):
    nc, S = build_ffn_launch()
    cwb, gb = prep_ffn_params(l, ffn_conv_w, ffn_conv_b, ln2_g, ln2_b)
    wu = np.ascontiguousarray(ffn_w_up[l], np.float32)
    wd = np.ascontiguousarray(ffn_w_down[l], np.float32)
    x1 = np.asarray(x1, np.float32)
    in_maps = []
    for c in range(NCORES):
        b, s = divmod(c, 4)
        t0 = s * T
        xt = np.zeros((D, T + 2), np.float32)
        xt[:, 2:] = x1[b, t0:t0 + T].T
        if s > 0:
            xt[:, 0:2] = x1[b, t0 - 2:t0].T
        in_maps.append({"xT": xt, "w_up": wu, "w_down": wd, "cwb": cwb, "ln2gb": gb})
    res = run_bass_kernel_spmd(nc, in_maps, core_ids=list(range(NCORES)))
    out = np.zeros((2, 4096, D), np.float32)
    for c in range(NCORES):
        b, s = divmod(c, 4)
        out[b, s * T:(s + 1) * T] = res.results[c]["yT"].T
    return out


TH = T + 128
B_WINDOWS = (2, 4, 8, 16)
GELU_C = 0.044715
GELU_S = 2.0 * 0.7978845608028654


def emit_gelu(S, dst, src_ps, t1, t2):
    S.act(t1, src_ps, AF.Square)
    S.ts("dve", t1, t1, GELU_C, 1.0, ALU.mult, ALU.add)
    S.tt("dve", t1, t1, src_ps, ALU.mult)
    S.act(t2, t1, AF.Sigmoid, scale=GELU_S)
    S.tt("dve", dst, t2, src_ps, ALU.mult)


def build_mix0_launch():
    nc = bass.Bass("TRN2", target_bir_lowering=False)
    x0T = nc.dram_tensor("x0T", [D, TH], F32, kind="ExternalInput").ap()
    w_in = nc.dram_tensor("w_in", [D, 3072], F32, kind="ExternalInput").ap()
    w_out = nc.dram_tensor("w_out", [D, D], F32, kind="ExternalInput").ap()
    wsT_d = nc.dram_tensor("wsT", [128, 8 * 128], F32, kind="ExternalInput").ap()
    mask_d = nc.dram_tensor("maskA", [128, 128], F32, kind="ExternalInput").ap()
    bsT_d = nc.dram_tensor("bsT", [128, 8 * 128], F32, kind="ExternalInput").ap()
    lnv_d = nc.dram_tensor("lnv", [128, 2 * 1024], F32, kind="ExternalInput").ap()
    wp_d = nc.dram_tensor("w_pool", [4 * 256, 256], F32, kind="ExternalInput").ap()
    psc_d = nc.dram_tensor("pscale", [128, 8], F32, kind="ExternalInput").ap()
    gb_d = nc.dram_tensor("ln1gb", [128, 32], F32, kind="ExternalInput").ap()
    rc_d = nc.dram_tensor("rcnt", [128, 64], F32, kind="ExternalInput").ap()
    flag_d = nc.dram_tensor("flag", [128, 1], F32, kind="ExternalInput").ap()
    x1T = nc.dram_tensor("x1T", [D, T + 2], F32, kind="ExternalOutput").ap()
    S = Sched()
    with ExitStack() as es:
        arena = es.enter_context(nc.sbuf_tensor("arena", [128, NCH * (T + 2)], F32))
        zf = arena[:, :].rearrange("p (c t) -> p c t", c=NCH, t=T + 2)
        x0b = arena[:, 0:NCH * TH // 2].bitcast(BF16).rearrange("p (c t) -> p c t", c=NCH, t=TH)
        u = es.enter_context(nc.sbuf_tensor("u", [128, 8, TH], BF16))
        vt = es.enter_context(nc.sbuf_tensor("vt", [128, 9, 1024], BF16))
        pp = es.enter_context(nc.sbuf_tensor("pp", [128, 8, TH], BF16))
        wsT = es.enter_context(nc.sbuf_tensor("wsT_s", [128, 8, 128], BF16))
        mask = es.enter_context(nc.sbuf_tensor("mask_s", [128, 128], F32))
        bsT = es.enter_context(nc.sbuf_tensor("bsT_s", [128, 8, 128], F32))
        lnv = es.enter_context(nc.sbuf_tensor("lnv_s", [128, 2, 1024], F32))
        wp = es.enter_context(nc.sbuf_tensor("wp_s", [128, 8, 256], BF16))
        psc = es.enter_context(nc.sbuf_tensor("psc_s", [128, 8], F32))
        gb = es.enter_context(nc.sbuf_tensor("gb_s", [128, NCH, 2], F32))
        rc = es.enter_context(nc.sbuf_tensor("rc_s", [128, 4, 16], F32))
        flag = es.enter_context(nc.sbuf_tensor("flag_s", [128, 1], F32))
        ones = es.enter_context(nc.sbuf_tensor("ones", [128, 128], BF16))
        eps = es.enter_context(nc.sbuf_tensor("eps", [128, 1], F32))
        xbf = es.enter_context(nc.sbuf_tensor("xbf", [128, 16 + TH], F32))
        g1 = [es.enter_context(nc.sbuf_tensor("g1_%d" % i, [128, TH], F32)) for i in range(2)]
        g2p = [es.enter_context(nc.sbuf_tensor("g2_%d" % i, [128, 16 + TH], F32)) for i in range(2)]
        g2 = [t[:, 16:16 + TH] for t in g2p]
        tA, tB = g2p
        wsTf = g1[1][:, 0:1024].rearrange("p (h t) -> p h t", h=8)
        st = es.enter_context(nc.sbuf_tensor("st", [128, 8], F32))
        small = es.enter_context(nc.sbuf_tensor("small", [128, 32], F32))
        tmps = {"eps": eps,
                "zb": [g2p[i][:, 16:16 + 513].bitcast(BF16) for i in range(2)],
                "zs": [xbf[:, 16:16 + 513].bitcast(BF16), xbf[:, 600:600 + 513].bitcast(BF16)],
                "mean": g1[0], "rstd": g1[1]}
        ps = es.enter_context(nc.psum_tensor("ps", [128, 4096], F32))
        ws = WeightStream(S, nc, es, 4, 16 * 256)
        wi = w_in.rearrange("(kc p) n -> p kc n", p=128)
        wo = w_out.rearrange("(kc p) n -> p kc n", p=128)
        u_xb = [ws.plan((16, 256), wi[:, :, 2048 + i * 256:2048 + (i + 1) * 256]) for i in range(4)]
        u_u = [ws.plan((16, 256), wi[:, :, i * 256:(i + 1) * 256]) for i in range(4)]
        u_v = [ws.plan((16, 256), wi[:, :, 1024 + i * 256:1024 + (i + 1) * 256]) for i in range(4)]
        u_o = [ws.plan((16, 256), wo[:, :, i * 256:(i + 1) * 256]) for i in range(8)]

        chp = S.new_chan(total=True)
        chx = S.new_chan(total=True)
        S.dma("sp", wsTf, wsT_d.rearrange("p (h t) -> p h t", h=8), chp)
        S.dma("sp", mask[:, :], mask_d, chp)
        S.dma("sp", bsT[:, :, :], bsT_d.rearrange("p (h t) -> p h t", h=8), chp)
        S.dma("sp", lnv[:, :, :], lnv_d.rearrange("p (a c) -> p a c", a=2), chp)
        S.dma("sp", psc[:, :], psc_d, chp)
        S.dma("sp", gb[:, :, :], gb_d.rearrange("p (c j) -> p c j", j=2), chp)
        S.dma("sp", rc[:, :, :], rc_d.rearrange("p (g j) -> p g j", g=4), chp)
        S.dma("sp", flag[:, :], flag_d, chp)
        S.dma("pool", wp[:, :, :], wp_d.rearrange("(a p) n -> p a n", p=128), chx)
        for c in range(NCH):
            S.dma("pool", x0b[:, c, :], x0T[c * 128:(c + 1) * 128, :], chx)
        ws.release(-1)
        S.memset("dve", ones[:, :], 1.0)
        S.memset("dve", eps[:, :], LN_EPS)
        S.memset("dve", xbf[:, 0:16], 0.0)
        S.memset("dve", tA[:, 0:16], 0.0)
        S.memset("dve", tB[:, 0:16], 0.0)
        for h in range(8):
            S.tt("dve", wsT[:, h, :], wsTf[:, h, :], mask[:, :], ALU.mult)

        GR = (0, 1536)
        TT3 = ((0, 512), (512, 512), (1024, 128))

        def proj_fm(wt, sub, g0):
            for kc in range(NCH):
                for (t0, w) in TT3:
                    S.mm(ps[:, g0 + t0:g0 + t0 + w], wt[:, kc, sub * 128:(sub + 1) * 128], x0b[:, kc, t0:t0 + w],
                         start=(kc == 0), stop=(kc == NCH - 1))

        gi = 0
        for i in range(4):
            wt = ws.get(u_xb[i])
            for sub in range(2):
                c = i * 2 + sub
                g = c // 2
                g0 = GR[gi % 2]
                gi += 1
                proj_fm(wt, sub, g0)
                S.act(xbf[:, 16:16 + TH], ps[:, g0:g0 + TH], AF.Identity)
                src = xbf
                dsts = [tA, tB]
                for k in range(g + 1):
                    sh = 1 << k
                    dst = dsts[k % 2]
                    S.tt("dve", dst[:, 16:16 + TH], src[:, 16:16 + TH], src[:, 16 - sh:16 + TH - sh], ALU.add)
                    src = dst
                win = B_WINDOWS[g]
                S.stt(pp[:, c, :], src[:, 16:16 + TH], 1.0 / win, xbf[:, 16:16 + TH], ALU.mult, ALU.subtract)
                S.tt("dve", small[:, 0:16], src[:, 16 + 128:16 + 144], rc[:, g, :], ALU.mult)
                S.tt("dve", pp[:, c, 128:144], small[:, 0:16], xbf[:, 16 + 128:16 + 144], ALU.subtract)
            ws.release(u_xb[i])
        for i in range(4):
            wt = ws.get(u_u[i])
            for sub in range(2):
                c = i * 2 + sub
                g0 = GR[gi % 2]
                proj_fm(wt, sub, g0)
                emit_gelu(S, u[:, c, :], ps[:, g0:g0 + TH], g1[gi % 2][:, :], g2[gi % 2])
                gi += 1
            ws.release(u_u[i])
        wv = [ws.get(k) for k in u_v]
        for tk in range(9):
            vr = g1[tk % 2]
            for cg in range(4):
                r0 = 3072 + ((tk * 4 + cg) % 2) * 512
                for kc in range(NCH):
                    S.mm(ps[:, r0:r0 + 256], x0b[:, kc, tk * 128:(tk + 1) * 128], wv[cg][:, kc, :],
                         start=(kc == 0), stop=(kc == NCH - 1))
                emit_gelu(S, vr[:, cg * 256:(cg + 1) * 256], ps[:, r0:r0 + 256],
                          g2[0][:, cg * 256:(cg + 1) * 256], g2[1][:, cg * 256:(cg + 1) * 256])
            sq = g2[0]
            S.add("dve", lambda e, vr=vr: e.reduce_sum(out=st[:, 0:1], in_=vr[:, 0:1024], axis=mybir.AxisListType.X),
                  reads=[vr[:, 0:1024]], writes=[st[:, 0:1]])
            S.act(sq[:, 0:1024], vr[:, 0:1024], AF.Square)
            S.add("dve", lambda e, sq=sq: e.reduce_sum(out=st[:, 1:2], in_=sq[:, 0:1024], axis=mybir.AxisListType.X),
                  reads=[sq[:, 0:1024]], writes=[st[:, 1:2]])
            S.ts("dve", st[:, 2:3], st[:, 0:1], 1.0 / 1024, None, ALU.mult)
            S.tt("dve", st[:, 3:4], st[:, 2:3], st[:, 2:3], ALU.mult)
            S.stt(st[:, 4:5], st[:, 1:2], 1.0 / 1024, st[:, 3:4], ALU.mult, ALU.subtract)
            S.act(st[:, 5:6], st[:, 4:5], AF.Sqrt, bias=eps[:, 0:1], scale=1.0)
            S.add("dve", lambda e: e.reciprocal(out=st[:, 6:7], in_=st[:, 5:6]), reads=[st[:, 5:6]],
                  writes=[st[:, 6:7]])
            S.ts("dve", vr[:, 0:1024], vr[:, 0:1024], st[:, 2:3], st[:, 6:7], ALU.subtract, ALU.mult)
            S.tt("dve", vr[:, 0:1024], vr[:, 0:1024], lnv[:, 0, :], ALU.mult)
            S.tt("dve", vt[:, tk, :], vr[:, 0:1024], lnv[:, 1, :], ALU.add)
        ws.release(u_v[3])
        for tk in range(9):
            for half in range(2):
                r0 = 2048 + half * 512
                for hh in range(4):
                    h = half * 4 + hh
                    S.mm(ps[:, r0 + hh * 128:r0 + (hh + 1) * 128], vt[:, tk, h * 128:(h + 1) * 128], wsT[:, h, :],
                         start=True, stop=True)
                tmp = g2[half][:, 0:512].rearrange("p (h t) -> p h t", h=4)
                S.tt("dve", tmp, ps[:, r0:r0 + 512].rearrange("p (h t) -> p h t", h=4),
                     bsT[:, half * 4:half * 4 + 4, :], ALU.add)
                uu = u[:, half * 4:half * 4 + 4, tk * 128:(tk + 1) * 128]
                S.tt("dve", uu, tmp, uu, ALU.mult)
        for g in range(4):
            for oc in range(2):
                g0 = GR[oc]
                for kc in range(2):
                    for (t0, w) in TT3:
                        S.mm(ps[:, g0 + t0:g0 + t0 + w], wp[:, g * 2 + kc, oc * 128:(oc + 1) * 128],
                             pp[:, g * 2 + kc, t0:t0 + w], start=(kc == 0), stop=(kc == 1))
            for oc in range(2):
                g0 = GR[oc]
                c = g * 2 + oc
                S.act(pp[:, c, :], ps[:, g0:g0 + TH], AF.Identity, scale=psc[:, c:c + 1])
        xs = [g1[0], g1[1]]
        chs = [S.new_chan(), S.new_chan()]
        for i in range(8):
            wt = ws.get(u_o[i])
            for sub in range(2):
                oc = i * 2 + sub
                g0 = GR[oc % 2]
                xst = xs[oc % 2]
                S.dma("sp", xst[:, 0:T + 2], x0T[oc * 128:(oc + 1) * 128, 126:TH], chs[oc % 2])
                for kc in range(NCH):
                    src = u[:, kc, :] if kc < 8 else pp[:, kc - 8, :]
                    lw = wt[:, kc, sub * 128:(sub + 1) * 128]
                    S.mm(ps[:, g0 + 510:g0 + 512], lw, src[:, 126:128], start=(kc == 0), stop=(kc == NCH - 1))
                    S.mm(ps[:, g0 + 512:g0 + 1024], lw, src[:, 128:640], start=(kc == 0), stop=(kc == NCH - 1))
                    S.mm(ps[:, g0 + 1024:g0 + 1536], lw, src[:, 640:1152], start=(kc == 0), stop=(kc == NCH - 1))
                S.stt(zf[:, oc, :], xst[:, 0:T + 2], ALPHA, ps[:, g0 + 510:g0 + 1536], ALU.mult, ALU.add)
            ws.release(u_o[i])
        emit_ln(S, zf, 0, T + 2, gb, ones, tmps, ps, [lambda c: zf[:, c, :]])
        S.ts("dve", zf[:, :, 0:2], zf[:, :, 0:2], flag[:, 0:1], None, ALU.mult)
        cho = [S.new_chan() for _ in range(4)]
        for c in range(NCH):
            S.dma("sp", x1T[c * 128:(c + 1) * 128, :], zf[:, c, :], cho[c % 4])
        S.emit(nc, es, final_chans=cho)
    return nc, S


def prep_mix0_inputs(x, ev_w_in, ev_ln_v_g, ev_ln_v_b, ev_w_s, ev_b_s, ev_w_pool, ev_pool_scale, ev_w_out,
                     ln1_g, ln1_b):
    x = np.asarray(x, np.float32)
    ws = np.asarray(ev_w_s[0], np.float32)
    wsT = np.ascontiguousarray(ws.transpose(2, 0, 1)).reshape(128, 8 * 128)
    tt_ = np.arange(128)
    maskA = (tt_[None, :] >= tt_[:, None]).astype(np.float32)
    bsT = np.ascontiguousarray(np.broadcast_to(np.asarray(ev_b_s[0], np.float32).reshape(1, 8 * 128), (128, 8 * 128)))
    lnv = np.ascontiguousarray(np.broadcast_to(
        np.concatenate([np.asarray(ev_ln_v_g[0], np.float32), np.asarray(ev_ln_v_b[0], np.float32)])[None, :],
        (128, 2048)))
    wp = np.ascontiguousarray(np.asarray(ev_w_pool[0], np.float32).reshape(4 * 256, 256))
    psc = _pm(ev_pool_scale[0], 8)
    gb = np.ascontiguousarray(np.stack([_pm(ln1_g[0], 16), _pm(ln1_b[0], 16)], axis=-1).reshape(128, 32))
    common = {"w_in": np.ascontiguousarray(ev_w_in[0], np.float32),
              "w_out": np.ascontiguousarray(ev_w_out[0], np.float32),
              "wsT": wsT, "maskA": maskA, "bsT": bsT, "lnv": lnv, "w_pool": wp, "pscale": psc, "ln1gb": gb}
    in_maps = []
    for c in range(NCORES):
        b, s = divmod(c, 4)
        t0 = s * T
        xt = np.zeros((D, TH), np.float32)
        xt[:, 128:] = x[b, t0:t0 + T].T
        if s > 0:
            xt[:, 0:128] = x[b, t0 - 128:t0].T
        rc = np.zeros((4, 16), np.float32)
        for g, win in enumerate(B_WINDOWS):
            pos = np.arange(t0 + 1, t0 + 17, dtype=np.float32)
            rc[g] = 1.0 / np.minimum(pos, float(win))
        rcb = np.ascontiguousarray(np.broadcast_to(rc.reshape(1, 64), (128, 64)))
        m = dict(common)
        m.update({"x0T": xt, "rcnt": rcb, "flag": np.full((128, 1), 1.0 if s > 0 else 0.0, np.float32)})
        in_maps.append(m)
    return in_maps


def run_mix0_launch(inputs):
    nc, S = build_mix0_launch()
    in_maps = prep_mix0_inputs(inputs["x"], inputs["ev_w_in"], inputs["ev_ln_v_g"], inputs["ev_ln_v_b"],
                               inputs["ev_w_s"], inputs["ev_b_s"], inputs["ev_w_pool"], inputs["ev_pool_scale"],
                               inputs["ev_w_out"], inputs["ln1_g"], inputs["ln1_b"])
    res = run_bass_kernel_spmd(nc, in_maps, core_ids=list(range(NCORES)))
    return [res.results[c]["x1T"] for c in range(NCORES)]


NH = 16
CH = 64
NCK = T // CH


def hgrn_consts(S, nc, es, lbp_d, mask2_d, ident_d, chp):
    c = {}
    lbp = es.enter_context(nc.sbuf_tensor("lbp_s", [128, 2, NH], F32))
    c["lb"] = es.enter_context(nc.sbuf_tensor("lb", [128, NH], F32))
    c["oml"] = es.enter_context(nc.sbuf_tensor("oml", [128, NH], F32))
    c["rm"] = es.enter_context(nc.sbuf_tensor("rm", [128, T], F32))
    c["mask2"] = es.enter_context(nc.sbuf_tensor("mask2_s", [128, 128], F32))
    identf = es.enter_context(nc.sbuf_tensor("identf", [128, 128], F32))
    c["ident"] = es.enter_context(nc.sbuf_tensor("ident_s", [128, 128], BF16))
    S.dma("sp", lbp[:, :, :], lbp_d.rearrange("p (l h) -> p l h", l=2), chp)
    S.dma("sp", c["mask2"][:, :], mask2_d, chp)
    S.dma("sp", identf[:, :], ident_d, chp)
    S.copy("dve", c["ident"][:, :], identf[:, :])
    S.tt("dve", c["lb"][:, :], lbp[:, 1, :], lbp[:, 0, :], ALU.subtract)
    S.act(c["lb"][:, :], c["lb"][:, :], AF.Sigmoid)
    S.ts("dve", c["oml"][:, :], c["lb"][:, :], -1.0, 1.0, ALU.mult, ALU.add)
    S.memset("dve", c["rm"][:, :], 1.0)
    S.memset("dve", c["rm"][:, :].rearrange("p (c t) -> p c t", t=CH)[:, :, 0:1], 0.0)
    c["pm"] = es.enter_context(nc.sbuf_tensor("pm", [128, 2], F32))
    S.memset("dve", c["pm"][:, :], 0.0)
    S.memset("dve", c["pm"][0:64, 0:1], 1.0)
    S.memset("dve", c["pm"][64:128, 1:2], 1.0)
    return c


def hgrn_gates(S, h, cst, f_ps, A, B, C, kend_bf, dec, kdec_bf=None, eC=None):
    S.act(A, f_ps, AF.Sigmoid)
    S.ts("dve", A, A, cst["oml"][:, h:h + 1], cst["lb"][:, h:h + 1], ALU.mult, ALU.add)
    S.act(B, A, AF.Ln)
    S.add("dve", lambda e: e.tensor_tensor_scan(out=C, data0=cst["rm"][:, :], data1=B, initial=0.0,
                                                op0=ALU.mult, op1=ALU.add),
          reads=[cst["rm"][:, :], B], writes=[C])
    S.ts("dve", A, A, -1.0, 1.0, ALU.mult, ALU.add)
    S.act(B, C, AF.Exp, scale=-1.0)
    S.tt("dve", B, A, B, ALU.mult)
    if kdec_bf is not None:
        S.act(kdec_bf, B, AF.Identity)
    C3 = C.rearrange("p (c t) -> p c t", t=CH)
    S.act(dec.rearrange("p (c o) -> p c o", o=1), C3[:, :, CH - 1:CH], AF.Exp)
    S.tt("dve", kend_bf.rearrange("p (c t) -> p c t", t=CH), B.rearrange("p (c t) -> p c t", t=CH),
         dec.rearrange("p (c o) -> p c o", o=1).to_broadcast([128, NCK, CH]), ALU.mult)
    if eC is not None:
        S.act(eC, C, AF.Exp)


def hgrn_transposes(S, cst, psb, src_bf, dst_tok, dst_tok1=None):
    for j in range(8):
        S.transpose(psb[:, 4096 + j * 128:4096 + (j + 1) * 128], src_bf[:, j * 128:(j + 1) * 128], cst["ident"][:, :])
    if dst_tok1 is None:
        S.act(dst_tok.rearrange("p j d -> p (j d)"), psb[:, 4096:5120], AF.Identity)
    else:
        S.act(dst_tok.rearrange("p j d -> p (j d)"), psb[:, 4096:5120], AF.Identity, scale=cst["pm"][:, 0:1])
        S.act(dst_tok1.rearrange("p j d -> p (j d)"), psb[:, 4096:5120], AF.Identity, scale=cst["pm"][:, 1:2])


def hgrn_state_scan(S, ps, kt, vtk, dec, Sf, Sb=None):
    cur = 0
    if Sb is not None:
        S.act(Sb[:, 0, :], Sf[0][:, :], AF.Identity)
    for g4 in range(4):
        for cc in range(4):
            c = g4 * 4 + cc
            j, par = divmod(c, 2)
            S.mm(ps[:, 2560 + cc * 128:2560 + (cc + 1) * 128], kt[par][:, j, :], vtk[:, j, :], start=True, stop=True)
        for cc in range(4):
            c = g4 * 4 + cc
            nxt = 1 - cur
            S.stt(Sf[nxt][:, :], Sf[cur][:, :], dec[:, c:c + 1], ps[:, 2560 + cc * 128:2560 + (cc + 1) * 128],
                  ALU.mult, ALU.add)
            cur = nxt
            if Sb is not None and c + 1 < NCK:
                S.act(Sb[:, c + 1, :], Sf[cur][:, :], AF.Identity)
    return cur


def build_hgrn_pre_launch(stage=9, nheads=NH):
    nc = bass.Bass("TRN2", target_bir_lowering=False)
    xT = nc.dram_tensor("xT", [D, T], F32, kind="ExternalInput").ap()
    w_in = nc.dram_tensor("w_in", [D, 4 * D], F32, kind="ExternalInput").ap()
    lbp_d = nc.dram_tensor("lbp", [128, 2 * NH], F32, kind="ExternalInput").ap()
    mask2_d = nc.dram_tensor("mask2", [128, 128], F32, kind="ExternalInput").ap()
    ident_d = nc.dram_tensor("ident", [128, 128], F32, kind="ExternalInput").ap()
    U_d = nc.dram_tensor("U", [NH, 128, 128], F32, kind="ExternalOutput").ap()
    D_d = nc.dram_tensor("Dd", [128, NH], F32, kind="ExternalOutput").ap()
    S = Sched()
    with ExitStack() as es:
        xb = es.enter_context(nc.sbuf_tensor("xb", [128, NCH, T], BF16))
        A = [es.enter_context(nc.sbuf_tensor("A%d" % i, [128, T], F32)) for i in range(2)]
        B = [es.enter_context(nc.sbuf_tensor("B%d" % i, [128, T], F32)) for i in range(2)]
        C = [es.enter_context(nc.sbuf_tensor("C%d" % i, [128, T], F32)) for i in range(2)]
        kend = [es.enter_context(nc.sbuf_tensor("kend%d" % i, [128, T], BF16)) for i in range(2)]
        ibf = [es.enter_context(nc.sbuf_tensor("ibf%d" % i, [128, T], BF16)) for i in range(2)]
        kt = [[es.enter_context(nc.sbuf_tensor("kt%d_%d" % (i, k), [128, 8, 128], BF16)) for k in range(2)]
              for i in range(2)]
        vtk = [es.enter_context(nc.sbuf_tensor("vtk%d" % i, [128, 8, 128], BF16)) for i in range(2)]
        dec = [es.enter_context(nc.sbuf_tensor("dec%d" % i, [128, NCK], F32)) for i in range(2)]
        Sf = [[es.enter_context(nc.sbuf_tensor("Sf%d_%d" % (i, k), [128, 128], F32)) for k in range(2)]
              for i in range(2)]
        Dd = es.enter_context(nc.sbuf_tensor("Dd_s", [128, NH], F32))
        ps = es.enter_context(nc.psum_tensor("ps", [128, 4096], F32))
        psb = ps[:, :].bitcast(BF16)
        chp = S.new_chan(total=True)
        chx = S.new_chan(total=True)
        cst = hgrn_consts(S, nc, es, lbp_d, mask2_d, ident_d, chp)
        ws = WeightStream(S, nc, es, 4, 16 * 256)
        wi = w_in.rearrange("(kc p) n -> p kc n", p=128)
        uses = [ws.plan((16, 256), wi[:, :, h * 512 + 256:h * 512 + 512]) for h in range(NH)]
        for c in range(NCH):
            S.dma("pool", xb[:, c, :], xT[c * 128:(c + 1) * 128, :], chx)
        ws.release(-1)
        cho = [S.new_chan() for _ in range(2)]
        for h in range(nheads):
            b = h % 2
            wt = ws.get(uses[h])
            for which in range(2):
                g0 = which * 1024
                for kc in range(NCH):
                    for t_ in range(2):
                        S.mm(ps[:, g0 + t_ * 512:g0 + (t_ + 1) * 512], wt[:, kc, which * 128:(which + 1) * 128],
                             xb[:, kc, t_ * 512:(t_ + 1) * 512], start=(kc == 0), stop=(kc == NCH - 1))
            ws.release(uses[h])
            hgrn_gates(S, h, cst, ps[:, 0:1024], A[b][:, :], B[b][:, :], C[b][:, :], kend[b][:, :], dec[b][:, :])
            S.act(ibf[b][:, :], ps[:, 1024:2048], AF.Identity)
            C3 = C[b][:, :].rearrange("p (c t) -> p c t", t=CH)
            S.add("dve", lambda e, C3=C3, h=h: e.reduce_sum(out=Dd[:, h:h + 1], in_=C3[:, :, CH - 1:CH],
                                                           axis=mybir.AxisListType.XY),
                  reads=[C[b][:, :]], writes=[Dd[:, h:h + 1]])
            S.act(Dd[:, h:h + 1], Dd[:, h:h + 1], AF.Exp)
            if stage >= 2:
                hgrn_transposes(S, cst, psb, kend[b][:, :], kt[b][0][:, :, :], kt[b][1][:, :, :])
                hgrn_transposes(S, cst, psb, ibf[b][:, :], vtk[b][:, :, :])
            S.memset("dve", Sf[b][0][:, :], 0.0)
            fin = 0
            if stage >= 3:
                fin = hgrn_state_scan(S, ps, kt[b], vtk[b], dec[b], Sf[b])
            S.dma("sp", U_d[h, :, :], Sf[b][fin][:, :], cho[b])
        chd = S.new_chan()
        S.dma("sp", D_d, Dd[:, :], chd)
        S.emit(nc, es, final_chans=cho + [chd])
    return nc, S


def regroup_w_in(od_w_in):
    w = np.asarray(od_w_in, np.float32).reshape(D, 4, NH, 128)
    w = w[:, [0, 3, 1, 2]]
    return np.ascontiguousarray(w.transpose(0, 2, 1, 3).reshape(D, 4 * D))


def hgrn_const_inputs(lb_param):
    lbp = np.ascontiguousarray(np.stack([_pm(lb_param[0], NH), _pm(lb_param[1], NH)], axis=1).reshape(128, 2 * NH))
    i = np.arange(128)
    mask2 = ((i[None, :] >= i[:, None]) & ((i[None, :] // CH) == (i[:, None] // CH))).astype(np.float32)
    return {"lbp": lbp, "mask2": mask2, "ident": np.eye(128, dtype=np.float32)}


def _carve(arena, off_bytes, n_elems, dt):
    assert off_bytes % 4 == 0
    nb = n_elems * _dsize(dt)
    assert nb % 4 == 0
    v = arena[:, off_bytes // 4:(off_bytes + nb) // 4]
    return v if dt == F32 else v.bitcast(dt)


def build_hgrn_main_launch():
    nc = bass.Bass("TRN2", target_bir_lowering=False)
    xT = nc.dram_tensor("xT", [D, T], F32, kind="ExternalInput").ap()
    w_in = nc.dram_tensor("w_in", [D, 4 * D], F32, kind="ExternalInput").ap()
    w_out = nc.dram_tensor("w_out", [D, D], F32, kind="ExternalInput").ap()
    lbp_d = nc.dram_tensor("lbp", [128, 2 * NH], F32, kind="ExternalInput").ap()
    mask2_d = nc.dram_tensor("mask2", [128, 128], F32, kind="ExternalInput").ap()
    ident_d = nc.dram_tensor("ident", [128, 128], F32, kind="ExternalInput").ap()
    up_d = nc.dram_tensor("Uprev", [3, NH, 128, 128], F32, kind="ExternalInput").ap()
    dp_d = nc.dram_tensor("Dprev", [128, 3 * NH], F32, kind="ExternalInput").ap()
    gn_d = nc.dram_tensor("gn", [128, NH], F32, kind="ExternalInput").ap()
    gb_d = nc.dram_tensor("ln1gb", [128, 32], F32, kind="ExternalInput").ap()
    x1T = nc.dram_tensor("x1T", [D, T], F32, kind="ExternalOutput").ap()
    S = Sched()
    with ExitStack() as es:
        arena = es.enter_context(nc.sbuf_tensor("arena", [128, NCH * T], F32))
        zf = arena[:, :].rearrange("p (c t) -> p c t", c=NCH, t=T)
        xb = _carve(arena, 0, NCH * T, BF16).rearrange("p (c t) -> p c t", c=NCH, t=T)
        off = NCH * T * 2
        A = _carve(arena, off, T, F32); off += 4 * T
        B = _carve(arena, off, T, F32); off += 4 * T
        C = _carve(arena, off, T, F32); off += 4 * T
        kend = _carve(arena, off, T, BF16); off += 2 * T
        kdec = _carve(arena, off, T, BF16); off += 2 * T
        qdec = _carve(arena, off, T, BF16); off += 2 * T
        ibf = _carve(arena, off, T, BF16); off += 2 * T
        sg = _carve(arena, off, T, BF16); off += 2 * T
        osq = _carve(arena, off, T, BF16); off += 2 * T
        attm = _carve(arena, off, T, BF16).rearrange("p (j t) -> p j t", j=8); off += 2 * T
        kt0 = _carve(arena, off, T, BF16).rearrange("p (j t) -> p j t", j=8); off += 2 * T
        kt1 = _carve(arena, off, T, BF16).rearrange("p (j t) -> p j t", j=8); off += 2 * T
        kt = (kt0, kt1)
        vtk = _carve(arena, off, T, BF16).rearrange("p (j t) -> p j t", j=8); off += 2 * T
        assert off <= NCH * T * 4
        y = es.enter_context(nc.sbuf_tensor("y", [128, NH, T], BF16))
        Sb = es.enter_context(nc.sbuf_tensor("Sb", [128, NCK, 128], BF16))
        Sf = [es.enter_context(nc.sbuf_tensor("Sf%d" % k, [128, 128], F32)) for k in range(2)]
        upst = es.enter_context(nc.sbuf_tensor("upst", [128, 3, 128], F32))
        dec = es.enter_context(nc.sbuf_tensor("dec", [128, NCK], F32))
        dp = es.enter_context(nc.sbuf_tensor("dp", [128, 3, NH], F32))
        gn = es.enter_context(nc.sbuf_tensor("gn_s", [128, NH], F32))
        gb = es.enter_context(nc.sbuf_tensor("gb_s", [128, NCH, 2], F32))
        ones = es.enter_context(nc.sbuf_tensor("ones", [128, 128], BF16))
        eps = es.enter_context(nc.sbuf_tensor("eps", [128, 1], F32))
        xs = [es.enter_context(nc.sbuf_tensor("xs%d" % i, [128, T], F32)) for i in range(2)]
        tmps = {"eps": eps,
                "zb": [es.enter_context(nc.sbuf_tensor("zb%d" % i, [128, T], BF16)) for i in range(2)],
                "zs": [es.enter_context(nc.sbuf_tensor("zs%d" % i, [128, T], BF16)) for i in range(2)],
                "mean": xs[0], "rstd": xs[1]}
        ps = es.enter_context(nc.psum_tensor("ps", [128, 4096], F32))
        psb = ps[:, :].bitcast(BF16)
        chp = S.new_chan(total=True)
        chx = S.new_chan(total=True)
        cst = hgrn_consts(S, nc, es, lbp_d, mask2_d, ident_d, chp)
        S.dma("sp", dp[:, :, :], dp_d.rearrange("p (j h) -> p j h", j=3), chp)
        S.dma("sp", gn[:, :], gn_d, chp)
        S.dma("sp", gb[:, :, :], gb_d.rearrange("p (c j) -> p c j", j=2), chp)
        S.memset("dve", ones[:, :], 1.0)
        S.memset("dve", eps[:, :], LN_EPS)
        ws = WeightStream(S, nc, es, 3, 16 * 512)
        wi = w_in.rearrange("(kc p) n -> p kc n", p=128)
        wo = w_out.rearrange("(kc p) n -> p kc n", p=128)
        uses = [ws.plan((16, 512), wi[:, :, h * 512:(h + 1) * 512]) for h in range(NH)]
        u_o = [ws.plan((16, 512), wo[:, :, i * 512:(i + 1) * 512]) for i in range(4)]
        for c in range(NCH):
            S.dma("pool", xb[:, c, :], xT[c * 128:(c + 1) * 128, :], chx)
        ws.release(-1)
        chu = S.new_chan()
        G0, G1 = 0, 1024
        for h in range(NH):
            wt = ws.get(uses[h])

            def proj(blk, g0):
                for kc in range(NCH):
                    for t_ in range(2):
                        S.mm(ps[:, g0 + t_ * 512:g0 + (t_ + 1) * 512], wt[:, kc, blk * 128:(blk + 1) * 128],
                             xb[:, kc, t_ * 512:(t_ + 1) * 512], start=(kc == 0), stop=(kc == NCH - 1))
            S.dma("sp", upst[:, :, :], up_d[:, h, :, :].rearrange("j d e -> d j e"), chu)
            S.memset("dve", Sf[0][:, :], 0.0)
            cur = 0
            for j in range(3):
                S.stt(Sf[1 - cur][:, :], Sf[cur][:, :], dp[:, j, h:h + 1], upst[:, j, :], ALU.mult, ALU.add)
                cur = 1 - cur
            Sfl = [Sf[cur], Sf[1 - cur]]
            proj(2, G0)
            proj(3, G1)
            hgrn_gates(S, h, cst, ps[:, G0:G0 + T], A, B, C, kend, dec[:, :], kdec_bf=kdec, eC=A)
            S.act(ibf, ps[:, G1:G1 + T], AF.Identity)
            proj(0, G0)
            proj(1, G1)
            ws.release(uses[h])
            S.act(B, ps[:, G0:G0 + T], AF.Silu)
            S.tt("dve", qdec, B, A, ALU.mult)
            S.act(sg, ps[:, G1:G1 + T], AF.Sigmoid)
            hgrn_transposes(S, cst, psb, kend, kt0, kt1)
            hgrn_transposes(S, cst, psb, ibf, vtk)
            hgrn_state_scan(S, ps, kt, vtk, dec, Sfl, Sb=Sb)
            for half in range(2):
                for jj in range(4):
                    j = half * 4 + jj
                    S.mm(ps[:, 2560 + jj * 128:2560 + (jj + 1) * 128], kdec[:, j * 128:(j + 1) * 128],
                         qdec[:, j * 128:(j + 1) * 128], start=True, stop=True)
                for jj in range(4):
                    j = half * 4 + jj
                    S.tt("dve", attm[:, j, :], ps[:, 2560 + jj * 128:2560 + (jj + 1) * 128], cst["mask2"][:, :],
                         ALU.mult)
            O0 = 3072
            for j in range(8):
                S.mm(ps[:, O0 + j * 128:O0 + (j + 1) * 128], vtk[:, j, :], attm[:, j, :], start=True, stop=False)
                S.mm(ps[:, O0 + j * 128:O0 + j * 128 + 64], Sb[:, 2 * j, :], qdec[:, j * 128:j * 128 + 64],
                     start=False, stop=False)
                S.mm(ps[:, O0 + j * 128 + 64:O0 + (j + 1) * 128], Sb[:, 2 * j + 1, :],
                     qdec[:, j * 128 + 64:(j + 1) * 128], start=False, stop=True)
            S.act(osq, ps[:, O0:O0 + T], AF.Square)
            for t_ in range(2):
                S.mm(ps[:, G0 + t_ * 512:G0 + (t_ + 1) * 512], ones[:, :], osq[:, t_ * 512:(t_ + 1) * 512],
                     start=True, stop=True)
            S.act(A, ps[:, G0:G0 + T], AF.Ln, bias=eps[:, 0:1], scale=1.0 / 128)
            S.act(A, A, AF.Exp, scale=-0.5)
            S.stt(C, ps[:, O0:O0 + T], gn[:, h:h + 1], A, ALU.mult, ALU.mult)
            S.tt("dve", y[:, h, :], C, sg, ALU.mult)
        chs = [S.new_chan(), S.new_chan()]
        for i in range(4):
            wt = ws.get(u_o[i])
            for sub in range(4):
                oc = i * 4 + sub
                g0 = (oc % 2) * 1024
                xst = xs[oc % 2]
                S.dma("sp", xst[:, :], xT[oc * 128:(oc + 1) * 128, :], chs[oc % 2])
                for kc in range(NCH):
                    for t_ in range(2):
                        S.mm(ps[:, g0 + t_ * 512:g0 + (t_ + 1) * 512], wt[:, kc, sub * 128:(sub + 1) * 128],
                             y[:, kc, t_ * 512:(t_ + 1) * 512], start=(kc == 0), stop=(kc == NCH - 1))
                S.stt(zf[:, oc, :], xst[:, :], ALPHA, ps[:, g0:g0 + T], ALU.mult, ALU.add)
            ws.release(u_o[i])
        emit_ln(S, zf, 0, T, gb, ones, tmps, ps, [lambda c: zf[:, c, :]])
        cho = [S.new_chan() for _ in range(4)]
        for c in range(NCH):
            S.dma("sp", x1T[c * 128:(c + 1) * 128, :], zf[:, c, :], cho[c % 4])
        S.emit(nc, es, final_chans=cho)
    return nc, S


def _run(nc, in_maps):
    return run_bass_kernel_spmd(nc, in_maps, core_ids=list(range(NCORES))).results


def kernel_unfused(x, ev_w_in, ev_ln_v_g, ev_ln_v_b, ev_w_s, ev_b_s, ev_w_pool, ev_pool_scale,
           ev_w_out, od_w_in, od_norm_g, od_w_out, lb_param, ffn_w_up, ffn_conv_w,
           ffn_conv_b, ffn_w_down, ln1_g, ln1_b, ln2_g, ln2_b):
    f32 = np.float32
    nc0, _ = build_mix0_launch()
    maps0 = prep_mix0_inputs(x, ev_w_in, ev_ln_v_g, ev_ln_v_b, ev_w_s, ev_b_s, ev_w_pool, ev_pool_scale,
                             ev_w_out, ln1_g, ln1_b)
    r0 = _run(nc0, maps0)
    x1T = [r0[c]["x1T"] for c in range(NCORES)]
    ncf, _ = build_ffn_launch()

    def ffn(l, xTs):
        cwb, gb = prep_ffn_params(l, ffn_conv_w, ffn_conv_b, ln2_g, ln2_b)
        wu = np.ascontiguousarray(ffn_w_up[l], f32)
        wd = np.ascontiguousarray(ffn_w_down[l], f32)
        maps = [{"xT": np.ascontiguousarray(xTs[c], f32), "w_up": wu, "w_down": wd, "cwb": cwb, "ln2gb": gb}
                for c in range(NCORES)]
        r = _run(ncf, maps)
        return [r[c]["yT"] for c in range(NCORES)]

    x2T = ffn(0, x1T)
    ncp, _ = build_hgrn_pre_launch()
    w_in_r = regroup_w_in(od_w_in[0])
    hc = hgrn_const_inputs(lb_param)
    mapsp = []
    for c in range(NCORES):
        m = {"xT": np.ascontiguousarray(x2T[c], f32), "w_in": w_in_r}
        m.update(hc)
        mapsp.append(m)
    rp = _run(ncp, mapsp)
    ncm, _ = build_hgrn_main_launch()
    gn = _pm(od_norm_g[0], NH)
    gb1 = np.ascontiguousarray(np.stack([_pm(ln1_g[1], 16), _pm(ln1_b[1], 16)], axis=-1).reshape(128, 32))
    w_out1 = np.ascontiguousarray(od_w_out[0], f32)
    mapsm = []
    for c in range(NCORES):
        b, s = divmod(c, 4)
        up = np.zeros((3, NH, 128, 128), f32)
        dp = np.zeros((128, 3, NH), f32)
        for j in range(s):
            pos = 3 - s + j
            up[pos] = rp[b * 4 + j]["U"]
            dp[:, pos, :] = rp[b * 4 + j]["Dd"]
        m = {"xT": np.ascontiguousarray(x2T[c], f32), "w_in": w_in_r, "w_out": w_out1, "Uprev": up,
             "Dprev": np.ascontiguousarray(dp.reshape(128, 3 * NH)), "gn": gn, "ln1gb": gb1}
        m.update(hc)
        mapsm.append(m)
    rm = _run(ncm, mapsm)
    x1bT = []
    for c in range(NCORES):
        b, s = divmod(c, 4)
        xt = np.zeros((D, T + 2), f32)
        xt[:, 2:] = rm[c]["x1T"]
        if s > 0:
            xt[:, 0:2] = rm[c - 1]["x1T"][:, T - 2:T]
        x1bT.append(xt)
    outT = ffn(1, x1bT)
    out = np.zeros((2, 4 * T, D), f32)
    for c in range(NCORES):
        b, s = divmod(c, 4)
        out[b, s * T:(s + 1) * T] = outT[c].T
    return out


R0 = 0
R0_SZ = NCH * (T + 2) * 4
R1 = R0 + R0_SZ
R1_SZ = NCH * (T + 2) * 2
R2 = R1 + R1_SZ
R2_SZ = 66560
AR_BYTES = R2 + R2_SZ
SEQ_GROUPS = [[0, 1, 2, 3], [4, 5, 6, 7]]


def build_fused(use_cc=True):
    nc = bass.Bass("TRN2", target_bir_lowering=False)

    def din(name, shape):
        return nc.dram_tensor(name, shape, F32, kind="ExternalInput").ap()
    x0T = din("x0T", [D, TH])
    ev_w_in = din("ev_w_in", [D, 3072])
    ev_w_out = din("ev_w_out", [D, D])
    wsT_d = din("wsT", [128, 8 * 128])
    mask_d = din("maskA", [128, 128])
    bsT_d = din("bsT", [128, 8 * 128])
    lnv_d = din("lnv", [128, 2 * 1024])
    wp_d = din("w_pool", [4 * 256, 256])
    psc_d = din("pscale", [128, 8])
    rc_d = din("rcnt", [128, 64])
    flag_d = din("flag", [128, 1])
    oh_d = din("oh", [128, 8])
    ln1gb_d = din("ln1gb", [128, 64])
    ln2gb_d = din("ln2gb", [128, 64])
    cwb_d = din("cwb", [128, 2 * 88 * 4])
    w_up = [din("w_up%d" % l, [D, 2 * DFF]) for l in range(2)]
    w_down = [din("w_down%d" % l, [DFF, D]) for l in range(2)]
    od_w_in = din("od_w_in", [D, 4 * D])
    od_w_out = din("od_w_out", [D, D])
    lbp_d = din("lbp", [128, 2 * NH])
    mask2_d = din("mask2", [128, 128])
    ident_d = din("ident", [128, 128])
    gn_d = din("gn", [128, NH])
    outT = nc.dram_tensor("outT", [D, T], F32, kind="ExternalOutput").ap()
    xsp = nc.dram_tensor("xsp", [D, T], F32).ap()
    ccin = [nc.dram_tensor("ccin%d" % g, [4 * 4 * 128, 129], F32) for g in range(4)]
    ccout = [nc.dram_tensor("ccout%d" % g, [4 * 4 * 128, 129], F32) for g in range(4)]
    cch_in = nc.dram_tensor("cch_in", [4 * 128, 32], F32)
    cch_out = nc.dram_tensor("cch_out", [4 * 128, 32], F32)

    S = Sched()
    with ExitStack() as es:
        AR = es.enter_context(nc.sbuf_tensor("AR", [128, AR_BYTES // 4], F32))

        def cv(off, shape, dt):
            n = 1
            for k in shape:
                n *= k
            v = _carve(AR, off, n, dt)
            if len(shape) == 2:
                v = v.rearrange("p (a b) -> p a b", a=shape[0], b=shape[1])
            return v

        def sb(name, shape, dt=F32):
            return es.enter_context(nc.sbuf_tensor(name, shape, dt))
        psc = sb("psc_s", [128, 8]); rc = sb("rc_s", [128, 4, 16]); flag = sb("flag_s", [128, 1])
        oh = sb("oh_s", [128, 8]); ln1gb = sb("ln1gb_s", [128, 2, NCH, 2]); ln2gb = sb("ln2gb_s", [128, 2, NCH, 2])
        cwb = sb("cwb_s", [128, 2, 88, 4]); ones = sb("ones", [128, 128], BF16); eps = sb("eps", [128, 1])
        st = sb("st", [128, 8]); small = sb("small", [128, 32]); gn = sb("gn_s", [128, NH])
        Dd = sb("Dd_s", [128, NH]); tiny = sb("tiny", [128, 8])
        hstg = sb("hstg", [128, 4, 32]); hld = sb("hld", [128, 4, 32]); hsum = sb("hsum", [128, 32])
        ps = es.enter_context(nc.psum_tensor("ps", [128, 4096], F32))
        psb = ps[:, :].bitcast(BF16)
        ccsem = [es.enter_context(nc.semaphore("ccs%d" % i)) for i in range(5)]
        ws = WeightStream(S, nc, es, 4, 16 * 256)

        wi0 = ev_w_in.rearrange("(kc p) n -> p kc n", p=128)
        wo0 = ev_w_out.rearrange("(kc p) n -> p kc n", p=128)
        u_xb = [ws.plan((16, 256), wi0[:, :, 2048 + i * 256:2048 + (i + 1) * 256]) for i in range(4)]
        u_u = [ws.plan((16, 256), wi0[:, :, i * 256:(i + 1) * 256]) for i in range(4)]
        u_v = [ws.plan((16, 256), wi0[:, :, 1024 + i * 256:1024 + (i + 1) * 256]) for i in range(4)]
        u_o = [ws.plan((16, 256), wo0[:, :, i * 256:(i + 1) * 256]) for i in range(8)]
        plan0 = plan_ffn_weights(ws, w_up[0], w_down[0])
        wi1 = od_w_in.rearrange("(kc p) n -> p kc n", p=128)
        wo1 = od_w_out.rearrange("(kc p) n -> p kc n", p=128)
        u_pre = [ws.plan((16, 256), wi1[:, :, h * 512 + 256:h * 512 + 512]) for h in range(NH)]
        u_main = []
        for h in range(NH):
            fi = ws.plan((16, 256), wi1[:, :, h * 512 + 256:h * 512 + 512])
            qg = ws.plan((16, 256), wi1[:, :, h * 512:h * 512 + 256])
            u_main.append((fi, qg))
        u_o1 = [ws.plan((16, 256), wo1[:, :, i * 256:(i + 1) * 256]) for i in range(8)]
        plan1 = plan_ffn_weights(ws, w_up[1], w_down[1])

        x0b = cv(R0, (NCH, TH), BF16)
        zf = cv(R0, (NCH, T + 2), F32)
        xb = cv(R1, (NCH, T + 2), BF16)
        pp = cv(R1, (8, TH), BF16)
        g1 = [cv(R1 + 18432, (TH,), F32), cv(R1 + 23040, (TH,), F32)]
        xbf = cv(R1 + 27648, (16 + TH,), F32)
        u = cv(R2, (8, TH), BF16)
        vt = cv(R2 + 18432, (9, 1024), BF16)
        g2p = [cv(R2 + 36864, (16 + TH,), F32), cv(R2 + 41536, (16 + TH,), F32)]
        g2 = [t[:, 16:16 + TH] for t in g2p]
        tA, tB = g2p
        wsT = cv(R2 + 46208, (8, 128), BF16)
        mask = cv(R2 + 48256, (128,), F32)
        bsT = cv(R2 + 48768, (8, 128), F32)
        lnv = cv(R2 + 52864, (2, 1024), F32)
        wp = cv(R2 + 61056, (8, 256), BF16)
        wsTf = g1[1][:, 0:1024].rearrange("p (h t) -> p h t", h=8)
        hb = [cv(R0 + 36864 + k * 4608, (TH,), F32) for k in range(2)]

        chp = S.new_chan(total=True)
        chx = S.new_chan(total=True)
        S.dma("sp", wsTf, wsT_d.rearrange("p (h t) -> p h t", h=8), chp)
        S.dma("sp", mask, mask_d, chp)
        S.dma("sp", bsT, bsT_d.rearrange("p (h t) -> p h t", h=8), chp)
        S.dma("sp", lnv, lnv_d.rearrange("p (a c) -> p a c", a=2), chp)
        S.dma("sp", psc[:, :], psc_d, chp)
        S.dma("sp", rc[:, :, :], rc_d.rearrange("p (g j) -> p g j", g=4), chp)
        S.dma("sp", flag[:, :], flag_d, chp)
        S.dma("sp", oh[:, :], oh_d, chp)
        S.dma("sp", ln1gb[:, :, :, :], ln1gb_d.rearrange("p (l c j) -> p l c j", l=2, j=2), chp)
        S.dma("sp", ln2gb[:, :, :, :], ln2gb_d.rearrange("p (l c j) -> p l c j", l=2, j=2), chp)
        S.dma("sp", cwb[:, :, :, :], cwb_d.rearrange("p (l c j) -> p l c j", l=2, j=4), chp)
        S.dma("sp", gn[:, :], gn_d, chp)
        S.dma("pool", wp, wp_d.rearrange("(a p) n -> p a n", p=128), chx)
        for c in range(NCH):
            S.dma("pool", x0b[:, c, :], x0T[c * 128:(c + 1) * 128, :], chx)
        ws.release(-1)
        S.memset("dve", ones[:, :], 1.0)
        S.memset("dve", eps[:, :], LN_EPS)
        S.memset("dve", xbf[:, 0:16], 0.0)
        S.memset("dve", tA[:, 0:16], 0.0)
        S.memset("dve", tB[:, 0:16], 0.0)
        for h in range(8):
            S.tt("dve", wsT[:, h, :], wsTf[:, h, :], mask, ALU.mult)

        GR = (0, 1536)
        TT3 = ((0, 512), (512, 512), (1024, 128))

        def proj_fm(wt, sub, g0):
            for kc in range(NCH):
                for (t0, w) in TT3:
                    S.mm(ps[:, g0 + t0:g0 + t0 + w], wt[:, kc, sub * 128:(sub + 1) * 128], x0b[:, kc, t0:t0 + w],
                         start=(kc == 0), stop=(kc == NCH - 1))
        gi = 0
        for i in range(4):
            wt = ws.get(u_xb[i])
            for sub in range(2):
                c = i * 2 + sub
                g = c // 2
                g0 = GR[gi % 2]
                gi += 1
                proj_fm(wt, sub, g0)
                S.act(xbf[:, 16:16 + TH], ps[:, g0:g0 + TH], AF.Identity)
                src = xbf
                dsts = [tA, tB]
                for k in range(g + 1):
                    sh = 1 << k
                    dst = dsts[k % 2]
                    S.tt("dve", dst[:, 16:16 + TH], src[:, 16:16 + TH], src[:, 16 - sh:16 + TH - sh], ALU.add)
                    src = dst
                win = B_WINDOWS[g]
                S.stt(pp[:, c, :], src[:, 16:16 + TH], 1.0 / win, xbf[:, 16:16 + TH], ALU.mult, ALU.subtract)
                S.tt("dve", small[:, 0:16], src[:, 16 + 128:16 + 144], rc[:, g, :], ALU.mult)
                S.tt("dve", pp[:, c, 128:144], small[:, 0:16], xbf[:, 16 + 128:16 + 144], ALU.subtract)
            ws.release(u_xb[i])
        for i in range(4):
            wt = ws.get(u_u[i])
            for sub in range(2):
                c = i * 2 + sub
                g0 = GR[gi % 2]
                proj_fm(wt, sub, g0)
                S.act(hb[gi % 2], ps[:, g0:g0 + TH], AF.Identity)
                emit_gelu(S, u[:, c, :], hb[gi % 2], g1[gi % 2], g2[gi % 2])
                gi += 1
            ws.release(u_u[i])
        wv = [ws.get(k) for k in u_v]
        for tk in range(9):
            vr = g1[tk % 2]
            for cg in range(4):
                r0 = 2048 + ((tk * 4 + cg) % 4) * 512
                for kc in range(NCH):
                    S.mm(ps[:, r0:r0 + 256], x0b[:, kc, tk * 128:(tk + 1) * 128], wv[cg][:, kc, :],
                         start=(kc == 0), stop=(kc == NCH - 1))
                hv = hb[tk % 2][:, cg * 256:(cg + 1) * 256]
                S.act(hv, ps[:, r0:r0 + 256], AF.Identity)
                emit_gelu(S, vr[:, cg * 256:(cg + 1) * 256], hv,
                          g2[0][:, cg * 256:(cg + 1) * 256], g2[1][:, cg * 256:(cg + 1) * 256])
            sq = g2[0]
            S.add("dve", lambda e, vr=vr: e.reduce_sum(out=st[:, 0:1], in_=vr[:, 0:1024], axis=mybir.AxisListType.X),
                  reads=[vr[:, 0:1024]], writes=[st[:, 0:1]])
            S.act(sq[:, 0:1024], vr[:, 0:1024], AF.Square)
            S.add("dve", lambda e, sq=sq: e.reduce_sum(out=st[:, 1:2], in_=sq[:, 0:1024], axis=mybir.AxisListType.X),
                  reads=[sq[:, 0:1024]], writes=[st[:, 1:2]])
            S.ts("dve", st[:, 2:3], st[:, 0:1], 1.0 / 1024, None, ALU.mult)
            S.tt("dve", st[:, 3:4], st[:, 2:3], st[:, 2:3], ALU.mult)
            S.stt(st[:, 4:5], st[:, 1:2], 1.0 / 1024, st[:, 3:4], ALU.mult, ALU.subtract)
            S.act(st[:, 5:6], st[:, 4:5], AF.Sqrt, bias=eps[:, 0:1], scale=1.0)
            S.add("dve", lambda e: e.reciprocal(out=st[:, 6:7], in_=st[:, 5:6]), reads=[st[:, 5:6]],
                  writes=[st[:, 6:7]])
            S.ts("dve", vr[:, 0:1024], vr[:, 0:1024], st[:, 2:3], st[:, 6:7], ALU.subtract, ALU.mult)
            S.tt("dve", vr[:, 0:1024], vr[:, 0:1024], lnv[:, 0, :], ALU.mult)
            S.tt("dve", vt[:, tk, :], vr[:, 0:1024], lnv[:, 1, :], ALU.add)
        ws.release(u_v[3])
        for tk in range(9):
            for half in range(2):
                r0 = 2048 + half * 512
                for hh in range(4):
                    h = half * 4 + hh
                    S.mm(ps[:, r0 + hh * 128:r0 + (hh + 1) * 128], vt[:, tk, h * 128:(h + 1) * 128], wsT[:, h, :],
                         start=True, stop=True)
                tmp = g2[half][:, 0:512].rearrange("p (h t) -> p h t", h=4)
                S.tt("dve", tmp, ps[:, r0:r0 + 512].rearrange("p (h t) -> p h t", h=4),
                     bsT[:, half * 4:half * 4 + 4, :], ALU.add)
                uu = u[:, half * 4:half * 4 + 4, tk * 128:(tk + 1) * 128]
                S.tt("dve", uu, tmp, uu, ALU.mult)
        for g in range(4):
            for oc in range(2):
                g0 = GR[oc]
                for kc in range(2):
                    for (t0, w) in TT3:
                        S.mm(ps[:, g0 + t0:g0 + t0 + w], wp[:, g * 2 + kc, oc * 128:(oc + 1) * 128],
                             pp[:, g * 2 + kc, t0:t0 + w], start=(kc == 0), stop=(kc == 1))
            for oc in range(2):
                g0 = GR[oc]
                c = g * 2 + oc
                S.act(pp[:, c, :], ps[:, g0:g0 + TH], AF.Identity, scale=psc[:, c:c + 1])
        xs = [g1[0], g1[1]]
        chs = [S.new_chan(), S.new_chan()]
        for i in range(8):
            wt = ws.get(u_o[i])
            for sub in range(2):
                oc = i * 2 + sub
                g0 = GR[oc % 2]
                xst = xs[oc % 2]
                S.dma("sp", xst[:, 0:T + 2], x0T[oc * 128:(oc + 1) * 128, 126:TH], chs[oc % 2])
                for kc in range(NCH):
                    src = u[:, kc, :] if kc < 8 else pp[:, kc - 8, :]
                    lw = wt[:, kc, sub * 128:(sub + 1) * 128]
                    S.mm(ps[:, g0 + 510:g0 + 512], lw, src[:, 126:128], start=(kc == 0), stop=(kc == NCH - 1))
                    S.mm(ps[:, g0 + 512:g0 + 1024], lw, src[:, 128:640], start=(kc == 0), stop=(kc == NCH - 1))
                    S.mm(ps[:, g0 + 1024:g0 + 1536], lw, src[:, 640:1152], start=(kc == 0), stop=(kc == NCH - 1))
                S.stt(zf[:, oc, :], xst[:, 0:T + 2], ALPHA, ps[:, g0 + 510:g0 + 1536], ALU.mult, ALU.add)
            ws.release(u_o[i])
        tm_ln1 = {"eps": eps, "mean": g2[0], "rstd": g2[1],
                  "zb": [cv(R2 + k * 2052, (T + 2,), BF16) for k in range(2)],
                  "zs": [cv(R2 + (2 + k) * 2052, (T + 2,), BF16) for k in range(2)]}
        emit_ln(S, zf, 0, T + 2, ln1gb[:, 0, :, :], ones, tm_ln1, ps, [lambda c: zf[:, c, :]])
        S.ts("dve", zf[:, :, 0:2], zf[:, :, 0:2], flag[:, 0:1], None, ALU.mult)
        for c in range(NCH):
            S.act(xb[:, c, :], zf[:, c, :], AF.Identity)

        gq = cv(R2, (12, T), BF16)
        ft = [cv(R2 + 24576 + k * 4096, (T,), F32) for k in range(6)]
        tm_ffn = {"a": ft[0:2], "v": ft[2:4], "s": ft[4:6], "eps": eps, "mean": ft[0], "rstd": ft[1],
                  "zb": [cv(R2 + 49152 + k * 2048, (T,), BF16) for k in range(2)],
                  "zs": [cv(R2 + 53248 + k * 2048, (T,), BF16) for k in range(2)]}
        xb2 = cv(R1, (NCH, T), BF16)
        emit_ffn(S, ws, plan0, zf, xb, cwb[:, 0, :, :], gq, tm_ffn, ps)
        emit_ln(S, zf, 2, T, ln2gb[:, 0, :, :], ones, tm_ffn, ps,
                [lambda c: xb2[:, c, :], lambda c: zf[:, c, 2:T + 2]])
        chsp = [S.new_chan() for _ in range(NCH)]
        for c in range(NCH):
            S.dma("sp", xsp[c * 128:(c + 1) * 128, :], zf[:, c, 2:T + 2], chsp[c])

        def mkset(k):
            o0 = R0 + k * 32768
            d_ = {"A": cv(o0, (T,), F32), "B": cv(o0 + 4096, (T,), F32), "C": cv(o0 + 8192, (T,), F32),
                  "kend": cv(o0 + 12288, (T,), BF16), "kdec": cv(o0 + 14336, (T,), BF16),
                  "qdec": cv(o0 + 16384, (T,), BF16), "ibf": cv(o0 + 18432, (T,), BF16),
                  "sg": cv(o0 + 20480, (T,), BF16), "osq": cv(o0 + 22528, (T,), BF16),
                  "attm": cv(o0 + 24576, (8, 128), BF16), "kt0": cv(o0 + 26624, (8, 128), BF16),
                  "kt1": cv(o0 + 28672, (8, 128), BF16), "vtk": cv(o0 + 30720, (8, 128), BF16),
                  "Sb": cv(R2 + 32768, (NCK, 128), BF16) if k == 0 else cv(R2 + 61472, (NCK, 128), BF16),
                  "dec": sb("dec%d" % k, [128, NCK]), "Sf": [sb("Sf%d_%d" % (k, i), [128, 128]) for i in range(2)],
                  "Pp": [sb("Pp%d_%d" % (k, i), [128, 128]) for i in range(2)],
                  "upst": cv(R2 + 59408, (4, 129), F32) if k == 0 else sb("upst1", [128, 4, 129]),
                  "stg": cv(R2 + 57344, (4, 129), F32),
                  "chu": S.new_chan(), "chst": S.new_chan()}
            return d_
        sets = [mkset(0), mkset(1)]
        y = cv(R2, (NH, T), BF16)
        xs1 = [cv(R2 + 40960, (T,), F32), cv(R2 + 45056, (T,), F32)]
        tm_ln1b = {"eps": eps, "mean": xs1[0], "rstd": xs1[1],
                   "zb": [cv(R2 + 49152 + k * 2048, (T,), BF16) for k in range(2)],
                   "zs": [cv(R2 + 53248 + k * 2048, (T,), BF16) for k in range(2)]}
        chc = S.new_chan(total=True)
        cst = {}
        lbp = sb("lbp_s", [128, 2, NH]); cst["lb"] = sb("lb", [128, NH]); cst["oml"] = sb("oml", [128, NH])
        cst["mask2"] = sb("mask2_s", [128, 128]); identf = sb("identf", [128, 128]); cst["ident"] = sb("ident_s", [128, 128], BF16)
        cst["pm"] = sb("pm", [128, 2])
        cst["rm"] = cv(R2 + 36864, (T,), F32)
        S.dma("sp", lbp[:, :, :], lbp_d.rearrange("p (l h) -> p l h", l=2), chc)
        S.dma("sp", cst["mask2"][:, :], mask2_d, chc)
        S.dma("sp", identf[:, :], ident_d, chc)
        S.copy("dve", cst["ident"][:, :], identf[:, :])
        S.tt("dve", cst["lb"][:, :], lbp[:, 1, :], lbp[:, 0, :], ALU.subtract)
        S.act(cst["lb"][:, :], cst["lb"][:, :], AF.Sigmoid)
        S.ts("dve", cst["oml"][:, :], cst["lb"][:, :], -1.0, 1.0, ALU.mult, ALU.add)
        S.memset("dve", cst["rm"], 1.0)
        S.memset("dve", cst["rm"].rearrange("p (c t) -> p c t", t=CH)[:, :, 0:1], 0.0)
        S.memset("dve", cst["pm"][:, :], 0.0)
        S.memset("dve", cst["pm"][0:64, 0:1], 1.0)
        S.memset("dve", cst["pm"][64:128, 1:2], 1.0)
        oh3 = oh[:, 0:4].rearrange("p (j o) -> p j o", o=1)
        G0, G1, PB5, O0 = 0, 1024, 2560, 3072

        def proj1(wt, blk, g0):
            for kc in range(NCH):
                for t_ in range(2):
                    S.mm(ps[:, g0 + t_ * 512:g0 + (t_ + 1) * 512], wt[:, kc, blk * 128:(blk + 1) * 128],
                         xb2[:, kc, t_ * 512:(t_ + 1) * 512], start=(kc == 0), stop=(kc == NCH - 1))

        def cc_op(idx, src_t, dst_t):
            if use_cc:
                def fn(e):
                    e.collective_compute("AllReduce", ALU.add, replica_groups=SEQ_GROUPS,
                                         ins=[src_t.ap().opt()], outs=[dst_t.ap().opt()]).then_inc(ccsem[idx])
                    return None
                S.add("pool", fn, reads=[src_t.ap()], writes=[])

                def fn2(e):
                    e.wait_ge(ccsem[idx], 1)
                    return e.memset(tiny[:, idx:idx + 1], 0.0)
                return lambda: S.add("pool", fn2, reads=[], writes=[dst_t.ap(), tiny[:, idx:idx + 1]])
            else:
                chq = S.new_chan()
                S.dma("sp", dst_t.ap(), src_t.ap(), chq)
                return lambda: None

        def scan_group(q, g4, st_):
            pb = (2560, 3072, 3584, 2560)[g4]
            for cc in range(4):
                c = g4 * 4 + cc
                j, par = divmod(c, 2)
                S.mm(ps[:, pb + cc * 128:pb + (cc + 1) * 128], (q["kt0"], q["kt1"])[par][:, j, :], q["vtk"][:, j, :],
                     start=True, stop=True)
            for cc in range(4):
                c = g4 * 4 + cc
                cur = st_["cur"]
                S.stt(q["Sf"][1 - cur][:, :], q["Sf"][cur][:, :], q["dec"][:, c:c + 1],
                      ps[:, pb + cc * 128:pb + (cc + 1) * 128], ALU.mult, ALU.add)
                st_["cur"] = 1 - cur
                if st_["sb"] and c + 1 < NCK:
                    S.act(q["Sb"][:, c + 1, :], q["Sf"][1 - cur][:, :], AF.Identity)

        def interleave(bsteps, asteps, after):
            ai = 0
            for bi, bstep in enumerate(bsteps):
                bstep()
                while ai < len(asteps) and after[ai] == bi:
                    asteps[ai]()
                    ai += 1
            while ai < len(asteps):
                asteps[ai]()
                ai += 1

        cc_done = []

        def pre_A(h):
            q = sets[h % 2]

            def a1():
                q["wt"] = ws.get(u_pre[h])
                proj1(q["wt"], 0, G0)

            def a2():
                proj1(q["wt"], 1, G1)
                ws.release(u_pre[h])
                hgrn_gates(S, h, cst, ps[:, G0:G0 + T], q["A"], q["B"], q["C"], q["kend"], q["dec"][:, :])
                S.act(q["ibf"], ps[:, G1:G1 + T], AF.Identity)
                C3 = q["C"].rearrange("p (c t) -> p c t", t=CH)
                S.add("dve", lambda e, C3=C3, h=h: e.reduce_sum(out=Dd[:, h:h + 1], in_=C3[:, :, CH - 1:CH],
                                                               axis=mybir.AxisListType.XY),
                      reads=[q["C"]], writes=[Dd[:, h:h + 1]])
                S.act(Dd[:, h:h + 1], Dd[:, h:h + 1], AF.Exp)
            return [a1, a2]

        def pre_B(h):
            q = sets[h % 2]
            st_ = {"cur": 0, "sb": False}

            def b1():
                hgrn_transposes(S, cst, psb, q["kend"], q["kt0"], q["kt1"])

            def b1b():
                hgrn_transposes(S, cst, psb, q["ibf"], q["vtk"])
                S.memset("dve", q["Sf"][0][:, :], 0.0)

            def bfin():
                fin = st_["cur"]
                for j in range(4):
                    S.ts("dve", q["stg"][:, j, 0:128], q["Sf"][fin][:, :], oh[:, j:j + 1], None, ALU.mult)
                S.ts("dve", q["stg"][:, :, 128:129], oh3, Dd[:, h:h + 1], None, ALU.mult)
                g, hl = divmod(h, 4)
                S.dma("sp", ccin[g].ap().rearrange("(j l d) n -> d j l n", j=4, l=4)[:, :, hl, :], q["stg"][:, :, :],
                      q["chst"])
                if hl == 3:
                    cc_done.append(cc_op(g, ccin[g], ccout[g]))
            return [b1, b1b] + [lambda g4=g4: scan_group(q, g4, st_) for g4 in range(4)] + [bfin]

        for stp in pre_A(0):
            stp()
        for h in range(NH):
            nxt = pre_A(h + 1) if h + 1 < NH else []
            interleave(pre_B(h), nxt, [0, 3])

        def main_A(h):
            q = sets[h % 2]
            g, hl = divmod(h, 4)
            fi, qg = u_main[h]

            def a1():
                if hl == 0:
                    cc_done[g]()
                up = q["upst"]
                S.dma("sp", up[:, :, :], ccout[g].ap().rearrange("(j l d) n -> d j l n", j=4, l=4)[:, :, hl, :], q["chu"])
                Pp_, Sf_ = q["Pp"], q["Sf"]
                S.stt(Pp_[0][:, :], up[:, 0, 0:128], up[:, 1, 128:129], up[:, 1, 0:128], ALU.mult, ALU.add)
                S.stt(Pp_[1][:, :], Pp_[0][:, :], up[:, 2, 128:129], up[:, 2, 0:128], ALU.mult, ALU.add)
                S.ts("dve", Sf_[0][:, :], up[:, 0, 0:128], oh[:, 1:2], None, ALU.mult)
                S.stt(Sf_[0][:, :], Pp_[0][:, :], oh[:, 2:3], Sf_[0][:, :], ALU.mult, ALU.add)
                S.stt(Sf_[0][:, :], Pp_[1][:, :], oh[:, 3:4], Sf_[0][:, :], ALU.mult, ALU.add)
                q["wt"] = ws.get(fi)
                proj1(q["wt"], 0, G0)

            def a2():
                proj1(q["wt"], 1, G1)
                ws.release(fi)
                hgrn_gates(S, h, cst, ps[:, G0:G0 + T], q["A"], q["B"], q["C"], q["kend"], q["dec"][:, :],
                           kdec_bf=q["kdec"], eC=q["A"])
                S.act(q["ibf"], ps[:, G1:G1 + T], AF.Identity)

            def a3():
                q["wt"] = ws.get(qg)
                proj1(q["wt"], 0, G0)

            def a4():
                proj1(q["wt"], 1, G1)
                ws.release(qg)
                S.act(q["B"], ps[:, G0:G0 + T], AF.Silu)
                S.tt("dve", q["qdec"], q["B"], q["A"], ALU.mult)
                S.act(q["sg"], ps[:, G1:G1 + T], AF.Sigmoid)
            return [a1, a2, a3, a4]

        def main_B(h):
            q = sets[h % 2]
            st_ = {"cur": 0, "sb": True}

            def b1():
                hgrn_transposes(S, cst, psb, q["kend"], q["kt0"], q["kt1"])

            def b1b():
                hgrn_transposes(S, cst, psb, q["ibf"], q["vtk"])
                S.act(q["Sb"][:, 0, :], q["Sf"][0][:, :], AF.Identity)

            def batt(half):
                pb = 3072 + half * 512
                for jj in range(4):
                    j = half * 4 + jj
                    S.mm(ps[:, pb + jj * 128:pb + (jj + 1) * 128], q["kdec"][:, j * 128:(j + 1) * 128],
                         q["qdec"][:, j * 128:(j + 1) * 128], start=True, stop=True)
                for jj in range(4):
                    j = half * 4 + jj
                    S.tt("dve", q["attm"][:, j, :], ps[:, pb + jj * 128:pb + (jj + 1) * 128], cst["mask2"][:, :],
                         ALU.mult)

            def bo():
                for j in range(8):
                    S.mm(ps[:, O0 + j * 128:O0 + (j + 1) * 128], q["vtk"][:, j, :], q["attm"][:, j, :], start=True,
                         stop=False)
                    S.mm(ps[:, O0 + j * 128:O0 + j * 128 + 64], q["Sb"][:, 2 * j, :], q["qdec"][:, j * 128:j * 128 + 64],
                         start=False, stop=False)
                    S.mm(ps[:, O0 + j * 128 + 64:O0 + (j + 1) * 128], q["Sb"][:, 2 * j + 1, :],
                         q["qdec"][:, j * 128 + 64:(j + 1) * 128], start=False, stop=True)
                S.act(q["osq"], ps[:, O0:O0 + T], AF.Square)

            def bnorm():
                for t_ in range(2):
                    sl = slice(t_ * 512, (t_ + 1) * 512)
                    S.mm(ps[:, PB5:PB5 + 512], ones[:, :], q["osq"][:, sl], start=True, stop=True)
                    S.act(q["A"][:, sl], ps[:, PB5:PB5 + 512], AF.Ln, bias=eps[:, 0:1], scale=1.0 / 128)
                S.act(q["A"], q["A"], AF.Exp, scale=-0.5)
                S.stt(q["C"], ps[:, O0:O0 + T], gn[:, h:h + 1], q["A"], ALU.mult, ALU.mult)
                S.tt("dve", y[:, h, :], q["C"], q["sg"], ALU.mult)
            return ([b1, b1b] + [lambda g4=g4: scan_group(q, g4, st_) for g4 in range(4)]
                    + [lambda: batt(0), lambda: batt(1), bo, bnorm])

        for stp in main_A(0):
            stp()
        for h in range(NH):
            nxt = main_A(h + 1) if h + 1 < NH else []
            interleave(main_B(h), nxt, [0, 3, 5, 8])
        chs1 = [S.new_chan(), S.new_chan()]
        for i in range(8):
            wt = ws.get(u_o1[i])
            for sub in range(2):
                oc = i * 2 + sub
                g0 = (oc % 2) * 1024
                xst = xs1[oc % 2]
                S.dma("sp", xst, xsp[oc * 128:(oc + 1) * 128, :], chs1[oc % 2])
                for kc in range(NCH):
                    for t_ in range(2):
                        S.mm(ps[:, g0 + t_ * 512:g0 + (t_ + 1) * 512], wt[:, kc, sub * 128:(sub + 1) * 128],
                             y[:, kc, t_ * 512:(t_ + 1) * 512], start=(kc == 0), stop=(kc == NCH - 1))
                S.stt(zf[:, oc, 2:T + 2], xst, ALPHA, ps[:, g0:g0 + T], ALU.mult, ALU.add)
            ws.release(u_o1[i])
        emit_ln(S, zf, 2, T, ln1gb[:, 1, :, :], ones, tm_ln1b, ps,
                [lambda c: xb[:, c, 2:T + 2], lambda c: zf[:, c, 2:T + 2]])
        for j in range(4):
            S.ts("dve", hstg[:, j, :].rearrange("p (c t) -> p c t", t=2), zf[:, :, T:T + 2], oh[:, j:j + 1], None,
                 ALU.mult)
        chh = S.new_chan()
        S.dma("sp", cch_in.ap().rearrange("(j p) n -> p j n", p=128), hstg[:, :, :], chh)
        done_h = cc_op(4, cch_in, cch_out)
        done_h()
        chh2 = S.new_chan()
        S.dma("sp", hld[:, :, :], cch_out.ap().rearrange("(j p) n -> p j n", p=128), chh2)
        S.ts("dve", hsum[:, :], hld[:, 0, :], oh[:, 4:5], None, ALU.mult)
        for j in range(1, 4):
            S.stt(hsum[:, :], hld[:, j, :], oh[:, 4 + j:5 + j], hsum[:, :], ALU.mult, ALU.add)
        S.act(xb[:, :, 0:2], hsum[:, :].rearrange("p (c t) -> p c t", t=2), AF.Identity)
        emit_ffn(S, ws, plan1, zf, xb, cwb[:, 1, :, :], gq, tm_ffn, ps)
        emit_ln(S, zf, 2, T, ln2gb[:, 1, :, :], ones, tm_ffn, ps, [lambda c: zf[:, c, 2:T + 2]])
        cho = [S.new_chan() for _ in range(4)]
        for c in range(NCH):
            S.dma("sp", outT[c * 128:(c + 1) * 128, :], zf[:, c, 2:T + 2], cho[c % 4])
        S.emit(nc, es, final_chans=cho)
    return nc, S


def fused_inputs(inp):
    f32 = np.float32
    maps = prep_mix0_inputs(inp["x"], inp["ev_w_in"], inp["ev_ln_v_g"], inp["ev_ln_v_b"], inp["ev_w_s"],
                            inp["ev_b_s"], inp["ev_w_pool"], inp["ev_pool_scale"], inp["ev_w_out"],
                            inp["ln1_g"], inp["ln1_b"])
    ln1gb = np.stack([np.stack([_pm(inp["ln1_g"][l], 16), _pm(inp["ln1_b"][l], 16)], axis=-1) for l in range(2)], axis=1)
    ln2gb = np.stack([np.stack([_pm(inp["ln2_g"][l], 16), _pm(inp["ln2_b"][l], 16)], axis=-1) for l in range(2)], axis=1)
    cwbs = []
    for l in range(2):
        cw = np.asarray(inp["ffn_conv_w"][l], f32)
        cb = np.asarray(inp["ffn_conv_b"][l], f32)
        cwbs.append(np.stack([_pm(cw[0], 88), _pm(cw[1], 88), _pm(cw[2], 88), _pm(cb, 88)], axis=-1))
    cwb = np.stack(cwbs, axis=1)
    common = {
        "ln1gb": np.ascontiguousarray(ln1gb.reshape(128, 64)), "ln2gb": np.ascontiguousarray(ln2gb.reshape(128, 64)),
        "cwb": np.ascontiguousarray(cwb.reshape(128, 2 * 88 * 4)),
        "w_up0": np.ascontiguousarray(inp["ffn_w_up"][0], f32), "w_up1": np.ascontiguousarray(inp["ffn_w_up"][1], f32),
        "w_down0": np.ascontiguousarray(inp["ffn_w_down"][0], f32),
        "w_down1": np.ascontiguousarray(inp["ffn_w_down"][1], f32),
        "od_w_in": regroup_w_in(inp["od_w_in"][0]), "od_w_out": np.ascontiguousarray(inp["od_w_out"][0], f32),
        "gn": _pm(inp["od_norm_g"][0], NH)}
    common.update(hgrn_const_inputs(inp["lb_param"]))
    out = []
    for c in range(NCORES):
        b, s = divmod(c, 4)
        m0 = maps[c]
        m = dict(common)
        for k in ("x0T", "wsT", "maskA", "bsT", "lnv", "w_pool", "pscale", "rcnt", "flag"):
            m[k] = m0[k]
        m["ev_w_in"] = m0["w_in"]
        m["ev_w_out"] = m0["w_out"]
        oh = np.zeros((128, 8), f32)
        oh[:, s] = 1.0
        if s > 0:
            oh[:, 4 + s - 1] = 1.0
        m["oh"] = oh
        out.append(m)
    return out


def kernel(**inputs):
    nc, _ = build_fused(use_cc=True)
    maps = fused_inputs(inputs)
    res = run_bass_kernel_spmd(nc, maps, core_ids=list(range(NCORES))).results
    out = np.zeros((2, 4 * T, D), np.float32)
    for c in range(NCORES):
        b, s = divmod(c, 4)
        out[b, s * T:(s + 1) * T] = res[c]["outT"].T
    return out
```

```python
import numpy as np
from contextlib import ExitStack
import concourse.bass as bass
import concourse.mybir as mybir
from concourse.bass_utils import run_bass_kernel_spmd

F32 = mybir.dt.float32
BF16 = mybir.dt.bfloat16
AF = mybir.ActivationFunctionType
ALU = mybir.AluOpType

D = 2048
NCH = 16
T = 1024
NCORES = 8
DFF = 5632
NFF = 44
ALPHA = 4.0 ** 0.25
LN_EPS = 1e-5
ENGS = ("pe", "act", "dve", "pool", "sp")
_DT_SIZE = {F32: 4, BF16: 2}


def _dsize(dt):
    return _DT_SIZE.get(dt, 4)


class _Op:
    __slots__ = ("eng", "idx", "fn", "deps", "chan", "chan_val", "signal", "val")


class Sched:
    def __init__(self):
        self.ops = {e: [] for e in ENGS}
        self.track = {}
        self.chan_cnt = []
        self.chan_total = []

    @staticmethod
    def _rng(ap):
        t = ap.tensor
        name = t.name
        sp = str(ap.space) if hasattr(ap, "space") else ""
        pat = ap.ap
        esz = _dsize(ap.dtype)
        if "DRAM" in sp.upper() or "Dram" in type(t).__name__ or "DRam" in type(t).__name__:
            ext = 1
            for (st, cnt) in pat:
                ext += abs(st) * (cnt - 1)
            return name, ap.offset * esz, (ap.offset + ext) * esz
        pstride = pat[0][0]
        lo = ap.offset % pstride if pstride > 0 else ap.offset
        ext = 1
        for (st, cnt) in pat[1:]:
            ext += abs(st) * (cnt - 1)
        return name, lo * esz, (lo + ext) * esz

    def _touch(self, name, lo, hi, op, is_write, deps):
        segs = self.track.setdefault(name, [])
        new = []
        covered = []
        for s in segs:
            slo, shi, w, rs = s
            if shi <= lo or slo >= hi:
                new.append(s)
                continue
            if slo < lo:
                new.append([slo, lo, w, list(rs)])
            if shi > hi:
                new.append([hi, shi, w, list(rs)])
            olo, ohi = max(slo, lo), min(shi, hi)
            if w is not None:
                deps.add(w)
            if is_write:
                for r in rs:
                    deps.add(r)
            else:
                covered.append([olo, ohi, w, rs + [op]])
        if is_write:
            new.append([lo, hi, op, []])
        else:
            covered.sort(key=lambda s: s[0])
            cur = lo
            for c in covered:
                if c[0] > cur:
                    new.append([cur, c[0], None, [op]])
                new.append(c)
                cur = c[1]
            if cur < hi:
                new.append([cur, hi, None, [op]])
        self.track[name] = new

    def add(self, eng, fn, reads=(), writes=(), chan=None):
        o = _Op()
        o.eng = eng
        o.fn = fn
        o.chan = chan
        o.signal = False
        o.val = None
        o.chan_val = None
        deps = set()
        for ap in reads:
            if ap is None or isinstance(ap, (int, float)):
                continue
            n, lo, hi = self._rng(ap)
            self._touch(n, lo, hi, o, False, deps)
        for ap in writes:
            n, lo, hi = self._rng(ap)
            if eng == "pe":
                lo = (lo // 2048) * 2048
                hi = ((hi + 2047) // 2048) * 2048
            self._touch(n, lo, hi, o, True, deps)
        deps.discard(o)
        o.deps = deps
        if chan is not None:
            self.chan_cnt[chan] += 1
            o.chan_val = 16 * self.chan_cnt[chan]
        o.idx = len(self.ops[eng])
        self.ops[eng].append(o)
        return o

    def new_chan(self, total=False):
        self.chan_cnt.append(0)
        self.chan_total.append(total)
        return len(self.chan_cnt) - 1

    def emit(self, nc, es, final_chans=()):
        for e in ENGS:
            for o in self.ops[e]:
                for d in o.deps:
                    if d.chan is None:
                        d.signal = True
        for e in ENGS:
            c = 0
            for o in self.ops[e]:
                if o.chan is None and o.signal:
                    c += 1
                    o.val = c
        esem = {e: es.enter_context(nc.semaphore("s_" + e)) for e in ENGS}
        csem = [es.enter_context(nc.semaphore("c_%d" % i)) for i in range(len(self.chan_cnt))]
        block = es.enter_context(nc.Block())
        nwaits = {e: 0 for e in ENGS}

        def run(engname, eobj):
            seen = {}
            for o in self.ops[engname]:
                need = {}
                for d in o.deps:
                    if d.chan is not None:
                        key = ("c", d.chan)
                        v = 16 * self.chan_cnt[d.chan] if self.chan_total[d.chan] else d.chan_val
                    else:
                        if d.eng == engname and engname == "pe":
                            continue
                        key = ("e", d.eng)
                        v = d.val
                    if v > need.get(key, 0):
                        need[key] = v
                for key, v in need.items():
                    if v <= seen.get(key, 0):
                        continue
                    seen[key] = v
                    sem = csem[key[1]] if key[0] == "c" else esem[key[1]]
                    eobj.wait_ge(sem, v)
                    nwaits[engname] += 1
                inst = o.fn(eobj)
                if o.chan is not None:
                    inst.then_inc(csem[o.chan], 16)
                elif o.signal:
                    assert inst is not None
                    inst.then_inc(esem[engname], 1)
            if engname == "sp":
                for ch in final_chans:
                    if self.chan_cnt[ch] > 0:
                        eobj.wait_ge(csem[ch], 16 * self.chan_cnt[ch])

        @block.tensor
        def _(e):
            run("pe", e)

        @block.scalar
        def _(e):
            run("act", e)

        @block.vector
        def _(e):
            run("dve", e)

        @block.gpsimd
        def _(e):
            run("pool", e)

        @block.sync
        def _(e):
            run("sp", e)

        self.nwaits = nwaits

    def mm(self, out, lhsT, rhs, start=True, stop=True):
        return self.add("pe", lambda e: e.matmul(out, lhsT=lhsT, rhs=rhs, start=start, stop=stop),
                        reads=[lhsT, rhs], writes=[out])

    def transpose(self, out, in_, ident):
        return self.add("pe", lambda e: e.transpose(out, in_, ident), reads=[in_, ident], writes=[out])

    def act(self, out, in_, func, bias=None, scale=None):
        kw = {}
        rd = [in_]
        if bias is not None:
            kw["bias"] = bias
            rd.append(bias)
        if scale is not None:
            kw["scale"] = scale
            rd.append(scale)
        return self.add("act", lambda e: e.activation(out=out, in_=in_, func=func, **kw), reads=rd, writes=[out])

    def tt(self, eng, out, in0, in1, op):
        return self.add(eng, lambda e: e.tensor_tensor(out=out, in0=in0, in1=in1, op=op),
                        reads=[in0, in1], writes=[out])

    def ts(self, eng, out, in0, s1, s2, op0, op1=None):
        if op1 is None:
            return self.add(eng, lambda e: e.tensor_scalar(out=out, in0=in0, scalar1=s1, scalar2=None, op0=op0),
                            reads=[in0, s1], writes=[out])
        return self.add(eng, lambda e: e.tensor_scalar(out=out, in0=in0, scalar1=s1, scalar2=s2, op0=op0, op1=op1),
                        reads=[in0, s1, s2], writes=[out])

    def stt(self, out, in0, scalar, in1, op0, op1):
        return self.add("dve", lambda e: e.scalar_tensor_tensor(out=out, in0=in0, scalar=scalar, in1=in1,
                                                                op0=op0, op1=op1),
                        reads=[in0, scalar, in1], writes=[out])

    def copy(self, eng, out, in_):
        if eng == "act":
            return self.add("act", lambda e: e.copy(out=out, in_=in_), reads=[in_], writes=[out])
        return self.add(eng, lambda e: e.tensor_copy(out=out, in_=in_), reads=[in_], writes=[out])

    def memset(self, eng, ap, val):
        return self.add(eng, lambda e: e.memset(ap, val), writes=[ap])

    def dma(self, eng, out, in_, chan):
        return self.add(eng, lambda e: e.dma_start(out=out, in_=in_), reads=[in_], writes=[out], chan=chan)


class WeightStream:
    def __init__(self, S, nc, es, nslots, free_elems, name="wslot"):
        self.S = S
        self.slots = [es.enter_context(nc.sbuf_tensor("%s%d" % (name, i), [128, free_elems], BF16))
                      for i in range(nslots)]
        self.chans = [S.new_chan() for _ in range(nslots)]
        self.uses = []
        self.loaded = 0
        self.released = -1
        self.n = nslots

    def plan(self, shape, src):
        self.uses.append((shape, src))
        return len(self.uses) - 1

    def view(self, k):
        shape, _ = self.uses[k]
        sl = self.slots[k % self.n]
        n = 1
        for s in shape:
            n *= s
        v = sl[:, 0:n]
        if len(shape) == 2:
            return v.rearrange("p (a b) -> p a b", a=shape[0], b=shape[1])
        return v

    def _load_upto(self, k):
        while self.loaded < len(self.uses) and self.loaded <= k:
            j = self.loaded
            _, src = self.uses[j]
            self.S.dma("pool", self.view(j), src, self.chans[j % self.n])
            self.loaded += 1

    def get(self, k):
        assert k <= self.released + self.n, (k, self.released)
        self._load_upto(k)
        return self.view(k)

    def release(self, k):
        self.released = max(self.released, k)
        self._load_upto(self.released + self.n)


FF_QUARTERS = (12, 10, 12, 10)


def plan_ffn_weights(ws, w_up, w_down):
    plan = []
    base = 0
    wu = w_up.rearrange("(kc p) n -> p kc n", p=128)
    for q, nq in enumerate(FF_QUARTERS):
        ups = []
        for j in range(0, nq, 2):
            ca = base + j
            ua = ws.plan((16, 256), wu[:, :, ca * 128:ca * 128 + 256])
            uv = ws.plan((16, 256), wu[:, :, (NFF + ca) * 128:(NFF + ca) * 128 + 256])
            ups.append((ca, ua, uv))
        downs = []
        wd = w_down[base * 128:(base + nq) * 128, :].rearrange("(j p) n -> p j n", p=128)
        for op_ in range(8):
            downs.append((op_, ws.plan((nq, 256), wd[:, :, op_ * 256:(op_ + 1) * 256])))
        plan.append((base, nq, ups, downs))
        base += nq
    return plan


def emit_ffn(S, ws, plan, xf, xb, cwb, gq, tmps, ps):
    G = (0, 1536)
    gi = 0
    for (base, nq, ups, downs) in plan:
        for (ca, ua, uv) in ups:
            wa = ws.get(ua)
            wv = ws.get(uv)
            for sub in range(2):
                c_a = ca + sub
                c_v = NFF + ca + sub
                j = c_a - base
                tm = {}
                for which, (wt, cc) in enumerate(((wa, c_a), (wv, c_v))):
                    g0 = G[which]
                    for kc in range(NCH):
                        lw = wt[:, kc, sub * 128:(sub + 1) * 128]
                        S.mm(ps[:, g0 + 510:g0 + 512], lw, xb[:, kc, 0:2], start=(kc == 0), stop=(kc == NCH - 1))
                        S.mm(ps[:, g0 + 512:g0 + 1024], lw, xb[:, kc, 2:514], start=(kc == 0), stop=(kc == NCH - 1))
                        S.mm(ps[:, g0 + 1024:g0 + 1536], lw, xb[:, kc, 514:1026], start=(kc == 0),
                             stop=(kc == NCH - 1))
                    tmp = tmps["a" if which == 0 else "v"][gi % 2]
                    tm[which] = tmp
                    S.act(tmp[:, :], ps[:, g0 + 512:g0 + 1536], AF.Identity, bias=cwb[:, cc, 3:4],
                          scale=cwb[:, cc, 2:3])
                    S.stt(tmp[:, :], ps[:, g0 + 511:g0 + 1535], cwb[:, cc, 1:2], tmp[:, :], ALU.mult, ALU.add)
                    S.stt(tmp[:, :], ps[:, g0 + 510:g0 + 1534], cwb[:, cc, 0:1], tmp[:, :], ALU.mult, ALU.add)
                sa = tmps["s"][gi % 2]
                S.act(sa[:, :], tm[0][:, :], AF.Silu)
                S.tt("dve", gq[:, j, :], sa[:, :], tm[1][:, :], ALU.mult)
                gi += 1
            ws.release(uv)
        for (op_, ud) in downs:
            wd = ws.get(ud)
            for sub in range(2):
                oc = op_ * 2 + sub
                for tt_ in range(2):
                    for j in range(nq):
                        S.mm(ps[:, 3072 + tt_ * 512:3072 + (tt_ + 1) * 512], wd[:, j, sub * 128:(sub + 1) * 128],
                             gq[:, j, tt_ * 512:(tt_ + 1) * 512], start=(j == 0), stop=(j == nq - 1))
                if base == 0:
                    S.stt(xf[:, oc, 2:1026], xf[:, oc, 2:1026], ALPHA, ps[:, 3072:4096], ALU.mult, ALU.add)
                else:
                    S.tt("dve", xf[:, oc, 2:1026], xf[:, oc, 2:1026], ps[:, 3072:4096], ALU.add)
            ws.release(ud)


def emit_ln(S, zf, c0, n, gb, ones, tmps, ps, outs, post=None, ps_off=(0, 2048)):
    nt = (n + 511) // 512
    zb = tmps["zb"]
    zs = tmps["zs"]
    for c in range(NCH):
        b0 = zb[c % 2]
        s0 = zs[c % 2]
        S.act(b0[:, 0:n], zf[:, c, c0:c0 + n], AF.Identity)
        S.act(s0[:, 0:n], zf[:, c, c0:c0 + n], AF.Square)
        for t_ in range(nt):
            w = min(512, n - t_ * 512)
            S.mm(ps[:, ps_off[0] + t_ * 512:ps_off[0] + t_ * 512 + w], ones[:, :], b0[:, t_ * 512:t_ * 512 + w],
                 start=(c == 0), stop=(c == NCH - 1))
            S.mm(ps[:, ps_off[1] + t_ * 512:ps_off[1] + t_ * 512 + w], ones[:, :], s0[:, t_ * 512:t_ * 512 + w],
                 start=(c == 0), stop=(c == NCH - 1))
    mean = tmps["mean"]
    rstd = tmps["rstd"]
    S.ts("dve", mean[:, 0:n], ps[:, ps_off[0]:ps_off[0] + n], 1.0 / D, None, ALU.mult)
    S.tt("dve", rstd[:, 0:n], mean[:, 0:n], mean[:, 0:n], ALU.mult)
    S.stt(rstd[:, 0:n], ps[:, ps_off[1]:ps_off[1] + n], 1.0 / D, rstd[:, 0:n], ALU.mult, ALU.subtract)
    S.act(rstd[:, 0:n], rstd[:, 0:n], AF.Sqrt, bias=tmps["eps"][:, 0:1], scale=1.0)
    S.add("dve", lambda e: e.reciprocal(out=rstd[:, 0:n], in_=rstd[:, 0:n]), reads=[rstd[:, 0:n]],
          writes=[rstd[:, 0:n]])
    for c in range(NCH):
        zc = zf[:, c, c0:c0 + n]
        S.tt("dve", zc, zc, mean[:, 0:n], ALU.subtract)
        S.tt("dve", zc, zc, rstd[:, 0:n], ALU.mult)
        for i, dst in enumerate(outs):
            S.act(dst(c), zc, AF.Identity, bias=gb[:, c, 1:2], scale=gb[:, c, 0:1])
        if post is not None:
            post(c)


def build_ffn_launch():
    nc = bass.Bass("TRN2", target_bir_lowering=False)
    xT = nc.dram_tensor("xT", [D, T + 2], F32, kind="ExternalInput").ap()
    w_up = nc.dram_tensor("w_up", [D, 2 * DFF], F32, kind="ExternalInput").ap()
    w_down = nc.dram_tensor("w_down", [DFF, D], F32, kind="ExternalInput").ap()
    cwb_d = nc.dram_tensor("cwb", [128, 88 * 4], F32, kind="ExternalInput").ap()
    gb_d = nc.dram_tensor("ln2gb", [128, 32], F32, kind="ExternalInput").ap()
    yT = nc.dram_tensor("yT", [D, T], F32, kind="ExternalOutput").ap()
    S = Sched()
    with ExitStack() as es:
        xf = es.enter_context(nc.sbuf_tensor("xf", [128, NCH, T + 2], F32))
        xb = es.enter_context(nc.sbuf_tensor("xb", [128, NCH, T + 2], BF16))
        cwb = es.enter_context(nc.sbuf_tensor("cwb_s", [128, 88, 4], F32))
        gb = es.enter_context(nc.sbuf_tensor("gb_s", [128, NCH, 2], F32))
        gq = es.enter_context(nc.sbuf_tensor("gq", [128, 12, T], BF16))
        ones = es.enter_context(nc.sbuf_tensor("ones", [128, 128], BF16))
        eps = es.enter_context(nc.sbuf_tensor("eps", [128, 1], F32))
        tmps = {
            "a": [es.enter_context(nc.sbuf_tensor("ta%d" % i, [128, T], F32)) for i in range(2)],
            "v": [es.enter_context(nc.sbuf_tensor("tv%d" % i, [128, T], F32)) for i in range(2)],
            "s": [es.enter_context(nc.sbuf_tensor("tsl%d" % i, [128, T], F32)) for i in range(2)],
            "eps": eps,
        }
        tmps["zb"] = [es.enter_context(nc.sbuf_tensor("zb%d" % i, [128, T], BF16)) for i in range(2)]
        tmps["zs"] = [es.enter_context(nc.sbuf_tensor("zs%d" % i, [128, T], BF16)) for i in range(2)]
        tmps["mean"] = tmps["a"][0]
        tmps["rstd"] = tmps["a"][1]
        ps = es.enter_context(nc.psum_tensor("ps", [128, 4096], F32))
        ws = WeightStream(S, nc, es, 4, 16 * 256)
        plan = plan_ffn_weights(ws, w_up, w_down)

        ch_in = S.new_chan(total=True)
        ch_p = S.new_chan(total=True)
        ch_out = [S.new_chan() for _ in range(4)]
        S.dma("sp", cwb[:, :, :], cwb_d.rearrange("p (c j) -> p c j", j=4), ch_p)
        S.dma("sp", gb[:, :, :], gb_d.rearrange("p (c j) -> p c j", j=2), ch_p)
        S.memset("dve", ones[:, :], 1.0)
        S.memset("dve", eps[:, :], LN_EPS)
        for c in range(NCH):
            S.dma("sp", xf[:, c, :], xT[c * 128:(c + 1) * 128, :], ch_in)
        for c in range(NCH):
            S.act(xb[:, c, :], xf[:, c, :], AF.Identity)
        emit_ffn(S, ws, plan, xf, xb, cwb, gq, tmps, ps)
        emit_ln(S, xf, 2, T, gb, ones, tmps, ps, [lambda c: xf[:, c, 2:T + 2]])
        for c in range(NCH):
            S.dma("sp", yT[c * 128:(c + 1) * 128, :], xf[:, c, 2:T + 2], ch_out[c % 4])
        S.emit(nc, es, final_chans=ch_out)
    return nc, S


def _pm(v, nch):
    return np.ascontiguousarray(np.asarray(v, np.float32).reshape(nch, 128).T)


def prep_ffn_params(l, ffn_conv_w, ffn_conv_b, ln2_g, ln2_b):
    cw = np.asarray(ffn_conv_w[l], np.float32)
    cb = np.asarray(ffn_conv_b[l], np.float32)
    cwb = np.stack([_pm(cw[0], 88), _pm(cw[1], 88), _pm(cw[2], 88), _pm(cb, 88)], axis=-1)
    gb = np.stack([_pm(ln2_g[l], 16), _pm(ln2_b[l], 16)], axis=-1)
    return np.ascontiguousarray(cwb.reshape(128, 88 * 4)), np.ascontiguousarray(gb.reshape(128, 32))


def run_ffn_launch(x1, l, ffn_w_up, ffn_conv_w, ffn_conv_b, ffn_w_down, ln2_g, ln2_b):
    nc, S = build_ffn_launch()
    cwb, gb = prep_ffn_params(l, ffn_conv_w, ffn_conv_b, ln2_g, ln2_b)
    wu = np.ascontiguousarray(ffn_w_up[l], np.float32)
    wd = np.ascontiguousarray(ffn_w_down[l], np.float32)
    x1 = np.asarray(x1, np.float32)
    in_maps = []
    for c in range(NCORES):
        b, s = divmod(c, 4)
        t0 = s * T
        xt = np.zeros((D, T + 2), np.float32)
        xt[:, 2:] = x1[b, t0:t0 + T].T
        if s > 0:
            xt[:, 0:2] = x1[b, t0 - 2:t0].T
        in_maps.append({"xT": xt, "w_up": wu, "w_down": wd, "cwb": cwb, "ln2gb": gb})
    res = run_bass_kernel_spmd(nc, in_maps, core_ids=list(range(NCORES)))
    out = np.zeros((2, 4096, D), np.float32)
    for c in range(NCORES):
        b, s = divmod(c, 4)
        out[b, s * T:(s + 1) * T] = res.results[c]["yT"].T
    return out


TH = T + 128
B_WINDOWS = (2, 4, 8, 16)
GELU_C = 0.044715
GELU_S = 2.0 * 0.7978845608028654


def emit_gelu(S, dst, src_ps, t1, t2):
    S.act(t1, src_ps, AF.Square)
    S.ts("dve", t1, t1, GELU_C, 1.0, ALU.mult, ALU.add)
    S.tt("dve", t1, t1, src_ps, ALU.mult)
    S.act(t2, t1, AF.Sigmoid, scale=GELU_S)
    S.tt("dve", dst, t2, src_ps, ALU.mult)


def build_mix0_launch():
    nc = bass.Bass("TRN2", target_bir_lowering=False)
    x0T = nc.dram_tensor("x0T", [D, TH], F32, kind="ExternalInput").ap()
    w_in = nc.dram_tensor("w_in", [D, 3072], F32, kind="ExternalInput").ap()
    w_out = nc.dram_tensor("w_out", [D, D], F32, kind="ExternalInput").ap()
    wsT_d = nc.dram_tensor("wsT", [128, 8 * 128], F32, kind="ExternalInput").ap()
    mask_d = nc.dram_tensor("maskA", [128, 128], F32, kind="ExternalInput").ap()
    bsT_d = nc.dram_tensor("bsT", [128, 8 * 128], F32, kind="ExternalInput").ap()
    lnv_d = nc.dram_tensor("lnv", [128, 2 * 1024], F32, kind="ExternalInput").ap()
    wp_d = nc.dram_tensor("w_pool", [4 * 256, 256], F32, kind="ExternalInput").ap()
    psc_d = nc.dram_tensor("pscale", [128, 8], F32, kind="ExternalInput").ap()
    gb_d = nc.dram_tensor("ln1gb", [128, 32], F32, kind="ExternalInput").ap()
    rc_d = nc.dram_tensor("rcnt", [128, 64], F32, kind="ExternalInput").ap()
    flag_d = nc.dram_tensor("flag", [128, 1], F32, kind="ExternalInput").ap()
    x1T = nc.dram_tensor("x1T", [D, T + 2], F32, kind="ExternalOutput").ap()
    S = Sched()
    with ExitStack() as es:
        arena = es.enter_context(nc.sbuf_tensor("arena", [128, NCH * (T + 2)], F32))
        zf = arena[:, :].rearrange("p (c t) -> p c t", c=NCH, t=T + 2)
        x0b = arena[:, 0:NCH * TH // 2].bitcast(BF16).rearrange("p (c t) -> p c t", c=NCH, t=TH)
        u = es.enter_context(nc.sbuf_tensor("u", [128, 8, TH], BF16))
        vt = es.enter_context(nc.sbuf_tensor("vt", [128, 9, 1024], BF16))
        pp = es.enter_context(nc.sbuf_tensor("pp", [128, 8, TH], BF16))
        wsT = es.enter_context(nc.sbuf_tensor("wsT_s", [128, 8, 128], BF16))
        mask = es.enter_context(nc.sbuf_tensor("mask_s", [128, 128], F32))
        bsT = es.enter_context(nc.sbuf_tensor("bsT_s", [128, 8, 128], F32))
        lnv = es.enter_context(nc.sbuf_tensor("lnv_s", [128, 2, 1024], F32))
        wp = es.enter_context(nc.sbuf_tensor("wp_s", [128, 8, 256], BF16))
        psc = es.enter_context(nc.sbuf_tensor("psc_s", [128, 8], F32))
        gb = es.enter_context(nc.sbuf_tensor("gb_s", [128, NCH, 2], F32))
        rc = es.enter_context(nc.sbuf_tensor("rc_s", [128, 4, 16], F32))
        flag = es.enter_context(nc.sbuf_tensor("flag_s", [128, 1], F32))
        ones = es.enter_context(nc.sbuf_tensor("ones", [128, 128], BF16))
        eps = es.enter_context(nc.sbuf_tensor("eps", [128, 1], F32))
        xbf = es.enter_context(nc.sbuf_tensor("xbf", [128, 16 + TH], F32))
        g1 = [es.enter_context(nc.sbuf_tensor("g1_%d" % i, [128, TH], F32)) for i in range(2)]
        g2p = [es.enter_context(nc.sbuf_tensor("g2_%d" % i, [128, 16 + TH], F32)) for i in range(2)]
        g2 = [t[:, 16:16 + TH] for t in g2p]
        tA, tB = g2p
        wsTf = g1[1][:, 0:1024].rearrange("p (h t) -> p h t", h=8)
        st = es.enter_context(nc.sbuf_tensor("st", [128, 8], F32))
        small = es.enter_context(nc.sbuf_tensor("small", [128, 32], F32))
        tmps = {"eps": eps,
                "zb": [g2p[i][:, 16:16 + 513].bitcast(BF16) for i in range(2)],
                "zs": [xbf[:, 16:16 + 513].bitcast(BF16), xbf[:, 600:600 + 513].bitcast(BF16)],
                "mean": g1[0], "rstd": g1[1]}
        ps = es.enter_context(nc.psum_tensor("ps", [128, 4096], F32))
        ws = WeightStream(S, nc, es, 4, 16 * 256)
        wi = w_in.rearrange("(kc p) n -> p kc n", p=128)
        wo = w_out.rearrange("(kc p) n -> p kc n", p=128)
        u_xb = [ws.plan((16, 256), wi[:, :, 2048 + i * 256:2048 + (i + 1) * 256]) for i in range(4)]
        u_u = [ws.plan((16, 256), wi[:, :, i * 256:(i + 1) * 256]) for i in range(4)]
        u_v = [ws.plan((16, 256), wi[:, :, 1024 + i * 256:1024 + (i + 1) * 256]) for i in range(4)]
        u_o = [ws.plan((16, 256), wo[:, :, i * 256:(i + 1) * 256]) for i in range(8)]

        chp = S.new_chan(total=True)
        chx = S.new_chan(total=True)
        S.dma("sp", wsTf, wsT_d.rearrange("p (h t) -> p h t", h=8), chp)
        S.dma("sp", mask[:, :], mask_d, chp)
        S.dma("sp", bsT[:, :, :], bsT_d.rearrange("p (h t) -> p h t", h=8), chp)
        S.dma("sp", lnv[:, :, :], lnv_d.rearrange("p (a c) -> p a c", a=2), chp)
        S.dma("sp", psc[:, :], psc_d, chp)
        S.dma("sp", gb[:, :, :], gb_d.rearrange("p (c j) -> p c j", j=2), chp)
        S.dma("sp", rc[:, :, :], rc_d.rearrange("p (g j) -> p g j", g=4), chp)
        S.dma("sp", flag[:, :], flag_d, chp)
        S.dma("pool", wp[:, :, :], wp_d.rearrange("(a p) n -> p a n", p=128), chx)
        for c in range(NCH):
            S.dma("pool", x0b[:, c, :], x0T[c * 128:(c + 1) * 128, :], chx)
        ws.release(-1)
        S.memset("dve", ones[:, :], 1.0)
        S.memset("dve", eps[:, :], LN_EPS)
        S.memset("dve", xbf[:, 0:16], 0.0)
        S.memset("dve", tA[:, 0:16], 0.0)
        S.memset("dve", tB[:, 0:16], 0.0)
        for h in range(8):
            S.tt("dve", wsT[:, h, :], wsTf[:, h, :], mask[:, :], ALU.mult)

        GR = (0, 1536)
        TT3 = ((0, 512), (512, 512), (1024, 128))

        def proj_fm(wt, sub, g0):
            for kc in range(NCH):
                for (t0, w) in TT3:
                    S.mm(ps[:, g0 + t0:g0 + t0 + w], wt[:, kc, sub * 128:(sub + 1) * 128], x0b[:, kc, t0:t0 + w],
                         start=(kc == 0), stop=(kc == NCH - 1))

        gi = 0
        for i in range(4):
            wt = ws.get(u_xb[i])
            for sub in range(2):
                c = i * 2 + sub
                g = c // 2
                g0 = GR[gi % 2]
                gi += 1
                proj_fm(wt, sub, g0)
                S.act(xbf[:, 16:16 + TH], ps[:, g0:g0 + TH], AF.Identity)
                src = xbf
                dsts = [tA, tB]
                for k in range(g + 1):
                    sh = 1 << k
                    dst = dsts[k % 2]
                    S.tt("dve", dst[:, 16:16 + TH], src[:, 16:16 + TH], src[:, 16 - sh:16 + TH - sh], ALU.add)
                    src = dst
                win = B_WINDOWS[g]
                S.stt(pp[:, c, :], src[:, 16:16 + TH], 1.0 / win, xbf[:, 16:16 + TH], ALU.mult, ALU.subtract)
                S.tt("dve", small[:, 0:16], src[:, 16 + 128:16 + 144], rc[:, g, :], ALU.mult)
                S.tt("dve", pp[:, c, 128:144], small[:, 0:16], xbf[:, 16 + 128:16 + 144], ALU.subtract)
            ws.release(u_xb[i])
        for i in range(4):
            wt = ws.get(u_u[i])
            for sub in range(2):
                c = i * 2 + sub
                g0 = GR[gi % 2]
                proj_fm(wt, sub, g0)
                emit_gelu(S, u[:, c, :], ps[:, g0:g0 + TH], g1[gi % 2][:, :], g2[gi % 2])
                gi += 1
            ws.release(u_u[i])
        wv = [ws.get(k) for k in u_v]
        for tk in range(9):
            vr = g1[tk % 2]
            for cg in range(4):
                r0 = 3072 + ((tk * 4 + cg) % 2) * 512
                for kc in range(NCH):
                    S.mm(ps[:, r0:r0 + 256], x0b[:, kc, tk * 128:(tk + 1) * 128], wv[cg][:, kc, :],
                         start=(kc == 0), stop=(kc == NCH - 1))
                emit_gelu(S, vr[:, cg * 256:(cg + 1) * 256], ps[:, r0:r0 + 256],
                          g2[0][:, cg * 256:(cg + 1) * 256], g2[1][:, cg * 256:(cg + 1) * 256])
            sq = g2[0]
            S.add("dve", lambda e, vr=vr: e.reduce_sum(out=st[:, 0:1], in_=vr[:, 0:1024], axis=mybir.AxisListType.X),
                  reads=[vr[:, 0:1024]], writes=[st[:, 0:1]])
            S.act(sq[:, 0:1024], vr[:, 0:1024], AF.Square)
            S.add("dve", lambda e, sq=sq: e.reduce_sum(out=st[:, 1:2], in_=sq[:, 0:1024], axis=mybir.AxisListType.X),
                  reads=[sq[:, 0:1024]], writes=[st[:, 1:2]])
            S.ts("dve", st[:, 2:3], st[:, 0:1], 1.0 / 1024, None, ALU.mult)
            S.tt("dve", st[:, 3:4], st[:, 2:3], st[:, 2:3], ALU.mult)
            S.stt(st[:, 4:5], st[:, 1:2], 1.0 / 1024, st[:, 3:4], ALU.mult, ALU.subtract)
            S.act(st[:, 5:6], st[:, 4:5], AF.Sqrt, bias=eps[:, 0:1], scale=1.0)
            S.add("dve", lambda e: e.reciprocal(out=st[:, 6:7], in_=st[:, 5:6]), reads=[st[:, 5:6]],
                  writes=[st[:, 6:7]])
            S.ts("dve", vr[:, 0:1024], vr[:, 0:1024], st[:, 2:3], st[:, 6:7], ALU.subtract, ALU.mult)
            S.tt("dve", vr[:, 0:1024], vr[:, 0:1024], lnv[:, 0, :], ALU.mult)
            S.tt("dve", vt[:, tk, :], vr[:, 0:1024], lnv[:, 1, :], ALU.add)
        ws.release(u_v[3])
        for tk in range(9):
            for half in range(2):
                r0 = 2048 + half * 512
                for hh in range(4):
                    h = half * 4 + hh
                    S.mm(ps[:, r0 + hh * 128:r0 + (hh + 1) * 128], vt[:, tk, h * 128:(h + 1) * 128], wsT[:, h, :],
                         start=True, stop=True)
                tmp = g2[half][:, 0:512].rearrange("p (h t) -> p h t", h=4)
                S.tt("dve", tmp, ps[:, r0:r0 + 512].rearrange("p (h t) -> p h t", h=4),
                     bsT[:, half * 4:half * 4 + 4, :], ALU.add)
                uu = u[:, half * 4:half * 4 + 4, tk * 128:(tk + 1) * 128]
                S.tt("dve", uu, tmp, uu, ALU.mult)
        for g in range(4):
            for oc in range(2):
                g0 = GR[oc]
                for kc in range(2):
                    for (t0, w) in TT3:
                        S.mm(ps[:, g0 + t0:g0 + t0 + w], wp[:, g * 2 + kc, oc * 128:(oc + 1) * 128],
                             pp[:, g * 2 + kc, t0:t0 + w], start=(kc == 0), stop=(kc == 1))
            for oc in range(2):
                g0 = GR[oc]
                c = g * 2 + oc
                S.act(pp[:, c, :], ps[:, g0:g0 + TH], AF.Identity, scale=psc[:, c:c + 1])
        xs = [g1[0], g1[1]]
        chs = [S.new_chan(), S.new_chan()]
        for i in range(8):
            wt = ws.get(u_o[i])
            for sub in range(2):
                oc = i * 2 + sub
                g0 = GR[oc % 2]
                xst = xs[oc % 2]
                S.dma("sp", xst[:, 0:T + 2], x0T[oc * 128:(oc + 1) * 128, 126:TH], chs[oc % 2])
                for kc in range(NCH):
                    src = u[:, kc, :] if kc < 8 else pp[:, kc - 8, :]
                    lw = wt[:, kc, sub * 128:(sub + 1) * 128]
                    S.mm(ps[:, g0 + 510:g0 + 512], lw, src[:, 126:128], start=(kc == 0), stop=(kc == NCH - 1))
                    S.mm(ps[:, g0 + 512:g0 + 1024], lw, src[:, 128:640], start=(kc == 0), stop=(kc == NCH - 1))
                    S.mm(ps[:, g0 + 1024:g0 + 1536], lw, src[:, 640:1152], start=(kc == 0), stop=(kc == NCH - 1))
                S.stt(zf[:, oc, :], xst[:, 0:T + 2], ALPHA, ps[:, g0 + 510:g0 + 1536], ALU.mult, ALU.add)
            ws.release(u_o[i])
        emit_ln(S, zf, 0, T + 2, gb, ones, tmps, ps, [lambda c: zf[:, c, :]])
        S.ts("dve", zf[:, :, 0:2], zf[:, :, 0:2], flag[:, 0:1], None, ALU.mult)
        cho = [S.new_chan() for _ in range(4)]
        for c in range(NCH):
            S.dma("sp", x1T[c * 128:(c + 1) * 128, :], zf[:, c, :], cho[c % 4])
        S.emit(nc, es, final_chans=cho)
    return nc, S


def prep_mix0_inputs(x, ev_w_in, ev_ln_v_g, ev_ln_v_b, ev_w_s, ev_b_s, ev_w_pool, ev_pool_scale, ev_w_out,
                     ln1_g, ln1_b):
    x = np.asarray(x, np.float32)
    ws = np.asarray(ev_w_s[0], np.float32)
    wsT = np.ascontiguousarray(ws.transpose(2, 0, 1)).reshape(128, 8 * 128)
    tt_ = np.arange(128)
    maskA = (tt_[None, :] >= tt_[:, None]).astype(np.float32)
    bsT = np.ascontiguousarray(np.broadcast_to(np.asarray(ev_b_s[0], np.float32).reshape(1, 8 * 128), (128, 8 * 128)))
    lnv = np.ascontiguousarray(np.broadcast_to(
        np.concatenate([np.asarray(ev_ln_v_g[0], np.float32), np.asarray(ev_ln_v_b[0], np.float32)])[None, :],
        (128, 2048)))
    wp = np.ascontiguousarray(np.asarray(ev_w_pool[0], np.float32).reshape(4 * 256, 256))
    psc = _pm(ev_pool_scale[0], 8)
    gb = np.ascontiguousarray(np.stack([_pm(ln1_g[0], 16), _pm(ln1_b[0], 16)], axis=-1).reshape(128, 32))
    common = {"w_in": np.ascontiguousarray(ev_w_in[0], np.float32),
              "w_out": np.ascontiguousarray(ev_w_out[0], np.float32),
              "wsT": wsT, "maskA": maskA, "bsT": bsT, "lnv": lnv, "w_pool": wp, "pscale": psc, "ln1gb": gb}
    in_maps = []
    for c in range(NCORES):
        b, s = divmod(c, 4)
        t0 = s * T
        xt = np.zeros((D, TH), np.float32)
        xt[:, 128:] = x[b, t0:t0 + T].T
        if s > 0:
            xt[:, 0:128] = x[b, t0 - 128:t0].T
        rc = np.zeros((4, 16), np.float32)
        for g, win in enumerate(B_WINDOWS):
            pos = np.arange(t0 + 1, t0 + 17, dtype=np.float32)
            rc[g] = 1.0 / np.minimum(pos, float(win))
        rcb = np.ascontiguousarray(np.broadcast_to(rc.reshape(1, 64), (128, 64)))
        m = dict(common)
        m.update({"x0T": xt, "rcnt": rcb, "flag": np.full((128, 1), 1.0 if s > 0 else 0.0, np.float32)})
        in_maps.append(m)
    return in_maps


def run_mix0_launch(inputs):
    nc, S = build_mix0_launch()
    in_maps = prep_mix0_inputs(inputs["x"], inputs["ev_w_in"], inputs["ev_ln_v_g"], inputs["ev_ln_v_b"],
                               inputs["ev_w_s"], inputs["ev_b_s"], inputs["ev_w_pool"], inputs["ev_pool_scale"],
                               inputs["ev_w_out"], inputs["ln1_g"], inputs["ln1_b"])
    res = run_bass_kernel_spmd(nc, in_maps, core_ids=list(range(NCORES)))
    return [res.results[c]["x1T"] for c in range(NCORES)]


NH = 16
CH = 64
NCK = T // CH


def hgrn_consts(S, nc, es, lbp_d, mask2_d, ident_d, chp):
    c = {}
    lbp = es.enter_context(nc.sbuf_tensor("lbp_s", [128, 2, NH], F32))
    c["lb"] = es.enter_context(nc.sbuf_tensor("lb", [128, NH], F32))
    c["oml"] = es.enter_context(nc.sbuf_tensor("oml", [128, NH], F32))
    c["rm"] = es.enter_context(nc.sbuf_tensor("rm", [128, T], F32))
    c["mask2"] = es.enter_context(nc.sbuf_tensor("mask2_s", [128, 128], F32))
    identf = es.enter_context(nc.sbuf_tensor("identf", [128, 128], F32))
    c["ident"] = es.enter_context(nc.sbuf_tensor("ident_s", [128, 128], BF16))
    S.dma("sp", lbp[:, :, :], lbp_d.rearrange("p (l h) -> p l h", l=2), chp)
    S.dma("sp", c["mask2"][:, :], mask2_d, chp)
    S.dma("sp", identf[:, :], ident_d, chp)
    S.copy("dve", c["ident"][:, :], identf[:, :])
    S.tt("dve", c["lb"][:, :], lbp[:, 1, :], lbp[:, 0, :], ALU.subtract)
    S.act(c["lb"][:, :], c["lb"][:, :], AF.Sigmoid)
    S.ts("dve", c["oml"][:, :], c["lb"][:, :], -1.0, 1.0, ALU.mult, ALU.add)
    S.memset("dve", c["rm"][:, :], 1.0)
    S.memset("dve", c["rm"][:, :].rearrange("p (c t) -> p c t", t=CH)[:, :, 0:1], 0.0)
    c["pm"] = es.enter_context(nc.sbuf_tensor("pm", [128, 2], F32))
    S.memset("dve", c["pm"][:, :], 0.0)
    S.memset("dve", c["pm"][0:64, 0:1], 1.0)
    S.memset("dve", c["pm"][64:128, 1:2], 1.0)
    return c


def hgrn_gates(S, h, cst, f_ps, A, B, C, kend_bf, dec, kdec_bf=None, eC=None):
    S.act(A, f_ps, AF.Sigmoid)
    S.ts("dve", A, A, cst["oml"][:, h:h + 1], cst["lb"][:, h:h + 1], ALU.mult, ALU.add)
    S.act(B, A, AF.Ln)
    S.add("dve", lambda e: e.tensor_tensor_scan(out=C, data0=cst["rm"][:, :], data1=B, initial=0.0,
                                                op0=ALU.mult, op1=ALU.add),
          reads=[cst["rm"][:, :], B], writes=[C])
    S.ts("dve", A, A, -1.0, 1.0, ALU.mult, ALU.add)
    S.act(B, C, AF.Exp, scale=-1.0)
    S.tt("dve", B, A, B, ALU.mult)
    if kdec_bf is not None:
        S.act(kdec_bf, B, AF.Identity)
    C3 = C.rearrange("p (c t) -> p c t", t=CH)
    S.act(dec.rearrange("p (c o) -> p c o", o=1), C3[:, :, CH - 1:CH], AF.Exp)
    S.tt("dve", kend_bf.rearrange("p (c t) -> p c t", t=CH), B.rearrange("p (c t) -> p c t", t=CH),
         dec.rearrange("p (c o) -> p c o", o=1).to_broadcast([128, NCK, CH]), ALU.mult)
    if eC is not None:
        S.act(eC, C, AF.Exp)


def hgrn_transposes(S, cst, psb, src_bf, dst_tok, dst_tok1=None):
    for j in range(8):
        S.transpose(psb[:, 4096 + j * 128:4096 + (j + 1) * 128], src_bf[:, j * 128:(j + 1) * 128], cst["ident"][:, :])
    if dst_tok1 is None:
        S.act(dst_tok.rearrange("p j d -> p (j d)"), psb[:, 4096:5120], AF.Identity)
    else:
        S.act(dst_tok.rearrange("p j d -> p (j d)"), psb[:, 4096:5120], AF.Identity, scale=cst["pm"][:, 0:1])
        S.act(dst_tok1.rearrange("p j d -> p (j d)"), psb[:, 4096:5120], AF.Identity, scale=cst["pm"][:, 1:2])


def hgrn_state_scan(S, ps, kt, vtk, dec, Sf, Sb=None):
    cur = 0
    if Sb is not None:
        S.act(Sb[:, 0, :], Sf[0][:, :], AF.Identity)
    for g4 in range(4):
        for cc in range(4):
            c = g4 * 4 + cc
            j, par = divmod(c, 2)
            S.mm(ps[:, 2560 + cc * 128:2560 + (cc + 1) * 128], kt[par][:, j, :], vtk[:, j, :], start=True, stop=True)
        for cc in range(4):
            c = g4 * 4 + cc
            nxt = 1 - cur
            S.stt(Sf[nxt][:, :], Sf[cur][:, :], dec[:, c:c + 1], ps[:, 2560 + cc * 128:2560 + (cc + 1) * 128],
                  ALU.mult, ALU.add)
            cur = nxt
            if Sb is not None and c + 1 < NCK:
                S.act(Sb[:, c + 1, :], Sf[cur][:, :], AF.Identity)
    return cur


def build_hgrn_pre_launch(stage=9, nheads=NH):
    nc = bass.Bass("TRN2", target_bir_lowering=False)
    xT = nc.dram_tensor("xT", [D, T], F32, kind="ExternalInput").ap()
    w_in = nc.dram_tensor("w_in", [D, 4 * D], F32, kind="ExternalInput").ap()
    lbp_d = nc.dram_tensor("lbp", [128, 2 * NH], F32, kind="ExternalInput").ap()
    mask2_d = nc.dram_tensor("mask2", [128, 128], F32, kind="ExternalInput").ap()
    ident_d = nc.dram_tensor("ident", [128, 128], F32, kind="ExternalInput").ap()
    U_d = nc.dram_tensor("U", [NH, 128, 128], F32, kind="ExternalOutput").ap()
    D_d = nc.dram_tensor("Dd", [128, NH], F32, kind="ExternalOutput").ap()
    S = Sched()
    with ExitStack() as es:
        xb = es.enter_context(nc.sbuf_tensor("xb", [128, NCH, T], BF16))
        A = [es.enter_context(nc.sbuf_tensor("A%d" % i, [128, T], F32)) for i in range(2)]
        B = [es.enter_context(nc.sbuf_tensor("B%d" % i, [128, T], F32)) for i in range(2)]
        C = [es.enter_context(nc.sbuf_tensor("C%d" % i, [128, T], F32)) for i in range(2)]
        kend = [es.enter_context(nc.sbuf_tensor("kend%d" % i, [128, T], BF16)) for i in range(2)]
        ibf = [es.enter_context(nc.sbuf_tensor("ibf%d" % i, [128, T], BF16)) for i in range(2)]
        kt = [[es.enter_context(nc.sbuf_tensor("kt%d_%d" % (i, k), [128, 8, 128], BF16)) for k in range(2)]
              for i in range(2)]
        vtk = [es.enter_context(nc.sbuf_tensor("vtk%d" % i, [128, 8, 128], BF16)) for i in range(2)]
        dec = [es.enter_context(nc.sbuf_tensor("dec%d" % i, [128, NCK], F32)) for i in range(2)]
        Sf = [[es.enter_context(nc.sbuf_tensor("Sf%d_%d" % (i, k), [128, 128], F32)) for k in range(2)]
              for i in range(2)]
        Dd = es.enter_context(nc.sbuf_tensor("Dd_s", [128, NH], F32))
        ps = es.enter_context(nc.psum_tensor("ps", [128, 4096], F32))
        psb = ps[:, :].bitcast(BF16)
        chp = S.new_chan(total=True)
        chx = S.new_chan(total=True)
        cst = hgrn_consts(S, nc, es, lbp_d, mask2_d, ident_d, chp)
        ws = WeightStream(S, nc, es, 4, 16 * 256)
        wi = w_in.rearrange("(kc p) n -> p kc n", p=128)
        uses = [ws.plan((16, 256), wi[:, :, h * 512 + 256:h * 512 + 512]) for h in range(NH)]
        for c in range(NCH):
            S.dma("pool", xb[:, c, :], xT[c * 128:(c + 1) * 128, :], chx)
        ws.release(-1)
        cho = [S.new_chan() for _ in range(2)]
        for h in range(nheads):
            b = h % 2
            wt = ws.get(uses[h])
            for which in range(2):
                g0 = which * 1024
                for kc in range(NCH):
                    for t_ in range(2):
                        S.mm(ps[:, g0 + t_ * 512:g0 + (t_ + 1) * 512], wt[:, kc, which * 128:(which + 1) * 128],
                             xb[:, kc, t_ * 512:(t_ + 1) * 512], start=(kc == 0), stop=(kc == NCH - 1))
            ws.release(uses[h])
            hgrn_gates(S, h, cst, ps[:, 0:1024], A[b][:, :], B[b][:, :], C[b][:, :], kend[b][:, :], dec[b][:, :])
            S.act(ibf[b][:, :], ps[:, 1024:2048], AF.Identity)
            C3 = C[b][:, :].rearrange("p (c t) -> p c t", t=CH)
            S.add("dve", lambda e, C3=C3, h=h: e.reduce_sum(out=Dd[:, h:h + 1], in_=C3[:, :, CH - 1:CH],
                                                           axis=mybir.AxisListType.XY),
                  reads=[C[b][:, :]], writes=[Dd[:, h:h + 1]])
            S.act(Dd[:, h:h + 1], Dd[:, h:h + 1], AF.Exp)
            if stage >= 2:
                hgrn_transposes(S, cst, psb, kend[b][:, :], kt[b][0][:, :, :], kt[b][1][:, :, :])
                hgrn_transposes(S, cst, psb, ibf[b][:, :], vtk[b][:, :, :])
            S.memset("dve", Sf[b][0][:, :], 0.0)
            fin = 0
            if stage >= 3:
                fin = hgrn_state_scan(S, ps, kt[b], vtk[b], dec[b], Sf[b])
            S.dma("sp", U_d[h, :, :], Sf[b][fin][:, :], cho[b])
        chd = S.new_chan()
        S.dma("sp", D_d, Dd[:, :], chd)
        S.emit(nc, es, final_chans=cho + [chd])
    return nc, S


def regroup_w_in(od_w_in):
    w = np.asarray(od_w_in, np.float32).reshape(D, 4, NH, 128)
    w = w[:, [0, 3, 1, 2]]
    return np.ascontiguousarray(w.transpose(0, 2, 1, 3).reshape(D, 4 * D))


def hgrn_const_inputs(lb_param):
    lbp = np.ascontiguousarray(np.stack([_pm(lb_param[0], NH), _pm(lb_param[1], NH)], axis=1).reshape(128, 2 * NH))
    i = np.arange(128)
    mask2 = ((i[None, :] >= i[:, None]) & ((i[None, :] // CH) == (i[:, None] // CH))).astype(np.float32)
    return {"lbp": lbp, "mask2": mask2, "ident": np.eye(128, dtype=np.float32)}


def _carve(arena, off_bytes, n_elems, dt):
    assert off_bytes % 4 == 0
    nb = n_elems * _dsize(dt)
    assert nb % 4 == 0
    v = arena[:, off_bytes // 4:(off_bytes + nb) // 4]
    return v if dt == F32 else v.bitcast(dt)


def build_hgrn_main_launch():
    nc = bass.Bass("TRN2", target_bir_lowering=False)
    xT = nc.dram_tensor("xT", [D, T], F32, kind="ExternalInput").ap()
    w_in = nc.dram_tensor("w_in", [D, 4 * D], F32, kind="ExternalInput").ap()
    w_out = nc.dram_tensor("w_out", [D, D], F32, kind="ExternalInput").ap()
    lbp_d = nc.dram_tensor("lbp", [128, 2 * NH], F32, kind="ExternalInput").ap()
    mask2_d = nc.dram_tensor("mask2", [128, 128], F32, kind="ExternalInput").ap()
    ident_d = nc.dram_tensor("ident", [128, 128], F32, kind="ExternalInput").ap()
    up_d = nc.dram_tensor("Uprev", [3, NH, 128, 128], F32, kind="ExternalInput").ap()
    dp_d = nc.dram_tensor("Dprev", [128, 3 * NH], F32, kind="ExternalInput").ap()
    gn_d = nc.dram_tensor("gn", [128, NH], F32, kind="ExternalInput").ap()
    gb_d = nc.dram_tensor("ln1gb", [128, 32], F32, kind="ExternalInput").ap()
    x1T = nc.dram_tensor("x1T", [D, T], F32, kind="ExternalOutput").ap()
    S = Sched()
    with ExitStack() as es:
        arena = es.enter_context(nc.sbuf_tensor("arena", [128, NCH * T], F32))
        zf = arena[:, :].rearrange("p (c t) -> p c t", c=NCH, t=T)
        xb = _carve(arena, 0, NCH * T, BF16).rearrange("p (c t) -> p c t", c=NCH, t=T)
        off = NCH * T * 2
        A = _carve(arena, off, T, F32); off += 4 * T
        B = _carve(arena, off, T, F32); off += 4 * T
        C = _carve(arena, off, T, F32); off += 4 * T
        kend = _carve(arena, off, T, BF16); off += 2 * T
        kdec = _carve(arena, off, T, BF16); off += 2 * T
        qdec = _carve(arena, off, T, BF16); off += 2 * T
        ibf = _carve(arena, off, T, BF16); off += 2 * T
        sg = _carve(arena, off, T, BF16); off += 2 * T
        osq = _carve(arena, off, T, BF16); off += 2 * T
        attm = _carve(arena, off, T, BF16).rearrange("p (j t) -> p j t", j=8); off += 2 * T
        kt0 = _carve(arena, off, T, BF16).rearrange("p (j t) -> p j t", j=8); off += 2 * T
        kt1 = _carve(arena, off, T, BF16).rearrange("p (j t) -> p j t", j=8); off += 2 * T
        kt = (kt0, kt1)
        vtk = _carve(arena, off, T, BF16).rearrange("p (j t) -> p j t", j=8); off += 2 * T
        assert off <= NCH * T * 4
        y = es.enter_context(nc.sbuf_tensor("y", [128, NH, T], BF16))
        Sb = es.enter_context(nc.sbuf_tensor("Sb", [128, NCK, 128], BF16))
        Sf = [es.enter_context(nc.sbuf_tensor("Sf%d" % k, [128, 128], F32)) for k in range(2)]
        upst = es.enter_context(nc.sbuf_tensor("upst", [128, 3, 128], F32))
        dec = es.enter_context(nc.sbuf_tensor("dec", [128, NCK], F32))
        dp = es.enter_context(nc.sbuf_tensor("dp", [128, 3, NH], F32))
        gn = es.enter_context(nc.sbuf_tensor("gn_s", [128, NH], F32))
        gb = es.enter_context(nc.sbuf_tensor("gb_s", [128, NCH, 2], F32))
        ones = es.enter_context(nc.sbuf_tensor("ones", [128, 128], BF16))
        eps = es.enter_context(nc.sbuf_tensor("eps", [128, 1], F32))
        xs = [es.enter_context(nc.sbuf_tensor("xs%d" % i, [128, T], F32)) for i in range(2)]
        tmps = {"eps": eps,
                "zb": [es.enter_context(nc.sbuf_tensor("zb%d" % i, [128, T], BF16)) for i in range(2)],
                "zs": [es.enter_context(nc.sbuf_tensor("zs%d" % i, [128, T], BF16)) for i in range(2)],
                "mean": xs[0], "rstd": xs[1]}
        ps = es.enter_context(nc.psum_tensor("ps", [128, 4096], F32))
        psb = ps[:, :].bitcast(BF16)
        chp = S.new_chan(total=True)
        chx = S.new_chan(total=True)
        cst = hgrn_consts(S, nc, es, lbp_d, mask2_d, ident_d, chp)
        S.dma("sp", dp[:, :, :], dp_d.rearrange("p (j h) -> p j h", j=3), chp)
        S.dma("sp", gn[:, :], gn_d, chp)
        S.dma("sp", gb[:, :, :], gb_d.rearrange("p (c j) -> p c j", j=2), chp)
        S.memset("dve", ones[:, :], 1.0)
        S.memset("dve", eps[:, :], LN_EPS)
        ws = WeightStream(S, nc, es, 3, 16 * 512)
        wi = w_in.rearrange("(kc p) n -> p kc n", p=128)
        wo = w_out.rearrange("(kc p) n -> p kc n", p=128)
        uses = [ws.plan((16, 512), wi[:, :, h * 512:(h + 1) * 512]) for h in range(NH)]
        u_o = [ws.plan((16, 512), wo[:, :, i * 512:(i + 1) * 512]) for i in range(4)]
        for c in range(NCH):
            S.dma("pool", xb[:, c, :], xT[c * 128:(c + 1) * 128, :], chx)
        ws.release(-1)
        chu = S.new_chan()
        G0, G1 = 0, 1024
        for h in range(NH):
            wt = ws.get(uses[h])

            def proj(blk, g0):
                for kc in range(NCH):
                    for t_ in range(2):
                        S.mm(ps[:, g0 + t_ * 512:g0 + (t_ + 1) * 512], wt[:, kc, blk * 128:(blk + 1) * 128],
                             xb[:, kc, t_ * 512:(t_ + 1) * 512], start=(kc == 0), stop=(kc == NCH - 1))
            S.dma("sp", upst[:, :, :], up_d[:, h, :, :].rearrange("j d e -> d j e"), chu)
            S.memset("dve", Sf[0][:, :], 0.0)
            cur = 0
            for j in range(3):
                S.stt(Sf[1 - cur][:, :], Sf[cur][:, :], dp[:, j, h:h + 1], upst[:, j, :], ALU.mult, ALU.add)
                cur = 1 - cur
            Sfl = [Sf[cur], Sf[1 - cur]]
            proj(2, G0)
            proj(3, G1)
            hgrn_gates(S, h, cst, ps[:, G0:G0 + T], A, B, C, kend, dec[:, :], kdec_bf=kdec, eC=A)
            S.act(ibf, ps[:, G1:G1 + T], AF.Identity)
            proj(0, G0)
            proj(1, G1)
            ws.release(uses[h])
            S.act(B, ps[:, G0:G0 + T], AF.Silu)
            S.tt("dve", qdec, B, A, ALU.mult)
            S.act(sg, ps[:, G1:G1 + T], AF.Sigmoid)
            hgrn_transposes(S, cst, psb, kend, kt0, kt1)
            hgrn_transposes(S, cst, psb, ibf, vtk)
            hgrn_state_scan(S, ps, kt, vtk, dec, Sfl, Sb=Sb)
            for half in range(2):
                for jj in range(4):
                    j = half * 4 + jj
                    S.mm(ps[:, 2560 + jj * 128:2560 + (jj + 1) * 128], kdec[:, j * 128:(j + 1) * 128],
                         qdec[:, j * 128:(j + 1) * 128], start=True, stop=True)
                for jj in range(4):
                    j = half * 4 + jj
                    S.tt("dve", attm[:, j, :], ps[:, 2560 + jj * 128:2560 + (jj + 1) * 128], cst["mask2"][:, :],
                         ALU.mult)
            O0 = 3072
            for j in range(8):
                S.mm(ps[:, O0 + j * 128:O0 + (j + 1) * 128], vtk[:, j, :], attm[:, j, :], start=True, stop=False)
                S.mm(ps[:, O0 + j * 128:O0 + j * 128 + 64], Sb[:, 2 * j, :], qdec[:, j * 128:j * 128 + 64],
                     start=False, stop=False)
                S.mm(ps[:, O0 + j * 128 + 64:O0 + (j + 1) * 128], Sb[:, 2 * j + 1, :],
                     qdec[:, j * 128 + 64:(j + 1) * 128], start=False, stop=True)
            S.act(osq, ps[:, O0:O0 + T], AF.Square)
            for t_ in range(2):
                S.mm(ps[:, G0 + t_ * 512:G0 + (t_ + 1) * 512], ones[:, :], osq[:, t_ * 512:(t_ + 1) * 512],
                     start=True, stop=True)
            S.act(A, ps[:, G0:G0 + T], AF.Ln, bias=eps[:, 0:1], scale=1.0 / 128)
            S.act(A, A, AF.Exp, scale=-0.5)
            S.stt(C, ps[:, O0:O0 + T], gn[:, h:h + 1], A, ALU.mult, ALU.mult)
            S.tt("dve", y[:, h, :], C, sg, ALU.mult)
        chs = [S.new_chan(), S.new_chan()]
        for i in range(4):
            wt = ws.get(u_o[i])
            for sub in range(4):
                oc = i * 4 + sub
                g0 = (oc % 2) * 1024
                xst = xs[oc % 2]
                S.dma("sp", xst[:, :], xT[oc * 128:(oc + 1) * 128, :], chs[oc % 2])
                for kc in range(NCH):
                    for t_ in range(2):
                        S.mm(ps[:, g0 + t_ * 512:g0 + (t_ + 1) * 512], wt[:, kc, sub * 128:(sub + 1) * 128],
                             y[:, kc, t_ * 512:(t_ + 1) * 512], start=(kc == 0), stop=(kc == NCH - 1))
                S.stt(zf[:, oc, :], xst[:, :], ALPHA, ps[:, g0:g0 + T], ALU.mult, ALU.add)
            ws.release(u_o[i])
        emit_ln(S, zf, 0, T, gb, ones, tmps, ps, [lambda c: zf[:, c, :]])
        cho = [S.new_chan() for _ in range(4)]
        for c in range(NCH):
            S.dma("sp", x1T[c * 128:(c + 1) * 128, :], zf[:, c, :], cho[c % 4])
        S.emit(nc, es, final_chans=cho)
    return nc, S


def _run(nc, in_maps):
    return run_bass_kernel_spmd(nc, in_maps, core_ids=list(range(NCORES))).results


def kernel_unfused(x, ev_w_in, ev_ln_v_g, ev_ln_v_b, ev_w_s, ev_b_s, ev_w_pool, ev_pool_scale,
           ev_w_out, od_w_in, od_norm_g, od_w_out, lb_param, ffn_w_up, ffn_conv_w,
           ffn_conv_b, ffn_w_down, ln1_g, ln1_b, ln2_g, ln2_b):
    f32 = np.float32
    nc0, _ = build_mix0_launch()
    maps0 = prep_mix0_inputs(x, ev_w_in, ev_ln_v_g, ev_ln_v_b, ev_w_s, ev_b_s, ev_w_pool, ev_pool_scale,
                             ev_w_out, ln1_g, ln1_b)
    r0 = _run(nc0, maps0)
    x1T = [r0[c]["x1T"] for c in range(NCORES)]
    ncf, _ = build_ffn_launch()

    def ffn(l, xTs):
        cwb, gb = prep_ffn_params(l, ffn_conv_w, ffn_conv_b, ln2_g, ln2_b)
        wu = np.ascontiguousarray(ffn_w_up[l], f32)
        wd = np.ascontiguousarray(ffn_w_down[l], f32)
        maps = [{"xT": np.ascontiguousarray(xTs[c], f32), "w_up": wu, "w_down": wd, "cwb": cwb, "ln2gb": gb}
                for c in range(NCORES)]
        r = _run(ncf, maps)
        return [r[c]["yT"] for c in range(NCORES)]

    x2T = ffn(0, x1T)
    ncp, _ = build_hgrn_pre_launch()
    w_in_r = regroup_w_in(od_w_in[0])
    hc = hgrn_const_inputs(lb_param)
    mapsp = []
    for c in range(NCORES):
        m = {"xT": np.ascontiguousarray(x2T[c], f32), "w_in": w_in_r}
        m.update(hc)
        mapsp.append(m)
    rp = _run(ncp, mapsp)
    ncm, _ = build_hgrn_main_launch()
    gn = _pm(od_norm_g[0], NH)
    gb1 = np.ascontiguousarray(np.stack([_pm(ln1_g[1], 16), _pm(ln1_b[1], 16)], axis=-1).reshape(128, 32))
    w_out1 = np.ascontiguousarray(od_w_out[0], f32)
    mapsm = []
    for c in range(NCORES):
        b, s = divmod(c, 4)
        up = np.zeros((3, NH, 128, 128), f32)
        dp = np.zeros((128, 3, NH), f32)
        for j in range(s):
            pos = 3 - s + j
            up[pos] = rp[b * 4 + j]["U"]
            dp[:, pos, :] = rp[b * 4 + j]["Dd"]
        m = {"xT": np.ascontiguousarray(x2T[c], f32), "w_in": w_in_r, "w_out": w_out1, "Uprev": up,
             "Dprev": np.ascontiguousarray(dp.reshape(128, 3 * NH)), "gn": gn, "ln1gb": gb1}
        m.update(hc)
        mapsm.append(m)
    rm = _run(ncm, mapsm)
    x1bT = []
    for c in range(NCORES):
        b, s = divmod(c, 4)
        xt = np.zeros((D, T + 2), f32)
        xt[:, 2:] = rm[c]["x1T"]
        if s > 0:
            xt[:, 0:2] = rm[c - 1]["x1T"][:, T - 2:T]
        x1bT.append(xt)
    outT = ffn(1, x1bT)
    out = np.zeros((2, 4 * T, D), f32)
    for c in range(NCORES):
        b, s = divmod(c, 4)
        out[b, s * T:(s + 1) * T] = outT[c].T
    return out


R0 = 0
R0_SZ = NCH * (T + 2) * 4
R1 = R0 + R0_SZ
R1_SZ = NCH * (T + 2) * 2
R2 = R1 + R1_SZ
R2_SZ = 66560
AR_BYTES = R2 + R2_SZ
SEQ_GROUPS = [[0, 1, 2, 3], [4, 5, 6, 7]]


def build_fused(use_cc=True):
    nc = bass.Bass("TRN2", target_bir_lowering=False)

    def din(name, shape):
        return nc.dram_tensor(name, shape, F32, kind="ExternalInput").ap()
    x0T = din("x0T", [D, TH])
    ev_w_in = din("ev_w_in", [D, 3072])
    ev_w_out = din("ev_w_out", [D, D])
    wsT_d = din("wsT", [128, 8 * 128])
    mask_d = din("maskA", [128, 128])
    bsT_d = din("bsT", [128, 8 * 128])
    lnv_d = din("lnv", [128, 2 * 1024])
    wp_d = din("w_pool", [4 * 256, 256])
    psc_d = din("pscale", [128, 8])
    rc_d = din("rcnt", [128, 64])
    flag_d = din("flag", [128, 1])
    oh_d = din("oh", [128, 8])
    ln1gb_d = din("ln1gb", [128, 64])
    ln2gb_d = din("ln2gb", [128, 64])
    cwb_d = din("cwb", [128, 2 * 88 * 4])
    w_up = [din("w_up%d" % l, [D, 2 * DFF]) for l in range(2)]
    w_down = [din("w_down%d" % l, [DFF, D]) for l in range(2)]
    od_w_in = din("od_w_in", [D, 4 * D])
    od_w_out = din("od_w_out", [D, D])
    lbp_d = din("lbp", [128, 2 * NH])
    mask2_d = din("mask2", [128, 128])
    ident_d = din("ident", [128, 128])
    gn_d = din("gn", [128, NH])
    outT = nc.dram_tensor("outT", [D, T], F32, kind="ExternalOutput").ap()
    xsp = nc.dram_tensor("xsp", [D, T], F32).ap()
    ccin = [nc.dram_tensor("ccin%d" % g, [4 * 4 * 128, 129], F32) for g in range(4)]
    ccout = [nc.dram_tensor("ccout%d" % g, [4 * 4 * 128, 129], F32) for g in range(4)]
    cch_in = nc.dram_tensor("cch_in", [4 * 128, 32], F32)
    cch_out = nc.dram_tensor("cch_out", [4 * 128, 32], F32)

    S = Sched()
    with ExitStack() as es:
        AR = es.enter_context(nc.sbuf_tensor("AR", [128, AR_BYTES // 4], F32))

        def cv(off, shape, dt):
            n = 1
            for k in shape:
                n *= k
            v = _carve(AR, off, n, dt)
            if len(shape) == 2:
                v = v.rearrange("p (a b) -> p a b", a=shape[0], b=shape[1])
            return v

        def sb(name, shape, dt=F32):
            return es.enter_context(nc.sbuf_tensor(name, shape, dt))
        psc = sb("psc_s", [128, 8]); rc = sb("rc_s", [128, 4, 16]); flag = sb("flag_s", [128, 1])
        oh = sb("oh_s", [128, 8]); ln1gb = sb("ln1gb_s", [128, 2, NCH, 2]); ln2gb = sb("ln2gb_s", [128, 2, NCH, 2])
        cwb = sb("cwb_s", [128, 2, 88, 4]); ones = sb("ones", [128, 128], BF16); eps = sb("eps", [128, 1])
        st = sb("st", [128, 8]); small = sb("small", [128, 32]); gn = sb("gn_s", [128, NH])
        Dd = sb("Dd_s", [128, NH]); tiny = sb("tiny", [128, 8])
        hstg = sb("hstg", [128, 4, 32]); hld = sb("hld", [128, 4, 32]); hsum = sb("hsum", [128, 32])
        ps = es.enter_context(nc.psum_tensor("ps", [128, 4096], F32))
        psb = ps[:, :].bitcast(BF16)
        ccsem = [es.enter_context(nc.semaphore("ccs%d" % i)) for i in range(5)]
        ws = WeightStream(S, nc, es, 4, 16 * 256)

        wi0 = ev_w_in.rearrange("(kc p) n -> p kc n", p=128)
        wo0 = ev_w_out.rearrange("(kc p) n -> p kc n", p=128)
        u_xb = [ws.plan((16, 256), wi0[:, :, 2048 + i * 256:2048 + (i + 1) * 256]) for i in range(4)]
        u_u = [ws.plan((16, 256), wi0[:, :, i * 256:(i + 1) * 256]) for i in range(4)]
        u_v = [ws.plan((16, 256), wi0[:, :, 1024 + i * 256:1024 + (i + 1) * 256]) for i in range(4)]
        u_o = [ws.plan((16, 256), wo0[:, :, i * 256:(i + 1) * 256]) for i in range(8)]
        plan0 = plan_ffn_weights(ws, w_up[0], w_down[0])
        wi1 = od_w_in.rearrange("(kc p) n -> p kc n", p=128)
        wo1 = od_w_out.rearrange("(kc p) n -> p kc n", p=128)
        u_pre = [ws.plan((16, 256), wi1[:, :, h * 512 + 256:h * 512 + 512]) for h in range(NH)]
        u_main = []
        for h in range(NH):
            fi = ws.plan((16, 256), wi1[:, :, h * 512 + 256:h * 512 + 512])
            qg = ws.plan((16, 256), wi1[:, :, h * 512:h * 512 + 256])
            u_main.append((fi, qg))
        u_o1 = [ws.plan((16, 256), wo1[:, :, (i % 8) * 256:((i % 8) + 1) * 256]) for i in range(16)]
        plan1 = plan_ffn_weights(ws, w_up[1], w_down[1])

        x0b = cv(R0, (NCH, TH), BF16)
        zf = cv(R0, (NCH, T + 2), F32)
        xb = cv(R1, (NCH, T + 2), BF16)
        pp = cv(R1, (8, TH), BF16)
        g1 = [cv(R1 + 18432, (TH,), F32), cv(R1 + 23040, (TH,), F32)]
        xbf = cv(R1 + 27648, (16 + TH,), F32)
        u = cv(R2, (8, TH), BF16)
        vt = cv(R2 + 18432, (9, 1024), BF16)
        g2p = [cv(R2 + 36864, (16 + TH,), F32), cv(R2 + 41536, (16 + TH,), F32)]
        g2 = [t[:, 16:16 + TH] for t in g2p]
        tA, tB = g2p
        wsT = cv(R2 + 46208, (8, 128), BF16)
        mask = cv(R2 + 48256, (128,), F32)
        bsT = cv(R2 + 48768, (8, 128), F32)
        lnv = cv(R2 + 52864, (2, 1024), F32)
        wp = cv(R2 + 61056, (8, 256), BF16)
        wsTf = g1[1][:, 0:1024].rearrange("p (h t) -> p h t", h=8)
        hb = [cv(R0 + 36864 + k * 4608, (TH,), F32) for k in range(2)]

        chp = S.new_chan(total=True)
        chx = S.new_chan(total=True)
        S.dma("sp", wsTf, wsT_d.rearrange("p (h t) -> p h t", h=8), chp)
        S.dma("sp", mask, mask_d, chp)
        S.dma("sp", bsT, bsT_d.rearrange("p (h t) -> p h t", h=8), chp)
        S.dma("sp", lnv, lnv_d.rearrange("p (a c) -> p a c", a=2), chp)
        S.dma("sp", psc[:, :], psc_d, chp)
        S.dma("sp", rc[:, :, :], rc_d.rearrange("p (g j) -> p g j", g=4), chp)
        S.dma("sp", flag[:, :], flag_d, chp)
        S.dma("sp", oh[:, :], oh_d, chp)
        S.dma("sp", ln1gb[:, :, :, :], ln1gb_d.rearrange("p (l c j) -> p l c j", l=2, j=2), chp)
        S.dma("sp", ln2gb[:, :, :, :], ln2gb_d.rearrange("p (l c j) -> p l c j", l=2, j=2), chp)
        S.dma("sp", cwb[:, :, :, :], cwb_d.rearrange("p (l c j) -> p l c j", l=2, j=4), chp)
        S.dma("sp", gn[:, :], gn_d, chp)
        S.dma("pool", wp, wp_d.rearrange("(a p) n -> p a n", p=128), chx)
        for c in range(NCH):
            S.dma("pool", x0b[:, c, :], x0T[c * 128:(c + 1) * 128, :], chx)
        ws.release(-1)
        S.memset("dve", ones[:, :], 1.0)
        S.memset("dve", eps[:, :], LN_EPS)
        S.memset("dve", xbf[:, 0:16], 0.0)
        S.memset("dve", tA[:, 0:16], 0.0)
        S.memset("dve", tB[:, 0:16], 0.0)
        for h in range(8):
            S.tt("dve", wsT[:, h, :], wsTf[:, h, :], mask, ALU.mult)

        GR = (0, 1536)
        TT3 = ((0, 512), (512, 512), (1024, 128))

        def proj_fm(wt, sub, g0):
            for kc in range(NCH):
                for (t0, w) in TT3:
                    S.mm(ps[:, g0 + t0:g0 + t0 + w], wt[:, kc, sub * 128:(sub + 1) * 128], x0b[:, kc, t0:t0 + w],
                         start=(kc == 0), stop=(kc == NCH - 1))
        gi = 0
        for i in range(4):
            wt = ws.get(u_xb[i])
            for sub in range(2):
                c = i * 2 + sub
                g = c // 2
                g0 = GR[gi % 2]
                gi += 1
                proj_fm(wt, sub, g0)
                S.act(xbf[:, 16:16 + TH], ps[:, g0:g0 + TH], AF.Identity)
                src = xbf
                dsts = [tA, tB]
                for k in range(g + 1):
                    sh = 1 << k
                    dst = dsts[k % 2]
                    S.tt("dve", dst[:, 16:16 + TH], src[:, 16:16 + TH], src[:, 16 - sh:16 + TH - sh], ALU.add)
                    src = dst
                win = B_WINDOWS[g]
                S.stt(pp[:, c, :], src[:, 16:16 + TH], 1.0 / win, xbf[:, 16:16 + TH], ALU.mult, ALU.subtract)
                S.tt("dve", small[:, 0:16], src[:, 16 + 128:16 + 144], rc[:, g, :], ALU.mult)
                S.tt("dve", pp[:, c, 128:144], small[:, 0:16], xbf[:, 16 + 128:16 + 144], ALU.subtract)
            ws.release(u_xb[i])
        for i in range(4):
            wt = ws.get(u_u[i])
            for sub in range(2):
                c = i * 2 + sub
                g0 = GR[gi % 2]
                proj_fm(wt, sub, g0)
                S.act(hb[gi % 2], ps[:, g0:g0 + TH], AF.Identity)
                emit_gelu(S, u[:, c, :], hb[gi % 2], g1[gi % 2], g2[gi % 2])
                gi += 1
            ws.release(u_u[i])
        wv = [ws.get(k) for k in u_v]
        for tk in range(9):
            vr = g1[tk % 2]
            for cg in range(4):
                r0 = 2048 + ((tk * 4 + cg) % 4) * 512
                for kc in range(NCH):
                    S.mm(ps[:, r0:r0 + 256], x0b[:, kc, tk * 128:(tk + 1) * 128], wv[cg][:, kc, :],
                         start=(kc == 0), stop=(kc == NCH - 1))
                hv = hb[tk % 2][:, cg * 256:(cg + 1) * 256]
                S.act(hv, ps[:, r0:r0 + 256], AF.Identity)
                emit_gelu(S, vr[:, cg * 256:(cg + 1) * 256], hv,
                          g2[0][:, cg * 256:(cg + 1) * 256], g2[1][:, cg * 256:(cg + 1) * 256])
            sq = g2[0]
            S.add("dve", lambda e, vr=vr: e.reduce_sum(out=st[:, 0:1], in_=vr[:, 0:1024], axis=mybir.AxisListType.X),
                  reads=[vr[:, 0:1024]], writes=[st[:, 0:1]])
            S.act(sq[:, 0:1024], vr[:, 0:1024], AF.Square)
            S.add("dve", lambda e, sq=sq: e.reduce_sum(out=st[:, 1:2], in_=sq[:, 0:1024], axis=mybir.AxisListType.X),
                  reads=[sq[:, 0:1024]], writes=[st[:, 1:2]])
            S.ts("dve", st[:, 2:3], st[:, 0:1], 1.0 / 1024, None, ALU.mult)
            S.tt("dve", st[:, 3:4], st[:, 2:3], st[:, 2:3], ALU.mult)
            S.stt(st[:, 4:5], st[:, 1:2], 1.0 / 1024, st[:, 3:4], ALU.mult, ALU.subtract)
            S.act(st[:, 5:6], st[:, 4:5], AF.Sqrt, bias=eps[:, 0:1], scale=1.0)
            S.add("dve", lambda e: e.reciprocal(out=st[:, 6:7], in_=st[:, 5:6]), reads=[st[:, 5:6]],
                  writes=[st[:, 6:7]])
            S.ts("dve", vr[:, 0:1024], vr[:, 0:1024], st[:, 2:3], st[:, 6:7], ALU.subtract, ALU.mult)
            S.tt("dve", vr[:, 0:1024], vr[:, 0:1024], lnv[:, 0, :], ALU.mult)
            S.tt("dve", vt[:, tk, :], vr[:, 0:1024], lnv[:, 1, :], ALU.add)
        ws.release(u_v[3])
        for tk in range(9):
            for half in range(2):
                r0 = 2048 + half * 512
                for hh in range(4):
                    h = half * 4 + hh
                    S.mm(ps[:, r0 + hh * 128:r0 + (hh + 1) * 128], vt[:, tk, h * 128:(h + 1) * 128], wsT[:, h, :],
                         start=True, stop=True)
                tmp = g2[half][:, 0:512].rearrange("p (h t) -> p h t", h=4)
                S.tt("dve", tmp, ps[:, r0:r0 + 512].rearrange("p (h t) -> p h t", h=4),
                     bsT[:, half * 4:half * 4 + 4, :], ALU.add)
                uu = u[:, half * 4:half * 4 + 4, tk * 128:(tk + 1) * 128]
                S.tt("dve", uu, tmp, uu, ALU.mult)
        for g in range(4):
            for oc in range(2):
                g0 = GR[oc]
                for kc in range(2):
                    for (t0, w) in TT3:
                        S.mm(ps[:, g0 + t0:g0 + t0 + w], wp[:, g * 2 + kc, oc * 128:(oc + 1) * 128],
                             pp[:, g * 2 + kc, t0:t0 + w], start=(kc == 0), stop=(kc == 1))
            for oc in range(2):
                g0 = GR[oc]
                c = g * 2 + oc
                S.act(pp[:, c, :], ps[:, g0:g0 + TH], AF.Identity, scale=psc[:, c:c + 1])
        xs = [g1[0], g1[1]]
        chs = [S.new_chan(), S.new_chan()]
        for i in range(8):
            wt = ws.get(u_o[i])
            for sub in range(2):
                oc = i * 2 + sub
                g0 = GR[oc % 2]
                xst = xs[oc % 2]
                S.dma("sp", xst[:, 0:T + 2], x0T[oc * 128:(oc + 1) * 128, 126:TH], chs[oc % 2])
                for kc in range(NCH):
                    src = u[:, kc, :] if kc < 8 else pp[:, kc - 8, :]
                    lw = wt[:, kc, sub * 128:(sub + 1) * 128]
                    S.mm(ps[:, g0 + 510:g0 + 512], lw, src[:, 126:128], start=(kc == 0), stop=(kc == NCH - 1))
                    S.mm(ps[:, g0 + 512:g0 + 1024], lw, src[:, 128:640], start=(kc == 0), stop=(kc == NCH - 1))
                    S.mm(ps[:, g0 + 1024:g0 + 1536], lw, src[:, 640:1152], start=(kc == 0), stop=(kc == NCH - 1))
                S.stt(zf[:, oc, :], xst[:, 0:T + 2], ALPHA, ps[:, g0 + 510:g0 + 1536], ALU.mult, ALU.add)
            ws.release(u_o[i])
        tm_ln1 = {"eps": eps, "mean": g2[0], "rstd": g2[1],
                  "zb": [cv(R2 + k * 2052, (T + 2,), BF16) for k in range(2)],
                  "zs": [cv(R2 + (2 + k) * 2052, (T + 2,), BF16) for k in range(2)]}
        def ln1_post(c):
            S.ts("dve", zf[:, c, 0:2], zf[:, c, 0:2], flag[:, 0:1], None, ALU.mult)
            S.act(xb[:, c, :], zf[:, c, :], AF.Identity)
        emit_ln(S, zf, 0, T + 2, ln1gb[:, 0, :, :], ones, tm_ln1, ps, [lambda c: zf[:, c, :]], post=ln1_post)

        gq = cv(R2, (12, T), BF16)
        ft = [cv(R2 + 24576 + k * 4096, (T,), F32) for k in range(6)]
        tm_ffn = {"a": ft[0:2], "v": ft[2:4], "s": ft[4:6], "eps": eps, "mean": ft[0], "rstd": ft[1],
                  "zb": [cv(R2 + 49152 + k * 2048, (T,), BF16) for k in range(2)],
                  "zs": [cv(R2 + 53248 + k * 2048, (T,), BF16) for k in range(2)]}
        xb2 = cv(R1, (NCH, T), BF16)
        emit_ffn(S, ws, plan0, zf, xb, cwb[:, 0, :, :], gq, tm_ffn, ps)
        emit_ln(S, zf, 2, T, ln2gb[:, 0, :, :], ones, tm_ffn, ps,
                [lambda c: xb2[:, c, :], lambda c: zf[:, c, 2:T + 2]])
        chsp = [S.new_chan() for _ in range(NCH)]
        for c in range(NCH):
            S.dma("sp", xsp[c * 128:(c + 1) * 128, :], zf[:, c, 2:T + 2], chsp[c])

        def mkset(k):
            o0 = R0 + k * 32768
            d_ = {"A": cv(o0, (T,), F32), "B": cv(o0 + 4096, (T,), F32), "C": cv(o0 + 8192, (T,), F32),
                  "kend": cv(o0 + 12288, (T,), BF16), "kdec": cv(o0 + 14336, (T,), BF16),
                  "qdec": cv(o0 + 16384, (T,), BF16), "ibf": cv(o0 + 18432, (T,), BF16),
                  "sg": cv(o0 + 20480, (T,), BF16), "osq": cv(o0 + 22528, (T,), BF16),
                  "attm": cv(o0 + 24576, (8, 128), BF16), "kt0": cv(o0 + 26624, (8, 128), BF16),
                  "kt1": cv(o0 + 28672, (8, 128), BF16), "vtk": cv(o0 + 30720, (8, 128), BF16),
                  "Sb": cv(R2 + 32768, (NCK, 128), BF16) if k == 0 else cv(R2 + 61472, (NCK, 128), BF16),
                  "dec": sb("dec%d" % k, [128, NCK]), "Sf": [sb("Sf%d_%d" % (k, i), [128, 128]) for i in range(2)],
                  "Pp": [sb("Pp%d_%d" % (k, i), [128, 128]) for i in range(2)],
                  "upst": cv(R2 + 59408, (4, 129), F32) if k == 0 else sb("upst1", [128, 4, 129]),
                  "stg": cv(R2 + 57344, (4, 129), F32),
                  "chu": S.new_chan(), "chst": S.new_chan()}
            return d_
        sets = [mkset(0), mkset(1)]
        y = cv(R2, (NH, T), BF16)
        xs1 = [cv(R2 + 40960, (T,), F32), cv(R2 + 45056, (T,), F32)]
        tm_ln1b = {"eps": eps, "mean": xs1[0], "rstd": xs1[1],
                   "zb": [cv(R2 + 49152 + k * 2048, (T,), BF16) for k in range(2)],
                   "zs": [cv(R2 + 53248 + k * 2048, (T,), BF16) for k in range(2)]}
        chc = S.new_chan(total=True)
        cst = {}
        lbp = sb("lbp_s", [128, 2, NH]); cst["lb"] = sb("lb", [128, NH]); cst["oml"] = sb("oml", [128, NH])
        cst["mask2"] = sb("mask2_s", [128, 128]); identf = sb("identf", [128, 128]); cst["ident"] = sb("ident_s", [128, 128], BF16)
        cst["pm"] = sb("pm", [128, 2])
        cst["rm"] = cv(R2 + 36864, (T,), F32)
        S.dma("sp", lbp[:, :, :], lbp_d.rearrange("p (l h) -> p l h", l=2), chc)
        S.dma("sp", cst["mask2"][:, :], mask2_d, chc)
        S.dma("sp", identf[:, :], ident_d, chc)
        S.copy("dve", cst["ident"][:, :], identf[:, :])
        S.tt("dve", cst["lb"][:, :], lbp[:, 1, :], lbp[:, 0, :], ALU.subtract)
        S.act(cst["lb"][:, :], cst["lb"][:, :], AF.Sigmoid)
        S.ts("dve", cst["oml"][:, :], cst["lb"][:, :], -1.0, 1.0, ALU.mult, ALU.add)
        S.memset("dve", cst["rm"], 1.0)
        S.memset("dve", cst["rm"].rearrange("p (c t) -> p c t", t=CH)[:, :, 0:1], 0.0)
        S.memset("dve", cst["pm"][:, :], 0.0)
        S.memset("dve", cst["pm"][0:64, 0:1], 1.0)
        S.memset("dve", cst["pm"][64:128, 1:2], 1.0)
        oh3 = oh[:, 0:4].rearrange("p (j o) -> p j o", o=1)
        G0, G1, PB5, O0 = 0, 1024, 2560, 3072

        def proj1(wt, blk, g0):
            for kc in range(NCH):
                for t_ in range(2):
                    S.mm(ps[:, g0 + t_ * 512:g0 + (t_ + 1) * 512], wt[:, kc, blk * 128:(blk + 1) * 128],
                         xb2[:, kc, t_ * 512:(t_ + 1) * 512], start=(kc == 0), stop=(kc == NCH - 1))

        def cc_op(idx, src_t, dst_t):
            if use_cc:
                def fn(e):
                    e.collective_compute("AllReduce", ALU.add, replica_groups=SEQ_GROUPS,
                                         ins=[src_t.ap().opt()], outs=[dst_t.ap().opt()]).then_inc(ccsem[idx])
                    return None
                S.add("pool", fn, reads=[src_t.ap()], writes=[])

                def fn2(e):
                    e.wait_ge(ccsem[idx], 1)
                    return e.memset(tiny[:, idx:idx + 1], 0.0)
                return lambda: S.add("pool", fn2, reads=[], writes=[dst_t.ap(), tiny[:, idx:idx + 1]])
            else:
                chq = S.new_chan()
                S.dma("sp", dst_t.ap(), src_t.ap(), chq)
                return lambda: None

        def scan_group(q, g4, st_):
            pb = (2560, 3072, 3584, 2560)[g4]
            for cc in range(4):
                c = g4 * 4 + cc
                j, par = divmod(c, 2)
                S.mm(ps[:, pb + cc * 128:pb + (cc + 1) * 128], (q["kt0"], q["kt1"])[par][:, j, :], q["vtk"][:, j, :],
                     start=True, stop=True)
            for cc in range(4):
                c = g4 * 4 + cc
                cur = st_["cur"]
                S.stt(q["Sf"][1 - cur][:, :], q["Sf"][cur][:, :], q["dec"][:, c:c + 1],
                      ps[:, pb + cc * 128:pb + (cc + 1) * 128], ALU.mult, ALU.add)
                st_["cur"] = 1 - cur
                if st_["sb"] and c + 1 < NCK:
                    S.act(q["Sb"][:, c + 1, :], q["Sf"][1 - cur][:, :], AF.Identity)

        def interleave(bsteps, asteps, after):
            ai = 0
            for bi, bstep in enumerate(bsteps):
                bstep()
                while ai < len(asteps) and after[ai] == bi:
                    asteps[ai]()
                    ai += 1
            while ai < len(asteps):
                asteps[ai]()
                ai += 1

        cc_done = []

        def pre_A(h):
            q = sets[h % 2]

            def a1():
                q["wt"] = ws.get(u_pre[h])
                proj1(q["wt"], 0, G0)

            def a2():
                proj1(q["wt"], 1, G1)
                ws.release(u_pre[h])
                hgrn_gates(S, h, cst, ps[:, G0:G0 + T], q["A"], q["B"], q["C"], q["kend"], q["dec"][:, :])
                S.act(q["ibf"], ps[:, G1:G1 + T], AF.Identity)
                C3 = q["C"].rearrange("p (c t) -> p c t", t=CH)
                S.add("dve", lambda e, C3=C3, h=h: e.reduce_sum(out=Dd[:, h:h + 1], in_=C3[:, :, CH - 1:CH],
                                                               axis=mybir.AxisListType.XY),
                      reads=[q["C"]], writes=[Dd[:, h:h + 1]])
                S.act(Dd[:, h:h + 1], Dd[:, h:h + 1], AF.Exp)
            return [a1, a2]

        def pre_B(h):
            q = sets[h % 2]
            st_ = {"cur": 0, "sb": False}

            def b1():
                hgrn_transposes(S, cst, psb, q["kend"], q["kt0"], q["kt1"])

            def b1b():
                hgrn_transposes(S, cst, psb, q["ibf"], q["vtk"])
                S.memset("dve", q["Sf"][0][:, :], 0.0)

            def bfin():
                fin = st_["cur"]
                for j in range(4):
                    S.ts("dve", q["stg"][:, j, 0:128], q["Sf"][fin][:, :], oh[:, j:j + 1], None, ALU.mult)
                S.ts("dve", q["stg"][:, :, 128:129], oh3, Dd[:, h:h + 1], None, ALU.mult)
                g, hl = divmod(h, 4)
                S.dma("sp", ccin[g].ap().rearrange("(j l d) n -> d j l n", j=4, l=4)[:, :, hl, :], q["stg"][:, :, :],
                      q["chst"])
                if hl == 3:
                    cc_done.append(cc_op(g, ccin[g], ccout[g]))
            return [b1, b1b] + [lambda g4=g4: scan_group(q, g4, st_) for g4 in range(4)] + [bfin]

        for stp in pre_A(0):
            stp()
        for h in range(NH):
            nxt = pre_A(h + 1) if h + 1 < NH else []
            interleave(pre_B(h), nxt, [0, 3])

        def main_A(h):
            q = sets[h % 2]
            g, hl = divmod(h, 4)
            fi, qg = u_main[h]

            def a1():
                if hl == 0:
                    cc_done[g]()
                up = q["upst"]
                S.dma("sp", up[:, :, :], ccout[g].ap().rearrange("(j l d) n -> d j l n", j=4, l=4)[:, :, hl, :], q["chu"])
                Pp_, Sf_ = q["Pp"], q["Sf"]
                S.stt(Pp_[0][:, :], up[:, 0, 0:128], up[:, 1, 128:129], up[:, 1, 0:128], ALU.mult, ALU.add)
                S.stt(Pp_[1][:, :], Pp_[0][:, :], up[:, 2, 128:129], up[:, 2, 0:128], ALU.mult, ALU.add)
                S.ts("dve", Sf_[0][:, :], up[:, 0, 0:128], oh[:, 1:2], None, ALU.mult)
                S.stt(Sf_[0][:, :], Pp_[0][:, :], oh[:, 2:3], Sf_[0][:, :], ALU.mult, ALU.add)
                S.stt(Sf_[0][:, :], Pp_[1][:, :], oh[:, 3:4], Sf_[0][:, :], ALU.mult, ALU.add)
                q["wt"] = ws.get(fi)
                proj1(q["wt"], 0, G0)

            def a2():
                proj1(q["wt"], 1, G1)
                ws.release(fi)
                hgrn_gates(S, h, cst, ps[:, G0:G0 + T], q["A"], q["B"], q["C"], q["kend"], q["dec"][:, :],
                           kdec_bf=q["kdec"], eC=q["A"])
                S.act(q["ibf"], ps[:, G1:G1 + T], AF.Identity)

            def a3():
                q["wt"] = ws.get(qg)
                proj1(q["wt"], 0, G0)

            def a4():
                proj1(q["wt"], 1, G1)
                ws.release(qg)
                S.act(q["B"], ps[:, G0:G0 + T], AF.Silu)
                S.tt("dve", q["qdec"], q["B"], q["A"], ALU.mult)
                S.act(q["sg"], ps[:, G1:G1 + T], AF.Sigmoid)
            return [a1, a2, a3, a4]

        def main_B(h):
            q = sets[h % 2]
            st_ = {"cur": 0, "sb": True}

            def b1():
                hgrn_transposes(S, cst, psb, q["kend"], q["kt0"], q["kt1"])

            def b1b():
                hgrn_transposes(S, cst, psb, q["ibf"], q["vtk"])
                S.act(q["Sb"][:, 0, :], q["Sf"][0][:, :], AF.Identity)

            def batt(half):
                pb = 3072 + half * 512
                for jj in range(4):
                    j = half * 4 + jj
                    S.mm(ps[:, pb + jj * 128:pb + (jj + 1) * 128], q["kdec"][:, j * 128:(j + 1) * 128],
                         q["qdec"][:, j * 128:(j + 1) * 128], start=True, stop=True)
                for jj in range(4):
                    j = half * 4 + jj
                    S.tt("dve", q["attm"][:, j, :], ps[:, pb + jj * 128:pb + (jj + 1) * 128], cst["mask2"][:, :],
                         ALU.mult)

            def bo():
                for j in range(8):
                    S.mm(ps[:, O0 + j * 128:O0 + (j + 1) * 128], q["vtk"][:, j, :], q["attm"][:, j, :], start=True,
                         stop=False)
                    S.mm(ps[:, O0 + j * 128:O0 + j * 128 + 64], q["Sb"][:, 2 * j, :], q["qdec"][:, j * 128:j * 128 + 64],
                         start=False, stop=False)
                    S.mm(ps[:, O0 + j * 128 + 64:O0 + (j + 1) * 128], q["Sb"][:, 2 * j + 1, :],
                         q["qdec"][:, j * 128 + 64:(j + 1) * 128], start=False, stop=True)
                S.act(q["osq"], ps[:, O0:O0 + T], AF.Square)

            def bnorm():
                for t_ in range(2):
                    sl = slice(t_ * 512, (t_ + 1) * 512)
                    S.mm(ps[:, PB5:PB5 + 512], ones[:, :], q["osq"][:, sl], start=True, stop=True)
                    S.act(q["A"][:, sl], ps[:, PB5:PB5 + 512], AF.Ln, bias=eps[:, 0:1], scale=1.0 / 128)
                S.act(q["A"], q["A"], AF.Exp, scale=-0.5)
                S.stt(q["C"], ps[:, O0:O0 + T], gn[:, h:h + 1], q["A"], ALU.mult, ALU.mult)
                S.tt("dve", y[:, h, :], q["C"], q["sg"], ALU.mult)
            return ([b1, b1b] + [lambda g4=g4: scan_group(q, g4, st_) for g4 in range(4)]
                    + [lambda: batt(0), lambda: batt(1), bo, bnorm])

        for stp in main_A(0):
            stp()
        for h in range(NH):
            nxt = main_A(h + 1) if h + 1 < NH else []
            interleave(main_B(h), nxt, [0, 3, 5, 8])
        chs1 = [S.new_chan() for _ in range(4)]
        xs4 = [cv(R2 + 40960 + k * 2048, (512,), F32) for k in range(4)]
        tm_t = {"eps": eps, "mean": cv(R2 + 49152, (512,), F32), "rstd": cv(R2 + 51200, (512,), F32),
                "zb": [cv(R2 + 53248 + k * 1024, (512,), BF16) for k in range(2)],
                "zs": [cv(R2 + 55296 + k * 1024, (512,), BF16) for k in range(2)]}
        OB = (512, 1024, 1536, 2560, 3072, 3584)
        done_h = None
        cnt = 0
        for tile in (1, 0):
            c0 = 2 + tile * 512
            for i in range(8):
                wt = ws.get(u_o1[(1 - tile) * 8 + i])
                for sub in range(2):
                    oc = i * 2 + sub
                    g0 = OB[cnt % 6]
                    xst = xs4[cnt % 4]
                    S.dma("sp", xst, xsp[oc * 128:(oc + 1) * 128, tile * 512:(tile + 1) * 512], chs1[cnt % 4])
                    for kc in range(NCH):
                        S.mm(ps[:, g0:g0 + 512], wt[:, kc, sub * 128:(sub + 1) * 128],
                             y[:, kc, tile * 512:(tile + 1) * 512], start=(kc == 0), stop=(kc == NCH - 1))
                    S.stt(zf[:, oc, c0:c0 + 512], xst, ALPHA, ps[:, g0:g0 + 512], ALU.mult, ALU.add)
                    cnt += 1
                ws.release(u_o1[(1 - tile) * 8 + i])
            emit_ln(S, zf, c0, 512, ln1gb[:, 1, :, :], ones, tm_t, ps,
                    [lambda c, c0=c0: xb[:, c, c0:c0 + 512], lambda c, c0=c0: zf[:, c, c0:c0 + 512]],
                    ps_off=(0, 2048))
            if tile == 1:
                for j in range(4):
                    S.ts("dve", hstg[:, j, :].rearrange("p (c t) -> p c t", t=2), zf[:, :, T:T + 2], oh[:, j:j + 1],
                         None, ALU.mult)
                chh = S.new_chan()
                S.dma("sp", cch_in.ap().rearrange("(j p) n -> p j n", p=128), hstg[:, :, :], chh)
                done_h = cc_op(4, cch_in, cch_out)
        done_h()
        chh2 = S.new_chan()
        S.dma("sp", hld[:, :, :], cch_out.ap().rearrange("(j p) n -> p j n", p=128), chh2)
        S.ts("dve", hsum[:, :], hld[:, 0, :], oh[:, 4:5], None, ALU.mult)
        for j in range(1, 4):
            S.stt(hsum[:, :], hld[:, j, :], oh[:, 4 + j:5 + j], hsum[:, :], ALU.mult, ALU.add)
        S.act(xb[:, :, 0:2], hsum[:, :].rearrange("p (c t) -> p c t", t=2), AF.Identity)
        emit_ffn(S, ws, plan1, zf, xb, cwb[:, 1, :, :], gq, tm_ffn, ps)
        emit_ln(S, zf, 2, T, ln2gb[:, 1, :, :], ones, tm_ffn, ps, [lambda c: zf[:, c, 2:T + 2]])
        cho = [S.new_chan() for _ in range(4)]
        for c in range(NCH):
            S.dma("sp", outT[c * 128:(c + 1) * 128, :], zf[:, c, 2:T + 2], cho[c % 4])
        S.emit(nc, es, final_chans=cho)
    return nc, S


def fused_inputs(inp):
    f32 = np.float32
    maps = prep_mix0_inputs(inp["x"], inp["ev_w_in"], inp["ev_ln_v_g"], inp["ev_ln_v_b"], inp["ev_w_s"],
                            inp["ev_b_s"], inp["ev_w_pool"], inp["ev_pool_scale"], inp["ev_w_out"],
                            inp["ln1_g"], inp["ln1_b"])
    ln1gb = np.stack([np.stack([_pm(inp["ln1_g"][l], 16), _pm(inp["ln1_b"][l], 16)], axis=-1) for l in range(2)], axis=1)
    ln2gb = np.stack([np.stack([_pm(inp["ln2_g"][l], 16), _pm(inp["ln2_b"][l], 16)], axis=-1) for l in range(2)], axis=1)
    cwbs = []
    for l in range(2):
        cw = np.asarray(inp["ffn_conv_w"][l], f32)
        cb = np.asarray(inp["ffn_conv_b"][l], f32)
        cwbs.append(np.stack([_pm(cw[0], 88), _pm(cw[1], 88), _pm(cw[2], 88), _pm(cb, 88)], axis=-1))
    cwb = np.stack(cwbs, axis=1)
    common = {
        "ln1gb": np.ascontiguousarray(ln1gb.reshape(128, 64)), "ln2gb": np.ascontiguousarray(ln2gb.reshape(128, 64)),
        "cwb": np.ascontiguousarray(cwb.reshape(128, 2 * 88 * 4)),
        "w_up0": np.ascontiguousarray(inp["ffn_w_up"][0], f32), "w_up1": np.ascontiguousarray(inp["ffn_w_up"][1], f32),
        "w_down0": np.ascontiguousarray(inp["ffn_w_down"][0], f32),
        "w_down1": np.ascontiguousarray(inp["ffn_w_down"][1], f32),
        "od_w_in": regroup_w_in(inp["od_w_in"][0]), "od_w_out": np.ascontiguousarray(inp["od_w_out"][0], f32),
        "gn": _pm(inp["od_norm_g"][0], NH)}
    common.update(hgrn_const_inputs(inp["lb_param"]))
    out = []
    for c in range(NCORES):
        b, s = divmod(c, 4)
        m0 = maps[c]
        m = dict(common)
        for k in ("x0T", "wsT", "maskA", "bsT", "lnv", "w_pool", "pscale", "rcnt", "flag"):
            m[k] = m0[k]
        m["ev_w_in"] = m0["w_in"]
        m["ev_w_out"] = m0["w_out"]
        oh = np.zeros((128, 8), f32)
        oh[:, s] = 1.0
        if s > 0:
            oh[:, 4 + s - 1] = 1.0
        m["oh"] = oh
        out.append(m)
    return out


def kernel(**inputs):
    nc, _ = build_fused(use_cc=True)
    maps = fused_inputs(inputs)
    res = run_bass_kernel_spmd(nc, maps, core_ids=list(range(NCORES))).results
    out = np.zeros((2, 4 * T, D), np.float32)
    for c in range(NCORES):
        b, s = divmod(c, 4)
        out[b, s * T:(s + 1) * T] = res[c]["outT"].T
    return out
```

```python
import numpy as np
from contextlib import ExitStack
import concourse.bass as bass
import concourse.mybir as mybir
from concourse.bass_utils import run_bass_kernel_spmd

F32 = mybir.dt.float32
BF16 = mybir.dt.bfloat16
AF = mybir.ActivationFunctionType
ALU = mybir.AluOpType

D = 2048
NCH = 16
T = 1024
NCORES = 8
DFF = 5632
NFF = 44
ALPHA = 4.0 ** 0.25
LN_EPS = 1e-5
ENGS = ("pe", "act", "dve", "pool", "sp")
_DT_SIZE = {F32: 4, BF16: 2}


def _dsize(dt):
    return _DT_SIZE.get(dt, 4)


class _Op:
    __slots__ = ("eng", "idx", "fn", "deps", "chan", "chan_val", "signal", "val")


class Sched:
    def __init__(self):
        self.ops = {e: [] for e in ENGS}
        self.track = {}
        self.chan_cnt = []
        self.chan_total = []

    @staticmethod
    def _rng(ap):
        t = ap.tensor
        name = t.name
        sp = str(ap.space) if hasattr(ap, "space") else ""
        pat = ap.ap
        esz = _dsize(ap.dtype)
        if "DRAM" in sp.upper() or "Dram" in type(t).__name__ or "DRam" in type(t).__name__:
            ext = 1
            for (st, cnt) in pat:
                ext += abs(st) * (cnt - 1)
            return name, ap.offset * esz, (ap.offset + ext) * esz
        pstride = pat[0][0]
        lo = ap.offset % pstride if pstride > 0 else ap.offset
        ext = 1
        for (st, cnt) in pat[1:]:
            ext += abs(st) * (cnt - 1)
        return name, lo * esz, (lo + ext) * esz

    def _touch(self, name, lo, hi, op, is_write, deps):
        segs = self.track.setdefault(name, [])
        new = []
        covered = []
        for s in segs:
            slo, shi, w, rs = s
            if shi <= lo or slo >= hi:
                new.append(s)
                continue
            if slo < lo:
                new.append([slo, lo, w, list(rs)])
            if shi > hi:
                new.append([hi, shi, w, list(rs)])
            olo, ohi = max(slo, lo), min(shi, hi)
            if w is not None:
                deps.add(w)
            if is_write:
                for r in rs:
                    deps.add(r)
            else:
                covered.append([olo, ohi, w, rs + [op]])
        if is_write:
            new.append([lo, hi, op, []])
        else:
            covered.sort(key=lambda s: s[0])
            cur = lo
            for c in covered:
                if c[0] > cur:
                    new.append([cur, c[0], None, [op]])
                new.append(c)
                cur = c[1]
            if cur < hi:
                new.append([cur, hi, None, [op]])
        self.track[name] = new

    def add(self, eng, fn, reads=(), writes=(), chan=None):
        o = _Op()
        o.eng = eng
        o.fn = fn
        o.chan = chan
        o.signal = False
        o.val = None
        o.chan_val = None
        deps = set()
        for ap in reads:
            if ap is None or isinstance(ap, (int, float)):
                continue
            n, lo, hi = self._rng(ap)
            self._touch(n, lo, hi, o, False, deps)
        for ap in writes:
            n, lo, hi = self._rng(ap)
            if eng == "pe":
                lo = (lo // 2048) * 2048
                hi = ((hi + 2047) // 2048) * 2048
            self._touch(n, lo, hi, o, True, deps)
        deps.discard(o)
        o.deps = deps
        if chan is not None:
            self.chan_cnt[chan] += 1
            o.chan_val = 16 * self.chan_cnt[chan]
        o.idx = len(self.ops[eng])
        self.ops[eng].append(o)
        return o

    def new_chan(self, total=False):
        self.chan_cnt.append(0)
        self.chan_total.append(total)
        return len(self.chan_cnt) - 1

    def emit(self, nc, es, final_chans=()):
        for e in ENGS:
            for o in self.ops[e]:
                for d in o.deps:
                    if d.chan is None:
                        d.signal = True
        for e in ENGS:
            c = 0
            for o in self.ops[e]:
                if o.chan is None and o.signal:
                    c += 1
                    o.val = c
        esem = {e: es.enter_context(nc.semaphore("s_" + e)) for e in ENGS}
        csem = [es.enter_context(nc.semaphore("c_%d" % i)) for i in range(len(self.chan_cnt))]
        block = es.enter_context(nc.Block())
        nwaits = {e: 0 for e in ENGS}

        def run(engname, eobj):
            seen = {}
            for o in self.ops[engname]:
                need = {}
                for d in o.deps:
                    if d.chan is not None:
                        key = ("c", d.chan)
                        v = 16 * self.chan_cnt[d.chan] if self.chan_total[d.chan] else d.chan_val
                    else:
                        if d.eng == engname and engname == "pe":
                            continue
                        key = ("e", d.eng)
                        v = d.val
                    if v > need.get(key, 0):
                        need[key] = v
                for key, v in need.items():
                    if v <= seen.get(key, 0):
                        continue
                    seen[key] = v
                    sem = csem[key[1]] if key[0] == "c" else esem[key[1]]
                    eobj.wait_ge(sem, v)
                    nwaits[engname] += 1
                inst = o.fn(eobj)
                if o.chan is not None:
                    inst.then_inc(csem[o.chan], 16)
                elif o.signal:
                    assert inst is not None
                    inst.then_inc(esem[engname], 1)
            if engname == "sp":
                for ch in final_chans:
                    if self.chan_cnt[ch] > 0:
                        eobj.wait_ge(csem[ch], 16 * self.chan_cnt[ch])

        @block.tensor
        def _(e):
            run("pe", e)

        @block.scalar
        def _(e):
            run("act", e)

        @block.vector
        def _(e):
            run("dve", e)

        @block.gpsimd
        def _(e):
            run("pool", e)

        @block.sync
        def _(e):
            run("sp", e)

        self.nwaits = nwaits

    def mm(self, out, lhsT, rhs, start=True, stop=True):
        return self.add("pe", lambda e: e.matmul(out, lhsT=lhsT, rhs=rhs, start=start, stop=stop),
                        reads=[lhsT, rhs], writes=[out])

    def transpose(self, out, in_, ident):
        return self.add("pe", lambda e: e.transpose(out, in_, ident), reads=[in_, ident], writes=[out])

    def act(self, out, in_, func, bias=None, scale=None):
        kw = {}
        rd = [in_]
        if bias is not None:
            kw["bias"] = bias
            rd.append(bias)
        if scale is not None:
            kw["scale"] = scale
            rd.append(scale)
        return self.add("act", lambda e: e.activation(out=out, in_=in_, func=func, **kw), reads=rd, writes=[out])

    def tt(self, eng, out, in0, in1, op):
        return self.add(eng, lambda e: e.tensor_tensor(out=out, in0=in0, in1=in1, op=op),
                        reads=[in0, in1], writes=[out])

    def ts(self, eng, out, in0, s1, s2, op0, op1=None):
        if op1 is None:
            return self.add(eng, lambda e: e.tensor_scalar(out=out, in0=in0, scalar1=s1, scalar2=None, op0=op0),
                            reads=[in0, s1], writes=[out])
        return self.add(eng, lambda e: e.tensor_scalar(out=out, in0=in0, scalar1=s1, scalar2=s2, op0=op0, op1=op1),
                        reads=[in0, s1, s2], writes=[out])

    def stt(self, out, in0, scalar, in1, op0, op1):
        return self.add("dve", lambda e: e.scalar_tensor_tensor(out=out, in0=in0, scalar=scalar, in1=in1,
                                                                op0=op0, op1=op1),
                        reads=[in0, scalar, in1], writes=[out])

    def copy(self, eng, out, in_):
        if eng == "act":
            return self.add("act", lambda e: e.copy(out=out, in_=in_), reads=[in_], writes=[out])
        return self.add(eng, lambda e: e.tensor_copy(out=out, in_=in_), reads=[in_], writes=[out])

    def memset(self, eng, ap, val):
        return self.add(eng, lambda e: e.memset(ap, val), writes=[ap])

    def dma(self, eng, out, in_, chan):
        return self.add(eng, lambda e: e.dma_start(out=out, in_=in_), reads=[in_], writes=[out], chan=chan)


class WeightStream:
    def __init__(self, S, nc, es, nslots, free_elems, name="wslot"):
        self.S = S
        self.slots = [es.enter_context(nc.sbuf_tensor("%s%d" % (name, i), [128, free_elems], BF16))
                      for i in range(nslots)]
        self.chans = [S.new_chan() for _ in range(nslots)]
        self.uses = []
        self.loaded = 0
        self.released = -1
        self.n = nslots

    def plan(self, shape, src):
        self.uses.append((shape, src))
        return len(self.uses) - 1

    def view(self, k):
        shape, _ = self.uses[k]
        sl = self.slots[k % self.n]
        n = 1
        for s in shape:
            n *= s
        v = sl[:, 0:n]
        if len(shape) == 2:
            return v.rearrange("p (a b) -> p a b", a=shape[0], b=shape[1])
        return v

    def _load_upto(self, k):
        while self.loaded < len(self.uses) and self.loaded <= k:
            j = self.loaded
            _, src = self.uses[j]
            self.S.dma("pool", self.view(j), src, self.chans[j % self.n])
            self.loaded += 1

    def get(self, k):
        assert k <= self.released + self.n, (k, self.released)
        self._load_upto(k)
        return self.view(k)

    def release(self, k):
        self.released = max(self.released, k)
        self._load_upto(self.released + self.n)


FF_QUARTERS = (12, 10, 12, 10)


def plan_ffn_weights(ws, w_up, w_down, split_last=False):
    plan = []
    base = 0
    wu = w_up.rearrange("(kc p) n -> p kc n", p=128)
    for q, nq in enumerate(FF_QUARTERS):
        ups = []
        for j in range(0, nq, 2):
            ca = base + j
            ua = ws.plan((16, 256), wu[:, :, ca * 128:ca * 128 + 256])
            uv = ws.plan((16, 256), wu[:, :, (NFF + ca) * 128:(NFF + ca) * 128 + 256])
            ups.append((ca, ua, uv))
        downs = []
        wd = w_down[base * 128:(base + nq) * 128, :].rearrange("(j p) n -> p j n", p=128)
        reps = 2 if (split_last and q == len(FF_QUARTERS) - 1) else 1
        for rep in range(reps):
            for op_ in range(8):
                downs.append((op_, ws.plan((nq, 256), wd[:, :, op_ * 256:(op_ + 1) * 256])))
        plan.append((base, nq, ups, downs))
        base += nq
    return plan


def emit_ffn(S, ws, plan, xf, xb, cwb, gq, tmps, ps, ln_cb=None):
    G = (0, 1536)
    gi = 0
    for (base, nq, ups, downs) in plan:
        for (ca, ua, uv) in ups:
            wa = ws.get(ua)
            wv = ws.get(uv)
            for sub in range(2):
                c_a = ca + sub
                c_v = NFF + ca + sub
                j = c_a - base
                tm = {}
                for which, (wt, cc) in enumerate(((wa, c_a), (wv, c_v))):
                    g0 = G[which]
                    for kc in range(NCH):
                        lw = wt[:, kc, sub * 128:(sub + 1) * 128]
                        S.mm(ps[:, g0 + 510:g0 + 512], lw, xb[:, kc, 0:2], start=(kc == 0), stop=(kc == NCH - 1))
                        S.mm(ps[:, g0 + 512:g0 + 1024], lw, xb[:, kc, 2:514], start=(kc == 0), stop=(kc == NCH - 1))
                        S.mm(ps[:, g0 + 1024:g0 + 1536], lw, xb[:, kc, 514:1026], start=(kc == 0),
                             stop=(kc == NCH - 1))
                    tmp = tmps["a" if which == 0 else "v"][gi % 2]
                    tm[which] = tmp
                    S.act(tmp[:, :], ps[:, g0 + 512:g0 + 1536], AF.Identity, bias=cwb[:, cc, 3:4],
                          scale=cwb[:, cc, 2:3])
                    S.stt(tmp[:, :], ps[:, g0 + 511:g0 + 1535], cwb[:, cc, 1:2], tmp[:, :], ALU.mult, ALU.add)
                    S.stt(tmp[:, :], ps[:, g0 + 510:g0 + 1534], cwb[:, cc, 0:1], tmp[:, :], ALU.mult, ALU.add)
                sa = tmps["s"][gi % 2]
                S.act(sa[:, :], tm[0][:, :], AF.Silu)
                S.tt("dve", gq[:, j, :], sa[:, :], tm[1][:, :], ALU.mult)
                gi += 1
            ws.release(uv)
        def down_tile(wd, oc, sub, tt_, bank):
            pr = ps[:, 3072 + bank * 512:3072 + (bank + 1) * 512]
            for j in range(nq):
                S.mm(pr, wd[:, j, sub * 128:(sub + 1) * 128], gq[:, j, tt_ * 512:(tt_ + 1) * 512],
                     start=(j == 0), stop=(j == nq - 1))
            dst = xf[:, oc, 2 + tt_ * 512:2 + (tt_ + 1) * 512]
            if base == 0:
                S.stt(dst, dst, ALPHA, pr, ALU.mult, ALU.add)
            else:
                S.tt("dve", dst, dst, pr, ALU.add)
        if len(downs) == 8:
            for (op_, ud) in downs:
                wd = ws.get(ud)
                for sub in range(2):
                    for tt_ in range(2):
                        down_tile(wd, op_ * 2 + sub, sub, tt_, tt_)
                ws.release(ud)
        else:
            for tt_ in range(2):
                for (op_, ud) in downs[tt_ * 8:(tt_ + 1) * 8]:
                    wd = ws.get(ud)
                    for sub in range(2):
                        down_tile(wd, op_ * 2 + sub, sub, tt_, sub)
                    ws.release(ud)
                ln_cb(tt_)


def emit_ln(S, zf, c0, n, gb, ones, tmps, ps, outs, post=None, ps_off=(0, 2048)):
    nt = (n + 511) // 512
    zb = tmps["zb"]
    zs = tmps["zs"]
    for c in range(NCH):
        b0 = zb[c % 2]
        s0 = zs[c % 2]
        S.act(b0[:, 0:n], zf[:, c, c0:c0 + n], AF.Identity)
        S.act(s0[:, 0:n], zf[:, c, c0:c0 + n], AF.Square)
        for t_ in range(nt):
            w = min(512, n - t_ * 512)
            S.mm(ps[:, ps_off[0] + t_ * 512:ps_off[0] + t_ * 512 + w], ones[:, :], b0[:, t_ * 512:t_ * 512 + w],
                 start=(c == 0), stop=(c == NCH - 1))
            S.mm(ps[:, ps_off[1] + t_ * 512:ps_off[1] + t_ * 512 + w], ones[:, :], s0[:, t_ * 512:t_ * 512 + w],
                 start=(c == 0), stop=(c == NCH - 1))
    mean = tmps["mean"]
    rstd = tmps["rstd"]
    S.ts("dve", mean[:, 0:n], ps[:, ps_off[0]:ps_off[0] + n], 1.0 / D, None, ALU.mult)
    S.tt("dve", rstd[:, 0:n], mean[:, 0:n], mean[:, 0:n], ALU.mult)
    S.stt(rstd[:, 0:n], ps[:, ps_off[1]:ps_off[1] + n], 1.0 / D, rstd[:, 0:n], ALU.mult, ALU.subtract)
    S.act(rstd[:, 0:n], rstd[:, 0:n], AF.Sqrt, bias=tmps["eps"][:, 0:1], scale=1.0)
    S.add("dve", lambda e: e.reciprocal(out=rstd[:, 0:n], in_=rstd[:, 0:n]), reads=[rstd[:, 0:n]],
          writes=[rstd[:, 0:n]])
    for c in range(NCH):
        zc = zf[:, c, c0:c0 + n]
        S.tt("dve", zc, zc, mean[:, 0:n], ALU.subtract)
        S.tt("dve", zc, zc, rstd[:, 0:n], ALU.mult)
        for i, dst in enumerate(outs):
            S.act(dst(c), zc, AF.Identity, bias=gb[:, c, 1:2], scale=gb[:, c, 0:1])
        if post is not None:
            post(c)


def build_ffn_launch():
    nc = bass.Bass("TRN2", target_bir_lowering=False)
    xT = nc.dram_tensor("xT", [D, T + 2], F32, kind="ExternalInput").ap()
    w_up = nc.dram_tensor("w_up", [D, 2 * DFF], F32, kind="ExternalInput").ap()
    w_down = nc.dram_tensor("w_down", [DFF, D], F32, kind="ExternalInput").ap()
    cwb_d = nc.dram_tensor("cwb", [128, 88 * 4], F32, kind="ExternalInput").ap()
    gb_d = nc.dram_tensor("ln2gb", [128, 32], F32, kind="ExternalInput").ap()
    yT = nc.dram_tensor("yT", [D, T], F32, kind="ExternalOutput").ap()
    S = Sched()
    with ExitStack() as es:
        xf = es.enter_context(nc.sbuf_tensor("xf", [128, NCH, T + 2], F32))
        xb = es.enter_context(nc.sbuf_tensor("xb", [128, NCH, T + 2], BF16))
        cwb = es.enter_context(nc.sbuf_tensor("cwb_s", [128, 88, 4], F32))
        gb = es.enter_context(nc.sbuf_tensor("gb_s", [128, NCH, 2], F32))
        gq = es.enter_context(nc.sbuf_tensor("gq", [128, 12, T], BF16))
        ones = es.enter_context(nc.sbuf_tensor("ones", [128, 128], BF16))
        eps = es.enter_context(nc.sbuf_tensor("eps", [128, 1], F32))
        tmps = {
            "a": [es.enter_context(nc.sbuf_tensor("ta%d" % i, [128, T], F32)) for i in range(2)],
            "v": [es.enter_context(nc.sbuf_tensor("tv%d" % i, [128, T], F32)) for i in range(2)],
            "s": [es.enter_context(nc.sbuf_tensor("tsl%d" % i, [128, T], F32)) for i in range(2)],
            "eps": eps,
        }
        tmps["zb"] = [es.enter_context(nc.sbuf_tensor("zb%d" % i, [128, T], BF16)) for i in range(2)]
        tmps["zs"] = [es.enter_context(nc.sbuf_tensor("zs%d" % i, [128, T], BF16)) for i in range(2)]
        tmps["mean"] = tmps["a"][0]
        tmps["rstd"] = tmps["a"][1]
        ps = es.enter_context(nc.psum_tensor("ps", [128, 4096], F32))
        ws = WeightStream(S, nc, es, 4, 16 * 256)
        plan = plan_ffn_weights(ws, w_up, w_down)

        ch_in = S.new_chan(total=True)
        ch_p = S.new_chan(total=True)
        ch_out = [S.new_chan() for _ in range(4)]
        S.dma("sp", cwb[:, :, :], cwb_d.rearrange("p (c j) -> p c j", j=4), ch_p)
        S.dma("sp", gb[:, :, :], gb_d.rearrange("p (c j) -> p c j", j=2), ch_p)
        S.memset("dve", ones[:, :], 1.0)
        S.memset("dve", eps[:, :], LN_EPS)
        for c in range(NCH):
            S.dma("sp", xf[:, c, :], xT[c * 128:(c + 1) * 128, :], ch_in)
        for c in range(NCH):
            S.act(xb[:, c, :], xf[:, c, :], AF.Identity)
        emit_ffn(S, ws, plan, xf, xb, cwb, gq, tmps, ps)
        emit_ln(S, xf, 2, T, gb, ones, tmps, ps, [lambda c: xf[:, c, 2:T + 2]])
        for c in range(NCH):
            S.dma("sp", yT[c * 128:(c + 1) * 128, :], xf[:, c, 2:T + 2], ch_out[c % 4])
        S.emit(nc, es, final_chans=ch_out)
    return nc, S


def _pm(v, nch):
    return np.ascontiguousarray(np.asarray(v, np.float32).reshape(nch, 128).T)


def prep_ffn_params(l, ffn_conv_w, ffn_conv_b, ln2_g, ln2_b):
    cw = np.asarray(ffn_conv_w[l], np.float32)
    cb = np.asarray(ffn_conv_b[l], np.float32)
    cwb = np.stack([_pm(cw[0], 88), _pm(cw[1], 88), _pm(cw[2], 88), _pm(cb, 88)], axis=-1)
    gb = np.stack([_pm(ln2_g[l], 16), _pm(ln2_b[l], 16)], axis=-1)
    return np.ascontiguousarray(cwb.reshape(128, 88 * 4)), np.ascontiguousarray(gb.reshape(128, 32))


def run_ffn_launch(x1, l, ffn_w_up, ffn_conv_w, ffn_conv_b, ffn_w_down, ln2_g, ln2_b):
    nc, S = build_ffn_launch()
    cwb, gb = prep_ffn_params(l, ffn_conv_w, ffn_conv_b, ln2_g, ln2_b)
    wu = np.ascontiguousarray(ffn_w_up[l], np.float32)
    wd = np.ascontiguousarray(ffn_w_down[l], np.float32)
    x1 = np.asarray(x1, np.float32)
    in_maps = []
    for c in range(NCORES):
        b, s = divmod(c, 4)
        t0 = s * T
        xt = np.zeros((D, T + 2), np.float32)
        xt[:, 2:] = x1[b, t0:t0 + T].T
        if s > 0:
            xt[:, 0:2] = x1[b, t0 - 2:t0].T
        in_maps.append({"xT": xt, "w_up": wu, "w_down": wd, "cwb": cwb, "ln2gb": gb})
    res = run_bass_kernel_spmd(nc, in_maps, core_ids=list(range(NCORES)))
    out = np.zeros((2, 4096, D), np.float32)
    for c in range(NCORES):
        b, s = divmod(c, 4)
        out[b, s * T:(s + 1) * T] = res.results[c]["yT"].T
    return out


TH = T + 128
B_WINDOWS = (2, 4, 8, 16)
GELU_C = 0.044715
GELU_S = 2.0 * 0.7978845608028654


def emit_gelu(S, dst, src_ps, t1, t2):
    S.act(t1, src_ps, AF.Square)
    S.ts("dve", t1, t1, GELU_C, 1.0, ALU.mult, ALU.add)
    S.tt("dve", t1, t1, src_ps, ALU.mult)
    S.act(t2, t1, AF.Sigmoid, scale=GELU_S)
    S.tt("dve", dst, t2, src_ps, ALU.mult)


def build_mix0_launch():
    nc = bass.Bass("TRN2", target_bir_lowering=False)
    x0T = nc.dram_tensor("x0T", [D, TH], F32, kind="ExternalInput").ap()
    w_in = nc.dram_tensor("w_in", [D, 3072], F32, kind="ExternalInput").ap()
    w_out = nc.dram_tensor("w_out", [D, D], F32, kind="ExternalInput").ap()
    wsT_d = nc.dram_tensor("wsT", [128, 8 * 128], F32, kind="ExternalInput").ap()
    mask_d = nc.dram_tensor("maskA", [128, 128], F32, kind="ExternalInput").ap()
    bsT_d = nc.dram_tensor("bsT", [128, 8 * 128], F32, kind="ExternalInput").ap()
    lnv_d = nc.dram_tensor("lnv", [128, 2 * 1024], F32, kind="ExternalInput").ap()
    wp_d = nc.dram_tensor("w_pool", [4 * 256, 256], F32, kind="ExternalInput").ap()
    psc_d = nc.dram_tensor("pscale", [128, 8], F32, kind="ExternalInput").ap()
    gb_d = nc.dram_tensor("ln1gb", [128, 32], F32, kind="ExternalInput").ap()
    rc_d = nc.dram_tensor("rcnt", [128, 64], F32, kind="ExternalInput").ap()
    flag_d = nc.dram_tensor("flag", [128, 1], F32, kind="ExternalInput").ap()
    x1T = nc.dram_tensor("x1T", [D, T + 2], F32, kind="ExternalOutput").ap()
    S = Sched()
    with ExitStack() as es:
        arena = es.enter_context(nc.sbuf_tensor("arena", [128, NCH * (T + 2)], F32))
        zf = arena[:, :].rearrange("p (c t) -> p c t", c=NCH, t=T + 2)
        x0b = arena[:, 0:NCH * TH // 2].bitcast(BF16).rearrange("p (c t) -> p c t", c=NCH, t=TH)
        u = es.enter_context(nc.sbuf_tensor("u", [128, 8, TH], BF16))
        vt = es.enter_context(nc.sbuf_tensor("vt", [128, 9, 1024], BF16))
        pp = es.enter_context(nc.sbuf_tensor("pp", [128, 8, TH], BF16))
        wsT = es.enter_context(nc.sbuf_tensor("wsT_s", [128, 8, 128], BF16))
        mask = es.enter_context(nc.sbuf_tensor("mask_s", [128, 128], F32))
        bsT = es.enter_context(nc.sbuf_tensor("bsT_s", [128, 8, 128], F32))
        lnv = es.enter_context(nc.sbuf_tensor("lnv_s", [128, 2, 1024], F32))
        wp = es.enter_context(nc.sbuf_tensor("wp_s", [128, 8, 256], BF16))
        psc = es.enter_context(nc.sbuf_tensor("psc_s", [128, 8], F32))
        gb = es.enter_context(nc.sbuf_tensor("gb_s", [128, NCH, 2], F32))
        rc = es.enter_context(nc.sbuf_tensor("rc_s", [128, 4, 16], F32))
        flag = es.enter_context(nc.sbuf_tensor("flag_s", [128, 1], F32))
        ones = es.enter_context(nc.sbuf_tensor("ones", [128, 128], BF16))
        eps = es.enter_context(nc.sbuf_tensor("eps", [128, 1], F32))
        xbf = es.enter_context(nc.sbuf_tensor("xbf", [128, 16 + TH], F32))
        g1 = [es.enter_context(nc.sbuf_tensor("g1_%d" % i, [128, TH], F32)) for i in range(2)]
        g2p = [es.enter_context(nc.sbuf_tensor("g2_%d" % i, [128, 16 + TH], F32)) for i in range(2)]
        g2 = [t[:, 16:16 + TH] for t in g2p]
        tA, tB = g2p
        wsTf = g1[1][:, 0:1024].rearrange("p (h t) -> p h t", h=8)
        st = es.enter_context(nc.sbuf_tensor("st", [128, 8], F32))
        small = es.enter_context(nc.sbuf_tensor("small", [128, 32], F32))
        tmps = {"eps": eps,
                "zb": [g2p[i][:, 16:16 + 513].bitcast(BF16) for i in range(2)],
                "zs": [xbf[:, 16:16 + 513].bitcast(BF16), xbf[:, 600:600 + 513].bitcast(BF16)],
                "mean": g1[0], "rstd": g1[1]}
        ps = es.enter_context(nc.psum_tensor("ps", [128, 4096], F32))
        ws = WeightStream(S, nc, es, 4, 16 * 256)
        wi = w_in.rearrange("(kc p) n -> p kc n", p=128)
        wo = w_out.rearrange("(kc p) n -> p kc n", p=128)
        u_xb = [ws.plan((16, 256), wi[:, :, 2048 + i * 256:2048 + (i + 1) * 256]) for i in range(4)]
        u_u = [ws.plan((16, 256), wi[:, :, i * 256:(i + 1) * 256]) for i in range(4)]
        u_v = [ws.plan((16, 256), wi[:, :, 1024 + i * 256:1024 + (i + 1) * 256]) for i in range(4)]
        u_o = [ws.plan((16, 256), wo[:, :, i * 256:(i + 1) * 256]) for i in range(8)]

        chp = S.new_chan(total=True)
        chx = S.new_chan(total=True)
        S.dma("sp", wsTf, wsT_d.rearrange("p (h t) -> p h t", h=8), chp)
        S.dma("sp", mask[:, :], mask_d, chp)
        S.dma("sp", bsT[:, :, :], bsT_d.rearrange("p (h t) -> p h t", h=8), chp)
        S.dma("sp", lnv[:, :, :], lnv_d.rearrange("p (a c) -> p a c", a=2), chp)
        S.dma("sp", psc[:, :], psc_d, chp)
        S.dma("sp", gb[:, :, :], gb_d.rearrange("p (c j) -> p c j", j=2), chp)
        S.dma("sp", rc[:, :, :], rc_d.rearrange("p (g j) -> p g j", g=4), chp)
        S.dma("sp", flag[:, :], flag_d, chp)
        S.dma("pool", wp[:, :, :], wp_d.rearrange("(a p) n -> p a n", p=128), chx)
        for c in range(NCH):
            S.dma("pool", x0b[:, c, :], x0T[c * 128:(c + 1) * 128, :], chx)
        ws.release(-1)
        S.memset("dve", ones[:, :], 1.0)
        S.memset("dve", eps[:, :], LN_EPS)
        S.memset("dve", xbf[:, 0:16], 0.0)
        S.memset("dve", tA[:, 0:16], 0.0)
        S.memset("dve", tB[:, 0:16], 0.0)
        for h in range(8):
            S.tt("dve", wsT[:, h, :], wsTf[:, h, :], mask[:, :], ALU.mult)

        GR = (0, 1536)
        TT3 = ((0, 512), (512, 512), (1024, 128))

        def proj_fm(wt, sub, g0):
            for kc in range(NCH):
                for (t0, w) in TT3:
                    S.mm(ps[:, g0 + t0:g0 + t0 + w], wt[:, kc, sub * 128:(sub + 1) * 128], x0b[:, kc, t0:t0 + w],
                         start=(kc == 0), stop=(kc == NCH - 1))

        gi = 0
        for i in range(4):
            wt = ws.get(u_xb[i])
            for sub in range(2):
                c = i * 2 + sub
                g = c // 2
                g0 = GR[gi % 2]
                gi += 1
                proj_fm(wt, sub, g0)
                S.act(xbf[:, 16:16 + TH], ps[:, g0:g0 + TH], AF.Identity)
                src = xbf
                dsts = [tA, tB]
                for k in range(g + 1):
                    sh = 1 << k
                    dst = dsts[k % 2]
                    S.tt("dve", dst[:, 16:16 + TH], src[:, 16:16 + TH], src[:, 16 - sh:16 + TH - sh], ALU.add)
                    src = dst
                win = B_WINDOWS[g]
                S.stt(pp[:, c, :], src[:, 16:16 + TH], 1.0 / win, xbf[:, 16:16 + TH], ALU.mult, ALU.subtract)
                S.tt("dve", small[:, 0:16], src[:, 16 + 128:16 + 144], rc[:, g, :], ALU.mult)
                S.tt("dve", pp[:, c, 128:144], small[:, 0:16], xbf[:, 16 + 128:16 + 144], ALU.subtract)
            ws.release(u_xb[i])
        for i in range(4):
            wt = ws.get(u_u[i])
            for sub in range(2):
                c = i * 2 + sub
                g0 = GR[gi % 2]
                proj_fm(wt, sub, g0)
                emit_gelu(S, u[:, c, :], ps[:, g0:g0 + TH], g1[gi % 2][:, :], g2[gi % 2])
                gi += 1
            ws.release(u_u[i])
        wv = [ws.get(k) for k in u_v]
        for tk in range(9):
            vr = g1[tk % 2]
            for cg in range(4):
                r0 = 3072 + ((tk * 4 + cg) % 2) * 512
                for kc in range(NCH):
                    S.mm(ps[:, r0:r0 + 256], x0b[:, kc, tk * 128:(tk + 1) * 128], wv[cg][:, kc, :],
                         start=(kc == 0), stop=(kc == NCH - 1))
                emit_gelu(S, vr[:, cg * 256:(cg + 1) * 256], ps[:, r0:r0 + 256],
                          g2[0][:, cg * 256:(cg + 1) * 256], g2[1][:, cg * 256:(cg + 1) * 256])
            sq = g2[0]
            S.add("dve", lambda e, vr=vr: e.reduce_sum(out=st[:, 0:1], in_=vr[:, 0:1024], axis=mybir.AxisListType.X),
                  reads=[vr[:, 0:1024]], writes=[st[:, 0:1]])
            S.act(sq[:, 0:1024], vr[:, 0:1024], AF.Square)
            S.add("dve", lambda e, sq=sq: e.reduce_sum(out=st[:, 1:2], in_=sq[:, 0:1024], axis=mybir.AxisListType.X),
                  reads=[sq[:, 0:1024]], writes=[st[:, 1:2]])
            S.ts("dve", st[:, 2:3], st[:, 0:1], 1.0 / 1024, None, ALU.mult)
            S.tt("dve", st[:, 3:4], st[:, 2:3], st[:, 2:3], ALU.mult)
            S.stt(st[:, 4:5], st[:, 1:2], 1.0 / 1024, st[:, 3:4], ALU.mult, ALU.subtract)
            S.act(st[:, 5:6], st[:, 4:5], AF.Sqrt, bias=eps[:, 0:1], scale=1.0)
            S.add("dve", lambda e: e.reciprocal(out=st[:, 6:7], in_=st[:, 5:6]), reads=[st[:, 5:6]],
                  writes=[st[:, 6:7]])
            S.ts("dve", vr[:, 0:1024], vr[:, 0:1024], st[:, 2:3], st[:, 6:7], ALU.subtract, ALU.mult)
            S.tt("dve", vr[:, 0:1024], vr[:, 0:1024], lnv[:, 0, :], ALU.mult)
            S.tt("dve", vt[:, tk, :], vr[:, 0:1024], lnv[:, 1, :], ALU.add)
        ws.release(u_v[3])
        for tk in range(9):
            for half in range(2):
                r0 = 2048 + half * 512
                for hh in range(4):
                    h = half * 4 + hh
                    S.mm(ps[:, r0 + hh * 128:r0 + (hh + 1) * 128], vt[:, tk, h * 128:(h + 1) * 128], wsT[:, h, :],
                         start=True, stop=True)
                tmp = g2[half][:, 0:512].rearrange("p (h t) -> p h t", h=4)
                S.tt("dve", tmp, ps[:, r0:r0 + 512].rearrange("p (h t) -> p h t", h=4),
                     bsT[:, half * 4:half * 4 + 4, :], ALU.add)
                uu = u[:, half * 4:half * 4 + 4, tk * 128:(tk + 1) * 128]
                S.tt("dve", uu, tmp, uu, ALU.mult)
        for g in range(4):
            for oc in range(2):
                g0 = GR[oc]
                for kc in range(2):
                    for (t0, w) in TT3:
                        S.mm(ps[:, g0 + t0:g0 + t0 + w], wp[:, g * 2 + kc, oc * 128:(oc + 1) * 128],
                             pp[:, g * 2 + kc, t0:t0 + w], start=(kc == 0), stop=(kc == 1))
            for oc in range(2):
                g0 = GR[oc]
                c = g * 2 + oc
                S.act(pp[:, c, :], ps[:, g0:g0 + TH], AF.Identity, scale=psc[:, c:c + 1])
        xs = [g1[0], g1[1]]
        chs = [S.new_chan(), S.new_chan()]
        for i in range(8):
            wt = ws.get(u_o[i])
            for sub in range(2):
                oc = i * 2 + sub
                g0 = GR[oc % 2]
                xst = xs[oc % 2]
                S.dma("sp", xst[:, 0:T + 2], x0T[oc * 128:(oc + 1) * 128, 126:TH], chs[oc % 2])
                for kc in range(NCH):
                    src = u[:, kc, :] if kc < 8 else pp[:, kc - 8, :]
                    lw = wt[:, kc, sub * 128:(sub + 1) * 128]
                    S.mm(ps[:, g0 + 510:g0 + 512], lw, src[:, 126:128], start=(kc == 0), stop=(kc == NCH - 1))
                    S.mm(ps[:, g0 + 512:g0 + 1024], lw, src[:, 128:640], start=(kc == 0), stop=(kc == NCH - 1))
                    S.mm(ps[:, g0 + 1024:g0 + 1536], lw, src[:, 640:1152], start=(kc == 0), stop=(kc == NCH - 1))
                S.stt(zf[:, oc, :], xst[:, 0:T + 2], ALPHA, ps[:, g0 + 510:g0 + 1536], ALU.mult, ALU.add)
            ws.release(u_o[i])
        emit_ln(S, zf, 0, T + 2, gb, ones, tmps, ps, [lambda c: zf[:, c, :]])
        S.ts("dve", zf[:, :, 0:2], zf[:, :, 0:2], flag[:, 0:1], None, ALU.mult)
        cho = [S.new_chan() for _ in range(4)]
        for c in range(NCH):
            S.dma("sp", x1T[c * 128:(c + 1) * 128, :], zf[:, c, :], cho[c % 4])
        S.emit(nc, es, final_chans=cho)
    return nc, S


def prep_mix0_inputs(x, ev_w_in, ev_ln_v_g, ev_ln_v_b, ev_w_s, ev_b_s, ev_w_pool, ev_pool_scale, ev_w_out,
                     ln1_g, ln1_b):
    x = np.asarray(x, np.float32)
    ws = np.asarray(ev_w_s[0], np.float32)
    wsT = np.ascontiguousarray(ws.transpose(2, 0, 1)).reshape(128, 8 * 128)
    tt_ = np.arange(128)
    maskA = (tt_[None, :] >= tt_[:, None]).astype(np.float32)
    bsT = np.ascontiguousarray(np.broadcast_to(np.asarray(ev_b_s[0], np.float32).reshape(1, 8 * 128), (128, 8 * 128)))
    lnv = np.ascontiguousarray(np.broadcast_to(
        np.concatenate([np.asarray(ev_ln_v_g[0], np.float32), np.asarray(ev_ln_v_b[0], np.float32)])[None, :],
        (128, 2048)))
    wp = np.ascontiguousarray(np.asarray(ev_w_pool[0], np.float32).reshape(4 * 256, 256))
    psc = _pm(ev_pool_scale[0], 8)
    gb = np.ascontiguousarray(np.stack([_pm(ln1_g[0], 16), _pm(ln1_b[0], 16)], axis=-1).reshape(128, 32))
    common = {"w_in": np.ascontiguousarray(ev_w_in[0], np.float32),
              "w_out": np.ascontiguousarray(ev_w_out[0], np.float32),
              "wsT": wsT, "maskA": maskA, "bsT": bsT, "lnv": lnv, "w_pool": wp, "pscale": psc, "ln1gb": gb}
    in_maps = []
    for c in range(NCORES):
        b, s = divmod(c, 4)
        t0 = s * T
        xt = np.zeros((D, TH), np.float32)
        xt[:, 128:] = x[b, t0:t0 + T].T
        if s > 0:
            xt[:, 0:128] = x[b, t0 - 128:t0].T
        rc = np.zeros((4, 16), np.float32)
        for g, win in enumerate(B_WINDOWS):
            pos = np.arange(t0 + 1, t0 + 17, dtype=np.float32)
            rc[g] = 1.0 / np.minimum(pos, float(win))
        rcb = np.ascontiguousarray(np.broadcast_to(rc.reshape(1, 64), (128, 64)))
        m = dict(common)
        m.update({"x0T": xt, "rcnt": rcb, "flag": np.full((128, 1), 1.0 if s > 0 else 0.0, np.float32)})
        in_maps.append(m)
    return in_maps


def run_mix0_launch(inputs):
    nc, S = build_mix0_launch()
    in_maps = prep_mix0_inputs(inputs["x"], inputs["ev_w_in"], inputs["ev_ln_v_g"], inputs["ev_ln_v_b"],
                               inputs["ev_w_s"], inputs["ev_b_s"], inputs["ev_w_pool"], inputs["ev_pool_scale"],
                               inputs["ev_w_out"], inputs["ln1_g"], inputs["ln1_b"])
    res = run_bass_kernel_spmd(nc, in_maps, core_ids=list(range(NCORES)))
    return [res.results[c]["x1T"] for c in range(NCORES)]


NH = 16
CH = 64
NCK = T // CH


def hgrn_consts(S, nc, es, lbp_d, mask2_d, ident_d, chp):
    c = {}
    lbp = es.enter_context(nc.sbuf_tensor("lbp_s", [128, 2, NH], F32))
    c["lb"] = es.enter_context(nc.sbuf_tensor("lb", [128, NH], F32))
    c["oml"] = es.enter_context(nc.sbuf_tensor("oml", [128, NH], F32))
    c["rm"] = es.enter_context(nc.sbuf_tensor("rm", [128, T], F32))
    c["mask2"] = es.enter_context(nc.sbuf_tensor("mask2_s", [128, 128], F32))
    identf = es.enter_context(nc.sbuf_tensor("identf", [128, 128], F32))
    c["ident"] = es.enter_context(nc.sbuf_tensor("ident_s", [128, 128], BF16))
    S.dma("sp", lbp[:, :, :], lbp_d.rearrange("p (l h) -> p l h", l=2), chp)
    S.dma("sp", c["mask2"][:, :], mask2_d, chp)
    S.dma("sp", identf[:, :], ident_d, chp)
    S.copy("dve", c["ident"][:, :], identf[:, :])
    S.tt("dve", c["lb"][:, :], lbp[:, 1, :], lbp[:, 0, :], ALU.subtract)
    S.act(c["lb"][:, :], c["lb"][:, :], AF.Sigmoid)
    S.ts("dve", c["oml"][:, :], c["lb"][:, :], -1.0, 1.0, ALU.mult, ALU.add)
    S.memset("dve", c["rm"][:, :], 1.0)
    S.memset("dve", c["rm"][:, :].rearrange("p (c t) -> p c t", t=CH)[:, :, 0:1], 0.0)
    c["pm"] = es.enter_context(nc.sbuf_tensor("pm", [128, 2], F32))
    S.memset("dve", c["pm"][:, :], 0.0)
    S.memset("dve", c["pm"][0:64, 0:1], 1.0)
    S.memset("dve", c["pm"][64:128, 1:2], 1.0)
    return c


def hgrn_gates(S, h, cst, f_ps, A, B, C, kend_bf, dec, kdec_bf=None, eC=None):
    S.act(A, f_ps, AF.Sigmoid)
    S.ts("dve", A, A, cst["oml"][:, h:h + 1], cst["lb"][:, h:h + 1], ALU.mult, ALU.add)
    S.act(B, A, AF.Ln)
    S.add("dve", lambda e: e.tensor_tensor_scan(out=C, data0=cst["rm"][:, :], data1=B, initial=0.0,
                                                op0=ALU.mult, op1=ALU.add),
          reads=[cst["rm"][:, :], B], writes=[C])
    S.ts("dve", A, A, -1.0, 1.0, ALU.mult, ALU.add)
    S.act(B, C, AF.Exp, scale=-1.0)
    S.tt("dve", B, A, B, ALU.mult)
    if kdec_bf is not None:
        S.act(kdec_bf, B, AF.Identity)
    C3 = C.rearrange("p (c t) -> p c t", t=CH)
    S.act(dec.rearrange("p (c o) -> p c o", o=1), C3[:, :, CH - 1:CH], AF.Exp)
    S.tt("dve", kend_bf.rearrange("p (c t) -> p c t", t=CH), B.rearrange("p (c t) -> p c t", t=CH),
         dec.rearrange("p (c o) -> p c o", o=1).to_broadcast([128, NCK, CH]), ALU.mult)
    if eC is not None:
        S.act(eC, C, AF.Exp)


def hgrn_transposes(S, cst, psb, src_bf, dst_tok, dst_tok1=None):
    for j in range(8):
        S.transpose(psb[:, 4096 + j * 128:4096 + (j + 1) * 128], src_bf[:, j * 128:(j + 1) * 128], cst["ident"][:, :])
    if dst_tok1 is None:
        S.act(dst_tok.rearrange("p j d -> p (j d)"), psb[:, 4096:5120], AF.Identity)
    else:
        S.act(dst_tok.rearrange("p j d -> p (j d)"), psb[:, 4096:5120], AF.Identity, scale=cst["pm"][:, 0:1])
        S.act(dst_tok1.rearrange("p j d -> p (j d)"), psb[:, 4096:5120], AF.Identity, scale=cst["pm"][:, 1:2])


def hgrn_state_scan(S, ps, kt, vtk, dec, Sf, Sb=None):
    cur = 0
    if Sb is not None:
        S.act(Sb[:, 0, :], Sf[0][:, :], AF.Identity)
    for g4 in range(4):
        for cc in range(4):
            c = g4 * 4 + cc
            j, par = divmod(c, 2)
            S.mm(ps[:, 2560 + cc * 128:2560 + (cc + 1) * 128], kt[par][:, j, :], vtk[:, j, :], start=True, stop=True)
        for cc in range(4):
            c = g4 * 4 + cc
            nxt = 1 - cur
            S.stt(Sf[nxt][:, :], Sf[cur][:, :], dec[:, c:c + 1], ps[:, 2560 + cc * 128:2560 + (cc + 1) * 128],
                  ALU.mult, ALU.add)
            cur = nxt
            if Sb is not None and c + 1 < NCK:
                S.act(Sb[:, c + 1, :], Sf[cur][:, :], AF.Identity)
    return cur


def build_hgrn_pre_launch(stage=9, nheads=NH):
    nc = bass.Bass("TRN2", target_bir_lowering=False)
    xT = nc.dram_tensor("xT", [D, T], F32, kind="ExternalInput").ap()
    w_in = nc.dram_tensor("w_in", [D, 4 * D], F32, kind="ExternalInput").ap()
    lbp_d = nc.dram_tensor("lbp", [128, 2 * NH], F32, kind="ExternalInput").ap()
    mask2_d = nc.dram_tensor("mask2", [128, 128], F32, kind="ExternalInput").ap()
    ident_d = nc.dram_tensor("ident", [128, 128], F32, kind="ExternalInput").ap()
    U_d = nc.dram_tensor("U", [NH, 128, 128], F32, kind="ExternalOutput").ap()
    D_d = nc.dram_tensor("Dd", [128, NH], F32, kind="ExternalOutput").ap()
    S = Sched()
    with ExitStack() as es:
        xb = es.enter_context(nc.sbuf_tensor("xb", [128, NCH, T], BF16))
        A = [es.enter_context(nc.sbuf_tensor("A%d" % i, [128, T], F32)) for i in range(2)]
        B = [es.enter_context(nc.sbuf_tensor("B%d" % i, [128, T], F32)) for i in range(2)]
        C = [es.enter_context(nc.sbuf_tensor("C%d" % i, [128, T], F32)) for i in range(2)]
        kend = [es.enter_context(nc.sbuf_tensor("kend%d" % i, [128, T], BF16)) for i in range(2)]
        ibf = [es.enter_context(nc.sbuf_tensor("ibf%d" % i, [128, T], BF16)) for i in range(2)]
        kt = [[es.enter_context(nc.sbuf_tensor("kt%d_%d" % (i, k), [128, 8, 128], BF16)) for k in range(2)]
              for i in range(2)]
        vtk = [es.enter_context(nc.sbuf_tensor("vtk%d" % i, [128, 8, 128], BF16)) for i in range(2)]
        dec = [es.enter_context(nc.sbuf_tensor("dec%d" % i, [128, NCK], F32)) for i in range(2)]
        Sf = [[es.enter_context(nc.sbuf_tensor("Sf%d_%d" % (i, k), [128, 128], F32)) for k in range(2)]
              for i in range(2)]
        Dd = es.enter_context(nc.sbuf_tensor("Dd_s", [128, NH], F32))
        ps = es.enter_context(nc.psum_tensor("ps", [128, 4096], F32))
        psb = ps[:, :].bitcast(BF16)
        chp = S.new_chan(total=True)
        chx = S.new_chan(total=True)
        cst = hgrn_consts(S, nc, es, lbp_d, mask2_d, ident_d, chp)
        ws = WeightStream(S, nc, es, 4, 16 * 256)
        wi = w_in.rearrange("(kc p) n -> p kc n", p=128)
        uses = [ws.plan((16, 256), wi[:, :, h * 512 + 256:h * 512 + 512]) for h in range(NH)]
        for c in range(NCH):
            S.dma("pool", xb[:, c, :], xT[c * 128:(c + 1) * 128, :], chx)
        ws.release(-1)
        cho = [S.new_chan() for _ in range(2)]
        for h in range(nheads):
            b = h % 2
            wt = ws.get(uses[h])
            for which in range(2):
                g0 = which * 1024
                for kc in range(NCH):
                    for t_ in range(2):
                        S.mm(ps[:, g0 + t_ * 512:g0 + (t_ + 1) * 512], wt[:, kc, which * 128:(which + 1) * 128],
                             xb[:, kc, t_ * 512:(t_ + 1) * 512], start=(kc == 0), stop=(kc == NCH - 1))
            ws.release(uses[h])
            hgrn_gates(S, h, cst, ps[:, 0:1024], A[b][:, :], B[b][:, :], C[b][:, :], kend[b][:, :], dec[b][:, :])
            S.act(ibf[b][:, :], ps[:, 1024:2048], AF.Identity)
            C3 = C[b][:, :].rearrange("p (c t) -> p c t", t=CH)
            S.add("dve", lambda e, C3=C3, h=h: e.reduce_sum(out=Dd[:, h:h + 1], in_=C3[:, :, CH - 1:CH],
                                                           axis=mybir.AxisListType.XY),
                  reads=[C[b][:, :]], writes=[Dd[:, h:h + 1]])
            S.act(Dd[:, h:h + 1], Dd[:, h:h + 1], AF.Exp)
            if stage >= 2:
                hgrn_transposes(S, cst, psb, kend[b][:, :], kt[b][0][:, :, :], kt[b][1][:, :, :])
                hgrn_transposes(S, cst, psb, ibf[b][:, :], vtk[b][:, :, :])
            S.memset("dve", Sf[b][0][:, :], 0.0)
            fin = 0
            if stage >= 3:
                fin = hgrn_state_scan(S, ps, kt[b], vtk[b], dec[b], Sf[b])
            S.dma("sp", U_d[h, :, :], Sf[b][fin][:, :], cho[b])
        chd = S.new_chan()
        S.dma("sp", D_d, Dd[:, :], chd)
        S.emit(nc, es, final_chans=cho + [chd])
    return nc, S


def regroup_w_in(od_w_in):
    w = np.asarray(od_w_in, np.float32).reshape(D, 4, NH, 128)
    w = w[:, [0, 3, 1, 2]]
    return np.ascontiguousarray(w.transpose(0, 2, 1, 3).reshape(D, 4 * D))


def hgrn_const_inputs(lb_param):
    lbp = np.ascontiguousarray(np.stack([_pm(lb_param[0], NH), _pm(lb_param[1], NH)], axis=1).reshape(128, 2 * NH))
    i = np.arange(128)
    mask2 = ((i[None, :] >= i[:, None]) & ((i[None, :] // CH) == (i[:, None] // CH))).astype(np.float32)
    return {"lbp": lbp, "mask2": mask2, "ident": np.eye(128, dtype=np.float32)}


def _carve(arena, off_bytes, n_elems, dt):
    assert off_bytes % 4 == 0
    nb = n_elems * _dsize(dt)
    assert nb % 4 == 0
    v = arena[:, off_bytes // 4:(off_bytes + nb) // 4]
    return v if dt == F32 else v.bitcast(dt)


def build_hgrn_main_launch():
    nc = bass.Bass("TRN2", target_bir_lowering=False)
    xT = nc.dram_tensor("xT", [D, T], F32, kind="ExternalInput").ap()
    w_in = nc.dram_tensor("w_in", [D, 4 * D], F32, kind="ExternalInput").ap()
    w_out = nc.dram_tensor("w_out", [D, D], F32, kind="ExternalInput").ap()
    lbp_d = nc.dram_tensor("lbp", [128, 2 * NH], F32, kind="ExternalInput").ap()
    mask2_d = nc.dram_tensor("mask2", [128, 128], F32, kind="ExternalInput").ap()
    ident_d = nc.dram_tensor("ident", [128, 128], F32, kind="ExternalInput").ap()
    up_d = nc.dram_tensor("Uprev", [3, NH, 128, 128], F32, kind="ExternalInput").ap()
    dp_d = nc.dram_tensor("Dprev", [128, 3 * NH], F32, kind="ExternalInput").ap()
    gn_d = nc.dram_tensor("gn", [128, NH], F32, kind="ExternalInput").ap()
    gb_d = nc.dram_tensor("ln1gb", [128, 32], F32, kind="ExternalInput").ap()
    x1T = nc.dram_tensor("x1T", [D, T], F32, kind="ExternalOutput").ap()
    S = Sched()
    with ExitStack() as es:
        arena = es.enter_context(nc.sbuf_tensor("arena", [128, NCH * T], F32))
        zf = arena[:, :].rearrange("p (c t) -> p c t", c=NCH, t=T)
        xb = _carve(arena, 0, NCH * T, BF16).rearrange("p (c t) -> p c t", c=NCH, t=T)
        off = NCH * T * 2
        A = _carve(arena, off, T, F32); off += 4 * T
        B = _carve(arena, off, T, F32); off += 4 * T
        C = _carve(arena, off, T, F32); off += 4 * T
        kend = _carve(arena, off, T, BF16); off += 2 * T
        kdec = _carve(arena, off, T, BF16); off += 2 * T
        qdec = _carve(arena, off, T, BF16); off += 2 * T
        ibf = _carve(arena, off, T, BF16); off += 2 * T
        sg = _carve(arena, off, T, BF16); off += 2 * T
        osq = _carve(arena, off, T, BF16); off += 2 * T
        attm = _carve(arena, off, T, BF16).rearrange("p (j t) -> p j t", j=8); off += 2 * T
        kt0 = _carve(arena, off, T, BF16).rearrange("p (j t) -> p j t", j=8); off += 2 * T
        kt1 = _carve(arena, off, T, BF16).rearrange("p (j t) -> p j t", j=8); off += 2 * T
        kt = (kt0, kt1)
        vtk = _carve(arena, off, T, BF16).rearrange("p (j t) -> p j t", j=8); off += 2 * T
        assert off <= NCH * T * 4
        y = es.enter_context(nc.sbuf_tensor("y", [128, NH, T], BF16))
        Sb = es.enter_context(nc.sbuf_tensor("Sb", [128, NCK, 128], BF16))
        Sf = [es.enter_context(nc.sbuf_tensor("Sf%d" % k, [128, 128], F32)) for k in range(2)]
        upst = es.enter_context(nc.sbuf_tensor("upst", [128, 3, 128], F32))
        dec = es.enter_context(nc.sbuf_tensor("dec", [128, NCK], F32))
        dp = es.enter_context(nc.sbuf_tensor("dp", [128, 3, NH], F32))
        gn = es.enter_context(nc.sbuf_tensor("gn_s", [128, NH], F32))
        gb = es.enter_context(nc.sbuf_tensor("gb_s", [128, NCH, 2], F32))
        ones = es.enter_context(nc.sbuf_tensor("ones", [128, 128], BF16))
        eps = es.enter_context(nc.sbuf_tensor("eps", [128, 1], F32))
        xs = [es.enter_context(nc.sbuf_tensor("xs%d" % i, [128, T], F32)) for i in range(2)]
        tmps = {"eps": eps,
                "zb": [es.enter_context(nc.sbuf_tensor("zb%d" % i, [128, T], BF16)) for i in range(2)],
                "zs": [es.enter_context(nc.sbuf_tensor("zs%d" % i, [128, T], BF16)) for i in range(2)],
                "mean": xs[0], "rstd": xs[1]}
        ps = es.enter_context(nc.psum_tensor("ps", [128, 4096], F32))
        psb = ps[:, :].bitcast(BF16)
        chp = S.new_chan(total=True)
        chx = S.new_chan(total=True)
        cst = hgrn_consts(S, nc, es, lbp_d, mask2_d, ident_d, chp)
        S.dma("sp", dp[:, :, :], dp_d.rearrange("p (j h) -> p j h", j=3), chp)
        S.dma("sp", gn[:, :], gn_d, chp)
        S.dma("sp", gb[:, :, :], gb_d.rearrange("p (c j) -> p c j", j=2), chp)
        S.memset("dve", ones[:, :], 1.0)
        S.memset("dve", eps[:, :], LN_EPS)
        ws = WeightStream(S, nc, es, 3, 16 * 512)
        wi = w_in.rearrange("(kc p) n -> p kc n", p=128)
        wo = w_out.rearrange("(kc p) n -> p kc n", p=128)
        uses = [ws.plan((16, 512), wi[:, :, h * 512:(h + 1) * 512]) for h in range(NH)]
        u_o = [ws.plan((16, 512), wo[:, :, i * 512:(i + 1) * 512]) for i in range(4)]
        for c in range(NCH):
            S.dma("pool", xb[:, c, :], xT[c * 128:(c + 1) * 128, :], chx)
        ws.release(-1)
        chu = S.new_chan()
        G0, G1 = 0, 1024
        for h in range(NH):
            wt = ws.get(uses[h])

            def proj(blk, g0):
                for kc in range(NCH):
                    for t_ in range(2):
                        S.mm(ps[:, g0 + t_ * 512:g0 + (t_ + 1) * 512], wt[:, kc, blk * 128:(blk + 1) * 128],
                             xb[:, kc, t_ * 512:(t_ + 1) * 512], start=(kc == 0), stop=(kc == NCH - 1))
            S.dma("sp", upst[:, :, :], up_d[:, h, :, :].rearrange("j d e -> d j e"), chu)
            S.memset("dve", Sf[0][:, :], 0.0)
            cur = 0
            for j in range(3):
                S.stt(Sf[1 - cur][:, :], Sf[cur][:, :], dp[:, j, h:h + 1], upst[:, j, :], ALU.mult, ALU.add)
                cur = 1 - cur
            Sfl = [Sf[cur], Sf[1 - cur]]
            proj(2, G0)
            proj(3, G1)
            hgrn_gates(S, h, cst, ps[:, G0:G0 + T], A, B, C, kend, dec[:, :], kdec_bf=kdec, eC=A)
            S.act(ibf, ps[:, G1:G1 + T], AF.Identity)
            proj(0, G0)
            proj(1, G1)
            ws.release(uses[h])
            S.act(B, ps[:, G0:G0 + T], AF.Silu)
            S.tt("dve", qdec, B, A, ALU.mult)
            S.act(sg, ps[:, G1:G1 + T], AF.Sigmoid)
            hgrn_transposes(S, cst, psb, kend, kt0, kt1)
            hgrn_transposes(S, cst, psb, ibf, vtk)
            hgrn_state_scan(S, ps, kt, vtk, dec, Sfl, Sb=Sb)
            for half in range(2):
                for jj in range(4):
                    j = half * 4 + jj
                    S.mm(ps[:, 2560 + jj * 128:2560 + (jj + 1) * 128], kdec[:, j * 128:(j + 1) * 128],
                         qdec[:, j * 128:(j + 1) * 128], start=True, stop=True)
                for jj in range(4):
                    j = half * 4 + jj
                    S.tt("dve", attm[:, j, :], ps[:, 2560 + jj * 128:2560 + (jj + 1) * 128], cst["mask2"][:, :],
                         ALU.mult)
            O0 = 3072
            for j in range(8):
                S.mm(ps[:, O0 + j * 128:O0 + (j + 1) * 128], vtk[:, j, :], attm[:, j, :], start=True, stop=False)
                S.mm(ps[:, O0 + j * 128:O0 + j * 128 + 64], Sb[:, 2 * j, :], qdec[:, j * 128:j * 128 + 64],
                     start=False, stop=False)
                S.mm(ps[:, O0 + j * 128 + 64:O0 + (j + 1) * 128], Sb[:, 2 * j + 1, :],
                     qdec[:, j * 128 + 64:(j + 1) * 128], start=False, stop=True)
            S.act(osq, ps[:, O0:O0 + T], AF.Square)
            for t_ in range(2):
                S.mm(ps[:, G0 + t_ * 512:G0 + (t_ + 1) * 512], ones[:, :], osq[:, t_ * 512:(t_ + 1) * 512],
                     start=True, stop=True)
            S.act(A, ps[:, G0:G0 + T], AF.Ln, bias=eps[:, 0:1], scale=1.0 / 128)
            S.act(A, A, AF.Exp, scale=-0.5)
            S.stt(C, ps[:, O0:O0 + T], gn[:, h:h + 1], A, ALU.mult, ALU.mult)
            S.tt("dve", y[:, h, :], C, sg, ALU.mult)
        chs = [S.new_chan(), S.new_chan()]
        for i in range(4):
            wt = ws.get(u_o[i])
            for sub in range(4):
                oc = i * 4 + sub
                g0 = (oc % 2) * 1024
                xst = xs[oc % 2]
                S.dma("sp", xst[:, :], xT[oc * 128:(oc + 1) * 128, :], chs[oc % 2])
                for kc in range(NCH):
                    for t_ in range(2):
                        S.mm(ps[:, g0 + t_ * 512:g0 + (t_ + 1) * 512], wt[:, kc, sub * 128:(sub + 1) * 128],
                             y[:, kc, t_ * 512:(t_ + 1) * 512], start=(kc == 0), stop=(kc == NCH - 1))
                S.stt(zf[:, oc, :], xst[:, :], ALPHA, ps[:, g0:g0 + T], ALU.mult, ALU.add)
            ws.release(u_o[i])
        emit_ln(S, zf, 0, T, gb, ones, tmps, ps, [lambda c: zf[:, c, :]])
        cho = [S.new_chan() for _ in range(4)]
        for c in range(NCH):
            S.dma("sp", x1T[c * 128:(c + 1) * 128, :], zf[:, c, :], cho[c % 4])
        S.emit(nc, es, final_chans=cho)
    return nc, S


def _run(nc, in_maps):
    return run_bass_kernel_spmd(nc, in_maps, core_ids=list(range(NCORES))).results


def kernel_unfused(x, ev_w_in, ev_ln_v_g, ev_ln_v_b, ev_w_s, ev_b_s, ev_w_pool, ev_pool_scale,
           ev_w_out, od_w_in, od_norm_g, od_w_out, lb_param, ffn_w_up, ffn_conv_w,
           ffn_conv_b, ffn_w_down, ln1_g, ln1_b, ln2_g, ln2_b):
    f32 = np.float32
    nc0, _ = build_mix0_launch()
    maps0 = prep_mix0_inputs(x, ev_w_in, ev_ln_v_g, ev_ln_v_b, ev_w_s, ev_b_s, ev_w_pool, ev_pool_scale,
                             ev_w_out, ln1_g, ln1_b)
    r0 = _run(nc0, maps0)
    x1T = [r0[c]["x1T"] for c in range(NCORES)]
    ncf, _ = build_ffn_launch()

    def ffn(l, xTs):
        cwb, gb = prep_ffn_params(l, ffn_conv_w, ffn_conv_b, ln2_g, ln2_b)
        wu = np.ascontiguousarray(ffn_w_up[l], f32)
        wd = np.ascontiguousarray(ffn_w_down[l], f32)
        maps = [{"xT": np.ascontiguousarray(xTs[c], f32), "w_up": wu, "w_down": wd, "cwb": cwb, "ln2gb": gb}
                for c in range(NCORES)]
        r = _run(ncf, maps)
        return [r[c]["yT"] for c in range(NCORES)]

    x2T = ffn(0, x1T)
    ncp, _ = build_hgrn_pre_launch()
    w_in_r = regroup_w_in(od_w_in[0])
    hc = hgrn_const_inputs(lb_param)
    mapsp = []
    for c in range(NCORES):
        m = {"xT": np.ascontiguousarray(x2T[c], f32), "w_in": w_in_r}
        m.update(hc)
        mapsp.append(m)
    rp = _run(ncp, mapsp)
    ncm, _ = build_hgrn_main_launch()
    gn = _pm(od_norm_g[0], NH)
    gb1 = np.ascontiguousarray(np.stack([_pm(ln1_g[1], 16), _pm(ln1_b[1], 16)], axis=-1).reshape(128, 32))
    w_out1 = np.ascontiguousarray(od_w_out[0], f32)
    mapsm = []
    for c in range(NCORES):
        b, s = divmod(c, 4)
        up = np.zeros((3, NH, 128, 128), f32)
        dp = np.zeros((128, 3, NH), f32)
        for j in range(s):
            pos = 3 - s + j
            up[pos] = rp[b * 4 + j]["U"]
            dp[:, pos, :] = rp[b * 4 + j]["Dd"]
        m = {"xT": np.ascontiguousarray(x2T[c], f32), "w_in": w_in_r, "w_out": w_out1, "Uprev": up,
             "Dprev": np.ascontiguousarray(dp.reshape(128, 3 * NH)), "gn": gn, "ln1gb": gb1}
        m.update(hc)
        mapsm.append(m)
    rm = _run(ncm, mapsm)
    x1bT = []
    for c in range(NCORES):
        b, s = divmod(c, 4)
        xt = np.zeros((D, T + 2), f32)
        xt[:, 2:] = rm[c]["x1T"]
        if s > 0:
            xt[:, 0:2] = rm[c - 1]["x1T"][:, T - 2:T]
        x1bT.append(xt)
    outT = ffn(1, x1bT)
    out = np.zeros((2, 4 * T, D), f32)
    for c in range(NCORES):
        b, s = divmod(c, 4)
        out[b, s * T:(s + 1) * T] = outT[c].T
    return out


R0 = 0
R0_SZ = NCH * (T + 2) * 4
R1 = R0 + R0_SZ
R1_SZ = NCH * (T + 2) * 2
R2 = R1 + R1_SZ
R2_SZ = 66560
AR_BYTES = R2 + R2_SZ
SEQ_GROUPS = [[0, 1, 2, 3], [4, 5, 6, 7]]


def build_fused(use_cc=True):
    nc = bass.Bass("TRN2", target_bir_lowering=False)

    def din(name, shape):
        return nc.dram_tensor(name, shape, F32, kind="ExternalInput").ap()
    x0T = din("x0T", [D, TH])
    ev_w_in = din("ev_w_in", [D, 3072])
    ev_w_out = din("ev_w_out", [D, D])
    wsT_d = din("wsT", [128, 8 * 128])
    mask_d = din("maskA", [128, 128])
    bsT_d = din("bsT", [128, 8 * 128])
    lnv_d = din("lnv", [128, 2 * 1024])
    wp_d = din("w_pool", [4 * 256, 256])
    psc_d = din("pscale", [128, 8])
    rc_d = din("rcnt", [128, 64])
    flag_d = din("flag", [128, 1])
    oh_d = din("oh", [128, 8])
    ln1gb_d = din("ln1gb", [128, 64])
    ln2gb_d = din("ln2gb", [128, 64])
    cwb_d = din("cwb", [128, 2 * 88 * 4])
    w_up = [din("w_up%d" % l, [D, 2 * DFF]) for l in range(2)]
    w_down = [din("w_down%d" % l, [DFF, D]) for l in range(2)]
    od_w_in = din("od_w_in", [D, 4 * D])
    od_w_out = din("od_w_out", [D, D])
    lbp_d = din("lbp", [128, 2 * NH])
    mask2_d = din("mask2", [128, 128])
    ident_d = din("ident", [128, 128])
    gn_d = din("gn", [128, NH])
    outT = nc.dram_tensor("outT", [D, T], F32, kind="ExternalOutput").ap()
    xsp = nc.dram_tensor("xsp", [D, T], F32).ap()
    ccin = [nc.dram_tensor("ccin%d" % g, [4 * 4 * 128, 129], F32) for g in range(4)]
    ccout = [nc.dram_tensor("ccout%d" % g, [4 * 4 * 128, 129], F32) for g in range(4)]
    cch_in = nc.dram_tensor("cch_in", [4 * 128, 32], F32)
    cch_out = nc.dram_tensor("cch_out", [4 * 128, 32], F32)

    S = Sched()
    with ExitStack() as es:
        AR = es.enter_context(nc.sbuf_tensor("AR", [128, AR_BYTES // 4], F32))

        def cv(off, shape, dt):
            n = 1
            for k in shape:
                n *= k
            v = _carve(AR, off, n, dt)
            if len(shape) == 2:
                v = v.rearrange("p (a b) -> p a b", a=shape[0], b=shape[1])
            return v

        def sb(name, shape, dt=F32):
            return es.enter_context(nc.sbuf_tensor(name, shape, dt))
        psc = sb("psc_s", [128, 8]); rc = sb("rc_s", [128, 4, 16]); flag = sb("flag_s", [128, 1])
        oh = sb("oh_s", [128, 8]); ln1gb = sb("ln1gb_s", [128, 2, NCH, 2]); ln2gb = sb("ln2gb_s", [128, 2, NCH, 2])
        cwb = sb("cwb_s", [128, 2, 88, 4]); ones = sb("ones", [128, 128], BF16); eps = sb("eps", [128, 1])
        st = sb("st", [128, 8]); small = sb("small", [128, 32]); gn = sb("gn_s", [128, NH])
        Dd = sb("Dd_s", [128, NH]); tiny = sb("tiny", [128, 8])
        hstg = sb("hstg", [128, 4, 32]); hld = sb("hld", [128, 4, 32]); hsum = sb("hsum", [128, 32])
        ps = es.enter_context(nc.psum_tensor("ps", [128, 4096], F32))
        psb = ps[:, :].bitcast(BF16)
        ccsem = [es.enter_context(nc.semaphore("ccs%d" % i)) for i in range(5)]
        ws = WeightStream(S, nc, es, 4, 16 * 256)

        wi0 = ev_w_in.rearrange("(kc p) n -> p kc n", p=128)
        wo0 = ev_w_out.rearrange("(kc p) n -> p kc n", p=128)
        u_xb = [ws.plan((16, 256), wi0[:, :, 2048 + i * 256:2048 + (i + 1) * 256]) for i in range(4)]
        u_u = [ws.plan((16, 256), wi0[:, :, i * 256:(i + 1) * 256]) for i in range(4)]
        u_v = [ws.plan((16, 256), wi0[:, :, 1024 + i * 256:1024 + (i + 1) * 256]) for i in range(4)]
        u_o = [ws.plan((16, 256), wo0[:, :, i * 256:(i + 1) * 256]) for i in range(8)]
        plan0 = plan_ffn_weights(ws, w_up[0], w_down[0], split_last=True)
        wi1 = od_w_in.rearrange("(kc p) n -> p kc n", p=128)
        wo1 = od_w_out.rearrange("(kc p) n -> p kc n", p=128)
        u_pre = [ws.plan((16, 256), wi1[:, :, h * 512 + 256:h * 512 + 512]) for h in range(NH)]
        u_main = []
        for h in range(NH):
            fi = ws.plan((16, 256), wi1[:, :, h * 512 + 256:h * 512 + 512])
            qg = ws.plan((16, 256), wi1[:, :, h * 512:h * 512 + 256])
            u_main.append((fi, qg))
        u_o1 = [ws.plan((16, 256), wo1[:, :, (i % 8) * 256:((i % 8) + 1) * 256]) for i in range(16)]
        plan1 = plan_ffn_weights(ws, w_up[1], w_down[1], split_last=True)

        x0b = cv(R0, (NCH, TH), BF16)
        zf = cv(R0, (NCH, T + 2), F32)
        xb = cv(R1, (NCH, T + 2), BF16)
        pp = cv(R1, (8, TH), BF16)
        g1 = [cv(R1 + 18432, (TH,), F32), cv(R1 + 23040, (TH,), F32)]
        xbf = cv(R1 + 27648, (16 + TH,), F32)
        u = cv(R2, (8, TH), BF16)
        vt = cv(R2 + 18432, (9, 1024), BF16)
        g2p = [cv(R2 + 36864, (16 + TH,), F32), cv(R2 + 41536, (16 + TH,), F32)]
        g2 = [t[:, 16:16 + TH] for t in g2p]
        tA, tB = g2p
        wsT = cv(R2 + 46208, (8, 128), BF16)
        mask = cv(R2 + 48256, (128,), F32)
        bsT = cv(R2 + 48768, (8, 128), F32)
        lnv = cv(R2 + 52864, (2, 1024), F32)
        wp = cv(R2 + 61056, (8, 256), BF16)
        wsTf = g1[1][:, 0:1024].rearrange("p (h t) -> p h t", h=8)
        hb = [cv(R0 + 36864 + k * 4608, (TH,), F32) for k in range(2)]

        chp = S.new_chan(total=True)
        chx = S.new_chan(total=True)
        S.dma("sp", wsTf, wsT_d.rearrange("p (h t) -> p h t", h=8), chp)
        S.dma("sp", mask, mask_d, chp)
        S.dma("sp", bsT, bsT_d.rearrange("p (h t) -> p h t", h=8), chp)
        S.dma("sp", lnv, lnv_d.rearrange("p (a c) -> p a c", a=2), chp)
        S.dma("sp", psc[:, :], psc_d, chp)
        S.dma("sp", rc[:, :, :], rc_d.rearrange("p (g j) -> p g j", g=4), chp)
        S.dma("sp", flag[:, :], flag_d, chp)
        S.dma("sp", oh[:, :], oh_d, chp)
        S.dma("sp", ln1gb[:, :, :, :], ln1gb_d.rearrange("p (l c j) -> p l c j", l=2, j=2), chp)
        S.dma("sp", ln2gb[:, :, :, :], ln2gb_d.rearrange("p (l c j) -> p l c j", l=2, j=2), chp)
        S.dma("sp", cwb[:, :, :, :], cwb_d.rearrange("p (l c j) -> p l c j", l=2, j=4), chp)
        S.dma("sp", gn[:, :], gn_d, chp)
        S.dma("pool", wp, wp_d.rearrange("(a p) n -> p a n", p=128), chx)
        for c in range(NCH):
            S.dma("pool", x0b[:, c, :], x0T[c * 128:(c + 1) * 128, :], chx)
        ws.release(-1)
        S.memset("dve", ones[:, :], 1.0)
        S.memset("dve", eps[:, :], LN_EPS)
        S.memset("dve", xbf[:, 0:16], 0.0)
        S.memset("dve", tA[:, 0:16], 0.0)
        S.memset("dve", tB[:, 0:16], 0.0)
        for h in range(8):
            S.tt("dve", wsT[:, h, :], wsTf[:, h, :], mask, ALU.mult)

        GR = (0, 1536)
        TT3 = ((0, 512), (512, 512), (1024, 128))

        def proj_fm(wt, sub, g0):
            for kc in range(NCH):
                for (t0, w) in TT3:
                    S.mm(ps[:, g0 + t0:g0 + t0 + w], wt[:, kc, sub * 128:(sub + 1) * 128], x0b[:, kc, t0:t0 + w],
                         start=(kc == 0), stop=(kc == NCH - 1))
        gi = 0
        for i in range(4):
            wt = ws.get(u_xb[i])
            for sub in range(2):
                c = i * 2 + sub
                g = c // 2
                g0 = GR[gi % 2]
                gi += 1
                proj_fm(wt, sub, g0)
                S.act(xbf[:, 16:16 + TH], ps[:, g0:g0 + TH], AF.Identity)
                src = xbf
                dsts = [tA, tB]
                for k in range(g + 1):
                    sh = 1 << k
                    dst = dsts[k % 2]
                    S.tt("dve", dst[:, 16:16 + TH], src[:, 16:16 + TH], src[:, 16 - sh:16 + TH - sh], ALU.add)
                    src = dst
                win = B_WINDOWS[g]
                S.stt(pp[:, c, :], src[:, 16:16 + TH], 1.0 / win, xbf[:, 16:16 + TH], ALU.mult, ALU.subtract)
                S.tt("dve", small[:, 0:16], src[:, 16 + 128:16 + 144], rc[:, g, :], ALU.mult)
                S.tt("dve", pp[:, c, 128:144], small[:, 0:16], xbf[:, 16 + 128:16 + 144], ALU.subtract)
            ws.release(u_xb[i])
        for i in range(4):
            wt = ws.get(u_u[i])
            for sub in range(2):
                c = i * 2 + sub
                g0 = GR[gi % 2]
                proj_fm(wt, sub, g0)
                S.act(hb[gi % 2], ps[:, g0:g0 + TH], AF.Identity)
                emit_gelu(S, u[:, c, :], hb[gi % 2], g1[gi % 2], g2[gi % 2])
                gi += 1
            ws.release(u_u[i])
        wv = [ws.get(k) for k in u_v]
        for tk in range(9):
            vr = g1[tk % 2]
            for cg in range(4):
                r0 = 2048 + ((tk * 4 + cg) % 4) * 512
                for kc in range(NCH):
                    S.mm(ps[:, r0:r0 + 256], x0b[:, kc, tk * 128:(tk + 1) * 128], wv[cg][:, kc, :],
                         start=(kc == 0), stop=(kc == NCH - 1))
                hv = hb[tk % 2][:, cg * 256:(cg + 1) * 256]
                S.act(hv, ps[:, r0:r0 + 256], AF.Identity)
                emit_gelu(S, vr[:, cg * 256:(cg + 1) * 256], hv,
                          g2[0][:, cg * 256:(cg + 1) * 256], g2[1][:, cg * 256:(cg + 1) * 256])
            sq = g2[0]
            S.add("dve", lambda e, vr=vr: e.reduce_sum(out=st[:, 0:1], in_=vr[:, 0:1024], axis=mybir.AxisListType.X),
                  reads=[vr[:, 0:1024]], writes=[st[:, 0:1]])
            S.act(sq[:, 0:1024], vr[:, 0:1024], AF.Square)
            S.add("dve", lambda e, sq=sq: e.reduce_sum(out=st[:, 1:2], in_=sq[:, 0:1024], axis=mybir.AxisListType.X),
                  reads=[sq[:, 0:1024]], writes=[st[:, 1:2]])
            S.ts("dve", st[:, 2:3], st[:, 0:1], 1.0 / 1024, None, ALU.mult)
            S.tt("dve", st[:, 3:4], st[:, 2:3], st[:, 2:3], ALU.mult)
            S.stt(st[:, 4:5], st[:, 1:2], 1.0 / 1024, st[:, 3:4], ALU.mult, ALU.subtract)
            S.act(st[:, 5:6], st[:, 4:5], AF.Sqrt, bias=eps[:, 0:1], scale=1.0)
            S.add("dve", lambda e: e.reciprocal(out=st[:, 6:7], in_=st[:, 5:6]), reads=[st[:, 5:6]],
                  writes=[st[:, 6:7]])
            S.ts("dve", vr[:, 0:1024], vr[:, 0:1024], st[:, 2:3], st[:, 6:7], ALU.subtract, ALU.mult)
            S.tt("dve", vr[:, 0:1024], vr[:, 0:1024], lnv[:, 0, :], ALU.mult)
            S.tt("dve", vt[:, tk, :], vr[:, 0:1024], lnv[:, 1, :], ALU.add)
        ws.release(u_v[3])
        for tk in range(9):
            for half in range(2):
                r0 = 2048 + half * 512
                for hh in range(4):
                    h = half * 4 + hh
                    S.mm(ps[:, r0 + hh * 128:r0 + (hh + 1) * 128], vt[:, tk, h * 128:(h + 1) * 128], wsT[:, h, :],
                         start=True, stop=True)
                tmp = g2[half][:, 0:512].rearrange("p (h t) -> p h t", h=4)
                S.tt("dve", tmp, ps[:, r0:r0 + 512].rearrange("p (h t) -> p h t", h=4),
                     bsT[:, half * 4:half * 4 + 4, :], ALU.add)
                uu = u[:, half * 4:half * 4 + 4, tk * 128:(tk + 1) * 128]
                S.tt("dve", uu, tmp, uu, ALU.mult)
        for g in range(4):
            for oc in range(2):
                g0 = GR[oc]
                for kc in range(2):
                    for (t0, w) in TT3:
                        S.mm(ps[:, g0 + t0:g0 + t0 + w], wp[:, g * 2 + kc, oc * 128:(oc + 1) * 128],
                             pp[:, g * 2 + kc, t0:t0 + w], start=(kc == 0), stop=(kc == 1))
            for oc in range(2):
                g0 = GR[oc]
                c = g * 2 + oc
                S.act(pp[:, c, :], ps[:, g0:g0 + TH], AF.Identity, scale=psc[:, c:c + 1])
        xs = [g1[0], g1[1]]
        chs = [S.new_chan(), S.new_chan()]
        for i in range(8):
            wt = ws.get(u_o[i])
            for sub in range(2):
                oc = i * 2 + sub
                g0 = GR[oc % 2]
                xst = xs[oc % 2]
                S.dma("sp", xst[:, 0:T + 2], x0T[oc * 128:(oc + 1) * 128, 126:TH], chs[oc % 2])
                for kc in range(NCH):
                    src = u[:, kc, :] if kc < 8 else pp[:, kc - 8, :]
                    lw = wt[:, kc, sub * 128:(sub + 1) * 128]
                    S.mm(ps[:, g0 + 510:g0 + 512], lw, src[:, 126:128], start=(kc == 0), stop=(kc == NCH - 1))
                    S.mm(ps[:, g0 + 512:g0 + 1024], lw, src[:, 128:640], start=(kc == 0), stop=(kc == NCH - 1))
                    S.mm(ps[:, g0 + 1024:g0 + 1536], lw, src[:, 640:1152], start=(kc == 0), stop=(kc == NCH - 1))
                S.stt(zf[:, oc, :], xst[:, 0:T + 2], ALPHA, ps[:, g0 + 510:g0 + 1536], ALU.mult, ALU.add)
            ws.release(u_o[i])
        tm_ln1 = {"eps": eps, "mean": g2[0], "rstd": g2[1],
                  "zb": [cv(R2 + k * 2052, (T + 2,), BF16) for k in range(2)],
                  "zs": [cv(R2 + (2 + k) * 2052, (T + 2,), BF16) for k in range(2)]}
        def ln1_post(c):
            S.ts("dve", zf[:, c, 0:2], zf[:, c, 0:2], flag[:, 0:1], None, ALU.mult)
            S.act(xb[:, c, :], zf[:, c, :], AF.Identity)
        emit_ln(S, zf, 0, T + 2, ln1gb[:, 0, :, :], ones, tm_ln1, ps, [lambda c: zf[:, c, :]], post=ln1_post)

        gq = cv(R2, (12, T), BF16)
        ft = [cv(R2 + 24576 + k * 4096, (T,), F32) for k in range(6)]
        tm_ffn = {"a": ft[0:2], "v": ft[2:4], "s": ft[4:6], "eps": eps, "mean": ft[0], "rstd": ft[1],
                  "zb": [cv(R2 + 49152 + k * 2048, (T,), BF16) for k in range(2)],
                  "zs": [cv(R2 + 53248 + k * 2048, (T,), BF16) for k in range(2)]}
        xb2 = cv(R1, (NCH, T), BF16)
        def ln2_l0(tt_):
            c0 = 2 + tt_ * 512
            emit_ln(S, zf, c0, 512, ln2gb[:, 0, :, :], ones, tm_ffn, ps,
                    [lambda c: xb2[:, c, tt_ * 512:(tt_ + 1) * 512], lambda c: zf[:, c, c0:c0 + 512]])
        emit_ffn(S, ws, plan0, zf, xb, cwb[:, 0, :, :], gq, tm_ffn, ps, ln_cb=ln2_l0)
        chsp = [S.new_chan() for _ in range(NCH)]
        for c in range(NCH):
            S.dma("sp", xsp[c * 128:(c + 1) * 128, :], zf[:, c, 2:T + 2], chsp[c])

        def mkset(k):
            o0 = R0 + k * 32768
            d_ = {"A": cv(o0, (T,), F32), "B": cv(o0 + 4096, (T,), F32), "C": cv(o0 + 8192, (T,), F32),
                  "kend": cv(o0 + 12288, (T,), BF16), "kdec": cv(o0 + 14336, (T,), BF16),
                  "qdec": cv(o0 + 16384, (T,), BF16), "ibf": cv(o0 + 18432, (T,), BF16),
                  "sg": cv(o0 + 20480, (T,), BF16), "osq": cv(o0 + 22528, (T,), BF16),
                  "attm": cv(o0 + 24576, (8, 128), BF16), "kt0": cv(o0 + 26624, (8, 128), BF16),
                  "kt1": cv(o0 + 28672, (8, 128), BF16), "vtk": cv(o0 + 30720, (8, 128), BF16),
                  "Sb": cv(R2 + 32768, (NCK, 128), BF16) if k == 0 else cv(R2 + 61472, (NCK, 128), BF16),
                  "dec": sb("dec%d" % k, [128, NCK]), "Sf": [sb("Sf%d_%d" % (k, i), [128, 128]) for i in range(2)],
                  "Pp": [sb("Pp%d_%d" % (k, i), [128, 128]) for i in range(2)],
                  "upst": cv(R2 + 59408, (4, 129), F32) if k == 0 else sb("upst1", [128, 4, 129]),
                  "stg": cv(R2 + 57344, (4, 129), F32),
                  "chu": S.new_chan(), "chst": S.new_chan()}
            return d_
        sets = [mkset(0), mkset(1)]
        y = cv(R2, (NH, T), BF16)
        xs1 = [cv(R2 + 40960, (T,), F32), cv(R2 + 45056, (T,), F32)]
        tm_ln1b = {"eps": eps, "mean": xs1[0], "rstd": xs1[1],
                   "zb": [cv(R2 + 49152 + k * 2048, (T,), BF16) for k in range(2)],
                   "zs": [cv(R2 + 53248 + k * 2048, (T,), BF16) for k in range(2)]}
        chc = S.new_chan(total=True)
        cst = {}
        lbp = sb("lbp_s", [128, 2, NH]); cst["lb"] = sb("lb", [128, NH]); cst["oml"] = sb("oml", [128, NH])
        cst["mask2"] = sb("mask2_s", [128, 128]); identf = sb("identf", [128, 128]); cst["ident"] = sb("ident_s", [128, 128], BF16)
        cst["pm"] = sb("pm", [128, 2])
        cst["rm"] = cv(R2 + 36864, (T,), F32)
        S.dma("sp", lbp[:, :, :], lbp_d.rearrange("p (l h) -> p l h", l=2), chc)
        S.dma("sp", cst["mask2"][:, :], mask2_d, chc)
        S.dma("sp", identf[:, :], ident_d, chc)
        S.copy("dve", cst["ident"][:, :], identf[:, :])
        S.tt("dve", cst["lb"][:, :], lbp[:, 1, :], lbp[:, 0, :], ALU.subtract)
        S.act(cst["lb"][:, :], cst["lb"][:, :], AF.Sigmoid)
        S.ts("dve", cst["oml"][:, :], cst["lb"][:, :], -1.0, 1.0, ALU.mult, ALU.add)
        S.memset("dve", cst["rm"], 1.0)
        S.memset("dve", cst["rm"].rearrange("p (c t) -> p c t", t=CH)[:, :, 0:1], 0.0)
        S.memset("dve", cst["pm"][:, :], 0.0)
        S.memset("dve", cst["pm"][0:64, 0:1], 1.0)
        S.memset("dve", cst["pm"][64:128, 1:2], 1.0)
        oh3 = oh[:, 0:4].rearrange("p (j o) -> p j o", o=1)
        G0, G1, PB5, O0 = 0, 1024, 2560, 3072

        def proj1(wt, blk, g0):
            for kc in range(NCH):
                for t_ in range(2):
                    S.mm(ps[:, g0 + t_ * 512:g0 + (t_ + 1) * 512], wt[:, kc, blk * 128:(blk + 1) * 128],
                         xb2[:, kc, t_ * 512:(t_ + 1) * 512], start=(kc == 0), stop=(kc == NCH - 1))

        def cc_op(idx, src_t, dst_t):
            if use_cc:
                def fn(e):
                    e.collective_compute("AllReduce", ALU.add, replica_groups=SEQ_GROUPS,
                                         ins=[src_t.ap().opt()], outs=[dst_t.ap().opt()]).then_inc(ccsem[idx])
                    return None
                S.add("pool", fn, reads=[src_t.ap()], writes=[])

                def fn2(e):
                    e.wait_ge(ccsem[idx], 1)
                    return e.memset(tiny[:, idx:idx + 1], 0.0)
                return lambda: S.add("pool", fn2, reads=[], writes=[dst_t.ap(), tiny[:, idx:idx + 1]])
            else:
                chq = S.new_chan()
                S.dma("sp", dst_t.ap(), src_t.ap(), chq)
                return lambda: None

        def scan_group(q, g4, st_):
            pb = (2560, 3072, 3584, 2560)[g4]
            for cc in range(4):
                c = g4 * 4 + cc
                j, par = divmod(c, 2)
                S.mm(ps[:, pb + cc * 128:pb + (cc + 1) * 128], (q["kt0"], q["kt1"])[par][:, j, :], q["vtk"][:, j, :],
                     start=True, stop=True)
            for cc in range(4):
                c = g4 * 4 + cc
                cur = st_["cur"]
                S.stt(q["Sf"][1 - cur][:, :], q["Sf"][cur][:, :], q["dec"][:, c:c + 1],
                      ps[:, pb + cc * 128:pb + (cc + 1) * 128], ALU.mult, ALU.add)
                st_["cur"] = 1 - cur
                if st_["sb"] and c + 1 < NCK:
                    S.act(q["Sb"][:, c + 1, :], q["Sf"][1 - cur][:, :], AF.Identity)

        def interleave(bsteps, asteps, after):
            ai = 0
            for bi, bstep in enumerate(bsteps):
                bstep()
                while ai < len(asteps) and after[ai] == bi:
                    asteps[ai]()
                    ai += 1
            while ai < len(asteps):
                asteps[ai]()
                ai += 1

        cc_done = []

        def pre_A(h):
            q = sets[h % 2]

            def a1():
                q["wt"] = ws.get(u_pre[h])
                proj1(q["wt"], 0, G0)

            def a2():
                proj1(q["wt"], 1, G1)
                ws.release(u_pre[h])
                hgrn_gates(S, h, cst, ps[:, G0:G0 + T], q["A"], q["B"], q["C"], q["kend"], q["dec"][:, :])
                S.act(q["ibf"], ps[:, G1:G1 + T], AF.Identity)
                C3 = q["C"].rearrange("p (c t) -> p c t", t=CH)
                S.add("dve", lambda e, C3=C3, h=h: e.reduce_sum(out=Dd[:, h:h + 1], in_=C3[:, :, CH - 1:CH],
                                                               axis=mybir.AxisListType.XY),
                      reads=[q["C"]], writes=[Dd[:, h:h + 1]])
                S.act(Dd[:, h:h + 1], Dd[:, h:h + 1], AF.Exp)
            return [a1, a2]

        def pre_B(h):
            q = sets[h % 2]
            st_ = {"cur": 0, "sb": False}

            def b1():
                hgrn_transposes(S, cst, psb, q["kend"], q["kt0"], q["kt1"])

            def b1b():
                hgrn_transposes(S, cst, psb, q["ibf"], q["vtk"])
                S.memset("dve", q["Sf"][0][:, :], 0.0)

            def bfin():
                fin = st_["cur"]
                for j in range(4):
                    S.ts("dve", q["stg"][:, j, 0:128], q["Sf"][fin][:, :], oh[:, j:j + 1], None, ALU.mult)
                S.ts("dve", q["stg"][:, :, 128:129], oh3, Dd[:, h:h + 1], None, ALU.mult)
                g, hl = divmod(h, 4)
                S.dma("sp", ccin[g].ap().rearrange("(j l d) n -> d j l n", j=4, l=4)[:, :, hl, :], q["stg"][:, :, :],
                      q["chst"])
                if hl == 3:
                    cc_done.append(cc_op(g, ccin[g], ccout[g]))
            return [b1, b1b] + [lambda g4=g4: scan_group(q, g4, st_) for g4 in range(4)] + [bfin]

        for stp in pre_A(0):
            stp()
        for h in range(NH):
            nxt = pre_A(h + 1) if h + 1 < NH else []
            interleave(pre_B(h), nxt, [0, 3])

        def main_A(h):
            q = sets[h % 2]
            g, hl = divmod(h, 4)
            fi, qg = u_main[h]

            def a1():
                if hl == 0:
                    cc_done[g]()
                up = q["upst"]
                S.dma("sp", up[:, :, :], ccout[g].ap().rearrange("(j l d) n -> d j l n", j=4, l=4)[:, :, hl, :], q["chu"])
                Pp_, Sf_ = q["Pp"], q["Sf"]
                S.stt(Pp_[0][:, :], up[:, 0, 0:128], up[:, 1, 128:129], up[:, 1, 0:128], ALU.mult, ALU.add)
                S.stt(Pp_[1][:, :], Pp_[0][:, :], up[:, 2, 128:129], up[:, 2, 0:128], ALU.mult, ALU.add)
                S.ts("dve", Sf_[0][:, :], up[:, 0, 0:128], oh[:, 1:2], None, ALU.mult)
                S.stt(Sf_[0][:, :], Pp_[0][:, :], oh[:, 2:3], Sf_[0][:, :], ALU.mult, ALU.add)
                S.stt(Sf_[0][:, :], Pp_[1][:, :], oh[:, 3:4], Sf_[0][:, :], ALU.mult, ALU.add)
                q["wt"] = ws.get(fi)
                proj1(q["wt"], 0, G0)

            def a2():
                proj1(q["wt"], 1, G1)
                ws.release(fi)
                hgrn_gates(S, h, cst, ps[:, G0:G0 + T], q["A"], q["B"], q["C"], q["kend"], q["dec"][:, :],
                           kdec_bf=q["kdec"], eC=q["A"])
                S.act(q["ibf"], ps[:, G1:G1 + T], AF.Identity)

            def a3():
                q["wt"] = ws.get(qg)
                proj1(q["wt"], 0, G0)

            def a4():
                proj1(q["wt"], 1, G1)
                ws.release(qg)
                S.act(q["B"], ps[:, G0:G0 + T], AF.Silu)
                S.tt("dve", q["qdec"], q["B"], q["A"], ALU.mult)
                S.act(q["sg"], ps[:, G1:G1 + T], AF.Sigmoid)
            return [a1, a2, a3, a4]

        def main_B(h):
            q = sets[h % 2]
            st_ = {"cur": 0, "sb": True}

            def b1():
                hgrn_transposes(S, cst, psb, q["kend"], q["kt0"], q["kt1"])

            def b1b():
                hgrn_transposes(S, cst, psb, q["ibf"], q["vtk"])
                S.act(q["Sb"][:, 0, :], q["Sf"][0][:, :], AF.Identity)

            def batt(half):
                pb = 3072 + half * 512
                for jj in range(4):
                    j = half * 4 + jj
                    S.mm(ps[:, pb + jj * 128:pb + (jj + 1) * 128], q["kdec"][:, j * 128:(j + 1) * 128],
                         q["qdec"][:, j * 128:(j + 1) * 128], start=True, stop=True)
                for jj in range(4):
                    j = half * 4 + jj
                    S.tt("dve", q["attm"][:, j, :], ps[:, pb + jj * 128:pb + (jj + 1) * 128], cst["mask2"][:, :],
                         ALU.mult)

            def bo():
                for j in range(8):
                    S.mm(ps[:, O0 + j * 128:O0 + (j + 1) * 128], q["vtk"][:, j, :], q["attm"][:, j, :], start=True,
                         stop=False)
                    S.mm(ps[:, O0 + j * 128:O0 + j * 128 + 64], q["Sb"][:, 2 * j, :], q["qdec"][:, j * 128:j * 128 + 64],
                         start=False, stop=False)
                    S.mm(ps[:, O0 + j * 128 + 64:O0 + (j + 1) * 128], q["Sb"][:, 2 * j + 1, :],
                         q["qdec"][:, j * 128 + 64:(j + 1) * 128], start=False, stop=True)
                S.act(q["osq"], ps[:, O0:O0 + T], AF.Square)

            def bnorm():
                for t_ in range(2):
                    sl = slice(t_ * 512, (t_ + 1) * 512)
                    S.mm(ps[:, PB5:PB5 + 512], ones[:, :], q["osq"][:, sl], start=True, stop=True)
                    S.act(q["A"][:, sl], ps[:, PB5:PB5 + 512], AF.Ln, bias=eps[:, 0:1], scale=1.0 / 128)
                S.act(q["A"], q["A"], AF.Exp, scale=-0.5)
                S.stt(q["C"], ps[:, O0:O0 + T], gn[:, h:h + 1], q["A"], ALU.mult, ALU.mult)
                S.tt("dve", y[:, h, :], q["C"], q["sg"], ALU.mult)
            return ([b1, b1b] + [lambda g4=g4: scan_group(q, g4, st_) for g4 in range(4)]
                    + [lambda: batt(0), lambda: batt(1), bo, bnorm])

        for stp in main_A(0):
            stp()
        for h in range(NH):
            nxt = main_A(h + 1) if h + 1 < NH else []
            interleave(main_B(h), nxt, [0, 3, 5, 8])
        chs1 = [S.new_chan() for _ in range(4)]
        xs4 = [cv(R2 + 40960 + k * 2048, (512,), F32) for k in range(4)]
        tm_t = {"eps": eps, "mean": cv(R2 + 49152, (512,), F32), "rstd": cv(R2 + 51200, (512,), F32),
                "zb": [cv(R2 + 53248 + k * 1024, (512,), BF16) for k in range(2)],
                "zs": [cv(R2 + 55296 + k * 1024, (512,), BF16) for k in range(2)]}
        OB = (512, 1024, 1536, 2560, 3072, 3584)
        done_h = None
        cnt = 0
        for tile in (1, 0):
            c0 = 2 + tile * 512
            for i in range(8):
                wt = ws.get(u_o1[(1 - tile) * 8 + i])
                for sub in range(2):
                    oc = i * 2 + sub
                    g0 = OB[cnt % 6]
                    xst = xs4[cnt % 4]
                    S.dma("sp", xst, xsp[oc * 128:(oc + 1) * 128, tile * 512:(tile + 1) * 512], chs1[cnt % 4])
                    for kc in range(NCH):
                        S.mm(ps[:, g0:g0 + 512], wt[:, kc, sub * 128:(sub + 1) * 128],
                             y[:, kc, tile * 512:(tile + 1) * 512], start=(kc == 0), stop=(kc == NCH - 1))
                    S.stt(zf[:, oc, c0:c0 + 512], xst, ALPHA, ps[:, g0:g0 + 512], ALU.mult, ALU.add)
                    cnt += 1
                ws.release(u_o1[(1 - tile) * 8 + i])
            emit_ln(S, zf, c0, 512, ln1gb[:, 1, :, :], ones, tm_t, ps,
                    [lambda c, c0=c0: xb[:, c, c0:c0 + 512], lambda c, c0=c0: zf[:, c, c0:c0 + 512]],
                    ps_off=(0, 2048))
            if tile == 1:
                for j in range(4):
                    S.ts("dve", hstg[:, j, :].rearrange("p (c t) -> p c t", t=2), zf[:, :, T:T + 2], oh[:, j:j + 1],
                         None, ALU.mult)
                chh = S.new_chan()
                S.dma("sp", cch_in.ap().rearrange("(j p) n -> p j n", p=128), hstg[:, :, :], chh)
                done_h = cc_op(4, cch_in, cch_out)
        done_h()
        chh2 = S.new_chan()
        S.dma("sp", hld[:, :, :], cch_out.ap().rearrange("(j p) n -> p j n", p=128), chh2)
        S.ts("dve", hsum[:, :], hld[:, 0, :], oh[:, 4:5], None, ALU.mult)
        for j in range(1, 4):
            S.stt(hsum[:, :], hld[:, j, :], oh[:, 4 + j:5 + j], hsum[:, :], ALU.mult, ALU.add)
        S.act(xb[:, :, 0:2], hsum[:, :].rearrange("p (c t) -> p c t", t=2), AF.Identity)
        def ln2_l1(tt_):
            c0 = 2 + tt_ * 512
            emit_ln(S, zf, c0, 512, ln2gb[:, 1, :, :], ones, tm_ffn, ps, [lambda c: zf[:, c, c0:c0 + 512]])
        emit_ffn(S, ws, plan1, zf, xb, cwb[:, 1, :, :], gq, tm_ffn, ps, ln_cb=ln2_l1)
        cho = [S.new_chan() for _ in range(4)]
        for c in range(NCH):
            S.dma("sp", outT[c * 128:(c + 1) * 128, :], zf[:, c, 2:T + 2], cho[c % 4])
        S.emit(nc, es, final_chans=cho)
    return nc, S


def fused_inputs(inp):
    f32 = np.float32
    maps = prep_mix0_inputs(inp["x"], inp["ev_w_in"], inp["ev_ln_v_g"], inp["ev_ln_v_b"], inp["ev_w_s"],
                            inp["ev_b_s"], inp["ev_w_pool"], inp["ev_pool_scale"], inp["ev_w_out"],
                            inp["ln1_g"], inp["ln1_b"])
    ln1gb = np.stack([np.stack([_pm(inp["ln1_g"][l], 16), _pm(inp["ln1_b"][l], 16)], axis=-1) for l in range(2)], axis=1)
    ln2gb = np.stack([np.stack([_pm(inp["ln2_g"][l], 16), _pm(inp["ln2_b"][l], 16)], axis=-1) for l in range(2)], axis=1)
    cwbs = []
    for l in range(2):
        cw = np.asarray(inp["ffn_conv_w"][l], f32)
        cb = np.asarray(inp["ffn_conv_b"][l], f32)
        cwbs.append(np.stack([_pm(cw[0], 88), _pm(cw[1], 88), _pm(cw[2], 88), _pm(cb, 88)], axis=-1))
    cwb = np.stack(cwbs, axis=1)
    common = {
        "ln1gb": np.ascontiguousarray(ln1gb.reshape(128, 64)), "ln2gb": np.ascontiguousarray(ln2gb.reshape(128, 64)),
        "cwb": np.ascontiguousarray(cwb.reshape(128, 2 * 88 * 4)),
        "w_up0": np.ascontiguousarray(inp["ffn_w_up"][0], f32), "w_up1": np.ascontiguousarray(inp["ffn_w_up"][1], f32),
        "w_down0": np.ascontiguousarray(inp["ffn_w_down"][0], f32),
        "w_down1": np.ascontiguousarray(inp["ffn_w_down"][1], f32),
        "od_w_in": regroup_w_in(inp["od_w_in"][0]), "od_w_out": np.ascontiguousarray(inp["od_w_out"][0], f32),
        "gn": _pm(inp["od_norm_g"][0], NH)}
    common.update(hgrn_const_inputs(inp["lb_param"]))
    out = []
    for c in range(NCORES):
        b, s = divmod(c, 4)
        m0 = maps[c]
        m = dict(common)
        for k in ("x0T", "wsT", "maskA", "bsT", "lnv", "w_pool", "pscale", "rcnt", "flag"):
            m[k] = m0[k]
        m["ev_w_in"] = m0["w_in"]
        m["ev_w_out"] = m0["w_out"]
        oh = np.zeros((128, 8), f32)
        oh[:, s] = 1.0
        if s > 0:
            oh[:, 4 + s - 1] = 1.0
        m["oh"] = oh
        out.append(m)
    return out


def kernel(**inputs):
    nc, _ = build_fused(use_cc=True)
    maps = fused_inputs(inputs)
    res = run_bass_kernel_spmd(nc, maps, core_ids=list(range(NCORES))).results
    out = np.zeros((2, 4 * T, D), np.float32)
    for c in range(NCORES):
        b, s = divmod(c, 4)
        out[b, s * T:(s + 1) * T] = res[c]["outT"].T
    return out
```

```python
import numpy as np
from contextlib import ExitStack
import concourse.bass as bass
import concourse.mybir as mybir
from concourse.bass_utils import run_bass_kernel_spmd

F32 = mybir.dt.float32
BF16 = mybir.dt.bfloat16
AF = mybir.ActivationFunctionType
ALU = mybir.AluOpType

D = 2048
NCH = 16
T = 1024
NCORES = 8
DFF = 5632
NFF = 44
ALPHA = 4.0 ** 0.25
LN_EPS = 1e-5
ENGS = ("pe", "act", "dve", "pool", "sp")
_DT_SIZE = {F32: 4, BF16: 2}


def _dsize(dt):
    return _DT_SIZE.get(dt, 4)


class _Op:
    __slots__ = ("eng", "idx", "fn", "deps", "chan", "chan_val", "signal", "val")


class Sched:
    def __init__(self):
        self.ops = {e: [] for e in ENGS}
        self.track = {}
        self.chan_cnt = []
        self.chan_total = []

    @staticmethod
    def _rng(ap):
        t = ap.tensor
        name = t.name
        sp = str(ap.space) if hasattr(ap, "space") else ""
        pat = ap.ap
        esz = _dsize(ap.dtype)
        if "DRAM" in sp.upper() or "Dram" in type(t).__name__ or "DRam" in type(t).__name__:
            ext = 1
            for (st, cnt) in pat:
                ext += abs(st) * (cnt - 1)
            return name, ap.offset * esz, (ap.offset + ext) * esz
        pstride = pat[0][0]
        lo = ap.offset % pstride if pstride > 0 else ap.offset
        ext = 1
        for (st, cnt) in pat[1:]:
            ext += abs(st) * (cnt - 1)
        return name, lo * esz, (lo + ext) * esz

    def _touch(self, name, lo, hi, op, is_write, deps):
        segs = self.track.setdefault(name, [])
        new = []
        covered = []
        for s in segs:
            slo, shi, w, rs = s
            if shi <= lo or slo >= hi:
                new.append(s)
                continue
            if slo < lo:
                new.append([slo, lo, w, list(rs)])
            if shi > hi:
                new.append([hi, shi, w, list(rs)])
            olo, ohi = max(slo, lo), min(shi, hi)
            if w is not None:
                deps.add(w)
            if is_write:
                for r in rs:
                    deps.add(r)
            else:
                covered.append([olo, ohi, w, rs + [op]])
        if is_write:
            new.append([lo, hi, op, []])
        else:
            covered.sort(key=lambda s: s[0])
            cur = lo
            for c in covered:
                if c[0] > cur:
                    new.append([cur, c[0], None, [op]])
                new.append(c)
                cur = c[1]
            if cur < hi:
                new.append([cur, hi, None, [op]])
        self.track[name] = new

    def add(self, eng, fn, reads=(), writes=(), chan=None):
        o = _Op()
        o.eng = eng
        o.fn = fn
        o.chan = chan
        o.signal = False
        o.val = None
        o.chan_val = None
        deps = set()
        for ap in reads:
            if ap is None or isinstance(ap, (int, float)):
                continue
            n, lo, hi = self._rng(ap)
            self._touch(n, lo, hi, o, False, deps)
        for ap in writes:
            n, lo, hi = self._rng(ap)
            if eng == "pe":
                lo = (lo // 2048) * 2048
                hi = ((hi + 2047) // 2048) * 2048
            self._touch(n, lo, hi, o, True, deps)
        deps.discard(o)
        o.deps = deps
        if chan is not None:
            self.chan_cnt[chan] += 1
            o.chan_val = 16 * self.chan_cnt[chan]
        o.idx = len(self.ops[eng])
        self.ops[eng].append(o)
        return o

    def new_chan(self, total=False):
        self.chan_cnt.append(0)
        self.chan_total.append(total)
        return len(self.chan_cnt) - 1

    def emit(self, nc, es, final_chans=()):
        for e in ENGS:
            for o in self.ops[e]:
                for d in o.deps:
                    if d.chan is None:
                        d.signal = True
        for e in ENGS:
            c = 0
            for o in self.ops[e]:
                if o.chan is None and o.signal:
                    c += 1
                    o.val = c
        esem = {e: es.enter_context(nc.semaphore("s_" + e)) for e in ENGS}
        csem = [es.enter_context(nc.semaphore("c_%d" % i)) for i in range(len(self.chan_cnt))]
        block = es.enter_context(nc.Block())
        nwaits = {e: 0 for e in ENGS}

        def run(engname, eobj):
            seen = {}
            for o in self.ops[engname]:
                need = {}
                for d in o.deps:
                    if d.chan is not None:
                        key = ("c", d.chan)
                        v = 16 * self.chan_cnt[d.chan] if self.chan_total[d.chan] else d.chan_val
                    else:
                        if d.eng == engname and engname == "pe":
                            continue
                        key = ("e", d.eng)
                        v = d.val
                    if v > need.get(key, 0):
                        need[key] = v
                for key, v in need.items():
                    if v <= seen.get(key, 0):
                        continue
                    seen[key] = v
                    sem = csem[key[1]] if key[0] == "c" else esem[key[1]]
                    eobj.wait_ge(sem, v)
                    nwaits[engname] += 1
                inst = o.fn(eobj)
                if o.chan is not None:
                    inst.then_inc(csem[o.chan], 16)
                elif o.signal:
                    assert inst is not None
                    inst.then_inc(esem[engname], 1)
            if engname == "sp":
                for ch in final_chans:
                    if self.chan_cnt[ch] > 0:
                        eobj.wait_ge(csem[ch], 16 * self.chan_cnt[ch])

        @block.tensor
        def _(e):
            run("pe", e)

        @block.scalar
        def _(e):
            run("act", e)

        @block.vector
        def _(e):
            run("dve", e)

        @block.gpsimd
        def _(e):
            run("pool", e)

        @block.sync
        def _(e):
            run("sp", e)

        self.nwaits = nwaits

    def mm(self, out, lhsT, rhs, start=True, stop=True):
        return self.add("pe", lambda e: e.matmul(out, lhsT=lhsT, rhs=rhs, start=start, stop=stop),
                        reads=[lhsT, rhs], writes=[out])

    def transpose(self, out, in_, ident):
        return self.add("pe", lambda e: e.transpose(out, in_, ident), reads=[in_, ident], writes=[out])

    def act(self, out, in_, func, bias=None, scale=None):
        kw = {}
        rd = [in_]
        if bias is not None:
            kw["bias"] = bias
            rd.append(bias)
        if scale is not None:
            kw["scale"] = scale
            rd.append(scale)
        return self.add("act", lambda e: e.activation(out=out, in_=in_, func=func, **kw), reads=rd, writes=[out])

    def tt(self, eng, out, in0, in1, op):
        return self.add(eng, lambda e: e.tensor_tensor(out=out, in0=in0, in1=in1, op=op),
                        reads=[in0, in1], writes=[out])

    def ts(self, eng, out, in0, s1, s2, op0, op1=None):
        if op1 is None:
            return self.add(eng, lambda e: e.tensor_scalar(out=out, in0=in0, scalar1=s1, scalar2=None, op0=op0),
                            reads=[in0, s1], writes=[out])
        return self.add(eng, lambda e: e.tensor_scalar(out=out, in0=in0, scalar1=s1, scalar2=s2, op0=op0, op1=op1),
                        reads=[in0, s1, s2], writes=[out])

    def stt(self, out, in0, scalar, in1, op0, op1):
        return self.add("dve", lambda e: e.scalar_tensor_tensor(out=out, in0=in0, scalar=scalar, in1=in1,
                                                                op0=op0, op1=op1),
                        reads=[in0, scalar, in1], writes=[out])

    def copy(self, eng, out, in_):
        if eng == "act":
            return self.add("act", lambda e: e.copy(out=out, in_=in_), reads=[in_], writes=[out])
        return self.add(eng, lambda e: e.tensor_copy(out=out, in_=in_), reads=[in_], writes=[out])

    def memset(self, eng, ap, val):
        return self.add(eng, lambda e: e.memset(ap, val), writes=[ap])

    def dma(self, eng, out, in_, chan):
        return self.add(eng, lambda e: e.dma_start(out=out, in_=in_), reads=[in_], writes=[out], chan=chan)


class WeightStream:
    def __init__(self, S, nc, es, nslots, free_elems, name="wslot"):
        self.S = S
        self.slots = [es.enter_context(nc.sbuf_tensor("%s%d" % (name, i), [128, free_elems], BF16))
                      for i in range(nslots)]
        self.chans = [S.new_chan() for _ in range(nslots)]
        self.uses = []
        self.loaded = 0
        self.released = -1
        self.n = nslots

    def plan(self, shape, src):
        self.uses.append((shape, src))
        return len(self.uses) - 1

    def view(self, k):
        shape, _ = self.uses[k]
        sl = self.slots[k % self.n]
        n = 1
        for s in shape:
            n *= s
        v = sl[:, 0:n]
        if len(shape) == 2:
            return v.rearrange("p (a b) -> p a b", a=shape[0], b=shape[1])
        return v

    def _load_upto(self, k):
        while self.loaded < len(self.uses) and self.loaded <= k:
            j = self.loaded
            _, src = self.uses[j]
            self.S.dma("pool", self.view(j), src, self.chans[j % self.n])
            self.loaded += 1

    def get(self, k):
        assert k <= self.released + self.n, (k, self.released)
        self._load_upto(k)
        return self.view(k)

    def release(self, k):
        self.released = max(self.released, k)
        self._load_upto(self.released + self.n)


FF_QUARTERS = (12, 10, 12, 10)


def plan_ffn_weights(ws, w_up, w_down, split_last=False):
    plan = []
    base = 0
    wu = w_up.rearrange("(kc p) n -> p kc n", p=128)
    for q, nq in enumerate(FF_QUARTERS):
        ups = []
        for j in range(0, nq, 2):
            ca = base + j
            ua = ws.plan((16, 256), wu[:, :, ca * 128:ca * 128 + 256])
            uv = ws.plan((16, 256), wu[:, :, (NFF + ca) * 128:(NFF + ca) * 128 + 256])
            ups.append((ca, ua, uv))
        downs = []
        wd = w_down[base * 128:(base + nq) * 128, :].rearrange("(j p) n -> p j n", p=128)
        reps = 2 if (split_last and q == len(FF_QUARTERS) - 1) else 1
        for rep in range(reps):
            for op_ in range(8):
                downs.append((op_, ws.plan((nq, 256), wd[:, :, op_ * 256:(op_ + 1) * 256])))
        plan.append((base, nq, ups, downs))
        base += nq
    return plan


def emit_ffn(S, ws, plan, xf, xb, cwb, gq, tmps, ps, ln_cb=None):
    G = (0, 1536)
    gi = 0
    for (base, nq, ups, downs) in plan:
        for (ca, ua, uv) in ups:
            wa = ws.get(ua)
            wv = ws.get(uv)
            for sub in range(2):
                c_a = ca + sub
                c_v = NFF + ca + sub
                j = c_a - base
                tm = {}
                for which, (wt, cc) in enumerate(((wa, c_a), (wv, c_v))):
                    g0 = G[which]
                    for kc in range(NCH):
                        lw = wt[:, kc, sub * 128:(sub + 1) * 128]
                        S.mm(ps[:, g0 + 510:g0 + 512], lw, xb[:, kc, 0:2], start=(kc == 0), stop=(kc == NCH - 1))
                        S.mm(ps[:, g0 + 512:g0 + 1024], lw, xb[:, kc, 2:514], start=(kc == 0), stop=(kc == NCH - 1))
                        S.mm(ps[:, g0 + 1024:g0 + 1536], lw, xb[:, kc, 514:1026], start=(kc == 0),
                             stop=(kc == NCH - 1))
                    tmp = tmps["a" if which == 0 else "v"][gi % 2]
                    tm[which] = tmp
                    S.act(tmp[:, :], ps[:, g0 + 512:g0 + 1536], AF.Identity, bias=cwb[:, cc, 3:4],
                          scale=cwb[:, cc, 2:3])
                    S.stt(tmp[:, :], ps[:, g0 + 511:g0 + 1535], cwb[:, cc, 1:2], tmp[:, :], ALU.mult, ALU.add)
                    S.stt(tmp[:, :], ps[:, g0 + 510:g0 + 1534], cwb[:, cc, 0:1], tmp[:, :], ALU.mult, ALU.add)
                sa = tmps["s"][gi % 2]
                S.act(sa[:, :], tm[0][:, :], AF.Silu)
                S.tt("dve", gq[:, j, :], sa[:, :], tm[1][:, :], ALU.mult)
                gi += 1
            ws.release(uv)
        def down_tile(wd, oc, sub, tt_, bank):
            pr = ps[:, 3072 + bank * 512:3072 + (bank + 1) * 512]
            for j in range(nq):
                S.mm(pr, wd[:, j, sub * 128:(sub + 1) * 128], gq[:, j, tt_ * 512:(tt_ + 1) * 512],
                     start=(j == 0), stop=(j == nq - 1))
            dst = xf[:, oc, 2 + tt_ * 512:2 + (tt_ + 1) * 512]
            if base == 0:
                S.stt(dst, dst, ALPHA, pr, ALU.mult, ALU.add)
            else:
                S.tt("dve", dst, dst, pr, ALU.add)
        if len(downs) == 8:
            for (op_, ud) in downs:
                wd = ws.get(ud)
                for sub in range(2):
                    for tt_ in range(2):
                        down_tile(wd, op_ * 2 + sub, sub, tt_, tt_)
                ws.release(ud)
        else:
            for tt_ in range(2):
                for (op_, ud) in downs[tt_ * 8:(tt_ + 1) * 8]:
                    wd = ws.get(ud)
                    for sub in range(2):
                        down_tile(wd, op_ * 2 + sub, sub, tt_, sub)
                    ws.release(ud)
                ln_cb(tt_)


def emit_ln(S, zf, c0, n, gb, ones, tmps, ps, outs, post=None, ps_off=(0, 2048)):
    nt = (n + 511) // 512
    zb = tmps["zb"]
    zs = tmps["zs"]
    for c in range(NCH):
        b0 = zb[c % 2]
        s0 = zs[c % 2]
        S.act(b0[:, 0:n], zf[:, c, c0:c0 + n], AF.Identity)
        S.act(s0[:, 0:n], zf[:, c, c0:c0 + n], AF.Square)
        for t_ in range(nt):
            w = min(512, n - t_ * 512)
            S.mm(ps[:, ps_off[0] + t_ * 512:ps_off[0] + t_ * 512 + w], ones[:, :], b0[:, t_ * 512:t_ * 512 + w],
                 start=(c == 0), stop=(c == NCH - 1))
            S.mm(ps[:, ps_off[1] + t_ * 512:ps_off[1] + t_ * 512 + w], ones[:, :], s0[:, t_ * 512:t_ * 512 + w],
                 start=(c == 0), stop=(c == NCH - 1))
    mean = tmps["mean"]
    rstd = tmps["rstd"]
    S.ts("dve", mean[:, 0:n], ps[:, ps_off[0]:ps_off[0] + n], 1.0 / D, None, ALU.mult)
    S.tt("dve", rstd[:, 0:n], mean[:, 0:n], mean[:, 0:n], ALU.mult)
    S.stt(rstd[:, 0:n], ps[:, ps_off[1]:ps_off[1] + n], 1.0 / D, rstd[:, 0:n], ALU.mult, ALU.subtract)
    S.act(rstd[:, 0:n], rstd[:, 0:n], AF.Sqrt, bias=tmps["eps"][:, 0:1], scale=1.0)
    S.add("dve", lambda e: e.reciprocal(out=rstd[:, 0:n], in_=rstd[:, 0:n]), reads=[rstd[:, 0:n]],
          writes=[rstd[:, 0:n]])
    for c in range(NCH):
        zc = zf[:, c, c0:c0 + n]
        S.tt("dve", zc, zc, mean[:, 0:n], ALU.subtract)
        S.tt("dve", zc, zc, rstd[:, 0:n], ALU.mult)
        for i, dst in enumerate(outs):
            S.act(dst(c), zc, AF.Identity, bias=gb[:, c, 1:2], scale=gb[:, c, 0:1])
        if post is not None:
            post(c)


def build_ffn_launch():
    nc = bass.Bass("TRN2", target_bir_lowering=False)
    xT = nc.dram_tensor("xT", [D, T + 2], F32, kind="ExternalInput").ap()
    w_up = nc.dram_tensor("w_up", [D, 2 * DFF], F32, kind="ExternalInput").ap()
    w_down = nc.dram_tensor("w_down", [DFF, D], F32, kind="ExternalInput").ap()
    cwb_d = nc.dram_tensor("cwb", [128, 88 * 4], F32, kind="ExternalInput").ap()
    gb_d = nc.dram_tensor("ln2gb", [128, 32], F32, kind="ExternalInput").ap()
    yT = nc.dram_tensor("yT", [D, T], F32, kind="ExternalOutput").ap()
    S = Sched()
    with ExitStack() as es:
        xf = es.enter_context(nc.sbuf_tensor("xf", [128, NCH, T + 2], F32))
        xb = es.enter_context(nc.sbuf_tensor("xb", [128, NCH, T + 2], BF16))
        cwb = es.enter_context(nc.sbuf_tensor("cwb_s", [128, 88, 4], F32))
        gb = es.enter_context(nc.sbuf_tensor("gb_s", [128, NCH, 2], F32))
        gq = es.enter_context(nc.sbuf_tensor("gq", [128, 12, T], BF16))
        ones = es.enter_context(nc.sbuf_tensor("ones", [128, 128], BF16))
        eps = es.enter_context(nc.sbuf_tensor("eps", [128, 1], F32))
        tmps = {
            "a": [es.enter_context(nc.sbuf_tensor("ta%d" % i, [128, T], F32)) for i in range(2)],
            "v": [es.enter_context(nc.sbuf_tensor("tv%d" % i, [128, T], F32)) for i in range(2)],
            "s": [es.enter_context(nc.sbuf_tensor("tsl%d" % i, [128, T], F32)) for i in range(2)],
            "eps": eps,
        }
        tmps["zb"] = [es.enter_context(nc.sbuf_tensor("zb%d" % i, [128, T], BF16)) for i in range(2)]
        tmps["zs"] = [es.enter_context(nc.sbuf_tensor("zs%d" % i, [128, T], BF16)) for i in range(2)]
        tmps["mean"] = tmps["a"][0]
        tmps["rstd"] = tmps["a"][1]
        ps = es.enter_context(nc.psum_tensor("ps", [128, 4096], F32))
        ws = WeightStream(S, nc, es, 4, 16 * 256)
        plan = plan_ffn_weights(ws, w_up, w_down)

        ch_in = S.new_chan(total=True)
        ch_p = S.new_chan(total=True)
        ch_out = [S.new_chan() for _ in range(4)]
        S.dma("sp", cwb[:, :, :], cwb_d.rearrange("p (c j) -> p c j", j=4), ch_p)
        S.dma("sp", gb[:, :, :], gb_d.rearrange("p (c j) -> p c j", j=2), ch_p)
        S.memset("dve", ones[:, :], 1.0)
        S.memset("dve", eps[:, :], LN_EPS)
        for c in range(NCH):
            S.dma("sp", xf[:, c, :], xT[c * 128:(c + 1) * 128, :], ch_in)
        for c in range(NCH):
            S.act(xb[:, c, :], xf[:, c, :], AF.Identity)
        emit_ffn(S, ws, plan, xf, xb, cwb, gq, tmps, ps)
        emit_ln(S, xf, 2, T, gb, ones, tmps, ps, [lambda c: xf[:, c, 2:T + 2]])
        for c in range(NCH):
            S.dma("sp", yT[c * 128:(c + 1) * 128, :], xf[:, c, 2:T + 2], ch_out[c % 4])
        S.emit(nc, es, final_chans=ch_out)
    return nc, S


def _pm(v, nch):
    return np.ascontiguousarray(np.asarray(v, np.float32).reshape(nch, 128).T)


def prep_ffn_params(l, ffn_conv_w, ffn_conv_b, ln2_g, ln2_b):
    cw = np.asarray(ffn_conv_w[l], np.float32)
    cb = np.asarray(ffn_conv_b[l], np.float32)
    cwb = np.stack([_pm(cw[0], 88), _pm(cw[1], 88), _pm(cw[2], 88), _pm(cb, 88)], axis=-1)
    gb = np.stack([_pm(ln2_g[l], 16), _pm(ln2_b[l], 16)], axis=-1)
    return np.ascontiguousarray(cwb.reshape(128, 88 * 4)), np.ascontiguousarray(gb.reshape(128, 32))


def run_ffn_launch(x1, l, ffn_w_up, ffn_conv_w, ffn_conv_b, ffn_w_down, ln2_g, ln2_b):
    nc, S = build_ffn_launch()
    cwb, gb = prep_ffn_params(l, ffn_conv_w, ffn_conv_b, ln2_g, ln2_b)
    wu = np.ascontiguousarray(ffn_w_up[l], np.float32)
    wd = np.ascontiguousarray(ffn_w_down[l], np.float32)
    x1 = np.asarray(x1, np.float32)
    in_maps = []
    for c in range(NCORES):
        b, s = divmod(c, 4)
        t0 = s * T
        xt = np.zeros((D, T + 2), np.float32)
        xt[:, 2:] = x1[b, t0:t0 + T].T
        if s > 0:
            xt[:, 0:2] = x1[b, t0 - 2:t0].T
        in_maps.append({"xT": xt, "w_up": wu, "w_down": wd, "cwb": cwb, "ln2gb": gb})
    res = run_bass_kernel_spmd(nc, in_maps, core_ids=list(range(NCORES)))
    out = np.zeros((2, 4096, D), np.float32)
    for c in range(NCORES):
        b, s = divmod(c, 4)
        out[b, s * T:(s + 1) * T] = res.results[c]["yT"].T
    return out


TH = T + 128
B_WINDOWS = (2, 4, 8, 16)
GELU_C = 0.044715
GELU_S = 2.0 * 0.7978845608028654


def emit_gelu(S, dst, src_ps, t1, t2):
    S.act(t1, src_ps, AF.Square)
    S.ts("dve", t1, t1, GELU_C, 1.0, ALU.mult, ALU.add)
    S.tt("dve", t1, t1, src_ps, ALU.mult)
    S.act(t2, t1, AF.Sigmoid, scale=GELU_S)
    S.tt("dve", dst, t2, src_ps, ALU.mult)


def build_mix0_launch():
    nc = bass.Bass("TRN2", target_bir_lowering=False)
    x0T = nc.dram_tensor("x0T", [D, TH], F32, kind="ExternalInput").ap()
    w_in = nc.dram_tensor("w_in", [D, 3072], F32, kind="ExternalInput").ap()
    w_out = nc.dram_tensor("w_out", [D, D], F32, kind="ExternalInput").ap()
    wsT_d = nc.dram_tensor("wsT", [128, 8 * 128], F32, kind="ExternalInput").ap()
    mask_d = nc.dram_tensor("maskA", [128, 128], F32, kind="ExternalInput").ap()
    bsT_d = nc.dram_tensor("bsT", [128, 8 * 128], F32, kind="ExternalInput").ap()
    lnv_d = nc.dram_tensor("lnv", [128, 2 * 1024], F32, kind="ExternalInput").ap()
    wp_d = nc.dram_tensor("w_pool", [4 * 256, 256], F32, kind="ExternalInput").ap()
    psc_d = nc.dram_tensor("pscale", [128, 8], F32, kind="ExternalInput").ap()
    gb_d = nc.dram_tensor("ln1gb", [128, 32], F32, kind="ExternalInput").ap()
    rc_d = nc.dram_tensor("rcnt", [128, 64], F32, kind="ExternalInput").ap()
    flag_d = nc.dram_tensor("flag", [128, 1], F32, kind="ExternalInput").ap()
    x1T = nc.dram_tensor("x1T", [D, T + 2], F32, kind="ExternalOutput").ap()
    S = Sched()
    with ExitStack() as es:
        arena = es.enter_context(nc.sbuf_tensor("arena", [128, NCH * (T + 2)], F32))
        zf = arena[:, :].rearrange("p (c t) -> p c t", c=NCH, t=T + 2)
        x0b = arena[:, 0:NCH * TH // 2].bitcast(BF16).rearrange("p (c t) -> p c t", c=NCH, t=TH)
        u = es.enter_context(nc.sbuf_tensor("u", [128, 8, TH], BF16))
        vt = es.enter_context(nc.sbuf_tensor("vt", [128, 9, 1024], BF16))
        pp = es.enter_context(nc.sbuf_tensor("pp", [128, 8, TH], BF16))
        wsT = es.enter_context(nc.sbuf_tensor("wsT_s", [128, 8, 128], BF16))
        mask = es.enter_context(nc.sbuf_tensor("mask_s", [128, 128], F32))
        bsT = es.enter_context(nc.sbuf_tensor("bsT_s", [128, 8, 128], F32))
        lnv = es.enter_context(nc.sbuf_tensor("lnv_s", [128, 2, 1024], F32))
        wp = es.enter_context(nc.sbuf_tensor("wp_s", [128, 8, 256], BF16))
        psc = es.enter_context(nc.sbuf_tensor("psc_s", [128, 8], F32))
        gb = es.enter_context(nc.sbuf_tensor("gb_s", [128, NCH, 2], F32))
        rc = es.enter_context(nc.sbuf_tensor("rc_s", [128, 4, 16], F32))
        flag = es.enter_context(nc.sbuf_tensor("flag_s", [128, 1], F32))
        ones = es.enter_context(nc.sbuf_tensor("ones", [128, 128], BF16))
        eps = es.enter_context(nc.sbuf_tensor("eps", [128, 1], F32))
        xbf = es.enter_context(nc.sbuf_tensor("xbf", [128, 16 + TH], F32))
        g1 = [es.enter_context(nc.sbuf_tensor("g1_%d" % i, [128, TH], F32)) for i in range(2)]
        g2p = [es.enter_context(nc.sbuf_tensor("g2_%d" % i, [128, 16 + TH], F32)) for i in range(2)]
        g2 = [t[:, 16:16 + TH] for t in g2p]
        tA, tB = g2p
        wsTf = g1[1][:, 0:1024].rearrange("p (h t) -> p h t", h=8)
        st = es.enter_context(nc.sbuf_tensor("st", [128, 8], F32))
        small = es.enter_context(nc.sbuf_tensor("small", [128, 32], F32))
        tmps = {"eps": eps,
                "zb": [g2p[i][:, 16:16 + 513].bitcast(BF16) for i in range(2)],
                "zs": [xbf[:, 16:16 + 513].bitcast(BF16), xbf[:, 600:600 + 513].bitcast(BF16)],
                "mean": g1[0], "rstd": g1[1]}
        ps = es.enter_context(nc.psum_tensor("ps", [128, 4096], F32))
        ws = WeightStream(S, nc, es, 4, 16 * 256)
        wi = w_in.rearrange("(kc p) n -> p kc n", p=128)
        wo = w_out.rearrange("(kc p) n -> p kc n", p=128)
        u_xb = [ws.plan((16, 256), wi[:, :, 2048 + i * 256:2048 + (i + 1) * 256]) for i in range(4)]
        u_u = [ws.plan((16, 256), wi[:, :, i * 256:(i + 1) * 256]) for i in range(4)]
        u_v = [ws.plan((16, 256), wi[:, :, 1024 + i * 256:1024 + (i + 1) * 256]) for i in range(4)]
        u_o = [ws.plan((16, 256), wo[:, :, i * 256:(i + 1) * 256]) for i in range(8)]

        chp = S.new_chan(total=True)
        chx = S.new_chan(total=True)
        S.dma("sp", wsTf, wsT_d.rearrange("p (h t) -> p h t", h=8), chp)
        S.dma("sp", mask[:, :], mask_d, chp)
        S.dma("sp", bsT[:, :, :], bsT_d.rearrange("p (h t) -> p h t", h=8), chp)
        S.dma("sp", lnv[:, :, :], lnv_d.rearrange("p (a c) -> p a c", a=2), chp)
        S.dma("sp", psc[:, :], psc_d, chp)
        S.dma("sp", gb[:, :, :], gb_d.rearrange("p (c j) -> p c j", j=2), chp)
        S.dma("sp", rc[:, :, :], rc_d.rearrange("p (g j) -> p g j", g=4), chp)
        S.dma("sp", flag[:, :], flag_d, chp)
        S.dma("pool", wp[:, :, :], wp_d.rearrange("(a p) n -> p a n", p=128), chx)
        for c in range(NCH):
            S.dma("pool", x0b[:, c, :], x0T[c * 128:(c + 1) * 128, :], chx)
        ws.release(-1)
        S.memset("dve", ones[:, :], 1.0)
        S.memset("dve", eps[:, :], LN_EPS)
        S.memset("dve", xbf[:, 0:16], 0.0)
        S.memset("dve", tA[:, 0:16], 0.0)
        S.memset("dve", tB[:, 0:16], 0.0)
        for h in range(8):
            S.tt("dve", wsT[:, h, :], wsTf[:, h, :], mask[:, :], ALU.mult)

        GR = (0, 1536)
        TT3 = ((0, 512), (512, 512), (1024, 128))

        def proj_fm(wt, sub, g0):
            for kc in range(NCH):
                for (t0, w) in TT3:
                    S.mm(ps[:, g0 + t0:g0 + t0 + w], wt[:, kc, sub * 128:(sub + 1) * 128], x0b[:, kc, t0:t0 + w],
                         start=(kc == 0), stop=(kc == NCH - 1))

        gi = 0
        for i in range(4):
            wt = ws.get(u_xb[i])
            for sub in range(2):
                c = i * 2 + sub
                g = c // 2
                g0 = GR[gi % 2]
                gi += 1
                proj_fm(wt, sub, g0)
                S.act(xbf[:, 16:16 + TH], ps[:, g0:g0 + TH], AF.Identity)
                src = xbf
                dsts = [tA, tB]
                for k in range(g + 1):
                    sh = 1 << k
                    dst = dsts[k % 2]
                    S.tt("dve", dst[:, 16:16 + TH], src[:, 16:16 + TH], src[:, 16 - sh:16 + TH - sh], ALU.add)
                    src = dst
                win = B_WINDOWS[g]
                S.stt(pp[:, c, :], src[:, 16:16 + TH], 1.0 / win, xbf[:, 16:16 + TH], ALU.mult, ALU.subtract)
                S.tt("dve", small[:, 0:16], src[:, 16 + 128:16 + 144], rc[:, g, :], ALU.mult)
                S.tt("dve", pp[:, c, 128:144], small[:, 0:16], xbf[:, 16 + 128:16 + 144], ALU.subtract)
            ws.release(u_xb[i])
        for i in range(4):
            wt = ws.get(u_u[i])
            for sub in range(2):
                c = i * 2 + sub
                g0 = GR[gi % 2]
                proj_fm(wt, sub, g0)
                emit_gelu(S, u[:, c, :], ps[:, g0:g0 + TH], g1[gi % 2][:, :], g2[gi % 2])
                gi += 1
            ws.release(u_u[i])
        wv = [ws.get(k) for k in u_v]
        for tk in range(9):
            vr = g1[tk % 2]
            for cg in range(4):
                r0 = 3072 + ((tk * 4 + cg) % 2) * 512
                for kc in range(NCH):
                    S.mm(ps[:, r0:r0 + 256], x0b[:, kc, tk * 128:(tk + 1) * 128], wv[cg][:, kc, :],
                         start=(kc == 0), stop=(kc == NCH - 1))
                emit_gelu(S, vr[:, cg * 256:(cg + 1) * 256], ps[:, r0:r0 + 256],
                          g2[0][:, cg * 256:(cg + 1) * 256], g2[1][:, cg * 256:(cg + 1) * 256])
            sq = g2[0]
            S.add("dve", lambda e, vr=vr: e.reduce_sum(out=st[:, 0:1], in_=vr[:, 0:1024], axis=mybir.AxisListType.X),
                  reads=[vr[:, 0:1024]], writes=[st[:, 0:1]])
            S.act(sq[:, 0:1024], vr[:, 0:1024], AF.Square)
            S.add("dve", lambda e, sq=sq: e.reduce_sum(out=st[:, 1:2], in_=sq[:, 0:1024], axis=mybir.AxisListType.X),
                  reads=[sq[:, 0:1024]], writes=[st[:, 1:2]])
            S.ts("dve", st[:, 2:3], st[:, 0:1], 1.0 / 1024, None, ALU.mult)
            S.tt("dve", st[:, 3:4], st[:, 2:3], st[:, 2:3], ALU.mult)
            S.stt(st[:, 4:5], st[:, 1:2], 1.0 / 1024, st[:, 3:4], ALU.mult, ALU.subtract)
            S.act(st[:, 5:6], st[:, 4:5], AF.Sqrt, bias=eps[:, 0:1], scale=1.0)
            S.add("dve", lambda e: e.reciprocal(out=st[:, 6:7], in_=st[:, 5:6]), reads=[st[:, 5:6]],
                  writes=[st[:, 6:7]])
            S.ts("dve", vr[:, 0:1024], vr[:, 0:1024], st[:, 2:3], st[:, 6:7], ALU.subtract, ALU.mult)
            S.tt("dve", vr[:, 0:1024], vr[:, 0:1024], lnv[:, 0, :], ALU.mult)
            S.tt("dve", vt[:, tk, :], vr[:, 0:1024], lnv[:, 1, :], ALU.add)
        ws.release(u_v[3])
        for tk in range(9):
            for half in range(2):
                r0 = 2048 + half * 512
                for hh in range(4):
                    h = half * 4 + hh
                    S.mm(ps[:, r0 + hh * 128:r0 + (hh + 1) * 128], vt[:, tk, h * 128:(h + 1) * 128], wsT[:, h, :],
                         start=True, stop=True)
                tmp = g2[half][:, 0:512].rearrange("p (h t) -> p h t", h=4)
                S.tt("dve", tmp, ps[:, r0:r0 + 512].rearrange("p (h t) -> p h t", h=4),
                     bsT[:, half * 4:half * 4 + 4, :], ALU.add)
                uu = u[:, half * 4:half * 4 + 4, tk * 128:(tk + 1) * 128]
                S.tt("dve", uu, tmp, uu, ALU.mult)
        for g in range(4):
            for oc in range(2):
                g0 = GR[oc]
                for kc in range(2):
                    for (t0, w) in TT3:
                        S.mm(ps[:, g0 + t0:g0 + t0 + w], wp[:, g * 2 + kc, oc * 128:(oc + 1) * 128],
                             pp[:, g * 2 + kc, t0:t0 + w], start=(kc == 0), stop=(kc == 1))
            for oc in range(2):
                g0 = GR[oc]
                c = g * 2 + oc
                S.act(pp[:, c, :], ps[:, g0:g0 + TH], AF.Identity, scale=psc[:, c:c + 1])
        xs = [g1[0], g1[1]]
        chs = [S.new_chan(), S.new_chan()]
        for i in range(8):
            wt = ws.get(u_o[i])
            for sub in range(2):
                oc = i * 2 + sub
                g0 = GR[oc % 2]
                xst = xs[oc % 2]
                S.dma("sp", xst[:, 0:T + 2], x0T[oc * 128:(oc + 1) * 128, 126:TH], chs[oc % 2])
                for kc in range(NCH):
                    src = u[:, kc, :] if kc < 8 else pp[:, kc - 8, :]
                    lw = wt[:, kc, sub * 128:(sub + 1) * 128]
                    S.mm(ps[:, g0 + 510:g0 + 512], lw, src[:, 126:128], start=(kc == 0), stop=(kc == NCH - 1))
                    S.mm(ps[:, g0 + 512:g0 + 1024], lw, src[:, 128:640], start=(kc == 0), stop=(kc == NCH - 1))
                    S.mm(ps[:, g0 + 1024:g0 + 1536], lw, src[:, 640:1152], start=(kc == 0), stop=(kc == NCH - 1))
                S.stt(zf[:, oc, :], xst[:, 0:T + 2], ALPHA, ps[:, g0 + 510:g0 + 1536], ALU.mult, ALU.add)
            ws.release(u_o[i])
        emit_ln(S, zf, 0, T + 2, gb, ones, tmps, ps, [lambda c: zf[:, c, :]])
        S.ts("dve", zf[:, :, 0:2], zf[:, :, 0:2], flag[:, 0:1], None, ALU.mult)
        cho = [S.new_chan() for _ in range(4)]
        for c in range(NCH):
            S.dma("sp", x1T[c * 128:(c + 1) * 128, :], zf[:, c, :], cho[c % 4])
        S.emit(nc, es, final_chans=cho)
    return nc, S


def prep_mix0_inputs(x, ev_w_in, ev_ln_v_g, ev_ln_v_b, ev_w_s, ev_b_s, ev_w_pool, ev_pool_scale, ev_w_out,
                     ln1_g, ln1_b):
    x = np.asarray(x, np.float32)
    ws = np.asarray(ev_w_s[0], np.float32)
    wsT = np.ascontiguousarray(ws.transpose(2, 0, 1)).reshape(128, 8 * 128)
    tt_ = np.arange(128)
    maskA = (tt_[None, :] >= tt_[:, None]).astype(np.float32)
    bsT = np.ascontiguousarray(np.broadcast_to(np.asarray(ev_b_s[0], np.float32).reshape(1, 8 * 128), (128, 8 * 128)))
    lnv = np.ascontiguousarray(np.broadcast_to(
        np.concatenate([np.asarray(ev_ln_v_g[0], np.float32), np.asarray(ev_ln_v_b[0], np.float32)])[None, :],
        (128, 2048)))
    wp = np.ascontiguousarray(np.asarray(ev_w_pool[0], np.float32).reshape(4 * 256, 256))
    psc = _pm(ev_pool_scale[0], 8)
    gb = np.ascontiguousarray(np.stack([_pm(ln1_g[0], 16), _pm(ln1_b[0], 16)], axis=-1).reshape(128, 32))
    common = {"w_in": np.ascontiguousarray(ev_w_in[0], np.float32),
              "w_out": np.ascontiguousarray(ev_w_out[0], np.float32),
              "wsT": wsT, "maskA": maskA, "bsT": bsT, "lnv": lnv, "w_pool": wp, "pscale": psc, "ln1gb": gb}
    in_maps = []
    for c in range(NCORES):
        b, s = divmod(c, 4)
        t0 = s * T
        xt = np.zeros((D, TH), np.float32)
        xt[:, 128:] = x[b, t0:t0 + T].T
        if s > 0:
            xt[:, 0:128] = x[b, t0 - 128:t0].T
        rc = np.zeros((4, 16), np.float32)
        for g, win in enumerate(B_WINDOWS):
            pos = np.arange(t0 + 1, t0 + 17, dtype=np.float32)
            rc[g] = 1.0 / np.minimum(pos, float(win))
        rcb = np.ascontiguousarray(np.broadcast_to(rc.reshape(1, 64), (128, 64)))
        m = dict(common)
        m.update({"x0T": xt, "rcnt": rcb, "flag": np.full((128, 1), 1.0 if s > 0 else 0.0, np.float32)})
        in_maps.append(m)
    return in_maps


def run_mix0_launch(inputs):
    nc, S = build_mix0_launch()
    in_maps = prep_mix0_inputs(inputs["x"], inputs["ev_w_in"], inputs["ev_ln_v_g"], inputs["ev_ln_v_b"],
                               inputs["ev_w_s"], inputs["ev_b_s"], inputs["ev_w_pool"], inputs["ev_pool_scale"],
                               inputs["ev_w_out"], inputs["ln1_g"], inputs["ln1_b"])
    res = run_bass_kernel_spmd(nc, in_maps, core_ids=list(range(NCORES)))
    return [res.results[c]["x1T"] for c in range(NCORES)]


NH = 16
CH = 64
NCK = T // CH


def hgrn_consts(S, nc, es, lbp_d, mask2_d, ident_d, chp):
    c = {}
    lbp = es.enter_context(nc.sbuf_tensor("lbp_s", [128, 2, NH], F32))
    c["lb"] = es.enter_context(nc.sbuf_tensor("lb", [128, NH], F32))
    c["oml"] = es.enter_context(nc.sbuf_tensor("oml", [128, NH], F32))
    c["rm"] = es.enter_context(nc.sbuf_tensor("rm", [128, T], F32))
    c["mask2"] = es.enter_context(nc.sbuf_tensor("mask2_s", [128, 128], F32))
    identf = es.enter_context(nc.sbuf_tensor("identf", [128, 128], F32))
    c["ident"] = es.enter_context(nc.sbuf_tensor("ident_s", [128, 128], BF16))
    S.dma("sp", lbp[:, :, :], lbp_d.rearrange("p (l h) -> p l h", l=2), chp)
    S.dma("sp", c["mask2"][:, :], mask2_d, chp)
    S.dma("sp", identf[:, :], ident_d, chp)
    S.copy("dve", c["ident"][:, :], identf[:, :])
    S.tt("dve", c["lb"][:, :], lbp[:, 1, :], lbp[:, 0, :], ALU.subtract)
    S.act(c["lb"][:, :], c["lb"][:, :], AF.Sigmoid)
    S.ts("dve", c["oml"][:, :], c["lb"][:, :], -1.0, 1.0, ALU.mult, ALU.add)
    S.memset("dve", c["rm"][:, :], 1.0)
    S.memset("dve", c["rm"][:, :].rearrange("p (c t) -> p c t", t=CH)[:, :, 0:1], 0.0)
    c["pm"] = es.enter_context(nc.sbuf_tensor("pm", [128, 2], F32))
    c["one"] = es.enter_context(nc.sbuf_tensor("one_c", [128, 1], F32))
    S.memset("dve", c["one"][:, :], 1.0)
    S.memset("dve", c["pm"][:, :], 0.0)
    S.memset("dve", c["pm"][0:64, 0:1], 1.0)
    S.memset("dve", c["pm"][64:128, 1:2], 1.0)
    return c


def hgrn_gates(S, h, cst, f_ps, A, B, C, kend_bf, dec, kdec_bf=None, eC=False):
    oml = cst["oml"][:, h:h + 1]
    lb = cst["lb"][:, h:h + 1]
    S.act(A, f_ps, AF.Sigmoid)
    S.act(A, A, AF.Identity, bias=lb, scale=oml)
    S.act(B, A, AF.Ln)
    rm_ap = cst["rm"][:, :]
    S.add("dve", lambda e: e.tensor_tensor_scan(out=C, data0=rm_ap, data1=B, initial=0.0, op0=ALU.mult, op1=ALU.add),
          reads=[rm_ap, B], writes=[C])
    S.act(A, A, AF.Identity, bias=cst["one"][:, 0:1], scale=-1.0)
    S.act(B, C, AF.Exp, scale=-1.0)
    C3 = C.rearrange("p (c t) -> p c t", t=CH)
    S.act(dec.rearrange("p (c o) -> p c o", o=1), C3[:, :, CH - 1:CH], AF.Exp)
    if eC:
        S.act(C, C, AF.Exp)
    S.tt("dve", B, A, B, ALU.mult)
    S.tt("dve", kend_bf.rearrange("p (c t) -> p c t", t=CH), B.rearrange("p (c t) -> p c t", t=CH),
         dec.rearrange("p (c o) -> p c o", o=1).to_broadcast([128, NCK, CH]), ALU.mult)
    if kdec_bf is not None:
        S.copy("dve", kdec_bf, B)


def hgrn_transposes(S, cst, psb, src_bf, dst_tok, dst_tok1=None):
    for j in range(8):
        S.transpose(psb[:, 4096 + j * 128:4096 + (j + 1) * 128], src_bf[:, j * 128:(j + 1) * 128], cst["ident"][:, :])
    if dst_tok1 is None:
        S.act(dst_tok.rearrange("p j d -> p (j d)"), psb[:, 4096:5120], AF.Identity)
    else:
        S.act(dst_tok.rearrange("p j d -> p (j d)"), psb[:, 4096:5120], AF.Identity, scale=cst["pm"][:, 0:1])
        S.act(dst_tok1.rearrange("p j d -> p (j d)"), psb[:, 4096:5120], AF.Identity, scale=cst["pm"][:, 1:2])


def hgrn_state_scan(S, ps, kt, vtk, dec, Sf, Sb=None):
    cur = 0
    if Sb is not None:
        S.act(Sb[:, 0, :], Sf[0][:, :], AF.Identity)
    for g4 in range(4):
        for cc in range(4):
            c = g4 * 4 + cc
            j, par = divmod(c, 2)
            S.mm(ps[:, 2560 + cc * 128:2560 + (cc + 1) * 128], kt[par][:, j, :], vtk[:, j, :], start=True, stop=True)
        for cc in range(4):
            c = g4 * 4 + cc
            nxt = 1 - cur
            S.stt(Sf[nxt][:, :], Sf[cur][:, :], dec[:, c:c + 1], ps[:, 2560 + cc * 128:2560 + (cc + 1) * 128],
                  ALU.mult, ALU.add)
            cur = nxt
            if Sb is not None and c + 1 < NCK:
                S.act(Sb[:, c + 1, :], Sf[cur][:, :], AF.Identity)
    return cur


def build_hgrn_pre_launch(stage=9, nheads=NH):
    nc = bass.Bass("TRN2", target_bir_lowering=False)
    xT = nc.dram_tensor("xT", [D, T], F32, kind="ExternalInput").ap()
    w_in = nc.dram_tensor("w_in", [D, 4 * D], F32, kind="ExternalInput").ap()
    lbp_d = nc.dram_tensor("lbp", [128, 2 * NH], F32, kind="ExternalInput").ap()
    mask2_d = nc.dram_tensor("mask2", [128, 128], F32, kind="ExternalInput").ap()
    ident_d = nc.dram_tensor("ident", [128, 128], F32, kind="ExternalInput").ap()
    U_d = nc.dram_tensor("U", [NH, 128, 128], F32, kind="ExternalOutput").ap()
    D_d = nc.dram_tensor("Dd", [128, NH], F32, kind="ExternalOutput").ap()
    S = Sched()
    with ExitStack() as es:
        xb = es.enter_context(nc.sbuf_tensor("xb", [128, NCH, T], BF16))
        A = [es.enter_context(nc.sbuf_tensor("A%d" % i, [128, T], F32)) for i in range(2)]
        B = [es.enter_context(nc.sbuf_tensor("B%d" % i, [128, T], F32)) for i in range(2)]
        C = [es.enter_context(nc.sbuf_tensor("C%d" % i, [128, T], F32)) for i in range(2)]
        kend = [es.enter_context(nc.sbuf_tensor("kend%d" % i, [128, T], BF16)) for i in range(2)]
        ibf = [es.enter_context(nc.sbuf_tensor("ibf%d" % i, [128, T], BF16)) for i in range(2)]
        kt = [[es.enter_context(nc.sbuf_tensor("kt%d_%d" % (i, k), [128, 8, 128], BF16)) for k in range(2)]
              for i in range(2)]
        vtk = [es.enter_context(nc.sbuf_tensor("vtk%d" % i, [128, 8, 128], BF16)) for i in range(2)]
        dec = [es.enter_context(nc.sbuf_tensor("dec%d" % i, [128, NCK], F32)) for i in range(2)]
        Sf = [[es.enter_context(nc.sbuf_tensor("Sf%d_%d" % (i, k), [128, 128], F32)) for k in range(2)]
              for i in range(2)]
        Dd = es.enter_context(nc.sbuf_tensor("Dd_s", [128, NH], F32))
        ps = es.enter_context(nc.psum_tensor("ps", [128, 4096], F32))
        psb = ps[:, :].bitcast(BF16)
        chp = S.new_chan(total=True)
        chx = S.new_chan(total=True)
        cst = hgrn_consts(S, nc, es, lbp_d, mask2_d, ident_d, chp)
        ws = WeightStream(S, nc, es, 4, 16 * 256)
        wi = w_in.rearrange("(kc p) n -> p kc n", p=128)
        uses = [ws.plan((16, 256), wi[:, :, h * 512 + 256:h * 512 + 512]) for h in range(NH)]
        for c in range(NCH):
            S.dma("pool", xb[:, c, :], xT[c * 128:(c + 1) * 128, :], chx)
        ws.release(-1)
        cho = [S.new_chan() for _ in range(2)]
        for h in range(nheads):
            b = h % 2
            wt = ws.get(uses[h])
            for which in range(2):
                g0 = which * 1024
                for kc in range(NCH):
                    for t_ in range(2):
                        S.mm(ps[:, g0 + t_ * 512:g0 + (t_ + 1) * 512], wt[:, kc, which * 128:(which + 1) * 128],
                             xb[:, kc, t_ * 512:(t_ + 1) * 512], start=(kc == 0), stop=(kc == NCH - 1))
            ws.release(uses[h])
            hgrn_gates(S, h, cst, ps[:, 0:1024], A[b][:, :], B[b][:, :], C[b][:, :], kend[b][:, :], dec[b][:, :])
            S.act(ibf[b][:, :], ps[:, 1024:2048], AF.Identity)
            C3 = C[b][:, :].rearrange("p (c t) -> p c t", t=CH)
            S.add("dve", lambda e, C3=C3, h=h: e.reduce_sum(out=Dd[:, h:h + 1], in_=C3[:, :, CH - 1:CH],
                                                           axis=mybir.AxisListType.XY),
                  reads=[C[b][:, :]], writes=[Dd[:, h:h + 1]])
            S.act(Dd[:, h:h + 1], Dd[:, h:h + 1], AF.Exp)
            if stage >= 2:
                hgrn_transposes(S, cst, psb, kend[b][:, :], kt[b][0][:, :, :], kt[b][1][:, :, :])
                hgrn_transposes(S, cst, psb, ibf[b][:, :], vtk[b][:, :, :])
            S.memset("dve", Sf[b][0][:, :], 0.0)
            fin = 0
            if stage >= 3:
                fin = hgrn_state_scan(S, ps, kt[b], vtk[b], dec[b], Sf[b])
            S.dma("sp", U_d[h, :, :], Sf[b][fin][:, :], cho[b])
        chd = S.new_chan()
        S.dma("sp", D_d, Dd[:, :], chd)
        S.emit(nc, es, final_chans=cho + [chd])
    return nc, S


def regroup_w_in(od_w_in):
    w = np.asarray(od_w_in, np.float32).reshape(D, 4, NH, 128)
    w = w[:, [0, 3, 1, 2]]
    return np.ascontiguousarray(w.transpose(0, 2, 1, 3).reshape(D, 4 * D))


def hgrn_const_inputs(lb_param):
    lbp = np.ascontiguousarray(np.stack([_pm(lb_param[0], NH), _pm(lb_param[1], NH)], axis=1).reshape(128, 2 * NH))
    i = np.arange(128)
    mask2 = ((i[None, :] >= i[:, None]) & ((i[None, :] // CH) == (i[:, None] // CH))).astype(np.float32)
    return {"lbp": lbp, "mask2": mask2, "ident": np.eye(128, dtype=np.float32)}


def _carve(arena, off_bytes, n_elems, dt):
    assert off_bytes % 4 == 0
    nb = n_elems * _dsize(dt)
    assert nb % 4 == 0
    v = arena[:, off_bytes // 4:(off_bytes + nb) // 4]
    return v if dt == F32 else v.bitcast(dt)


def build_hgrn_main_launch():
    nc = bass.Bass("TRN2", target_bir_lowering=False)
    xT = nc.dram_tensor("xT", [D, T], F32, kind="ExternalInput").ap()
    w_in = nc.dram_tensor("w_in", [D, 4 * D], F32, kind="ExternalInput").ap()
    w_out = nc.dram_tensor("w_out", [D, D], F32, kind="ExternalInput").ap()
    lbp_d = nc.dram_tensor("lbp", [128, 2 * NH], F32, kind="ExternalInput").ap()
    mask2_d = nc.dram_tensor("mask2", [128, 128], F32, kind="ExternalInput").ap()
    ident_d = nc.dram_tensor("ident", [128, 128], F32, kind="ExternalInput").ap()
    up_d = nc.dram_tensor("Uprev", [3, NH, 128, 128], F32, kind="ExternalInput").ap()
    dp_d = nc.dram_tensor("Dprev", [128, 3 * NH], F32, kind="ExternalInput").ap()
    gn_d = nc.dram_tensor("gn", [128, NH], F32, kind="ExternalInput").ap()
    gb_d = nc.dram_tensor("ln1gb", [128, 32], F32, kind="ExternalInput").ap()
    x1T = nc.dram_tensor("x1T", [D, T], F32, kind="ExternalOutput").ap()
    S = Sched()
    with ExitStack() as es:
        arena = es.enter_context(nc.sbuf_tensor("arena", [128, NCH * T], F32))
        zf = arena[:, :].rearrange("p (c t) -> p c t", c=NCH, t=T)
        xb = _carve(arena, 0, NCH * T, BF16).rearrange("p (c t) -> p c t", c=NCH, t=T)
        off = NCH * T * 2
        A = _carve(arena, off, T, F32); off += 4 * T
        B = _carve(arena, off, T, F32); off += 4 * T
        C = _carve(arena, off, T, F32); off += 4 * T
        kend = _carve(arena, off, T, BF16); off += 2 * T
        kdec = _carve(arena, off, T, BF16); off += 2 * T
        qdec = _carve(arena, off, T, BF16); off += 2 * T
        ibf = _carve(arena, off, T, BF16); off += 2 * T
        sg = _carve(arena, off, T, BF16); off += 2 * T
        osq = _carve(arena, off, T, BF16); off += 2 * T
        attm = _carve(arena, off, T, BF16).rearrange("p (j t) -> p j t", j=8); off += 2 * T
        kt0 = _carve(arena, off, T, BF16).rearrange("p (j t) -> p j t", j=8); off += 2 * T
        kt1 = _carve(arena, off, T, BF16).rearrange("p (j t) -> p j t", j=8); off += 2 * T
        kt = (kt0, kt1)
        vtk = _carve(arena, off, T, BF16).rearrange("p (j t) -> p j t", j=8); off += 2 * T
        assert off <= NCH * T * 4
        y = es.enter_context(nc.sbuf_tensor("y", [128, NH, T], BF16))
        Sb = es.enter_context(nc.sbuf_tensor("Sb", [128, NCK, 128], BF16))
        Sf = [es.enter_context(nc.sbuf_tensor("Sf%d" % k, [128, 128], F32)) for k in range(2)]
        upst = es.enter_context(nc.sbuf_tensor("upst", [128, 3, 128], F32))
        dec = es.enter_context(nc.sbuf_tensor("dec", [128, NCK], F32))
        dp = es.enter_context(nc.sbuf_tensor("dp", [128, 3, NH], F32))
        gn = es.enter_context(nc.sbuf_tensor("gn_s", [128, NH], F32))
        gb = es.enter_context(nc.sbuf_tensor("gb_s", [128, NCH, 2], F32))
        ones = es.enter_context(nc.sbuf_tensor("ones", [128, 128], BF16))
        eps = es.enter_context(nc.sbuf_tensor("eps", [128, 1], F32))
        xs = [es.enter_context(nc.sbuf_tensor("xs%d" % i, [128, T], F32)) for i in range(2)]
        tmps = {"eps": eps,
                "zb": [es.enter_context(nc.sbuf_tensor("zb%d" % i, [128, T], BF16)) for i in range(2)],
                "zs": [es.enter_context(nc.sbuf_tensor("zs%d" % i, [128, T], BF16)) for i in range(2)],
                "mean": xs[0], "rstd": xs[1]}
        ps = es.enter_context(nc.psum_tensor("ps", [128, 4096], F32))
        psb = ps[:, :].bitcast(BF16)
        chp = S.new_chan(total=True)
        chx = S.new_chan(total=True)
        cst = hgrn_consts(S, nc, es, lbp_d, mask2_d, ident_d, chp)
        S.dma("sp", dp[:, :, :], dp_d.rearrange("p (j h) -> p j h", j=3), chp)
        S.dma("sp", gn[:, :], gn_d, chp)
        S.dma("sp", gb[:, :, :], gb_d.rearrange("p (c j) -> p c j", j=2), chp)
        S.memset("dve", ones[:, :], 1.0)
        S.memset("dve", eps[:, :], LN_EPS)
        ws = WeightStream(S, nc, es, 3, 16 * 512)
        wi = w_in.rearrange("(kc p) n -> p kc n", p=128)
        wo = w_out.rearrange("(kc p) n -> p kc n", p=128)
        uses = [ws.plan((16, 512), wi[:, :, h * 512:(h + 1) * 512]) for h in range(NH)]
        u_o = [ws.plan((16, 512), wo[:, :, i * 512:(i + 1) * 512]) for i in range(4)]
        for c in range(NCH):
            S.dma("pool", xb[:, c, :], xT[c * 128:(c + 1) * 128, :], chx)
        ws.release(-1)
        chu = S.new_chan()
        G0, G1 = 0, 1024
        for h in range(NH):
            wt = ws.get(uses[h])

            def proj(blk, g0):
                for kc in range(NCH):
                    for t_ in range(2):
                        S.mm(ps[:, g0 + t_ * 512:g0 + (t_ + 1) * 512], wt[:, kc, blk * 128:(blk + 1) * 128],
                             xb[:, kc, t_ * 512:(t_ + 1) * 512], start=(kc == 0), stop=(kc == NCH - 1))
            S.dma("sp", upst[:, :, :], up_d[:, h, :, :].rearrange("j d e -> d j e"), chu)
            S.memset("dve", Sf[0][:, :], 0.0)
            cur = 0
            for j in range(3):
                S.stt(Sf[1 - cur][:, :], Sf[cur][:, :], dp[:, j, h:h + 1], upst[:, j, :], ALU.mult, ALU.add)
                cur = 1 - cur
            Sfl = [Sf[cur], Sf[1 - cur]]
            proj(2, G0)
            proj(3, G1)
            hgrn_gates(S, h, cst, ps[:, G0:G0 + T], A, B, C, kend, dec[:, :], kdec_bf=kdec, eC=True)
            S.act(ibf, ps[:, G1:G1 + T], AF.Identity)
            proj(0, G0)
            proj(1, G1)
            ws.release(uses[h])
            S.act(B, ps[:, G0:G0 + T], AF.Silu)
            S.tt("dve", qdec, B, C, ALU.mult)
            S.act(sg, ps[:, G1:G1 + T], AF.Sigmoid)
            hgrn_transposes(S, cst, psb, kend, kt0, kt1)
            hgrn_transposes(S, cst, psb, ibf, vtk)
            hgrn_state_scan(S, ps, kt, vtk, dec, Sfl, Sb=Sb)
            for half in range(2):
                for jj in range(4):
                    j = half * 4 + jj
                    S.mm(ps[:, 2560 + jj * 128:2560 + (jj + 1) * 128], kdec[:, j * 128:(j + 1) * 128],
                         qdec[:, j * 128:(j + 1) * 128], start=True, stop=True)
                for jj in range(4):
                    j = half * 4 + jj
                    S.tt("dve", attm[:, j, :], ps[:, 2560 + jj * 128:2560 + (jj + 1) * 128], cst["mask2"][:, :],
                         ALU.mult)
            O0 = 3072
            for j in range(8):
                S.mm(ps[:, O0 + j * 128:O0 + (j + 1) * 128], vtk[:, j, :], attm[:, j, :], start=True, stop=False)
                S.mm(ps[:, O0 + j * 128:O0 + j * 128 + 64], Sb[:, 2 * j, :], qdec[:, j * 128:j * 128 + 64],
                     start=False, stop=False)
                S.mm(ps[:, O0 + j * 128 + 64:O0 + (j + 1) * 128], Sb[:, 2 * j + 1, :],
                     qdec[:, j * 128 + 64:(j + 1) * 128], start=False, stop=True)
            S.act(osq, ps[:, O0:O0 + T], AF.Square)
            for t_ in range(2):
                S.mm(ps[:, G0 + t_ * 512:G0 + (t_ + 1) * 512], ones[:, :], osq[:, t_ * 512:(t_ + 1) * 512],
                     start=True, stop=True)
            S.act(A, ps[:, G0:G0 + T], AF.Ln, bias=eps[:, 0:1], scale=1.0 / 128)
            S.act(A, A, AF.Exp, scale=-0.5)
            S.stt(C, ps[:, O0:O0 + T], gn[:, h:h + 1], A, ALU.mult, ALU.mult)
            S.tt("dve", y[:, h, :], C, sg, ALU.mult)
        chs = [S.new_chan(), S.new_chan()]
        for i in range(4):
            wt = ws.get(u_o[i])
            for sub in range(4):
                oc = i * 4 + sub
                g0 = (oc % 2) * 1024
                xst = xs[oc % 2]
                S.dma("sp", xst[:, :], xT[oc * 128:(oc + 1) * 128, :], chs[oc % 2])
                for kc in range(NCH):
                    for t_ in range(2):
                        S.mm(ps[:, g0 + t_ * 512:g0 + (t_ + 1) * 512], wt[:, kc, sub * 128:(sub + 1) * 128],
                             y[:, kc, t_ * 512:(t_ + 1) * 512], start=(kc == 0), stop=(kc == NCH - 1))
                S.stt(zf[:, oc, :], xst[:, :], ALPHA, ps[:, g0:g0 + T], ALU.mult, ALU.add)
            ws.release(u_o[i])
        emit_ln(S, zf, 0, T, gb, ones, tmps, ps, [lambda c: zf[:, c, :]])
        cho = [S.new_chan() for _ in range(4)]
        for c in range(NCH):
            S.dma("sp", x1T[c * 128:(c + 1) * 128, :], zf[:, c, :], cho[c % 4])
        S.emit(nc, es, final_chans=cho)
    return nc, S


def _run(nc, in_maps):
    return run_bass_kernel_spmd(nc, in_maps, core_ids=list(range(NCORES))).results


def kernel_unfused(x, ev_w_in, ev_ln_v_g, ev_ln_v_b, ev_w_s, ev_b_s, ev_w_pool, ev_pool_scale,
           ev_w_out, od_w_in, od_norm_g, od_w_out, lb_param, ffn_w_up, ffn_conv_w,
           ffn_conv_b, ffn_w_down, ln1_g, ln1_b, ln2_g, ln2_b):
    f32 = np.float32
    nc0, _ = build_mix0_launch()
    maps0 = prep_mix0_inputs(x, ev_w_in, ev_ln_v_g, ev_ln_v_b, ev_w_s, ev_b_s, ev_w_pool, ev_pool_scale,
                             ev_w_out, ln1_g, ln1_b)
    r0 = _run(nc0, maps0)
    x1T = [r0[c]["x1T"] for c in range(NCORES)]
    ncf, _ = build_ffn_launch()

    def ffn(l, xTs):
        cwb, gb = prep_ffn_params(l, ffn_conv_w, ffn_conv_b, ln2_g, ln2_b)
        wu = np.ascontiguousarray(ffn_w_up[l], f32)
        wd = np.ascontiguousarray(ffn_w_down[l], f32)
        maps = [{"xT": np.ascontiguousarray(xTs[c], f32), "w_up": wu, "w_down": wd, "cwb": cwb, "ln2gb": gb}
                for c in range(NCORES)]
        r = _run(ncf, maps)
        return [r[c]["yT"] for c in range(NCORES)]

    x2T = ffn(0, x1T)
    ncp, _ = build_hgrn_pre_launch()
    w_in_r = regroup_w_in(od_w_in[0])
    hc = hgrn_const_inputs(lb_param)
    mapsp = []
    for c in range(NCORES):
        m = {"xT": np.ascontiguousarray(x2T[c], f32), "w_in": w_in_r}
        m.update(hc)
        mapsp.append(m)
    rp = _run(ncp, mapsp)
    ncm, _ = build_hgrn_main_launch()
    gn = _pm(od_norm_g[0], NH)
    gb1 = np.ascontiguousarray(np.stack([_pm(ln1_g[1], 16), _pm(ln1_b[1], 16)], axis=-1).reshape(128, 32))
    w_out1 = np.ascontiguousarray(od_w_out[0], f32)
    mapsm = []
    for c in range(NCORES):
        b, s = divmod(c, 4)
        up = np.zeros((3, NH, 128, 128), f32)
        dp = np.zeros((128, 3, NH), f32)
        for j in range(s):
            pos = 3 - s + j
            up[pos] = rp[b * 4 + j]["U"]
            dp[:, pos, :] = rp[b * 4 + j]["Dd"]
        m = {"xT": np.ascontiguousarray(x2T[c], f32), "w_in": w_in_r, "w_out": w_out1, "Uprev": up,
             "Dprev": np.ascontiguousarray(dp.reshape(128, 3 * NH)), "gn": gn, "ln1gb": gb1}
        m.update(hc)
        mapsm.append(m)
    rm = _run(ncm, mapsm)
    x1bT = []
    for c in range(NCORES):
        b, s = divmod(c, 4)
        xt = np.zeros((D, T + 2), f32)
        xt[:, 2:] = rm[c]["x1T"]
        if s > 0:
            xt[:, 0:2] = rm[c - 1]["x1T"][:, T - 2:T]
        x1bT.append(xt)
    outT = ffn(1, x1bT)
    out = np.zeros((2, 4 * T, D), f32)
    for c in range(NCORES):
        b, s = divmod(c, 4)
        out[b, s * T:(s + 1) * T] = outT[c].T
    return out


R0 = 0
R0_SZ = NCH * (T + 2) * 4
R1 = R0 + R0_SZ
R1_SZ = NCH * (T + 2) * 2
R2 = R1 + R1_SZ
R2_SZ = 66560
AR_BYTES = R2 + R2_SZ
SEQ_GROUPS = [[0, 1, 2, 3], [4, 5, 6, 7]]


def build_fused(use_cc=True):
    nc = bass.Bass("TRN2", target_bir_lowering=False)

    def din(name, shape):
        return nc.dram_tensor(name, shape, F32, kind="ExternalInput").ap()
    x0T = din("x0T", [D, TH])
    ev_w_in = din("ev_w_in", [D, 3072])
    ev_w_out = din("ev_w_out", [D, D])
    wsT_d = din("wsT", [128, 8 * 128])
    mask_d = din("maskA", [128, 128])
    bsT_d = din("bsT", [128, 8 * 128])
    lnv_d = din("lnv", [128, 2 * 1024])
    wp_d = din("w_pool", [4 * 256, 256])
    psc_d = din("pscale", [128, 8])
    rc_d = din("rcnt", [128, 64])
    flag_d = din("flag", [128, 1])
    oh_d = din("oh", [128, 8])
    ln1gb_d = din("ln1gb", [128, 64])
    ln2gb_d = din("ln2gb", [128, 64])
    cwb_d = din("cwb", [128, 2 * 88 * 4])
    w_up = [din("w_up%d" % l, [D, 2 * DFF]) for l in range(2)]
    w_down = [din("w_down%d" % l, [DFF, D]) for l in range(2)]
    od_w_in = din("od_w_in", [D, 4 * D])
    od_w_out = din("od_w_out", [D, D])
    lbp_d = din("lbp", [128, 2 * NH])
    mask2_d = din("mask2", [128, 128])
    ident_d = din("ident", [128, 128])
    gn_d = din("gn", [128, NH])
    outT = nc.dram_tensor("outT", [D, T], F32, kind="ExternalOutput").ap()
    xsp = nc.dram_tensor("xsp", [D, T], F32).ap()
    ccin = [nc.dram_tensor("ccin%d" % g, [4 * 4 * 128, 129], F32) for g in range(4)]
    ccout = [nc.dram_tensor("ccout%d" % g, [4 * 4 * 128, 129], F32) for g in range(4)]
    cch_in = nc.dram_tensor("cch_in", [4 * 128, 32], F32)
    cch_out = nc.dram_tensor("cch_out", [4 * 128, 32], F32)

    S = Sched()
    with ExitStack() as es:
        AR = es.enter_context(nc.sbuf_tensor("AR", [128, AR_BYTES // 4], F32))

        def cv(off, shape, dt):
            n = 1
            for k in shape:
                n *= k
            v = _carve(AR, off, n, dt)
            if len(shape) == 2:
                v = v.rearrange("p (a b) -> p a b", a=shape[0], b=shape[1])
            return v

        def sb(name, shape, dt=F32):
            return es.enter_context(nc.sbuf_tensor(name, shape, dt))
        psc = sb("psc_s", [128, 8]); rc = sb("rc_s", [128, 4, 16]); flag = sb("flag_s", [128, 1])
        oh = sb("oh_s", [128, 8]); ln1gb = sb("ln1gb_s", [128, 2, NCH, 2]); ln2gb = sb("ln2gb_s", [128, 2, NCH, 2])
        cwb = sb("cwb_s", [128, 2, 88, 4]); ones = sb("ones", [128, 128], BF16); eps = sb("eps", [128, 1])
        st = sb("st", [128, 8]); small = sb("small", [128, 32]); gn = sb("gn_s", [128, NH])
        Dd = sb("Dd_s", [128, NH]); tiny = sb("tiny", [128, 8])
        hstg = sb("hstg", [128, 4, 32]); hld = sb("hld", [128, 4, 32]); hsum = sb("hsum", [128, 32])
        ps = es.enter_context(nc.psum_tensor("ps", [128, 4096], F32))
        psb = ps[:, :].bitcast(BF16)
        ccsem = [es.enter_context(nc.semaphore("ccs%d" % i)) for i in range(5)]
        ws = WeightStream(S, nc, es, 4, 16 * 256)

        wi0 = ev_w_in.rearrange("(kc p) n -> p kc n", p=128)
        wo0 = ev_w_out.rearrange("(kc p) n -> p kc n", p=128)
        u_xb = [ws.plan((16, 256), wi0[:, :, 2048 + i * 256:2048 + (i + 1) * 256]) for i in range(4)]
        u_u = [ws.plan((16, 256), wi0[:, :, i * 256:(i + 1) * 256]) for i in range(4)]
        u_v = [ws.plan((16, 256), wi0[:, :, 1024 + i * 256:1024 + (i + 1) * 256]) for i in range(4)]
        u_o = [ws.plan((16, 256), wo0[:, :, i * 256:(i + 1) * 256]) for i in range(8)]
        plan0 = plan_ffn_weights(ws, w_up[0], w_down[0], split_last=True)
        wi1 = od_w_in.rearrange("(kc p) n -> p kc n", p=128)
        wo1 = od_w_out.rearrange("(kc p) n -> p kc n", p=128)
        u_pre = [ws.plan((16, 256), wi1[:, :, h * 512 + 256:h * 512 + 512]) for h in range(NH)]
        u_main = []
        for h in range(NH):
            fi = ws.plan((16, 256), wi1[:, :, h * 512 + 256:h * 512 + 512])
            qg = ws.plan((16, 256), wi1[:, :, h * 512:h * 512 + 256])
            u_main.append((fi, qg))
        u_o1 = [ws.plan((16, 256), wo1[:, :, (i % 8) * 256:((i % 8) + 1) * 256]) for i in range(16)]
        plan1 = plan_ffn_weights(ws, w_up[1], w_down[1], split_last=True)

        x0b = cv(R0, (NCH, TH), BF16)
        zf = cv(R0, (NCH, T + 2), F32)
        xb = cv(R1, (NCH, T + 2), BF16)
        pp = cv(R1, (8, TH), BF16)
        g1 = [cv(R1 + 18432, (TH,), F32), cv(R1 + 23040, (TH,), F32)]
        xbf = cv(R1 + 27648, (16 + TH,), F32)
        u = cv(R2, (8, TH), BF16)
        vt = cv(R2 + 18432, (9, 1024), BF16)
        g2p = [cv(R2 + 36864, (16 + TH,), F32), cv(R2 + 41536, (16 + TH,), F32)]
        g2 = [t[:, 16:16 + TH] for t in g2p]
        tA, tB = g2p
        wsT = cv(R2 + 46208, (8, 128), BF16)
        mask = cv(R2 + 48256, (128,), F32)
        bsT = cv(R2 + 48768, (8, 128), F32)
        lnv = cv(R2 + 52864, (2, 1024), F32)
        wp = cv(R2 + 61056, (8, 256), BF16)
        wsTf = g1[1][:, 0:1024].rearrange("p (h t) -> p h t", h=8)
        hb = [cv(R0 + 36864 + k * 4608, (TH,), F32) for k in range(2)]

        chp = S.new_chan(total=True)
        chx = S.new_chan(total=True)
        S.dma("sp", wsTf, wsT_d.rearrange("p (h t) -> p h t", h=8), chp)
        S.dma("sp", mask, mask_d, chp)
        S.dma("sp", bsT, bsT_d.rearrange("p (h t) -> p h t", h=8), chp)
        S.dma("sp", lnv, lnv_d.rearrange("p (a c) -> p a c", a=2), chp)
        S.dma("sp", psc[:, :], psc_d, chp)
        S.dma("sp", rc[:, :, :], rc_d.rearrange("p (g j) -> p g j", g=4), chp)
        S.dma("sp", flag[:, :], flag_d, chp)
        S.dma("sp", oh[:, :], oh_d, chp)
        S.dma("sp", ln1gb[:, :, :, :], ln1gb_d.rearrange("p (l c j) -> p l c j", l=2, j=2), chp)
        S.dma("sp", ln2gb[:, :, :, :], ln2gb_d.rearrange("p (l c j) -> p l c j", l=2, j=2), chp)
        S.dma("sp", cwb[:, :, :, :], cwb_d.rearrange("p (l c j) -> p l c j", l=2, j=4), chp)
        S.dma("sp", gn[:, :], gn_d, chp)
        S.dma("pool", wp, wp_d.rearrange("(a p) n -> p a n", p=128), chx)
        for c in range(NCH):
            S.dma("pool", x0b[:, c, :], x0T[c * 128:(c + 1) * 128, :], chx)
        ws.release(-1)
        S.memset("dve", ones[:, :], 1.0)
        S.memset("dve", eps[:, :], LN_EPS)
        S.memset("dve", xbf[:, 0:16], 0.0)
        S.memset("dve", tA[:, 0:16], 0.0)
        S.memset("dve", tB[:, 0:16], 0.0)
        for h in range(8):
            S.tt("dve", wsT[:, h, :], wsTf[:, h, :], mask, ALU.mult)

        GR = (0, 1536)
        TT3 = ((0, 512), (512, 512), (1024, 128))

        def proj_fm(wt, sub, g0):
            for kc in range(NCH):
                for (t0, w) in TT3:
                    S.mm(ps[:, g0 + t0:g0 + t0 + w], wt[:, kc, sub * 128:(sub + 1) * 128], x0b[:, kc, t0:t0 + w],
                         start=(kc == 0), stop=(kc == NCH - 1))
        gi = 0
        for i in range(4):
            wt = ws.get(u_xb[i])
            for sub in range(2):
                c = i * 2 + sub
                g = c // 2
                g0 = GR[gi % 2]
                gi += 1
                proj_fm(wt, sub, g0)
                S.act(xbf[:, 16:16 + TH], ps[:, g0:g0 + TH], AF.Identity)
                src = xbf
                dsts = [tA, tB]
                for k in range(g + 1):
                    sh = 1 << k
                    dst = dsts[k % 2]
                    S.tt("dve", dst[:, 16:16 + TH], src[:, 16:16 + TH], src[:, 16 - sh:16 + TH - sh], ALU.add)
                    src = dst
                win = B_WINDOWS[g]
                S.stt(pp[:, c, :], src[:, 16:16 + TH], 1.0 / win, xbf[:, 16:16 + TH], ALU.mult, ALU.subtract)
                S.tt("dve", small[:, 0:16], src[:, 16 + 128:16 + 144], rc[:, g, :], ALU.mult)
                S.tt("dve", pp[:, c, 128:144], small[:, 0:16], xbf[:, 16 + 128:16 + 144], ALU.subtract)
            ws.release(u_xb[i])
        for i in range(4):
            wt = ws.get(u_u[i])
            for sub in range(2):
                c = i * 2 + sub
                g0 = GR[gi % 2]
                proj_fm(wt, sub, g0)
                S.act(hb[gi % 2], ps[:, g0:g0 + TH], AF.Identity)
                emit_gelu(S, u[:, c, :], hb[gi % 2], g1[gi % 2], g2[gi % 2])
                gi += 1
            ws.release(u_u[i])
        wv = [ws.get(k) for k in u_v]
        for tk in range(9):
            vr = g1[tk % 2]
            for cg in range(4):
                r0 = 2048 + ((tk * 4 + cg) % 4) * 512
                for kc in range(NCH):
                    S.mm(ps[:, r0:r0 + 256], x0b[:, kc, tk * 128:(tk + 1) * 128], wv[cg][:, kc, :],
                         start=(kc == 0), stop=(kc == NCH - 1))
                hv = hb[tk % 2][:, cg * 256:(cg + 1) * 256]
                S.act(hv, ps[:, r0:r0 + 256], AF.Identity)
                emit_gelu(S, vr[:, cg * 256:(cg + 1) * 256], hv,
                          g2[0][:, cg * 256:(cg + 1) * 256], g2[1][:, cg * 256:(cg + 1) * 256])
            sq = g2[0]
            S.add("dve", lambda e, vr=vr: e.reduce_sum(out=st[:, 0:1], in_=vr[:, 0:1024], axis=mybir.AxisListType.X),
                  reads=[vr[:, 0:1024]], writes=[st[:, 0:1]])
            S.act(sq[:, 0:1024], vr[:, 0:1024], AF.Square)
            S.add("dve", lambda e, sq=sq: e.reduce_sum(out=st[:, 1:2], in_=sq[:, 0:1024], axis=mybir.AxisListType.X),
                  reads=[sq[:, 0:1024]], writes=[st[:, 1:2]])
            S.ts("dve", st[:, 2:3], st[:, 0:1], 1.0 / 1024, None, ALU.mult)
            S.tt("dve", st[:, 3:4], st[:, 2:3], st[:, 2:3], ALU.mult)
            S.stt(st[:, 4:5], st[:, 1:2], 1.0 / 1024, st[:, 3:4], ALU.mult, ALU.subtract)
            S.act(st[:, 5:6], st[:, 4:5], AF.Sqrt, bias=eps[:, 0:1], scale=1.0)
            S.add("dve", lambda e: e.reciprocal(out=st[:, 6:7], in_=st[:, 5:6]), reads=[st[:, 5:6]],
                  writes=[st[:, 6:7]])
            S.ts("dve", vr[:, 0:1024], vr[:, 0:1024], st[:, 2:3], st[:, 6:7], ALU.subtract, ALU.mult)
            S.tt("dve", vr[:, 0:1024], vr[:, 0:1024], lnv[:, 0, :], ALU.mult)
            S.tt("dve", vt[:, tk, :], vr[:, 0:1024], lnv[:, 1, :], ALU.add)
        ws.release(u_v[3])
        for tk in range(9):
            for half in range(2):
                r0 = 2048 + half * 512
                for hh in range(4):
                    h = half * 4 + hh
                    S.mm(ps[:, r0 + hh * 128:r0 + (hh + 1) * 128], vt[:, tk, h * 128:(h + 1) * 128], wsT[:, h, :],
                         start=True, stop=True)
                tmp = g2[half][:, 0:512].rearrange("p (h t) -> p h t", h=4)
                S.tt("dve", tmp, ps[:, r0:r0 + 512].rearrange("p (h t) -> p h t", h=4),
                     bsT[:, half * 4:half * 4 + 4, :], ALU.add)
                uu = u[:, half * 4:half * 4 + 4, tk * 128:(tk + 1) * 128]
                S.tt("dve", uu, tmp, uu, ALU.mult)
        for g in range(4):
            for oc in range(2):
                g0 = GR[oc]
                for kc in range(2):
                    for (t0, w) in TT3:
                        S.mm(ps[:, g0 + t0:g0 + t0 + w], wp[:, g * 2 + kc, oc * 128:(oc + 1) * 128],
                             pp[:, g * 2 + kc, t0:t0 + w], start=(kc == 0), stop=(kc == 1))
            for oc in range(2):
                g0 = GR[oc]
                c = g * 2 + oc
                S.act(pp[:, c, :], ps[:, g0:g0 + TH], AF.Identity, scale=psc[:, c:c + 1])
        xs = [g1[0], g1[1]]
        chs = [S.new_chan(), S.new_chan()]
        for i in range(8):
            wt = ws.get(u_o[i])
            for sub in range(2):
                oc = i * 2 + sub
                g0 = GR[oc % 2]
                xst = xs[oc % 2]
                S.dma("sp", xst[:, 0:T + 2], x0T[oc * 128:(oc + 1) * 128, 126:TH], chs[oc % 2])
                for kc in range(NCH):
                    src = u[:, kc, :] if kc < 8 else pp[:, kc - 8, :]
                    lw = wt[:, kc, sub * 128:(sub + 1) * 128]
                    S.mm(ps[:, g0 + 510:g0 + 512], lw, src[:, 126:128], start=(kc == 0), stop=(kc == NCH - 1))
                    S.mm(ps[:, g0 + 512:g0 + 1024], lw, src[:, 128:640], start=(kc == 0), stop=(kc == NCH - 1))
                    S.mm(ps[:, g0 + 1024:g0 + 1536], lw, src[:, 640:1152], start=(kc == 0), stop=(kc == NCH - 1))
                S.stt(zf[:, oc, :], xst[:, 0:T + 2], ALPHA, ps[:, g0 + 510:g0 + 1536], ALU.mult, ALU.add)
            ws.release(u_o[i])
        tm_ln1 = {"eps": eps, "mean": g2[0], "rstd": g2[1],
                  "zb": [cv(R2 + k * 2052, (T + 2,), BF16) for k in range(2)],
                  "zs": [cv(R2 + (2 + k) * 2052, (T + 2,), BF16) for k in range(2)]}
        def ln1_post(c):
            S.ts("dve", zf[:, c, 0:2], zf[:, c, 0:2], flag[:, 0:1], None, ALU.mult)
            S.act(xb[:, c, :], zf[:, c, :], AF.Identity)
        emit_ln(S, zf, 0, T + 2, ln1gb[:, 0, :, :], ones, tm_ln1, ps, [lambda c: zf[:, c, :]], post=ln1_post)

        gq = cv(R2, (12, T), BF16)
        ft = [cv(R2 + 24576 + k * 4096, (T,), F32) for k in range(6)]
        tm_ffn = {"a": ft[0:2], "v": ft[2:4], "s": ft[4:6], "eps": eps, "mean": ft[0], "rstd": ft[1],
                  "zb": [cv(R2 + 49152 + k * 2048, (T,), BF16) for k in range(2)],
                  "zs": [cv(R2 + 53248 + k * 2048, (T,), BF16) for k in range(2)]}
        xb2 = cv(R1, (NCH, T), BF16)
        def ln2_l0(tt_):
            c0 = 2 + tt_ * 512
            emit_ln(S, zf, c0, 512, ln2gb[:, 0, :, :], ones, tm_ffn, ps,
                    [lambda c: xb2[:, c, tt_ * 512:(tt_ + 1) * 512], lambda c: zf[:, c, c0:c0 + 512]])
        emit_ffn(S, ws, plan0, zf, xb, cwb[:, 0, :, :], gq, tm_ffn, ps, ln_cb=ln2_l0)
        chsp = [S.new_chan() for _ in range(NCH)]
        for c in range(NCH):
            S.dma("sp", xsp[c * 128:(c + 1) * 128, :], zf[:, c, 2:T + 2], chsp[c])

        def mkset(k):
            o0 = R0 + k * 32768
            d_ = {"A": cv(o0, (T,), F32), "B": cv(o0 + 4096, (T,), F32), "C": cv(o0 + 8192, (T,), F32),
                  "kend": cv(o0 + 12288, (T,), BF16), "kdec": cv(o0 + 14336, (T,), BF16),
                  "qdec": cv(o0 + 16384, (T,), BF16), "ibf": cv(o0 + 18432, (T,), BF16),
                  "sg": cv(o0 + 20480, (T,), BF16), "osq": cv(o0 + 22528, (T,), BF16),
                  "attm": cv(o0 + 24576, (8, 128), BF16), "kt0": cv(o0 + 26624, (8, 128), BF16),
                  "kt1": cv(o0 + 28672, (8, 128), BF16), "vtk": cv(o0 + 30720, (8, 128), BF16),
                  "Sb": cv(R2 + 32768, (NCK, 128), BF16) if k == 0 else cv(R2 + 61472, (NCK, 128), BF16),
                  "dec": sb("dec%d" % k, [128, NCK]), "Sf": [sb("Sf%d_%d" % (k, i), [128, 128]) for i in range(2)],
                  "Pp": [sb("Pp%d_%d" % (k, i), [128, 128]) for i in range(2)],
                  "upst": cv(R2 + 59408, (4, 129), F32) if k == 0 else sb("upst1", [128, 4, 129]),
                  "stg": cv(R2 + 57344, (4, 129), F32),
                  "chu": S.new_chan(), "chst": S.new_chan()}
            return d_
        sets = [mkset(0), mkset(1)]
        y = cv(R2, (NH, T), BF16)
        xs1 = [cv(R2 + 40960, (T,), F32), cv(R2 + 45056, (T,), F32)]
        tm_ln1b = {"eps": eps, "mean": xs1[0], "rstd": xs1[1],
                   "zb": [cv(R2 + 49152 + k * 2048, (T,), BF16) for k in range(2)],
                   "zs": [cv(R2 + 53248 + k * 2048, (T,), BF16) for k in range(2)]}
        chc = S.new_chan(total=True)
        cst = {}
        lbp = sb("lbp_s", [128, 2, NH]); cst["lb"] = sb("lb", [128, NH]); cst["oml"] = sb("oml", [128, NH])
        cst["mask2"] = sb("mask2_s", [128, 128]); identf = sb("identf", [128, 128]); cst["ident"] = sb("ident_s", [128, 128], BF16)
        cst["pm"] = sb("pm", [128, 2])
        cst["one"] = sb("one_c", [128, 1])
        S.memset("dve", cst["one"][:, :], 1.0)
        cst["rm"] = cv(R2 + 36864, (T,), F32)
        S.dma("sp", lbp[:, :, :], lbp_d.rearrange("p (l h) -> p l h", l=2), chc)
        S.dma("sp", cst["mask2"][:, :], mask2_d, chc)
        S.dma("sp", identf[:, :], ident_d, chc)
        S.copy("dve", cst["ident"][:, :], identf[:, :])
        S.tt("dve", cst["lb"][:, :], lbp[:, 1, :], lbp[:, 0, :], ALU.subtract)
        S.act(cst["lb"][:, :], cst["lb"][:, :], AF.Sigmoid)
        S.ts("dve", cst["oml"][:, :], cst["lb"][:, :], -1.0, 1.0, ALU.mult, ALU.add)
        S.memset("dve", cst["rm"], 1.0)
        S.memset("dve", cst["rm"].rearrange("p (c t) -> p c t", t=CH)[:, :, 0:1], 0.0)
        S.memset("dve", cst["pm"][:, :], 0.0)
        S.memset("dve", cst["pm"][0:64, 0:1], 1.0)
        S.memset("dve", cst["pm"][64:128, 1:2], 1.0)
        oh3 = oh[:, 0:4].rearrange("p (j o) -> p j o", o=1)
        G0, G1, PB5, O0 = 0, 1024, 2560, 3072

        def proj1(wt, blk, g0):
            for kc in range(NCH):
                for t_ in range(2):
                    S.mm(ps[:, g0 + t_ * 512:g0 + (t_ + 1) * 512], wt[:, kc, blk * 128:(blk + 1) * 128],
                         xb2[:, kc, t_ * 512:(t_ + 1) * 512], start=(kc == 0), stop=(kc == NCH - 1))

        def cc_op(idx, src_t, dst_t):
            if use_cc:
                def fn(e):
                    e.collective_compute("AllReduce", ALU.add, replica_groups=SEQ_GROUPS,
                                         ins=[src_t.ap().opt()], outs=[dst_t.ap().opt()]).then_inc(ccsem[idx])
                    return None
                S.add("pool", fn, reads=[src_t.ap()], writes=[])

                def fn2(e):
                    e.wait_ge(ccsem[idx], 1)
                    return e.memset(tiny[:, idx:idx + 1], 0.0)
                return lambda: S.add("pool", fn2, reads=[], writes=[dst_t.ap(), tiny[:, idx:idx + 1]])
            else:
                chq = S.new_chan()
                S.dma("sp", dst_t.ap(), src_t.ap(), chq)
                return lambda: None

        def scan_group(q, g4, st_):
            pb = (2560, 3072, 3584, 2560)[g4]
            for cc in range(4):
                c = g4 * 4 + cc
                j, par = divmod(c, 2)
                S.mm(ps[:, pb + cc * 128:pb + (cc + 1) * 128], (q["kt0"], q["kt1"])[par][:, j, :], q["vtk"][:, j, :],
                     start=True, stop=True)
            for cc in range(4):
                c = g4 * 4 + cc
                cur = st_["cur"]
                S.stt(q["Sf"][1 - cur][:, :], q["Sf"][cur][:, :], q["dec"][:, c:c + 1],
                      ps[:, pb + cc * 128:pb + (cc + 1) * 128], ALU.mult, ALU.add)
                st_["cur"] = 1 - cur
                if st_["sb"] and c + 1 < NCK:
                    S.act(q["Sb"][:, c + 1, :], q["Sf"][1 - cur][:, :], AF.Identity)

        def interleave(bsteps, asteps, after):
            ai = 0
            for bi, bstep in enumerate(bsteps):
                bstep()
                while ai < len(asteps) and after[ai] == bi:
                    asteps[ai]()
                    ai += 1
            while ai < len(asteps):
                asteps[ai]()
                ai += 1

        cc_done = []

        def pre_A(h):
            q = sets[h % 2]

            def a1():
                q["wt"] = ws.get(u_pre[h])
                proj1(q["wt"], 0, G0)

            def a2():
                proj1(q["wt"], 1, G1)
                ws.release(u_pre[h])

            def a2g():
                hgrn_gates(S, h, cst, ps[:, G0:G0 + T], q["A"], q["B"], q["C"], q["kend"], q["dec"][:, :])
                S.act(q["ibf"], ps[:, G1:G1 + T], AF.Identity)
                C3 = q["C"].rearrange("p (c t) -> p c t", t=CH)
                S.add("dve", lambda e, C3=C3, h=h: e.reduce_sum(out=Dd[:, h:h + 1], in_=C3[:, :, CH - 1:CH],
                                                               axis=mybir.AxisListType.XY),
                      reads=[q["C"]], writes=[Dd[:, h:h + 1]])
                S.act(Dd[:, h:h + 1], Dd[:, h:h + 1], AF.Exp)
            return [a1, a2, a2g]

        def pre_B(h):
            q = sets[h % 2]
            st_ = {"cur": 0, "sb": False}

            def b1():
                hgrn_transposes(S, cst, psb, q["kend"], q["kt0"], q["kt1"])

            def b1b():
                hgrn_transposes(S, cst, psb, q["ibf"], q["vtk"])
                S.memset("dve", q["Sf"][0][:, :], 0.0)

            def bfin():
                fin = st_["cur"]
                for j in range(4):
                    S.ts("dve", q["stg"][:, j, 0:128], q["Sf"][fin][:, :], oh[:, j:j + 1], None, ALU.mult)
                S.ts("dve", q["stg"][:, :, 128:129], oh3, Dd[:, h:h + 1], None, ALU.mult)
                g, hl = divmod(h, 4)
                S.dma("sp", ccin[g].ap().rearrange("(j l d) n -> d j l n", j=4, l=4)[:, :, hl, :], q["stg"][:, :, :],
                      q["chst"])
                if hl == 3:
                    cc_done.append(cc_op(g, ccin[g], ccout[g]))
            return [b1, b1b] + [lambda g4=g4: scan_group(q, g4, st_) for g4 in range(4)] + [bfin]

        for stp in pre_A(0):
            stp()
        for h in range(NH):
            nxt = pre_A(h + 1) if h + 1 < NH else []
            interleave(pre_B(h), nxt, [0, 4, 5])

        def main_A(h):
            q = sets[h % 2]
            g, hl = divmod(h, 4)
            fi, qg = u_main[h]

            def a1():
                if hl == 0:
                    cc_done[g]()
                up = q["upst"]
                S.dma("sp", up[:, :, :], ccout[g].ap().rearrange("(j l d) n -> d j l n", j=4, l=4)[:, :, hl, :], q["chu"])
                Pp_, Sf_ = q["Pp"], q["Sf"]
                S.stt(Pp_[0][:, :], up[:, 0, 0:128], up[:, 1, 128:129], up[:, 1, 0:128], ALU.mult, ALU.add)
                S.stt(Pp_[1][:, :], Pp_[0][:, :], up[:, 2, 128:129], up[:, 2, 0:128], ALU.mult, ALU.add)
                S.ts("dve", Sf_[0][:, :], up[:, 0, 0:128], oh[:, 1:2], None, ALU.mult)
                S.stt(Sf_[0][:, :], Pp_[0][:, :], oh[:, 2:3], Sf_[0][:, :], ALU.mult, ALU.add)
                S.stt(Sf_[0][:, :], Pp_[1][:, :], oh[:, 3:4], Sf_[0][:, :], ALU.mult, ALU.add)
                q["wt"] = ws.get(fi)
                proj1(q["wt"], 0, G0)

            def a2():
                proj1(q["wt"], 1, G1)
                ws.release(fi)

            def a2g():
                hgrn_gates(S, h, cst, ps[:, G0:G0 + T], q["A"], q["B"], q["C"], q["kend"], q["dec"][:, :],
                           kdec_bf=q["kdec"], eC=True)
                S.act(q["ibf"], ps[:, G1:G1 + T], AF.Identity)

            def a3():
                q["wt"] = ws.get(qg)
                proj1(q["wt"], 0, G0)

            def a4():
                proj1(q["wt"], 1, G1)
                ws.release(qg)
                S.act(q["B"], ps[:, G0:G0 + T], AF.Silu)
                S.tt("dve", q["qdec"], q["B"], q["C"], ALU.mult)
                S.act(q["sg"], ps[:, G1:G1 + T], AF.Sigmoid)
            return [a1, a2, a2g, a3, a4]

        def main_B(h):
            q = sets[h % 2]
            st_ = {"cur": 0, "sb": True}

            def b1():
                hgrn_transposes(S, cst, psb, q["kend"], q["kt0"], q["kt1"])

            def b1b():
                hgrn_transposes(S, cst, psb, q["ibf"], q["vtk"])
                S.act(q["Sb"][:, 0, :], q["Sf"][0][:, :], AF.Identity)

            def batt(half):
                pb = 3072 + half * 512
                for jj in range(4):
                    j = half * 4 + jj
                    S.mm(ps[:, pb + jj * 128:pb + (jj + 1) * 128], q["kdec"][:, j * 128:(j + 1) * 128],
                         q["qdec"][:, j * 128:(j + 1) * 128], start=True, stop=True)
                for jj in range(4):
                    j = half * 4 + jj
                    S.tt("dve", q["attm"][:, j, :], ps[:, pb + jj * 128:pb + (jj + 1) * 128], cst["mask2"][:, :],
                         ALU.mult)

            def bo():
                for j in range(8):
                    S.mm(ps[:, O0 + j * 128:O0 + (j + 1) * 128], q["vtk"][:, j, :], q["attm"][:, j, :], start=True,
                         stop=False)
                    S.mm(ps[:, O0 + j * 128:O0 + j * 128 + 64], q["Sb"][:, 2 * j, :], q["qdec"][:, j * 128:j * 128 + 64],
                         start=False, stop=False)
                    S.mm(ps[:, O0 + j * 128 + 64:O0 + (j + 1) * 128], q["Sb"][:, 2 * j + 1, :],
                         q["qdec"][:, j * 128 + 64:(j + 1) * 128], start=False, stop=True)
                S.act(q["osq"], ps[:, O0:O0 + T], AF.Square)

            def bnorm():
                for t_ in range(2):
                    sl = slice(t_ * 512, (t_ + 1) * 512)
                    S.mm(ps[:, PB5:PB5 + 512], ones[:, :], q["osq"][:, sl], start=True, stop=True)
                    S.act(q["A"][:, sl], ps[:, PB5:PB5 + 512], AF.Ln, bias=eps[:, 0:1], scale=1.0 / 128)
                S.act(q["A"], q["A"], AF.Exp, scale=-0.5)
                S.stt(q["C"], ps[:, O0:O0 + T], gn[:, h:h + 1], q["A"], ALU.mult, ALU.mult)
                S.tt("dve", y[:, h, :], q["C"], q["sg"], ALU.mult)
            return ([b1, b1b] + [lambda g4=g4: scan_group(q, g4, st_) for g4 in range(4)]
                    + [lambda: batt(0), lambda: batt(1), bo, bnorm])

        for stp in main_A(0):
            stp()
        for h in range(NH):
            nxt = main_A(h + 1) if h + 1 < NH else []
            interleave(main_B(h), nxt, [0, 4, 5, 5, 8])
        chs1 = [S.new_chan() for _ in range(4)]
        xs4 = [cv(R2 + 40960 + k * 2048, (512,), F32) for k in range(4)]
        tm_t = {"eps": eps, "mean": cv(R2 + 49152, (512,), F32), "rstd": cv(R2 + 51200, (512,), F32),
                "zb": [cv(R2 + 53248 + k * 1024, (512,), BF16) for k in range(2)],
                "zs": [cv(R2 + 55296 + k * 1024, (512,), BF16) for k in range(2)]}
        OB = (512, 1024, 1536, 2560, 3072, 3584)
        done_h = None
        cnt = 0
        for tile in (1, 0):
            c0 = 2 + tile * 512
            for i in range(8):
                wt = ws.get(u_o1[(1 - tile) * 8 + i])
                for sub in range(2):
                    oc = i * 2 + sub
                    g0 = OB[cnt % 6]
                    xst = xs4[cnt % 4]
                    S.dma("sp", xst, xsp[oc * 128:(oc + 1) * 128, tile * 512:(tile + 1) * 512], chs1[cnt % 4])
                    for kc in range(NCH):
                        S.mm(ps[:, g0:g0 + 512], wt[:, kc, sub * 128:(sub + 1) * 128],
                             y[:, kc, tile * 512:(tile + 1) * 512], start=(kc == 0), stop=(kc == NCH - 1))
                    S.stt(zf[:, oc, c0:c0 + 512], xst, ALPHA, ps[:, g0:g0 + 512], ALU.mult, ALU.add)
                    cnt += 1
                ws.release(u_o1[(1 - tile) * 8 + i])
            emit_ln(S, zf, c0, 512, ln1gb[:, 1, :, :], ones, tm_t, ps,
                    [lambda c, c0=c0: xb[:, c, c0:c0 + 512], lambda c, c0=c0: zf[:, c, c0:c0 + 512]],
                    ps_off=(0, 2048))
            if tile == 1:
                for j in range(4):
                    S.ts("dve", hstg[:, j, :].rearrange("p (c t) -> p c t", t=2), zf[:, :, T:T + 2], oh[:, j:j + 1],
                         None, ALU.mult)
                chh = S.new_chan()
                S.dma("sp", cch_in.ap().rearrange("(j p) n -> p j n", p=128), hstg[:, :, :], chh)
                done_h = cc_op(4, cch_in, cch_out)
        done_h()
        chh2 = S.new_chan()
        S.dma("sp", hld[:, :, :], cch_out.ap().rearrange("(j p) n -> p j n", p=128), chh2)
        S.ts("dve", hsum[:, :], hld[:, 0, :], oh[:, 4:5], None, ALU.mult)
        for j in range(1, 4):
            S.stt(hsum[:, :], hld[:, j, :], oh[:, 4 + j:5 + j], hsum[:, :], ALU.mult, ALU.add)
        S.act(xb[:, :, 0:2], hsum[:, :].rearrange("p (c t) -> p c t", t=2), AF.Identity)
        def ln2_l1(tt_):
            c0 = 2 + tt_ * 512
            emit_ln(S, zf, c0, 512, ln2gb[:, 1, :, :], ones, tm_ffn, ps, [lambda c: zf[:, c, c0:c0 + 512]])
        emit_ffn(S, ws, plan1, zf, xb, cwb[:, 1, :, :], gq, tm_ffn, ps, ln_cb=ln2_l1)
        cho = [S.new_chan() for _ in range(4)]
        for c in range(NCH):
            S.dma("sp", outT[c * 128:(c + 1) * 128, :], zf[:, c, 2:T + 2], cho[c % 4])
        S.emit(nc, es, final_chans=cho)
    return nc, S


def fused_inputs(inp):
    f32 = np.float32
    maps = prep_mix0_inputs(inp["x"], inp["ev_w_in"], inp["ev_ln_v_g"], inp["ev_ln_v_b"], inp["ev_w_s"],
                            inp["ev_b_s"], inp["ev_w_pool"], inp["ev_pool_scale"], inp["ev_w_out"],
                            inp["ln1_g"], inp["ln1_b"])
    ln1gb = np.stack([np.stack([_pm(inp["ln1_g"][l], 16), _pm(inp["ln1_b"][l], 16)], axis=-1) for l in range(2)], axis=1)
    ln2gb = np.stack([np.stack([_pm(inp["ln2_g"][l], 16), _pm(inp["ln2_b"][l], 16)], axis=-1) for l in range(2)], axis=1)
    cwbs = []
    for l in range(2):
        cw = np.asarray(inp["ffn_conv_w"][l], f32)
        cb = np.asarray(inp["ffn_conv_b"][l], f32)
        cwbs.append(np.stack([_pm(cw[0], 88), _pm(cw[1], 88), _pm(cw[2], 88), _pm(cb, 88)], axis=-1))
    cwb = np.stack(cwbs, axis=1)
    common = {
        "ln1gb": np.ascontiguousarray(ln1gb.reshape(128, 64)), "ln2gb": np.ascontiguousarray(ln2gb.reshape(128, 64)),
        "cwb": np.ascontiguousarray(cwb.reshape(128, 2 * 88 * 4)),
        "w_up0": np.ascontiguousarray(inp["ffn_w_up"][0], f32), "w_up1": np.ascontiguousarray(inp["ffn_w_up"][1], f32),
        "w_down0": np.ascontiguousarray(inp["ffn_w_down"][0], f32),
        "w_down1": np.ascontiguousarray(inp["ffn_w_down"][1], f32),
        "od_w_in": regroup_w_in(inp["od_w_in"][0]), "od_w_out": np.ascontiguousarray(inp["od_w_out"][0], f32),
        "gn": _pm(inp["od_norm_g"][0], NH)}
    common.update(hgrn_const_inputs(inp["lb_param"]))
    out = []
    for c in range(NCORES):
        b, s = divmod(c, 4)
        m0 = maps[c]
        m = dict(common)
        for k in ("x0T", "wsT", "maskA", "bsT", "lnv", "w_pool", "pscale", "rcnt", "flag"):
            m[k] = m0[k]
        m["ev_w_in"] = m0["w_in"]
        m["ev_w_out"] = m0["w_out"]
        oh = np.zeros((128, 8), f32)
        oh[:, s] = 1.0
        if s > 0:
            oh[:, 4 + s - 1] = 1.0
        m["oh"] = oh
        out.append(m)
    return out


def kernel(**inputs):
    nc, _ = build_fused(use_cc=True)
    maps = fused_inputs(inputs)
    res = run_bass_kernel_spmd(nc, maps, core_ids=list(range(NCORES))).results
    out = np.zeros((2, 4 * T, D), np.float32)
    for c in range(NCORES):
        b, s = divmod(c, 4)
        out[b, s * T:(s + 1) * T] = res[c]["outT"].T
    return out
```

```python
import numpy as np
from contextlib import ExitStack
import concourse.bass as bass
import concourse.mybir as mybir
from concourse.bass_utils import run_bass_kernel_spmd

F32 = mybir.dt.float32
BF16 = mybir.dt.bfloat16
AF = mybir.ActivationFunctionType
ALU = mybir.AluOpType

D = 2048
NCH = 16
T = 1024
NCORES = 8
DFF = 5632
NFF = 44
ALPHA = 4.0 ** 0.25
LN_EPS = 1e-5
ENGS = ("pe", "act", "dve", "pool", "sp")
_DT_SIZE = {F32: 4, BF16: 2}


def _dsize(dt):
    return _DT_SIZE.get(dt, 4)


class _Op:
    __slots__ = ("eng", "idx", "fn", "deps", "chan", "chan_val", "signal", "val")


class Sched:
    def __init__(self):
        self.ops = {e: [] for e in ENGS}
        self.track = {}
        self.chan_cnt = []
        self.chan_total = []

    @staticmethod
    def _rng(ap):
        t = ap.tensor
        name = t.name
        sp = str(ap.space) if hasattr(ap, "space") else ""
        pat = ap.ap
        esz = _dsize(ap.dtype)
        if "DRAM" in sp.upper() or "Dram" in type(t).__name__ or "DRam" in type(t).__name__:
            ext = 1
            for (st, cnt) in pat:
                ext += abs(st) * (cnt - 1)
            return name, ap.offset * esz, (ap.offset + ext) * esz
        pstride = pat[0][0]
        lo = ap.offset % pstride if pstride > 0 else ap.offset
        ext = 1
        for (st, cnt) in pat[1:]:
            ext += abs(st) * (cnt - 1)
        return name, lo * esz, (lo + ext) * esz

    def _touch(self, name, lo, hi, op, is_write, deps):
        segs = self.track.setdefault(name, [])
        new = []
        covered = []
        for s in segs:
            slo, shi, w, rs = s
            if shi <= lo or slo >= hi:
                new.append(s)
                continue
            if slo < lo:
                new.append([slo, lo, w, list(rs)])
            if shi > hi:
                new.append([hi, shi, w, list(rs)])
            olo, ohi = max(slo, lo), min(shi, hi)
            if w is not None:
                deps.add(w)
            if is_write:
                for r in rs:
                    deps.add(r)
            else:
                covered.append([olo, ohi, w, rs + [op]])
        if is_write:
            new.append([lo, hi, op, []])
        else:
            covered.sort(key=lambda s: s[0])
            cur = lo
            for c in covered:
                if c[0] > cur:
                    new.append([cur, c[0], None, [op]])
                new.append(c)
                cur = c[1]
            if cur < hi:
                new.append([cur, hi, None, [op]])
        self.track[name] = new

    def add(self, eng, fn, reads=(), writes=(), chan=None):
        o = _Op()
        o.eng = eng
        o.fn = fn
        o.chan = chan
        o.signal = False
        o.val = None
        o.chan_val = None
        deps = set()
        for ap in reads:
            if ap is None or isinstance(ap, (int, float)):
                continue
            n, lo, hi = self._rng(ap)
            self._touch(n, lo, hi, o, False, deps)
        for ap in writes:
            n, lo, hi = self._rng(ap)
            if eng == "pe":
                lo = (lo // 2048) * 2048
                hi = ((hi + 2047) // 2048) * 2048
            self._touch(n, lo, hi, o, True, deps)
        deps.discard(o)
        o.deps = deps
        if chan is not None:
            self.chan_cnt[chan] += 1
            o.chan_val = 16 * self.chan_cnt[chan]
        o.idx = len(self.ops[eng])
        self.ops[eng].append(o)
        return o

    def new_chan(self, total=False):
        self.chan_cnt.append(0)
        self.chan_total.append(total)
        return len(self.chan_cnt) - 1

    def emit(self, nc, es, final_chans=()):
        for e in ENGS:
            for o in self.ops[e]:
                for d in o.deps:
                    if d.chan is None:
                        d.signal = True
        for e in ENGS:
            c = 0
            for o in self.ops[e]:
                if o.chan is None and o.signal:
                    c += 1
                    o.val = c
        esem = {e: es.enter_context(nc.semaphore("s_" + e)) for e in ENGS}
        csem = [es.enter_context(nc.semaphore("c_%d" % i)) for i in range(len(self.chan_cnt))]
        block = es.enter_context(nc.Block())
        nwaits = {e: 0 for e in ENGS}

        def run(engname, eobj):
            seen = {}
            for o in self.ops[engname]:
                need = {}
                for d in o.deps:
                    if d.chan is not None:
                        key = ("c", d.chan)
                        v = 16 * self.chan_cnt[d.chan] if self.chan_total[d.chan] else d.chan_val
                    else:
                        if d.eng == engname and engname == "pe":
                            continue
                        key = ("e", d.eng)
                        v = d.val
                    if v > need.get(key, 0):
                        need[key] = v
                for key, v in need.items():
                    if v <= seen.get(key, 0):
                        continue
                    seen[key] = v
                    sem = csem[key[1]] if key[0] == "c" else esem[key[1]]
                    eobj.wait_ge(sem, v)
                    nwaits[engname] += 1
                inst = o.fn(eobj)
                if o.chan is not None:
                    inst.then_inc(csem[o.chan], 16)
                elif o.signal:
                    assert inst is not None
                    inst.then_inc(esem[engname], 1)
            if engname == "sp":
                for ch in final_chans:
                    if self.chan_cnt[ch] > 0:
                        eobj.wait_ge(csem[ch], 16 * self.chan_cnt[ch])

        @block.tensor
        def _(e):
            run("pe", e)

        @block.scalar
        def _(e):
            run("act", e)

        @block.vector
        def _(e):
            run("dve", e)

        @block.gpsimd
        def _(e):
            run("pool", e)

        @block.sync
        def _(e):
            run("sp", e)

        self.nwaits = nwaits

    def mm(self, out, lhsT, rhs, start=True, stop=True):
        return self.add("pe", lambda e: e.matmul(out, lhsT=lhsT, rhs=rhs, start=start, stop=stop),
                        reads=[lhsT, rhs], writes=[out])

    def transpose(self, out, in_, ident):
        return self.add("pe", lambda e: e.transpose(out, in_, ident), reads=[in_, ident], writes=[out])

    def act(self, out, in_, func, bias=None, scale=None):
        kw = {}
        rd = [in_]
        if bias is not None:
            kw["bias"] = bias
            rd.append(bias)
        if scale is not None:
            kw["scale"] = scale
            rd.append(scale)
        return self.add("act", lambda e: e.activation(out=out, in_=in_, func=func, **kw), reads=rd, writes=[out])

    def tt(self, eng, out, in0, in1, op):
        return self.add(eng, lambda e: e.tensor_tensor(out=out, in0=in0, in1=in1, op=op),
                        reads=[in0, in1], writes=[out])

    def ts(self, eng, out, in0, s1, s2, op0, op1=None):
        if op1 is None:
            return self.add(eng, lambda e: e.tensor_scalar(out=out, in0=in0, scalar1=s1, scalar2=None, op0=op0),
                            reads=[in0, s1], writes=[out])
        return self.add(eng, lambda e: e.tensor_scalar(out=out, in0=in0, scalar1=s1, scalar2=s2, op0=op0, op1=op1),
                        reads=[in0, s1, s2], writes=[out])

    def stt(self, out, in0, scalar, in1, op0, op1):
        return self.add("dve", lambda e: e.scalar_tensor_tensor(out=out, in0=in0, scalar=scalar, in1=in1,
                                                                op0=op0, op1=op1),
                        reads=[in0, scalar, in1], writes=[out])

    def copy(self, eng, out, in_):
        if eng == "act":
            return self.add("act", lambda e: e.copy(out=out, in_=in_), reads=[in_], writes=[out])
        return self.add(eng, lambda e: e.tensor_copy(out=out, in_=in_), reads=[in_], writes=[out])

    def memset(self, eng, ap, val):
        return self.add(eng, lambda e: e.memset(ap, val), writes=[ap])

    def dma(self, eng, out, in_, chan):
        return self.add(eng, lambda e: e.dma_start(out=out, in_=in_), reads=[in_], writes=[out], chan=chan)


class WeightStream:
    def __init__(self, S, nc, es, nslots, free_elems, name="wslot"):
        self.S = S
        self.slots = [es.enter_context(nc.sbuf_tensor("%s%d" % (name, i), [128, free_elems], BF16))
                      for i in range(nslots)]
        self.chans = [S.new_chan() for _ in range(nslots)]
        self.uses = []
        self.loaded = 0
        self.released = -1
        self.n = nslots

    def plan(self, shape, src):
        self.uses.append((shape, src))
        return len(self.uses) - 1

    def view(self, k):
        shape, _ = self.uses[k]
        sl = self.slots[k % self.n]
        n = 1
        for s in shape:
            n *= s
        v = sl[:, 0:n]
        if len(shape) == 2:
            return v.rearrange("p (a b) -> p a b", a=shape[0], b=shape[1])
        return v

    def _load_upto(self, k):
        while self.loaded < len(self.uses) and self.loaded <= k:
            j = self.loaded
            _, src = self.uses[j]
            self.S.dma("pool", self.view(j), src, self.chans[j % self.n])
            self.loaded += 1

    def get(self, k):
        assert k <= self.released + self.n, (k, self.released)
        self._load_upto(k)
        return self.view(k)

    def release(self, k):
        self.released = max(self.released, k)
        self._load_upto(self.released + self.n)


FF_QUARTERS = (12, 10, 12, 10)


def plan_ffn_weights(ws, w_up, w_down, split_last=False):
    plan = []
    base = 0
    wu = w_up.rearrange("(kc p) n -> p kc n", p=128)
    for q, nq in enumerate(FF_QUARTERS):
        ups = []
        for j in range(0, nq, 2):
            ca = base + j
            ua = ws.plan((16, 256), wu[:, :, ca * 128:ca * 128 + 256])
            uv = ws.plan((16, 256), wu[:, :, (NFF + ca) * 128:(NFF + ca) * 128 + 256])
            ups.append((ca, ua, uv))
        downs = []
        wd = w_down[base * 128:(base + nq) * 128, :].rearrange("(j p) n -> p j n", p=128)
        reps = 2 if (split_last and q == len(FF_QUARTERS) - 1) else 1
        for rep in range(reps):
            for op_ in range(8):
                downs.append((op_, ws.plan((nq, 256), wd[:, :, op_ * 256:(op_ + 1) * 256])))
        plan.append((base, nq, ups, downs))
        base += nq
    return plan


def emit_ffn(S, ws, plan, xf, xb, cwb, gq, tmps, ps, ln_cb=None):
    G = (0, 1536)
    gi = 0
    for (base, nq, ups, downs) in plan:
        for (ca, ua, uv) in ups:
            wa = ws.get(ua)
            wv = ws.get(uv)
            for sub in range(2):
                c_a = ca + sub
                c_v = NFF + ca + sub
                j = c_a - base
                tm = {}
                for which, (wt, cc) in enumerate(((wa, c_a), (wv, c_v))):
                    g0 = G[which]
                    for kc in range(NCH):
                        lw = wt[:, kc, sub * 128:(sub + 1) * 128]
                        S.mm(ps[:, g0 + 510:g0 + 512], lw, xb[:, kc, 0:2], start=(kc == 0), stop=(kc == NCH - 1))
                        S.mm(ps[:, g0 + 512:g0 + 1024], lw, xb[:, kc, 2:514], start=(kc == 0), stop=(kc == NCH - 1))
                        S.mm(ps[:, g0 + 1024:g0 + 1536], lw, xb[:, kc, 514:1026], start=(kc == 0),
                             stop=(kc == NCH - 1))
                    tmp = tmps["a" if which == 0 else "v"][gi % 2]
                    tm[which] = tmp
                    S.act(tmp[:, :], ps[:, g0 + 512:g0 + 1536], AF.Identity, bias=cwb[:, cc, 3:4],
                          scale=cwb[:, cc, 2:3])
                    S.stt(tmp[:, :], ps[:, g0 + 511:g0 + 1535], cwb[:, cc, 1:2], tmp[:, :], ALU.mult, ALU.add)
                    S.stt(tmp[:, :], ps[:, g0 + 510:g0 + 1534], cwb[:, cc, 0:1], tmp[:, :], ALU.mult, ALU.add)
                sa = tmps["s"][gi % 2]
                S.act(sa[:, :], tm[0][:, :], AF.Silu)
                S.tt("dve", gq[:, j, :], sa[:, :], tm[1][:, :], ALU.mult)
                gi += 1
            ws.release(uv)
        def down_tile(wd, oc, sub, tt_, bank):
            pr = ps[:, 3072 + bank * 512:3072 + (bank + 1) * 512]
            for j in range(nq):
                S.mm(pr, wd[:, j, sub * 128:(sub + 1) * 128], gq[:, j, tt_ * 512:(tt_ + 1) * 512],
                     start=(j == 0), stop=(j == nq - 1))
            dst = xf[:, oc, 2 + tt_ * 512:2 + (tt_ + 1) * 512]
            if base == 0:
                S.stt(dst, dst, ALPHA, pr, ALU.mult, ALU.add)
            else:
                S.tt("dve", dst, dst, pr, ALU.add)
        if len(downs) == 8:
            for (op_, ud) in downs:
                wd = ws.get(ud)
                for sub in range(2):
                    for tt_ in range(2):
                        down_tile(wd, op_ * 2 + sub, sub, tt_, tt_)
                ws.release(ud)
        else:
            for tt_ in range(2):
                for (op_, ud) in downs[tt_ * 8:(tt_ + 1) * 8]:
                    wd = ws.get(ud)
                    for sub in range(2):
                        down_tile(wd, op_ * 2 + sub, sub, tt_, sub)
                    ws.release(ud)
                ln_cb(tt_)


def emit_ln(S, zf, c0, n, gb, ones, tmps, ps, outs, post=None, ps_off=(0, 2048)):
    nt = (n + 511) // 512
    zb = tmps["zb"]
    zs = tmps["zs"]
    for c in range(NCH):
        b0 = zb[c % 2]
        s0 = zs[c % 2]
        S.act(b0[:, 0:n], zf[:, c, c0:c0 + n], AF.Identity)
        S.act(s0[:, 0:n], zf[:, c, c0:c0 + n], AF.Square)
        for t_ in range(nt):
            w = min(512, n - t_ * 512)
            S.mm(ps[:, ps_off[0] + t_ * 512:ps_off[0] + t_ * 512 + w], ones[:, :], b0[:, t_ * 512:t_ * 512 + w],
                 start=(c == 0), stop=(c == NCH - 1))
            S.mm(ps[:, ps_off[1] + t_ * 512:ps_off[1] + t_ * 512 + w], ones[:, :], s0[:, t_ * 512:t_ * 512 + w],
                 start=(c == 0), stop=(c == NCH - 1))
    mean = tmps["mean"]
    rstd = tmps["rstd"]
    S.ts("dve", mean[:, 0:n], ps[:, ps_off[0]:ps_off[0] + n], 1.0 / D, None, ALU.mult)
    S.tt("dve", rstd[:, 0:n], mean[:, 0:n], mean[:, 0:n], ALU.mult)
    S.stt(rstd[:, 0:n], ps[:, ps_off[1]:ps_off[1] + n], 1.0 / D, rstd[:, 0:n], ALU.mult, ALU.subtract)
    S.act(rstd[:, 0:n], rstd[:, 0:n], AF.Sqrt, bias=tmps["eps"][:, 0:1], scale=1.0)
    S.add("dve", lambda e: e.reciprocal(out=rstd[:, 0:n], in_=rstd[:, 0:n]), reads=[rstd[:, 0:n]],
          writes=[rstd[:, 0:n]])
    for c in range(NCH):
        zc = zf[:, c, c0:c0 + n]
        S.tt("dve", zc, zc, mean[:, 0:n], ALU.subtract)
        S.tt("dve", zc, zc, rstd[:, 0:n], ALU.mult)
        for i, dst in enumerate(outs):
            S.act(dst(c), zc, AF.Identity, bias=gb[:, c, 1:2], scale=gb[:, c, 0:1])
        if post is not None:
            post(c)


def build_ffn_launch():
    nc = bass.Bass("TRN2", target_bir_lowering=False)
    xT = nc.dram_tensor("xT", [D, T + 2], F32, kind="ExternalInput").ap()
    w_up = nc.dram_tensor("w_up", [D, 2 * DFF], F32, kind="ExternalInput").ap()
    w_down = nc.dram_tensor("w_down", [DFF, D], F32, kind="ExternalInput").ap()
    cwb_d = nc.dram_tensor("cwb", [128, 88 * 4], F32, kind="ExternalInput").ap()
    gb_d = nc.dram_tensor("ln2gb", [128, 32], F32, kind="ExternalInput").ap()
    yT = nc.dram_tensor("yT", [D, T], F32, kind="ExternalOutput").ap()
    S = Sched()
    with ExitStack() as es:
        xf = es.enter_context(nc.sbuf_tensor("xf", [128, NCH, T + 2], F32))
        xb = es.enter_context(nc.sbuf_tensor("xb", [128, NCH, T + 2], BF16))
        cwb = es.enter_context(nc.sbuf_tensor("cwb_s", [128, 88, 4], F32))
        gb = es.enter_context(nc.sbuf_tensor("gb_s", [128, NCH, 2], F32))
        gq = es.enter_context(nc.sbuf_tensor("gq", [128, 12, T], BF16))
        ones = es.enter_context(nc.sbuf_tensor("ones", [128, 128], BF16))
        eps = es.enter_context(nc.sbuf_tensor("eps", [128, 1], F32))
        tmps = {
            "a": [es.enter_context(nc.sbuf_tensor("ta%d" % i, [128, T], F32)) for i in range(2)],
            "v": [es.enter_context(nc.sbuf_tensor("tv%d" % i, [128, T], F32)) for i in range(2)],
            "s": [es.enter_context(nc.sbuf_tensor("tsl%d" % i, [128, T], F32)) for i in range(2)],
            "eps": eps,
        }
        tmps["zb"] = [es.enter_context(nc.sbuf_tensor("zb%d" % i, [128, T], BF16)) for i in range(2)]
        tmps["zs"] = [es.enter_context(nc.sbuf_tensor("zs%d" % i, [128, T], BF16)) for i in range(2)]
        tmps["mean"] = tmps["a"][0]
        tmps["rstd"] = tmps["a"][1]
        ps = es.enter_context(nc.psum_tensor("ps", [128, 4096], F32))
        ws = WeightStream(S, nc, es, 4, 16 * 256)
        plan = plan_ffn_weights(ws, w_up, w_down)

        ch_in = S.new_chan(total=True)
        ch_p = S.new_chan(total=True)
        ch_out = [S.new_chan() for _ in range(4)]
        S.dma("sp", cwb[:, :, :], cwb_d.rearrange("p (c j) -> p c j", j=4), ch_p)
        S.dma("sp", gb[:, :, :], gb_d.rearrange("p (c j) -> p c j", j=2), ch_p)
        S.memset("dve", ones[:, :], 1.0)
        S.memset("dve", eps[:, :], LN_EPS)
        for c in range(NCH):
            S.dma("sp", xf[:, c, :], xT[c * 128:(c + 1) * 128, :], ch_in)
        for c in range(NCH):
            S.act(xb[:, c, :], xf[:, c, :], AF.Identity)
        emit_ffn(S, ws, plan, xf, xb, cwb, gq, tmps, ps)
        emit_ln(S, xf, 2, T, gb, ones, tmps, ps, [lambda c: xf[:, c, 2:T + 2]])
        for c in range(NCH):
            S.dma("sp", yT[c * 128:(c + 1) * 128, :], xf[:, c, 2:T + 2], ch_out[c % 4])
        S.emit(nc, es, final_chans=ch_out)
    return nc, S


def _pm(v, nch):
    return np.ascontiguousarray(np.asarray(v, np.float32).reshape(nch, 128).T)


def prep_ffn_params(l, ffn_conv_w, ffn_conv_b, ln2_g, ln2_b):
    cw = np.asarray(ffn_conv_w[l], np.float32)
    cb = np.asarray(ffn_conv_b[l], np.float32)
    cwb = np.stack([_pm(cw[0], 88), _pm(cw[1], 88), _pm(cw[2], 88), _pm(cb, 88)], axis=-1)
    gb = np.stack([_pm(ln2_g[l], 16), _pm(ln2_b[l], 16)], axis=-1)
    return np.ascontiguousarray(cwb.reshape(128, 88 * 4)), np.ascontiguousarray(gb.reshape(128, 32))


def run_ffn_launch(x1, l, ffn_w_up, ffn_conv_w, ffn_conv_b, ffn_w_down, ln2_g, ln2_b):
    nc, S = build_ffn_launch()
    cwb, gb = prep_ffn_params(l, ffn_conv_w, ffn_conv_b, ln2_g, ln2_b)
    wu = np.ascontiguousarray(ffn_w_up[l], np.float32)
    wd = np.ascontiguousarray(ffn_w_down[l], np.float32)
    x1 = np.asarray(x1, np.float32)
    in_maps = []
    for c in range(NCORES):
        b, s = divmod(c, 4)
        t0 = s * T
        xt = np.zeros((D, T + 2), np.float32)
        xt[:, 2:] = x1[b, t0:t0 + T].T
        if s > 0:
            xt[:, 0:2] = x1[b, t0 - 2:t0].T
        in_maps.append({"xT": xt, "w_up": wu, "w_down": wd, "cwb": cwb, "ln2gb": gb})
    res = run_bass_kernel_spmd(nc, in_maps, core_ids=list(range(NCORES)))
    out = np.zeros((2, 4096, D), np.float32)
    for c in range(NCORES):
        b, s = divmod(c, 4)
        out[b, s * T:(s + 1) * T] = res.results[c]["yT"].T
    return out


TH = T + 128
B_WINDOWS = (2, 4, 8, 16)
GELU_C = 0.044715
GELU_S = 2.0 * 0.7978845608028654


def emit_gelu(S, dst, src_ps, t1, t2):
    S.act(t1, src_ps, AF.Square)
    S.ts("dve", t1, t1, GELU_C, 1.0, ALU.mult, ALU.add)
    S.tt("dve", t1, t1, src_ps, ALU.mult)
    S.act(t2, t1, AF.Sigmoid, scale=GELU_S)
    S.tt("dve", dst, t2, src_ps, ALU.mult)


def build_mix0_launch():
    nc = bass.Bass("TRN2", target_bir_lowering=False)
    x0T = nc.dram_tensor("x0T", [D, TH], F32, kind="ExternalInput").ap()
    w_in = nc.dram_tensor("w_in", [D, 3072], F32, kind="ExternalInput").ap()
    w_out = nc.dram_tensor("w_out", [D, D], F32, kind="ExternalInput").ap()
    wsT_d = nc.dram_tensor("wsT", [128, 8 * 128], F32, kind="ExternalInput").ap()
    mask_d = nc.dram_tensor("maskA", [128, 128], F32, kind="ExternalInput").ap()
    bsT_d = nc.dram_tensor("bsT", [128, 8 * 128], F32, kind="ExternalInput").ap()
    lnv_d = nc.dram_tensor("lnv", [128, 2 * 1024], F32, kind="ExternalInput").ap()
    wp_d = nc.dram_tensor("w_pool", [4 * 256, 256], F32, kind="ExternalInput").ap()
    psc_d = nc.dram_tensor("pscale", [128, 8], F32, kind="ExternalInput").ap()
    gb_d = nc.dram_tensor("ln1gb", [128, 32], F32, kind="ExternalInput").ap()
    rc_d = nc.dram_tensor("rcnt", [128, 64], F32, kind="ExternalInput").ap()
    flag_d = nc.dram_tensor("flag", [128, 1], F32, kind="ExternalInput").ap()
    x1T = nc.dram_tensor("x1T", [D, T + 2], F32, kind="ExternalOutput").ap()
    S = Sched()
    with ExitStack() as es:
        arena = es.enter_context(nc.sbuf_tensor("arena", [128, NCH * (T + 2)], F32))
        zf = arena[:, :].rearrange("p (c t) -> p c t", c=NCH, t=T + 2)
        x0b = arena[:, 0:NCH * TH // 2].bitcast(BF16).rearrange("p (c t) -> p c t", c=NCH, t=TH)
        u = es.enter_context(nc.sbuf_tensor("u", [128, 8, TH], BF16))
        vt = es.enter_context(nc.sbuf_tensor("vt", [128, 9, 1024], BF16))
        pp = es.enter_context(nc.sbuf_tensor("pp", [128, 8, TH], BF16))
        wsT = es.enter_context(nc.sbuf_tensor("wsT_s", [128, 8, 128], BF16))
        mask = es.enter_context(nc.sbuf_tensor("mask_s", [128, 128], F32))
        bsT = es.enter_context(nc.sbuf_tensor("bsT_s", [128, 8, 128], F32))
        lnv = es.enter_context(nc.sbuf_tensor("lnv_s", [128, 2, 1024], F32))
        wp = es.enter_context(nc.sbuf_tensor("wp_s", [128, 8, 256], BF16))
        psc = es.enter_context(nc.sbuf_tensor("psc_s", [128, 8], F32))
        gb = es.enter_context(nc.sbuf_tensor("gb_s", [128, NCH, 2], F32))
        rc = es.enter_context(nc.sbuf_tensor("rc_s", [128, 4, 16], F32))
        flag = es.enter_context(nc.sbuf_tensor("flag_s", [128, 1], F32))
        ones = es.enter_context(nc.sbuf_tensor("ones", [128, 128], BF16))
        eps = es.enter_context(nc.sbuf_tensor("eps", [128, 1], F32))
        xbf = es.enter_context(nc.sbuf_tensor("xbf", [128, 16 + TH], F32))
        g1 = [es.enter_context(nc.sbuf_tensor("g1_%d" % i, [128, TH], F32)) for i in range(2)]
        g2p = [es.enter_context(nc.sbuf_tensor("g2_%d" % i, [128, 16 + TH], F32)) for i in range(2)]
        g2 = [t[:, 16:16 + TH] for t in g2p]
        tA, tB = g2p
        wsTf = g1[1][:, 0:1024].rearrange("p (h t) -> p h t", h=8)
        st = es.enter_context(nc.sbuf_tensor("st", [128, 8], F32))
        small = es.enter_context(nc.sbuf_tensor("small", [128, 32], F32))
        tmps = {"eps": eps,
                "zb": [g2p[i][:, 16:16 + 513].bitcast(BF16) for i in range(2)],
                "zs": [xbf[:, 16:16 + 513].bitcast(BF16), xbf[:, 600:600 + 513].bitcast(BF16)],
                "mean": g1[0], "rstd": g1[1]}
        ps = es.enter_context(nc.psum_tensor("ps", [128, 4096], F32))
        ws = WeightStream(S, nc, es, 4, 16 * 256)
        wi = w_in.rearrange("(kc p) n -> p kc n", p=128)
        wo = w_out.rearrange("(kc p) n -> p kc n", p=128)
        u_xb = [ws.plan((16, 256), wi[:, :, 2048 + i * 256:2048 + (i + 1) * 256]) for i in range(4)]
        u_u = [ws.plan((16, 256), wi[:, :, i * 256:(i + 1) * 256]) for i in range(4)]
        u_v = [ws.plan((16, 256), wi[:, :, 1024 + i * 256:1024 + (i + 1) * 256]) for i in range(4)]
        u_o = [ws.plan((16, 256), wo[:, :, i * 256:(i + 1) * 256]) for i in range(8)]

        chp = S.new_chan(total=True)
        chx = S.new_chan(total=True)
        S.dma("sp", wsTf, wsT_d.rearrange("p (h t) -> p h t", h=8), chp)
        S.dma("sp", mask[:, :], mask_d, chp)
        S.dma("sp", bsT[:, :, :], bsT_d.rearrange("p (h t) -> p h t", h=8), chp)
        S.dma("sp", lnv[:, :, :], lnv_d.rearrange("p (a c) -> p a c", a=2), chp)
        S.dma("sp", psc[:, :], psc_d, chp)
        S.dma("sp", gb[:, :, :], gb_d.rearrange("p (c j) -> p c j", j=2), chp)
        S.dma("sp", rc[:, :, :], rc_d.rearrange("p (g j) -> p g j", g=4), chp)
        S.dma("sp", flag[:, :], flag_d, chp)
        S.dma("pool", wp[:, :, :], wp_d.rearrange("(a p) n -> p a n", p=128), chx)
        for c in range(NCH):
            S.dma("pool", x0b[:, c, :], x0T[c * 128:(c + 1) * 128, :], chx)
        ws.release(-1)
        S.memset("dve", ones[:, :], 1.0)
        S.memset("dve", eps[:, :], LN_EPS)
        S.memset("dve", xbf[:, 0:16], 0.0)
        S.memset("dve", tA[:, 0:16], 0.0)
        S.memset("dve", tB[:, 0:16], 0.0)
        for h in range(8):
            S.tt("dve", wsT[:, h, :], wsTf[:, h, :], mask[:, :], ALU.mult)

        GR = (0, 1536)
        TT3 = ((0, 512), (512, 512), (1024, 128))

        def proj_fm(wt, sub, g0):
            for kc in range(NCH):
                for (t0, w) in TT3:
                    S.mm(ps[:, g0 + t0:g0 + t0 + w], wt[:, kc, sub * 128:(sub + 1) * 128], x0b[:, kc, t0:t0 + w],
                         start=(kc == 0), stop=(kc == NCH - 1))

        gi = 0
        for i in range(4):
            wt = ws.get(u_xb[i])
            for sub in range(2):
                c = i * 2 + sub
                g = c // 2
                g0 = GR[gi % 2]
                gi += 1
                proj_fm(wt, sub, g0)
                S.act(xbf[:, 16:16 + TH], ps[:, g0:g0 + TH], AF.Identity)
                src = xbf
                dsts = [tA, tB]
                for k in range(g + 1):
                    sh = 1 << k
                    dst = dsts[k % 2]
                    S.tt("dve", dst[:, 16:16 + TH], src[:, 16:16 + TH], src[:, 16 - sh:16 + TH - sh], ALU.add)
                    src = dst
                win = B_WINDOWS[g]
                S.stt(pp[:, c, :], src[:, 16:16 + TH], 1.0 / win, xbf[:, 16:16 + TH], ALU.mult, ALU.subtract)
                S.tt("dve", small[:, 0:16], src[:, 16 + 128:16 + 144], rc[:, g, :], ALU.mult)
                S.tt("dve", pp[:, c, 128:144], small[:, 0:16], xbf[:, 16 + 128:16 + 144], ALU.subtract)
            ws.release(u_xb[i])
        for i in range(4):
            wt = ws.get(u_u[i])
            for sub in range(2):
                c = i * 2 + sub
                g0 = GR[gi % 2]
                proj_fm(wt, sub, g0)
                emit_gelu(S, u[:, c, :], ps[:, g0:g0 + TH], g1[gi % 2][:, :], g2[gi % 2])
                gi += 1
            ws.release(u_u[i])
        wv = [ws.get(k) for k in u_v]
        for tk in range(9):
            vr = g1[tk % 2]
            for cg in range(4):
                r0 = 3072 + ((tk * 4 + cg) % 2) * 512
                for kc in range(NCH):
                    S.mm(ps[:, r0:r0 + 256], x0b[:, kc, tk * 128:(tk + 1) * 128], wv[cg][:, kc, :],
                         start=(kc == 0), stop=(kc == NCH - 1))
                emit_gelu(S, vr[:, cg * 256:(cg + 1) * 256], ps[:, r0:r0 + 256],
                          g2[0][:, cg * 256:(cg + 1) * 256], g2[1][:, cg * 256:(cg + 1) * 256])
            sq = g2[0]
            S.add("dve", lambda e, vr=vr: e.reduce_sum(out=st[:, 0:1], in_=vr[:, 0:1024], axis=mybir.AxisListType.X),
                  reads=[vr[:, 0:1024]], writes=[st[:, 0:1]])
            S.act(sq[:, 0:1024], vr[:, 0:1024], AF.Square)
            S.add("dve", lambda e, sq=sq: e.reduce_sum(out=st[:, 1:2], in_=sq[:, 0:1024], axis=mybir.AxisListType.X),
                  reads=[sq[:, 0:1024]], writes=[st[:, 1:2]])
            S.ts("dve", st[:, 2:3], st[:, 0:1], 1.0 / 1024, None, ALU.mult)
            S.tt("dve", st[:, 3:4], st[:, 2:3], st[:, 2:3], ALU.mult)
            S.stt(st[:, 4:5], st[:, 1:2], 1.0 / 1024, st[:, 3:4], ALU.mult, ALU.subtract)
            S.act(st[:, 5:6], st[:, 4:5], AF.Sqrt, bias=eps[:, 0:1], scale=1.0)
            S.add("dve", lambda e: e.reciprocal(out=st[:, 6:7], in_=st[:, 5:6]), reads=[st[:, 5:6]],
                  writes=[st[:, 6:7]])
            S.ts("dve", vr[:, 0:1024], vr[:, 0:1024], st[:, 2:3], st[:, 6:7], ALU.subtract, ALU.mult)
            S.tt("dve", vr[:, 0:1024], vr[:, 0:1024], lnv[:, 0, :], ALU.mult)
            S.tt("dve", vt[:, tk, :], vr[:, 0:1024], lnv[:, 1, :], ALU.add)
        ws.release(u_v[3])
        for tk in range(9):
            for half in range(2):
                r0 = 2048 + half * 512
                for hh in range(4):
                    h = half * 4 + hh
                    S.mm(ps[:, r0 + hh * 128:r0 + (hh + 1) * 128], vt[:, tk, h * 128:(h + 1) * 128], wsT[:, h, :],
                         start=True, stop=True)
                tmp = g2[half][:, 0:512].rearrange("p (h t) -> p h t", h=4)
                S.tt("dve", tmp, ps[:, r0:r0 + 512].rearrange("p (h t) -> p h t", h=4),
                     bsT[:, half * 4:half * 4 + 4, :], ALU.add)
                uu = u[:, half * 4:half * 4 + 4, tk * 128:(tk + 1) * 128]
                S.tt("dve", uu, tmp, uu, ALU.mult)
        for g in range(4):
            for oc in range(2):
                g0 = GR[oc]
                for kc in range(2):
                    for (t0, w) in TT3:
                        S.mm(ps[:, g0 + t0:g0 + t0 + w], wp[:, g * 2 + kc, oc * 128:(oc + 1) * 128],
                             pp[:, g * 2 + kc, t0:t0 + w], start=(kc == 0), stop=(kc == 1))
            for oc in range(2):
                g0 = GR[oc]
                c = g * 2 + oc
                S.act(pp[:, c, :], ps[:, g0:g0 + TH], AF.Identity, scale=psc[:, c:c + 1])
        xs = [g1[0], g1[1]]
        chs = [S.new_chan(), S.new_chan()]
        for i in range(8):
            wt = ws.get(u_o[i])
            for sub in range(2):
                oc = i * 2 + sub
                g0 = GR[oc % 2]
                xst = xs[oc % 2]
                S.dma("sp", xst[:, 0:T + 2], x0T[oc * 128:(oc + 1) * 128, 126:TH], chs[oc % 2])
                for kc in range(NCH):
                    src = u[:, kc, :] if kc < 8 else pp[:, kc - 8, :]
                    lw = wt[:, kc, sub * 128:(sub + 1) * 128]
                    S.mm(ps[:, g0 + 510:g0 + 512], lw, src[:, 126:128], start=(kc == 0), stop=(kc == NCH - 1))
                    S.mm(ps[:, g0 + 512:g0 + 1024], lw, src[:, 128:640], start=(kc == 0), stop=(kc == NCH - 1))
                    S.mm(ps[:, g0 + 1024:g0 + 1536], lw, src[:, 640:1152], start=(kc == 0), stop=(kc == NCH - 1))
                S.stt(zf[:, oc, :], xst[:, 0:T + 2], ALPHA, ps[:, g0 + 510:g0 + 1536], ALU.mult, ALU.add)
            ws.release(u_o[i])
        emit_ln(S, zf, 0, T + 2, gb, ones, tmps, ps, [lambda c: zf[:, c, :]])
        S.ts("dve", zf[:, :, 0:2], zf[:, :, 0:2], flag[:, 0:1], None, ALU.mult)
        cho = [S.new_chan() for _ in range(4)]
        for c in range(NCH):
            S.dma("sp", x1T[c * 128:(c + 1) * 128, :], zf[:, c, :], cho[c % 4])
        S.emit(nc, es, final_chans=cho)
    return nc, S


def prep_mix0_inputs(x, ev_w_in, ev_ln_v_g, ev_ln_v_b, ev_w_s, ev_b_s, ev_w_pool, ev_pool_scale, ev_w_out,
                     ln1_g, ln1_b):
    x = np.asarray(x, np.float32)
    ws = np.asarray(ev_w_s[0], np.float32)
    wsT = np.ascontiguousarray(ws.transpose(2, 0, 1)).reshape(128, 8 * 128)
    tt_ = np.arange(128)
    maskA = (tt_[None, :] >= tt_[:, None]).astype(np.float32)
    bsT = np.ascontiguousarray(np.broadcast_to(np.asarray(ev_b_s[0], np.float32).reshape(1, 8 * 128), (128, 8 * 128)))
    lnv = np.ascontiguousarray(np.broadcast_to(
        np.concatenate([np.asarray(ev_ln_v_g[0], np.float32), np.asarray(ev_ln_v_b[0], np.float32)])[None, :],
        (128, 2048)))
    wp = np.ascontiguousarray(np.asarray(ev_w_pool[0], np.float32).reshape(4 * 256, 256))
    psc = _pm(ev_pool_scale[0], 8)
    gb = np.ascontiguousarray(np.stack([_pm(ln1_g[0], 16), _pm(ln1_b[0], 16)], axis=-1).reshape(128, 32))
    common = {"w_in": np.ascontiguousarray(ev_w_in[0], np.float32),
              "w_out": np.ascontiguousarray(ev_w_out[0], np.float32),
              "wsT": wsT, "maskA": maskA, "bsT": bsT, "lnv": lnv, "w_pool": wp, "pscale": psc, "ln1gb": gb}
    in_maps = []
    for c in range(NCORES):
        b, s = divmod(c, 4)
        t0 = s * T
        xt = np.zeros((D, TH), np.float32)
        xt[:, 128:] = x[b, t0:t0 + T].T
        if s > 0:
            xt[:, 0:128] = x[b, t0 - 128:t0].T
        rc = np.zeros((4, 16), np.float32)
        for g, win in enumerate(B_WINDOWS):
            pos = np.arange(t0 + 1, t0 + 17, dtype=np.float32)
            rc[g] = 1.0 / np.minimum(pos, float(win))
        rcb = np.ascontiguousarray(np.broadcast_to(rc.reshape(1, 64), (128, 64)))
        m = dict(common)
        m.update({"x0T": xt, "rcnt": rcb, "flag": np.full((128, 1), 1.0 if s > 0 else 0.0, np.float32)})
        in_maps.append(m)
    return in_maps


def run_mix0_launch(inputs):
    nc, S = build_mix0_launch()
    in_maps = prep_mix0_inputs(inputs["x"], inputs["ev_w_in"], inputs["ev_ln_v_g"], inputs["ev_ln_v_b"],
                               inputs["ev_w_s"], inputs["ev_b_s"], inputs["ev_w_pool"], inputs["ev_pool_scale"],
                               inputs["ev_w_out"], inputs["ln1_g"], inputs["ln1_b"])
    res = run_bass_kernel_spmd(nc, in_maps, core_ids=list(range(NCORES)))
    return [res.results[c]["x1T"] for c in range(NCORES)]


NH = 16
CH = 64
NCK = T // CH


def hgrn_consts(S, nc, es, lbp_d, mask2_d, ident_d, chp):
    c = {}
    lbp = es.enter_context(nc.sbuf_tensor("lbp_s", [128, 2, NH], F32))
    c["lb"] = es.enter_context(nc.sbuf_tensor("lb", [128, NH], F32))
    c["oml"] = es.enter_context(nc.sbuf_tensor("oml", [128, NH], F32))
    c["rm"] = es.enter_context(nc.sbuf_tensor("rm", [128, T], F32))
    c["mask2"] = es.enter_context(nc.sbuf_tensor("mask2_s", [128, 128], F32))
    identf = es.enter_context(nc.sbuf_tensor("identf", [128, 128], F32))
    c["ident"] = es.enter_context(nc.sbuf_tensor("ident_s", [128, 128], BF16))
    S.dma("sp", lbp[:, :, :], lbp_d.rearrange("p (l h) -> p l h", l=2), chp)
    S.dma("sp", c["mask2"][:, :], mask2_d, chp)
    S.dma("sp", identf[:, :], ident_d, chp)
    S.copy("dve", c["ident"][:, :], identf[:, :])
    S.tt("dve", c["lb"][:, :], lbp[:, 1, :], lbp[:, 0, :], ALU.subtract)
    S.act(c["lb"][:, :], c["lb"][:, :], AF.Sigmoid)
    S.ts("dve", c["oml"][:, :], c["lb"][:, :], -1.0, 1.0, ALU.mult, ALU.add)
    S.memset("dve", c["rm"][:, :], 1.0)
    S.memset("dve", c["rm"][:, :].rearrange("p (c t) -> p c t", t=CH)[:, :, 0:1], 0.0)
    c["pm"] = es.enter_context(nc.sbuf_tensor("pm", [128, 2], F32))
    c["one"] = es.enter_context(nc.sbuf_tensor("one_c", [128, 1], F32))
    S.memset("dve", c["one"][:, :], 1.0)
    S.memset("dve", c["pm"][:, :], 0.0)
    S.memset("dve", c["pm"][0:64, 0:1], 1.0)
    S.memset("dve", c["pm"][64:128, 1:2], 1.0)
    return c


def hgrn_gates(S, h, cst, f_ps, A, B, C, kend_bf, dec, kdec_bf=None, eC=False):
    oml = cst["oml"][:, h:h + 1]
    lb = cst["lb"][:, h:h + 1]
    S.act(A, f_ps, AF.Sigmoid)
    S.act(A, A, AF.Identity, bias=lb, scale=oml)
    S.act(B, A, AF.Ln)
    rm_ap = cst["rm"][:, :]
    S.add("dve", lambda e: e.tensor_tensor_scan(out=C, data0=rm_ap, data1=B, initial=0.0, op0=ALU.mult, op1=ALU.add),
          reads=[rm_ap, B], writes=[C])
    S.act(A, A, AF.Identity, bias=cst["one"][:, 0:1], scale=-1.0)
    S.act(B, C, AF.Exp, scale=-1.0)
    C3 = C.rearrange("p (c t) -> p c t", t=CH)
    S.act(dec.rearrange("p (c o) -> p c o", o=1), C3[:, :, CH - 1:CH], AF.Exp)
    if eC:
        S.act(C, C, AF.Exp)
    S.tt("dve", B, A, B, ALU.mult)
    S.tt("dve", kend_bf.rearrange("p (c t) -> p c t", t=CH), B.rearrange("p (c t) -> p c t", t=CH),
         dec.rearrange("p (c o) -> p c o", o=1).to_broadcast([128, NCK, CH]), ALU.mult)
    if kdec_bf is not None:
        S.copy("dve", kdec_bf, B)


def hgrn_transposes(S, cst, psb, src_bf, dst_tok, dst_tok1=None):
    for j in range(8):
        S.transpose(psb[:, 4096 + j * 128:4096 + (j + 1) * 128], src_bf[:, j * 128:(j + 1) * 128], cst["ident"][:, :])
    if dst_tok1 is None:
        S.act(dst_tok.rearrange("p j d -> p (j d)"), psb[:, 4096:5120], AF.Identity)
    else:
        S.act(dst_tok.rearrange("p j d -> p (j d)"), psb[:, 4096:5120], AF.Identity, scale=cst["pm"][:, 0:1])
        S.act(dst_tok1.rearrange("p j d -> p (j d)"), psb[:, 4096:5120], AF.Identity, scale=cst["pm"][:, 1:2])


def hgrn_state_scan(S, ps, kt, vtk, dec, Sf, Sb=None):
    cur = 0
    if Sb is not None:
        S.act(Sb[:, 0, :], Sf[0][:, :], AF.Identity)
    for g4 in range(4):
        for cc in range(4):
            c = g4 * 4 + cc
            j, par = divmod(c, 2)
            S.mm(ps[:, 2560 + cc * 128:2560 + (cc + 1) * 128], kt[par][:, j, :], vtk[:, j, :], start=True, stop=True)
        for cc in range(4):
            c = g4 * 4 + cc
            nxt = 1 - cur
            S.stt(Sf[nxt][:, :], Sf[cur][:, :], dec[:, c:c + 1], ps[:, 2560 + cc * 128:2560 + (cc + 1) * 128],
                  ALU.mult, ALU.add)
            cur = nxt
            if Sb is not None and c + 1 < NCK:
                S.act(Sb[:, c + 1, :], Sf[cur][:, :], AF.Identity)
    return cur


def build_hgrn_pre_launch(stage=9, nheads=NH):
    nc = bass.Bass("TRN2", target_bir_lowering=False)
    xT = nc.dram_tensor("xT", [D, T], F32, kind="ExternalInput").ap()
    w_in = nc.dram_tensor("w_in", [D, 4 * D], F32, kind="ExternalInput").ap()
    lbp_d = nc.dram_tensor("lbp", [128, 2 * NH], F32, kind="ExternalInput").ap()
    mask2_d = nc.dram_tensor("mask2", [128, 128], F32, kind="ExternalInput").ap()
    ident_d = nc.dram_tensor("ident", [128, 128], F32, kind="ExternalInput").ap()
    U_d = nc.dram_tensor("U", [NH, 128, 128], F32, kind="ExternalOutput").ap()
    D_d = nc.dram_tensor("Dd", [128, NH], F32, kind="ExternalOutput").ap()
    S = Sched()
    with ExitStack() as es:
        xb = es.enter_context(nc.sbuf_tensor("xb", [128, NCH, T], BF16))
        A = [es.enter_context(nc.sbuf_tensor("A%d" % i, [128, T], F32)) for i in range(2)]
        B = [es.enter_context(nc.sbuf_tensor("B%d" % i, [128, T], F32)) for i in range(2)]
        C = [es.enter_context(nc.sbuf_tensor("C%d" % i, [128, T], F32)) for i in range(2)]
        kend = [es.enter_context(nc.sbuf_tensor("kend%d" % i, [128, T], BF16)) for i in range(2)]
        ibf = [es.enter_context(nc.sbuf_tensor("ibf%d" % i, [128, T], BF16)) for i in range(2)]
        kt = [[es.enter_context(nc.sbuf_tensor("kt%d_%d" % (i, k), [128, 8, 128], BF16)) for k in range(2)]
              for i in range(2)]
        vtk = [es.enter_context(nc.sbuf_tensor("vtk%d" % i, [128, 8, 128], BF16)) for i in range(2)]
        dec = [es.enter_context(nc.sbuf_tensor("dec%d" % i, [128, NCK], F32)) for i in range(2)]
        Sf = [[es.enter_context(nc.sbuf_tensor("Sf%d_%d" % (i, k), [128, 128], F32)) for k in range(2)]
              for i in range(2)]
        Dd = es.enter_context(nc.sbuf_tensor("Dd_s", [128, NH], F32))
        ps = es.enter_context(nc.psum_tensor("ps", [128, 4096], F32))
        psb = ps[:, :].bitcast(BF16)
        chp = S.new_chan(total=True)
        chx = S.new_chan(total=True)
        cst = hgrn_consts(S, nc, es, lbp_d, mask2_d, ident_d, chp)
        ws = WeightStream(S, nc, es, 4, 16 * 256)
        wi = w_in.rearrange("(kc p) n -> p kc n", p=128)
        uses = [ws.plan((16, 256), wi[:, :, h * 512 + 256:h * 512 + 512]) for h in range(NH)]
        for c in range(NCH):
            S.dma("pool", xb[:, c, :], xT[c * 128:(c + 1) * 128, :], chx)
        ws.release(-1)
        cho = [S.new_chan() for _ in range(2)]
        for h in range(nheads):
            b = h % 2
            wt = ws.get(uses[h])
            for which in range(2):
                g0 = which * 1024
                for kc in range(NCH):
                    for t_ in range(2):
                        S.mm(ps[:, g0 + t_ * 512:g0 + (t_ + 1) * 512], wt[:, kc, which * 128:(which + 1) * 128],
                             xb[:, kc, t_ * 512:(t_ + 1) * 512], start=(kc == 0), stop=(kc == NCH - 1))
            ws.release(uses[h])
            hgrn_gates(S, h, cst, ps[:, 0:1024], A[b][:, :], B[b][:, :], C[b][:, :], kend[b][:, :], dec[b][:, :])
            S.act(ibf[b][:, :], ps[:, 1024:2048], AF.Identity)
            C3 = C[b][:, :].rearrange("p (c t) -> p c t", t=CH)
            S.add("dve", lambda e, C3=C3, h=h: e.reduce_sum(out=Dd[:, h:h + 1], in_=C3[:, :, CH - 1:CH],
                                                           axis=mybir.AxisListType.XY),
                  reads=[C[b][:, :]], writes=[Dd[:, h:h + 1]])
            S.act(Dd[:, h:h + 1], Dd[:, h:h + 1], AF.Exp)
            if stage >= 2:
                hgrn_transposes(S, cst, psb, kend[b][:, :], kt[b][0][:, :, :], kt[b][1][:, :, :])
                hgrn_transposes(S, cst, psb, ibf[b][:, :], vtk[b][:, :, :])
            S.memset("dve", Sf[b][0][:, :], 0.0)
            fin = 0
            if stage >= 3:
                fin = hgrn_state_scan(S, ps, kt[b], vtk[b], dec[b], Sf[b])
            S.dma("sp", U_d[h, :, :], Sf[b][fin][:, :], cho[b])
        chd = S.new_chan()
        S.dma("sp", D_d, Dd[:, :], chd)
        S.emit(nc, es, final_chans=cho + [chd])
    return nc, S


def regroup_w_in(od_w_in):
    w = np.asarray(od_w_in, np.float32).reshape(D, 4, NH, 128)
    w = w[:, [0, 3, 1, 2]]
    return np.ascontiguousarray(w.transpose(0, 2, 1, 3).reshape(D, 4 * D))


def hgrn_const_inputs(lb_param):
    lbp = np.ascontiguousarray(np.stack([_pm(lb_param[0], NH), _pm(lb_param[1], NH)], axis=1).reshape(128, 2 * NH))
    i = np.arange(128)
    mask2 = ((i[None, :] >= i[:, None]) & ((i[None, :] // CH) == (i[:, None] // CH))).astype(np.float32)
    return {"lbp": lbp, "mask2": mask2, "ident": np.eye(128, dtype=np.float32)}


def _carve(arena, off_bytes, n_elems, dt):
    assert off_bytes % 4 == 0
    nb = n_elems * _dsize(dt)
    assert nb % 4 == 0
    v = arena[:, off_bytes // 4:(off_bytes + nb) // 4]
    return v if dt == F32 else v.bitcast(dt)


def build_hgrn_main_launch():
    nc = bass.Bass("TRN2", target_bir_lowering=False)
    xT = nc.dram_tensor("xT", [D, T], F32, kind="ExternalInput").ap()
    w_in = nc.dram_tensor("w_in", [D, 4 * D], F32, kind="ExternalInput").ap()
    w_out = nc.dram_tensor("w_out", [D, D], F32, kind="ExternalInput").ap()
    lbp_d = nc.dram_tensor("lbp", [128, 2 * NH], F32, kind="ExternalInput").ap()
    mask2_d = nc.dram_tensor("mask2", [128, 128], F32, kind="ExternalInput").ap()
    ident_d = nc.dram_tensor("ident", [128, 128], F32, kind="ExternalInput").ap()
    up_d = nc.dram_tensor("Uprev", [3, NH, 128, 128], F32, kind="ExternalInput").ap()
    dp_d = nc.dram_tensor("Dprev", [128, 3 * NH], F32, kind="ExternalInput").ap()
    gn_d = nc.dram_tensor("gn", [128, NH], F32, kind="ExternalInput").ap()
    gb_d = nc.dram_tensor("ln1gb", [128, 32], F32, kind="ExternalInput").ap()
    x1T = nc.dram_tensor("x1T", [D, T], F32, kind="ExternalOutput").ap()
    S = Sched()
    with ExitStack() as es:
        arena = es.enter_context(nc.sbuf_tensor("arena", [128, NCH * T], F32))
        zf = arena[:, :].rearrange("p (c t) -> p c t", c=NCH, t=T)
        xb = _carve(arena, 0, NCH * T, BF16).rearrange("p (c t) -> p c t", c=NCH, t=T)
        off = NCH * T * 2
        A = _carve(arena, off, T, F32); off += 4 * T
        B = _carve(arena, off, T, F32); off += 4 * T
        C = _carve(arena, off, T, F32); off += 4 * T
        kend = _carve(arena, off, T, BF16); off += 2 * T
        kdec = _carve(arena, off, T, BF16); off += 2 * T
        qdec = _carve(arena, off, T, BF16); off += 2 * T
        ibf = _carve(arena, off, T, BF16); off += 2 * T
        sg = _carve(arena, off, T, BF16); off += 2 * T
        osq = _carve(arena, off, T, BF16); off += 2 * T
        attm = _carve(arena, off, T, BF16).rearrange("p (j t) -> p j t", j=8); off += 2 * T
        kt0 = _carve(arena, off, T, BF16).rearrange("p (j t) -> p j t", j=8); off += 2 * T
        kt1 = _carve(arena, off, T, BF16).rearrange("p (j t) -> p j t", j=8); off += 2 * T
        kt = (kt0, kt1)
        vtk = _carve(arena, off, T, BF16).rearrange("p (j t) -> p j t", j=8); off += 2 * T
        assert off <= NCH * T * 4
        y = es.enter_context(nc.sbuf_tensor("y", [128, NH, T], BF16))
        Sb = es.enter_context(nc.sbuf_tensor("Sb", [128, NCK, 128], BF16))
        Sf = [es.enter_context(nc.sbuf_tensor("Sf%d" % k, [128, 128], F32)) for k in range(2)]
        upst = es.enter_context(nc.sbuf_tensor("upst", [128, 3, 128], F32))
        dec = es.enter_context(nc.sbuf_tensor("dec", [128, NCK], F32))
        dp = es.enter_context(nc.sbuf_tensor("dp", [128, 3, NH], F32))
        gn = es.enter_context(nc.sbuf_tensor("gn_s", [128, NH], F32))
        gb = es.enter_context(nc.sbuf_tensor("gb_s", [128, NCH, 2], F32))
        ones = es.enter_context(nc.sbuf_tensor("ones", [128, 128], BF16))
        eps = es.enter_context(nc.sbuf_tensor("eps", [128, 1], F32))
        xs = [es.enter_context(nc.sbuf_tensor("xs%d" % i, [128, T], F32)) for i in range(2)]
        tmps = {"eps": eps,
                "zb": [es.enter_context(nc.sbuf_tensor("zb%d" % i, [128, T], BF16)) for i in range(2)],
                "zs": [es.enter_context(nc.sbuf_tensor("zs%d" % i, [128, T], BF16)) for i in range(2)],
                "mean": xs[0], "rstd": xs[1]}
        ps = es.enter_context(nc.psum_tensor("ps", [128, 4096], F32))
        psb = ps[:, :].bitcast(BF16)
        chp = S.new_chan(total=True)
        chx = S.new_chan(total=True)
        cst = hgrn_consts(S, nc, es, lbp_d, mask2_d, ident_d, chp)
        S.dma("sp", dp[:, :, :], dp_d.rearrange("p (j h) -> p j h", j=3), chp)
        S.dma("sp", gn[:, :], gn_d, chp)
        S.dma("sp", gb[:, :, :], gb_d.rearrange("p (c j) -> p c j", j=2), chp)
        S.memset("dve", ones[:, :], 1.0)
        S.memset("dve", eps[:, :], LN_EPS)
        ws = WeightStream(S, nc, es, 3, 16 * 512)
        wi = w_in.rearrange("(kc p) n -> p kc n", p=128)
        wo = w_out.rearrange("(kc p) n -> p kc n", p=128)
        uses = [ws.plan((16, 512), wi[:, :, h * 512:(h + 1) * 512]) for h in range(NH)]
        u_o = [ws.plan((16, 512), wo[:, :, i * 512:(i + 1) * 512]) for i in range(4)]
        for c in range(NCH):
            S.dma("pool", xb[:, c, :], xT[c * 128:(c + 1) * 128, :], chx)
        ws.release(-1)
        chu = S.new_chan()
        G0, G1 = 0, 1024
        for h in range(NH):
            wt = ws.get(uses[h])

            def proj(blk, g0):
                for kc in range(NCH):
                    for t_ in range(2):
                        S.mm(ps[:, g0 + t_ * 512:g0 + (t_ + 1) * 512], wt[:, kc, blk * 128:(blk + 1) * 128],
                             xb[:, kc, t_ * 512:(t_ + 1) * 512], start=(kc == 0), stop=(kc == NCH - 1))
            S.dma("sp", upst[:, :, :], up_d[:, h, :, :].rearrange("j d e -> d j e"), chu)
            S.memset("dve", Sf[0][:, :], 0.0)
            cur = 0
            for j in range(3):
                S.stt(Sf[1 - cur][:, :], Sf[cur][:, :], dp[:, j, h:h + 1], upst[:, j, :], ALU.mult, ALU.add)
                cur = 1 - cur
            Sfl = [Sf[cur], Sf[1 - cur]]
            proj(2, G0)
            proj(3, G1)
            hgrn_gates(S, h, cst, ps[:, G0:G0 + T], A, B, C, kend, dec[:, :], kdec_bf=kdec, eC=True)
            S.act(ibf, ps[:, G1:G1 + T], AF.Identity)
            proj(0, G0)
            proj(1, G1)
            ws.release(uses[h])
            S.act(B, ps[:, G0:G0 + T], AF.Silu)
            S.tt("dve", qdec, B, C, ALU.mult)
            S.act(sg, ps[:, G1:G1 + T], AF.Sigmoid)
            hgrn_transposes(S, cst, psb, kend, kt0, kt1)
            hgrn_transposes(S, cst, psb, ibf, vtk)
            hgrn_state_scan(S, ps, kt, vtk, dec, Sfl, Sb=Sb)
            for half in range(2):
                for jj in range(4):
                    j = half * 4 + jj
                    S.mm(ps[:, 2560 + jj * 128:2560 + (jj + 1) * 128], kdec[:, j * 128:(j + 1) * 128],
                         qdec[:, j * 128:(j + 1) * 128], start=True, stop=True)
                for jj in range(4):
                    j = half * 4 + jj
                    S.tt("dve", attm[:, j, :], ps[:, 2560 + jj * 128:2560 + (jj + 1) * 128], cst["mask2"][:, :],
                         ALU.mult)
            O0 = 3072
            for j in range(8):
                S.mm(ps[:, O0 + j * 128:O0 + (j + 1) * 128], vtk[:, j, :], attm[:, j, :], start=True, stop=False)
                S.mm(ps[:, O0 + j * 128:O0 + j * 128 + 64], Sb[:, 2 * j, :], qdec[:, j * 128:j * 128 + 64],
                     start=False, stop=False)
                S.mm(ps[:, O0 + j * 128 + 64:O0 + (j + 1) * 128], Sb[:, 2 * j + 1, :],
                     qdec[:, j * 128 + 64:(j + 1) * 128], start=False, stop=True)
            S.act(osq, ps[:, O0:O0 + T], AF.Square)
            for t_ in range(2):
                S.mm(ps[:, G0 + t_ * 512:G0 + (t_ + 1) * 512], ones[:, :], osq[:, t_ * 512:(t_ + 1) * 512],
                     start=True, stop=True)
            S.act(A, ps[:, G0:G0 + T], AF.Ln, bias=eps[:, 0:1], scale=1.0 / 128)
            S.act(A, A, AF.Exp, scale=-0.5)
            S.stt(C, ps[:, O0:O0 + T], gn[:, h:h + 1], A, ALU.mult, ALU.mult)
            S.tt("dve", y[:, h, :], C, sg, ALU.mult)
        chs = [S.new_chan(), S.new_chan()]
        for i in range(4):
            wt = ws.get(u_o[i])
            for sub in range(4):
                oc = i * 4 + sub
                g0 = (oc % 2) * 1024
                xst = xs[oc % 2]
                S.dma("sp", xst[:, :], xT[oc * 128:(oc + 1) * 128, :], chs[oc % 2])
                for kc in range(NCH):
                    for t_ in range(2):
                        S.mm(ps[:, g0 + t_ * 512:g0 + (t_ + 1) * 512], wt[:, kc, sub * 128:(sub + 1) * 128],
                             y[:, kc, t_ * 512:(t_ + 1) * 512], start=(kc == 0), stop=(kc == NCH - 1))
                S.stt(zf[:, oc, :], xst[:, :], ALPHA, ps[:, g0:g0 + T], ALU.mult, ALU.add)
            ws.release(u_o[i])
        emit_ln(S, zf, 0, T, gb, ones, tmps, ps, [lambda c: zf[:, c, :]])
        cho = [S.new_chan() for _ in range(4)]
        for c in range(NCH):
            S.dma("sp", x1T[c * 128:(c + 1) * 128, :], zf[:, c, :], cho[c % 4])
        S.emit(nc, es, final_chans=cho)
    return nc, S


def _run(nc, in_maps):
    return run_bass_kernel_spmd(nc, in_maps, core_ids=list(range(NCORES))).results


def kernel_unfused(x, ev_w_in, ev_ln_v_g, ev_ln_v_b, ev_w_s, ev_b_s, ev_w_pool, ev_pool_scale,
           ev_w_out, od_w_in, od_norm_g, od_w_out, lb_param, ffn_w_up, ffn_conv_w,
           ffn_conv_b, ffn_w_down, ln1_g, ln1_b, ln2_g, ln2_b):
    f32 = np.float32
    nc0, _ = build_mix0_launch()
    maps0 = prep_mix0_inputs(x, ev_w_in, ev_ln_v_g, ev_ln_v_b, ev_w_s, ev_b_s, ev_w_pool, ev_pool_scale,
                             ev_w_out, ln1_g, ln1_b)
    r0 = _run(nc0, maps0)
    x1T = [r0[c]["x1T"] for c in range(NCORES)]
    ncf, _ = build_ffn_launch()

    def ffn(l, xTs):
        cwb, gb = prep_ffn_params(l, ffn_conv_w, ffn_conv_b, ln2_g, ln2_b)
        wu = np.ascontiguousarray(ffn_w_up[l], f32)
        wd = np.ascontiguousarray(ffn_w_down[l], f32)
        maps = [{"xT": np.ascontiguousarray(xTs[c], f32), "w_up": wu, "w_down": wd, "cwb": cwb, "ln2gb": gb}
                for c in range(NCORES)]
        r = _run(ncf, maps)
        return [r[c]["yT"] for c in range(NCORES)]

    x2T = ffn(0, x1T)
    ncp, _ = build_hgrn_pre_launch()
    w_in_r = regroup_w_in(od_w_in[0])
    hc = hgrn_const_inputs(lb_param)
    mapsp = []
    for c in range(NCORES):
        m = {"xT": np.ascontiguousarray(x2T[c], f32), "w_in": w_in_r}
        m.update(hc)
        mapsp.append(m)
    rp = _run(ncp, mapsp)
    ncm, _ = build_hgrn_main_launch()
    gn = _pm(od_norm_g[0], NH)
    gb1 = np.ascontiguousarray(np.stack([_pm(ln1_g[1], 16), _pm(ln1_b[1], 16)], axis=-1).reshape(128, 32))
    w_out1 = np.ascontiguousarray(od_w_out[0], f32)
    mapsm = []
    for c in range(NCORES):
        b, s = divmod(c, 4)
        up = np.zeros((3, NH, 128, 128), f32)
        dp = np.zeros((128, 3, NH), f32)
        for j in range(s):
            pos = 3 - s + j
            up[pos] = rp[b * 4 + j]["U"]
            dp[:, pos, :] = rp[b * 4 + j]["Dd"]
        m = {"xT": np.ascontiguousarray(x2T[c], f32), "w_in": w_in_r, "w_out": w_out1, "Uprev": up,
             "Dprev": np.ascontiguousarray(dp.reshape(128, 3 * NH)), "gn": gn, "ln1gb": gb1}
        m.update(hc)
        mapsm.append(m)
    rm = _run(ncm, mapsm)
    x1bT = []
    for c in range(NCORES):
        b, s = divmod(c, 4)
        xt = np.zeros((D, T + 2), f32)
        xt[:, 2:] = rm[c]["x1T"]
        if s > 0:
            xt[:, 0:2] = rm[c - 1]["x1T"][:, T - 2:T]
        x1bT.append(xt)
    outT = ffn(1, x1bT)
    out = np.zeros((2, 4 * T, D), f32)
    for c in range(NCORES):
        b, s = divmod(c, 4)
        out[b, s * T:(s + 1) * T] = outT[c].T
    return out


R0 = 0
R0_SZ = NCH * (T + 2) * 4
R1 = R0 + R0_SZ
R1_SZ = NCH * (T + 2) * 2
R2 = R1 + R1_SZ
R2_SZ = 66560
AR_BYTES = R2 + R2_SZ
SEQ_GROUPS = [[0, 1, 2, 3], [4, 5, 6, 7]]


def build_fused(use_cc=True):
    nc = bass.Bass("TRN2", target_bir_lowering=False)

    def din(name, shape):
        return nc.dram_tensor(name, shape, F32, kind="ExternalInput").ap()
    x0T = din("x0T", [D, TH])
    ev_w_in = din("ev_w_in", [D, 3072])
    ev_w_out = din("ev_w_out", [D, D])
    wsT_d = din("wsT", [128, 8 * 128])
    mask_d = din("maskA", [128, 128])
    bsT_d = din("bsT", [128, 8 * 128])
    lnv_d = din("lnv", [128, 2 * 1024])
    wp_d = din("w_pool", [4 * 256, 256])
    psc_d = din("pscale", [128, 8])
    rc_d = din("rcnt", [128, 64])
    flag_d = din("flag", [128, 1])
    oh_d = din("oh", [128, 8])
    ln1gb_d = din("ln1gb", [128, 64])
    ln2gb_d = din("ln2gb", [128, 64])
    cwb_d = din("cwb", [128, 2 * 88 * 4])
    w_up = [din("w_up%d" % l, [D, 2 * DFF]) for l in range(2)]
    w_down = [din("w_down%d" % l, [DFF, D]) for l in range(2)]
    od_w_in = din("od_w_in", [D, 4 * D])
    od_w_out = din("od_w_out", [D, D])
    lbp_d = din("lbp", [128, 2 * NH])
    mask2_d = din("mask2", [128, 128])
    ident_d = din("ident", [128, 128])
    gn_d = din("gn", [128, NH])
    outT = nc.dram_tensor("outT", [D, T], F32, kind="ExternalOutput").ap()
    xsp = nc.dram_tensor("xsp", [D, T], F32).ap()
    ccin = [nc.dram_tensor("ccin%d" % g, [4 * 4 * 128, 129], F32) for g in range(4)]
    ccout = [nc.dram_tensor("ccout%d" % g, [4 * 4 * 128, 129], F32) for g in range(4)]
    cch_in = nc.dram_tensor("cch_in", [4 * 128, 32], F32)
    cch_out = nc.dram_tensor("cch_out", [4 * 128, 32], F32)

    S = Sched()
    with ExitStack() as es:
        AR = es.enter_context(nc.sbuf_tensor("AR", [128, AR_BYTES // 4], F32))

        def cv(off, shape, dt):
            n = 1
            for k in shape:
                n *= k
            v = _carve(AR, off, n, dt)
            if len(shape) == 2:
                v = v.rearrange("p (a b) -> p a b", a=shape[0], b=shape[1])
            return v

        def sb(name, shape, dt=F32):
            return es.enter_context(nc.sbuf_tensor(name, shape, dt))
        psc = sb("psc_s", [128, 8]); rc = sb("rc_s", [128, 4, 16]); flag = sb("flag_s", [128, 1])
        oh = sb("oh_s", [128, 8]); ln1gb = sb("ln1gb_s", [128, 2, NCH, 2]); ln2gb = sb("ln2gb_s", [128, 2, NCH, 2])
        cwb = sb("cwb_s", [128, 2, 88, 4]); ones = sb("ones", [128, 128], BF16); eps = sb("eps", [128, 1])
        st = sb("st", [128, 8]); small = sb("small", [128, 32]); gn = sb("gn_s", [128, NH])
        Dd = sb("Dd_s", [128, NH]); tiny = sb("tiny", [128, 8])
        hstg = sb("hstg", [128, 4, 32]); hld = sb("hld", [128, 4, 32]); hsum = sb("hsum", [128, 32])
        ps = es.enter_context(nc.psum_tensor("ps", [128, 4096], F32))
        psb = ps[:, :].bitcast(BF16)
        ccsem = [es.enter_context(nc.semaphore("ccs%d" % i)) for i in range(5)]
        ws = WeightStream(S, nc, es, 4, 16 * 256)

        wi0 = ev_w_in.rearrange("(kc p) n -> p kc n", p=128)
        wo0 = ev_w_out.rearrange("(kc p) n -> p kc n", p=128)
        u_xb = [ws.plan((16, 256), wi0[:, :, 2048 + i * 256:2048 + (i + 1) * 256]) for i in range(4)]
        u_u = [ws.plan((16, 256), wi0[:, :, i * 256:(i + 1) * 256]) for i in range(4)]
        u_v = [ws.plan((16, 256), wi0[:, :, 1024 + i * 256:1024 + (i + 1) * 256]) for i in range(4)]
        u_o = [ws.plan((16, 256), wo0[:, :, i * 256:(i + 1) * 256]) for i in range(8)]
        plan0 = plan_ffn_weights(ws, w_up[0], w_down[0], split_last=True)
        wi1 = od_w_in.rearrange("(kc p) n -> p kc n", p=128)
        wo1 = od_w_out.rearrange("(kc p) n -> p kc n", p=128)
        u_pre = [ws.plan((16, 256), wi1[:, :, h * 512 + 256:h * 512 + 512]) for h in range(NH)]
        u_main = []
        for h in range(NH):
            fi = ws.plan((16, 256), wi1[:, :, h * 512 + 256:h * 512 + 512])
            qg = ws.plan((16, 256), wi1[:, :, h * 512:h * 512 + 256])
            u_main.append((fi, qg))
        u_o1 = [ws.plan((16, 256), wo1[:, :, (i % 8) * 256:((i % 8) + 1) * 256]) for i in range(16)]
        plan1 = plan_ffn_weights(ws, w_up[1], w_down[1], split_last=True)

        x0b = cv(R0, (NCH, TH), BF16)
        zf = cv(R0, (NCH, T + 2), F32)
        xb = cv(R1, (NCH, T + 2), BF16)
        pp = cv(R1, (8, TH), BF16)
        g1 = [cv(R1 + 18432, (TH,), F32), cv(R1 + 23040, (TH,), F32)]
        xbf = cv(R1 + 27648, (16 + TH,), F32)
        u = cv(R2, (8, TH), BF16)
        vt = cv(R2 + 18432, (9, 1024), BF16)
        g2p = [cv(R2 + 36864, (16 + TH,), F32), cv(R2 + 41536, (16 + TH,), F32)]
        g2 = [t[:, 16:16 + TH] for t in g2p]
        tA, tB = g2p
        wsT = cv(R2 + 46208, (8, 128), BF16)
        mask = cv(R2 + 48256, (128,), F32)
        bsT = cv(R2 + 48768, (8, 128), F32)
        lnv = cv(R2 + 52864, (2, 1024), F32)
        wp = cv(R2 + 61056, (8, 256), BF16)
        wsTf = g1[1][:, 0:1024].rearrange("p (h t) -> p h t", h=8)
        hb = [cv(R0 + 36864 + k * 4608, (TH,), F32) for k in range(2)]

        chp = S.new_chan(total=True)
        chx = S.new_chan(total=True)
        S.dma("sp", wsTf, wsT_d.rearrange("p (h t) -> p h t", h=8), chp)
        S.dma("sp", mask, mask_d, chp)
        S.dma("sp", bsT, bsT_d.rearrange("p (h t) -> p h t", h=8), chp)
        S.dma("sp", lnv, lnv_d.rearrange("p (a c) -> p a c", a=2), chp)
        S.dma("sp", psc[:, :], psc_d, chp)
        S.dma("sp", rc[:, :, :], rc_d.rearrange("p (g j) -> p g j", g=4), chp)
        S.dma("sp", flag[:, :], flag_d, chp)
        S.dma("sp", oh[:, :], oh_d, chp)
        S.dma("sp", ln1gb[:, :, :, :], ln1gb_d.rearrange("p (l c j) -> p l c j", l=2, j=2), chp)
        S.dma("sp", ln2gb[:, :, :, :], ln2gb_d.rearrange("p (l c j) -> p l c j", l=2, j=2), chp)
        S.dma("sp", cwb[:, :, :, :], cwb_d.rearrange("p (l c j) -> p l c j", l=2, j=4), chp)
        S.dma("sp", gn[:, :], gn_d, chp)
        S.dma("pool", wp, wp_d.rearrange("(a p) n -> p a n", p=128), chx)
        for c in range(NCH):
            S.dma("pool", x0b[:, c, :], x0T[c * 128:(c + 1) * 128, :], chx)
        ws.release(-1)
        S.memset("dve", ones[:, :], 1.0)
        S.memset("dve", eps[:, :], LN_EPS)
        S.memset("dve", xbf[:, 0:16], 0.0)
        S.memset("dve", tA[:, 0:16], 0.0)
        S.memset("dve", tB[:, 0:16], 0.0)
        for h in range(8):
            S.tt("dve", wsT[:, h, :], wsTf[:, h, :], mask, ALU.mult)

        GR = (0, 1536)
        TT3 = ((0, 512), (512, 512), (1024, 128))

        def proj_fm(wt, sub, g0):
            for kc in range(NCH):
                for (t0, w) in TT3:
                    S.mm(ps[:, g0 + t0:g0 + t0 + w], wt[:, kc, sub * 128:(sub + 1) * 128], x0b[:, kc, t0:t0 + w],
                         start=(kc == 0), stop=(kc == NCH - 1))
        gi = 0
        for i in range(4):
            wt = ws.get(u_xb[i])
            for sub in range(2):
                c = i * 2 + sub
                g = c // 2
                g0 = GR[gi % 2]
                gi += 1
                proj_fm(wt, sub, g0)
                S.act(xbf[:, 16:16 + TH], ps[:, g0:g0 + TH], AF.Identity)
                src = xbf
                dsts = [tA, tB]
                for k in range(g + 1):
                    sh = 1 << k
                    dst = dsts[k % 2]
                    S.tt("dve", dst[:, 16:16 + TH], src[:, 16:16 + TH], src[:, 16 - sh:16 + TH - sh], ALU.add)
                    src = dst
                win = B_WINDOWS[g]
                S.stt(pp[:, c, :], src[:, 16:16 + TH], 1.0 / win, xbf[:, 16:16 + TH], ALU.mult, ALU.subtract)
                S.tt("dve", small[:, 0:16], src[:, 16 + 128:16 + 144], rc[:, g, :], ALU.mult)
                S.tt("dve", pp[:, c, 128:144], small[:, 0:16], xbf[:, 16 + 128:16 + 144], ALU.subtract)
            ws.release(u_xb[i])
        for i in range(4):
            wt = ws.get(u_u[i])
            for sub in range(2):
                c = i * 2 + sub
                g0 = GR[gi % 2]
                proj_fm(wt, sub, g0)
                S.act(hb[gi % 2], ps[:, g0:g0 + TH], AF.Identity)
                emit_gelu(S, u[:, c, :], hb[gi % 2], g1[gi % 2], g2[gi % 2])
                gi += 1
            ws.release(u_u[i])
        wv = [ws.get(k) for k in u_v]
        pend = None
        for tk in range(9):
            vr = g1[tk % 2]
            for cg in range(4):
                r0 = 2048 + ((tk * 4 + cg) % 4) * 512
                for kc in range(NCH):
                    S.mm(ps[:, r0:r0 + 256], x0b[:, kc, tk * 128:(tk + 1) * 128], wv[cg][:, kc, :],
                         start=(kc == 0), stop=(kc == NCH - 1))
                hv = hb[tk % 2][:, cg * 256:(cg + 1) * 256]
                t1v = g2[0][:, cg * 256:(cg + 1) * 256]
                t2v = g2[1][:, cg * 256:(cg + 1) * 256]
                S.act(hv, ps[:, r0:r0 + 256], AF.Identity)
                S.act(t1v, hv, AF.Square)
                S.ts("dve", t1v, t1v, GELU_C, 1.0, ALU.mult, ALU.add)
                S.tt("dve", t1v, t1v, hv, ALU.mult)
                if pend is not None:
                    pend()

                def pend(t1v=t1v, t2v=t2v, hv=hv, dstv=vr[:, cg * 256:(cg + 1) * 256]):
                    S.act(t2v, t1v, AF.Sigmoid, scale=GELU_S)
                    S.tt("dve", dstv, t2v, hv, ALU.mult)
            pend()
            pend = None
            sq = g2[0]
            S.add("dve", lambda e, vr=vr: e.reduce_sum(out=st[:, 0:1], in_=vr[:, 0:1024], axis=mybir.AxisListType.X),
                  reads=[vr[:, 0:1024]], writes=[st[:, 0:1]])
            S.act(sq[:, 0:1024], vr[:, 0:1024], AF.Square)
            S.add("dve", lambda e, sq=sq: e.reduce_sum(out=st[:, 1:2], in_=sq[:, 0:1024], axis=mybir.AxisListType.X),
                  reads=[sq[:, 0:1024]], writes=[st[:, 1:2]])
            S.ts("dve", st[:, 2:3], st[:, 0:1], 1.0 / 1024, None, ALU.mult)
            S.tt("dve", st[:, 3:4], st[:, 2:3], st[:, 2:3], ALU.mult)
            S.stt(st[:, 4:5], st[:, 1:2], 1.0 / 1024, st[:, 3:4], ALU.mult, ALU.subtract)
            S.act(st[:, 5:6], st[:, 4:5], AF.Sqrt, bias=eps[:, 0:1], scale=1.0)
            S.add("dve", lambda e: e.reciprocal(out=st[:, 6:7], in_=st[:, 5:6]), reads=[st[:, 5:6]],
                  writes=[st[:, 6:7]])
            S.ts("dve", vr[:, 0:1024], vr[:, 0:1024], st[:, 2:3], st[:, 6:7], ALU.subtract, ALU.mult)
            S.tt("dve", vr[:, 0:1024], vr[:, 0:1024], lnv[:, 0, :], ALU.mult)
            S.tt("dve", vt[:, tk, :], vr[:, 0:1024], lnv[:, 1, :], ALU.add)
        ws.release(u_v[3])
        for tk in range(9):
            for half in range(2):
                r0 = 2048 + half * 512
                for hh in range(4):
                    h = half * 4 + hh
                    S.mm(ps[:, r0 + hh * 128:r0 + (hh + 1) * 128], vt[:, tk, h * 128:(h + 1) * 128], wsT[:, h, :],
                         start=True, stop=True)
                tmp = g2[half][:, 0:512].rearrange("p (h t) -> p h t", h=4)
                S.tt("dve", tmp, ps[:, r0:r0 + 512].rearrange("p (h t) -> p h t", h=4),
                     bsT[:, half * 4:half * 4 + 4, :], ALU.add)
                uu = u[:, half * 4:half * 4 + 4, tk * 128:(tk + 1) * 128]
                S.tt("dve", uu, tmp, uu, ALU.mult)
        for g in range(4):
            for oc in range(2):
                g0 = GR[oc]
                for kc in range(2):
                    for (t0, w) in TT3:
                        S.mm(ps[:, g0 + t0:g0 + t0 + w], wp[:, g * 2 + kc, oc * 128:(oc + 1) * 128],
                             pp[:, g * 2 + kc, t0:t0 + w], start=(kc == 0), stop=(kc == 1))
            for oc in range(2):
                g0 = GR[oc]
                c = g * 2 + oc
                S.act(pp[:, c, :], ps[:, g0:g0 + TH], AF.Identity, scale=psc[:, c:c + 1])
        xs = [g1[0], g1[1]]
        chs = [S.new_chan(), S.new_chan()]
        for i in range(8):
            wt = ws.get(u_o[i])
            for sub in range(2):
                oc = i * 2 + sub
                g0 = GR[oc % 2]
                xst = xs[oc % 2]
                S.dma("sp", xst[:, 0:T + 2], x0T[oc * 128:(oc + 1) * 128, 126:TH], chs[oc % 2])
                for kc in range(NCH):
                    src = u[:, kc, :] if kc < 8 else pp[:, kc - 8, :]
                    lw = wt[:, kc, sub * 128:(sub + 1) * 128]
                    S.mm(ps[:, g0 + 510:g0 + 512], lw, src[:, 126:128], start=(kc == 0), stop=(kc == NCH - 1))
                    S.mm(ps[:, g0 + 512:g0 + 1024], lw, src[:, 128:640], start=(kc == 0), stop=(kc == NCH - 1))
                    S.mm(ps[:, g0 + 1024:g0 + 1536], lw, src[:, 640:1152], start=(kc == 0), stop=(kc == NCH - 1))
                S.stt(zf[:, oc, :], xst[:, 0:T + 2], ALPHA, ps[:, g0 + 510:g0 + 1536], ALU.mult, ALU.add)
            ws.release(u_o[i])
        tm_ln1 = {"eps": eps, "mean": g2[0], "rstd": g2[1],
                  "zb": [cv(R2 + k * 2052, (T + 2,), BF16) for k in range(2)],
                  "zs": [cv(R2 + (2 + k) * 2052, (T + 2,), BF16) for k in range(2)]}
        def ln1_post(c):
            S.ts("dve", zf[:, c, 0:2], zf[:, c, 0:2], flag[:, 0:1], None, ALU.mult)
            S.act(xb[:, c, :], zf[:, c, :], AF.Identity)
        emit_ln(S, zf, 0, T + 2, ln1gb[:, 0, :, :], ones, tm_ln1, ps, [lambda c: zf[:, c, :]], post=ln1_post)

        gq = cv(R2, (12, T), BF16)
        ft = [cv(R2 + 24576 + k * 4096, (T,), F32) for k in range(6)]
        tm_ffn = {"a": ft[0:2], "v": ft[2:4], "s": ft[4:6], "eps": eps, "mean": ft[0], "rstd": ft[1],
                  "zb": [cv(R2 + 49152 + k * 2048, (T,), BF16) for k in range(2)],
                  "zs": [cv(R2 + 53248 + k * 2048, (T,), BF16) for k in range(2)]}
        xb2 = cv(R1, (NCH, T), BF16)
        def ln2_l0(tt_):
            c0 = 2 + tt_ * 512
            emit_ln(S, zf, c0, 512, ln2gb[:, 0, :, :], ones, tm_ffn, ps,
                    [lambda c: xb2[:, c, tt_ * 512:(tt_ + 1) * 512], lambda c: zf[:, c, c0:c0 + 512]])
        emit_ffn(S, ws, plan0, zf, xb, cwb[:, 0, :, :], gq, tm_ffn, ps, ln_cb=ln2_l0)
        chsp = [S.new_chan() for _ in range(NCH)]
        for c in range(NCH):
            S.dma("sp", xsp[c * 128:(c + 1) * 128, :], zf[:, c, 2:T + 2], chsp[c])

        def mkset(k):
            o0 = R0 + k * 32768
            d_ = {"A": cv(o0, (T,), F32), "B": cv(o0 + 4096, (T,), F32), "C": cv(o0 + 8192, (T,), F32),
                  "kend": cv(o0 + 12288, (T,), BF16), "kdec": cv(o0 + 14336, (T,), BF16),
                  "qdec": cv(o0 + 16384, (T,), BF16), "ibf": cv(o0 + 18432, (T,), BF16),
                  "sg": cv(o0 + 20480, (T,), BF16), "osq": cv(o0 + 22528, (T,), BF16),
                  "attm": cv(o0 + 24576, (8, 128), BF16), "kt0": cv(o0 + 26624, (8, 128), BF16),
                  "kt1": cv(o0 + 28672, (8, 128), BF16), "vtk": cv(o0 + 30720, (8, 128), BF16),
                  "Sb": cv(R2 + 32768, (NCK, 128), BF16) if k == 0 else cv(R2 + 61472, (NCK, 128), BF16),
                  "dec": sb("dec%d" % k, [128, NCK]), "Sf": [sb("Sf%d_%d" % (k, i), [128, 128]) for i in range(2)],
                  "Pp": [sb("Pp%d_%d" % (k, i), [128, 128]) for i in range(2)],
                  "upst": cv(R2 + 59408, (4, 129), F32) if k == 0 else sb("upst1", [128, 4, 129]),
                  "stg": cv(R2 + 57344, (4, 129), F32),
                  "chu": S.new_chan(), "chst": S.new_chan()}
            return d_
        sets = [mkset(0), mkset(1)]
        y = cv(R2, (NH, T), BF16)
        xs1 = [cv(R2 + 40960, (T,), F32), cv(R2 + 45056, (T,), F32)]
        tm_ln1b = {"eps": eps, "mean": xs1[0], "rstd": xs1[1],
                   "zb": [cv(R2 + 49152 + k * 2048, (T,), BF16) for k in range(2)],
                   "zs": [cv(R2 + 53248 + k * 2048, (T,), BF16) for k in range(2)]}
        chc = S.new_chan(total=True)
        cst = {}
        lbp = sb("lbp_s", [128, 2, NH]); cst["lb"] = sb("lb", [128, NH]); cst["oml"] = sb("oml", [128, NH])
        cst["mask2"] = sb("mask2_s", [128, 128]); identf = sb("identf", [128, 128]); cst["ident"] = sb("ident_s", [128, 128], BF16)
        cst["pm"] = sb("pm", [128, 2])
        cst["one"] = sb("one_c", [128, 1])
        S.memset("dve", cst["one"][:, :], 1.0)
        cst["rm"] = cv(R2 + 36864, (T,), F32)
        S.dma("sp", lbp[:, :, :], lbp_d.rearrange("p (l h) -> p l h", l=2), chc)
        S.dma("sp", cst["mask2"][:, :], mask2_d, chc)
        S.dma("sp", identf[:, :], ident_d, chc)
        S.copy("dve", cst["ident"][:, :], identf[:, :])
        S.tt("dve", cst["lb"][:, :], lbp[:, 1, :], lbp[:, 0, :], ALU.subtract)
        S.act(cst["lb"][:, :], cst["lb"][:, :], AF.Sigmoid)
        S.ts("dve", cst["oml"][:, :], cst["lb"][:, :], -1.0, 1.0, ALU.mult, ALU.add)
        S.memset("dve", cst["rm"], 1.0)
        S.memset("dve", cst["rm"].rearrange("p (c t) -> p c t", t=CH)[:, :, 0:1], 0.0)
        S.memset("dve", cst["pm"][:, :], 0.0)
        S.memset("dve", cst["pm"][0:64, 0:1], 1.0)
        S.memset("dve", cst["pm"][64:128, 1:2], 1.0)
        oh3 = oh[:, 0:4].rearrange("p (j o) -> p j o", o=1)
        G0, G1, PB5, O0 = 0, 1024, 2560, 3072

        def proj1(wt, blk, g0):
            for kc in range(NCH):
                for t_ in range(2):
                    S.mm(ps[:, g0 + t_ * 512:g0 + (t_ + 1) * 512], wt[:, kc, blk * 128:(blk + 1) * 128],
                         xb2[:, kc, t_ * 512:(t_ + 1) * 512], start=(kc == 0), stop=(kc == NCH - 1))

        def cc_op(idx, src_t, dst_t):
            if use_cc:
                def fn(e):
                    e.collective_compute("AllReduce", ALU.add, replica_groups=SEQ_GROUPS,
                                         ins=[src_t.ap().opt()], outs=[dst_t.ap().opt()]).then_inc(ccsem[idx])
                    return None
                S.add("pool", fn, reads=[src_t.ap()], writes=[])

                def fn2(e):
                    e.wait_ge(ccsem[idx], 1)
                    return e.memset(tiny[:, idx:idx + 1], 0.0)
                return lambda: S.add("pool", fn2, reads=[], writes=[dst_t.ap(), tiny[:, idx:idx + 1]])
            else:
                chq = S.new_chan()
                S.dma("sp", dst_t.ap(), src_t.ap(), chq)
                return lambda: None

        def scan_group(q, g4, st_):
            pb = (2560, 3072, 3584, 2560)[g4]
            for cc in range(4):
                c = g4 * 4 + cc
                j, par = divmod(c, 2)
                S.mm(ps[:, pb + cc * 128:pb + (cc + 1) * 128], (q["kt0"], q["kt1"])[par][:, j, :], q["vtk"][:, j, :],
                     start=True, stop=True)
            for cc in range(4):
                c = g4 * 4 + cc
                cur = st_["cur"]
                S.stt(q["Sf"][1 - cur][:, :], q["Sf"][cur][:, :], q["dec"][:, c:c + 1],
                      ps[:, pb + cc * 128:pb + (cc + 1) * 128], ALU.mult, ALU.add)
                st_["cur"] = 1 - cur
                if st_["sb"] and c + 1 < NCK:
                    S.act(q["Sb"][:, c + 1, :], q["Sf"][1 - cur][:, :], AF.Identity)

        def interleave(bsteps, asteps, after):
            ai = 0
            for bi, bstep in enumerate(bsteps):
                bstep()
                while ai < len(asteps) and after[ai] == bi:
                    asteps[ai]()
                    ai += 1
            while ai < len(asteps):
                asteps[ai]()
                ai += 1

        cc_done = []

        def pre_A(h):
            q = sets[h % 2]

            def a1():
                q["wt"] = ws.get(u_pre[h])
                proj1(q["wt"], 0, G0)

            def a2():
                proj1(q["wt"], 1, G1)
                ws.release(u_pre[h])

            def a2g():
                hgrn_gates(S, h, cst, ps[:, G0:G0 + T], q["A"], q["B"], q["C"], q["kend"], q["dec"][:, :])
                S.act(q["ibf"], ps[:, G1:G1 + T], AF.Identity)
                C3 = q["C"].rearrange("p (c t) -> p c t", t=CH)
                S.add("dve", lambda e, C3=C3, h=h: e.reduce_sum(out=Dd[:, h:h + 1], in_=C3[:, :, CH - 1:CH],
                                                               axis=mybir.AxisListType.XY),
                      reads=[q["C"]], writes=[Dd[:, h:h + 1]])
                S.act(Dd[:, h:h + 1], Dd[:, h:h + 1], AF.Exp)
            return [a1, a2, a2g]

        def pre_B(h):
            q = sets[h % 2]
            st_ = {"cur": 0, "sb": False}

            def b1():
                hgrn_transposes(S, cst, psb, q["kend"], q["kt0"], q["kt1"])

            def b1b():
                hgrn_transposes(S, cst, psb, q["ibf"], q["vtk"])
                S.memset("dve", q["Sf"][0][:, :], 0.0)

            def bfin():
                fin = st_["cur"]
                for j in range(4):
                    S.ts("dve", q["stg"][:, j, 0:128], q["Sf"][fin][:, :], oh[:, j:j + 1], None, ALU.mult)
                S.ts("dve", q["stg"][:, :, 128:129], oh3, Dd[:, h:h + 1], None, ALU.mult)
                g, hl = divmod(h, 4)
                S.dma("sp", ccin[g].ap().rearrange("(j l d) n -> d j l n", j=4, l=4)[:, :, hl, :], q["stg"][:, :, :],
                      q["chst"])
                if hl == 3:
                    cc_done.append(cc_op(g, ccin[g], ccout[g]))
            return [b1, b1b] + [lambda g4=g4: scan_group(q, g4, st_) for g4 in range(4)] + [bfin]

        for stp in pre_A(0):
            stp()
        for h in range(NH):
            nxt = pre_A(h + 1) if h + 1 < NH else []
            interleave(pre_B(h), nxt, [0, 3, 3])

        def main_A(h):
            q = sets[h % 2]
            g, hl = divmod(h, 4)
            fi, qg = u_main[h]

            def a1():
                if hl == 0:
                    cc_done[g]()
                up = q["upst"]
                S.dma("sp", up[:, :, :], ccout[g].ap().rearrange("(j l d) n -> d j l n", j=4, l=4)[:, :, hl, :], q["chu"])
                Pp_, Sf_ = q["Pp"], q["Sf"]
                S.stt(Pp_[0][:, :], up[:, 0, 0:128], up[:, 1, 128:129], up[:, 1, 0:128], ALU.mult, ALU.add)
                S.stt(Pp_[1][:, :], Pp_[0][:, :], up[:, 2, 128:129], up[:, 2, 0:128], ALU.mult, ALU.add)
                S.ts("dve", Sf_[0][:, :], up[:, 0, 0:128], oh[:, 1:2], None, ALU.mult)
                S.stt(Sf_[0][:, :], Pp_[0][:, :], oh[:, 2:3], Sf_[0][:, :], ALU.mult, ALU.add)
                S.stt(Sf_[0][:, :], Pp_[1][:, :], oh[:, 3:4], Sf_[0][:, :], ALU.mult, ALU.add)
                q["wt"] = ws.get(fi)
                proj1(q["wt"], 0, G0)

            def a2():
                proj1(q["wt"], 1, G1)
                ws.release(fi)

            def a2g():
                hgrn_gates(S, h, cst, ps[:, G0:G0 + T], q["A"], q["B"], q["C"], q["kend"], q["dec"][:, :],
                           kdec_bf=q["kdec"], eC=True)
                S.act(q["ibf"], ps[:, G1:G1 + T], AF.Identity)

            def a3():
                q["wt"] = ws.get(qg)
                proj1(q["wt"], 0, G0)

            def a4():
                proj1(q["wt"], 1, G1)
                ws.release(qg)
                S.act(q["B"], ps[:, G0:G0 + T], AF.Silu)
                S.tt("dve", q["qdec"], q["B"], q["C"], ALU.mult)
                S.act(q["sg"], ps[:, G1:G1 + T], AF.Sigmoid)
            return [a1, a2, a2g, a3, a4]

        def main_B(h):
            q = sets[h % 2]
            st_ = {"cur": 0, "sb": True}

            def b1():
                hgrn_transposes(S, cst, psb, q["kend"], q["kt0"], q["kt1"])

            def b1b():
                hgrn_transposes(S, cst, psb, q["ibf"], q["vtk"])
                S.act(q["Sb"][:, 0, :], q["Sf"][0][:, :], AF.Identity)

            def batt(half):
                pb = 3072 + half * 512
                for jj in range(4):
                    j = half * 4 + jj
                    S.mm(ps[:, pb + jj * 128:pb + (jj + 1) * 128], q["kdec"][:, j * 128:(j + 1) * 128],
                         q["qdec"][:, j * 128:(j + 1) * 128], start=True, stop=True)
                for jj in range(4):
                    j = half * 4 + jj
                    S.tt("dve", q["attm"][:, j, :], ps[:, pb + jj * 128:pb + (jj + 1) * 128], cst["mask2"][:, :],
                         ALU.mult)

            def bo():
                for j in range(8):
                    S.mm(ps[:, O0 + j * 128:O0 + (j + 1) * 128], q["vtk"][:, j, :], q["attm"][:, j, :], start=True,
                         stop=False)
                    S.mm(ps[:, O0 + j * 128:O0 + j * 128 + 64], q["Sb"][:, 2 * j, :], q["qdec"][:, j * 128:j * 128 + 64],
                         start=False, stop=False)
                    S.mm(ps[:, O0 + j * 128 + 64:O0 + (j + 1) * 128], q["Sb"][:, 2 * j + 1, :],
                         q["qdec"][:, j * 128 + 64:(j + 1) * 128], start=False, stop=True)
                S.act(q["osq"], ps[:, O0:O0 + T], AF.Square)

            def bnorm():
                for t_ in range(2):
                    sl = slice(t_ * 512, (t_ + 1) * 512)
                    S.mm(ps[:, PB5:PB5 + 512], ones[:, :], q["osq"][:, sl], start=True, stop=True)
                    S.act(q["A"][:, sl], ps[:, PB5:PB5 + 512], AF.Ln, bias=eps[:, 0:1], scale=1.0 / 128)
                S.act(q["A"], q["A"], AF.Exp, scale=-0.5)
                S.stt(q["C"], ps[:, O0:O0 + T], gn[:, h:h + 1], q["A"], ALU.mult, ALU.mult)
                S.tt("dve", y[:, h, :], q["C"], q["sg"], ALU.mult)
            return ([b1, b1b] + [lambda g4=g4: scan_group(q, g4, st_) for g4 in range(4)]
                    + [lambda: batt(0), lambda: batt(1), bo, bnorm])

        for stp in main_A(0):
            stp()
        for h in range(NH):
            nxt = main_A(h + 1) if h + 1 < NH else []
            interleave(main_B(h), nxt, [0, 3, 3, 5, 8])
        chs1 = [S.new_chan() for _ in range(4)]
        xs4 = [cv(R2 + 40960 + k * 2048, (512,), F32) for k in range(4)]
        tm_t = {"eps": eps, "mean": cv(R2 + 49152, (512,), F32), "rstd": cv(R2 + 51200, (512,), F32),
                "zb": [cv(R2 + 53248 + k * 1024, (512,), BF16) for k in range(2)],
                "zs": [cv(R2 + 55296 + k * 1024, (512,), BF16) for k in range(2)]}
        OB = (512, 1024, 1536, 2560, 3072, 3584)
        done_h = None
        cnt = 0
        for tile in (1, 0):
            c0 = 2 + tile * 512
            for i in range(8):
                wt = ws.get(u_o1[(1 - tile) * 8 + i])
                for sub in range(2):
                    oc = i * 2 + sub
                    g0 = OB[cnt % 6]
                    xst = xs4[cnt % 4]
                    S.dma("sp", xst, xsp[oc * 128:(oc + 1) * 128, tile * 512:(tile + 1) * 512], chs1[cnt % 4])
                    for kc in range(NCH):
                        S.mm(ps[:, g0:g0 + 512], wt[:, kc, sub * 128:(sub + 1) * 128],
                             y[:, kc, tile * 512:(tile + 1) * 512], start=(kc == 0), stop=(kc == NCH - 1))
                    S.stt(zf[:, oc, c0:c0 + 512], xst, ALPHA, ps[:, g0:g0 + 512], ALU.mult, ALU.add)
                    cnt += 1
                ws.release(u_o1[(1 - tile) * 8 + i])
            emit_ln(S, zf, c0, 512, ln1gb[:, 1, :, :], ones, tm_t, ps,
                    [lambda c, c0=c0: xb[:, c, c0:c0 + 512], lambda c, c0=c0: zf[:, c, c0:c0 + 512]],
                    ps_off=(0, 2048))
            if tile == 1:
                for j in range(4):
                    S.ts("dve", hstg[:, j, :].rearrange("p (c t) -> p c t", t=2), zf[:, :, T:T + 2], oh[:, j:j + 1],
                         None, ALU.mult)
                chh = S.new_chan()
                S.dma("sp", cch_in.ap().rearrange("(j p) n -> p j n", p=128), hstg[:, :, :], chh)
                done_h = cc_op(4, cch_in, cch_out)
        done_h()
        chh2 = S.new_chan()
        S.dma("sp", hld[:, :, :], cch_out.ap().rearrange("(j p) n -> p j n", p=128), chh2)
        S.ts("dve", hsum[:, :], hld[:, 0, :], oh[:, 4:5], None, ALU.mult)
        for j in range(1, 4):
            S.stt(hsum[:, :], hld[:, j, :], oh[:, 4 + j:5 + j], hsum[:, :], ALU.mult, ALU.add)
        S.act(xb[:, :, 0:2], hsum[:, :].rearrange("p (c t) -> p c t", t=2), AF.Identity)
        def ln2_l1(tt_):
            c0 = 2 + tt_ * 512
            emit_ln(S, zf, c0, 512, ln2gb[:, 1, :, :], ones, tm_ffn, ps, [lambda c: zf[:, c, c0:c0 + 512]])
        emit_ffn(S, ws, plan1, zf, xb, cwb[:, 1, :, :], gq, tm_ffn, ps, ln_cb=ln2_l1)
        cho = [S.new_chan() for _ in range(4)]
        for c in range(NCH):
            S.dma("sp", outT[c * 128:(c + 1) * 128, :], zf[:, c, 2:T + 2], cho[c % 4])
        S.emit(nc, es, final_chans=cho)
    return nc, S


def fused_inputs(inp):
    f32 = np.float32
    maps = prep_mix0_inputs(inp["x"], inp["ev_w_in"], inp["ev_ln_v_g"], inp["ev_ln_v_b"], inp["ev_w_s"],
                            inp["ev_b_s"], inp["ev_w_pool"], inp["ev_pool_scale"], inp["ev_w_out"],
                            inp["ln1_g"], inp["ln1_b"])
    ln1gb = np.stack([np.stack([_pm(inp["ln1_g"][l], 16), _pm(inp["ln1_b"][l], 16)], axis=-1) for l in range(2)], axis=1)
    ln2gb = np.stack([np.stack([_pm(inp["ln2_g"][l], 16), _pm(inp["ln2_b"][l], 16)], axis=-1) for l in range(2)], axis=1)
    cwbs = []
    for l in range(2):
        cw = np.asarray(inp["ffn_conv_w"][l], f32)
        cb = np.asarray(inp["ffn_conv_b"][l], f32)
        cwbs.append(np.stack([_pm(cw[0], 88), _pm(cw[1], 88), _pm(cw[2], 88), _pm(cb, 88)], axis=-1))
    cwb = np.stack(cwbs, axis=1)
    common = {
        "ln1gb": np.ascontiguousarray(ln1gb.reshape(128, 64)), "ln2gb": np.ascontiguousarray(ln2gb.reshape(128, 64)),
        "cwb": np.ascontiguousarray(cwb.reshape(128, 2 * 88 * 4)),
        "w_up0": np.ascontiguousarray(inp["ffn_w_up"][0], f32), "w_up1": np.ascontiguousarray(inp["ffn_w_up"][1], f32),
        "w_down0": np.ascontiguousarray(inp["ffn_w_down"][0], f32),
        "w_down1": np.ascontiguousarray(inp["ffn_w_down"][1], f32),
        "od_w_in": regroup_w_in(inp["od_w_in"][0]), "od_w_out": np.ascontiguousarray(inp["od_w_out"][0], f32),
        "gn": _pm(inp["od_norm_g"][0], NH)}
    common.update(hgrn_const_inputs(inp["lb_param"]))
    out = []
    for c in range(NCORES):
        b, s = divmod(c, 4)
        m0 = maps[c]
        m = dict(common)
        for k in ("x0T", "wsT", "maskA", "bsT", "lnv", "w_pool", "pscale", "rcnt", "flag"):
            m[k] = m0[k]
        m["ev_w_in"] = m0["w_in"]
        m["ev_w_out"] = m0["w_out"]
        oh = np.zeros((128, 8), f32)
        oh[:, s] = 1.0
        if s > 0:
            oh[:, 4 + s - 1] = 1.0
        m["oh"] = oh
        out.append(m)
    return out


def kernel(**inputs):
    nc, _ = build_fused(use_cc=True)
    maps = fused_inputs(inputs)
    res = run_bass_kernel_spmd(nc, maps, core_ids=list(range(NCORES))).results
    out = np.zeros((2, 4 * T, D), np.float32)
    for c in range(NCORES):
        b, s = divmod(c, 4)
        out[b, s * T:(s + 1) * T] = res[c]["outT"].T
    return out
```

```python
import numpy as np
from contextlib import ExitStack
import concourse.bass as bass
import concourse.mybir as mybir
from concourse.bass_utils import run_bass_kernel_spmd

F32 = mybir.dt.float32
BF16 = mybir.dt.bfloat16
AF = mybir.ActivationFunctionType
ALU = mybir.AluOpType

D = 2048
NCH = 16
T = 1024
NCORES = 8
DFF = 5632
NFF = 44
ALPHA = 4.0 ** 0.25
LN_EPS = 1e-5
ENGS = ("pe", "act", "dve", "pool", "sp")
_DT_SIZE = {F32: 4, BF16: 2}


def _dsize(dt):
    return _DT_SIZE.get(dt, 4)


class _Op:
    __slots__ = ("eng", "idx", "fn", "deps", "chan", "chan_val", "signal", "val")


class Sched:
    def __init__(self):
        self.ops = {e: [] for e in ENGS}
        self.track = {}
        self.chan_cnt = []
        self.chan_total = []

    @staticmethod
    def _rng(ap):
        t = ap.tensor
        name = t.name
        sp = str(ap.space) if hasattr(ap, "space") else ""
        pat = ap.ap
        esz = _dsize(ap.dtype)
        if "DRAM" in sp.upper() or "Dram" in type(t).__name__ or "DRam" in type(t).__name__:
            ext = 1
            for (st, cnt) in pat:
                ext += abs(st) * (cnt - 1)
            return name, ap.offset * esz, (ap.offset + ext) * esz
        pstride = pat[0][0]
        lo = ap.offset % pstride if pstride > 0 else ap.offset
        ext = 1
        for (st, cnt) in pat[1:]:
            ext += abs(st) * (cnt - 1)
        return name, lo * esz, (lo + ext) * esz

    def _touch(self, name, lo, hi, op, is_write, deps):
        segs = self.track.setdefault(name, [])
        new = []
        covered = []
        for s in segs:
            slo, shi, w, rs = s
            if shi <= lo or slo >= hi:
                new.append(s)
                continue
            if slo < lo:
                new.append([slo, lo, w, list(rs)])
            if shi > hi:
                new.append([hi, shi, w, list(rs)])
            olo, ohi = max(slo, lo), min(shi, hi)
            if w is not None:
                deps.add(w)
            if is_write:
                for r in rs:
                    deps.add(r)
            else:
                covered.append([olo, ohi, w, rs + [op]])
        if is_write:
            new.append([lo, hi, op, []])
        else:
            covered.sort(key=lambda s: s[0])
            cur = lo
            for c in covered:
                if c[0] > cur:
                    new.append([cur, c[0], None, [op]])
                new.append(c)
                cur = c[1]
            if cur < hi:
                new.append([cur, hi, None, [op]])
        self.track[name] = new

    def add(self, eng, fn, reads=(), writes=(), chan=None):
        o = _Op()
        o.eng = eng
        o.fn = fn
        o.chan = chan
        o.signal = False
        o.val = None
        o.chan_val = None
        deps = set()
        for ap in reads:
            if ap is None or isinstance(ap, (int, float)):
                continue
            n, lo, hi = self._rng(ap)
            self._touch(n, lo, hi, o, False, deps)
        for ap in writes:
            n, lo, hi = self._rng(ap)
            if eng == "pe":
                lo = (lo // 2048) * 2048
                hi = ((hi + 2047) // 2048) * 2048
            self._touch(n, lo, hi, o, True, deps)
        deps.discard(o)
        o.deps = deps
        if chan is not None:
            self.chan_cnt[chan] += 1
            o.chan_val = 16 * self.chan_cnt[chan]
        o.idx = len(self.ops[eng])
        self.ops[eng].append(o)
        return o

    def new_chan(self, total=False):
        self.chan_cnt.append(0)
        self.chan_total.append(total)
        return len(self.chan_cnt) - 1

    def emit(self, nc, es, final_chans=()):
        for e in ENGS:
            for o in self.ops[e]:
                for d in o.deps:
                    if d.chan is None:
                        d.signal = True
        for e in ENGS:
            c = 0
            for o in self.ops[e]:
                if o.chan is None and o.signal:
                    c += 1
                    o.val = c
        esem = {e: es.enter_context(nc.semaphore("s_" + e)) for e in ENGS}
        csem = [es.enter_context(nc.semaphore("c_%d" % i)) for i in range(len(self.chan_cnt))]
        block = es.enter_context(nc.Block())
        nwaits = {e: 0 for e in ENGS}

        def run(engname, eobj):
            seen = {}
            for o in self.ops[engname]:
                need = {}
                for d in o.deps:
                    if d.chan is not None:
                        key = ("c", d.chan)
                        v = 16 * self.chan_cnt[d.chan] if self.chan_total[d.chan] else d.chan_val
                    else:
                        if d.eng == engname and engname == "pe":
                            continue
                        key = ("e", d.eng)
                        v = d.val
                    if v > need.get(key, 0):
                        need[key] = v
                for key, v in need.items():
                    if v <= seen.get(key, 0):
                        continue
                    seen[key] = v
                    sem = csem[key[1]] if key[0] == "c" else esem[key[1]]
                    eobj.wait_ge(sem, v)
                    nwaits[engname] += 1
                inst = o.fn(eobj)
                if o.chan is not None:
                    inst.then_inc(csem[o.chan], 16)
                elif o.signal:
                    assert inst is not None
                    inst.then_inc(esem[engname], 1)
            if engname == "sp":
                for ch in final_chans:
                    if self.chan_cnt[ch] > 0:
                        eobj.wait_ge(csem[ch], 16 * self.chan_cnt[ch])

        @block.tensor
        def _(e):
            run("pe", e)

        @block.scalar
        def _(e):
            run("act", e)

        @block.vector
        def _(e):
            run("dve", e)

        @block.gpsimd
        def _(e):
            run("pool", e)

        @block.sync
        def _(e):
            run("sp", e)

        self.nwaits = nwaits

    def mm(self, out, lhsT, rhs, start=True, stop=True):
        return self.add("pe", lambda e: e.matmul(out, lhsT=lhsT, rhs=rhs, start=start, stop=stop),
                        reads=[lhsT, rhs], writes=[out])

    def transpose(self, out, in_, ident):
        return self.add("pe", lambda e: e.transpose(out, in_, ident), reads=[in_, ident], writes=[out])

    def act(self, out, in_, func, bias=None, scale=None):
        kw = {}
        rd = [in_]
        if bias is not None:
            kw["bias"] = bias
            rd.append(bias)
        if scale is not None:
            kw["scale"] = scale
            rd.append(scale)
        return self.add("act", lambda e: e.activation(out=out, in_=in_, func=func, **kw), reads=rd, writes=[out])

    def tt(self, eng, out, in0, in1, op):
        return self.add(eng, lambda e: e.tensor_tensor(out=out, in0=in0, in1=in1, op=op),
                        reads=[in0, in1], writes=[out])

    def ts(self, eng, out, in0, s1, s2, op0, op1=None):
        if op1 is None:
            return self.add(eng, lambda e: e.tensor_scalar(out=out, in0=in0, scalar1=s1, scalar2=None, op0=op0),
                            reads=[in0, s1], writes=[out])
        return self.add(eng, lambda e: e.tensor_scalar(out=out, in0=in0, scalar1=s1, scalar2=s2, op0=op0, op1=op1),
                        reads=[in0, s1, s2], writes=[out])

    def stt(self, out, in0, scalar, in1, op0, op1):
        return self.add("dve", lambda e: e.scalar_tensor_tensor(out=out, in0=in0, scalar=scalar, in1=in1,
                                                                op0=op0, op1=op1),
                        reads=[in0, scalar, in1], writes=[out])

    def copy(self, eng, out, in_):
        if eng == "act":
            return self.add("act", lambda e: e.copy(out=out, in_=in_), reads=[in_], writes=[out])
        return self.add(eng, lambda e: e.tensor_copy(out=out, in_=in_), reads=[in_], writes=[out])

    def memset(self, eng, ap, val):
        return self.add(eng, lambda e: e.memset(ap, val), writes=[ap])

    def dma(self, eng, out, in_, chan):
        return self.add(eng, lambda e: e.dma_start(out=out, in_=in_), reads=[in_], writes=[out], chan=chan)


class WeightStream:
    def __init__(self, S, nc, es, nslots, free_elems, name="wslot"):
        self.S = S
        self.slots = [es.enter_context(nc.sbuf_tensor("%s%d" % (name, i), [128, free_elems], BF16))
                      for i in range(nslots)]
        self.chans = [S.new_chan() for _ in range(nslots)]
        self.uses = []
        self.loaded = 0
        self.released = -1
        self.n = nslots

    def plan(self, shape, src):
        self.uses.append((shape, src))
        return len(self.uses) - 1

    def view(self, k):
        shape, _ = self.uses[k]
        sl = self.slots[k % self.n]
        n = 1
        for s in shape:
            n *= s
        v = sl[:, 0:n]
        if len(shape) == 2:
            return v.rearrange("p (a b) -> p a b", a=shape[0], b=shape[1])
        return v

    def _load_upto(self, k):
        while self.loaded < len(self.uses) and self.loaded <= k:
            j = self.loaded
            _, src = self.uses[j]
            self.S.dma("pool", self.view(j), src, self.chans[j % self.n])
            self.loaded += 1

    def get(self, k):
        assert k <= self.released + self.n, (k, self.released)
        self._load_upto(k)
        return self.view(k)

    def release(self, k):
        self.released = max(self.released, k)
        self._load_upto(self.released + self.n)


FF_QUARTERS = (12, 10, 12, 10)


def plan_ffn_weights(ws, w_up, w_down, split_last=False):
    plan = []
    base = 0
    wu = w_up.rearrange("(kc p) n -> p kc n", p=128)
    for q, nq in enumerate(FF_QUARTERS):
        ups = []
        for j in range(0, nq, 2):
            ca = base + j
            ua = ws.plan((16, 256), wu[:, :, ca * 128:ca * 128 + 256])
            uv = ws.plan((16, 256), wu[:, :, (NFF + ca) * 128:(NFF + ca) * 128 + 256])
            ups.append((ca, ua, uv))
        downs = []
        wd = w_down[base * 128:(base + nq) * 128, :].rearrange("(j p) n -> p j n", p=128)
        reps = 2 if (split_last and q == len(FF_QUARTERS) - 1) else 1
        for rep in range(reps):
            for op_ in range(8):
                downs.append((op_, ws.plan((nq, 256), wd[:, :, op_ * 256:(op_ + 1) * 256])))
        plan.append((base, nq, ups, downs))
        base += nq
    return plan


def emit_ffn(S, ws, plan, xf, xb, cwb, gq, tmps, ps, ln_cb=None):
    G = (0, 1536)
    gi = 0
    for (base, nq, ups, downs) in plan:
        for (ca, ua, uv) in ups:
            wa = ws.get(ua)
            wv = ws.get(uv)
            for sub in range(2):
                c_a = ca + sub
                c_v = NFF + ca + sub
                j = c_a - base
                tm = {}
                for which, (wt, cc) in enumerate(((wa, c_a), (wv, c_v))):
                    g0 = G[which]
                    for kc in range(NCH):
                        lw = wt[:, kc, sub * 128:(sub + 1) * 128]
                        S.mm(ps[:, g0 + 510:g0 + 512], lw, xb[:, kc, 0:2], start=(kc == 0), stop=(kc == NCH - 1))
                        S.mm(ps[:, g0 + 512:g0 + 1024], lw, xb[:, kc, 2:514], start=(kc == 0), stop=(kc == NCH - 1))
                        S.mm(ps[:, g0 + 1024:g0 + 1536], lw, xb[:, kc, 514:1026], start=(kc == 0),
                             stop=(kc == NCH - 1))
                    tmp = tmps["a" if which == 0 else "v"][gi % 2]
                    tm[which] = tmp
                    S.act(tmp[:, :], ps[:, g0 + 512:g0 + 1536], AF.Identity, bias=cwb[:, cc, 3:4],
                          scale=cwb[:, cc, 2:3])
                    S.stt(tmp[:, :], ps[:, g0 + 511:g0 + 1535], cwb[:, cc, 1:2], tmp[:, :], ALU.mult, ALU.add)
                    S.stt(tmp[:, :], ps[:, g0 + 510:g0 + 1534], cwb[:, cc, 0:1], tmp[:, :], ALU.mult, ALU.add)
                sa = tmps["s"][gi % 2]
                S.act(sa[:, :], tm[0][:, :], AF.Silu)
                S.tt("dve", gq[:, j, :], sa[:, :], tm[1][:, :], ALU.mult)
                gi += 1
            ws.release(uv)
        def down_tile(wd, oc, sub, tt_, bank):
            pr = ps[:, 3072 + bank * 512:3072 + (bank + 1) * 512]
            for j in range(nq):
                S.mm(pr, wd[:, j, sub * 128:(sub + 1) * 128], gq[:, j, tt_ * 512:(tt_ + 1) * 512],
                     start=(j == 0), stop=(j == nq - 1))
            dst = xf[:, oc, 2 + tt_ * 512:2 + (tt_ + 1) * 512]
            if base == 0:
                S.stt(dst, dst, ALPHA, pr, ALU.mult, ALU.add)
            else:
                S.tt("dve", dst, dst, pr, ALU.add)
        if len(downs) == 8:
            for (op_, ud) in downs:
                wd = ws.get(ud)
                for sub in range(2):
                    for tt_ in range(2):
                        down_tile(wd, op_ * 2 + sub, sub, tt_, tt_)
                ws.release(ud)
        else:
            for tt_ in range(2):
                for (op_, ud) in downs[tt_ * 8:(tt_ + 1) * 8]:
                    wd = ws.get(ud)
                    for sub in range(2):
                        down_tile(wd, op_ * 2 + sub, sub, tt_, sub)
                    ws.release(ud)
                ln_cb(tt_)


def emit_ln(S, zf, c0, n, gb, ones, tmps, ps, outs, post=None, ps_off=(0, 2048)):
    nt = (n + 511) // 512
    zb = tmps["zb"]
    zs = tmps["zs"]
    for c in range(NCH):
        b0 = zb[c % 2]
        s0 = zs[c % 2]
        S.act(b0[:, 0:n], zf[:, c, c0:c0 + n], AF.Identity)
        S.act(s0[:, 0:n], zf[:, c, c0:c0 + n], AF.Square)
        for t_ in range(nt):
            w = min(512, n - t_ * 512)
            S.mm(ps[:, ps_off[0] + t_ * 512:ps_off[0] + t_ * 512 + w], ones[:, :], b0[:, t_ * 512:t_ * 512 + w],
                 start=(c == 0), stop=(c == NCH - 1))
            S.mm(ps[:, ps_off[1] + t_ * 512:ps_off[1] + t_ * 512 + w], ones[:, :], s0[:, t_ * 512:t_ * 512 + w],
                 start=(c == 0), stop=(c == NCH - 1))
    mean = tmps["mean"]
    rstd = tmps["rstd"]
    S.ts("dve", mean[:, 0:n], ps[:, ps_off[0]:ps_off[0] + n], 1.0 / D, None, ALU.mult)
    S.tt("dve", rstd[:, 0:n], mean[:, 0:n], mean[:, 0:n], ALU.mult)
    S.stt(rstd[:, 0:n], ps[:, ps_off[1]:ps_off[1] + n], 1.0 / D, rstd[:, 0:n], ALU.mult, ALU.subtract)
    S.act(rstd[:, 0:n], rstd[:, 0:n], AF.Ln, bias=tmps["eps"][:, 0:1], scale=1.0)
    S.act(rstd[:, 0:n], rstd[:, 0:n], AF.Exp, scale=-0.5)
    for c in range(NCH):
        zc = zf[:, c, c0:c0 + n]
        S.tt("dve", zc, zc, mean[:, 0:n], ALU.subtract)
        S.tt("dve", zc, zc, rstd[:, 0:n], ALU.mult)
        for i, dst in enumerate(outs):
            S.act(dst(c), zc, AF.Identity, bias=gb[:, c, 1:2], scale=gb[:, c, 0:1])
        if post is not None:
            post(c)


def build_ffn_launch():
    nc = bass.Bass("TRN2", target_bir_lowering=False)
    xT = nc.dram_tensor("xT", [D, T + 2], F32, kind="ExternalInput").ap()
    w_up = nc.dram_tensor("w_up", [D, 2 * DFF], F32, kind="ExternalInput").ap()
    w_down = nc.dram_tensor("w_down", [DFF, D], F32, kind="ExternalInput").ap()
    cwb_d = nc.dram_tensor("cwb", [128, 88 * 4], F32, kind="ExternalInput").ap()
    gb_d = nc.dram_tensor("ln2gb", [128, 32], F32, kind="ExternalInput").ap()
    yT = nc.dram_tensor("yT", [D, T], F32, kind="ExternalOutput").ap()
    S = Sched()
    with ExitStack() as es:
        xf = es.enter_context(nc.sbuf_tensor("xf", [128, NCH, T + 2], F32))
        xb = es.enter_context(nc.sbuf_tensor("xb", [128, NCH, T + 2], BF16))
        cwb = es.enter_context(nc.sbuf_tensor("cwb_s", [128, 88, 4], F32))
        gb = es.enter_context(nc.sbuf_tensor("gb_s", [128, NCH, 2], F32))
        gq = es.enter_context(nc.sbuf_tensor("gq", [128, 12, T], BF16))
        ones = es.enter_context(nc.sbuf_tensor("ones", [128, 128], BF16))
        eps = es.enter_context(nc.sbuf_tensor("eps", [128, 1], F32))
        tmps = {
            "a": [es.enter_context(nc.sbuf_tensor("ta%d" % i, [128, T], F32)) for i in range(2)],
            "v": [es.enter_context(nc.sbuf_tensor("tv%d" % i, [128, T], F32)) for i in range(2)],
            "s": [es.enter_context(nc.sbuf_tensor("tsl%d" % i, [128, T], F32)) for i in range(2)],
            "eps": eps,
        }
        tmps["zb"] = [es.enter_context(nc.sbuf_tensor("zb%d" % i, [128, T], BF16)) for i in range(2)]
        tmps["zs"] = [es.enter_context(nc.sbuf_tensor("zs%d" % i, [128, T], BF16)) for i in range(2)]
        tmps["mean"] = tmps["a"][0]
        tmps["rstd"] = tmps["a"][1]
        ps = es.enter_context(nc.psum_tensor("ps", [128, 4096], F32))
        ws = WeightStream(S, nc, es, 4, 16 * 256)
        plan = plan_ffn_weights(ws, w_up, w_down)

        ch_in = S.new_chan(total=True)
        ch_p = S.new_chan(total=True)
        ch_out = [S.new_chan() for _ in range(4)]
        S.dma("sp", cwb[:, :, :], cwb_d.rearrange("p (c j) -> p c j", j=4), ch_p)
        S.dma("sp", gb[:, :, :], gb_d.rearrange("p (c j) -> p c j", j=2), ch_p)
        S.memset("dve", ones[:, :], 1.0)
        S.memset("dve", eps[:, :], LN_EPS)
        for c in range(NCH):
            S.dma("sp", xf[:, c, :], xT[c * 128:(c + 1) * 128, :], ch_in)
        for c in range(NCH):
            S.act(xb[:, c, :], xf[:, c, :], AF.Identity)
        emit_ffn(S, ws, plan, xf, xb, cwb, gq, tmps, ps)
        emit_ln(S, xf, 2, T, gb, ones, tmps, ps, [lambda c: xf[:, c, 2:T + 2]])
        for c in range(NCH):
            S.dma("sp", yT[c * 128:(c + 1) * 128, :], xf[:, c, 2:T + 2], ch_out[c % 4])
        S.emit(nc, es, final_chans=ch_out)
    return nc, S


def _pm(v, nch):
    return np.ascontiguousarray(np.asarray(v, np.float32).reshape(nch, 128).T)


def prep_ffn_params(l, ffn_conv_w, ffn_conv_b, ln2_g, ln2_b):
    cw = np.asarray(ffn_conv_w[l], np.float32)
    cb = np.asarray(ffn_conv_b[l], np.float32)
    cwb = np.stack([_pm(cw[0], 88), _pm(cw[1], 88), _pm(cw[2], 88), _pm(cb, 88)], axis=-1)
    gb = np.stack([_pm(ln2_g[l], 16), _pm(ln2_b[l], 16)], axis=-1)
    return np.ascontiguousarray(cwb.reshape(128, 88 * 4)), np.ascontiguousarray(gb.reshape(128, 32))


def run_ffn_launch(x1, l, ffn_w_up, ffn_conv_w, ffn_conv_b, ffn_w_down, ln2_g, ln2_b):
    nc, S = build_ffn_launch()
    cwb, gb = prep_ffn_params(l, ffn_conv_w, ffn_conv_b, ln2_g, ln2_b)
    wu = np.ascontiguousarray(ffn_w_up[l], np.float32)
    wd = np.ascontiguousarray(ffn_w_down[l], np.float32)
    x1 = np.asarray(x1, np.float32)
    in_maps = []
    for c in range(NCORES):
        b, s = divmod(c, 4)
        t0 = s * T
        xt = np.zeros((D, T + 2), np.float32)
        xt[:, 2:] = x1[b, t0:t0 + T].T
        if s > 0:
            xt[:, 0:2] = x1[b, t0 - 2:t0].T
        in_maps.append({"xT": xt, "w_up": wu, "w_down": wd, "cwb": cwb, "ln2gb": gb})
    res = run_bass_kernel_spmd(nc, in_maps, core_ids=list(range(NCORES)))
    out = np.zeros((2, 4096, D), np.float32)
    for c in range(NCORES):
        b, s = divmod(c, 4)
        out[b, s * T:(s + 1) * T] = res.results[c]["yT"].T
    return out


TH = T + 128
B_WINDOWS = (2, 4, 8, 16)
GELU_C = 0.044715
GELU_S = 2.0 * 0.7978845608028654


def emit_gelu(S, dst, src_ps, t1, t2):
    S.act(t1, src_ps, AF.Square)
    S.ts("dve", t1, t1, GELU_C, 1.0, ALU.mult, ALU.add)
    S.tt("dve", t1, t1, src_ps, ALU.mult)
    S.act(t2, t1, AF.Sigmoid, scale=GELU_S)
    S.tt("dve", dst, t2, src_ps, ALU.mult)


def build_mix0_launch():
    nc = bass.Bass("TRN2", target_bir_lowering=False)
    x0T = nc.dram_tensor("x0T", [D, TH], F32, kind="ExternalInput").ap()
    w_in = nc.dram_tensor("w_in", [D, 3072], F32, kind="ExternalInput").ap()
    w_out = nc.dram_tensor("w_out", [D, D], F32, kind="ExternalInput").ap()
    wsT_d = nc.dram_tensor("wsT", [128, 8 * 128], F32, kind="ExternalInput").ap()
    mask_d = nc.dram_tensor("maskA", [128, 128], F32, kind="ExternalInput").ap()
    bsT_d = nc.dram_tensor("bsT", [128, 8 * 128], F32, kind="ExternalInput").ap()
    lnv_d = nc.dram_tensor("lnv", [128, 2 * 1024], F32, kind="ExternalInput").ap()
    wp_d = nc.dram_tensor("w_pool", [4 * 256, 256], F32, kind="ExternalInput").ap()
    psc_d = nc.dram_tensor("pscale", [128, 8], F32, kind="ExternalInput").ap()
    gb_d = nc.dram_tensor("ln1gb", [128, 32], F32, kind="ExternalInput").ap()
    rc_d = nc.dram_tensor("rcnt", [128, 64], F32, kind="ExternalInput").ap()
    flag_d = nc.dram_tensor("flag", [128, 1], F32, kind="ExternalInput").ap()
    x1T = nc.dram_tensor("x1T", [D, T + 2], F32, kind="ExternalOutput").ap()
    S = Sched()
    with ExitStack() as es:
        arena = es.enter_context(nc.sbuf_tensor("arena", [128, NCH * (T + 2)], F32))
        zf = arena[:, :].rearrange("p (c t) -> p c t", c=NCH, t=T + 2)
        x0b = arena[:, 0:NCH * TH // 2].bitcast(BF16).rearrange("p (c t) -> p c t", c=NCH, t=TH)
        u = es.enter_context(nc.sbuf_tensor("u", [128, 8, TH], BF16))
        vt = es.enter_context(nc.sbuf_tensor("vt", [128, 9, 1024], BF16))
        pp = es.enter_context(nc.sbuf_tensor("pp", [128, 8, TH], BF16))
        wsT = es.enter_context(nc.sbuf_tensor("wsT_s", [128, 8, 128], BF16))
        mask = es.enter_context(nc.sbuf_tensor("mask_s", [128, 128], F32))
        bsT = es.enter_context(nc.sbuf_tensor("bsT_s", [128, 8, 128], F32))
        lnv = es.enter_context(nc.sbuf_tensor("lnv_s", [128, 2, 1024], F32))
        wp = es.enter_context(nc.sbuf_tensor("wp_s", [128, 8, 256], BF16))
        psc = es.enter_context(nc.sbuf_tensor("psc_s", [128, 8], F32))
        gb = es.enter_context(nc.sbuf_tensor("gb_s", [128, NCH, 2], F32))
        rc = es.enter_context(nc.sbuf_tensor("rc_s", [128, 4, 16], F32))
        flag = es.enter_context(nc.sbuf_tensor("flag_s", [128, 1], F32))
        ones = es.enter_context(nc.sbuf_tensor("ones", [128, 128], BF16))
        eps = es.enter_context(nc.sbuf_tensor("eps", [128, 1], F32))
        xbf = es.enter_context(nc.sbuf_tensor("xbf", [128, 16 + TH], F32))
        g1 = [es.enter_context(nc.sbuf_tensor("g1_%d" % i, [128, TH], F32)) for i in range(2)]
        g2p = [es.enter_context(nc.sbuf_tensor("g2_%d" % i, [128, 16 + TH], F32)) for i in range(2)]
        g2 = [t[:, 16:16 + TH] for t in g2p]
        tA, tB = g2p
        wsTf = g1[1][:, 0:1024].rearrange("p (h t) -> p h t", h=8)
        st = es.enter_context(nc.sbuf_tensor("st", [128, 8], F32))
        small = es.enter_context(nc.sbuf_tensor("small", [128, 32], F32))
        tmps = {"eps": eps,
                "zb": [g2p[i][:, 16:16 + 513].bitcast(BF16) for i in range(2)],
                "zs": [xbf[:, 16:16 + 513].bitcast(BF16), xbf[:, 600:600 + 513].bitcast(BF16)],
                "mean": g1[0], "rstd": g1[1]}
        ps = es.enter_context(nc.psum_tensor("ps", [128, 4096], F32))
        ws = WeightStream(S, nc, es, 4, 16 * 256)
        wi = w_in.rearrange("(kc p) n -> p kc n", p=128)
        wo = w_out.rearrange("(kc p) n -> p kc n", p=128)
        u_xb = [ws.plan((16, 256), wi[:, :, 2048 + i * 256:2048 + (i + 1) * 256]) for i in range(4)]
        u_u = [ws.plan((16, 256), wi[:, :, i * 256:(i + 1) * 256]) for i in range(4)]
        u_v = [ws.plan((16, 256), wi[:, :, 1024 + i * 256:1024 + (i + 1) * 256]) for i in range(4)]
        u_o = [ws.plan((16, 256), wo[:, :, i * 256:(i + 1) * 256]) for i in range(8)]

        chp = S.new_chan(total=True)
        chx = S.new_chan(total=True)
        S.dma("sp", wsTf, wsT_d.rearrange("p (h t) -> p h t", h=8), chp)
        S.dma("sp", mask[:, :], mask_d, chp)
        S.dma("sp", bsT[:, :, :], bsT_d.rearrange("p (h t) -> p h t", h=8), chp)
        S.dma("sp", lnv[:, :, :], lnv_d.rearrange("p (a c) -> p a c", a=2), chp)
        S.dma("sp", psc[:, :], psc_d, chp)
        S.dma("sp", gb[:, :, :], gb_d.rearrange("p (c j) -> p c j", j=2), chp)
        S.dma("sp", rc[:, :, :], rc_d.rearrange("p (g j) -> p g j", g=4), chp)
        S.dma("sp", flag[:, :], flag_d, chp)
        S.dma("pool", wp[:, :, :], wp_d.rearrange("(a p) n -> p a n", p=128), chx)
        for c in range(NCH):
            S.dma("pool", x0b[:, c, :], x0T[c * 128:(c + 1) * 128, :], chx)
        ws.release(-1)
        S.memset("dve", ones[:, :], 1.0)
        S.memset("dve", eps[:, :], LN_EPS)
        S.memset("dve", xbf[:, 0:16], 0.0)
        S.memset("dve", tA[:, 0:16], 0.0)
        S.memset("dve", tB[:, 0:16], 0.0)
        for h in range(8):
            S.tt("dve", wsT[:, h, :], wsTf[:, h, :], mask[:, :], ALU.mult)

        GR = (0, 1536)
        TT3 = ((0, 512), (512, 512), (1024, 128))

        def proj_fm(wt, sub, g0):
            for kc in range(NCH):
                for (t0, w) in TT3:
                    S.mm(ps[:, g0 + t0:g0 + t0 + w], wt[:, kc, sub * 128:(sub + 1) * 128], x0b[:, kc, t0:t0 + w],
                         start=(kc == 0), stop=(kc == NCH - 1))

        gi = 0
        for i in range(4):
            wt = ws.get(u_xb[i])
            for sub in range(2):
                c = i * 2 + sub
                g = c // 2
                g0 = GR[gi % 2]
                gi += 1
                proj_fm(wt, sub, g0)
                S.act(xbf[:, 16:16 + TH], ps[:, g0:g0 + TH], AF.Identity)
                src = xbf
                dsts = [tA, tB]
                for k in range(g + 1):
                    sh = 1 << k
                    dst = dsts[k % 2]
                    S.tt("dve", dst[:, 16:16 + TH], src[:, 16:16 + TH], src[:, 16 - sh:16 + TH - sh], ALU.add)
                    src = dst
                win = B_WINDOWS[g]
                S.stt(pp[:, c, :], src[:, 16:16 + TH], 1.0 / win, xbf[:, 16:16 + TH], ALU.mult, ALU.subtract)
                S.tt("dve", small[:, 0:16], src[:, 16 + 128:16 + 144], rc[:, g, :], ALU.mult)
                S.tt("dve", pp[:, c, 128:144], small[:, 0:16], xbf[:, 16 + 128:16 + 144], ALU.subtract)
            ws.release(u_xb[i])
        for i in range(4):
            wt = ws.get(u_u[i])
            for sub in range(2):
                c = i * 2 + sub
                g0 = GR[gi % 2]
                proj_fm(wt, sub, g0)
                emit_gelu(S, u[:, c, :], ps[:, g0:g0 + TH], g1[gi % 2][:, :], g2[gi % 2])
                gi += 1
            ws.release(u_u[i])
        wv = [ws.get(k) for k in u_v]
        for tk in range(9):
            vr = g1[tk % 2]
            for cg in range(4):
                r0 = 3072 + ((tk * 4 + cg) % 2) * 512
                for kc in range(NCH):
                    S.mm(ps[:, r0:r0 + 256], x0b[:, kc, tk * 128:(tk + 1) * 128], wv[cg][:, kc, :],
                         start=(kc == 0), stop=(kc == NCH - 1))
                emit_gelu(S, vr[:, cg * 256:(cg + 1) * 256], ps[:, r0:r0 + 256],
                          g2[0][:, cg * 256:(cg + 1) * 256], g2[1][:, cg * 256:(cg + 1) * 256])
            sq = g2[0]
            S.add("dve", lambda e, vr=vr: e.reduce_sum(out=st[:, 0:1], in_=vr[:, 0:1024], axis=mybir.AxisListType.X),
                  reads=[vr[:, 0:1024]], writes=[st[:, 0:1]])
            S.act(sq[:, 0:1024], vr[:, 0:1024], AF.Square)
            S.add("dve", lambda e, sq=sq: e.reduce_sum(out=st[:, 1:2], in_=sq[:, 0:1024], axis=mybir.AxisListType.X),
                  reads=[sq[:, 0:1024]], writes=[st[:, 1:2]])
            S.ts("dve", st[:, 2:3], st[:, 0:1], 1.0 / 1024, None, ALU.mult)
            S.tt("dve", st[:, 3:4], st[:, 2:3], st[:, 2:3], ALU.mult)
            S.stt(st[:, 4:5], st[:, 1:2], 1.0 / 1024, st[:, 3:4], ALU.mult, ALU.subtract)
            S.act(st[:, 5:6], st[:, 4:5], AF.Sqrt, bias=eps[:, 0:1], scale=1.0)
            S.add("dve", lambda e: e.reciprocal(out=st[:, 6:7], in_=st[:, 5:6]), reads=[st[:, 5:6]],
                  writes=[st[:, 6:7]])
            S.ts("dve", vr[:, 0:1024], vr[:, 0:1024], st[:, 2:3], st[:, 6:7], ALU.subtract, ALU.mult)
            S.tt("dve", vr[:, 0:1024], vr[:, 0:1024], lnv[:, 0, :], ALU.mult)
            S.tt("dve", vt[:, tk, :], vr[:, 0:1024], lnv[:, 1, :], ALU.add)
        ws.release(u_v[3])
        for tk in range(9):
            for half in range(2):
                r0 = 2048 + half * 512
                for hh in range(4):
                    h = half * 4 + hh
                    S.mm(ps[:, r0 + hh * 128:r0 + (hh + 1) * 128], vt[:, tk, h * 128:(h + 1) * 128], wsT[:, h, :],
                         start=True, stop=True)
                tmp = g2[half][:, 0:512].rearrange("p (h t) -> p h t", h=4)
                S.tt("dve", tmp, ps[:, r0:r0 + 512].rearrange("p (h t) -> p h t", h=4),
                     bsT[:, half * 4:half * 4 + 4, :], ALU.add)
                uu = u[:, half * 4:half * 4 + 4, tk * 128:(tk + 1) * 128]
                S.tt("dve", uu, tmp, uu, ALU.mult)
        for g in range(4):
            for oc in range(2):
                g0 = GR[oc]
                for kc in range(2):
                    for (t0, w) in TT3:
                        S.mm(ps[:, g0 + t0:g0 + t0 + w], wp[:, g * 2 + kc, oc * 128:(oc + 1) * 128],
                             pp[:, g * 2 + kc, t0:t0 + w], start=(kc == 0), stop=(kc == 1))
            for oc in range(2):
                g0 = GR[oc]
                c = g * 2 + oc
                S.act(pp[:, c, :], ps[:, g0:g0 + TH], AF.Identity, scale=psc[:, c:c + 1])
        xs = [g1[0], g1[1]]
        chs = [S.new_chan(), S.new_chan()]
        for i in range(8):
            wt = ws.get(u_o[i])
            for sub in range(2):
                oc = i * 2 + sub
                g0 = GR[oc % 2]
                xst = xs[oc % 2]
                S.dma("sp", xst[:, 0:T + 2], x0T[oc * 128:(oc + 1) * 128, 126:TH], chs[oc % 2])
                for kc in range(NCH):
                    src = u[:, kc, :] if kc < 8 else pp[:, kc - 8, :]
                    lw = wt[:, kc, sub * 128:(sub + 1) * 128]
                    S.mm(ps[:, g0 + 510:g0 + 512], lw, src[:, 126:128], start=(kc == 0), stop=(kc == NCH - 1))
                    S.mm(ps[:, g0 + 512:g0 + 1024], lw, src[:, 128:640], start=(kc == 0), stop=(kc == NCH - 1))
                    S.mm(ps[:, g0 + 1024:g0 + 1536], lw, src[:, 640:1152], start=(kc == 0), stop=(kc == NCH - 1))
                S.stt(zf[:, oc, :], xst[:, 0:T + 2], ALPHA, ps[:, g0 + 510:g0 + 1536], ALU.mult, ALU.add)
            ws.release(u_o[i])
        emit_ln(S, zf, 0, T + 2, gb, ones, tmps, ps, [lambda c: zf[:, c, :]])
        S.ts("dve", zf[:, :, 0:2], zf[:, :, 0:2], flag[:, 0:1], None, ALU.mult)
        cho = [S.new_chan() for _ in range(4)]
        for c in range(NCH):
            S.dma("sp", x1T[c * 128:(c + 1) * 128, :], zf[:, c, :], cho[c % 4])
        S.emit(nc, es, final_chans=cho)
    return nc, S


def prep_mix0_inputs(x, ev_w_in, ev_ln_v_g, ev_ln_v_b, ev_w_s, ev_b_s, ev_w_pool, ev_pool_scale, ev_w_out,
                     ln1_g, ln1_b):
    x = np.asarray(x, np.float32)
    ws = np.asarray(ev_w_s[0], np.float32)
    wsT = np.ascontiguousarray(ws.transpose(2, 0, 1)).reshape(128, 8 * 128)
    tt_ = np.arange(128)
    maskA = (tt_[None, :] >= tt_[:, None]).astype(np.float32)
    bsT = np.ascontiguousarray(np.broadcast_to(np.asarray(ev_b_s[0], np.float32).reshape(1, 8 * 128), (128, 8 * 128)))
    lnv = np.ascontiguousarray(np.broadcast_to(
        np.concatenate([np.asarray(ev_ln_v_g[0], np.float32), np.asarray(ev_ln_v_b[0], np.float32)])[None, :],
        (128, 2048)))
    wp = np.ascontiguousarray(np.asarray(ev_w_pool[0], np.float32).reshape(4 * 256, 256))
    psc = _pm(ev_pool_scale[0], 8)
    gb = np.ascontiguousarray(np.stack([_pm(ln1_g[0], 16), _pm(ln1_b[0], 16)], axis=-1).reshape(128, 32))
    common = {"w_in": np.ascontiguousarray(ev_w_in[0], np.float32),
              "w_out": np.ascontiguousarray(ev_w_out[0], np.float32),
              "wsT": wsT, "maskA": maskA, "bsT": bsT, "lnv": lnv, "w_pool": wp, "pscale": psc, "ln1gb": gb}
    in_maps = []
    for c in range(NCORES):
        b, s = divmod(c, 4)
        t0 = s * T
        xt = np.zeros((D, TH), np.float32)
        xt[:, 128:] = x[b, t0:t0 + T].T
        if s > 0:
            xt[:, 0:128] = x[b, t0 - 128:t0].T
        rc = np.zeros((4, 16), np.float32)
        for g, win in enumerate(B_WINDOWS):
            pos = np.arange(t0 + 1, t0 + 17, dtype=np.float32)
            rc[g] = 1.0 / np.minimum(pos, float(win))
        rcb = np.ascontiguousarray(np.broadcast_to(rc.reshape(1, 64), (128, 64)))
        m = dict(common)
        m.update({"x0T": xt, "rcnt": rcb, "flag": np.full((128, 1), 1.0 if s > 0 else 0.0, np.float32)})
        in_maps.append(m)
    return in_maps


def run_mix0_launch(inputs):
    nc, S = build_mix0_launch()
    in_maps = prep_mix0_inputs(inputs["x"], inputs["ev_w_in"], inputs["ev_ln_v_g"], inputs["ev_ln_v_b"],
                               inputs["ev_w_s"], inputs["ev_b_s"], inputs["ev_w_pool"], inputs["ev_pool_scale"],
                               inputs["ev_w_out"], inputs["ln1_g"], inputs["ln1_b"])
    res = run_bass_kernel_spmd(nc, in_maps, core_ids=list(range(NCORES)))
    return [res.results[c]["x1T"] for c in range(NCORES)]


NH = 16
CH = 64
NCK = T // CH


def hgrn_consts(S, nc, es, lbp_d, mask2_d, ident_d, chp):
    c = {}
    lbp = es.enter_context(nc.sbuf_tensor("lbp_s", [128, 2, NH], F32))
    c["lb"] = es.enter_context(nc.sbuf_tensor("lb", [128, NH], F32))
    c["oml"] = es.enter_context(nc.sbuf_tensor("oml", [128, NH], F32))
    c["rm"] = es.enter_context(nc.sbuf_tensor("rm", [128, T], F32))
    c["mask2"] = es.enter_context(nc.sbuf_tensor("mask2_s", [128, 128], F32))
    identf = es.enter_context(nc.sbuf_tensor("identf", [128, 128], F32))
    c["ident"] = es.enter_context(nc.sbuf_tensor("ident_s", [128, 128], BF16))
    S.dma("sp", lbp[:, :, :], lbp_d.rearrange("p (l h) -> p l h", l=2), chp)
    S.dma("sp", c["mask2"][:, :], mask2_d, chp)
    S.dma("sp", identf[:, :], ident_d, chp)
    S.copy("dve", c["ident"][:, :], identf[:, :])
    S.tt("dve", c["lb"][:, :], lbp[:, 1, :], lbp[:, 0, :], ALU.subtract)
    S.act(c["lb"][:, :], c["lb"][:, :], AF.Sigmoid)
    S.ts("dve", c["oml"][:, :], c["lb"][:, :], -1.0, 1.0, ALU.mult, ALU.add)
    S.memset("dve", c["rm"][:, :], 1.0)
    S.memset("dve", c["rm"][:, :].rearrange("p (c t) -> p c t", t=CH)[:, :, 0:1], 0.0)
    c["pm"] = es.enter_context(nc.sbuf_tensor("pm", [128, 2], F32))
    c["one"] = es.enter_context(nc.sbuf_tensor("one_c", [128, 1], F32))
    S.memset("dve", c["one"][:, :], 1.0)
    S.memset("dve", c["pm"][:, :], 0.0)
    S.memset("dve", c["pm"][0:64, 0:1], 1.0)
    S.memset("dve", c["pm"][64:128, 1:2], 1.0)
    return c


def hgrn_gates(S, h, cst, f_ps, A, B, C, kend_bf, dec, kdec_bf=None, eC=False):
    oml = cst["oml"][:, h:h + 1]
    lb = cst["lb"][:, h:h + 1]
    S.act(A, f_ps, AF.Sigmoid)
    S.act(A, A, AF.Identity, bias=lb, scale=oml)
    S.act(B, A, AF.Ln)
    rm_ap = cst["rm"][:, :]
    S.add("dve", lambda e: e.tensor_tensor_scan(out=C, data0=rm_ap, data1=B, initial=0.0, op0=ALU.mult, op1=ALU.add),
          reads=[rm_ap, B], writes=[C])
    S.act(A, A, AF.Identity, bias=cst["one"][:, 0:1], scale=-1.0)
    S.act(B, C, AF.Exp, scale=-1.0)
    C3 = C.rearrange("p (c t) -> p c t", t=CH)
    S.act(dec.rearrange("p (c o) -> p c o", o=1), C3[:, :, CH - 1:CH], AF.Exp)
    if eC:
        S.act(C, C, AF.Exp)
    S.tt("dve", B, A, B, ALU.mult)
    S.tt("dve", kend_bf.rearrange("p (c t) -> p c t", t=CH), B.rearrange("p (c t) -> p c t", t=CH),
         dec.rearrange("p (c o) -> p c o", o=1).to_broadcast([128, NCK, CH]), ALU.mult)
    if kdec_bf is not None:
        S.copy("dve", kdec_bf, B)


def hgrn_transposes(S, cst, psb, src_bf, dst_tok, dst_tok1=None):
    for j in range(8):
        S.transpose(psb[:, 4096 + j * 128:4096 + (j + 1) * 128], src_bf[:, j * 128:(j + 1) * 128], cst["ident"][:, :])
    if dst_tok1 is None:
        S.act(dst_tok.rearrange("p j d -> p (j d)"), psb[:, 4096:5120], AF.Identity)
    else:
        S.act(dst_tok.rearrange("p j d -> p (j d)"), psb[:, 4096:5120], AF.Identity, scale=cst["pm"][:, 0:1])
        S.act(dst_tok1.rearrange("p j d -> p (j d)"), psb[:, 4096:5120], AF.Identity, scale=cst["pm"][:, 1:2])


def hgrn_state_scan(S, ps, kt, vtk, dec, Sf, Sb=None):
    cur = 0
    if Sb is not None:
        S.act(Sb[:, 0, :], Sf[0][:, :], AF.Identity)
    for g4 in range(4):
        for cc in range(4):
            c = g4 * 4 + cc
            j, par = divmod(c, 2)
            S.mm(ps[:, 2560 + cc * 128:2560 + (cc + 1) * 128], kt[par][:, j, :], vtk[:, j, :], start=True, stop=True)
        for cc in range(4):
            c = g4 * 4 + cc
            nxt = 1 - cur
            S.stt(Sf[nxt][:, :], Sf[cur][:, :], dec[:, c:c + 1], ps[:, 2560 + cc * 128:2560 + (cc + 1) * 128],
                  ALU.mult, ALU.add)
            cur = nxt
            if Sb is not None and c + 1 < NCK:
                S.act(Sb[:, c + 1, :], Sf[cur][:, :], AF.Identity)
    return cur


def build_hgrn_pre_launch(stage=9, nheads=NH):
    nc = bass.Bass("TRN2", target_bir_lowering=False)
    xT = nc.dram_tensor("xT", [D, T], F32, kind="ExternalInput").ap()
    w_in = nc.dram_tensor("w_in", [D, 4 * D], F32, kind="ExternalInput").ap()
    lbp_d = nc.dram_tensor("lbp", [128, 2 * NH], F32, kind="ExternalInput").ap()
    mask2_d = nc.dram_tensor("mask2", [128, 128], F32, kind="ExternalInput").ap()
    ident_d = nc.dram_tensor("ident", [128, 128], F32, kind="ExternalInput").ap()
    U_d = nc.dram_tensor("U", [NH, 128, 128], F32, kind="ExternalOutput").ap()
    D_d = nc.dram_tensor("Dd", [128, NH], F32, kind="ExternalOutput").ap()
    S = Sched()
    with ExitStack() as es:
        xb = es.enter_context(nc.sbuf_tensor("xb", [128, NCH, T], BF16))
        A = [es.enter_context(nc.sbuf_tensor("A%d" % i, [128, T], F32)) for i in range(2)]
        B = [es.enter_context(nc.sbuf_tensor("B%d" % i, [128, T], F32)) for i in range(2)]
        C = [es.enter_context(nc.sbuf_tensor("C%d" % i, [128, T], F32)) for i in range(2)]
        kend = [es.enter_context(nc.sbuf_tensor("kend%d" % i, [128, T], BF16)) for i in range(2)]
        ibf = [es.enter_context(nc.sbuf_tensor("ibf%d" % i, [128, T], BF16)) for i in range(2)]
        kt = [[es.enter_context(nc.sbuf_tensor("kt%d_%d" % (i, k), [128, 8, 128], BF16)) for k in range(2)]
              for i in range(2)]
        vtk = [es.enter_context(nc.sbuf_tensor("vtk%d" % i, [128, 8, 128], BF16)) for i in range(2)]
        dec = [es.enter_context(nc.sbuf_tensor("dec%d" % i, [128, NCK], F32)) for i in range(2)]
        Sf = [[es.enter_context(nc.sbuf_tensor("Sf%d_%d" % (i, k), [128, 128], F32)) for k in range(2)]
              for i in range(2)]
        Dd = es.enter_context(nc.sbuf_tensor("Dd_s", [128, NH], F32))
        ps = es.enter_context(nc.psum_tensor("ps", [128, 4096], F32))
        psb = ps[:, :].bitcast(BF16)
        chp = S.new_chan(total=True)
        chx = S.new_chan(total=True)
        cst = hgrn_consts(S, nc, es, lbp_d, mask2_d, ident_d, chp)
        ws = WeightStream(S, nc, es, 4, 16 * 256)
        wi = w_in.rearrange("(kc p) n -> p kc n", p=128)
        uses = [ws.plan((16, 256), wi[:, :, h * 512 + 256:h * 512 + 512]) for h in range(NH)]
        for c in range(NCH):
            S.dma("pool", xb[:, c, :], xT[c * 128:(c + 1) * 128, :], chx)
        ws.release(-1)
        cho = [S.new_chan() for _ in range(2)]
        for h in range(nheads):
            b = h % 2
            wt = ws.get(uses[h])
            for which in range(2):
                g0 = which * 1024
                for kc in range(NCH):
                    for t_ in range(2):
                        S.mm(ps[:, g0 + t_ * 512:g0 + (t_ + 1) * 512], wt[:, kc, which * 128:(which + 1) * 128],
                             xb[:, kc, t_ * 512:(t_ + 1) * 512], start=(kc == 0), stop=(kc == NCH - 1))
            ws.release(uses[h])
            hgrn_gates(S, h, cst, ps[:, 0:1024], A[b][:, :], B[b][:, :], C[b][:, :], kend[b][:, :], dec[b][:, :])
            S.act(ibf[b][:, :], ps[:, 1024:2048], AF.Identity)
            C3 = C[b][:, :].rearrange("p (c t) -> p c t", t=CH)
            S.add("dve", lambda e, C3=C3, h=h: e.reduce_sum(out=Dd[:, h:h + 1], in_=C3[:, :, CH - 1:CH],
                                                           axis=mybir.AxisListType.XY),
                  reads=[C[b][:, :]], writes=[Dd[:, h:h + 1]])
            S.act(Dd[:, h:h + 1], Dd[:, h:h + 1], AF.Exp)
            if stage >= 2:
                hgrn_transposes(S, cst, psb, kend[b][:, :], kt[b][0][:, :, :], kt[b][1][:, :, :])
                hgrn_transposes(S, cst, psb, ibf[b][:, :], vtk[b][:, :, :])
            S.memset("dve", Sf[b][0][:, :], 0.0)
            fin = 0
            if stage >= 3:
                fin = hgrn_state_scan(S, ps, kt[b], vtk[b], dec[b], Sf[b])
            S.dma("sp", U_d[h, :, :], Sf[b][fin][:, :], cho[b])
        chd = S.new_chan()
        S.dma("sp", D_d, Dd[:, :], chd)
        S.emit(nc, es, final_chans=cho + [chd])
    return nc, S


def regroup_w_in(od_w_in):
    w = np.asarray(od_w_in, np.float32).reshape(D, 4, NH, 128)
    w = w[:, [0, 3, 1, 2]]
    return np.ascontiguousarray(w.transpose(0, 2, 1, 3).reshape(D, 4 * D))


def hgrn_const_inputs(lb_param):
    lbp = np.ascontiguousarray(np.stack([_pm(lb_param[0], NH), _pm(lb_param[1], NH)], axis=1).reshape(128, 2 * NH))
    i = np.arange(128)
    mask2 = ((i[None, :] >= i[:, None]) & ((i[None, :] // CH) == (i[:, None] // CH))).astype(np.float32)
    return {"lbp": lbp, "mask2": mask2, "ident": np.eye(128, dtype=np.float32)}


def _carve(arena, off_bytes, n_elems, dt):
    assert off_bytes % 4 == 0
    nb = n_elems * _dsize(dt)
    assert nb % 4 == 0
    v = arena[:, off_bytes // 4:(off_bytes + nb) // 4]
    return v if dt == F32 else v.bitcast(dt)


def build_hgrn_main_launch():
    nc = bass.Bass("TRN2", target_bir_lowering=False)
    xT = nc.dram_tensor("xT", [D, T], F32, kind="ExternalInput").ap()
    w_in = nc.dram_tensor("w_in", [D, 4 * D], F32, kind="ExternalInput").ap()
    w_out = nc.dram_tensor("w_out", [D, D], F32, kind="ExternalInput").ap()
    lbp_d = nc.dram_tensor("lbp", [128, 2 * NH], F32, kind="ExternalInput").ap()
    mask2_d = nc.dram_tensor("mask2", [128, 128], F32, kind="ExternalInput").ap()
    ident_d = nc.dram_tensor("ident", [128, 128], F32, kind="ExternalInput").ap()
    up_d = nc.dram_tensor("Uprev", [3, NH, 128, 128], F32, kind="ExternalInput").ap()
    dp_d = nc.dram_tensor("Dprev", [128, 3 * NH], F32, kind="ExternalInput").ap()
    gn_d = nc.dram_tensor("gn", [128, NH], F32, kind="ExternalInput").ap()
    gb_d = nc.dram_tensor("ln1gb", [128, 32], F32, kind="ExternalInput").ap()
    x1T = nc.dram_tensor("x1T", [D, T], F32, kind="ExternalOutput").ap()
    S = Sched()
    with ExitStack() as es:
        arena = es.enter_context(nc.sbuf_tensor("arena", [128, NCH * T], F32))
        zf = arena[:, :].rearrange("p (c t) -> p c t", c=NCH, t=T)
        xb = _carve(arena, 0, NCH * T, BF16).rearrange("p (c t) -> p c t", c=NCH, t=T)
        off = NCH * T * 2
        A = _carve(arena, off, T, F32); off += 4 * T
        B = _carve(arena, off, T, F32); off += 4 * T
        C = _carve(arena, off, T, F32); off += 4 * T
        kend = _carve(arena, off, T, BF16); off += 2 * T
        kdec = _carve(arena, off, T, BF16); off += 2 * T
        qdec = _carve(arena, off, T, BF16); off += 2 * T
        ibf = _carve(arena, off, T, BF16); off += 2 * T
        sg = _carve(arena, off, T, BF16); off += 2 * T
        osq = _carve(arena, off, T, BF16); off += 2 * T
        attm = _carve(arena, off, T, BF16).rearrange("p (j t) -> p j t", j=8); off += 2 * T
        kt0 = _carve(arena, off, T, BF16).rearrange("p (j t) -> p j t", j=8); off += 2 * T
        kt1 = _carve(arena, off, T, BF16).rearrange("p (j t) -> p j t", j=8); off += 2 * T
        kt = (kt0, kt1)
        vtk = _carve(arena, off, T, BF16).rearrange("p (j t) -> p j t", j=8); off += 2 * T
        assert off <= NCH * T * 4
        y = es.enter_context(nc.sbuf_tensor("y", [128, NH, T], BF16))
        Sb = es.enter_context(nc.sbuf_tensor("Sb", [128, NCK, 128], BF16))
        Sf = [es.enter_context(nc.sbuf_tensor("Sf%d" % k, [128, 128], F32)) for k in range(2)]
        upst = es.enter_context(nc.sbuf_tensor("upst", [128, 3, 128], F32))
        dec = es.enter_context(nc.sbuf_tensor("dec", [128, NCK], F32))
        dp = es.enter_context(nc.sbuf_tensor("dp", [128, 3, NH], F32))
        gn = es.enter_context(nc.sbuf_tensor("gn_s", [128, NH], F32))
        gb = es.enter_context(nc.sbuf_tensor("gb_s", [128, NCH, 2], F32))
        ones = es.enter_context(nc.sbuf_tensor("ones", [128, 128], BF16))
        eps = es.enter_context(nc.sbuf_tensor("eps", [128, 1], F32))
        xs = [es.enter_context(nc.sbuf_tensor("xs%d" % i, [128, T], F32)) for i in range(2)]
        tmps = {"eps": eps,
                "zb": [es.enter_context(nc.sbuf_tensor("zb%d" % i, [128, T], BF16)) for i in range(2)],
                "zs": [es.enter_context(nc.sbuf_tensor("zs%d" % i, [128, T], BF16)) for i in range(2)],
                "mean": xs[0], "rstd": xs[1]}
        ps = es.enter_context(nc.psum_tensor("ps", [128, 4096], F32))
        psb = ps[:, :].bitcast(BF16)
        chp = S.new_chan(total=True)
        chx = S.new_chan(total=True)
        cst = hgrn_consts(S, nc, es, lbp_d, mask2_d, ident_d, chp)
        S.dma("sp", dp[:, :, :], dp_d.rearrange("p (j h) -> p j h", j=3), chp)
        S.dma("sp", gn[:, :], gn_d, chp)
        S.dma("sp", gb[:, :, :], gb_d.rearrange("p (c j) -> p c j", j=2), chp)
        S.memset("dve", ones[:, :], 1.0)
        S.memset("dve", eps[:, :], LN_EPS)
        ws = WeightStream(S, nc, es, 3, 16 * 512)
        wi = w_in.rearrange("(kc p) n -> p kc n", p=128)
        wo = w_out.rearrange("(kc p) n -> p kc n", p=128)
        uses = [ws.plan((16, 512), wi[:, :, h * 512:(h + 1) * 512]) for h in range(NH)]
        u_o = [ws.plan((16, 512), wo[:, :, i * 512:(i + 1) * 512]) for i in range(4)]
        for c in range(NCH):
            S.dma("pool", xb[:, c, :], xT[c * 128:(c + 1) * 128, :], chx)
        ws.release(-1)
        chu = S.new_chan()
        G0, G1 = 0, 1024
        for h in range(NH):
            wt = ws.get(uses[h])

            def proj(blk, g0):
                for kc in range(NCH):
                    for t_ in range(2):
                        S.mm(ps[:, g0 + t_ * 512:g0 + (t_ + 1) * 512], wt[:, kc, blk * 128:(blk + 1) * 128],
                             xb[:, kc, t_ * 512:(t_ + 1) * 512], start=(kc == 0), stop=(kc == NCH - 1))
            S.dma("sp", upst[:, :, :], up_d[:, h, :, :].rearrange("j d e -> d j e"), chu)
            S.memset("dve", Sf[0][:, :], 0.0)
            cur = 0
            for j in range(3):
                S.stt(Sf[1 - cur][:, :], Sf[cur][:, :], dp[:, j, h:h + 1], upst[:, j, :], ALU.mult, ALU.add)
                cur = 1 - cur
            Sfl = [Sf[cur], Sf[1 - cur]]
            proj(2, G0)
            proj(3, G1)
            hgrn_gates(S, h, cst, ps[:, G0:G0 + T], A, B, C, kend, dec[:, :], kdec_bf=kdec, eC=True)
            S.act(ibf, ps[:, G1:G1 + T], AF.Identity)
            proj(0, G0)
            proj(1, G1)
            ws.release(uses[h])
            S.act(B, ps[:, G0:G0 + T], AF.Silu)
            S.tt("dve", qdec, B, C, ALU.mult)
            S.act(sg, ps[:, G1:G1 + T], AF.Sigmoid)
            hgrn_transposes(S, cst, psb, kend, kt0, kt1)
            hgrn_transposes(S, cst, psb, ibf, vtk)
            hgrn_state_scan(S, ps, kt, vtk, dec, Sfl, Sb=Sb)
            for half in range(2):
                for jj in range(4):
                    j = half * 4 + jj
                    S.mm(ps[:, 2560 + jj * 128:2560 + (jj + 1) * 128], kdec[:, j * 128:(j + 1) * 128],
                         qdec[:, j * 128:(j + 1) * 128], start=True, stop=True)
                for jj in range(4):
                    j = half * 4 + jj
                    S.tt("dve", attm[:, j, :], ps[:, 2560 + jj * 128:2560 + (jj + 1) * 128], cst["mask2"][:, :],
                         ALU.mult)
            O0 = 3072
            for j in range(8):
                S.mm(ps[:, O0 + j * 128:O0 + (j + 1) * 128], vtk[:, j, :], attm[:, j, :], start=True, stop=False)
                S.mm(ps[:, O0 + j * 128:O0 + j * 128 + 64], Sb[:, 2 * j, :], qdec[:, j * 128:j * 128 + 64],
                     start=False, stop=False)
                S.mm(ps[:, O0 + j * 128 + 64:O0 + (j + 1) * 128], Sb[:, 2 * j + 1, :],
                     qdec[:, j * 128 + 64:(j + 1) * 128], start=False, stop=True)
            S.act(osq, ps[:, O0:O0 + T], AF.Square)
            for t_ in range(2):
                S.mm(ps[:, G0 + t_ * 512:G0 + (t_ + 1) * 512], ones[:, :], osq[:, t_ * 512:(t_ + 1) * 512],
                     start=True, stop=True)
            S.act(A, ps[:, G0:G0 + T], AF.Ln, bias=eps[:, 0:1], scale=1.0 / 128)
            S.act(A, A, AF.Exp, scale=-0.5)
            S.stt(C, ps[:, O0:O0 + T], gn[:, h:h + 1], A, ALU.mult, ALU.mult)
            S.tt("dve", y[:, h, :], C, sg, ALU.mult)
        chs = [S.new_chan(), S.new_chan()]
        for i in range(4):
            wt = ws.get(u_o[i])
            for sub in range(4):
                oc = i * 4 + sub
                g0 = (oc % 2) * 1024
                xst = xs[oc % 2]
                S.dma("sp", xst[:, :], xT[oc * 128:(oc + 1) * 128, :], chs[oc % 2])
                for kc in range(NCH):
                    for t_ in range(2):
                        S.mm(ps[:, g0 + t_ * 512:g0 + (t_ + 1) * 512], wt[:, kc, sub * 128:(sub + 1) * 128],
                             y[:, kc, t_ * 512:(t_ + 1) * 512], start=(kc == 0), stop=(kc == NCH - 1))
                S.stt(zf[:, oc, :], xst[:, :], ALPHA, ps[:, g0:g0 + T], ALU.mult, ALU.add)
            ws.release(u_o[i])
        emit_ln(S, zf, 0, T, gb, ones, tmps, ps, [lambda c: zf[:, c, :]])
        cho = [S.new_chan() for _ in range(4)]
        for c in range(NCH):
            S.dma("sp", x1T[c * 128:(c + 1) * 128, :], zf[:, c, :], cho[c % 4])
        S.emit(nc, es, final_chans=cho)
    return nc, S


def _run(nc, in_maps):
    return run_bass_kernel_spmd(nc, in_maps, core_ids=list(range(NCORES))).results


def kernel_unfused(x, ev_w_in, ev_ln_v_g, ev_ln_v_b, ev_w_s, ev_b_s, ev_w_pool, ev_pool_scale,
           ev_w_out, od_w_in, od_norm_g, od_w_out, lb_param, ffn_w_up, ffn_conv_w,
           ffn_conv_b, ffn_w_down, ln1_g, ln1_b, ln2_g, ln2_b):
    f32 = np.float32
    nc0, _ = build_mix0_launch()
    maps0 = prep_mix0_inputs(x, ev_w_in, ev_ln_v_g, ev_ln_v_b, ev_w_s, ev_b_s, ev_w_pool, ev_pool_scale,
                             ev_w_out, ln1_g, ln1_b)
    r0 = _run(nc0, maps0)
    x1T = [r0[c]["x1T"] for c in range(NCORES)]
    ncf, _ = build_ffn_launch()

    def ffn(l, xTs):
        cwb, gb = prep_ffn_params(l, ffn_conv_w, ffn_conv_b, ln2_g, ln2_b)
        wu = np.ascontiguousarray(ffn_w_up[l], f32)
        wd = np.ascontiguousarray(ffn_w_down[l], f32)
        maps = [{"xT": np.ascontiguousarray(xTs[c], f32), "w_up": wu, "w_down": wd, "cwb": cwb, "ln2gb": gb}
                for c in range(NCORES)]
        r = _run(ncf, maps)
        return [r[c]["yT"] for c in range(NCORES)]

    x2T = ffn(0, x1T)
    ncp, _ = build_hgrn_pre_launch()
    w_in_r = regroup_w_in(od_w_in[0])
    hc = hgrn_const_inputs(lb_param)
    mapsp = []
    for c in range(NCORES):
        m = {"xT": np.ascontiguousarray(x2T[c], f32), "w_in": w_in_r}
        m.update(hc)
        mapsp.append(m)
    rp = _run(ncp, mapsp)
    ncm, _ = build_hgrn_main_launch()
    gn = _pm(od_norm_g[0], NH)
    gb1 = np.ascontiguousarray(np.stack([_pm(ln1_g[1], 16), _pm(ln1_b[1], 16)], axis=-1).reshape(128, 32))
    w_out1 = np.ascontiguousarray(od_w_out[0], f32)
    mapsm = []
    for c in range(NCORES):
        b, s = divmod(c, 4)
        up = np.zeros((3, NH, 128, 128), f32)
        dp = np.zeros((128, 3, NH), f32)
        for j in range(s):
            pos = 3 - s + j
            up[pos] = rp[b * 4 + j]["U"]
            dp[:, pos, :] = rp[b * 4 + j]["Dd"]
        m = {"xT": np.ascontiguousarray(x2T[c], f32), "w_in": w_in_r, "w_out": w_out1, "Uprev": up,
             "Dprev": np.ascontiguousarray(dp.reshape(128, 3 * NH)), "gn": gn, "ln1gb": gb1}
        m.update(hc)
        mapsm.append(m)
    rm = _run(ncm, mapsm)
    x1bT = []
    for c in range(NCORES):
        b, s = divmod(c, 4)
        xt = np.zeros((D, T + 2), f32)
        xt[:, 2:] = rm[c]["x1T"]
        if s > 0:
            xt[:, 0:2] = rm[c - 1]["x1T"][:, T - 2:T]
        x1bT.append(xt)
    outT = ffn(1, x1bT)
    out = np.zeros((2, 4 * T, D), f32)
    for c in range(NCORES):
        b, s = divmod(c, 4)
        out[b, s * T:(s + 1) * T] = outT[c].T
    return out


R0 = 0
R0_SZ = NCH * (T + 2) * 4
R1 = R0 + R0_SZ
R1_SZ = NCH * (T + 2) * 2
R2 = R1 + R1_SZ
R2_SZ = 66560
AR_BYTES = R2 + R2_SZ
SEQ_GROUPS = [[0, 1, 2, 3], [4, 5, 6, 7]]


def build_fused(use_cc=True):
    nc = bass.Bass("TRN2", target_bir_lowering=False)

    def din(name, shape):
        return nc.dram_tensor(name, shape, F32, kind="ExternalInput").ap()
    x0T = din("x0T", [D, TH])
    ev_w_in = din("ev_w_in", [D, 3072])
    ev_w_out = din("ev_w_out", [D, D])
    wsT_d = din("wsT", [128, 8 * 128])
    mask_d = din("maskA", [128, 128])
    bsT_d = din("bsT", [128, 8 * 128])
    lnv_d = din("lnv", [128, 2 * 1024])
    wp_d = din("w_pool", [4 * 256, 256])
    psc_d = din("pscale", [128, 8])
    rc_d = din("rcnt", [128, 64])
    flag_d = din("flag", [128, 1])
    oh_d = din("oh", [128, 8])
    ln1gb_d = din("ln1gb", [128, 64])
    ln2gb_d = din("ln2gb", [128, 64])
    cwb_d = din("cwb", [128, 2 * 88 * 4])
    w_up = [din("w_up%d" % l, [D, 2 * DFF]) for l in range(2)]
    w_down = [din("w_down%d" % l, [DFF, D]) for l in range(2)]
    od_w_in = din("od_w_in", [D, 4 * D])
    od_w_out = din("od_w_out", [D, D])
    lbp_d = din("lbp", [128, 2 * NH])
    mask2_d = din("mask2", [128, 128])
    ident_d = din("ident", [128, 128])
    gn_d = din("gn", [128, NH])
    outT = nc.dram_tensor("outT", [D, T], F32, kind="ExternalOutput").ap()
    xsp = nc.dram_tensor("xsp", [D, T], F32).ap()
    ccin = [nc.dram_tensor("ccin%d" % g, [4 * 4 * 128, 129], F32) for g in range(4)]
    ccout = [nc.dram_tensor("ccout%d" % g, [4 * 4 * 128, 129], F32) for g in range(4)]
    cch_in = nc.dram_tensor("cch_in", [4 * 128, 32], F32)
    cch_out = nc.dram_tensor("cch_out", [4 * 128, 32], F32)

    S = Sched()
    with ExitStack() as es:
        AR = es.enter_context(nc.sbuf_tensor("AR", [128, AR_BYTES // 4], F32))

        def cv(off, shape, dt):
            n = 1
            for k in shape:
                n *= k
            v = _carve(AR, off, n, dt)
            if len(shape) == 2:
                v = v.rearrange("p (a b) -> p a b", a=shape[0], b=shape[1])
            return v

        def sb(name, shape, dt=F32):
            return es.enter_context(nc.sbuf_tensor(name, shape, dt))
        psc = sb("psc_s", [128, 8]); rc = sb("rc_s", [128, 4, 16]); flag = sb("flag_s", [128, 1])
        oh = sb("oh_s", [128, 8]); ln1gb = sb("ln1gb_s", [128, 2, NCH, 2]); ln2gb = sb("ln2gb_s", [128, 2, NCH, 2])
        cwb = sb("cwb_s", [128, 2, 88, 4]); ones = sb("ones", [128, 128], BF16); eps = sb("eps", [128, 1])
        st = sb("st", [128, 8]); small = sb("small", [128, 32]); gn = sb("gn_s", [128, NH])
        Dd = sb("Dd_s", [128, NH]); tiny = sb("tiny", [128, 8])
        hstg = sb("hstg", [128, 4, 32]); hld = sb("hld", [128, 4, 32]); hsum = sb("hsum", [128, 32])
        ps = es.enter_context(nc.psum_tensor("ps", [128, 4096], F32))
        psb = ps[:, :].bitcast(BF16)
        ccsem = [es.enter_context(nc.semaphore("ccs%d" % i)) for i in range(5)]
        ws = WeightStream(S, nc, es, 4, 16 * 256)

        wi0 = ev_w_in.rearrange("(kc p) n -> p kc n", p=128)
        wo0 = ev_w_out.rearrange("(kc p) n -> p kc n", p=128)
        u_xb = [ws.plan((16, 256), wi0[:, :, 2048 + i * 256:2048 + (i + 1) * 256]) for i in range(4)]
        u_u = [ws.plan((16, 256), wi0[:, :, i * 256:(i + 1) * 256]) for i in range(4)]
        u_v = [ws.plan((16, 256), wi0[:, :, 1024 + i * 256:1024 + (i + 1) * 256]) for i in range(4)]
        u_o = [ws.plan((16, 256), wo0[:, :, i * 256:(i + 1) * 256]) for i in range(8)]
        plan0 = plan_ffn_weights(ws, w_up[0], w_down[0], split_last=True)
        wi1 = od_w_in.rearrange("(kc p) n -> p kc n", p=128)
        wo1 = od_w_out.rearrange("(kc p) n -> p kc n", p=128)
        u_pre = [ws.plan((16, 256), wi1[:, :, h * 512 + 256:h * 512 + 512]) for h in range(NH)]
        u_main = []
        for h in range(NH):
            fi = ws.plan((16, 256), wi1[:, :, h * 512 + 256:h * 512 + 512])
            qg = ws.plan((16, 256), wi1[:, :, h * 512:h * 512 + 256])
            u_main.append((fi, qg))
        u_o1 = [ws.plan((16, 256), wo1[:, :, (i % 8) * 256:((i % 8) + 1) * 256]) for i in range(16)]
        plan1 = plan_ffn_weights(ws, w_up[1], w_down[1], split_last=True)

        x0b = cv(R0, (NCH, TH), BF16)
        zf = cv(R0, (NCH, T + 2), F32)
        xb = cv(R1, (NCH, T + 2), BF16)
        pp = cv(R1, (8, TH), BF16)
        g1 = [cv(R1 + 18432, (TH,), F32), cv(R1 + 23040, (TH,), F32)]
        xbf = cv(R1 + 27648, (16 + TH,), F32)
        u = cv(R2, (8, TH), BF16)
        vt = cv(R2 + 18432, (9, 1024), BF16)
        g2p = [cv(R2 + 36864, (16 + TH,), F32), cv(R2 + 41536, (16 + TH,), F32)]
        g2 = [t[:, 16:16 + TH] for t in g2p]
        tA, tB = g2p
        wsT = cv(R2 + 46208, (8, 128), BF16)
        mask = cv(R2 + 48256, (128,), F32)
        bsT = cv(R2 + 48768, (8, 128), F32)
        lnv = cv(R2 + 52864, (2, 1024), F32)
        wp = cv(R2 + 61056, (8, 256), BF16)
        wsTf = g1[1][:, 0:1024].rearrange("p (h t) -> p h t", h=8)
        hb = [cv(R0 + 36864 + k * 4608, (TH,), F32) for k in range(2)]

        chp = S.new_chan(total=True)
        chx = S.new_chan(total=True)
        S.dma("sp", wsTf, wsT_d.rearrange("p (h t) -> p h t", h=8), chp)
        S.dma("sp", mask, mask_d, chp)
        S.dma("sp", bsT, bsT_d.rearrange("p (h t) -> p h t", h=8), chp)
        S.dma("sp", lnv, lnv_d.rearrange("p (a c) -> p a c", a=2), chp)
        S.dma("sp", psc[:, :], psc_d, chp)
        S.dma("sp", rc[:, :, :], rc_d.rearrange("p (g j) -> p g j", g=4), chp)
        S.dma("sp", flag[:, :], flag_d, chp)
        S.dma("sp", oh[:, :], oh_d, chp)
        S.dma("sp", ln1gb[:, :, :, :], ln1gb_d.rearrange("p (l c j) -> p l c j", l=2, j=2), chp)
        S.dma("sp", ln2gb[:, :, :, :], ln2gb_d.rearrange("p (l c j) -> p l c j", l=2, j=2), chp)
        S.dma("sp", cwb[:, :, :, :], cwb_d.rearrange("p (l c j) -> p l c j", l=2, j=4), chp)
        S.dma("sp", gn[:, :], gn_d, chp)
        S.dma("pool", wp, wp_d.rearrange("(a p) n -> p a n", p=128), chx)
        for c in range(NCH):
            S.dma("pool", x0b[:, c, :], x0T[c * 128:(c + 1) * 128, :], chx)
        ws.release(-1)
        S.memset("dve", ones[:, :], 1.0)
        S.memset("dve", eps[:, :], LN_EPS)
        S.memset("dve", xbf[:, 0:16], 0.0)
        S.memset("dve", tA[:, 0:16], 0.0)
        S.memset("dve", tB[:, 0:16], 0.0)
        for h in range(8):
            S.tt("dve", wsT[:, h, :], wsTf[:, h, :], mask, ALU.mult)

        GR = (0, 1536)
        TT3 = ((0, 512), (512, 512), (1024, 128))

        def proj_fm(wt, sub, g0):
            for kc in range(NCH):
                for (t0, w) in TT3:
                    S.mm(ps[:, g0 + t0:g0 + t0 + w], wt[:, kc, sub * 128:(sub + 1) * 128], x0b[:, kc, t0:t0 + w],
                         start=(kc == 0), stop=(kc == NCH - 1))
        gi = 0
        for i in range(4):
            wt = ws.get(u_xb[i])
            for sub in range(2):
                c = i * 2 + sub
                g = c // 2
                g0 = GR[gi % 2]
                gi += 1
                proj_fm(wt, sub, g0)
                S.act(xbf[:, 16:16 + TH], ps[:, g0:g0 + TH], AF.Identity)
                src = xbf
                dsts = [tA, tB]
                for k in range(g + 1):
                    sh = 1 << k
                    dst = dsts[k % 2]
                    S.tt("dve", dst[:, 16:16 + TH], src[:, 16:16 + TH], src[:, 16 - sh:16 + TH - sh], ALU.add)
                    src = dst
                win = B_WINDOWS[g]
                S.stt(pp[:, c, :], src[:, 16:16 + TH], 1.0 / win, xbf[:, 16:16 + TH], ALU.mult, ALU.subtract)
                S.tt("dve", small[:, 0:16], src[:, 16 + 128:16 + 144], rc[:, g, :], ALU.mult)
                S.tt("dve", pp[:, c, 128:144], small[:, 0:16], xbf[:, 16 + 128:16 + 144], ALU.subtract)
            ws.release(u_xb[i])
        for i in range(4):
            wt = ws.get(u_u[i])
            for sub in range(2):
                c = i * 2 + sub
                g0 = GR[gi % 2]
                proj_fm(wt, sub, g0)
                S.act(hb[gi % 2], ps[:, g0:g0 + TH], AF.Identity)
                emit_gelu(S, u[:, c, :], hb[gi % 2], g1[gi % 2], g2[gi % 2])
                gi += 1
            ws.release(u_u[i])
        def gating(tk):
            for half in range(2):
                r0 = 2048 + half * 512
                for hh in range(4):
                    h = half * 4 + hh
                    S.mm(ps[:, r0 + hh * 128:r0 + (hh + 1) * 128], vt[:, tk, h * 128:(h + 1) * 128], wsT[:, h, :],
                         start=True, stop=True)
                tmp = xbf[:, 16 + half * 512:16 + (half + 1) * 512].rearrange("p (h t) -> p h t", h=4)
                S.tt("dve", tmp, ps[:, r0:r0 + 512].rearrange("p (h t) -> p h t", h=4),
                     bsT[:, half * 4:half * 4 + 4, :], ALU.add)
                uu = u[:, half * 4:half * 4 + 4, tk * 128:(tk + 1) * 128]
                S.tt("dve", uu, tmp, uu, ALU.mult)
        wv = [ws.get(k) for k in u_v]
        pend = None
        for tk in range(9):
            vr = g1[tk % 2]
            for cg in range(4):
                r0 = 3072 + ((tk * 4 + cg) % 2) * 512
                for kc in range(NCH):
                    S.mm(ps[:, r0:r0 + 256], x0b[:, kc, tk * 128:(tk + 1) * 128], wv[cg][:, kc, :],
                         start=(kc == 0), stop=(kc == NCH - 1))
                hv = hb[tk % 2][:, cg * 256:(cg + 1) * 256]
                t1v = g2[0][:, cg * 256:(cg + 1) * 256]
                t2v = g2[1][:, cg * 256:(cg + 1) * 256]
                S.act(hv, ps[:, r0:r0 + 256], AF.Identity)
                S.act(t1v, hv, AF.Square)
                S.ts("dve", t1v, t1v, GELU_C, 1.0, ALU.mult, ALU.add)
                S.tt("dve", t1v, t1v, hv, ALU.mult)
                if pend is not None:
                    pend()

                def pend(t1v=t1v, t2v=t2v, hv=hv, dstv=vr[:, cg * 256:(cg + 1) * 256]):
                    S.act(t2v, t1v, AF.Sigmoid, scale=GELU_S)
                    S.tt("dve", dstv, t2v, hv, ALU.mult)
            pend()
            pend = None
            if tk > 0:
                gating(tk - 1)
            sq = g2[0]
            S.add("dve", lambda e, vr=vr: e.reduce_sum(out=st[:, 0:1], in_=vr[:, 0:1024], axis=mybir.AxisListType.X),
                  reads=[vr[:, 0:1024]], writes=[st[:, 0:1]])
            S.act(sq[:, 0:1024], vr[:, 0:1024], AF.Square)
            S.add("dve", lambda e, sq=sq: e.reduce_sum(out=st[:, 1:2], in_=sq[:, 0:1024], axis=mybir.AxisListType.X),
                  reads=[sq[:, 0:1024]], writes=[st[:, 1:2]])
            S.ts("dve", st[:, 2:3], st[:, 0:1], 1.0 / 1024, None, ALU.mult)
            S.tt("dve", st[:, 3:4], st[:, 2:3], st[:, 2:3], ALU.mult)
            S.stt(st[:, 4:5], st[:, 1:2], 1.0 / 1024, st[:, 3:4], ALU.mult, ALU.subtract)
            S.act(st[:, 5:6], st[:, 4:5], AF.Sqrt, bias=eps[:, 0:1], scale=1.0)
            S.add("dve", lambda e: e.reciprocal(out=st[:, 6:7], in_=st[:, 5:6]), reads=[st[:, 5:6]],
                  writes=[st[:, 6:7]])
            S.ts("dve", vr[:, 0:1024], vr[:, 0:1024], st[:, 2:3], st[:, 6:7], ALU.subtract, ALU.mult)
            S.tt("dve", vr[:, 0:1024], vr[:, 0:1024], lnv[:, 0, :], ALU.mult)
            S.tt("dve", vt[:, tk, :], vr[:, 0:1024], lnv[:, 1, :], ALU.add)
        ws.release(u_v[3])
        gating(8)
        for g in range(4):
            for oc in range(2):
                g0 = GR[oc]
                for kc in range(2):
                    for (t0, w) in TT3:
                        S.mm(ps[:, g0 + t0:g0 + t0 + w], wp[:, g * 2 + kc, oc * 128:(oc + 1) * 128],
                             pp[:, g * 2 + kc, t0:t0 + w], start=(kc == 0), stop=(kc == 1))
            for oc in range(2):
                g0 = GR[oc]
                c = g * 2 + oc
                S.act(pp[:, c, :], ps[:, g0:g0 + TH], AF.Identity, scale=psc[:, c:c + 1])
        xs = [g1[0], g1[1]]
        chs = [S.new_chan(), S.new_chan()]
        for i in range(8):
            wt = ws.get(u_o[i])
            for sub in range(2):
                oc = i * 2 + sub
                g0 = GR[oc % 2]
                xst = xs[oc % 2]
                S.dma("sp", xst[:, 0:T + 2], x0T[oc * 128:(oc + 1) * 128, 126:TH], chs[oc % 2])
                for kc in range(NCH):
                    src = u[:, kc, :] if kc < 8 else pp[:, kc - 8, :]
                    lw = wt[:, kc, sub * 128:(sub + 1) * 128]
                    S.mm(ps[:, g0 + 510:g0 + 512], lw, src[:, 126:128], start=(kc == 0), stop=(kc == NCH - 1))
                    S.mm(ps[:, g0 + 512:g0 + 1024], lw, src[:, 128:640], start=(kc == 0), stop=(kc == NCH - 1))
                    S.mm(ps[:, g0 + 1024:g0 + 1536], lw, src[:, 640:1152], start=(kc == 0), stop=(kc == NCH - 1))
                S.stt(zf[:, oc, :], xst[:, 0:T + 2], ALPHA, ps[:, g0 + 510:g0 + 1536], ALU.mult, ALU.add)
            ws.release(u_o[i])
        tm_ln1 = {"eps": eps, "mean": g2[0], "rstd": g2[1],
                  "zb": [cv(R2 + k * 2052, (T + 2,), BF16) for k in range(2)],
                  "zs": [cv(R2 + (2 + k) * 2052, (T + 2,), BF16) for k in range(2)]}
        def ln1_post(c):
            S.ts("dve", zf[:, c, 0:2], zf[:, c, 0:2], flag[:, 0:1], None, ALU.mult)
            S.act(xb[:, c, :], zf[:, c, :], AF.Identity)
        emit_ln(S, zf, 0, T + 2, ln1gb[:, 0, :, :], ones, tm_ln1, ps, [lambda c: zf[:, c, :]], post=ln1_post)

        gq = cv(R2, (12, T), BF16)
        ft = [cv(R2 + 24576 + k * 4096, (T,), F32) for k in range(6)]
        tm_ffn = {"a": ft[0:2], "v": ft[2:4], "s": ft[4:6], "eps": eps, "mean": ft[0], "rstd": ft[1],
                  "zb": [cv(R2 + 49152 + k * 2048, (T,), BF16) for k in range(2)],
                  "zs": [cv(R2 + 53248 + k * 2048, (T,), BF16) for k in range(2)]}
        xb2 = cv(R1, (NCH, T), BF16)
        def ln2_l0(tt_):
            c0 = 2 + tt_ * 512
            emit_ln(S, zf, c0, 512, ln2gb[:, 0, :, :], ones, tm_ffn, ps,
                    [lambda c: xb2[:, c, tt_ * 512:(tt_ + 1) * 512], lambda c: zf[:, c, c0:c0 + 512]])
        emit_ffn(S, ws, plan0, zf, xb, cwb[:, 0, :, :], gq, tm_ffn, ps, ln_cb=ln2_l0)
        chsp = [S.new_chan() for _ in range(NCH)]
        for c in range(NCH):
            S.dma("sp", xsp[c * 128:(c + 1) * 128, :], zf[:, c, 2:T + 2], chsp[c])

        def mkset(k):
            o0 = R0 + k * 32768
            d_ = {"A": cv(o0, (T,), F32), "B": cv(o0 + 4096, (T,), F32), "C": cv(o0 + 8192, (T,), F32),
                  "kend": cv(o0 + 12288, (T,), BF16), "kdec": cv(o0 + 14336, (T,), BF16),
                  "qdec": cv(o0 + 16384, (T,), BF16), "ibf": cv(o0 + 18432, (T,), BF16),
                  "sg": cv(o0 + 20480, (T,), BF16), "osq": cv(o0 + 22528, (T,), BF16),
                  "attm": cv(o0 + 24576, (8, 128), BF16), "kt0": cv(o0 + 26624, (8, 128), BF16),
                  "kt1": cv(o0 + 28672, (8, 128), BF16), "vtk": cv(o0 + 30720, (8, 128), BF16),
                  "Sb": cv(R2 + 32768, (NCK, 128), BF16) if k == 0 else cv(R2 + 61472, (NCK, 128), BF16),
                  "dec": sb("dec%d" % k, [128, NCK]), "Sf": [sb("Sf%d_%d" % (k, i), [128, 128]) for i in range(2)],
                  "Pp": [sb("Pp%d_%d" % (k, i), [128, 128]) for i in range(2)],
                  "upst": cv(R2 + 59408, (4, 129), F32) if k == 0 else sb("upst1", [128, 4, 129]),
                  "stg": cv(R2 + 57344, (4, 129), F32),
                  "chu": S.new_chan(), "chst": S.new_chan()}
            return d_
        sets = [mkset(0), mkset(1)]
        y = cv(R2, (NH, T), BF16)
        xs1 = [cv(R2 + 40960, (T,), F32), cv(R2 + 45056, (T,), F32)]
        tm_ln1b = {"eps": eps, "mean": xs1[0], "rstd": xs1[1],
                   "zb": [cv(R2 + 49152 + k * 2048, (T,), BF16) for k in range(2)],
                   "zs": [cv(R2 + 53248 + k * 2048, (T,), BF16) for k in range(2)]}
        chc = S.new_chan(total=True)
        cst = {}
        lbp = sb("lbp_s", [128, 2, NH]); cst["lb"] = sb("lb", [128, NH]); cst["oml"] = sb("oml", [128, NH])
        cst["mask2"] = sb("mask2_s", [128, 128]); identf = sb("identf", [128, 128]); cst["ident"] = sb("ident_s", [128, 128], BF16)
        cst["pm"] = sb("pm", [128, 2])
        cst["one"] = sb("one_c", [128, 1])
        S.memset("dve", cst["one"][:, :], 1.0)
        cst["rm"] = cv(R2 + 36864, (T,), F32)
        S.dma("sp", lbp[:, :, :], lbp_d.rearrange("p (l h) -> p l h", l=2), chc)
        S.dma("sp", cst["mask2"][:, :], mask2_d, chc)
        S.dma("sp", identf[:, :], ident_d, chc)
        S.copy("dve", cst["ident"][:, :], identf[:, :])
        S.tt("dve", cst["lb"][:, :], lbp[:, 1, :], lbp[:, 0, :], ALU.subtract)
        S.act(cst["lb"][:, :], cst["lb"][:, :], AF.Sigmoid)
        S.ts("dve", cst["oml"][:, :], cst["lb"][:, :], -1.0, 1.0, ALU.mult, ALU.add)
        S.memset("dve", cst["rm"], 1.0)
        S.memset("dve", cst["rm"].rearrange("p (c t) -> p c t", t=CH)[:, :, 0:1], 0.0)
        S.memset("dve", cst["pm"][:, :], 0.0)
        S.memset("dve", cst["pm"][0:64, 0:1], 1.0)
        S.memset("dve", cst["pm"][64:128, 1:2], 1.0)
        oh3 = oh[:, 0:4].rearrange("p (j o) -> p j o", o=1)
        G0, G1, PB5, O0 = 0, 1024, 2560, 3072

        def proj1(wt, blk, g0):
            for kc in range(NCH):
                for t_ in range(2):
                    S.mm(ps[:, g0 + t_ * 512:g0 + (t_ + 1) * 512], wt[:, kc, blk * 128:(blk + 1) * 128],
                         xb2[:, kc, t_ * 512:(t_ + 1) * 512], start=(kc == 0), stop=(kc == NCH - 1))

        def cc_op(idx, src_t, dst_t):
            if use_cc:
                def fn(e):
                    e.collective_compute("AllReduce", ALU.add, replica_groups=SEQ_GROUPS,
                                         ins=[src_t.ap().opt()], outs=[dst_t.ap().opt()]).then_inc(ccsem[idx])
                    return None
                S.add("pool", fn, reads=[src_t.ap()], writes=[])

                def fn2(e):
                    e.wait_ge(ccsem[idx], 1)
                    return e.memset(tiny[:, idx:idx + 1], 0.0)
                return lambda: S.add("pool", fn2, reads=[], writes=[dst_t.ap(), tiny[:, idx:idx + 1]])
            else:
                chq = S.new_chan()
                S.dma("sp", dst_t.ap(), src_t.ap(), chq)
                return lambda: None

        def scan_group(q, g4, st_):
            pb = (2560, 3072, 3584, 2560)[g4]
            for cc in range(4):
                c = g4 * 4 + cc
                j, par = divmod(c, 2)
                S.mm(ps[:, pb + cc * 128:pb + (cc + 1) * 128], (q["kt0"], q["kt1"])[par][:, j, :], q["vtk"][:, j, :],
                     start=True, stop=True)
            for cc in range(4):
                c = g4 * 4 + cc
                cur = st_["cur"]
                S.stt(q["Sf"][1 - cur][:, :], q["Sf"][cur][:, :], q["dec"][:, c:c + 1],
                      ps[:, pb + cc * 128:pb + (cc + 1) * 128], ALU.mult, ALU.add)
                st_["cur"] = 1 - cur
                if st_["sb"] and c + 1 < NCK:
                    S.act(q["Sb"][:, c + 1, :], q["Sf"][1 - cur][:, :], AF.Identity)

        def interleave(bsteps, asteps, after):
            ai = 0
            for bi, bstep in enumerate(bsteps):
                bstep()
                while ai < len(asteps) and after[ai] == bi:
                    asteps[ai]()
                    ai += 1
            while ai < len(asteps):
                asteps[ai]()
                ai += 1

        cc_done = []

        def pre_A(h):
            q = sets[h % 2]

            def a1():
                q["wt"] = ws.get(u_pre[h])
                proj1(q["wt"], 0, G0)

            def a2():
                proj1(q["wt"], 1, G1)
                ws.release(u_pre[h])

            def a2g():
                hgrn_gates(S, h, cst, ps[:, G0:G0 + T], q["A"], q["B"], q["C"], q["kend"], q["dec"][:, :])
                S.act(q["ibf"], ps[:, G1:G1 + T], AF.Identity)
                C3 = q["C"].rearrange("p (c t) -> p c t", t=CH)
                S.add("dve", lambda e, C3=C3, h=h: e.reduce_sum(out=Dd[:, h:h + 1], in_=C3[:, :, CH - 1:CH],
                                                               axis=mybir.AxisListType.XY),
                      reads=[q["C"]], writes=[Dd[:, h:h + 1]])
                S.act(Dd[:, h:h + 1], Dd[:, h:h + 1], AF.Exp)
            return [a1, a2, a2g]

        def pre_B(h):
            q = sets[h % 2]
            st_ = {"cur": 0, "sb": False}

            def b1():
                hgrn_transposes(S, cst, psb, q["kend"], q["kt0"], q["kt1"])

            def b1b():
                hgrn_transposes(S, cst, psb, q["ibf"], q["vtk"])
                S.memset("dve", q["Sf"][0][:, :], 0.0)

            def bfin():
                fin = st_["cur"]
                for j in range(4):
                    S.ts("dve", q["stg"][:, j, 0:128], q["Sf"][fin][:, :], oh[:, j:j + 1], None, ALU.mult)
                S.ts("dve", q["stg"][:, :, 128:129], oh3, Dd[:, h:h + 1], None, ALU.mult)
                g, hl = divmod(h, 4)
                S.dma("sp", ccin[g].ap().rearrange("(j l d) n -> d j l n", j=4, l=4)[:, :, hl, :], q["stg"][:, :, :],
                      q["chst"])
                if hl == 3:
                    cc_done.append(cc_op(g, ccin[g], ccout[g]))
            return [b1, b1b] + [lambda g4=g4: scan_group(q, g4, st_) for g4 in range(4)] + [bfin]

        for stp in pre_A(0):
            stp()
        for h in range(NH):
            nxt = pre_A(h + 1) if h + 1 < NH else []
            interleave(pre_B(h), nxt, [0, 3, 3])

        def main_A(h):
            q = sets[h % 2]
            g, hl = divmod(h, 4)
            fi, qg = u_main[h]

            def a1():
                if hl == 0:
                    cc_done[g]()
                up = q["upst"]
                S.dma("sp", up[:, :, :], ccout[g].ap().rearrange("(j l d) n -> d j l n", j=4, l=4)[:, :, hl, :], q["chu"])
                Pp_, Sf_ = q["Pp"], q["Sf"]
                S.stt(Pp_[0][:, :], up[:, 0, 0:128], up[:, 1, 128:129], up[:, 1, 0:128], ALU.mult, ALU.add)
                S.stt(Pp_[1][:, :], Pp_[0][:, :], up[:, 2, 128:129], up[:, 2, 0:128], ALU.mult, ALU.add)
                S.ts("dve", Sf_[0][:, :], up[:, 0, 0:128], oh[:, 1:2], None, ALU.mult)
                S.stt(Sf_[0][:, :], Pp_[0][:, :], oh[:, 2:3], Sf_[0][:, :], ALU.mult, ALU.add)
                S.stt(Sf_[0][:, :], Pp_[1][:, :], oh[:, 3:4], Sf_[0][:, :], ALU.mult, ALU.add)
                q["wt"] = ws.get(fi)
                proj1(q["wt"], 0, G0)

            def a2():
                proj1(q["wt"], 1, G1)
                ws.release(fi)

            def a2g():
                hgrn_gates(S, h, cst, ps[:, G0:G0 + T], q["A"], q["B"], q["C"], q["kend"], q["dec"][:, :],
                           kdec_bf=q["kdec"], eC=True)
                S.act(q["ibf"], ps[:, G1:G1 + T], AF.Identity)

            def a3():
                q["wt"] = ws.get(qg)
                proj1(q["wt"], 0, G0)

            def a4():
                proj1(q["wt"], 1, G1)
                ws.release(qg)
                S.act(q["B"], ps[:, G0:G0 + T], AF.Silu)
                S.tt("dve", q["qdec"], q["B"], q["C"], ALU.mult)
                S.act(q["sg"], ps[:, G1:G1 + T], AF.Sigmoid)
            return [a1, a2, a2g, a3, a4]

        def main_B(h):
            q = sets[h % 2]
            st_ = {"cur": 0, "sb": True}

            def b1():
                hgrn_transposes(S, cst, psb, q["kend"], q["kt0"], q["kt1"])

            def b1b():
                hgrn_transposes(S, cst, psb, q["ibf"], q["vtk"])
                S.act(q["Sb"][:, 0, :], q["Sf"][0][:, :], AF.Identity)

            def batt(half):
                pb = 3072 + half * 512
                for jj in range(4):
                    j = half * 4 + jj
                    S.mm(ps[:, pb + jj * 128:pb + (jj + 1) * 128], q["kdec"][:, j * 128:(j + 1) * 128],
                         q["qdec"][:, j * 128:(j + 1) * 128], start=True, stop=True)
                for jj in range(4):
                    j = half * 4 + jj
                    S.tt("dve", q["attm"][:, j, :], ps[:, pb + jj * 128:pb + (jj + 1) * 128], cst["mask2"][:, :],
                         ALU.mult)

            def bo():
                for j in range(8):
                    S.mm(ps[:, O0 + j * 128:O0 + (j + 1) * 128], q["vtk"][:, j, :], q["attm"][:, j, :], start=True,
                         stop=False)
                    S.mm(ps[:, O0 + j * 128:O0 + j * 128 + 64], q["Sb"][:, 2 * j, :], q["qdec"][:, j * 128:j * 128 + 64],
                         start=False, stop=False)
                    S.mm(ps[:, O0 + j * 128 + 64:O0 + (j + 1) * 128], q["Sb"][:, 2 * j + 1, :],
                         q["qdec"][:, j * 128 + 64:(j + 1) * 128], start=False, stop=True)
                S.act(q["osq"], ps[:, O0:O0 + T], AF.Square)

            def bnorm():
                for t_ in range(2):
                    sl = slice(t_ * 512, (t_ + 1) * 512)
                    S.mm(ps[:, PB5:PB5 + 512], ones[:, :], q["osq"][:, sl], start=True, stop=True)
                    S.act(q["A"][:, sl], ps[:, PB5:PB5 + 512], AF.Ln, bias=eps[:, 0:1], scale=1.0 / 128)
                S.act(q["A"], q["A"], AF.Exp, scale=-0.5)
                S.stt(q["C"], ps[:, O0:O0 + T], gn[:, h:h + 1], q["A"], ALU.mult, ALU.mult)
                S.tt("dve", y[:, h, :], q["C"], q["sg"], ALU.mult)
            return ([b1, b1b] + [lambda g4=g4: scan_group(q, g4, st_) for g4 in range(4)]
                    + [lambda: batt(0), lambda: batt(1), bo, bnorm])

        for stp in main_A(0):
            stp()
        for h in range(NH):
            nxt = main_A(h + 1) if h + 1 < NH else []
            interleave(main_B(h), nxt, [0, 3, 3, 5, 8])
        chs1 = [S.new_chan() for _ in range(4)]
        xs4 = [cv(R2 + 40960 + k * 2048, (512,), F32) for k in range(4)]
        tm_t = {"eps": eps, "mean": cv(R2 + 49152, (512,), F32), "rstd": cv(R2 + 51200, (512,), F32),
                "zb": [cv(R2 + 53248 + k * 1024, (512,), BF16) for k in range(2)],
                "zs": [cv(R2 + 55296 + k * 1024, (512,), BF16) for k in range(2)]}
        OB = (512, 1024, 1536, 2560, 3072, 3584)
        done_h = None
        cnt = 0
        for tile in (1, 0):
            c0 = 2 + tile * 512
            for i in range(8):
                wt = ws.get(u_o1[(1 - tile) * 8 + i])
                for sub in range(2):
                    oc = i * 2 + sub
                    g0 = OB[cnt % 6]
                    xst = xs4[cnt % 4]
                    S.dma("sp", xst, xsp[oc * 128:(oc + 1) * 128, tile * 512:(tile + 1) * 512], chs1[cnt % 4])
                    for kc in range(NCH):
                        S.mm(ps[:, g0:g0 + 512], wt[:, kc, sub * 128:(sub + 1) * 128],
                             y[:, kc, tile * 512:(tile + 1) * 512], start=(kc == 0), stop=(kc == NCH - 1))
                    S.stt(zf[:, oc, c0:c0 + 512], xst, ALPHA, ps[:, g0:g0 + 512], ALU.mult, ALU.add)
                    cnt += 1
                ws.release(u_o1[(1 - tile) * 8 + i])
            emit_ln(S, zf, c0, 512, ln1gb[:, 1, :, :], ones, tm_t, ps,
                    [lambda c, c0=c0: xb[:, c, c0:c0 + 512], lambda c, c0=c0: zf[:, c, c0:c0 + 512]],
                    ps_off=(0, 2048))
            if tile == 1:
                for j in range(4):
                    S.ts("dve", hstg[:, j, :].rearrange("p (c t) -> p c t", t=2), zf[:, :, T:T + 2], oh[:, j:j + 1],
                         None, ALU.mult)
                chh = S.new_chan()
                S.dma("sp", cch_in.ap().rearrange("(j p) n -> p j n", p=128), hstg[:, :, :], chh)
                done_h = cc_op(4, cch_in, cch_out)
        done_h()
        chh2 = S.new_chan()
        S.dma("sp", hld[:, :, :], cch_out.ap().rearrange("(j p) n -> p j n", p=128), chh2)
        S.ts("dve", hsum[:, :], hld[:, 0, :], oh[:, 4:5], None, ALU.mult)
        for j in range(1, 4):
            S.stt(hsum[:, :], hld[:, j, :], oh[:, 4 + j:5 + j], hsum[:, :], ALU.mult, ALU.add)
        S.act(xb[:, :, 0:2], hsum[:, :].rearrange("p (c t) -> p c t", t=2), AF.Identity)
        def ln2_l1(tt_):
            c0 = 2 + tt_ * 512
            emit_ln(S, zf, c0, 512, ln2gb[:, 1, :, :], ones, tm_ffn, ps, [lambda c: zf[:, c, c0:c0 + 512]])
        emit_ffn(S, ws, plan1, zf, xb, cwb[:, 1, :, :], gq, tm_ffn, ps, ln_cb=ln2_l1)
        cho = [S.new_chan() for _ in range(4)]
        for c in range(NCH):
            S.dma("sp", outT[c * 128:(c + 1) * 128, :], zf[:, c, 2:T + 2], cho[c % 4])
        S.emit(nc, es, final_chans=cho)
    return nc, S


def fused_inputs(inp):
    f32 = np.float32
    maps = prep_mix0_inputs(inp["x"], inp["ev_w_in"], inp["ev_ln_v_g"], inp["ev_ln_v_b"], inp["ev_w_s"],
                            inp["ev_b_s"], inp["ev_w_pool"], inp["ev_pool_scale"], inp["ev_w_out"],
                            inp["ln1_g"], inp["ln1_b"])
    ln1gb = np.stack([np.stack([_pm(inp["ln1_g"][l], 16), _pm(inp["ln1_b"][l], 16)], axis=-1) for l in range(2)], axis=1)
    ln2gb = np.stack([np.stack([_pm(inp["ln2_g"][l], 16), _pm(inp["ln2_b"][l], 16)], axis=-1) for l in range(2)], axis=1)
    cwbs = []
    for l in range(2):
        cw = np.asarray(inp["ffn_conv_w"][l], f32)
        cb = np.asarray(inp["ffn_conv_b"][l], f32)
        cwbs.append(np.stack([_pm(cw[0], 88), _pm(cw[1], 88), _pm(cw[2], 88), _pm(cb, 88)], axis=-1))
    cwb = np.stack(cwbs, axis=1)
    common = {
        "ln1gb": np.ascontiguousarray(ln1gb.reshape(128, 64)), "ln2gb": np.ascontiguousarray(ln2gb.reshape(128, 64)),
        "cwb": np.ascontiguousarray(cwb.reshape(128, 2 * 88 * 4)),
        "w_up0": np.ascontiguousarray(inp["ffn_w_up"][0], f32), "w_up1": np.ascontiguousarray(inp["ffn_w_up"][1], f32),
        "w_down0": np.ascontiguousarray(inp["ffn_w_down"][0], f32),
        "w_down1": np.ascontiguousarray(inp["ffn_w_down"][1], f32),
        "od_w_in": regroup_w_in(inp["od_w_in"][0]), "od_w_out": np.ascontiguousarray(inp["od_w_out"][0], f32),
        "gn": _pm(inp["od_norm_g"][0], NH)}
    common.update(hgrn_const_inputs(inp["lb_param"]))
    out = []
    for c in range(NCORES):
        b, s = divmod(c, 4)
        m0 = maps[c]
        m = dict(common)
        for k in ("x0T", "wsT", "maskA", "bsT", "lnv", "w_pool", "pscale", "rcnt", "flag"):
            m[k] = m0[k]
        m["ev_w_in"] = m0["w_in"]
        m["ev_w_out"] = m0["w_out"]
        oh = np.zeros((128, 8), f32)
        oh[:, s] = 1.0
        if s > 0:
            oh[:, 4 + s - 1] = 1.0
        m["oh"] = oh
        out.append(m)
    return out


def kernel(**inputs):
    nc, _ = build_fused(use_cc=True)
    maps = fused_inputs(inputs)
    res = run_bass_kernel_spmd(nc, maps, core_ids=list(range(NCORES))).results
    out = np.zeros((2, 4 * T, D), np.float32)
    for c in range(NCORES):
        b, s = divmod(c, 4)
        out[b, s * T:(s + 1) * T] = res[c]["outT"].T
    return out
```

```python
import numpy as np
from contextlib import ExitStack
import concourse.bass as bass
import concourse.mybir as mybir
from concourse.bass_utils import run_bass_kernel_spmd

F32 = mybir.dt.float32
BF16 = mybir.dt.bfloat16
AF = mybir.ActivationFunctionType
ALU = mybir.AluOpType

D = 2048
NCH = 16
T = 1024
NCORES = 8
DFF = 5632
NFF = 44
ALPHA = 4.0 ** 0.25
LN_EPS = 1e-5
ENGS = ("pe", "act", "dve", "pool", "sp")
_DT_SIZE = {F32: 4, BF16: 2}


def _dsize(dt):
    return _DT_SIZE.get(dt, 4)


class _Op:
    __slots__ = ("eng", "idx", "fn", "deps", "chan", "chan_val", "signal", "val")


class Sched:
    def __init__(self):
        self.ops = {e: [] for e in ENGS}
        self.track = {}
        self.chan_cnt = []
        self.chan_total = []

    @staticmethod
    def _rng(ap):
        t = ap.tensor
        name = t.name
        sp = str(ap.space) if hasattr(ap, "space") else ""
        pat = ap.ap
        esz = _dsize(ap.dtype)
        if "DRAM" in sp.upper() or "Dram" in type(t).__name__ or "DRam" in type(t).__name__:
            ext = 1
            for (st, cnt) in pat:
                ext += abs(st) * (cnt - 1)
            return name, ap.offset * esz, (ap.offset + ext) * esz
        pstride = pat[0][0]
        lo = ap.offset % pstride if pstride > 0 else ap.offset
        ext = 1
        for (st, cnt) in pat[1:]:
            ext += abs(st) * (cnt - 1)
        return name, lo * esz, (lo + ext) * esz

    def _touch(self, name, lo, hi, op, is_write, deps):
        segs = self.track.setdefault(name, [])
        new = []
        covered = []
        for s in segs:
            slo, shi, w, rs = s
            if shi <= lo or slo >= hi:
                new.append(s)
                continue
            if slo < lo:
                new.append([slo, lo, w, list(rs)])
            if shi > hi:
                new.append([hi, shi, w, list(rs)])
            olo, ohi = max(slo, lo), min(shi, hi)
            if w is not None:
                deps.add(w)
            if is_write:
                for r in rs:
                    deps.add(r)
            else:
                covered.append([olo, ohi, w, rs + [op]])
        if is_write:
            new.append([lo, hi, op, []])
        else:
            covered.sort(key=lambda s: s[0])
            cur = lo
            for c in covered:
                if c[0] > cur:
                    new.append([cur, c[0], None, [op]])
                new.append(c)
                cur = c[1]
            if cur < hi:
                new.append([cur, hi, None, [op]])
        self.track[name] = new

    def add(self, eng, fn, reads=(), writes=(), chan=None):
        o = _Op()
        o.eng = eng
        o.fn = fn
        o.chan = chan
        o.signal = False
        o.val = None
        o.chan_val = None
        deps = set()
        for ap in reads:
            if ap is None or isinstance(ap, (int, float)):
                continue
            n, lo, hi = self._rng(ap)
            self._touch(n, lo, hi, o, False, deps)
        for ap in writes:
            n, lo, hi = self._rng(ap)
            if eng == "pe":
                lo = (lo // 2048) * 2048
                hi = ((hi + 2047) // 2048) * 2048
            self._touch(n, lo, hi, o, True, deps)
        deps.discard(o)
        o.deps = deps
        if chan is not None:
            self.chan_cnt[chan] += 1
            o.chan_val = 16 * self.chan_cnt[chan]
        o.idx = len(self.ops[eng])
        self.ops[eng].append(o)
        return o

    def new_chan(self, total=False):
        self.chan_cnt.append(0)
        self.chan_total.append(total)
        return len(self.chan_cnt) - 1

    def emit(self, nc, es, final_chans=()):
        for e in ENGS:
            for o in self.ops[e]:
                for d in o.deps:
                    if d.chan is None:
                        d.signal = True
        for e in ENGS:
            c = 0
            for o in self.ops[e]:
                if o.chan is None and o.signal:
                    c += 1
                    o.val = c
        esem = {e: es.enter_context(nc.semaphore("s_" + e)) for e in ENGS}
        csem = [es.enter_context(nc.semaphore("c_%d" % i)) for i in range(len(self.chan_cnt))]
        block = es.enter_context(nc.Block())
        nwaits = {e: 0 for e in ENGS}

        def run(engname, eobj):
            seen = {}
            for o in self.ops[engname]:
                need = {}
                for d in o.deps:
                    if d.chan is not None:
                        key = ("c", d.chan)
                        v = 16 * self.chan_cnt[d.chan] if self.chan_total[d.chan] else d.chan_val
                    else:
                        if d.eng == engname and engname == "pe":
                            continue
                        key = ("e", d.eng)
                        v = d.val
                    if v > need.get(key, 0):
                        need[key] = v
                for key, v in need.items():
                    if v <= seen.get(key, 0):
                        continue
                    seen[key] = v
                    sem = csem[key[1]] if key[0] == "c" else esem[key[1]]
                    eobj.wait_ge(sem, v)
                    nwaits[engname] += 1
                inst = o.fn(eobj)
                if o.chan is not None:
                    inst.then_inc(csem[o.chan], 16)
                elif o.signal:
                    assert inst is not None
                    inst.then_inc(esem[engname], 1)
            if engname == "sp":
                for ch in final_chans:
                    if self.chan_cnt[ch] > 0:
                        eobj.wait_ge(csem[ch], 16 * self.chan_cnt[ch])

        @block.tensor
        def _(e):
            run("pe", e)

        @block.scalar
        def _(e):
            run("act", e)

        @block.vector
        def _(e):
            run("dve", e)

        @block.gpsimd
        def _(e):
            run("pool", e)

        @block.sync
        def _(e):
            run("sp", e)

        self.nwaits = nwaits

    def mm(self, out, lhsT, rhs, start=True, stop=True):
        return self.add("pe", lambda e: e.matmul(out, lhsT=lhsT, rhs=rhs, start=start, stop=stop),
                        reads=[lhsT, rhs], writes=[out])

    def transpose(self, out, in_, ident):
        return self.add("pe", lambda e: e.transpose(out, in_, ident), reads=[in_, ident], writes=[out])

    def act(self, out, in_, func, bias=None, scale=None):
        kw = {}
        rd = [in_]
        if bias is not None:
            kw["bias"] = bias
            rd.append(bias)
        if scale is not None:
            kw["scale"] = scale
            rd.append(scale)
        return self.add("act", lambda e: e.activation(out=out, in_=in_, func=func, **kw), reads=rd, writes=[out])

    def tt(self, eng, out, in0, in1, op):
        return self.add(eng, lambda e: e.tensor_tensor(out=out, in0=in0, in1=in1, op=op),
                        reads=[in0, in1], writes=[out])

    def ts(self, eng, out, in0, s1, s2, op0, op1=None):
        if op1 is None:
            return self.add(eng, lambda e: e.tensor_scalar(out=out, in0=in0, scalar1=s1, scalar2=None, op0=op0),
                            reads=[in0, s1], writes=[out])
        return self.add(eng, lambda e: e.tensor_scalar(out=out, in0=in0, scalar1=s1, scalar2=s2, op0=op0, op1=op1),
                        reads=[in0, s1, s2], writes=[out])

    def stt(self, out, in0, scalar, in1, op0, op1):
        return self.add("dve", lambda e: e.scalar_tensor_tensor(out=out, in0=in0, scalar=scalar, in1=in1,
                                                                op0=op0, op1=op1),
                        reads=[in0, scalar, in1], writes=[out])

    def copy(self, eng, out, in_):
        if eng == "act":
            return self.add("act", lambda e: e.copy(out=out, in_=in_), reads=[in_], writes=[out])
        return self.add(eng, lambda e: e.tensor_copy(out=out, in_=in_), reads=[in_], writes=[out])

    def memset(self, eng, ap, val):
        return self.add(eng, lambda e: e.memset(ap, val), writes=[ap])

    def dma(self, eng, out, in_, chan):
        return self.add(eng, lambda e: e.dma_start(out=out, in_=in_), reads=[in_], writes=[out], chan=chan)


class WeightStream:
    def __init__(self, S, nc, es, nslots, free_elems, name="wslot"):
        self.S = S
        self.slots = [es.enter_context(nc.sbuf_tensor("%s%d" % (name, i), [128, free_elems], BF16))
                      for i in range(nslots)]
        self.chans = [S.new_chan() for _ in range(nslots)]
        self.uses = []
        self.loaded = 0
        self.released = -1
        self.n = nslots

    def plan(self, shape, src):
        self.uses.append((shape, src))
        return len(self.uses) - 1

    def view(self, k):
        shape, _ = self.uses[k]
        sl = self.slots[k % self.n]
        n = 1
        for s in shape:
            n *= s
        v = sl[:, 0:n]
        if len(shape) == 2:
            return v.rearrange("p (a b) -> p a b", a=shape[0], b=shape[1])
        return v

    def _load_upto(self, k):
        while self.loaded < len(self.uses) and self.loaded <= k:
            j = self.loaded
            _, src = self.uses[j]
            self.S.dma("pool", self.view(j), src, self.chans[j % self.n])
            self.loaded += 1

    def get(self, k):
        assert k <= self.released + self.n, (k, self.released)
        self._load_upto(k)
        return self.view(k)

    def release(self, k):
        self.released = max(self.released, k)
        self._load_upto(self.released + self.n)


FF_QUARTERS = (12, 10, 12, 10)


def plan_ffn_weights(ws, w_up, w_down, split_last=False):
    plan = []
    base = 0
    wu = w_up.rearrange("(kc p) n -> p kc n", p=128)
    for q, nq in enumerate(FF_QUARTERS):
        ups = []
        for j in range(0, nq, 2):
            ca = base + j
            ua = ws.plan((16, 256), wu[:, :, ca * 128:ca * 128 + 256])
            uv = ws.plan((16, 256), wu[:, :, (NFF + ca) * 128:(NFF + ca) * 128 + 256])
            ups.append((ca, ua, uv))
        downs = []
        wd = w_down[base * 128:(base + nq) * 128, :].rearrange("(j p) n -> p j n", p=128)
        reps = 2 if (split_last and q == len(FF_QUARTERS) - 1) else 1
        for rep in range(reps):
            for op_ in range(8):
                downs.append((op_, ws.plan((nq, 256), wd[:, :, op_ * 256:(op_ + 1) * 256])))
        plan.append((base, nq, ups, downs))
        base += nq
    return plan


def emit_ffn(S, ws, plan, xf, xb, cwb, gq, tmps, ps, ln_cb=None):
    G = (0, 1536)
    gi = 0
    for (base, nq, ups, downs) in plan:
        for (ca, ua, uv) in ups:
            wa = ws.get(ua)
            wv = ws.get(uv)
            for sub in range(2):
                c_a = ca + sub
                c_v = NFF + ca + sub
                j = c_a - base
                tm = {}
                for which, (wt, cc) in enumerate(((wa, c_a), (wv, c_v))):
                    g0 = G[which]
                    for kc in range(NCH):
                        lw = wt[:, kc, sub * 128:(sub + 1) * 128]
                        S.mm(ps[:, g0 + 510:g0 + 512], lw, xb[:, kc, 0:2], start=(kc == 0), stop=(kc == NCH - 1))
                        S.mm(ps[:, g0 + 512:g0 + 1024], lw, xb[:, kc, 2:514], start=(kc == 0), stop=(kc == NCH - 1))
                        S.mm(ps[:, g0 + 1024:g0 + 1536], lw, xb[:, kc, 514:1026], start=(kc == 0),
                             stop=(kc == NCH - 1))
                    tmp = tmps["a" if which == 0 else "v"][gi % 2]
                    tm[which] = tmp
                    S.act(tmp[:, :], ps[:, g0 + 512:g0 + 1536], AF.Identity, bias=cwb[:, cc, 3:4],
                          scale=cwb[:, cc, 2:3])
                    S.stt(tmp[:, :], ps[:, g0 + 511:g0 + 1535], cwb[:, cc, 1:2], tmp[:, :], ALU.mult, ALU.add)
                    S.stt(tmp[:, :], ps[:, g0 + 510:g0 + 1534], cwb[:, cc, 0:1], tmp[:, :], ALU.mult, ALU.add)
                sa = tmps["s"][gi % 2]
                S.act(sa[:, :], tm[0][:, :], AF.Silu)
                S.tt("dve", gq[:, j, :], sa[:, :], tm[1][:, :], ALU.mult)
                gi += 1
            ws.release(uv)
        def down_tile(wd, oc, sub, tt_, bank):
            pr = ps[:, 3072 + bank * 512:3072 + (bank + 1) * 512]
            for j in range(nq):
                S.mm(pr, wd[:, j, sub * 128:(sub + 1) * 128], gq[:, j, tt_ * 512:(tt_ + 1) * 512],
                     start=(j == 0), stop=(j == nq - 1))
            dst = xf[:, oc, 2 + tt_ * 512:2 + (tt_ + 1) * 512]
            if base == 0:
                S.stt(dst, dst, ALPHA, pr, ALU.mult, ALU.add)
            else:
                S.tt("dve", dst, dst, pr, ALU.add)
        if len(downs) == 8:
            for (op_, ud) in downs:
                wd = ws.get(ud)
                for sub in range(2):
                    for tt_ in range(2):
                        down_tile(wd, op_ * 2 + sub, sub, tt_, tt_)
                ws.release(ud)
        else:
            for tt_ in range(2):
                for (op_, ud) in downs[tt_ * 8:(tt_ + 1) * 8]:
                    wd = ws.get(ud)
                    for sub in range(2):
                        down_tile(wd, op_ * 2 + sub, sub, tt_, sub)
                    ws.release(ud)
                ln_cb(tt_)


def emit_ln(S, zf, c0, n, gb, ones, tmps, ps, outs, post=None, ps_off=(0, 2048)):
    nt = (n + 511) // 512
    zb = tmps["zb"]
    zs = tmps["zs"]
    for c in range(NCH):
        b0 = zb[c % 2]
        s0 = zs[c % 2]
        S.act(b0[:, 0:n], zf[:, c, c0:c0 + n], AF.Identity)
        S.act(s0[:, 0:n], zf[:, c, c0:c0 + n], AF.Square)
        for t_ in range(nt):
            w = min(512, n - t_ * 512)
            S.mm(ps[:, ps_off[0] + t_ * 512:ps_off[0] + t_ * 512 + w], ones[:, :], b0[:, t_ * 512:t_ * 512 + w],
                 start=(c == 0), stop=(c == NCH - 1))
            S.mm(ps[:, ps_off[1] + t_ * 512:ps_off[1] + t_ * 512 + w], ones[:, :], s0[:, t_ * 512:t_ * 512 + w],
                 start=(c == 0), stop=(c == NCH - 1))
    mean = tmps["mean"]
    rstd = tmps["rstd"]
    S.ts("dve", mean[:, 0:n], ps[:, ps_off[0]:ps_off[0] + n], 1.0 / D, None, ALU.mult)
    S.tt("dve", rstd[:, 0:n], mean[:, 0:n], mean[:, 0:n], ALU.mult)
    S.stt(rstd[:, 0:n], ps[:, ps_off[1]:ps_off[1] + n], 1.0 / D, rstd[:, 0:n], ALU.mult, ALU.subtract)
    S.act(rstd[:, 0:n], rstd[:, 0:n], AF.Ln, bias=tmps["eps"][:, 0:1], scale=1.0)
    S.act(rstd[:, 0:n], rstd[:, 0:n], AF.Exp, scale=-0.5)
    for c in range(NCH):
        zc = zf[:, c, c0:c0 + n]
        S.tt("dve", zc, zc, mean[:, 0:n], ALU.subtract)
        S.tt("dve", zc, zc, rstd[:, 0:n], ALU.mult)
        for i, dst in enumerate(outs):
            S.act(dst(c), zc, AF.Identity, bias=gb[:, c, 1:2], scale=gb[:, c, 0:1])
        if post is not None:
            post(c)


def build_ffn_launch():
    nc = bass.Bass("TRN2", target_bir_lowering=False)
    xT = nc.dram_tensor("xT", [D, T + 2], F32, kind="ExternalInput").ap()
    w_up = nc.dram_tensor("w_up", [D, 2 * DFF], F32, kind="ExternalInput").ap()
    w_down = nc.dram_tensor("w_down", [DFF, D], F32, kind="ExternalInput").ap()
    cwb_d = nc.dram_tensor("cwb", [128, 88 * 4], F32, kind="ExternalInput").ap()
    gb_d = nc.dram_tensor("ln2gb", [128, 32], F32, kind="ExternalInput").ap()
    yT = nc.dram_tensor("yT", [D, T], F32, kind="ExternalOutput").ap()
    S = Sched()
    with ExitStack() as es:
        xf = es.enter_context(nc.sbuf_tensor("xf", [128, NCH, T + 2], F32))
        xb = es.enter_context(nc.sbuf_tensor("xb", [128, NCH, T + 2], BF16))
        cwb = es.enter_context(nc.sbuf_tensor("cwb_s", [128, 88, 4], F32))
        gb = es.enter_context(nc.sbuf_tensor("gb_s", [128, NCH, 2], F32))
        gq = es.enter_context(nc.sbuf_tensor("gq", [128, 12, T], BF16))
        ones = es.enter_context(nc.sbuf_tensor("ones", [128, 128], BF16))
        eps = es.enter_context(nc.sbuf_tensor("eps", [128, 1], F32))
        tmps = {
            "a": [es.enter_context(nc.sbuf_tensor("ta%d" % i, [128, T], F32)) for i in range(2)],
            "v": [es.enter_context(nc.sbuf_tensor("tv%d" % i, [128, T], F32)) for i in range(2)],
            "s": [es.enter_context(nc.sbuf_tensor("tsl%d" % i, [128, T], F32)) for i in range(2)],
            "eps": eps,
        }
        tmps["zb"] = [es.enter_context(nc.sbuf_tensor("zb%d" % i, [128, T], BF16)) for i in range(2)]
        tmps["zs"] = [es.enter_context(nc.sbuf_tensor("zs%d" % i, [128, T], BF16)) for i in range(2)]
        tmps["mean"] = tmps["a"][0]
        tmps["rstd"] = tmps["a"][1]
        ps = es.enter_context(nc.psum_tensor("ps", [128, 4096], F32))
        ws = WeightStream(S, nc, es, 4, 16 * 256)
        plan = plan_ffn_weights(ws, w_up, w_down)

        ch_in = S.new_chan(total=True)
        ch_p = S.new_chan(total=True)
        ch_out = [S.new_chan() for _ in range(4)]
        S.dma("sp", cwb[:, :, :], cwb_d.rearrange("p (c j) -> p c j", j=4), ch_p)
        S.dma("sp", gb[:, :, :], gb_d.rearrange("p (c j) -> p c j", j=2), ch_p)
        S.memset("dve", ones[:, :], 1.0)
        S.memset("dve", eps[:, :], LN_EPS)
        for c in range(NCH):
            S.dma("sp", xf[:, c, :], xT[c * 128:(c + 1) * 128, :], ch_in)
        for c in range(NCH):
            S.act(xb[:, c, :], xf[:, c, :], AF.Identity)
        emit_ffn(S, ws, plan, xf, xb, cwb, gq, tmps, ps)
        emit_ln(S, xf, 2, T, gb, ones, tmps, ps, [lambda c: xf[:, c, 2:T + 2]])
        for c in range(NCH):
            S.dma("sp", yT[c * 128:(c + 1) * 128, :], xf[:, c, 2:T + 2], ch_out[c % 4])
        S.emit(nc, es, final_chans=ch_out)
    return nc, S


def _pm(v, nch):
    return np.ascontiguousarray(np.asarray(v, np.float32).reshape(nch, 128).T)


def prep_ffn_params(l, ffn_conv_w, ffn_conv_b, ln2_g, ln2_b):
    cw = np.asarray(ffn_conv_w[l], np.float32)
    cb = np.asarray(ffn_conv_b[l], np.float32)
    cwb = np.stack([_pm(cw[0], 88), _pm(cw[1], 88), _pm(cw[2], 88), _pm(cb, 88)], axis=-1)
    gb = np.stack([_pm(ln2_g[l], 16), _pm(ln2_b[l], 16)], axis=-1)
    return np.ascontiguousarray(cwb.reshape(128, 88 * 4)), np.ascontiguousarray(gb.reshape(128, 32))


def run_ffn_launch(x1, l, ffn_w_up, ffn_conv_w, ffn_conv_b, ffn_w_down, ln2_g, ln2_b):
    nc, S = build_ffn_launch()
    cwb, gb = prep_ffn_params(l, ffn_conv_w, ffn_conv_b, ln2_g, ln2_b)
    wu = np.ascontiguousarray(ffn_w_up[l], np.float32)
    wd = np.ascontiguousarray(ffn_w_down[l], np.float32)
    x1 = np.asarray(x1, np.float32)
    in_maps = []
    for c in range(NCORES):
        b, s = divmod(c, 4)
        t0 = s * T
        xt = np.zeros((D, T + 2), np.float32)
        xt[:, 2:] = x1[b, t0:t0 + T].T
        if s > 0:
            xt[:, 0:2] = x1[b, t0 - 2:t0].T
        in_maps.append({"xT": xt, "w_up": wu, "w_down": wd, "cwb": cwb, "ln2gb": gb})
    res = run_bass_kernel_spmd(nc, in_maps, core_ids=list(range(NCORES)))
    out = np.zeros((2, 4096, D), np.float32)
    for c in range(NCORES):
        b, s = divmod(c, 4)
        out[b, s * T:(s + 1) * T] = res.results[c]["yT"].T
    return out


TH = T + 128
B_WINDOWS = (2, 4, 8, 16)
GELU_C = 0.044715
GELU_S = 2.0 * 0.7978845608028654


def emit_gelu(S, dst, src_ps, t1, t2):
    S.act(t1, src_ps, AF.Square)
    S.ts("dve", t1, t1, GELU_C, 1.0, ALU.mult, ALU.add)
    S.tt("dve", t1, t1, src_ps, ALU.mult)
    S.act(t2, t1, AF.Sigmoid, scale=GELU_S)
    S.tt("dve", dst, t2, src_ps, ALU.mult)


def build_mix0_launch():
    nc = bass.Bass("TRN2", target_bir_lowering=False)
    x0T = nc.dram_tensor("x0T", [D, TH], F32, kind="ExternalInput").ap()
    w_in = nc.dram_tensor("w_in", [D, 3072], F32, kind="ExternalInput").ap()
    w_out = nc.dram_tensor("w_out", [D, D], F32, kind="ExternalInput").ap()
    wsT_d = nc.dram_tensor("wsT", [128, 8 * 128], F32, kind="ExternalInput").ap()
    mask_d = nc.dram_tensor("maskA", [128, 128], F32, kind="ExternalInput").ap()
    bsT_d = nc.dram_tensor("bsT", [128, 8 * 128], F32, kind="ExternalInput").ap()
    lnv_d = nc.dram_tensor("lnv", [128, 2 * 1024], F32, kind="ExternalInput").ap()
    wp_d = nc.dram_tensor("w_pool", [4 * 256, 256], F32, kind="ExternalInput").ap()
    psc_d = nc.dram_tensor("pscale", [128, 8], F32, kind="ExternalInput").ap()
    gb_d = nc.dram_tensor("ln1gb", [128, 32], F32, kind="ExternalInput").ap()
    rc_d = nc.dram_tensor("rcnt", [128, 64], F32, kind="ExternalInput").ap()
    flag_d = nc.dram_tensor("flag", [128, 1], F32, kind="ExternalInput").ap()
    x1T = nc.dram_tensor("x1T", [D, T + 2], F32, kind="ExternalOutput").ap()
    S = Sched()
    with ExitStack() as es:
        arena = es.enter_context(nc.sbuf_tensor("arena", [128, NCH * (T + 2)], F32))
        zf = arena[:, :].rearrange("p (c t) -> p c t", c=NCH, t=T + 2)
        x0b = arena[:, 0:NCH * TH // 2].bitcast(BF16).rearrange("p (c t) -> p c t", c=NCH, t=TH)
        u = es.enter_context(nc.sbuf_tensor("u", [128, 8, TH], BF16))
        vt = es.enter_context(nc.sbuf_tensor("vt", [128, 9, 1024], BF16))
        pp = es.enter_context(nc.sbuf_tensor("pp", [128, 8, TH], BF16))
        wsT = es.enter_context(nc.sbuf_tensor("wsT_s", [128, 8, 128], BF16))
        mask = es.enter_context(nc.sbuf_tensor("mask_s", [128, 128], F32))
        bsT = es.enter_context(nc.sbuf_tensor("bsT_s", [128, 8, 128], F32))
        lnv = es.enter_context(nc.sbuf_tensor("lnv_s", [128, 2, 1024], F32))
        wp = es.enter_context(nc.sbuf_tensor("wp_s", [128, 8, 256], BF16))
        psc = es.enter_context(nc.sbuf_tensor("psc_s", [128, 8], F32))
        gb = es.enter_context(nc.sbuf_tensor("gb_s", [128, NCH, 2], F32))
        rc = es.enter_context(nc.sbuf_tensor("rc_s", [128, 4, 16], F32))
        flag = es.enter_context(nc.sbuf_tensor("flag_s", [128, 1], F32))
        ones = es.enter_context(nc.sbuf_tensor("ones", [128, 128], BF16))
        eps = es.enter_context(nc.sbuf_tensor("eps", [128, 1], F32))
        xbf = es.enter_context(nc.sbuf_tensor("xbf", [128, 16 + TH], F32))
        g1 = [es.enter_context(nc.sbuf_tensor("g1_%d" % i, [128, TH], F32)) for i in range(2)]
        g2p = [es.enter_context(nc.sbuf_tensor("g2_%d" % i, [128, 16 + TH], F32)) for i in range(2)]
        g2 = [t[:, 16:16 + TH] for t in g2p]
        tA, tB = g2p
        wsTf = g1[1][:, 0:1024].rearrange("p (h t) -> p h t", h=8)
        st = es.enter_context(nc.sbuf_tensor("st", [128, 8], F32))
        small = es.enter_context(nc.sbuf_tensor("small", [128, 32], F32))
        tmps = {"eps": eps,
                "zb": [g2p[i][:, 16:16 + 513].bitcast(BF16) for i in range(2)],
                "zs": [xbf[:, 16:16 + 513].bitcast(BF16), xbf[:, 600:600 + 513].bitcast(BF16)],
                "mean": g1[0], "rstd": g1[1]}
        ps = es.enter_context(nc.psum_tensor("ps", [128, 4096], F32))
        ws = WeightStream(S, nc, es, 4, 16 * 256)
        wi = w_in.rearrange("(kc p) n -> p kc n", p=128)
        wo = w_out.rearrange("(kc p) n -> p kc n", p=128)
        u_xb = [ws.plan((16, 256), wi[:, :, 2048 + i * 256:2048 + (i + 1) * 256]) for i in range(4)]
        u_u = [ws.plan((16, 256), wi[:, :, i * 256:(i + 1) * 256]) for i in range(4)]
        u_v = [ws.plan((16, 256), wi[:, :, 1024 + i * 256:1024 + (i + 1) * 256]) for i in range(4)]
        u_o = [ws.plan((16, 256), wo[:, :, i * 256:(i + 1) * 256]) for i in range(8)]

        chp = S.new_chan(total=True)
        chx = S.new_chan(total=True)
        S.dma("sp", wsTf, wsT_d.rearrange("p (h t) -> p h t", h=8), chp)
        S.dma("sp", mask[:, :], mask_d, chp)
        S.dma("sp", bsT[:, :, :], bsT_d.rearrange("p (h t) -> p h t", h=8), chp)
        S.dma("sp", lnv[:, :, :], lnv_d.rearrange("p (a c) -> p a c", a=2), chp)
        S.dma("sp", psc[:, :], psc_d, chp)
        S.dma("sp", gb[:, :, :], gb_d.rearrange("p (c j) -> p c j", j=2), chp)
        S.dma("sp", rc[:, :, :], rc_d.rearrange("p (g j) -> p g j", g=4), chp)
        S.dma("sp", flag[:, :], flag_d, chp)
        S.dma("pool", wp[:, :, :], wp_d.rearrange("(a p) n -> p a n", p=128), chx)
        for c in range(NCH):
            S.dma("pool", x0b[:, c, :], x0T[c * 128:(c + 1) * 128, :], chx)
        ws.release(-1)
        S.memset("dve", ones[:, :], 1.0)
        S.memset("dve", eps[:, :], LN_EPS)
        S.memset("dve", xbf[:, 0:16], 0.0)
        S.memset("dve", tA[:, 0:16], 0.0)
        S.memset("dve", tB[:, 0:16], 0.0)
        for h in range(8):
            S.tt("dve", wsT[:, h, :], wsTf[:, h, :], mask[:, :], ALU.mult)

        GR = (0, 1536)
        TT3 = ((0, 512), (512, 512), (1024, 128))

        def proj_fm(wt, sub, g0):
            for kc in range(NCH):
                for (t0, w) in TT3:
                    S.mm(ps[:, g0 + t0:g0 + t0 + w], wt[:, kc, sub * 128:(sub + 1) * 128], x0b[:, kc, t0:t0 + w],
                         start=(kc == 0), stop=(kc == NCH - 1))

        gi = 0
        for i in range(4):
            wt = ws.get(u_xb[i])
            for sub in range(2):
                c = i * 2 + sub
                g = c // 2
                g0 = GR[gi % 2]
                gi += 1
                proj_fm(wt, sub, g0)
                S.act(xbf[:, 16:16 + TH], ps[:, g0:g0 + TH], AF.Identity)
                src = xbf
                dsts = [tA, tB]
                for k in range(g + 1):
                    sh = 1 << k
                    dst = dsts[k % 2]
                    S.tt("dve", dst[:, 16:16 + TH], src[:, 16:16 + TH], src[:, 16 - sh:16 + TH - sh], ALU.add)
                    src = dst
                win = B_WINDOWS[g]
                S.stt(pp[:, c, :], src[:, 16:16 + TH], 1.0 / win, xbf[:, 16:16 + TH], ALU.mult, ALU.subtract)
                S.tt("dve", small[:, 0:16], src[:, 16 + 128:16 + 144], rc[:, g, :], ALU.mult)
                S.tt("dve", pp[:, c, 128:144], small[:, 0:16], xbf[:, 16 + 128:16 + 144], ALU.subtract)
            ws.release(u_xb[i])
        for i in range(4):
            wt = ws.get(u_u[i])
            for sub in range(2):
                c = i * 2 + sub
                g0 = GR[gi % 2]
                proj_fm(wt, sub, g0)
                emit_gelu(S, u[:, c, :], ps[:, g0:g0 + TH], g1[gi % 2][:, :], g2[gi % 2])
                gi += 1
            ws.release(u_u[i])
        wv = [ws.get(k) for k in u_v]
        for tk in range(9):
            vr = g1[tk % 2]
            for cg in range(4):
                r0 = 3072 + ((tk * 4 + cg) % 2) * 512
                for kc in range(NCH):
                    S.mm(ps[:, r0:r0 + 256], x0b[:, kc, tk * 128:(tk + 1) * 128], wv[cg][:, kc, :],
                         start=(kc == 0), stop=(kc == NCH - 1))
                emit_gelu(S, vr[:, cg * 256:(cg + 1) * 256], ps[:, r0:r0 + 256],
                          g2[0][:, cg * 256:(cg + 1) * 256], g2[1][:, cg * 256:(cg + 1) * 256])
            sq = g2[0]
            S.add("dve", lambda e, vr=vr: e.reduce_sum(out=st[:, 0:1], in_=vr[:, 0:1024], axis=mybir.AxisListType.X),
                  reads=[vr[:, 0:1024]], writes=[st[:, 0:1]])
            S.act(sq[:, 0:1024], vr[:, 0:1024], AF.Square)
            S.add("dve", lambda e, sq=sq: e.reduce_sum(out=st[:, 1:2], in_=sq[:, 0:1024], axis=mybir.AxisListType.X),
                  reads=[sq[:, 0:1024]], writes=[st[:, 1:2]])
            S.ts("dve", st[:, 2:3], st[:, 0:1], 1.0 / 1024, None, ALU.mult)
            S.tt("dve", st[:, 3:4], st[:, 2:3], st[:, 2:3], ALU.mult)
            S.stt(st[:, 4:5], st[:, 1:2], 1.0 / 1024, st[:, 3:4], ALU.mult, ALU.subtract)
            S.act(st[:, 5:6], st[:, 4:5], AF.Sqrt, bias=eps[:, 0:1], scale=1.0)
            S.add("dve", lambda e: e.reciprocal(out=st[:, 6:7], in_=st[:, 5:6]), reads=[st[:, 5:6]],
                  writes=[st[:, 6:7]])
            S.ts("dve", vr[:, 0:1024], vr[:, 0:1024], st[:, 2:3], st[:, 6:7], ALU.subtract, ALU.mult)
            S.tt("dve", vr[:, 0:1024], vr[:, 0:1024], lnv[:, 0, :], ALU.mult)
            S.tt("dve", vt[:, tk, :], vr[:, 0:1024], lnv[:, 1, :], ALU.add)
        ws.release(u_v[3])
        for tk in range(9):
            for half in range(2):
                r0 = 2048 + half * 512
                for hh in range(4):
                    h = half * 4 + hh
                    S.mm(ps[:, r0 + hh * 128:r0 + (hh + 1) * 128], vt[:, tk, h * 128:(h + 1) * 128], wsT[:, h, :],
                         start=True, stop=True)
                tmp = g2[half][:, 0:512].rearrange("p (h t) -> p h t", h=4)
                S.tt("dve", tmp, ps[:, r0:r0 + 512].rearrange("p (h t) -> p h t", h=4),
                     bsT[:, half * 4:half * 4 + 4, :], ALU.add)
                uu = u[:, half * 4:half * 4 + 4, tk * 128:(tk + 1) * 128]
                S.tt("dve", uu, tmp, uu, ALU.mult)
        for g in range(4):
            for oc in range(2):
                g0 = GR[oc]
                for kc in range(2):
                    for (t0, w) in TT3:
                        S.mm(ps[:, g0 + t0:g0 + t0 + w], wp[:, g * 2 + kc, oc * 128:(oc + 1) * 128],
                             pp[:, g * 2 + kc, t0:t0 + w], start=(kc == 0), stop=(kc == 1))
            for oc in range(2):
                g0 = GR[oc]
                c = g * 2 + oc
                S.act(pp[:, c, :], ps[:, g0:g0 + TH], AF.Identity, scale=psc[:, c:c + 1])
        xs = [g1[0], g1[1]]
        chs = [S.new_chan(), S.new_chan()]
        for i in range(8):
            wt = ws.get(u_o[i])
            for sub in range(2):
                oc = i * 2 + sub
                g0 = GR[oc % 2]
                xst = xs[oc % 2]
                S.dma("sp", xst[:, 0:T + 2], x0T[oc * 128:(oc + 1) * 128, 126:TH], chs[oc % 2])
                for kc in range(NCH):
                    src = u[:, kc, :] if kc < 8 else pp[:, kc - 8, :]
                    lw = wt[:, kc, sub * 128:(sub + 1) * 128]
                    S.mm(ps[:, g0 + 510:g0 + 512], lw, src[:, 126:128], start=(kc == 0), stop=(kc == NCH - 1))
                    S.mm(ps[:, g0 + 512:g0 + 1024], lw, src[:, 128:640], start=(kc == 0), stop=(kc == NCH - 1))
                    S.mm(ps[:, g0 + 1024:g0 + 1536], lw, src[:, 640:1152], start=(kc == 0), stop=(kc == NCH - 1))
                S.stt(zf[:, oc, :], xst[:, 0:T + 2], ALPHA, ps[:, g0 + 510:g0 + 1536], ALU.mult, ALU.add)
            ws.release(u_o[i])
        emit_ln(S, zf, 0, T + 2, gb, ones, tmps, ps, [lambda c: zf[:, c, :]])
        S.ts("dve", zf[:, :, 0:2], zf[:, :, 0:2], flag[:, 0:1], None, ALU.mult)
        cho = [S.new_chan() for _ in range(4)]
        for c in range(NCH):
            S.dma("sp", x1T[c * 128:(c + 1) * 128, :], zf[:, c, :], cho[c % 4])
        S.emit(nc, es, final_chans=cho)
    return nc, S


def prep_mix0_inputs(x, ev_w_in, ev_ln_v_g, ev_ln_v_b, ev_w_s, ev_b_s, ev_w_pool, ev_pool_scale, ev_w_out,
                     ln1_g, ln1_b):
    x = np.asarray(x, np.float32)
    ws = np.asarray(ev_w_s[0], np.float32)
    wsT = np.ascontiguousarray(ws.transpose(2, 0, 1)).reshape(128, 8 * 128)
    tt_ = np.arange(128)
    maskA = (tt_[None, :] >= tt_[:, None]).astype(np.float32)
    bsT = np.ascontiguousarray(np.broadcast_to(np.asarray(ev_b_s[0], np.float32).reshape(1, 8 * 128), (128, 8 * 128)))
    lnv = np.ascontiguousarray(np.broadcast_to(
        np.concatenate([np.asarray(ev_ln_v_g[0], np.float32), np.asarray(ev_ln_v_b[0], np.float32)])[None, :],
        (128, 2048)))
    wp = np.ascontiguousarray(np.asarray(ev_w_pool[0], np.float32).reshape(4 * 256, 256))
    psc = _pm(ev_pool_scale[0], 8)
    gb = np.ascontiguousarray(np.stack([_pm(ln1_g[0], 16), _pm(ln1_b[0], 16)], axis=-1).reshape(128, 32))
    common = {"w_in": np.ascontiguousarray(ev_w_in[0], np.float32),
              "w_out": np.ascontiguousarray(ev_w_out[0], np.float32),
              "wsT": wsT, "maskA": maskA, "bsT": bsT, "lnv": lnv, "w_pool": wp, "pscale": psc, "ln1gb": gb}
    in_maps = []
    for c in range(NCORES):
        b, s = divmod(c, 4)
        t0 = s * T
        xt = np.zeros((D, TH), np.float32)
        xt[:, 128:] = x[b, t0:t0 + T].T
        if s > 0:
            xt[:, 0:128] = x[b, t0 - 128:t0].T
        rc = np.zeros((4, 16), np.float32)
        for g, win in enumerate(B_WINDOWS):
            pos = np.arange(t0 + 1, t0 + 17, dtype=np.float32)
            rc[g] = 1.0 / np.minimum(pos, float(win))
        rcb = np.ascontiguousarray(np.broadcast_to(rc.reshape(1, 64), (128, 64)))
        m = dict(common)
        m.update({"x0T": xt, "rcnt": rcb, "flag": np.full((128, 1), 1.0 if s > 0 else 0.0, np.float32)})
        in_maps.append(m)
    return in_maps


def run_mix0_launch(inputs):
    nc, S = build_mix0_launch()
    in_maps = prep_mix0_inputs(inputs["x"], inputs["ev_w_in"], inputs["ev_ln_v_g"], inputs["ev_ln_v_b"],
                               inputs["ev_w_s"], inputs["ev_b_s"], inputs["ev_w_pool"], inputs["ev_pool_scale"],
                               inputs["ev_w_out"], inputs["ln1_g"], inputs["ln1_b"])
    res = run_bass_kernel_spmd(nc, in_maps, core_ids=list(range(NCORES)))
    return [res.results[c]["x1T"] for c in range(NCORES)]


NH = 16
CH = 64
NCK = T // CH


def hgrn_consts(S, nc, es, lbp_d, mask2_d, ident_d, chp):
    c = {}
    lbp = es.enter_context(nc.sbuf_tensor("lbp_s", [128, 2, NH], F32))
    c["lb"] = es.enter_context(nc.sbuf_tensor("lb", [128, NH], F32))
    c["oml"] = es.enter_context(nc.sbuf_tensor("oml", [128, NH], F32))
    c["rm"] = es.enter_context(nc.sbuf_tensor("rm", [128, T], F32))
    c["mask2"] = es.enter_context(nc.sbuf_tensor("mask2_s", [128, 128], F32))
    identf = es.enter_context(nc.sbuf_tensor("identf", [128, 128], F32))
    c["ident"] = es.enter_context(nc.sbuf_tensor("ident_s", [128, 128], BF16))
    S.dma("sp", lbp[:, :, :], lbp_d.rearrange("p (l h) -> p l h", l=2), chp)
    S.dma("sp", c["mask2"][:, :], mask2_d, chp)
    S.dma("sp", identf[:, :], ident_d, chp)
    S.copy("dve", c["ident"][:, :], identf[:, :])
    S.tt("dve", c["lb"][:, :], lbp[:, 1, :], lbp[:, 0, :], ALU.subtract)
    S.act(c["lb"][:, :], c["lb"][:, :], AF.Sigmoid)
    S.ts("dve", c["oml"][:, :], c["lb"][:, :], -1.0, 1.0, ALU.mult, ALU.add)
    S.memset("dve", c["rm"][:, :], 1.0)
    S.memset("dve", c["rm"][:, :].rearrange("p (c t) -> p c t", t=CH)[:, :, 0:1], 0.0)
    c["pm"] = es.enter_context(nc.sbuf_tensor("pm", [128, 2], F32))
    c["one"] = es.enter_context(nc.sbuf_tensor("one_c", [128, 1], F32))
    S.memset("dve", c["one"][:, :], 1.0)
    S.memset("dve", c["pm"][:, :], 0.0)
    S.memset("dve", c["pm"][0:64, 0:1], 1.0)
    S.memset("dve", c["pm"][64:128, 1:2], 1.0)
    return c


def hgrn_gates(S, h, cst, f_ps, A, B, C, kend_bf, dec, kdec_bf=None, eC=False):
    oml = cst["oml"][:, h:h + 1]
    lb = cst["lb"][:, h:h + 1]
    S.act(A, f_ps, AF.Sigmoid)
    S.act(A, A, AF.Identity, bias=lb, scale=oml)
    S.act(B, A, AF.Ln)
    rm_ap = cst["rm"][:, :]
    S.add("dve", lambda e: e.tensor_tensor_scan(out=C, data0=rm_ap, data1=B, initial=0.0, op0=ALU.mult, op1=ALU.add),
          reads=[rm_ap, B], writes=[C])
    S.act(A, A, AF.Identity, bias=cst["one"][:, 0:1], scale=-1.0)
    S.act(B, C, AF.Exp, scale=-1.0)
    C3 = C.rearrange("p (c t) -> p c t", t=CH)
    S.act(dec.rearrange("p (c o) -> p c o", o=1), C3[:, :, CH - 1:CH], AF.Exp)
    if eC:
        S.act(C, C, AF.Exp)
    S.tt("dve", B, A, B, ALU.mult)
    S.tt("dve", kend_bf.rearrange("p (c t) -> p c t", t=CH), B.rearrange("p (c t) -> p c t", t=CH),
         dec.rearrange("p (c o) -> p c o", o=1).to_broadcast([128, NCK, CH]), ALU.mult)
    if kdec_bf is not None:
        S.copy("dve", kdec_bf, B)


def hgrn_transposes(S, cst, psb, src_bf, dst_tok, dst_tok1=None):
    for j in range(8):
        S.transpose(psb[:, 4096 + j * 128:4096 + (j + 1) * 128], src_bf[:, j * 128:(j + 1) * 128], cst["ident"][:, :])
    if dst_tok1 is None:
        S.act(dst_tok.rearrange("p j d -> p (j d)"), psb[:, 4096:5120], AF.Identity)
    else:
        S.act(dst_tok.rearrange("p j d -> p (j d)"), psb[:, 4096:5120], AF.Identity, scale=cst["pm"][:, 0:1])
        S.act(dst_tok1.rearrange("p j d -> p (j d)"), psb[:, 4096:5120], AF.Identity, scale=cst["pm"][:, 1:2])


def hgrn_state_scan(S, ps, kt, vtk, dec, Sf, Sb=None):
    cur = 0
    if Sb is not None:
        S.act(Sb[:, 0, :], Sf[0][:, :], AF.Identity)
    for g4 in range(4):
        for cc in range(4):
            c = g4 * 4 + cc
            j, par = divmod(c, 2)
            S.mm(ps[:, 2560 + cc * 128:2560 + (cc + 1) * 128], kt[par][:, j, :], vtk[:, j, :], start=True, stop=True)
        for cc in range(4):
            c = g4 * 4 + cc
            nxt = 1 - cur
            S.stt(Sf[nxt][:, :], Sf[cur][:, :], dec[:, c:c + 1], ps[:, 2560 + cc * 128:2560 + (cc + 1) * 128],
                  ALU.mult, ALU.add)
            cur = nxt
            if Sb is not None and c + 1 < NCK:
                S.act(Sb[:, c + 1, :], Sf[cur][:, :], AF.Identity)
    return cur


def build_hgrn_pre_launch(stage=9, nheads=NH):
    nc = bass.Bass("TRN2", target_bir_lowering=False)
    xT = nc.dram_tensor("xT", [D, T], F32, kind="ExternalInput").ap()
    w_in = nc.dram_tensor("w_in", [D, 4 * D], F32, kind="ExternalInput").ap()
    lbp_d = nc.dram_tensor("lbp", [128, 2 * NH], F32, kind="ExternalInput").ap()
    mask2_d = nc.dram_tensor("mask2", [128, 128], F32, kind="ExternalInput").ap()
    ident_d = nc.dram_tensor("ident", [128, 128], F32, kind="ExternalInput").ap()
    U_d = nc.dram_tensor("U", [NH, 128, 128], F32, kind="ExternalOutput").ap()
    D_d = nc.dram_tensor("Dd", [128, NH], F32, kind="ExternalOutput").ap()
    S = Sched()
    with ExitStack() as es:
        xb = es.enter_context(nc.sbuf_tensor("xb", [128, NCH, T], BF16))
        A = [es.enter_context(nc.sbuf_tensor("A%d" % i, [128, T], F32)) for i in range(2)]
        B = [es.enter_context(nc.sbuf_tensor("B%d" % i, [128, T], F32)) for i in range(2)]
        C = [es.enter_context(nc.sbuf_tensor("C%d" % i, [128, T], F32)) for i in range(2)]
        kend = [es.enter_context(nc.sbuf_tensor("kend%d" % i, [128, T], BF16)) for i in range(2)]
        ibf = [es.enter_context(nc.sbuf_tensor("ibf%d" % i, [128, T], BF16)) for i in range(2)]
        kt = [[es.enter_context(nc.sbuf_tensor("kt%d_%d" % (i, k), [128, 8, 128], BF16)) for k in range(2)]
              for i in range(2)]
        vtk = [es.enter_context(nc.sbuf_tensor("vtk%d" % i, [128, 8, 128], BF16)) for i in range(2)]
        dec = [es.enter_context(nc.sbuf_tensor("dec%d" % i, [128, NCK], F32)) for i in range(2)]
        Sf = [[es.enter_context(nc.sbuf_tensor("Sf%d_%d" % (i, k), [128, 128], F32)) for k in range(2)]
              for i in range(2)]
        Dd = es.enter_context(nc.sbuf_tensor("Dd_s", [128, NH], F32))
        ps = es.enter_context(nc.psum_tensor("ps", [128, 4096], F32))
        psb = ps[:, :].bitcast(BF16)
        chp = S.new_chan(total=True)
        chx = S.new_chan(total=True)
        cst = hgrn_consts(S, nc, es, lbp_d, mask2_d, ident_d, chp)
        ws = WeightStream(S, nc, es, 4, 16 * 256)
        wi = w_in.rearrange("(kc p) n -> p kc n", p=128)
        uses = [ws.plan((16, 256), wi[:, :, h * 512 + 256:h * 512 + 512]) for h in range(NH)]
        for c in range(NCH):
            S.dma("pool", xb[:, c, :], xT[c * 128:(c + 1) * 128, :], chx)
        ws.release(-1)
        cho = [S.new_chan() for _ in range(2)]
        for h in range(nheads):
            b = h % 2
            wt = ws.get(uses[h])
            for which in range(2):
                g0 = which * 1024
                for kc in range(NCH):
                    for t_ in range(2):
                        S.mm(ps[:, g0 + t_ * 512:g0 + (t_ + 1) * 512], wt[:, kc, which * 128:(which + 1) * 128],
                             xb[:, kc, t_ * 512:(t_ + 1) * 512], start=(kc == 0), stop=(kc == NCH - 1))
            ws.release(uses[h])
            hgrn_gates(S, h, cst, ps[:, 0:1024], A[b][:, :], B[b][:, :], C[b][:, :], kend[b][:, :], dec[b][:, :])
            S.act(ibf[b][:, :], ps[:, 1024:2048], AF.Identity)
            C3 = C[b][:, :].rearrange("p (c t) -> p c t", t=CH)
            S.add("dve", lambda e, C3=C3, h=h: e.reduce_sum(out=Dd[:, h:h + 1], in_=C3[:, :, CH - 1:CH],
                                                           axis=mybir.AxisListType.XY),
                  reads=[C[b][:, :]], writes=[Dd[:, h:h + 1]])
            S.act(Dd[:, h:h + 1], Dd[:, h:h + 1], AF.Exp)
            if stage >= 2:
                hgrn_transposes(S, cst, psb, kend[b][:, :], kt[b][0][:, :, :], kt[b][1][:, :, :])
                hgrn_transposes(S, cst, psb, ibf[b][:, :], vtk[b][:, :, :])
            S.memset("dve", Sf[b][0][:, :], 0.0)
            fin = 0
            if stage >= 3:
                fin = hgrn_state_scan(S, ps, kt[b], vtk[b], dec[b], Sf[b])
            S.dma("sp", U_d[h, :, :], Sf[b][fin][:, :], cho[b])
        chd = S.new_chan()
        S.dma("sp", D_d, Dd[:, :], chd)
        S.emit(nc, es, final_chans=cho + [chd])
    return nc, S


def regroup_w_in(od_w_in):
    w = np.asarray(od_w_in, np.float32).reshape(D, 4, NH, 128)
    w = w[:, [0, 3, 1, 2]]
    return np.ascontiguousarray(w.transpose(0, 2, 1, 3).reshape(D, 4 * D))


def hgrn_const_inputs(lb_param):
    lbp = np.ascontiguousarray(np.stack([_pm(lb_param[0], NH), _pm(lb_param[1], NH)], axis=1).reshape(128, 2 * NH))
    i = np.arange(128)
    mask2 = ((i[None, :] >= i[:, None]) & ((i[None, :] // CH) == (i[:, None] // CH))).astype(np.float32)
    return {"lbp": lbp, "mask2": mask2, "ident": np.eye(128, dtype=np.float32)}


def _carve(arena, off_bytes, n_elems, dt):
    assert off_bytes % 4 == 0
    nb = n_elems * _dsize(dt)
    assert nb % 4 == 0
    v = arena[:, off_bytes // 4:(off_bytes + nb) // 4]
    return v if dt == F32 else v.bitcast(dt)


def build_hgrn_main_launch():
    nc = bass.Bass("TRN2", target_bir_lowering=False)
    xT = nc.dram_tensor("xT", [D, T], F32, kind="ExternalInput").ap()
    w_in = nc.dram_tensor("w_in", [D, 4 * D], F32, kind="ExternalInput").ap()
    w_out = nc.dram_tensor("w_out", [D, D], F32, kind="ExternalInput").ap()
    lbp_d = nc.dram_tensor("lbp", [128, 2 * NH], F32, kind="ExternalInput").ap()
    mask2_d = nc.dram_tensor("mask2", [128, 128], F32, kind="ExternalInput").ap()
    ident_d = nc.dram_tensor("ident", [128, 128], F32, kind="ExternalInput").ap()
    up_d = nc.dram_tensor("Uprev", [3, NH, 128, 128], F32, kind="ExternalInput").ap()
    dp_d = nc.dram_tensor("Dprev", [128, 3 * NH], F32, kind="ExternalInput").ap()
    gn_d = nc.dram_tensor("gn", [128, NH], F32, kind="ExternalInput").ap()
    gb_d = nc.dram_tensor("ln1gb", [128, 32], F32, kind="ExternalInput").ap()
    x1T = nc.dram_tensor("x1T", [D, T], F32, kind="ExternalOutput").ap()
    S = Sched()
    with ExitStack() as es:
        arena = es.enter_context(nc.sbuf_tensor("arena", [128, NCH * T], F32))
        zf = arena[:, :].rearrange("p (c t) -> p c t", c=NCH, t=T)
        xb = _carve(arena, 0, NCH * T, BF16).rearrange("p (c t) -> p c t", c=NCH, t=T)
        off = NCH * T * 2
        A = _carve(arena, off, T, F32); off += 4 * T
        B = _carve(arena, off, T, F32); off += 4 * T
        C = _carve(arena, off, T, F32); off += 4 * T
        kend = _carve(arena, off, T, BF16); off += 2 * T
        kdec = _carve(arena, off, T, BF16); off += 2 * T
        qdec = _carve(arena, off, T, BF16); off += 2 * T
        ibf = _carve(arena, off, T, BF16); off += 2 * T
        sg = _carve(arena, off, T, BF16); off += 2 * T
        osq = _carve(arena, off, T, BF16); off += 2 * T
        attm = _carve(arena, off, T, BF16).rearrange("p (j t) -> p j t", j=8); off += 2 * T
        kt0 = _carve(arena, off, T, BF16).rearrange("p (j t) -> p j t", j=8); off += 2 * T
        kt1 = _carve(arena, off, T, BF16).rearrange("p (j t) -> p j t", j=8); off += 2 * T
        kt = (kt0, kt1)
        vtk = _carve(arena, off, T, BF16).rearrange("p (j t) -> p j t", j=8); off += 2 * T
        assert off <= NCH * T * 4
        y = es.enter_context(nc.sbuf_tensor("y", [128, NH, T], BF16))
        Sb = es.enter_context(nc.sbuf_tensor("Sb", [128, NCK, 128], BF16))
        Sf = [es.enter_context(nc.sbuf_tensor("Sf%d" % k, [128, 128], F32)) for k in range(2)]
        upst = es.enter_context(nc.sbuf_tensor("upst", [128, 3, 128], F32))
        dec = es.enter_context(nc.sbuf_tensor("dec", [128, NCK], F32))
        dp = es.enter_context(nc.sbuf_tensor("dp", [128, 3, NH], F32))
        gn = es.enter_context(nc.sbuf_tensor("gn_s", [128, NH], F32))
        gb = es.enter_context(nc.sbuf_tensor("gb_s", [128, NCH, 2], F32))
        ones = es.enter_context(nc.sbuf_tensor("ones", [128, 128], BF16))
        eps = es.enter_context(nc.sbuf_tensor("eps", [128, 1], F32))
        xs = [es.enter_context(nc.sbuf_tensor("xs%d" % i, [128, T], F32)) for i in range(2)]
        tmps = {"eps": eps,
                "zb": [es.enter_context(nc.sbuf_tensor("zb%d" % i, [128, T], BF16)) for i in range(2)],
                "zs": [es.enter_context(nc.sbuf_tensor("zs%d" % i, [128, T], BF16)) for i in range(2)],
                "mean": xs[0], "rstd": xs[1]}
        ps = es.enter_context(nc.psum_tensor("ps", [128, 4096], F32))
        psb = ps[:, :].bitcast(BF16)
        chp = S.new_chan(total=True)
        chx = S.new_chan(total=True)
        cst = hgrn_consts(S, nc, es, lbp_d, mask2_d, ident_d, chp)
        S.dma("sp", dp[:, :, :], dp_d.rearrange("p (j h) -> p j h", j=3), chp)
        S.dma("sp", gn[:, :], gn_d, chp)
        S.dma("sp", gb[:, :, :], gb_d.rearrange("p (c j) -> p c j", j=2), chp)
        S.memset("dve", ones[:, :], 1.0)
        S.memset("dve", eps[:, :], LN_EPS)
        ws = WeightStream(S, nc, es, 3, 16 * 512)
        wi = w_in.rearrange("(kc p) n -> p kc n", p=128)
        wo = w_out.rearrange("(kc p) n -> p kc n", p=128)
        uses = [ws.plan((16, 512), wi[:, :, h * 512:(h + 1) * 512]) for h in range(NH)]
        u_o = [ws.plan((16, 512), wo[:, :, i * 512:(i + 1) * 512]) for i in range(4)]
        for c in range(NCH):
            S.dma("pool", xb[:, c, :], xT[c * 128:(c + 1) * 128, :], chx)
        ws.release(-1)
        chu = S.new_chan()
        G0, G1 = 0, 1024
        for h in range(NH):
            wt = ws.get(uses[h])

            def proj(blk, g0):
                for kc in range(NCH):
                    for t_ in range(2):
                        S.mm(ps[:, g0 + t_ * 512:g0 + (t_ + 1) * 512], wt[:, kc, blk * 128:(blk + 1) * 128],
                             xb[:, kc, t_ * 512:(t_ + 1) * 512], start=(kc == 0), stop=(kc == NCH - 1))
            S.dma("sp", upst[:, :, :], up_d[:, h, :, :].rearrange("j d e -> d j e"), chu)
            S.memset("dve", Sf[0][:, :], 0.0)
            cur = 0
            for j in range(3):
                S.stt(Sf[1 - cur][:, :], Sf[cur][:, :], dp[:, j, h:h + 1], upst[:, j, :], ALU.mult, ALU.add)
                cur = 1 - cur
            Sfl = [Sf[cur], Sf[1 - cur]]
            proj(2, G0)
            proj(3, G1)
            hgrn_gates(S, h, cst, ps[:, G0:G0 + T], A, B, C, kend, dec[:, :], kdec_bf=kdec, eC=True)
            S.act(ibf, ps[:, G1:G1 + T], AF.Identity)
            proj(0, G0)
            proj(1, G1)
            ws.release(uses[h])
            S.act(B, ps[:, G0:G0 + T], AF.Silu)
            S.tt("dve", qdec, B, C, ALU.mult)
            S.act(sg, ps[:, G1:G1 + T], AF.Sigmoid)
            hgrn_transposes(S, cst, psb, kend, kt0, kt1)
            hgrn_transposes(S, cst, psb, ibf, vtk)
            hgrn_state_scan(S, ps, kt, vtk, dec, Sfl, Sb=Sb)
            for half in range(2):
                for jj in range(4):
                    j = half * 4 + jj
                    S.mm(ps[:, 2560 + jj * 128:2560 + (jj + 1) * 128], kdec[:, j * 128:(j + 1) * 128],
                         qdec[:, j * 128:(j + 1) * 128], start=True, stop=True)
                for jj in range(4):
                    j = half * 4 + jj
                    S.tt("dve", attm[:, j, :], ps[:, 2560 + jj * 128:2560 + (jj + 1) * 128], cst["mask2"][:, :],
                         ALU.mult)
            O0 = 3072
            for j in range(8):
                S.mm(ps[:, O0 + j * 128:O0 + (j + 1) * 128], vtk[:, j, :], attm[:, j, :], start=True, stop=False)
                S.mm(ps[:, O0 + j * 128:O0 + j * 128 + 64], Sb[:, 2 * j, :], qdec[:, j * 128:j * 128 + 64],
                     start=False, stop=False)
                S.mm(ps[:, O0 + j * 128 + 64:O0 + (j + 1) * 128], Sb[:, 2 * j + 1, :],
                     qdec[:, j * 128 + 64:(j + 1) * 128], start=False, stop=True)
            S.act(osq, ps[:, O0:O0 + T], AF.Square)
            for t_ in range(2):
                S.mm(ps[:, G0 + t_ * 512:G0 + (t_ + 1) * 512], ones[:, :], osq[:, t_ * 512:(t_ + 1) * 512],
                     start=True, stop=True)
            S.act(A, ps[:, G0:G0 + T], AF.Ln, bias=eps[:, 0:1], scale=1.0 / 128)
            S.act(A, A, AF.Exp, scale=-0.5)
            S.stt(C, ps[:, O0:O0 + T], gn[:, h:h + 1], A, ALU.mult, ALU.mult)
            S.tt("dve", y[:, h, :], C, sg, ALU.mult)
        chs = [S.new_chan(), S.new_chan()]
        for i in range(4):
            wt = ws.get(u_o[i])
            for sub in range(4):
                oc = i * 4 + sub
                g0 = (oc % 2) * 1024
                xst = xs[oc % 2]
                S.dma("sp", xst[:, :], xT[oc * 128:(oc + 1) * 128, :], chs[oc % 2])
                for kc in range(NCH):
                    for t_ in range(2):
                        S.mm(ps[:, g0 + t_ * 512:g0 + (t_ + 1) * 512], wt[:, kc, sub * 128:(sub + 1) * 128],
                             y[:, kc, t_ * 512:(t_ + 1) * 512], start=(kc == 0), stop=(kc == NCH - 1))
                S.stt(zf[:, oc, :], xst[:, :], ALPHA, ps[:, g0:g0 + T], ALU.mult, ALU.add)
            ws.release(u_o[i])
        emit_ln(S, zf, 0, T, gb, ones, tmps, ps, [lambda c: zf[:, c, :]])
        cho = [S.new_chan() for _ in range(4)]
        for c in range(NCH):
            S.dma("sp", x1T[c * 128:(c + 1) * 128, :], zf[:, c, :], cho[c % 4])
        S.emit(nc, es, final_chans=cho)
    return nc, S


def _run(nc, in_maps):
    return run_bass_kernel_spmd(nc, in_maps, core_ids=list(range(NCORES))).results


def kernel_unfused(x, ev_w_in, ev_ln_v_g, ev_ln_v_b, ev_w_s, ev_b_s, ev_w_pool, ev_pool_scale,
           ev_w_out, od_w_in, od_norm_g, od_w_out, lb_param, ffn_w_up, ffn_conv_w,
           ffn_conv_b, ffn_w_down, ln1_g, ln1_b, ln2_g, ln2_b):
    f32 = np.float32
    nc0, _ = build_mix0_launch()
    maps0 = prep_mix0_inputs(x, ev_w_in, ev_ln_v_g, ev_ln_v_b, ev_w_s, ev_b_s, ev_w_pool, ev_pool_scale,
                             ev_w_out, ln1_g, ln1_b)
    r0 = _run(nc0, maps0)
    x1T = [r0[c]["x1T"] for c in range(NCORES)]
    ncf, _ = build_ffn_launch()

    def ffn(l, xTs):
        cwb, gb = prep_ffn_params(l, ffn_conv_w, ffn_conv_b, ln2_g, ln2_b)
        wu = np.ascontiguousarray(ffn_w_up[l], f32)
        wd = np.ascontiguousarray(ffn_w_down[l], f32)
        maps = [{"xT": np.ascontiguousarray(xTs[c], f32), "w_up": wu, "w_down": wd, "cwb": cwb, "ln2gb": gb}
                for c in range(NCORES)]
        r = _run(ncf, maps)
        return [r[c]["yT"] for c in range(NCORES)]

    x2T = ffn(0, x1T)
    ncp, _ = build_hgrn_pre_launch()
    w_in_r = regroup_w_in(od_w_in[0])
    hc = hgrn_const_inputs(lb_param)
    mapsp = []
    for c in range(NCORES):
        m = {"xT": np.ascontiguousarray(x2T[c], f32), "w_in": w_in_r}
        m.update(hc)
        mapsp.append(m)
    rp = _run(ncp, mapsp)
    ncm, _ = build_hgrn_main_launch()
    gn = _pm(od_norm_g[0], NH)
    gb1 = np.ascontiguousarray(np.stack([_pm(ln1_g[1], 16), _pm(ln1_b[1], 16)], axis=-1).reshape(128, 32))
    w_out1 = np.ascontiguousarray(od_w_out[0], f32)
    mapsm = []
    for c in range(NCORES):
        b, s = divmod(c, 4)
        up = np.zeros((3, NH, 128, 128), f32)
        dp = np.zeros((128, 3, NH), f32)
        for j in range(s):
            pos = 3 - s + j
            up[pos] = rp[b * 4 + j]["U"]
            dp[:, pos, :] = rp[b * 4 + j]["Dd"]
        m = {"xT": np.ascontiguousarray(x2T[c], f32), "w_in": w_in_r, "w_out": w_out1, "Uprev": up,
             "Dprev": np.ascontiguousarray(dp.reshape(128, 3 * NH)), "gn": gn, "ln1gb": gb1}
        m.update(hc)
        mapsm.append(m)
    rm = _run(ncm, mapsm)
    x1bT = []
    for c in range(NCORES):
        b, s = divmod(c, 4)
        xt = np.zeros((D, T + 2), f32)
        xt[:, 2:] = rm[c]["x1T"]
        if s > 0:
            xt[:, 0:2] = rm[c - 1]["x1T"][:, T - 2:T]
        x1bT.append(xt)
    outT = ffn(1, x1bT)
    out = np.zeros((2, 4 * T, D), f32)
    for c in range(NCORES):
        b, s = divmod(c, 4)
        out[b, s * T:(s + 1) * T] = outT[c].T
    return out


R0 = 0
R0_SZ = NCH * (T + 2) * 4
R1 = R0 + R0_SZ
R1_SZ = NCH * (T + 2) * 2
R2 = R1 + R1_SZ
R2_SZ = 66560
AR_BYTES = R2 + R2_SZ
SEQ_GROUPS = [[0, 1, 2, 3], [4, 5, 6, 7]]


def build_fused(use_cc=True):
    nc = bass.Bass("TRN2", target_bir_lowering=False)

    def din(name, shape):
        return nc.dram_tensor(name, shape, F32, kind="ExternalInput").ap()
    x0T = din("x0T", [D, TH])
    ev_w_in = din("ev_w_in", [D, 3072])
    ev_w_out = din("ev_w_out", [D, D])
    wsT_d = din("wsT", [128, 8 * 128])
    mask_d = din("maskA", [128, 128])
    bsT_d = din("bsT", [128, 8 * 128])
    lnv_d = din("lnv", [128, 2 * 1024])
    wp_d = din("w_pool", [4 * 256, 256])
    psc_d = din("pscale", [128, 8])
    rc_d = din("rcnt", [128, 64])
    flag_d = din("flag", [128, 1])
    oh_d = din("oh", [128, 8])
    ln1gb_d = din("ln1gb", [128, 64])
    ln2gb_d = din("ln2gb", [128, 64])
    cwb_d = din("cwb", [128, 2 * 88 * 4])
    w_up = [din("w_up%d" % l, [D, 2 * DFF]) for l in range(2)]
    w_down = [din("w_down%d" % l, [DFF, D]) for l in range(2)]
    od_w_in = din("od_w_in", [D, 4 * D])
    od_w_out = din("od_w_out", [D, D])
    lbp_d = din("lbp", [128, 2 * NH])
    mask2_d = din("mask2", [128, 128])
    ident_d = din("ident", [128, 128])
    gn_d = din("gn", [128, NH])
    outT = nc.dram_tensor("outT", [D, T], F32, kind="ExternalOutput").ap()
    xsp = nc.dram_tensor("xsp", [D, T], F32).ap()
    ccin = [nc.dram_tensor("ccin%d" % g, [4 * 4 * 128, 129], F32) for g in range(4)]
    ccout = [nc.dram_tensor("ccout%d" % g, [4 * 4 * 128, 129], F32) for g in range(4)]
    cch_in = nc.dram_tensor("cch_in", [4 * 128, 32], F32)
    cch_out = nc.dram_tensor("cch_out", [4 * 128, 32], F32)

    S = Sched()
    with ExitStack() as es:
        AR = es.enter_context(nc.sbuf_tensor("AR", [128, AR_BYTES // 4], F32))

        def cv(off, shape, dt):
            n = 1
            for k in shape:
                n *= k
            v = _carve(AR, off, n, dt)
            if len(shape) == 2:
                v = v.rearrange("p (a b) -> p a b", a=shape[0], b=shape[1])
            return v

        def sb(name, shape, dt=F32):
            return es.enter_context(nc.sbuf_tensor(name, shape, dt))
        psc = sb("psc_s", [128, 8]); rc = sb("rc_s", [128, 4, 16]); flag = sb("flag_s", [128, 1])
        oh = sb("oh_s", [128, 8]); ln1gb = sb("ln1gb_s", [128, 2, NCH, 2]); ln2gb = sb("ln2gb_s", [128, 2, NCH, 2])
        cwb = sb("cwb_s", [128, 2, 88, 4]); ones = sb("ones", [128, 128], BF16); eps = sb("eps", [128, 1])
        st = sb("st", [128, 8]); small = sb("small", [128, 32]); gn = sb("gn_s", [128, NH])
        Dd = sb("Dd_s", [128, NH]); tiny = sb("tiny", [128, 8])
        hstg = sb("hstg", [128, 4, 32]); hld = sb("hld", [128, 4, 32]); hsum = sb("hsum", [128, 32])
        ps = es.enter_context(nc.psum_tensor("ps", [128, 4096], F32))
        psb = ps[:, :].bitcast(BF16)
        ccsem = [es.enter_context(nc.semaphore("ccs%d" % i)) for i in range(5)]
        ws = WeightStream(S, nc, es, 4, 16 * 256)

        wi0 = ev_w_in.rearrange("(kc p) n -> p kc n", p=128)
        wo0 = ev_w_out.rearrange("(kc p) n -> p kc n", p=128)
        u_xb = [ws.plan((16, 256), wi0[:, :, 2048 + i * 256:2048 + (i + 1) * 256]) for i in range(4)]
        u_u = [ws.plan((16, 256), wi0[:, :, i * 256:(i + 1) * 256]) for i in range(4)]
        u_v = [ws.plan((16, 256), wi0[:, :, 1024 + i * 256:1024 + (i + 1) * 256]) for i in range(4)]
        u_o = [ws.plan((16, 256), wo0[:, :, i * 256:(i + 1) * 256]) for i in range(8)]
        plan0 = plan_ffn_weights(ws, w_up[0], w_down[0], split_last=True)
        wi1 = od_w_in.rearrange("(kc p) n -> p kc n", p=128)
        wo1 = od_w_out.rearrange("(kc p) n -> p kc n", p=128)
        u_pre = [ws.plan((16, 256), wi1[:, :, h * 512 + 256:h * 512 + 512]) for h in range(NH)]
        u_main = []
        for h in range(NH):
            fi = ws.plan((16, 256), wi1[:, :, h * 512 + 256:h * 512 + 512])
            qg = ws.plan((16, 256), wi1[:, :, h * 512:h * 512 + 256])
            u_main.append((fi, qg))
        u_o1 = [ws.plan((16, 256), wo1[:, :, (i % 8) * 256:((i % 8) + 1) * 256]) for i in range(16)]
        plan1 = plan_ffn_weights(ws, w_up[1], w_down[1], split_last=True)

        x0b = cv(R0, (NCH, TH), BF16)
        zf = cv(R0, (NCH, T + 2), F32)
        xb = cv(R1, (NCH, T + 2), BF16)
        pp = cv(R1, (8, TH), BF16)
        g1 = [cv(R1 + 18432, (TH,), F32), cv(R1 + 23040, (TH,), F32)]
        xbf = cv(R1 + 27648, (16 + TH,), F32)
        u = cv(R2, (8, TH), BF16)
        vt = cv(R2 + 18432, (9, 1024), BF16)
        g2p = [cv(R2 + 36864, (16 + TH,), F32), cv(R2 + 41536, (16 + TH,), F32)]
        g2 = [t[:, 16:16 + TH] for t in g2p]
        tA, tB = g2p
        wsT = cv(R2 + 46208, (8, 128), BF16)
        mask = cv(R2 + 48256, (128,), F32)
        bsT = cv(R2 + 48768, (8, 128), F32)
        lnv = cv(R2 + 52864, (2, 1024), F32)
        wp = cv(R2 + 61056, (8, 256), BF16)
        wsTf = g1[1][:, 0:1024].rearrange("p (h t) -> p h t", h=8)
        hb = [cv(R0 + 36864 + k * 4608, (TH,), F32) for k in range(2)]

        chp = S.new_chan(total=True)
        chx = S.new_chan(total=True)
        S.dma("sp", wsTf, wsT_d.rearrange("p (h t) -> p h t", h=8), chp)
        S.dma("sp", mask, mask_d, chp)
        S.dma("sp", bsT, bsT_d.rearrange("p (h t) -> p h t", h=8), chp)
        S.dma("sp", lnv, lnv_d.rearrange("p (a c) -> p a c", a=2), chp)
        S.dma("sp", psc[:, :], psc_d, chp)
        S.dma("sp", rc[:, :, :], rc_d.rearrange("p (g j) -> p g j", g=4), chp)
        S.dma("sp", flag[:, :], flag_d, chp)
        S.dma("sp", oh[:, :], oh_d, chp)
        S.dma("sp", ln1gb[:, :, :, :], ln1gb_d.rearrange("p (l c j) -> p l c j", l=2, j=2), chp)
        S.dma("sp", ln2gb[:, :, :, :], ln2gb_d.rearrange("p (l c j) -> p l c j", l=2, j=2), chp)
        S.dma("sp", cwb[:, :, :, :], cwb_d.rearrange("p (l c j) -> p l c j", l=2, j=4), chp)
        S.dma("sp", gn[:, :], gn_d, chp)
        S.dma("pool", wp, wp_d.rearrange("(a p) n -> p a n", p=128), chx)
        for c in range(NCH):
            S.dma("pool", x0b[:, c, :], x0T[c * 128:(c + 1) * 128, :], chx)
        ws.release(-1)
        S.memset("dve", ones[:, :], 1.0)
        S.memset("dve", eps[:, :], LN_EPS)
        S.memset("dve", xbf[:, 0:16], 0.0)
        S.memset("dve", tA[:, 0:16], 0.0)
        S.memset("dve", tB[:, 0:16], 0.0)
        for h in range(8):
            S.tt("dve", wsT[:, h, :], wsTf[:, h, :], mask, ALU.mult)

        GR = (0, 1536)
        TT3 = ((0, 512), (512, 512), (1024, 128))

        def proj_fm(wt, sub, g0):
            for kc in range(NCH):
                for (t0, w) in TT3:
                    S.mm(ps[:, g0 + t0:g0 + t0 + w], wt[:, kc, sub * 128:(sub + 1) * 128], x0b[:, kc, t0:t0 + w],
                         start=(kc == 0), stop=(kc == NCH - 1))
        gi = 0
        for i in range(4):
            wt = ws.get(u_xb[i])
            for sub in range(2):
                c = i * 2 + sub
                g = c // 2
                g0 = GR[gi % 2]
                gi += 1
                proj_fm(wt, sub, g0)
                S.act(xbf[:, 16:16 + TH], ps[:, g0:g0 + TH], AF.Identity)
                src = xbf
                dsts = [tA, tB]
                for k in range(g + 1):
                    sh = 1 << k
                    dst = dsts[k % 2]
                    S.tt("dve", dst[:, 16:16 + TH], src[:, 16:16 + TH], src[:, 16 - sh:16 + TH - sh], ALU.add)
                    src = dst
                win = B_WINDOWS[g]
                S.stt(pp[:, c, :], src[:, 16:16 + TH], 1.0 / win, xbf[:, 16:16 + TH], ALU.mult, ALU.subtract)
                S.tt("dve", small[:, 0:16], src[:, 16 + 128:16 + 144], rc[:, g, :], ALU.mult)
                S.tt("dve", pp[:, c, 128:144], small[:, 0:16], xbf[:, 16 + 128:16 + 144], ALU.subtract)
            ws.release(u_xb[i])
        for i in range(4):
            wt = ws.get(u_u[i])
            for sub in range(2):
                c = i * 2 + sub
                g0 = GR[gi % 2]
                proj_fm(wt, sub, g0)
                S.act(hb[gi % 2], ps[:, g0:g0 + TH], AF.Identity)
                emit_gelu(S, u[:, c, :], hb[gi % 2], g1[gi % 2], g2[gi % 2])
                gi += 1
            ws.release(u_u[i])
        def gating(tk):
            for half in range(2):
                r0 = 2048 + half * 512
                for hh in range(4):
                    h = half * 4 + hh
                    S.mm(ps[:, r0 + hh * 128:r0 + (hh + 1) * 128], vt[:, tk, h * 128:(h + 1) * 128], wsT[:, h, :],
                         start=True, stop=True)
                tmp = xbf[:, 16 + half * 512:16 + (half + 1) * 512].rearrange("p (h t) -> p h t", h=4)
                S.tt("dve", tmp, ps[:, r0:r0 + 512].rearrange("p (h t) -> p h t", h=4),
                     bsT[:, half * 4:half * 4 + 4, :], ALU.add)
                uu = u[:, half * 4:half * 4 + 4, tk * 128:(tk + 1) * 128]
                S.tt("dve", uu, tmp, uu, ALU.mult)
        wv = [ws.get(k) for k in u_v]
        pend = None
        for tk in range(9):
            vr = g1[tk % 2]
            for cg in range(4):
                r0 = 3072 + ((tk * 4 + cg) % 2) * 512
                for kc in range(NCH):
                    S.mm(ps[:, r0:r0 + 256], x0b[:, kc, tk * 128:(tk + 1) * 128], wv[cg][:, kc, :],
                         start=(kc == 0), stop=(kc == NCH - 1))
                hv = hb[tk % 2][:, cg * 256:(cg + 1) * 256]
                t1v = g2[0][:, cg * 256:(cg + 1) * 256]
                t2v = g2[1][:, cg * 256:(cg + 1) * 256]
                S.act(hv, ps[:, r0:r0 + 256], AF.Identity)
                S.act(t1v, hv, AF.Square)
                S.ts("dve", t1v, t1v, GELU_C, 1.0, ALU.mult, ALU.add)
                S.tt("dve", t1v, t1v, hv, ALU.mult)
                if pend is not None:
                    pend()

                def pend(t1v=t1v, t2v=t2v, hv=hv, dstv=vr[:, cg * 256:(cg + 1) * 256]):
                    S.act(t2v, t1v, AF.Sigmoid, scale=GELU_S)
                    S.tt("dve", dstv, t2v, hv, ALU.mult)
            pend()
            pend = None
            if tk > 0:
                gating(tk - 1)
            sq = g2[0]
            S.add("dve", lambda e, vr=vr: e.reduce_sum(out=st[:, 0:1], in_=vr[:, 0:1024], axis=mybir.AxisListType.X),
                  reads=[vr[:, 0:1024]], writes=[st[:, 0:1]])
            S.act(sq[:, 0:1024], vr[:, 0:1024], AF.Square)
            S.add("dve", lambda e, sq=sq: e.reduce_sum(out=st[:, 1:2], in_=sq[:, 0:1024], axis=mybir.AxisListType.X),
                  reads=[sq[:, 0:1024]], writes=[st[:, 1:2]])
            S.ts("dve", st[:, 2:3], st[:, 0:1], 1.0 / 1024, None, ALU.mult)
            S.tt("dve", st[:, 3:4], st[:, 2:3], st[:, 2:3], ALU.mult)
            S.stt(st[:, 4:5], st[:, 1:2], 1.0 / 1024, st[:, 3:4], ALU.mult, ALU.subtract)
            S.act(st[:, 5:6], st[:, 4:5], AF.Sqrt, bias=eps[:, 0:1], scale=1.0)
            S.add("dve", lambda e: e.reciprocal(out=st[:, 6:7], in_=st[:, 5:6]), reads=[st[:, 5:6]],
                  writes=[st[:, 6:7]])
            S.ts("dve", vr[:, 0:1024], vr[:, 0:1024], st[:, 2:3], st[:, 6:7], ALU.subtract, ALU.mult)
            S.tt("dve", vr[:, 0:1024], vr[:, 0:1024], lnv[:, 0, :], ALU.mult)
            S.tt("dve", vt[:, tk, :], vr[:, 0:1024], lnv[:, 1, :], ALU.add)
        ws.release(u_v[3])
        gating(8)
        for g in range(4):
            for oc in range(2):
                g0 = GR[oc]
                for kc in range(2):
                    for (t0, w) in TT3:
                        S.mm(ps[:, g0 + t0:g0 + t0 + w], wp[:, g * 2 + kc, oc * 128:(oc + 1) * 128],
                             pp[:, g * 2 + kc, t0:t0 + w], start=(kc == 0), stop=(kc == 1))
            for oc in range(2):
                g0 = GR[oc]
                c = g * 2 + oc
                S.act(pp[:, c, :], ps[:, g0:g0 + TH], AF.Identity, scale=psc[:, c:c + 1])
        xs = [g1[0], g1[1]]
        chs = [S.new_chan(), S.new_chan()]
        for i in range(8):
            wt = ws.get(u_o[i])
            for sub in range(2):
                oc = i * 2 + sub
                g0 = GR[oc % 2]
                xst = xs[oc % 2]
                S.dma("sp", xst[:, 0:T + 2], x0T[oc * 128:(oc + 1) * 128, 126:TH], chs[oc % 2])
                for kc in range(NCH):
                    src = u[:, kc, :] if kc < 8 else pp[:, kc - 8, :]
                    lw = wt[:, kc, sub * 128:(sub + 1) * 128]
                    S.mm(ps[:, g0 + 510:g0 + 512], lw, src[:, 126:128], start=(kc == 0), stop=(kc == NCH - 1))
                    S.mm(ps[:, g0 + 512:g0 + 1024], lw, src[:, 128:640], start=(kc == 0), stop=(kc == NCH - 1))
                    S.mm(ps[:, g0 + 1024:g0 + 1536], lw, src[:, 640:1152], start=(kc == 0), stop=(kc == NCH - 1))
                S.stt(zf[:, oc, :], xst[:, 0:T + 2], ALPHA, ps[:, g0 + 510:g0 + 1536], ALU.mult, ALU.add)
            ws.release(u_o[i])
        tm_ln1 = {"eps": eps, "mean": g2[0], "rstd": g2[1],
                  "zb": [cv(R2 + k * 2052, (T + 2,), BF16) for k in range(2)],
                  "zs": [cv(R2 + (2 + k) * 2052, (T + 2,), BF16) for k in range(2)]}
        def ln1_post(c):
            S.ts("dve", zf[:, c, 0:2], zf[:, c, 0:2], flag[:, 0:1], None, ALU.mult)
            S.act(xb[:, c, :], zf[:, c, :], AF.Identity)
        emit_ln(S, zf, 0, T + 2, ln1gb[:, 0, :, :], ones, tm_ln1, ps, [lambda c: zf[:, c, :]], post=ln1_post)

        gq = cv(R2, (12, T), BF16)
        ft = [cv(R2 + 24576 + k * 4096, (T,), F32) for k in range(6)]
        tm_ffn = {"a": ft[0:2], "v": ft[2:4], "s": ft[4:6], "eps": eps, "mean": ft[0], "rstd": ft[1],
                  "zb": [cv(R2 + 49152 + k * 2048, (T,), BF16) for k in range(2)],
                  "zs": [cv(R2 + 53248 + k * 2048, (T,), BF16) for k in range(2)]}
        xb2 = cv(R1, (NCH, T), BF16)
        def ln2_l0(tt_):
            c0 = 2 + tt_ * 512
            emit_ln(S, zf, c0, 512, ln2gb[:, 0, :, :], ones, tm_ffn, ps,
                    [lambda c: xb2[:, c, tt_ * 512:(tt_ + 1) * 512], lambda c: zf[:, c, c0:c0 + 512]])
        emit_ffn(S, ws, plan0, zf, xb, cwb[:, 0, :, :], gq, tm_ffn, ps, ln_cb=ln2_l0)
        chsp = [S.new_chan() for _ in range(NCH)]
        for c in range(NCH):
            S.dma("sp", xsp[c * 128:(c + 1) * 128, :], zf[:, c, 2:T + 2], chsp[c])

        def mkset(k):
            o0 = R0 + k * 32768
            d_ = {"A": cv(o0, (T,), F32), "B": cv(o0 + 4096, (T,), F32), "C": cv(o0 + 8192, (T,), F32),
                  "kend": cv(o0 + 12288, (T,), BF16), "kdec": cv(o0 + 14336, (T,), BF16),
                  "qdec": cv(o0 + 16384, (T,), BF16), "ibf": cv(o0 + 18432, (T,), BF16),
                  "sg": cv(o0 + 20480, (T,), BF16), "osq": cv(o0 + 22528, (T,), BF16),
                  "attm": cv(o0 + 24576, (8, 128), BF16), "kt0": cv(o0 + 26624, (8, 128), BF16),
                  "kt1": cv(o0 + 28672, (8, 128), BF16), "vtk": cv(o0 + 30720, (8, 128), BF16),
                  "Sb": cv(R2 + 32768, (NCK, 128), BF16) if k == 0 else cv(R2 + 61472, (NCK, 128), BF16),
                  "dec": sb("dec%d" % k, [128, NCK]), "Sf": [sb("Sf%d_%d" % (k, i), [128, 128]) for i in range(2)],
                  "Pp": [sb("Pp%d_%d" % (k, i), [128, 128]) for i in range(2)],
                  "upst": cv(R2 + 59408, (4, 129), F32) if k == 0 else sb("upst1", [128, 4, 129]),
                  "stg": cv(R2 + 57344, (4, 129), F32),
                  "chu": S.new_chan(), "chst": S.new_chan()}
            return d_
        sets = [mkset(0), mkset(1)]
        y = cv(R2, (NH, T), BF16)
        xs1 = [cv(R2 + 40960, (T,), F32), cv(R2 + 45056, (T,), F32)]
        tm_ln1b = {"eps": eps, "mean": xs1[0], "rstd": xs1[1],
                   "zb": [cv(R2 + 49152 + k * 2048, (T,), BF16) for k in range(2)],
                   "zs": [cv(R2 + 53248 + k * 2048, (T,), BF16) for k in range(2)]}
        chc = S.new_chan(total=True)
        cst = {}
        lbp = sb("lbp_s", [128, 2, NH]); cst["lb"] = sb("lb", [128, NH]); cst["oml"] = sb("oml", [128, NH])
        cst["mask2"] = sb("mask2_s", [128, 128]); identf = sb("identf", [128, 128]); cst["ident"] = sb("ident_s", [128, 128], BF16)
        cst["pm"] = sb("pm", [128, 2])
        cst["one"] = sb("one_c", [128, 1])
        S.memset("dve", cst["one"][:, :], 1.0)
        cst["rm"] = cv(R2 + 36864, (T,), F32)
        S.dma("sp", lbp[:, :, :], lbp_d.rearrange("p (l h) -> p l h", l=2), chc)
        S.dma("sp", cst["mask2"][:, :], mask2_d, chc)
        S.dma("sp", identf[:, :], ident_d, chc)
        S.copy("dve", cst["ident"][:, :], identf[:, :])
        S.tt("dve", cst["lb"][:, :], lbp[:, 1, :], lbp[:, 0, :], ALU.subtract)
        S.act(cst["lb"][:, :], cst["lb"][:, :], AF.Sigmoid)
        S.ts("dve", cst["oml"][:, :], cst["lb"][:, :], -1.0, 1.0, ALU.mult, ALU.add)
        S.memset("dve", cst["rm"], 1.0)
        S.memset("dve", cst["rm"].rearrange("p (c t) -> p c t", t=CH)[:, :, 0:1], 0.0)
        S.memset("dve", cst["pm"][:, :], 0.0)
        S.memset("dve", cst["pm"][0:64, 0:1], 1.0)
        S.memset("dve", cst["pm"][64:128, 1:2], 1.0)
        oh3 = oh[:, 0:4].rearrange("p (j o) -> p j o", o=1)
        G0, G1, PB5, O0 = 0, 1024, 2560, 3072

        def proj1(wt, blk, g0):
            for kc in range(NCH):
                for t_ in range(2):
                    S.mm(ps[:, g0 + t_ * 512:g0 + (t_ + 1) * 512], wt[:, kc, blk * 128:(blk + 1) * 128],
                         xb2[:, kc, t_ * 512:(t_ + 1) * 512], start=(kc == 0), stop=(kc == NCH - 1))

        def cc_op(idx, src_t, dst_t):
            if use_cc:
                def fn(e):
                    e.collective_compute("AllReduce", ALU.add, replica_groups=SEQ_GROUPS,
                                         ins=[src_t.ap().opt()], outs=[dst_t.ap().opt()]).then_inc(ccsem[idx])
                    return None
                S.add("pool", fn, reads=[src_t.ap()], writes=[])

                def fn2(e):
                    e.wait_ge(ccsem[idx], 1)
                    return e.memset(tiny[:, idx:idx + 1], 0.0)
                return lambda: S.add("pool", fn2, reads=[], writes=[dst_t.ap(), tiny[:, idx:idx + 1]])
            else:
                chq = S.new_chan()
                S.dma("sp", dst_t.ap(), src_t.ap(), chq)
                return lambda: None

        def scan_group(q, g4, st_):
            pb = (2560, 3072, 3584, 2560)[g4]
            for cc in range(4):
                c = g4 * 4 + cc
                j, par = divmod(c, 2)
                S.mm(ps[:, pb + cc * 128:pb + (cc + 1) * 128], (q["kt0"], q["kt1"])[par][:, j, :], q["vtk"][:, j, :],
                     start=True, stop=True)
            for cc in range(4):
                c = g4 * 4 + cc
                cur = st_["cur"]
                S.stt(q["Sf"][1 - cur][:, :], q["Sf"][cur][:, :], q["dec"][:, c:c + 1],
                      ps[:, pb + cc * 128:pb + (cc + 1) * 128], ALU.mult, ALU.add)
                st_["cur"] = 1 - cur
                if st_["sb"] and c + 1 < NCK:
                    S.act(q["Sb"][:, c + 1, :], q["Sf"][1 - cur][:, :], AF.Identity)

        def interleave(bsteps, asteps, after):
            ai = 0
            for bi, bstep in enumerate(bsteps):
                bstep()
                while ai < len(asteps) and after[ai] == bi:
                    asteps[ai]()
                    ai += 1
            while ai < len(asteps):
                asteps[ai]()
                ai += 1

        cc_done = []

        def pre_A(h):
            q = sets[h % 2]

            def a1():
                q["wt"] = ws.get(u_pre[h])
                proj1(q["wt"], 0, G0)

            def a2():
                proj1(q["wt"], 1, G1)
                ws.release(u_pre[h])

            def a2g():
                hgrn_gates(S, h, cst, ps[:, G0:G0 + T], q["A"], q["B"], q["C"], q["kend"], q["dec"][:, :])
                S.act(q["ibf"], ps[:, G1:G1 + T], AF.Identity)
                C3 = q["C"].rearrange("p (c t) -> p c t", t=CH)
                S.add("dve", lambda e, C3=C3, h=h: e.reduce_sum(out=Dd[:, h:h + 1], in_=C3[:, :, CH - 1:CH],
                                                               axis=mybir.AxisListType.XY),
                      reads=[q["C"]], writes=[Dd[:, h:h + 1]])
                S.act(Dd[:, h:h + 1], Dd[:, h:h + 1], AF.Exp)
            return [a1, a2, a2g]

        def pre_B(h):
            q = sets[h % 2]
            st_ = {"cur": 0, "sb": False}

            def b1():
                hgrn_transposes(S, cst, psb, q["kend"], q["kt0"], q["kt1"])

            def b1b():
                hgrn_transposes(S, cst, psb, q["ibf"], q["vtk"])
                S.memset("dve", q["Sf"][0][:, :], 0.0)

            def bfin():
                fin = st_["cur"]
                for j in range(4):
                    S.ts("dve", q["stg"][:, j, 0:128], q["Sf"][fin][:, :], oh[:, j:j + 1], None, ALU.mult)
                S.ts("dve", q["stg"][:, :, 128:129], oh3, Dd[:, h:h + 1], None, ALU.mult)
                g, hl = divmod(h, 4)
                S.dma("sp", ccin[g].ap().rearrange("(j l d) n -> d j l n", j=4, l=4)[:, :, hl, :], q["stg"][:, :, :],
                      q["chst"])
                if hl == 3:
                    cc_done.append(cc_op(g, ccin[g], ccout[g]))
            return [b1, b1b] + [lambda g4=g4: scan_group(q, g4, st_) for g4 in range(4)] + [bfin]

        for stp in pre_A(0):
            stp()
        for h in range(NH):
            nxt = pre_A(h + 1) if h + 1 < NH else []
            interleave(pre_B(h), nxt, [0, 3, 3])

        def main_A(h):
            q = sets[h % 2]
            g, hl = divmod(h, 4)
            fi, qg = u_main[h]

            def a1():
                if hl == 0:
                    cc_done[g]()
                up = q["upst"]
                S.dma("sp", up[:, :, :], ccout[g].ap().rearrange("(j l d) n -> d j l n", j=4, l=4)[:, :, hl, :], q["chu"])
                Pp_, Sf_ = q["Pp"], q["Sf"]
                S.stt(Pp_[0][:, :], up[:, 0, 0:128], up[:, 1, 128:129], up[:, 1, 0:128], ALU.mult, ALU.add)
                S.stt(Pp_[1][:, :], Pp_[0][:, :], up[:, 2, 128:129], up[:, 2, 0:128], ALU.mult, ALU.add)
                S.ts("dve", Sf_[0][:, :], up[:, 0, 0:128], oh[:, 1:2], None, ALU.mult)
                S.stt(Sf_[0][:, :], Pp_[0][:, :], oh[:, 2:3], Sf_[0][:, :], ALU.mult, ALU.add)
                S.stt(Sf_[0][:, :], Pp_[1][:, :], oh[:, 3:4], Sf_[0][:, :], ALU.mult, ALU.add)
                q["wt"] = ws.get(fi)
                proj1(q["wt"], 0, G0)

            def a2():
                proj1(q["wt"], 1, G1)
                ws.release(fi)

            def a2g():
                hgrn_gates(S, h, cst, ps[:, G0:G0 + T], q["A"], q["B"], q["C"], q["kend"], q["dec"][:, :],
                           kdec_bf=q["kdec"], eC=True)
                S.act(q["ibf"], ps[:, G1:G1 + T], AF.Identity)

            def a3():
                q["wt"] = ws.get(qg)
                proj1(q["wt"], 0, G0)

            def a4():
                proj1(q["wt"], 1, G1)
                ws.release(qg)
                S.act(q["B"], ps[:, G0:G0 + T], AF.Silu)
                S.tt("dve", q["qdec"], q["B"], q["C"], ALU.mult)
                S.act(q["sg"], ps[:, G1:G1 + T], AF.Sigmoid)
            return [a1, a2, a2g, a3, a4]

        def main_B(h):
            q = sets[h % 2]
            st_ = {"cur": 0, "sb": True}

            def b1():
                hgrn_transposes(S, cst, psb, q["kend"], q["kt0"], q["kt1"])

            def b1b():
                hgrn_transposes(S, cst, psb, q["ibf"], q["vtk"])
                S.act(q["Sb"][:, 0, :], q["Sf"][0][:, :], AF.Identity)

            def batt(half):
                pb = 3072 + half * 512
                for jj in range(4):
                    j = half * 4 + jj
                    S.mm(ps[:, pb + jj * 128:pb + (jj + 1) * 128], q["kdec"][:, j * 128:(j + 1) * 128],
                         q["qdec"][:, j * 128:(j + 1) * 128], start=True, stop=True)
                for jj in range(4):
                    j = half * 4 + jj
                    S.tt("dve", q["attm"][:, j, :], ps[:, pb + jj * 128:pb + (jj + 1) * 128], cst["mask2"][:, :],
                         ALU.mult)

            def bo():
                for j in range(8):
                    S.mm(ps[:, O0 + j * 128:O0 + (j + 1) * 128], q["vtk"][:, j, :], q["attm"][:, j, :], start=True,
                         stop=False)
                    S.mm(ps[:, O0 + j * 128:O0 + j * 128 + 64], q["Sb"][:, 2 * j, :], q["qdec"][:, j * 128:j * 128 + 64],
                         start=False, stop=False)
                    S.mm(ps[:, O0 + j * 128 + 64:O0 + (j + 1) * 128], q["Sb"][:, 2 * j + 1, :],
                         q["qdec"][:, j * 128 + 64:(j + 1) * 128], start=False, stop=True)
                S.act(q["osq"], ps[:, O0:O0 + T], AF.Square)

            def bnorm():
                for t_ in range(2):
                    sl = slice(t_ * 512, (t_ + 1) * 512)
                    pb = PB5 if t_ == 0 else 2048
                    S.mm(ps[:, pb:pb + 512], ones[:, :], q["osq"][:, sl], start=True, stop=True)
                for t_ in range(2):
                    sl = slice(t_ * 512, (t_ + 1) * 512)
                    pb = PB5 if t_ == 0 else 2048
                    S.act(q["A"][:, sl], ps[:, pb:pb + 512], AF.Ln, bias=eps[:, 0:1], scale=1.0 / 128)
                S.act(q["A"], q["A"], AF.Exp, scale=-0.5)
                S.stt(q["C"], ps[:, O0:O0 + T], gn[:, h:h + 1], q["A"], ALU.mult, ALU.mult)
                S.tt("dve", y[:, h, :], q["C"], q["sg"], ALU.mult)
            return ([b1, b1b] + [lambda g4=g4: scan_group(q, g4, st_) for g4 in range(4)]
                    + [lambda: batt(0), lambda: batt(1), bo, bnorm])

        for stp in main_A(0):
            stp()
        for h in range(NH):
            nxt = main_A(h + 1) if h + 1 < NH else []
            interleave(main_B(h), nxt, [0, 3, 3, 5, 9])
        chs1 = [S.new_chan() for _ in range(4)]
        xs4 = [cv(R2 + 40960 + k * 2048, (512,), F32) for k in range(4)]
        tm_t = {"eps": eps, "mean": cv(R2 + 49152, (512,), F32), "rstd": cv(R2 + 51200, (512,), F32),
                "zb": [cv(R2 + 53248 + k * 1024, (512,), BF16) for k in range(2)],
                "zs": [cv(R2 + 55296 + k * 1024, (512,), BF16) for k in range(2)]}
        OB = (512, 1024, 1536, 2560, 3072, 3584)
        done_h = None
        cnt = 0
        for tile in (1, 0):
            c0 = 2 + tile * 512
            for i in range(8):
                wt = ws.get(u_o1[(1 - tile) * 8 + i])
                for sub in range(2):
                    oc = i * 2 + sub
                    g0 = OB[cnt % 6]
                    xst = xs4[cnt % 4]
                    S.dma("sp", xst, xsp[oc * 128:(oc + 1) * 128, tile * 512:(tile + 1) * 512], chs1[cnt % 4])
                    for kc in range(NCH):
                        S.mm(ps[:, g0:g0 + 512], wt[:, kc, sub * 128:(sub + 1) * 128],
                             y[:, kc, tile * 512:(tile + 1) * 512], start=(kc == 0), stop=(kc == NCH - 1))
                    S.stt(zf[:, oc, c0:c0 + 512], xst, ALPHA, ps[:, g0:g0 + 512], ALU.mult, ALU.add)
                    cnt += 1
                ws.release(u_o1[(1 - tile) * 8 + i])
            emit_ln(S, zf, c0, 512, ln1gb[:, 1, :, :], ones, tm_t, ps,
                    [lambda c, c0=c0: xb[:, c, c0:c0 + 512], lambda c, c0=c0: zf[:, c, c0:c0 + 512]],
                    ps_off=(0, 2048))
            if tile == 1:
                for j in range(4):
                    S.ts("dve", hstg[:, j, :].rearrange("p (c t) -> p c t", t=2), zf[:, :, T:T + 2], oh[:, j:j + 1],
                         None, ALU.mult)
                chh = S.new_chan()
                S.dma("sp", cch_in.ap().rearrange("(j p) n -> p j n", p=128), hstg[:, :, :], chh)
                done_h = cc_op(4, cch_in, cch_out)
        done_h()
        chh2 = S.new_chan()
        S.dma("sp", hld[:, :, :], cch_out.ap().rearrange("(j p) n -> p j n", p=128), chh2)
        S.ts("dve", hsum[:, :], hld[:, 0, :], oh[:, 4:5], None, ALU.mult)
        for j in range(1, 4):
            S.stt(hsum[:, :], hld[:, j, :], oh[:, 4 + j:5 + j], hsum[:, :], ALU.mult, ALU.add)
        S.act(xb[:, :, 0:2], hsum[:, :].rearrange("p (c t) -> p c t", t=2), AF.Identity)
        def ln2_l1(tt_):
            c0 = 2 + tt_ * 512
            emit_ln(S, zf, c0, 512, ln2gb[:, 1, :, :], ones, tm_ffn, ps, [lambda c: zf[:, c, c0:c0 + 512]])
        emit_ffn(S, ws, plan1, zf, xb, cwb[:, 1, :, :], gq, tm_ffn, ps, ln_cb=ln2_l1)
        cho = [S.new_chan() for _ in range(4)]
        for c in range(NCH):
            S.dma("sp", outT[c * 128:(c + 1) * 128, :], zf[:, c, 2:T + 2], cho[c % 4])
        S.emit(nc, es, final_chans=cho)
    return nc, S


def fused_inputs(inp):
    f32 = np.float32
    maps = prep_mix0_inputs(inp["x"], inp["ev_w_in"], inp["ev_ln_v_g"], inp["ev_ln_v_b"], inp["ev_w_s"],
                            inp["ev_b_s"], inp["ev_w_pool"], inp["ev_pool_scale"], inp["ev_w_out"],
                            inp["ln1_g"], inp["ln1_b"])
    ln1gb = np.stack([np.stack([_pm(inp["ln1_g"][l], 16), _pm(inp["ln1_b"][l], 16)], axis=-1) for l in range(2)], axis=1)
    ln2gb = np.stack([np.stack([_pm(inp["ln2_g"][l], 16), _pm(inp["ln2_b"][l], 16)], axis=-1) for l in range(2)], axis=1)
    cwbs = []
    for l in range(2):
        cw = np.asarray(inp["ffn_conv_w"][l], f32)
        cb = np.asarray(inp["ffn_conv_b"][l], f32)
        cwbs.append(np.stack([_pm(cw[0], 88), _pm(cw[1], 88), _pm(cw[2], 88), _pm(cb, 88)], axis=-1))
    cwb = np.stack(cwbs, axis=1)
    common = {
        "ln1gb": np.ascontiguousarray(ln1gb.reshape(128, 64)), "ln2gb": np.ascontiguousarray(ln2gb.reshape(128, 64)),
        "cwb": np.ascontiguousarray(cwb.reshape(128, 2 * 88 * 4)),
        "w_up0": np.ascontiguousarray(inp["ffn_w_up"][0], f32), "w_up1": np.ascontiguousarray(inp["ffn_w_up"][1], f32),
        "w_down0": np.ascontiguousarray(inp["ffn_w_down"][0], f32),
        "w_down1": np.ascontiguousarray(inp["ffn_w_down"][1], f32),
        "od_w_in": regroup_w_in(inp["od_w_in"][0]), "od_w_out": np.ascontiguousarray(inp["od_w_out"][0], f32),
        "gn": _pm(inp["od_norm_g"][0], NH)}
    common.update(hgrn_const_inputs(inp["lb_param"]))
    out = []
    for c in range(NCORES):
        b, s = divmod(c, 4)
        m0 = maps[c]
        m = dict(common)
        for k in ("x0T", "wsT", "maskA", "bsT", "lnv", "w_pool", "pscale", "rcnt", "flag"):
            m[k] = m0[k]
        m["ev_w_in"] = m0["w_in"]
        m["ev_w_out"] = m0["w_out"]
        oh = np.zeros((128, 8), f32)
        oh[:, s] = 1.0
        if s > 0:
            oh[:, 4 + s - 1] = 1.0
        m["oh"] = oh
        out.append(m)
    return out


def kernel(**inputs):
    nc, _ = build_fused(use_cc=True)
    maps = fused_inputs(inputs)
    res = run_bass_kernel_spmd(nc, maps, core_ids=list(range(NCORES))).results
    out = np.zeros((2, 4 * T, D), np.float32)
    for c in range(NCORES):
        b, s = divmod(c, 4)
        out[b, s * T:(s + 1) * T] = res[c]["outT"].T
    return out
```
